# Optimizing a Trainium2 kernel written in Bass

```python
import math
import jax, jax.numpy as jnp
from jax import lax
import numpy as np

D_MODEL = 1024
BATCH = 4
SEQ = 8192
DEPTH = 2

GRID_W = 64
CTX_LEN = 256
N_EVEN = (DEPTH + 1) // 2
N_ODD = DEPTH // 2
N_MOD = 6
EPS = 1e-6

RNN_W = 3 * D_MODEL // 4
RNN_HD = 64
RNN_HEADS = RNN_W // RNN_HD
RNN_CONV = 4
LRU_C = 8.0
POOL_W = D_MODEL // 4
POOL_WINDOWS = (2, 4, 8, 16)
POOL_GROUPS = len(POOL_WINDOWS)
POOL_GD = POOL_W // POOL_GROUPS
ATT_HD = 64
ATT_VD = 2 * ATT_HD
ATT_HEADS = (3 * D_MODEL // 4) // ATT_VD
ATT_W = ATT_HEADS * 2 * ATT_HD
Q_BLOCK = 128
ROPE_BASE = 10000.0
ROPE_AX = ATT_HD // 4
CONF_W = D_MODEL // 4
CONF_K = 31
D_FF = 2816
FFN_K = 3

AB_IN = 2 * RNN_W + POOL_W
AB_OUT = RNN_W + POOL_W
CD_IN = 3 * ATT_W + 2 * CONF_W
CD_OUT = ATT_HEADS * ATT_VD + CONF_W

kernel_name = 'hybrid_lru_pool_diffattn_conformer_dit'


def rmsnorm(x, g):
    xf = x.astype(jnp.float32)
    y = xf * lax.rsqrt(jnp.mean(xf * xf, axis=-1, keepdims=True) + EPS)
    return (y * g.astype(jnp.float32)).astype(x.dtype)


def layernorm(x, g, b):
    xf = x.astype(jnp.float32)
    mu = jnp.mean(xf, axis=-1, keepdims=True)
    var = jnp.mean(jnp.square(xf - mu), axis=-1, keepdims=True)
    y = (xf - mu) * lax.rsqrt(var + EPS)
    return (y * g.astype(jnp.float32) + b.astype(jnp.float32)).astype(x.dtype)


def modulate(x, g, shift, scale):
    return rmsnorm(x, g) * (1.0 + scale) + shift


def adaln(cvec, w, b):
    m = jax.nn.silu(cvec) @ w + b
    return jnp.split(m, N_MOD, axis=-1)


def dwconv(x, w, b, pad_lo, pad_hi):
    y = lax.conv_general_dilated(x, w[:, None, :].astype(x.dtype), window_strides=(1,),
                                 padding=[(pad_lo, pad_hi)], dimension_numbers=('NWC', 'WIO', 'NWC'),
                                 feature_group_count=x.shape[-1])
    return y + b


def centred_mean(x, w):
    L = x.shape[1]
    xf = x.astype(jnp.float32)
    S = jnp.concatenate([jnp.zeros_like(xf[:, :1]), jnp.cumsum(xf, axis=1)], axis=1)
    t = jnp.arange(L)
    lo = jnp.clip(t - w // 2, 0, L)
    hi = jnp.clip(t + w - w // 2, 0, L)
    cnt = (hi - lo).astype(jnp.float32)
    return ((S[:, hi] - S[:, lo]) / cnt[None, :, None]).astype(x.dtype)


def pool_mixer(xb, w_pool, scale):
    B, L, _ = xb.shape
    xg = xb.reshape(B, L, POOL_GROUPS, POOL_GD)
    d = jnp.stack([centred_mean(xg[:, :, g], POOL_WINDOWS[g]) - xg[:, :, g] for g in range(POOL_GROUPS)], axis=2)
    y = jnp.einsum('blgc,gcd->blgd', d, w_pool)
    return y.reshape(B, L, POOL_W) * scale


def lru_coeffs(xc, wa, ba, wx, bx, lam):
    B, L, _ = xc.shape
    xh = xc.reshape(B, L, RNN_HEADS, RNN_HD)
    r = jax.nn.sigmoid(jnp.einsum('blhi,hij->blhj', xh, wa).reshape(B, L, RNN_W) + ba)
    i = jax.nn.sigmoid(jnp.einsum('blhi,hij->blhj', xh, wx).reshape(B, L, RNN_W) + bx)
    log_a = -LRU_C * r.astype(jnp.float32) * jax.nn.softplus(-lam.astype(jnp.float32))
    a = jnp.exp(log_a)
    b = jnp.sqrt(-jnp.expm1(2.0 * log_a)) * (i * xc).astype(jnp.float32)
    return a, b


def _combine(e1, e2):
    a1, b1 = e1
    a2, b2 = e2
    return a1 * a2, a2 * b1 + b2


def linear_scan(a, b, reverse, h0=None):
    A, H = lax.associative_scan(_combine, (a, b), reverse=reverse, axis=1)
    if h0 is None:
        return H
    return H + A * h0[:, None, :]


def mix_ab(h, hc, w_in, w_out, conv_w, conv_b, wa, ba, wx, bx, lam, pool_w, pool_scale, need_ctx):
    pad_lo, pad_hi = RNN_CONV // 2, RNN_CONV - 1 - RNN_CONV // 2
    z = h @ w_in
    xa = dwconv(z[..., :RNN_W], conv_w, conv_b, pad_lo, pad_hi)
    ga = z[..., RNN_W:2 * RNN_W]
    xb = z[..., 2 * RNN_W:]
    xa_c = dwconv(hc @ w_in[:, :RNN_W], conv_w, conv_b, pad_lo, pad_hi)
    af, bf = lru_coeffs(xa_c, wa[0], ba[0], wx[0], bx[0], lam[0])
    ab, bb = lru_coeffs(xa_c, wa[1], ba[1], wx[1], bx[1], lam[1])
    hf_c = linear_scan(af, bf, False)
    hb_c = linear_scan(ab, bb, True)
    af, bf = lru_coeffs(xa, wa[0], ba[0], wx[0], bx[0], lam[0])
    ab, bb = lru_coeffs(xa, wa[1], ba[1], wx[1], bx[1], lam[1])
    hf = linear_scan(af, bf, False, hf_c[:, -1])
    hb = linear_scan(ab, bb, True, hb_c[:, 0])
    y_a = (hf + hb).astype(h.dtype) * jax.nn.gelu(ga)
    y_b = pool_mixer(xb, pool_w, pool_scale)
    y = jnp.concatenate([y_a, y_b], axis=-1) @ w_out
    if not need_ctx:
        return y, None
    zc = hc @ w_in[:, RNN_W:]
    yc_a = (hf_c + hb_c).astype(hc.dtype) * jax.nn.gelu(zc[..., :RNN_W])
    yc_b = pool_mixer(zc[..., RNN_W:], pool_w, pool_scale)
    yc = jnp.concatenate([yc_a, yc_b], axis=-1) @ w_out
    return y, yc


def _rot(x, cos, sin):
    x1, x2 = x[..., :ROPE_AX], x[..., ROPE_AX:]
    return jnp.concatenate([x1 * cos - x2 * sin, x1 * sin + x2 * cos], axis=-1)


def apply_rope(x, rope):
    cr, sr, cc, sc = [r[None, :, None, None, :] for r in rope]
    y = jnp.concatenate([_rot(x[..., :2 * ROPE_AX], cr, sr), _rot(x[..., 2 * ROPE_AX:], cc, sc)], axis=-1)
    return y.astype(x.dtype)


def diff_attend(q, k, v, lam):
    s = jnp.einsum('bqhmd,bkhmd->bhmqk', q, k, preferred_element_type=jnp.float32) * (ATT_HD ** -0.5)
    p = jax.nn.softmax(s, axis=-1)
    w = p[:, :, 0] - lam * p[:, :, 1]
    return jnp.einsum('bhqk,bkhe->bqhe', w.astype(v.dtype), v)


def conformer_conv(u, dw_w, dw_b, ln_g, ln_b):
    g = u[..., :CONF_W] * jax.nn.sigmoid(u[..., CONF_W:])
    g = dwconv(g, dw_w, dw_b, CONF_K // 2, CONF_K // 2)
    return jax.nn.silu(layernorm(g, ln_g, ln_b))


def mix_cd(h, hc, w_in, w_out, lq1, lk1, lq2, lk2, subln_g, dw_w, dw_b, ln_g, ln_b, rope, lambda_init, need_ctx):
    B, L, _ = h.shape
    Lc = hc.shape[1]
    z = h @ w_in
    q = apply_rope(z[..., :ATT_W].reshape(B, L, ATT_HEADS, 2, ATT_HD), rope)
    k = apply_rope(z[..., ATT_W:2 * ATT_W].reshape(B, L, ATT_HEADS, 2, ATT_HD), rope)
    v = z[..., 2 * ATT_W:3 * ATT_W].reshape(B, L, ATT_HEADS, ATT_VD)
    u = z[..., 3 * ATT_W:]
    zkv = hc @ w_in[:, ATT_W:3 * ATT_W]
    kc = zkv[..., :ATT_W].reshape(B, Lc, ATT_HEADS, 2, ATT_HD)
    vc = zkv[..., ATT_W:].reshape(B, Lc, ATT_HEADS, ATT_VD)
    lam = (jnp.exp(jnp.sum(lq1.astype(jnp.float32) * lk1.astype(jnp.float32)))
           - jnp.exp(jnp.sum(lq2.astype(jnp.float32) * lk2.astype(jnp.float32))) + lambda_init)
    k_all = jnp.concatenate([k, kc], axis=1)
    v_all = jnp.concatenate([v, vc], axis=1)
    nb = L // Q_BLOCK
    qb = jnp.moveaxis(q.reshape(B, nb, Q_BLOCK, ATT_HEADS, 2, ATT_HD), 1, 0)
    o = lax.map(lambda qq: diff_attend(qq, k_all, v_all, lam), qb)
    o = jnp.moveaxis(o, 0, 1).reshape(B, L, ATT_HEADS, ATT_VD)
    y_c = (rmsnorm(o, subln_g) * (1.0 - lambda_init)).reshape(B, L, ATT_HEADS * ATT_VD)
    y_d = conformer_conv(u, dw_w, dw_b, ln_g, ln_b)
    y = jnp.concatenate([y_c, y_d], axis=-1) @ w_out
    if not need_ctx:
        return y, None
    qc = (hc @ w_in[:, :ATT_W]).reshape(B, Lc, ATT_HEADS, 2, ATT_HD)
    uc = hc @ w_in[:, 3 * ATT_W:]
    oc = diff_attend(qc, kc, vc, lam)
    yc_c = (rmsnorm(oc, subln_g) * (1.0 - lambda_init)).reshape(B, Lc, ATT_HEADS * ATT_VD)
    yc_d = conformer_conv(uc, dw_w, dw_b, ln_g, ln_b)
    yc = jnp.concatenate([yc_c, yc_d], axis=-1) @ w_out
    return y, yc


def conv_ffn(h, w_up, conv_w, conv_b, w_down):
    u = dwconv(h @ w_up, conv_w, conv_b, FFN_K // 2, FFN_K // 2)
    return (jax.nn.silu(u[..., D_FF:]) * u[..., :D_FF]) @ w_down


def setup_inputs(seed: int = 0) -> dict:
    key = jax.random.key(seed)
    ks = iter(jax.random.split(key, 48))
    D = D_MODEL

    def nrm(shape, scale):
        return scale * jax.random.normal(next(ks), shape, jnp.float32)

    inp = {}
    inp['x'] = nrm((BATCH, SEQ, D), 1.0)
    inp['c'] = nrm((BATCH, D), 1.0)
    inp['ctx'] = nrm((BATCH, CTX_LEN, D), 1.0)
    inp['c_ctx'] = nrm((D,), 1.0)
    inp['mod_w'] = nrm((DEPTH, D, N_MOD * D), 0.5 * D ** -0.5)
    inp['mod_b'] = nrm((DEPTH, N_MOD * D), 0.05)
    inp['norm_mix_g'] = 1.0 + nrm((DEPTH, D), 0.05)
    inp['norm_ffn_g'] = 1.0 + nrm((DEPTH, D), 0.05)
    inp['ab_w_in'] = nrm((N_EVEN, D, AB_IN), D ** -0.5)
    inp['ab_w_out'] = nrm((N_EVEN, AB_OUT, D), AB_OUT ** -0.5)
    inp['lru_conv_w'] = nrm((N_EVEN, RNN_CONV, RNN_W), RNN_CONV ** -0.5)
    inp['lru_conv_b'] = nrm((N_EVEN, RNN_W), 0.02)
    inp['lru_wa'] = nrm((N_EVEN, 2, RNN_HEADS, RNN_HD, RNN_HD), RNN_HD ** -0.5)
    inp['lru_ba'] = nrm((N_EVEN, 2, RNN_W), 0.1)
    inp['lru_wx'] = nrm((N_EVEN, 2, RNN_HEADS, RNN_HD, RNN_HD), RNN_HD ** -0.5)
    inp['lru_bx'] = nrm((N_EVEN, 2, RNN_W), 0.1)
    a8 = jax.random.uniform(next(ks), (N_EVEN, 2, RNN_W), jnp.float32, 0.9, 0.999)
    s = a8 ** (1.0 / LRU_C)
    inp['lru_lambda'] = jnp.log(s) - jnp.log1p(-s)
    inp['pool_w'] = nrm((N_EVEN, POOL_GROUPS, POOL_GD, POOL_GD), POOL_GD ** -0.5)
    inp['pool_scale'] = 1.0 + nrm((N_EVEN, POOL_W), 0.1)
    inp['cd_w_in'] = nrm((N_ODD, D, CD_IN), D ** -0.5)
    inp['cd_w_out'] = nrm((N_ODD, CD_OUT, D), CD_OUT ** -0.5)
    inp['diff_lq1'] = nrm((N_ODD, ATT_HD), 0.1)
    inp['diff_lk1'] = nrm((N_ODD, ATT_HD), 0.1)
    inp['diff_lq2'] = nrm((N_ODD, ATT_HD), 0.1)
    inp['diff_lk2'] = nrm((N_ODD, ATT_HD), 0.1)
    inp['diff_subln_g'] = 1.0 + nrm((N_ODD, ATT_VD), 0.05)
    inp['conf_dw_w'] = nrm((N_ODD, CONF_K, CONF_W), CONF_K ** -0.5)
    inp['conf_dw_b'] = nrm((N_ODD, CONF_W), 0.02)
    inp['conf_ln_g'] = 1.0 + nrm((N_ODD, CONF_W), 0.05)
    inp['conf_ln_b'] = nrm((N_ODD, CONF_W), 0.02)
    inp['ffn_w_up'] = nrm((DEPTH, D, 2 * D_FF), D ** -0.5)
    inp['ffn_conv_w'] = nrm((DEPTH, FFN_K, 2 * D_FF), FFN_K ** -0.5)
    inp['ffn_conv_b'] = nrm((DEPTH, 2 * D_FF), 0.02)
    inp['ffn_w_down'] = nrm((DEPTH, D_FF, D), D_FF ** -0.5)
    inp['final_g'] = 1.0 + nrm((D,), 0.05)
    return inp


def reference(x, c, ctx, c_ctx, mod_w, mod_b, norm_mix_g, norm_ffn_g, ab_w_in, ab_w_out, lru_conv_w, lru_conv_b,
              lru_wa, lru_ba, lru_wx, lru_bx, lru_lambda, pool_w, pool_scale, cd_w_in, cd_w_out, diff_lq1, diff_lk1,
              diff_lq2, diff_lk2, diff_subln_g, conf_dw_w, conf_dw_b, conf_ln_g, conf_ln_b, ffn_w_up, ffn_conv_w,
              ffn_conv_b, ffn_w_down, final_g):
    L = x.shape[1]
    ROWS = L // GRID_W
    row = jnp.repeat(jnp.arange(ROWS), GRID_W).astype(jnp.float32)
    col = jnp.tile(jnp.arange(GRID_W), ROWS).astype(jnp.float32)
    inv = ROPE_BASE ** (-jnp.arange(ROPE_AX, dtype=jnp.float32) / ROPE_AX)
    ang_r = row[:, None] * inv
    ang_c = col[:, None] * inv
    rope = (jnp.cos(ang_r), jnp.sin(ang_r), jnp.cos(ang_c), jnp.sin(ang_c))

    for i in range(DEPTH):
        j = i // 2
        need_ctx = i < DEPTH - 1
        sh1, sc1, g1, sh2, sc2, g2 = [m[:, None, :] for m in adaln(c, mod_w[i], mod_b[i])]
        sh1c, sc1c, g1c, sh2c, sc2c, g2c = adaln(c_ctx, mod_w[i], mod_b[i])
        h = modulate(x, norm_mix_g[i], sh1, sc1)
        hc = modulate(ctx, norm_mix_g[i], sh1c, sc1c)
        if i % 2 == 0:
            y, yc = mix_ab(h, hc, ab_w_in[j], ab_w_out[j], lru_conv_w[j], lru_conv_b[j], lru_wa[j], lru_ba[j],
                           lru_wx[j], lru_bx[j], lru_lambda[j], pool_w[j], pool_scale[j], need_ctx)
        else:
            lambda_init = 0.8 - 0.6 * math.exp(-0.3 * i)
            y, yc = mix_cd(h, hc, cd_w_in[j], cd_w_out[j], diff_lq1[j], diff_lk1[j], diff_lq2[j], diff_lk2[j],
                           diff_subln_g[j], conf_dw_w[j], conf_dw_b[j], conf_ln_g[j], conf_ln_b[j], rope,
                           lambda_init, need_ctx)
        x = x + g1 * y
        x = x + g2 * conv_ffn(modulate(x, norm_ffn_g[i], sh2, sc2), ffn_w_up[i], ffn_conv_w[i], ffn_conv_b[i], ffn_w_down[i])
        if need_ctx:
            ctx = ctx + g1c * yc
            ctx = ctx + g2c * conv_ffn(modulate(ctx, norm_ffn_g[i], sh2c, sc2c), ffn_w_up[i], ffn_conv_w[i],
                                       ffn_conv_b[i], ffn_w_down[i])
    return rmsnorm(x, final_g)
```

```python
import numpy as np
import math
from contextlib import ExitStack
import concourse.bass as bass
import concourse.mybir as mybir
from concourse.bass_utils import run_bass_kernel_spmd
from concourse.ap import AP

F32 = mybir.dt.float32
BF16 = mybir.dt.bfloat16
AF = mybir.ActivationFunctionType
ALU = mybir.AluOpType
AX = mybir.AxisListType

D = 1024
LC = 256
DFF = 2816
NFC = DFF // 128
PADZ = 16
EPS = 1e-6
GRID_W = 64
LAMBDA_INIT1 = 0.8 - 0.6 * math.exp(-0.3 * 1)


def rev(ap):
    a = [list(x) for x in ap.ap]
    st, n = a[-1]
    a[-1] = [-st, n]
    return AP(ap.tensor, ap.offset + st * (n - 1), a)


class Prog:
    def __init__(self, nc, es):
        self.nc = nc
        self.es = es
        self.eng = {'pe': nc.tensor, 'act': nc.scalar, 'dve': nc.vector, 'pool': nc.gpsimd, 'sp': nc.sync}
        self.esem = {e: es.enter_context(nc.semaphore('S_' + e)) for e in ('pe', 'act', 'dve', 'pool')}
        self.ecnt = {e: 0 for e in self.esem}
        self.dsem = {}
        self.dpool = []
        self.nd = 0
        self.waited = {e: {} for e in self.eng}
        self.lastw = {}
        self.rd = {}
        self.ninstr = 0

    def _need(self, reads, writes):
        ev = {}

        def add(e):
            if e is None:
                return
            k, sem, val = e
            if k not in ev or ev[k][1] < val:
                ev[k] = (sem, val)
        for r in reads:
            add(self.lastw.get(r))
        for w in writes:
            add(self.lastw.get(w))
            for e in self.rd.get(w, {}).items():
                add((e[0], e[1][0], e[1][1]))
        return ev

    def _wait(self, e, ev):
        for k, (sem, val) in ev.items():
            if e == 'pe' and k == 'S_pe':
                continue
            if self.waited[e].get(k, 0) < val:
                self.eng[e].wait_ge(sem, val)
                self.waited[e][k] = val
                self.ninstr += 1

    def _commit(self, ev, reads, writes):
        k, sem, val = ev
        for w in writes:
            self.lastw[w] = ev
            self.rd[w] = {}
        for r in reads:
            self.rd.setdefault(r, {})[k] = (sem, val)

    def op(self, e, fn, reads, writes):
        self._wait(e, self._need(reads, writes))
        ins = fn(self.eng[e])
        self.ecnt[e] += 1
        ins.then_inc(self.esem[e], 1)
        self.ninstr += 1
        self._commit(('S_' + e, self.esem[e], self.ecnt[e]), reads, writes)

    def dma(self, q, out, in_, reads, writes, key):
        self._wait(q, self._need(reads, writes))
        if key not in self.dsem:
            if self.dpool:
                self.dsem[key] = self.dpool.pop()
            else:
                nm = 'D%d' % self.nd
                self.nd += 1
                self.dsem[key] = [self.es.enter_context(self.nc.semaphore(nm)), 0, nm]
        d = self.dsem[key]
        ins = self.eng[q].dma_start(out=out, in_=in_)
        d[1] += 16
        ins.then_inc(d[0], 16)
        self.ninstr += 1
        self._commit((d[2], d[0], d[1]), reads, writes)

    def barrier(self):
        ev = {}
        for e in self.esem:
            if self.ecnt[e] > 0:
                ev['S_' + e] = (self.esem[e], self.ecnt[e])
        for k, d in self.dsem.items():
            if d[1] > 0:
                ev[d[2]] = (d[0], d[1])
        for e in self.eng:
            self._wait(e, dict(ev))
        self.lastw = {}
        self.rd = {}
        for k, d in self.dsem.items():
            self.dpool.append(d)
        self.dsem = {}


def keys(name, n):
    return [f"{name}.{i}" for i in range(n)]


def tiles_of(T, w=512):
    out = []
    s = 0
    while s < T:
        n = min(w, T - s)
        out.append((s, n))
        s += n
    return out


class K:
    pass


def build(T, dbg=False, stop_after=None):
    nc = bass.Bass("TRN2", target_bir_lowering=False)
    k = K()
    k.nc = nc
    k.T = T

    def din(name, shape, dt=F32):
        return nc.dram_tensor(name, list(shape), dt, kind="ExternalInput").ap()

    def dscr(name, shape, dt=F32, out=False):
        kind = "ExternalOutput" if (out or dbg) else "Internal"
        return nc.dram_tensor(name, list(shape), dt, kind=kind).ap()

    I = {}
    I['x'] = din('x', [T, D])
    I['ctx'] = din('ctx', [LC, D])
    I['c_fm'] = din('c_fm', [128, 2, 8])
    I['mod_w'] = din('mod_w', [2, D, 6 * D])
    I['modb_fm'] = din('modb_fm', [128, 2, 6, 8])
    I['mod_b'] = din('mod_b', [2, 6 * D])
    I['ng_fm'] = din('ng_fm', [128, 2, 2, 8])
    I['final_g_bc'] = din('final_g_bc', [128, D])
    I['ident'] = din('ident', [128, 128])
    I['ab_w_in'] = din('ab_w_in', [D, 1792])
    I['ab_w_out'] = din('ab_w_out', [D, D])
    I['lru_cw'] = din('lru_cw', [128, 6, 4])
    I['lru_cb'] = din('lru_cb', [128, 6])
    I['lru_bdA'] = din('lru_bdA', [128, 2, 6, 128])
    I['lru_bdX'] = din('lru_bdX', [128, 2, 6, 128])
    I['lru_ba'] = din('lru_ba', [128, 2, 6])
    I['lru_bx'] = din('lru_bx', [128, 2, 6])
    I['lru_lam'] = din('lru_lam', [128, 2, 6])
    I['pool_bd'] = din('pool_bd', [128, 2, 128])
    I['pool_scale'] = din('pool_scale', [128, 2])
    I['pool_invw'] = din('pool_invw', [128, 2])
    I['pool_corr'] = din('pool_corr', [128, 2, 2, 16])
    I['ffn_w_up'] = din('ffn_w_up', [2, D, 2 * DFF])
    I['ffn_w_down'] = din('ffn_w_down', [2, DFF, D])
    I['ffn_cw'] = din('ffn_cw', [128, 2, 2 * NFC, 3])
    I['ffn_cb'] = din('ffn_cb', [128, 2, 2 * NFC])
    I['cd_w_in'] = din('cd_w_in', [D, 2816])
    I['cd_w_sw'] = din('cd_w_sw', [D, 1536])
    I['cd_w_out'] = din('cd_w_out', [D, D])
    I['rope_c'] = din('rope_c', [128, T])
    I['rope_s'] = din('rope_s', [128, T])
    I['diff_l'] = din('diff_l', [1, 4, 64])
    I['subln_g'] = din('subln_g', [128, 1])
    I['conf_w'] = din('conf_w', [128, 2, 31])
    I['conf_b'] = din('conf_b', [128, 2])
    I['conf_lng'] = din('conf_lng', [128, 2])
    I['conf_lnb'] = din('conf_lnb', [128, 2])
    k.I = I

    k.out = nc.dram_tensor('out', [T, D], F32, kind="ExternalOutput").ap()
    S = {}
    for nm, TT in (('l', T), ('c', LC)):
        S['z_' + nm] = dscr('z_' + nm, [1792, PADZ + TT + PADZ])
        S['xa_' + nm] = dscr('xa_' + nm, [768, TT])
        S['hf_' + nm] = dscr('hf_' + nm, [768, TT])
        S['x05_' + nm] = dscr('x05_' + nm, [1 + TT, D])
        S['x1_' + nm] = dscr('x1_' + nm, [1 + TT + 128, D])
    S['x15'] = dscr('x15', [1 + T, D])
    S['x2'] = dscr('x2', [1 + T + 128, D])
    S['qT'] = dscr('qT', [768, T], BF16)
    S['kT'] = dscr('kT', [768, T + LC], BF16)
    S['v'] = dscr('v', [T + LC, 768], BF16)
    S['gl'] = dscr('gl', [256, PADZ + T + PADZ])
    k.S = S

    with ExitStack() as es:
        P = Prog(nc, es)
        k.P = P

        uid = [0]

        def sb(name, shape, dt, st=es):
            uid[0] += 1
            return st.enter_context(nc.sbuf_tensor(f"{name}_s{uid[0]}", list(shape), dt))

        def ps(name, shape, dt, st=es):
            uid[0] += 1
            return st.enter_context(nc.psum_tensor(f"{name}_p{uid[0]}", list(shape), dt))
        k.sb = sb
        k.ps = ps

        ident = sb('ident', [128, 128], BF16)
        P.dma('pool', ident[:], I['ident'], [], ['ident'], 'ident')
        k.ident = ident
        cst = sb('cst', [128, 8], F32)
        P.op('dve', lambda e: e.memset(cst[:, 0:1], EPS), [], ['cst'])
        P.op('dve', lambda e: e.memset(cst[:, 1:2], 1.0), [], ['cst'])
        P.op('dve', lambda e: e.memset(cst[:, 2:3], 0.0), [], ['cst'])
        k.cst = cst
        zero = sb('zero', [128, 512], F32)
        P.op('dve', lambda e: e.memset(zero[:], 0.0), [], ['zero'])
        k.zero = zero
        ones_bf = sb('ones_bf', [128, 128], BF16)
        P.op('dve', lambda e: e.memset(ones_bf[:], 1.0), [], ['ones_bf'])
        k.ones_bf = ones_bf
        ones_f = sb('ones_f', [128, 128], F32)
        P.op('dve', lambda e: e.memset(ones_f[:], 1.0), [], ['ones_f'])
        k.ones_f = ones_f

        modfm = sb('modfm', [128, 2, 2, 4, 8], F32)
        k.modfm = modfm
        k.gbc_d = nc.dram_tensor('gbc_d', [2, 2, 2, 128, D], F32, kind="Internal").ap()

        phase_adaln(k)
        P.barrier()
        if dbg:
            mdbg = nc.dram_tensor('modfm_dbg', [128, 2, 2, 4, 8], F32, kind="ExternalOutput").ap()
            P.dma('sp', mdbg, modfm[:], [], [], 'mdbg')
            gdbg = nc.dram_tensor('gbc_dbg', [2, 2, 2, 128, D], F32, kind="ExternalOutput").ap()
            P.dma('sp', gdbg, k.gbc_d, [], [], 'gdbg')
        if stop_after == 'adaln':
            return finish(k, es)

        layer0(k, stop_after)
        if stop_after is not None and stop_after.startswith('l0'):
            return finish(k, es)
        layer1(k, stop_after)
        return finish(k, es)


def finish(k, es):
    k.P.barrier()
    k.ninstr = k.P.ninstr
    return k


def phase_adaln(k):
    nc, P, I = k.nc, k.P, k.I
    with ExitStack() as st:
        sb = lambda n, s, d: k.sb(n, s, d, st)
        ps = lambda n, s, d: k.ps(n, s, d, st)
        cf = sb('ad_cf', [128, 2, 8], F32)
        P.dma('sp', cf[:], I['c_fm'], [], ['ad_cf'], 'ad_cf')
        sc = sb('ad_sc', [128, 2, 8], F32)
        P.op('act', lambda e: e.activation(out=sc[:], in_=cf[:], func=AF.Silu), ['ad_cf'], ['ad_sc'])
        rep = sb('ad_rep', [128, 2, 8, 128], F32)
        for s_ in range(2):
            for kc in range(8):
                P.op('dve', lambda e, s_=s_, kc=kc: e.tensor_copy(out=rep[:, s_, kc, :], in_=sc[:, s_, kc:kc + 1].to_broadcast([128, 128])),
                     ['ad_sc'], [f'ad_rep.{s_}.{kc}'])
        modb = sb('ad_modb', [128, 2, 6, 8], F32)
        P.dma('sp', modb[:], I['modb_fm'], [], ['ad_modb'], 'ad_modb')
        ng = sb('ad_ng', [128, 2, 2, 8], F32)
        P.dma('sp', ng[:], I['ng_fm'], [], ['ad_ng'], 'ad_ng')
        brow = sb('ad_brow', [1, 2, 6 * D], F32)
        P.dma('sp', brow[:], I['mod_b'].rearrange("(o l) n -> o l n", o=1), [], ['ad_brow'], 'ad_brow')
        wt = [sb(f'ad_w{i}', [128, 6 * D], F32) for i in range(2)]
        gst = sb('ad_gst', [128, 2, D], F32)
        facc = sb('ad_facc', [128, 32, 2], F32)
        pfm = ps('ad_pfm', [128, 32, 2], F32)
        pbc = [ps(f'ad_pbc{i}', [128, 512], F32) for i in range(4)]
        for l in range(2):
            for pss in range(2):
                for kc in range(8):
                    w = wt[kc % 2]
                    wk = f'ad_w{kc % 2}'
                    P.dma('sp', w[:], I['mod_w'][l, kc * 128:(kc + 1) * 128, :], [], [wk], wk)
                    if pss == 0:
                        jmap = [0, 1, 3, 4]
                        for jj, j in enumerate(jmap):
                            for fc in range(8):
                                col = j * D + fc * 128
                                P.op('pe', lambda e, w=w, col=col, jj=jj, fc=fc, kc=kc: e.matmul(
                                    pfm[:, jj * 8 + fc, :], lhsT=w[:, col:col + 128], rhs=sc[:, :, kc],
                                    start=True, stop=True), [wk, 'ad_sc'], ['ad_pfm'])
                        if kc == 0:
                            P.op('dve', lambda e: e.tensor_copy(out=facc[:], in_=pfm[:]), ['ad_pfm'], ['ad_facc'])
                        else:
                            P.op('dve', lambda e: e.tensor_tensor(out=facc[:], in0=pfm[:], in1=facc[:], op=ALU.add), ['ad_pfm', 'ad_facc'], ['ad_facc'])
                    if True:
                        s_ = pss
                        for nt in range(4):
                            gj = 2 if nt < 2 else 5
                            col = gj * D + (nt % 2) * 512
                            P.op('pe', lambda e, w=w, col=col, nt=nt, kc=kc, s_=s_: e.matmul(
                                pbc[nt][:], lhsT=rep[:, s_, kc, :], rhs=w[:, col:col + 512],
                                start=(kc == 0), stop=False), [wk, f'ad_rep.{s_}.{kc}'], [f'ad_pbc{nt}'])
                if pss == 0:
                    jmap = [0, 1, 3, 4]
                    for s_ in range(2):
                        for jj, j in enumerate(jmap):
                            P.op('dve', lambda e, s_=s_, jj=jj, j=j, l=l: e.tensor_tensor(
                                out=k.modfm[:, l, s_, jj, :], in0=facc[:, jj * 8:(jj + 1) * 8, s_], in1=modb[:, l, j, :], op=ALU.add),
                                ['ad_facc', 'ad_modb'], [f'modfm.{l}.{s_}.{jj}'])
                        for jj, which in ((1, 0), (3, 1)):
                            P.op('dve', lambda e, s_=s_, jj=jj, which=which, l=l: e.scalar_tensor_tensor(
                                out=k.modfm[:, l, s_, jj, :], in0=k.modfm[:, l, s_, jj, :], scalar=1.0, in1=ng[:, l, which, :],
                                op0=ALU.add, op1=ALU.mult), [f'modfm.{l}.{s_}.{jj}', 'ad_ng'], [f'modfm.{l}.{s_}.{jj}'])
                if True:
                    s_ = pss
                    for nt in range(4):
                        gj = 2 if nt < 2 else 5
                        col = gj * D + (nt % 2) * 512
                        P.op('pe', lambda e, nt=nt, col=col, l=l: e.matmul(
                            pbc[nt][:], lhsT=k.ones_f[0:1, :], rhs=brow[0:1, l, col:col + 512], start=False, stop=True),
                            ['ones_f', 'ad_brow'], [f'ad_pbc{nt}'])
                        P.op('act', lambda e, nt=nt: e.copy(out=gst[:, nt // 2, (nt % 2) * 512:(nt % 2) * 512 + 512], in_=pbc[nt][:]),
                            [f'ad_pbc{nt}'], [f'ad_gst.{nt}'])
                    for j2 in range(2):
                        P.dma('sp', k.gbc_d[l, s_, j2], gst[:, j2, :], [f'ad_gst.{2 * j2}', f'ad_gst.{2 * j2 + 1}'], [], 'ad_gst')
        P.barrier()


def load_w_bf16(k, name, dst, src_ap, nk, ncols, colblk=2048):
    P = k.P
    for kc in range(nk):
        for c0 in range(0, ncols, colblk):
            c1 = min(ncols, c0 + colblk)
            P.dma('pool', dst[:, kc, c0:c1], src_ap[kc * 128:(kc + 1) * 128, c0:c1], [], [f'{name}.{kc}'], f'{name}.{kc}')


def fold_w_bf16(k, st, name, dst, src_ap, nk, gb_dram):
    P = k.P
    stg = [k.sb(f'{name}_stg{i}', [128, D], F32, st) for i in range(2)]
    gb = k.sb(f'{name}_gb', [128, D], F32, st)
    P.dma('sp', gb[:], gb_dram, [], [f'{name}_gb'], f'{name}_gb')
    gb_ap = gb[:]
    for kc in range(nk):
        s_ = stg[kc % 2]
        sk = f'{name}_stg{kc % 2}'
        P.dma('sp', s_[:], src_ap[kc * 128:(kc + 1) * 128, :], [], [sk], sk)
        P.op('dve', lambda e, s_=s_, kc=kc: e.tensor_tensor(out=dst[:, kc, :], in0=s_[:], in1=gb_ap, op=ALU.mult),
             [sk, f'{name}_gb'], [f'{name}.{kc}'])


def modulate_tile(k, B, src_rows, n, l, s_, jsh, hT, hTname):
    P = k.P
    ng = n // 128
    X, Xn = B['xt'], B['xtname']
    ss = B['ss']
    xn = B['xn']
    for g0 in range(0, ng, 2):
        gg = min(2, ng - g0)
        P.dma('sp', X[:, 0:gg, :], src_rows[g0 * 128:(g0 + gg) * 128, :].rearrange("(g p) f -> p g f", p=128), [], [Xn], Xn)
        P.op('dve', lambda e: e.memset(ss[:, 0:2], 0.0), [], [B['ssname']])
        for g in range(gg):
            P.op('act', lambda e, g=g: e.activation(out=B['junk'][:], in_=X[:, g, :], func=AF.Square, accum_out=ss[:, g:g + 1]),
                 [Xn], [B['ssname'], B['junkname']])
        P.op('act', lambda e, gg=gg: e.activation(out=ss[:, 4:4 + gg], in_=ss[:, 0:gg], func=AF.Sqrt, bias=k.cst[:, 0:1], scale=1.0 / D),
             [B['ssname'], 'cst'], [B['ssname']])
        P.op('dve', lambda e, gg=gg: e.reciprocal(out=ss[:, 8:8 + gg], in_=ss[:, 4:4 + gg]), [B['ssname']], [B['ssname']])
        for g in range(gg):
            eng = 'dve' if g % 2 == 0 else 'pool'
            P.op(eng, lambda e, g=g, g0=g0: e.tensor_scalar(out=xn[:, g0 + g, :], in0=X[:, g, :], scalar1=ss[:, 8 + g:9 + g], scalar2=None, op0=ALU.mult),
                 [Xn, B['ssname']], [f"{B['xnname']}.{g0 + g}"])
    for fc in range(8):
        tp = B['tp'][fc % 2]
        tpn = B['tpname'][fc % 2]
        for g in range(ng):
            P.op('pe', lambda e, g=g, fc=fc, tp=tp: e.transpose(out=tp[:, g * 128:(g + 1) * 128], in_=xn[:, g, fc * 128:(fc + 1) * 128], identity=k.ident[:]),
                 [f"{B['xnname']}.{g}", 'ident'], [tpn])
        P.op('act', lambda e, fc=fc, tp=tp: e.activation(out=hT[:, fc, 0:n], in_=tp[:, 0:n], func=AF.Identity,
                                                          scale=k.modfm[:, l, s_, jsh + 1, fc:fc + 1], bias=k.modfm[:, l, s_, jsh, fc:fc + 1]),
             [tpn], [f'{hTname}.{fc}'])


def mod_bufs(k, st, pfx):
    B = {}
    B['xt'] = k.sb(pfx + 'xt', [128, 2, D], F32, st)
    B['xtname'] = pfx + 'xt'
    B['xn'] = k.sb(pfx + 'xn', [128, 4, D], BF16, st)
    B['xnname'] = pfx + 'xn'
    B['junk'] = k.sb(pfx + 'junk', [128, D], BF16, st)
    B['junkname'] = pfx + 'junk'
    B['ss'] = k.sb(pfx + 'ss', [128, 12], F32, st)
    B['ssname'] = pfx + 'ss'
    B['tp'] = [k.ps(pfx + f'tp{i}', [128, 512], BF16, st) for i in range(2)]
    B['tpname'] = [pfx + f'tp{i}' for i in range(2)]
    return B


def layer0(k, stop_after):
    nc, P, I, S = k.nc, k.P, k.I, k.S
    with ExitStack() as st:
        sb = lambda n, s, d: k.sb(n, s, d, st)
        lp = {}
        for nm, shp in (('lru_cw', [128, 6, 4]), ('lru_cb', [128, 6]), ('lru_ba', [128, 2, 6]), ('lru_bx', [128, 2, 6]),
                        ('lru_lam', [128, 2, 6]), ('pool_scale', [128, 2]), ('pool_invw', [128, 2]), ('pool_corr', [128, 2, 2, 16])):
            lp[nm] = sb('p_' + nm, shp, F32)
            P.dma('sp', lp[nm][:], I[nm], [], ['p_' + nm], 'p_' + nm)
        for nm, shp in (('lru_bdA', [128, 2, 6, 128]), ('lru_bdX', [128, 2, 6, 128]), ('pool_bd', [128, 2, 128])):
            lp[nm] = sb('p_' + nm, shp, BF16)
            P.dma('pool', lp[nm][:], I[nm], [], ['p_' + nm], 'p_' + nm)
        cl = sb('p_cl', [128, 2, 2, 6], F32)
        tmp = sb('p_cltmp', [128, 2, 6], F32)
        P.op('act', lambda e: e.activation(out=tmp[:], in_=lp['lru_lam'][:], func=AF.Exp, scale=-1.0), ['p_lru_lam'], ['p_cltmp'])
        P.op('act', lambda e: e.activation(out=tmp[:], in_=tmp[:], func=AF.Ln, bias=k.cst[:, 1:2], scale=1.0), ['p_cltmp', 'cst'], ['p_cltmp'])
        P.op('dve', lambda e: e.tensor_scalar(out=cl[:, 0, :, :], in0=tmp[:], scalar1=-8.0, scalar2=None, op0=ALU.mult), ['p_cltmp'], ['p_cl'])
        P.op('dve', lambda e: e.tensor_scalar(out=cl[:, 1, :, :], in0=tmp[:], scalar1=-16.0, scalar2=None, op0=ALU.mult), ['p_cltmp'], ['p_cl'])
        lp['cl'] = cl
        stt = sb('p_state', [128, 2, 6], F32)
        P.op('dve', lambda e: e.memset(stt[:], 0.0), [], ['p_state'])
        lp['state'] = stt
        k.lp = lp
        w_in = sb('w_in0', [128, 8, 1792], BF16)
        load_w_bf16(k, 'w_in0', w_in, I['ab_w_in'], 8, 1792, colblk=1792)
        w_out = sb('w_out0', [128, 8, D], BF16)
        k.w_in0, k.w_out0 = w_in, w_out

        for s_, nm, TT, xsrc in ((1, 'c', LC, I['ctx']), (0, 'l', k.T, I['x'])):
            seg = K()
            seg.nm, seg.T, seg.x, seg.set = nm, TT, xsrc, s_
            seg.z, seg.xa, seg.hf, seg.x05, seg.x1 = S['z_' + nm], S['xa_' + nm], S['hf_' + nm], S['x05_' + nm], S['x1_' + nm]
            seg.tiles = tiles_of(TT)
            with ExitStack() as st2:
                fold_w_bf16(k, st2, 'w_out0', w_out, I['ab_w_out'], 8, k.gbc_d[0, s_, 0])
            P.barrier()
            l0_phaseA(k, seg)
            P.barrier()
            if stop_after == 'l0A' and nm == 'l':
                return
            l0_phaseB(k, seg)
            P.barrier()
            if stop_after == 'l0B' and nm == 'l':
                return
            l0_phaseC(k, seg)
            P.barrier()
            if stop_after == 'l0C' and nm == 'l':
                return
    for s_, nm, TT in ((1, 'c', LC), (0, 'l', k.T)):
        ffn_phase(k, 0, s_, S['x05_' + nm], S['x1_' + nm], TT, final=False)
        P.barrier()


def l0_phaseA(k, seg):
    nc, P = k.nc, k.P
    with ExitStack() as st:
        sb = lambda n, s, d: k.sb(n, s, d, st)
        ps = lambda n, s, d: k.ps(n, s, d, st)
        B = mod_bufs(k, st, 'A_')
        hT = sb('A_hT', [128, 8, 512], BF16)
        zt = [sb(f'A_zt{i}', [128, 14, 512], F32) for i in range(2)]
        zp = [ps(f'A_zp{i}', [128, 512], F32) for i in range(4)]
        zv = seg.z.rearrange("(c p) t -> p c t", p=128)
        P.dma('sp', zv[:, :, 0:PADZ], k.zero[:, 0:14 * PADZ].rearrange("p (c t) -> p c t", c=14), ['zero'], [], 'A_zpad')
        P.dma('sp', zv[:, :, PADZ + seg.T:PADZ + seg.T + PADZ], k.zero[:, 0:14 * PADZ].rearrange("p (c t) -> p c t", c=14), ['zero'], [], 'A_zpad')
        for ti, (s, n) in enumerate(seg.tiles):
            modulate_tile(k, B, seg.x[s:s + n, :], n, 0, seg.set, 0, hT, 'A_hT')
            Z = zt[ti % 2]
            Zn = f'A_zt{ti % 2}'
            for mc in range(14):
                zpp = zp[mc % 4]
                for kc in range(8):
                    P.op('pe', lambda e, mc=mc, kc=kc, zpp=zpp: e.matmul(zpp[:, 0:n], lhsT=k.w_in0[:, kc, mc * 128:(mc + 1) * 128], rhs=hT[:, kc, 0:n],
                                                                         start=(kc == 0), stop=(kc == 7)),
                         [f'w_in0.{kc}', f'A_hT.{kc}'], [f'A_zp{mc % 4}'])
                eng = 'act' if mc % 2 == 0 else 'dve'
                if eng == 'act':
                    P.op('act', lambda e, mc=mc, zpp=zpp: e.copy(out=Z[:, mc, 0:n], in_=zpp[:, 0:n]), [f'A_zp{mc % 4}'], [f'{Zn}.{mc}'])
                else:
                    P.op('dve', lambda e, mc=mc, zpp=zpp: e.tensor_copy(out=Z[:, mc, 0:n], in_=zpp[:, 0:n]), [f'A_zp{mc % 4}'], [f'{Zn}.{mc}'])
            P.dma('sp', zv[:, :, PADZ + s:PADZ + s + n], Z[:, :, 0:n], keys(Zn, 14), [], Zn)


def lru_coeffs(k, C, d, n, xa, xab):
    P, lp = k.P, k.lp
    for c in range(6):
        pr, pi = C['pg'][(2 * c) % 4], C['pg'][(2 * c + 1) % 4]
        prn, pin = C['pgname'][(2 * c) % 4], C['pgname'][(2 * c + 1) % 4]
        P.op('pe', lambda e, c=c, pr=pr: e.matmul(pr[:, 0:n], lhsT=lp['lru_bdA'][:, d, c, :], rhs=xab[:, c, 0:n], start=True, stop=True),
             ['p_lru_bdA', f"{C['xabname']}.{c}"], [prn])
        P.op('pe', lambda e, c=c, pi=pi: e.matmul(pi[:, 0:n], lhsT=lp['lru_bdX'][:, d, c, :], rhs=xab[:, c, 0:n], start=True, stop=True),
             ['p_lru_bdX', f"{C['xabname']}.{c}"], [pin])
        P.op('act', lambda e, c=c, pr=pr: e.activation(out=C['r'][:, c, 0:n], in_=pr[:, 0:n], func=AF.Sigmoid, bias=lp['lru_ba'][:, d, c:c + 1], scale=1.0),
             [prn, 'p_lru_ba'], [f"{C['pfx']}r.{c}"])
        P.op('act', lambda e, c=c, pi=pi: e.activation(out=C['ig'][:, c, 0:n], in_=pi[:, 0:n], func=AF.Sigmoid, bias=lp['lru_bx'][:, d, c:c + 1], scale=1.0),
             [pin, 'p_lru_bx'], [f"{C['pfx']}ig.{c}"])
    for c in range(6):
        P.op('act', lambda e, c=c: e.activation(out=C['a'][:, c, 0:n], in_=C['r'][:, c, 0:n], func=AF.Exp, scale=lp['cl'][:, 0, d, c:c + 1]),
             [f"{C['pfx']}r.{c}", 'p_cl'], [f"{C['pfx']}a.{c}"])
        P.op('act', lambda e, c=c: e.activation(out=C['r'][:, c, 0:n], in_=C['r'][:, c, 0:n], func=AF.Exp, scale=lp['cl'][:, 1, d, c:c + 1]),
             [f"{C['pfx']}r.{c}", 'p_cl'], [f"{C['pfx']}r.{c}"])
    for c in range(6):
        P.op('act', lambda e, c=c: e.activation(out=C['r'][:, c, 0:n], in_=C['r'][:, c, 0:n], func=AF.Sqrt, bias=k.cst[:, 1:2], scale=-1.0),
             [f"{C['pfx']}r.{c}", 'cst'], [f"{C['pfx']}r.{c}"])
        P.op('dve', lambda e, c=c: e.tensor_tensor(out=C['ig'][:, c, 0:n], in0=C['ig'][:, c, 0:n], in1=C['r'][:, c, 0:n], op=ALU.mult),
             [f"{C['pfx']}r.{c}", f"{C['pfx']}ig.{c}"], [f"{C['pfx']}ig.{c}"])
        P.op('pool', lambda e, c=c: e.tensor_tensor(out=C['ig'][:, c, 0:n], in0=C['ig'][:, c, 0:n], in1=xa[:, c, 0:n], op=ALU.mult),
             [f"{C['pfx']}ig.{c}", f"{C['xaname']}.{c}"], [f"{C['pfx']}ig.{c}"])


def coeff_bufs(k, st, pfx):
    C = {'pfx': pfx}
    for nm in ('r', 'ig', 'a'):
        C[nm] = k.sb(pfx + nm, [128, 6, 512], F32, st)
    C['pg'] = [k.ps(pfx + f'pg{i}', [128, 512], F32, st) for i in range(4)]
    C['pgname'] = [pfx + f'pg{i}' for i in range(4)]
    return C


def l0_phaseB(k, seg):
    P, lp = k.P, k.lp
    with ExitStack() as st:
        sb = lambda n, s, d: k.sb(n, s, d, st)
        zin = sb('B_zin', [128, 6, 515], F32)
        xa = sb('B_xa', [128, 6, 512], F32)
        xab = sb('B_xab', [128, 6, 512], BF16)
        hf = sb('B_hf', [128, 6, 512], F32)
        C = coeff_bufs(k, st, 'B_')
        C['xabname'], C['xaname'] = 'B_xab', 'B_xa'
        zv = seg.z[0:768, :].rearrange("(c p) t -> p c t", p=128)
        xav = seg.xa.rearrange("(c p) t -> p c t", p=128)
        hfv = seg.hf.rearrange("(c p) t -> p c t", p=128)
        if seg.nm == 'c':
            P.op('dve', lambda e: e.memset(lp['state'][:], 0.0), [], ['p_state'])
        for ti, (s, n) in enumerate(seg.tiles):
            P.dma('sp', zin[:, :, 0:n + 3], zv[:, :, PADZ + s - 2:PADZ + s + n + 1], [], keys('B_zin', 6), 'B_zin')
            for c in range(6):
                P.op('act', lambda e, c=c: e.activation(out=xa[:, c, 0:n], in_=zin[:, c, 0:n], func=AF.Identity,
                                                        scale=lp['lru_cw'][:, c, 0:1], bias=lp['lru_cb'][:, c:c + 1]),
                     [f'B_zin.{c}', 'p_lru_cw', 'p_lru_cb'], [f'B_xa.{c}'])
                for t in range(1, 4):
                    P.op('dve', lambda e, c=c, t=t: e.scalar_tensor_tensor(out=xa[:, c, 0:n], in0=zin[:, c, t:t + n], scalar=lp['lru_cw'][:, c, t:t + 1],
                                                                           in1=xa[:, c, 0:n], op0=ALU.mult, op1=ALU.add),
                         [f'B_zin.{c}', f'B_xa.{c}'], [f'B_xa.{c}'])
                P.op('pool', lambda e, c=c: e.tensor_copy(out=xab[:, c, 0:n], in_=xa[:, c, 0:n]), [f'B_xa.{c}'], [f'B_xab.{c}'])
            P.dma('sp', xav[:, :, s:s + n], xa[:, :, 0:n], keys('B_xa', 6), [], 'B_xa')
            lru_coeffs(k, C, 0, n, xa, xab)
            for c in range(6):
                P.op('dve', lambda e, c=c: e.tensor_tensor_scan(out=hf[:, c, 0:n], data0=C['a'][:, c, 0:n], data1=C['ig'][:, c, 0:n],
                                                                initial=lp['state'][:, 0, c:c + 1], op0=ALU.mult, op1=ALU.add),
                     [f'B_a.{c}', f'B_ig.{c}', 'p_state'], [f'B_hf.{c}'])
                P.op('dve', lambda e, c=c: e.tensor_copy(out=lp['state'][:, 0, c:c + 1], in_=hf[:, c, n - 1:n]), [f'B_hf.{c}'], ['p_state'])
            P.dma('sp', hfv[:, :, s:s + n], hf[:, :, 0:n], keys('B_hf', 6), [], 'B_hf')


def l0_phaseC(k, seg):
    P, lp, I = k.P, k.lp, k.I
    with ExitStack() as st:
        sb = lambda n, s, d: k.sb(n, s, d, st)
        ps = lambda n, s, d: k.ps(n, s, d, st)
        xa = sb('C_xa', [128, 6, 512], F32)
        xab = sb('C_xab', [128, 6, 512], BF16)
        hb = sb('C_hb', [128, 6, 512], F32)
        hf = sb('C_hf', [128, 6, 512], F32)
        ga = sb('C_ga', [128, 6, 512], F32)
        yT = sb('C_yT', [128, 8, 512], BF16)
        zb = sb('C_zb', [128, 2, 527], F32)
        p2 = sb('C_p2', [128, 527], F32)
        p4 = sb('C_p4', [128, 527], F32)
        p8 = sb('C_p8', [128, 527], F32)
        Ssum = sb('C_S', [128, 512], F32)
        dd = sb('C_dd', [128, 2, 512], BF16)
        xt = sb('C_xt', [128, 4, D], F32)
        xo = sb('C_xo', [128, 4, D], F32)
        C = coeff_bufs(k, st, 'C_')
        C['xabname'], C['xaname'] = 'C_xab', 'C_xa'
        po = [ps(f'C_po{i}', [128, 512], F32) for i in range(4)]
        zg = seg.z[768:1536, :].rearrange("(c p) t -> p c t", p=128)
        zbv = seg.z[1536:1792, :].rearrange("(c p) t -> p c t", p=128)
        xav = seg.xa.rearrange("(c p) t -> p c t", p=128)
        hfv = seg.hf.rearrange("(c p) t -> p c t", p=128)
        if seg.nm == 'c':
            P.op('dve', lambda e: e.memset(lp['state'][:, 1, :], 0.0), [], ['p_state'])
        nt_ = len(seg.tiles)
        for ti in range(nt_ - 1, -1, -1):
            s, n = seg.tiles[ti]
            ng = n // 128
            P.dma('sp', xa[:, :, 0:n], xav[:, :, s:s + n], [], keys('C_xa', 6), 'C_xa')
            P.dma('sp', hf[:, :, 0:n], hfv[:, :, s:s + n], [], keys('C_hf', 6), 'C_hf')
            P.dma('sp', ga[:, :, 0:n], zg[:, :, PADZ + s:PADZ + s + n], [], keys('C_ga', 6), 'C_ga')
            P.dma('sp', zb[:, :, 0:n + 15], zbv[:, :, PADZ + s - 8:PADZ + s + n + 7], [], keys('C_zb', 2), 'C_zb')
            P.dma('sp', xt[:, 0:ng, :], seg.x[s:s + n, :].rearrange("(g p) f -> p g f", p=128), [], ['C_xt'], 'C_xt')
            for c in range(6):
                P.op('pool', lambda e, c=c: e.tensor_copy(out=xab[:, c, 0:n], in_=xa[:, c, 0:n]), [f'C_xa.{c}'], [f'C_xab.{c}'])
            lru_coeffs(k, C, 1, n, xa, xab)
            for c in range(6):
                P.op('dve', lambda e, c=c: e.tensor_tensor_scan(out=rev(hb[:, c, 0:n]), data0=rev(C['a'][:, c, 0:n]), data1=rev(C['ig'][:, c, 0:n]),
                                                                initial=lp['state'][:, 1, c:c + 1], op0=ALU.mult, op1=ALU.add),
                     [f'C_a.{c}', f'C_ig.{c}', 'p_state'], [f'C_hb.{c}'])
                P.op('dve', lambda e, c=c: e.tensor_copy(out=lp['state'][:, 1, c:c + 1], in_=hb[:, c, 0:1]), [f'C_hb.{c}'], ['p_state'])
                P.op('act', lambda e, c=c: e.activation(out=ga[:, c, 0:n], in_=ga[:, c, 0:n], func=AF.Gelu_apprx_tanh), [f'C_ga.{c}'], [f'C_ga.{c}'])
                P.op('pool', lambda e, c=c: e.tensor_tensor(out=hb[:, c, 0:n], in0=hb[:, c, 0:n], in1=hf[:, c, 0:n], op=ALU.add),
                     [f'C_hb.{c}', f'C_hf.{c}'], [f'C_hb.{c}'])
                P.op('dve', lambda e, c=c: e.tensor_tensor(out=yT[:, c, 0:n], in0=hb[:, c, 0:n], in1=ga[:, c, 0:n], op=ALU.mult),
                     [f'C_hb.{c}', f'C_ga.{c}'], [f'C_yT.{c}'])
            W = n + 15
            for ch in range(2):
                zc = zb[:, ch, :]
                P.op('dve', lambda e, zc=zc: e.tensor_tensor(out=p2[:, 0:W - 1], in0=zc[:, 0:W - 1], in1=zc[:, 1:W], op=ALU.add),
                     [f'C_zb.{ch}'], ['C_p2'])
                if ch == 0:
                    P.op('dve', lambda e: e.tensor_copy(out=Ssum[0:64, 0:n], in_=p2[0:64, 7:7 + n]), ['C_p2'], ['C_S'])
                    P.op('dve', lambda e: e.tensor_tensor(out=Ssum[64:128, 0:n], in0=p2[64:128, 6:6 + n], in1=p2[64:128, 8:8 + n], op=ALU.add),
                         ['C_p2'], ['C_S'])
                else:
                    P.op('dve', lambda e: e.tensor_tensor(out=p4[:, 0:W - 3], in0=p2[:, 0:W - 3], in1=p2[:, 2:W - 1], op=ALU.add), ['C_p2'], ['C_p4'])
                    P.op('dve', lambda e: e.tensor_tensor(out=Ssum[0:64, 0:n], in0=p4[0:64, 4:4 + n], in1=p4[0:64, 8:8 + n], op=ALU.add),
                         ['C_p4'], ['C_S'])
                    P.op('dve', lambda e: e.tensor_tensor(out=p8[64:128, 0:W - 7], in0=p4[64:128, 0:W - 7], in1=p4[64:128, 4:W - 3], op=ALU.add),
                         ['C_p4'], ['C_p8'])
                    P.op('dve', lambda e: e.tensor_tensor(out=Ssum[64:128, 0:n], in0=p8[64:128, 0:n], in1=p8[64:128, 8:8 + n], op=ALU.add),
                         ['C_p8'], ['C_S'])
                if ti == 0:
                    P.op('dve', lambda e, ch=ch: e.tensor_tensor(out=Ssum[:, 0:16], in0=Ssum[:, 0:16], in1=lp['pool_corr'][:, ch, 0, :], op=ALU.mult),
                         ['C_S', 'p_pool_corr'], ['C_S'])
                if ti == nt_ - 1:
                    P.op('dve', lambda e, ch=ch: e.tensor_tensor(out=Ssum[:, n - 16:n], in0=Ssum[:, n - 16:n], in1=lp['pool_corr'][:, ch, 1, :], op=ALU.mult),
                         ['C_S', 'p_pool_corr'], ['C_S'])
                P.op('dve', lambda e, ch=ch, zc=zc: e.scalar_tensor_tensor(out=dd[:, ch, 0:n], in0=Ssum[:, 0:n], scalar=lp['pool_invw'][:, ch:ch + 1],
                                                                          in1=zc[:, 8:8 + n], op0=ALU.mult, op1=ALU.subtract),
                     ['C_S', f'C_zb.{ch}', 'p_pool_invw'], [f'C_dd.{ch}'])
                pp = po[ch]
                P.op('pe', lambda e, ch=ch, pp=pp: e.matmul(pp[:, 0:n], lhsT=lp['pool_bd'][:, ch, :], rhs=dd[:, ch, 0:n], start=True, stop=True),
                     ['p_pool_bd', f'C_dd.{ch}'], [f'C_po{ch}'])
                P.op('act', lambda e, ch=ch, pp=pp: e.activation(out=yT[:, 6 + ch, 0:n], in_=pp[:, 0:n], func=AF.Identity, scale=lp['pool_scale'][:, ch:ch + 1], bias=k.cst[:, 2:3]),
                     [f'C_po{ch}', 'p_pool_scale', 'cst'], [f'C_yT.{6 + ch}'])
            for g in range(ng):
                for nt in range(2):
                    pp = po[(2 * g + nt) % 4]
                    ppn = f'C_po{(2 * g + nt) % 4}'
                    for kc in range(8):
                        P.op('pe', lambda e, g=g, nt=nt, kc=kc, pp=pp: e.matmul(pp[:, :], lhsT=yT[:, kc, g * 128:(g + 1) * 128], rhs=k.w_out0[:, kc, nt * 512:(nt + 1) * 512],
                                                                               start=(kc == 0), stop=(kc == 7)),
                             [f'C_yT.{kc}', f'w_out0.{kc}'], [ppn])
                    P.op('dve', lambda e, g=g, nt=nt, pp=pp: e.tensor_tensor(out=xo[:, g, nt * 512:(nt + 1) * 512], in0=pp[:, :], in1=xt[:, g, nt * 512:(nt + 1) * 512], op=ALU.add),
                         [ppn, 'C_xt'], [f'C_xo.{g}'])
            P.dma('sp', seg.x05[1 + s:1 + s + n, :].rearrange("(g p) f -> p g f", p=128), xo[:, 0:ng, :], keys('C_xo', 4)[0:ng], [], 'C_xo')


def ffn_phase(k, l, s_, src, dst, TT, final):
    nc, P, I = k.nc, k.P, k.I
    tiles = tiles_of(TT)
    with ExitStack() as st:
        sb = lambda n, s, d: k.sb(n, s, d, st)
        ps = lambda n, s, d: k.ps(n, s, d, st)
        w_up = sb('F_wup', [128, 8, 2 * DFF], BF16)
        load_w_bf16(k, 'F_wup', w_up, I['ffn_w_up'][l], 8, 2 * DFF)
        w_dn = sb('F_wdn', [128, NFC, D], BF16)
        with ExitStack() as st2:
            fold_w_bf16(k, st2, 'F_wdn', w_dn, I['ffn_w_down'][l], NFC, k.gbc_d[l, s_, 1])
            P.barrier()
        cw = sb('F_cw', [128, 2 * NFC, 3], F32)
        cb = sb('F_cb', [128, 2 * NFC], F32)
        P.dma('sp', cw[:], I['ffn_cw'][:, l, :, :], [], ['F_cw'], 'F_cw')
        P.dma('sp', cb[:], I['ffn_cb'][:, l, :], [], ['F_cb'], 'F_cb')
        B = mod_bufs(k, st, 'F_')
        hT = sb('F_hT', [128, 8, 512], BF16)
        gT = sb('F_gT', [128, NFC, 512], BF16)
        prevu = sb('F_prevu', [128, 2 * NFC, 2], F32)
        P.op('dve', lambda e: e.memset(prevu[:], 0.0), [], ['F_prevu'])
        ucat = [sb(f'F_ucat{i}', [128, 516], F32) for i in range(2)]
        acc = [sb(f'F_acc{i}', [128, 512], F32) for i in range(2)]
        xs = sb('F_xs', [128, 1, D], F32)
        xo = xs
        pu = [ps(f'F_pu{i}', [128, 512], F32) for i in range(2)]
        pd = [ps(f'F_pd{i}', [128, 512], F32) for i in range(2)]
        if final:
            fg = sb('F_fg', [128, D], F32)
            P.dma('sp', fg[:], I['final_g_bc'], [], ['F_fg'], 'F_fg')
            fss = sb('F_fss', [128, 12], F32)
            fjunk = B['junk']

        def conv_gate(n, zero_u):
            for c in range(NFC):
                for vi in range(2):
                    cc = c + vi * NFC
                    U, Un = ucat[vi], f'F_ucat{vi}'
                    A_, An = acc[vi], f'F_acc{vi}'
                    P.op('pool', lambda e, U=U, cc=cc: e.tensor_copy(out=U[:, 0:2], in_=prevu[:, cc, :]), ['F_prevu'], [Un])
                    if zero_u:
                        P.op('pool', lambda e, U=U: e.memset(U[:, 2:2 + n], 0.0), [], [Un])
                    else:
                        pp = pu[vi]
                        for kc in range(8):
                            P.op('pe', lambda e, kc=kc, cc=cc, pp=pp: e.matmul(pp[:, 0:n], lhsT=w_up[:, kc, cc * 128:(cc + 1) * 128], rhs=hT[:, kc, 0:n],
                                                                              start=(kc == 0), stop=(kc == 7)),
                                 [f'F_wup.{kc}', f'F_hT.{kc}'], [f'F_pu{vi}'])
                        P.op('act', lambda e, U=U, pp=pp: e.copy(out=U[:, 2:2 + n], in_=pp[:, 0:n]), [f'F_pu{vi}'], [Un])
                    P.op('pool', lambda e, U=U, cc=cc: e.tensor_copy(out=prevu[:, cc, :], in_=U[:, n:n + 2]), [Un], ['F_prevu'])
                    P.op('act', lambda e, U=U, A_=A_, cc=cc: e.activation(out=A_[:, 0:n], in_=U[:, 1:1 + n], func=AF.Identity,
                                                                         scale=cw[:, cc, 1:2], bias=cb[:, cc:cc + 1]),
                         [Un, 'F_cw', 'F_cb'], [An])
                    P.op('dve', lambda e, U=U, A_=A_, cc=cc: e.scalar_tensor_tensor(out=A_[:, 0:n], in0=U[:, 0:n], scalar=cw[:, cc, 0:1], in1=A_[:, 0:n],
                                                                                   op0=ALU.mult, op1=ALU.add), [Un, An], [An])
                    P.op('dve', lambda e, U=U, A_=A_, cc=cc: e.scalar_tensor_tensor(out=A_[:, 0:n], in0=U[:, 2:2 + n], scalar=cw[:, cc, 2:3], in1=A_[:, 0:n],
                                                                                   op0=ALU.mult, op1=ALU.add), [Un, An], [An])
                P.op('act', lambda e: e.activation(out=acc[1][:, 0:n], in_=acc[1][:, 0:n], func=AF.Silu), ['F_acc1'], ['F_acc1'])
                P.op('pool', lambda e, c=c: e.tensor_tensor(out=gT[:, c, 0:n], in0=acc[0][:, 0:n], in1=acc[1][:, 0:n], op=ALU.mult),
                     ['F_acc0', 'F_acc1'], [f'F_gT.{c}'])

        def down_res(tok0, n, nrows_last=128):
            ng = n // 128
            nr = lambda g: (nrows_last if g == ng - 1 else 128)
            for g_ in range(ng):
                g = 0
                r0 = 1 + tok0 + g_ * 128
                P.dma('sp', xs[0:nr(g_), g, :], src[r0:r0 + nr(g_), :], [], [f'F_xs.{g}'], f'F_xs{g}')
                for nt in range(2):
                    pp = pd[nt]
                    for kc in range(NFC):
                        P.op('pe', lambda e, g_=g_, nt=nt, kc=kc, pp=pp: e.matmul(pp[:, :], lhsT=gT[:, kc, g_ * 128:(g_ + 1) * 128], rhs=w_dn[:, kc, nt * 512:(nt + 1) * 512],
                                                                               start=(kc == 0), stop=(kc == NFC - 1)),
                             [f'F_gT.{kc}', f'F_wdn.{kc}'], [f'F_pd{nt}'])
                    P.op('dve', lambda e, g=g, g_=g_, nt=nt, pp=pp: e.tensor_tensor(out=xo[0:nr(g_), g, nt * 512:(nt + 1) * 512], in0=pp[0:nr(g_), :],
                                                                            in1=xs[0:nr(g_), g, nt * 512:(nt + 1) * 512], op=ALU.add),
                         [f'F_pd{nt}', f'F_xs.{g}'], [f'F_xs.{g}'])
                if final:
                    P.op('dve', lambda e: e.memset(fss[:, 0:1], 0.0), [], ['F_fss'])
                    P.op('act', lambda e, g=g: e.activation(out=fjunk[:], in_=xo[:, g, :], func=AF.Square, accum_out=fss[:, 0:1]),
                         [f'F_xs.{g}'], ['F_fss', 'F_junk'])
                    P.op('act', lambda e: e.activation(out=fss[:, 1:2], in_=fss[:, 0:1], func=AF.Sqrt, bias=k.cst[:, 0:1], scale=1.0 / D),
                         ['F_fss', 'cst'], ['F_fss'])
                    P.op('dve', lambda e: e.reciprocal(out=fss[:, 2:3], in_=fss[:, 1:2]), ['F_fss'], ['F_fss'])
                    P.op('dve', lambda e, g=g: e.scalar_tensor_tensor(out=xo[:, g, :], in0=xo[:, g, :], scalar=fss[:, 2:3], in1=fg[:],
                                                                     op0=ALU.mult, op1=ALU.mult), [f'F_xs.{g}', 'F_fss', 'F_fg'], [f'F_xs.{g}'])
                t0 = tok0 + g_ * 128
                if final:
                    lo = max(t0, 0)
                    hi = min(t0 + nr(g_), TT)
                    if hi > lo:
                        P.dma('sp', k.out[lo:hi, :], xo[lo - t0:hi - t0, g, :], [f'F_xs.{g}'], [], f'F_xs{g}')
                else:
                    P.dma('sp', dst[1 + t0:1 + t0 + nr(g_), :], xo[0:nr(g_), g, :], [f'F_xs.{g}'], [], f'F_xs{g}')

        for ti, (s, n) in enumerate(tiles):
            modulate_tile(k, B, src[1 + s:1 + s + n, :], n, l, s_, 2, hT, 'F_hT')
            conv_gate(n, False)
            down_res(s - 1, n)
        conv_gate(128, True)
        down_res(TT - 1, 128, nrows_last=1)


def layer1(k, stop_after):
    nc, P, I, S = k.nc, k.P, k.I, k.S
    T = k.T
    TK = T + LC
    NKB = TK // 128
    yc_d = nc.dram_tensor('yc_d', [768, T], BF16, kind="Internal").ap()
    with ExitStack() as st:
        sb = lambda n, s, d: k.sb(n, s, d, st)
        ps = lambda n, s, d: k.ps(n, s, d, st)
        w_in = sb('w_in1', [128, 8, 2816], BF16)
        load_w_bf16(k, 'w_in1', w_in, I['cd_w_in'], 8, 2816, colblk=1408)
        w_sw = sb('w_sw1', [128, 8, 1536], BF16)
        load_w_bf16(k, 'w_sw1', w_sw, I['cd_w_sw'], 8, 1536, colblk=1536)
        B = mod_bufs(k, st, 'E_')
        hT = sb('E_hT', [128, 8, 512], BF16)
        qk = sb('E_qk', [128, 12, 512], BF16)
        vt = sb('E_vt', [128, 4, 768], BF16)
        gl = sb('E_gl', [128, 2, 512], F32)
        rc = sb('E_rc', [128, 512], F32)
        rs = sb('E_rs', [128, 512], F32)
        t1 = sb('E_t1', [128, 512], F32)
        t2 = sb('E_t2', [128, 512], F32)
        sg = sb('E_sg', [128, 512], F32)
        pa = [ps(f'E_pa{i}', [128, 512], F32) for i in range(2)]
        pb = [ps(f'E_pb{i}', [128, 512], F32) for i in range(2)]
        glv = S['gl'].rearrange("(c p) t -> p c t", p=128)
        P.dma('sp', glv[:, :, 0:PADZ], k.zero[:, 0:2 * PADZ].rearrange("p (c t) -> p c t", c=2), ['zero'], [], 'E_glpad')
        P.dma('sp', glv[:, :, PADZ + T:PADZ + T + PADZ], k.zero[:, 0:2 * PADZ].rearrange("p (c t) -> p c t", c=2), ['zero'], [], 'E_glpad')
        qTv = S['qT'].rearrange("(c p) t -> p c t", p=128)
        kTv = S['kT'].rearrange("(c p) t -> p c t", p=128)

        def proj_plain(cols0, nch, dst, dstname, d0, n):
            for c in range(nch):
                pp = pa[c % 2]
                for kc in range(8):
                    P.op('pe', lambda e, c=c, kc=kc, pp=pp: e.matmul(pp[:, 0:n], lhsT=w_in[:, kc, cols0 + c * 128:cols0 + (c + 1) * 128], rhs=hT[:, kc, 0:n],
                                                                    start=(kc == 0), stop=(kc == 7)), [f'w_in1.{kc}', f'E_hT.{kc}'], [f'E_pa{c % 2}'])
                P.op('act', lambda e, c=c, pp=pp: e.copy(out=dst[:, d0 + c, 0:n], in_=pp[:, 0:n]), [f'E_pa{c % 2}'], [f'{dstname}.{d0 + c}'])

        def proj_v(n):
            ng = n // 128
            for g in range(ng):
                for (c0, cn, pp, ppn) in ((0, 512, pa[g % 2], f'E_pa{g % 2}'), (512, 256, pb[g % 2], f'E_pb{g % 2}')):
                    for kc in range(8):
                        P.op('pe', lambda e, g=g, kc=kc, pp=pp, c0=c0, cn=cn: e.matmul(pp[:, 0:cn], lhsT=hT[:, kc, g * 128:(g + 1) * 128],
                                                                                       rhs=w_in[:, kc, 1536 + c0:1536 + c0 + cn], start=(kc == 0), stop=(kc == 7)),
                             [f'w_in1.{kc}', f'E_hT.{kc}'], [ppn])
                    P.op('dve', lambda e, g=g, pp=pp, c0=c0, cn=cn: e.tensor_copy(out=vt[:, g, c0:c0 + cn], in_=pp[:, 0:cn]), [ppn], [f'E_vt.{g}'])

        n = LC
        modulate_tile(k, B, S['x1_c'][1:1 + LC, :], n, 1, 1, 0, hT, 'E_hT')
        proj_plain(768, 6, qk, 'E_qk', 6, n)
        P.dma('sp', kTv[:, :, T:T + n], qk[:, 6:12, 0:n], keys('E_qk', 12)[6:12], [], 'E_qk')
        proj_v(n)
        P.dma('sp', S['v'][T:T + n, :].rearrange("(g p) f -> p g f", p=128), vt[:, 0:n // 128, :], keys('E_vt', 4), [], 'E_vt')
        for ti, (s, n) in enumerate(tiles_of(T)):
            modulate_tile(k, B, S['x1_l'][1 + s:1 + s + n, :], n, 1, 0, 0, hT, 'E_hT')
            P.dma('sp', rc[:, 0:n], I['rope_c'][:, s:s + n], [], ['E_rc'], 'E_rc')
            P.dma('sp', rs[:, 0:n], I['rope_s'][:, s:s + n], [], ['E_rs'], 'E_rs')
            for c in range(12):
                pp, pq = pa[c % 2], pb[c % 2]
                for kc in range(8):
                    P.op('pe', lambda e, c=c, kc=kc, pp=pp: e.matmul(pp[:, 0:n], lhsT=w_in[:, kc, c * 128:(c + 1) * 128], rhs=hT[:, kc, 0:n],
                                                                    start=(kc == 0), stop=(kc == 7)), [f'w_in1.{kc}', f'E_hT.{kc}'], [f'E_pa{c % 2}'])
                for kc in range(8):
                    P.op('pe', lambda e, c=c, kc=kc, pq=pq: e.matmul(pq[:, 0:n], lhsT=w_sw[:, kc, c * 128:(c + 1) * 128], rhs=hT[:, kc, 0:n],
                                                                    start=(kc == 0), stop=(kc == 7)), [f'w_sw1.{kc}', f'E_hT.{kc}'], [f'E_pb{c % 2}'])
                P.op('dve', lambda e, pp=pp: e.tensor_tensor(out=t1[:, 0:n], in0=pp[:, 0:n], in1=rc[:, 0:n], op=ALU.mult), [f'E_pa{c % 2}', 'E_rc'], ['E_t1'])
                P.op('dve', lambda e, pq=pq: e.tensor_tensor(out=t2[:, 0:n], in0=pq[:, 0:n], in1=rs[:, 0:n], op=ALU.mult), [f'E_pb{c % 2}', 'E_rs'], ['E_t2'])
                P.op('pool', lambda e, c=c: e.tensor_tensor(out=qk[:, c, 0:n], in0=t1[:, 0:n], in1=t2[:, 0:n], op=ALU.add), ['E_t1', 'E_t2'], [f'E_qk.{c}'])
            P.dma('sp', qTv[:, :, s:s + n], qk[:, 0:6, 0:n], keys('E_qk', 12)[0:6], [], 'E_qk')
            P.dma('sp', kTv[:, :, s:s + n], qk[:, 6:12, 0:n], keys('E_qk', 12)[6:12], [], 'E_qk')
            proj_v(n)
            P.dma('sp', S['v'][s:s + n, :].rearrange("(g p) f -> p g f", p=128), vt[:, 0:n // 128, :], keys('E_vt', 4), [], 'E_vt')
            for c in range(2):
                pp, pq = pa[c % 2], pb[c % 2]
                for (pz, pzn, cols) in ((pp, f'E_pa{c % 2}', 2304 + c * 128), (pq, f'E_pb{c % 2}', 2304 + 256 + c * 128)):
                    for kc in range(8):
                        P.op('pe', lambda e, kc=kc, pz=pz, cols=cols: e.matmul(pz[:, 0:n], lhsT=w_in[:, kc, cols:cols + 128], rhs=hT[:, kc, 0:n],
                                                                              start=(kc == 0), stop=(kc == 7)), [f'w_in1.{kc}', f'E_hT.{kc}'], [pzn])
                P.op('act', lambda e, pq=pq: e.activation(out=sg[:, 0:n], in_=pq[:, 0:n], func=AF.Sigmoid), [f'E_pb{c % 2}'], ['E_sg'])
                P.op('dve', lambda e, c=c, pp=pp: e.tensor_tensor(out=gl[:, c, 0:n], in0=pp[:, 0:n], in1=sg[:, 0:n], op=ALU.mult), [f'E_pa{c % 2}', 'E_sg'], [f'E_gl.{c}'])
            P.dma('sp', glv[:, :, PADZ + s:PADZ + s + n], gl[:, :, 0:n], keys('E_gl', 2), [], 'E_gl')
    P.barrier()
    if stop_after == 'l1E':
        return

    with ExitStack() as st:
        sb = lambda n, s, d: k.sb(n, s, d, st)
        ps = lambda n, s, d: k.ps(n, s, d, st)
        dl = sb('G_dl', [1, 4, 64], F32)
        P.dma('sp', dl[:], I['diff_l'], [], ['G_dl'], 'G_dl')
        sm = sb('G_sm', [1, 8], F32)
        pr_ = sb('G_pr', [1, 2, 64], F32)
        P.op('dve', lambda e: e.tensor_tensor(out=pr_[:, 0, :], in0=dl[:, 0, :], in1=dl[:, 1, :], op=ALU.mult), ['G_dl'], ['G_pr'])
        P.op('dve', lambda e: e.tensor_tensor(out=pr_[:, 1, :], in0=dl[:, 2, :], in1=dl[:, 3, :], op=ALU.mult), ['G_dl'], ['G_pr'])
        P.op('dve', lambda e: e.reduce_sum(out=sm[:, 0:1], in_=pr_[:, 0, :], axis=AX.X), ['G_pr'], ['G_sm'])
        P.op('dve', lambda e: e.reduce_sum(out=sm[:, 1:2], in_=pr_[:, 1, :], axis=AX.X), ['G_pr'], ['G_sm'])
        P.op('act', lambda e: e.activation(out=sm[:, 2:4], in_=sm[:, 0:2], func=AF.Exp), ['G_sm'], ['G_sm'])
        P.op('dve', lambda e: e.tensor_tensor(out=sm[:, 4:5], in0=sm[:, 3:4], in1=sm[:, 2:3], op=ALU.subtract), ['G_sm'], ['G_sm'])
        P.op('dve', lambda e: e.tensor_scalar(out=sm[:, 5:6], in0=sm[:, 4:5], scalar1=-LAMBDA_INIT1, scalar2=None, op0=ALU.add), ['G_sm'], ['G_sm'])
        neglam = sb('G_neglam', [128, 2], F32)
        gsub = sb('G_gsub', [128, 2], F32)
        P.dma('sp', gsub[:, 0:1], I['subln_g'], [], ['G_gsub'], 'G_gsub')
        P.op('dve', lambda e: e.tensor_scalar(out=gsub[:, 1:2], in0=gsub[:, 0:1], scalar1=(1.0 - LAMBDA_INIT1), scalar2=None, op0=ALU.mult), ['G_gsub'], ['G_gsub'])
        pS = [ps(f'G_pS{i}', [128, 2, 512], F32) for i in range(2)]
        po = [ps(f'G_po{i}', [128, 512], F32) for i in range(2)]
        pl = ps('G_pl', [128, 2, 512], F32)
        P.op('pe', lambda e: e.matmul(pl[:, 0, 0:1], lhsT=k.ones_f[0:1, :], rhs=sm[0:1, 5:6], start=True, stop=True), ['ones_f', 'G_sm'], ['G_pl'])
        P.op('dve', lambda e: e.tensor_copy(out=neglam[:, 0:1], in_=pl[:, 0, 0:1]), ['G_pl'], ['G_neglam'])
        kh = [sb(f'G_kh{i}', [128, TK], BF16) for i in range(2)]
        P.op('pool', lambda e: e.memset(kh[0][64:128, :], 0.0), [], ['G_kh0'])
        P.op('pool', lambda e: e.memset(kh[1][0:64, :], 0.0), [], ['G_kh1'])
        vh = sb('G_vh', [128, NKB, 128], BF16)
        qh = sb('G_qh', [128, 512], BF16)
        pT = [sb(f'G_pT{i}', [128, 2, 512], BF16) for i in range(2)]
        accs = [sb(f'G_acc{i}', [128, 2, 512], F32) for i in range(2)]
        rl = sb('G_rl', [128, 2, 512], F32)
        o1 = sb('G_o1', [128, 512], F32)
        o2 = sb('G_o2', [128, 512], F32)
        sq = sb('G_sq', [128, 512], F32)
        ych = sb('G_ych', [128, 512], BF16)
        vv = S['v'].rearrange("(kb p) f -> p kb f", p=128)
        for h in range(6):
            P.dma('sp', kh[0][0:64, :], S['kT'][h * 128:h * 128 + 64, :], [], ['G_kh0'], 'G_kh0')
            P.dma('sp', kh[1][64:128, :], S['kT'][h * 128 + 64:h * 128 + 128, :], [], ['G_kh1'], 'G_kh1')
            for b0 in range(0, NKB, 16):
                b1 = min(NKB, b0 + 16)
                P.dma('sp', vh[:, b0:b1, :], vv[:, b0:b1, h * 128:(h + 1) * 128], [], ['G_vh'], 'G_vh')
            for (s, n) in tiles_of(T):
                P.dma('sp', qh[:, 0:n], S['qT'][h * 128:(h + 1) * 128, s:s + n], [], ['G_qh'], 'G_qh')
                for kb in range(NKB):
                    pp = pS[kb % 2]
                    ppn = f'G_pS{kb % 2}'
                    pt = pT[kb % 2]
                    ptn = f'G_pT{kb % 2}'
                    for comp in range(2):
                        P.op('pe', lambda e, comp=comp, kb=kb, pp=pp: e.matmul(pp[:, comp, 0:n], lhsT=kh[comp][:, kb * 128:(kb + 1) * 128], rhs=qh[:, 0:n], start=True, stop=True),
                             [f'G_kh{comp}', 'G_qh'], [ppn])
                    P.op('act', lambda e, pp=pp, pt=pt: e.activation(out=pt[:, :, 0:n], in_=pp[:, :, 0:n], func=AF.Exp, scale=0.125), [ppn], [ptn])
                    for comp in range(2):
                        P.op('pe', lambda e, comp=comp, kb=kb, pt=pt: e.matmul(po[comp][:, 0:n], lhsT=vh[:, kb, :], rhs=pt[:, comp, 0:n], start=(kb == 0), stop=(kb == NKB - 1)),
                             ['G_vh', ptn], [f'G_po{comp}'])
                    ae = 'dve' if kb % 2 == 0 else 'pool'
                    ac = accs[kb % 2]
                    acn = f'G_acc{kb % 2}'
                    if kb < 2:
                        P.op(ae, lambda e, ac=ac, pt=pt: e.tensor_copy(out=ac[:, :, 0:n], in_=pt[:, :, 0:n]), [ptn], [acn])
                    else:
                        P.op(ae, lambda e, ac=ac, pt=pt: e.tensor_tensor(out=ac[:, :, 0:n], in0=ac[:, :, 0:n], in1=pt[:, :, 0:n], op=ALU.add), [ptn, acn], [acn])
                for comp in range(2):
                    for i in range(2):
                        P.op('pe', lambda e, comp=comp, i=i: e.matmul(pl[:, comp, 0:n], lhsT=k.ones_f[:], rhs=accs[i][:, comp, 0:n], start=(i == 0), stop=(i == 1)),
                             ['ones_f', f'G_acc{i}'], ['G_pl'])
                P.op('dve', lambda e: e.reciprocal(out=rl[:, :, 0:n], in_=pl[:, :, 0:n]), ['G_pl'], ['G_rl'])
                P.op('dve', lambda e: e.tensor_tensor(out=o1[:, 0:n], in0=po[0][:, 0:n], in1=rl[:, 0, 0:n], op=ALU.mult), ['G_po0', 'G_rl'], ['G_o1'])
                P.op('dve', lambda e: e.tensor_tensor(out=o2[:, 0:n], in0=po[1][:, 0:n], in1=rl[:, 1, 0:n], op=ALU.mult), ['G_po1', 'G_rl'], ['G_o2'])
                P.op('dve', lambda e: e.scalar_tensor_tensor(out=o1[:, 0:n], in0=o2[:, 0:n], scalar=neglam[:, 0:1], in1=o1[:, 0:n], op0=ALU.mult, op1=ALU.add),
                     ['G_o1', 'G_o2', 'G_neglam'], ['G_o1'])
                P.op('act', lambda e: e.activation(out=sq[:, 0:n], in_=o1[:, 0:n], func=AF.Square), ['G_o1'], ['G_sq'])
                P.op('pe', lambda e: e.matmul(pl[:, 0, 0:n], lhsT=k.ones_f[:], rhs=sq[:, 0:n], start=True, stop=True), ['ones_f', 'G_sq'], ['G_pl'])
                P.op('act', lambda e: e.activation(out=sq[:, 0:n], in_=pl[:, 0, 0:n], func=AF.Sqrt, bias=k.cst[:, 0:1], scale=1.0 / 128), ['G_pl', 'cst'], ['G_sq'])
                P.op('dve', lambda e: e.reciprocal(out=sq[:, 0:n], in_=sq[:, 0:n]), ['G_sq'], ['G_sq'])
                P.op('dve', lambda e: e.scalar_tensor_tensor(out=ych[:, 0:n], in0=o1[:, 0:n], scalar=gsub[:, 1:2], in1=sq[:, 0:n], op0=ALU.mult, op1=ALU.mult),
                     ['G_o1', 'G_sq', 'G_gsub'], ['G_ych'])
                P.dma('sp', yc_d[h * 128:(h + 1) * 128, s:s + n], ych[:, 0:n], ['G_ych'], [], 'G_ych')
    P.barrier()
    if stop_after == 'l1F1':
        return

    with ExitStack() as st:
        sb = lambda n, s, d: k.sb(n, s, d, st)
        ps = lambda n, s, d: k.ps(n, s, d, st)
        w_out = sb('w_out1', [128, 8, D], BF16)
        with ExitStack() as st2:
            fold_w_bf16(k, st2, 'w_out1', w_out, I['cd_w_out'], 8, k.gbc_d[1, 0, 0])
            P.barrier()
        cp = {}
        for nm, shp in (('conf_w', [128, 2, 31]), ('conf_b', [128, 2]), ('conf_lng', [128, 2]), ('conf_lnb', [128, 2])):
            cp[nm] = sb('H_' + nm, shp, F32)
            P.dma('sp', cp[nm][:], I[nm], [], ['H_' + nm], 'H_' + nm)
        yT = sb('H_yT', [128, 8, 512], BF16)
        gin = sb('H_gin', [128, 2, 542], F32)
        ca = sb('H_ca', [128, 512], F32)
        cb_ = sb('H_cb', [128, 512], F32)
        ct = [sb(f'H_ct{i}', [128, 512], F32) for i in range(2)]
        xm = sb('H_xm', [128, 2, 512], F32)
        sq = sb('H_sq', [128, 2, 512], F32)
        rstd = sb('H_rstd', [128, 512], F32)
        xt = sb('H_xt', [128, 4, D], F32)
        xo = sb('H_xo', [128, 4, D], F32)
        pm = ps('H_pm', [128, 512], F32)
        pv = ps('H_pv', [128, 512], F32)
        po = [ps(f'H_po{i}', [128, 512], F32) for i in range(4)]
        glv = S['gl'].rearrange("(c p) t -> p c t", p=128)
        ycv = yc_d.rearrange("(c p) t -> p c t", p=128)
        for (s, n) in tiles_of(T):
            ng = n // 128
            P.dma('sp', yT[:, 0:6, 0:n], ycv[:, :, s:s + n], [], keys('H_yT', 8)[0:6], 'H_yT')
            P.dma('sp', gin[:, :, 0:n + 30], glv[:, :, PADZ + s - 15:PADZ + s + n + 15], [], keys('H_gin', 2), 'H_gin')
            P.dma('sp', xt[:, 0:ng, :], S['x1_l'][1 + s:1 + s + n, :].rearrange("(g p) f -> p g f", p=128), [], ['H_xt'], 'H_xt')
            for c in range(2):
                P.op('act', lambda e, c=c: e.activation(out=ca[:, 0:n], in_=gin[:, c, 0:n], func=AF.Identity, scale=cp['conf_w'][:, c, 0:1], bias=cp['conf_b'][:, c:c + 1]),
                     [f'H_gin.{c}', 'H_conf_w', 'H_conf_b'], ['H_ca'])
                P.op('pool', lambda e, c=c: e.tensor_scalar(out=cb_[:, 0:n], in0=gin[:, c, 1:1 + n], scalar1=cp['conf_w'][:, c, 1:2], scalar2=None, op0=ALU.mult),
                     [f'H_gin.{c}', 'H_conf_w'], ['H_cb'])
                for t in range(2, 31):
                    if t % 2 == 0:
                        P.op('dve', lambda e, c=c, t=t: e.scalar_tensor_tensor(out=ca[:, 0:n], in0=gin[:, c, t:t + n], scalar=cp['conf_w'][:, c, t:t + 1], in1=ca[:, 0:n],
                                                                               op0=ALU.mult, op1=ALU.add), [f'H_gin.{c}', 'H_ca'], ['H_ca'])
                    else:
                        ctt = ct[(t // 2) % 2]
                        ctn = f'H_ct{(t // 2) % 2}'
                        P.op('act', lambda e, c=c, t=t, ctt=ctt: e.activation(out=ctt[:, 0:n], in_=gin[:, c, t:t + n], func=AF.Identity, scale=cp['conf_w'][:, c, t:t + 1], bias=k.cst[:, 2:3]),
                             [f'H_gin.{c}', 'H_conf_w', 'cst'], [ctn])
                        P.op('pool', lambda e, ctt=ctt: e.tensor_tensor(out=cb_[:, 0:n], in0=cb_[:, 0:n], in1=ctt[:, 0:n], op=ALU.add), [ctn, 'H_cb'], ['H_cb'])
                P.op('dve', lambda e, c=c: e.tensor_tensor(out=xm[:, c, 0:n], in0=ca[:, 0:n], in1=cb_[:, 0:n], op=ALU.add), ['H_ca', 'H_cb'], [f'H_xm.{c}'])
            for c in range(2):
                P.op('pe', lambda e, c=c: e.matmul(pm[:, 0:n], lhsT=k.ones_f[:], rhs=xm[:, c, 0:n], start=(c == 0), stop=(c == 1)), ['ones_f', f'H_xm.{c}'], ['H_pm'])
            for c in range(2):
                P.op('dve', lambda e, c=c: e.scalar_tensor_tensor(out=xm[:, c, 0:n], in0=pm[:, 0:n], scalar=-1.0 / 256, in1=xm[:, c, 0:n], op0=ALU.mult, op1=ALU.add),
                     ['H_pm', f'H_xm.{c}'], [f'H_xm.{c}'])
                P.op('act', lambda e, c=c: e.activation(out=sq[:, c, 0:n], in_=xm[:, c, 0:n], func=AF.Square), [f'H_xm.{c}'], [f'H_sq.{c}'])
            for c in range(2):
                P.op('pe', lambda e, c=c: e.matmul(pv[:, 0:n], lhsT=k.ones_f[:], rhs=sq[:, c, 0:n], start=(c == 0), stop=(c == 1)), ['ones_f', f'H_sq.{c}'], ['H_pv'])
            P.op('act', lambda e: e.activation(out=rstd[:, 0:n], in_=pv[:, 0:n], func=AF.Sqrt, bias=k.cst[:, 0:1], scale=1.0 / 256), ['H_pv', 'cst'], ['H_rstd'])
            P.op('dve', lambda e: e.reciprocal(out=rstd[:, 0:n], in_=rstd[:, 0:n]), ['H_rstd'], ['H_rstd'])
            for c in range(2):
                P.op('dve', lambda e, c=c: e.tensor_tensor(out=xm[:, c, 0:n], in0=xm[:, c, 0:n], in1=rstd[:, 0:n], op=ALU.mult), [f'H_xm.{c}', 'H_rstd'], [f'H_xm.{c}'])
                P.op('act', lambda e, c=c: e.activation(out=yT[:, 6 + c, 0:n], in_=xm[:, c, 0:n], func=AF.Silu, scale=cp['conf_lng'][:, c:c + 1], bias=cp['conf_lnb'][:, c:c + 1]),
                     [f'H_xm.{c}', 'H_conf_lng', 'H_conf_lnb'], [f'H_yT.{6 + c}'])
            for g in range(ng):
                for nt in range(2):
                    pp = po[(2 * g + nt) % 4]
                    ppn = f'H_po{(2 * g + nt) % 4}'
                    for kc in range(8):
                        P.op('pe', lambda e, g=g, nt=nt, kc=kc, pp=pp: e.matmul(pp[:, :], lhsT=yT[:, kc, g * 128:(g + 1) * 128], rhs=w_out[:, kc, nt * 512:(nt + 1) * 512],
                                                                               start=(kc == 0), stop=(kc == 7)), [f'H_yT.{kc}', f'w_out1.{kc}'], [ppn])
                    P.op('dve', lambda e, g=g, nt=nt, pp=pp: e.tensor_tensor(out=xo[:, g, nt * 512:(nt + 1) * 512], in0=pp[:, :], in1=xt[:, g, nt * 512:(nt + 1) * 512], op=ALU.add),
                         [ppn, 'H_xt'], [f'H_xo.{g}'])
            P.dma('sp', S['x15'][1 + s:1 + s + n, :].rearrange("(g p) f -> p g f", p=128), xo[:, 0:ng, :], keys('H_xo', 4)[0:ng], [], 'H_xo')
    P.barrier()
    if stop_after == 'l1F2':
        return
    ffn_phase(k, 1, 0, S['x15'], None, T, final=True)
    P.barrier()


def _fm(v, nch):
    return np.ascontiguousarray(np.asarray(v, np.float32).reshape(nch, 128).T)


def prep_shared(inp, T):
    f = lambda a: np.ascontiguousarray(np.asarray(a, np.float32))
    d = {}
    d['mod_w'] = f(inp['mod_w'])
    d['mod_b'] = f(inp['mod_b'])
    d['modb_fm'] = f(np.asarray(inp['mod_b']).reshape(2, 6, 8, 128).transpose(3, 0, 1, 2))
    ng = np.stack([np.asarray(inp['norm_mix_g']), np.asarray(inp['norm_ffn_g'])], axis=1)
    d['ng_fm'] = f(ng.reshape(2, 2, 8, 128).transpose(3, 0, 1, 2))
    d['final_g_bc'] = f(np.broadcast_to(np.asarray(inp['final_g'])[None, :], (128, D)))
    d['ident'] = f(np.eye(128))
    d['ab_w_in'] = f(inp['ab_w_in'][0])
    d['ab_w_out'] = f(inp['ab_w_out'][0])
    d['lru_cw'] = f(np.asarray(inp['lru_conv_w'][0]).reshape(4, 6, 128).transpose(2, 1, 0))
    d['lru_cb'] = _fm(inp['lru_conv_b'][0], 6)
    for nm, src in (('lru_bdA', inp['lru_wa'][0]), ('lru_bdX', inp['lru_wx'][0])):
        src = np.asarray(src)
        bd = np.zeros((128, 2, 6, 128), np.float32)
        for dd in range(2):
            for c in range(6):
                bd[0:64, dd, c, 0:64] = src[dd, 2 * c]
                bd[64:128, dd, c, 64:128] = src[dd, 2 * c + 1]
        d[nm] = bd
    for nm, src in (('lru_ba', inp['lru_ba'][0]), ('lru_bx', inp['lru_bx'][0]), ('lru_lam', inp['lru_lambda'][0])):
        d[nm] = f(np.asarray(src).reshape(2, 6, 128).transpose(2, 0, 1))
    pw = np.asarray(inp['pool_w'][0])
    bd = np.zeros((128, 2, 128), np.float32)
    for ch in range(2):
        bd[0:64, ch, 0:64] = pw[2 * ch]
        bd[64:128, ch, 64:128] = pw[2 * ch + 1]
    d['pool_bd'] = bd
    d['pool_scale'] = _fm(inp['pool_scale'][0], 2)
    invw = np.zeros((128, 2), np.float32)
    corr = np.ones((128, 2, 2, 16), np.float32)
    wins = (2, 4, 8, 16)
    Lbig = 1 << 20
    for g, w in enumerate(wins):
        ch, half = g // 2, g % 2
        psl = slice(64 * half, 64 * half + 64)
        invw[psl, ch] = 1.0 / w
        for i in range(16):
            t = i
            cnt = (t + w - w // 2) - max(t - w // 2, 0)
            corr[psl, ch, 0, i] = float(w) / cnt
            t = Lbig - 16 + i
            cnt = min(t + w - w // 2, Lbig) - (t - w // 2)
            corr[psl, ch, 1, i] = float(w) / cnt
    d['pool_invw'] = invw
    d['pool_corr'] = corr
    d['ffn_w_up'] = f(inp['ffn_w_up'])
    d['ffn_w_down'] = f(inp['ffn_w_down'])
    d['ffn_cw'] = f(np.asarray(inp['ffn_conv_w']).reshape(2, 3, 2 * NFC, 128).transpose(3, 0, 2, 1))
    d['ffn_cb'] = f(np.asarray(inp['ffn_conv_b']).reshape(2, 2 * NFC, 128).transpose(2, 0, 1))
    w_in = np.asarray(inp['cd_w_in'][0], np.float32)
    d['cd_w_in'] = f(w_in)
    qk = w_in[:, :1536].reshape(D, 1536 // 32, 2, 16)
    d['cd_w_sw'] = f(qk[:, :, ::-1, :].reshape(D, 1536))
    d['cd_w_out'] = f(inp['cd_w_out'][0])
    t = np.arange(T)
    row = (t // GRID_W).astype(np.float32)
    col = (t % GRID_W).astype(np.float32)
    inv = (10000.0 ** (-np.arange(16, dtype=np.float32) / 16)).astype(np.float32)
    ang_r = (row[:, None] * inv).astype(np.float32)
    ang_c = (col[:, None] * inv).astype(np.float32)
    rc = np.zeros((128, T), np.float32)
    rs = np.zeros((128, T), np.float32)
    for p in range(128):
        dd = p % 64
        ang = ang_r if dd < 32 else ang_c
        fq = dd % 16
        first = (dd % 32) < 16
        rc[p] = np.cos(ang[:, fq])
        rs[p] = (-1.0 if first else 1.0) * np.sin(ang[:, fq])
    d['rope_c'] = rc
    d['rope_s'] = rs
    d['diff_l'] = f(np.stack([np.asarray(inp['diff_lq1'][0]), np.asarray(inp['diff_lk1'][0]),
                              np.asarray(inp['diff_lq2'][0]), np.asarray(inp['diff_lk2'][0])])[None])
    d['subln_g'] = f(np.asarray(inp['diff_subln_g'][0]).reshape(128, 1))
    d['conf_w'] = f(np.asarray(inp['conf_dw_w'][0]).reshape(31, 2, 128).transpose(2, 1, 0))
    d['conf_b'] = _fm(inp['conf_dw_b'][0], 2)
    d['conf_lng'] = _fm(inp['conf_ln_g'][0], 2)
    d['conf_lnb'] = _fm(inp['conf_ln_b'][0], 2)
    return d


def prep_core(inp, shared, b, T):
    m = dict(shared)
    m['x'] = np.ascontiguousarray(np.asarray(inp['x'][b, :T], np.float32))
    m['ctx'] = np.ascontiguousarray(np.asarray(inp['ctx'][b], np.float32))
    m['c_fm'] = np.ascontiguousarray(np.stack([_fm(inp['c'][b], 8), _fm(inp['c_ctx'], 8)], axis=1))
    return m


_CACHE = {}


def kernel(**inputs):
    T = inputs['x'].shape[1]
    Bn = inputs['x'].shape[0]
    if T not in _CACHE:
        _CACHE[T] = build(T)
    kk = _CACHE[T]
    shared = prep_shared(inputs, T)
    ncores = 8
    in_maps = [prep_core(inputs, shared, c % Bn, T) for c in range(ncores)]
    res = run_bass_kernel_spmd(kk.nc, in_maps, core_ids=list(range(ncores)))
    out = np.stack([np.asarray(res.results[b]['out'], np.float32) for b in range(Bn)], axis=0)
    return out
```

```python
import numpy as np
import math
from contextlib import ExitStack
import concourse.bass as bass
import concourse.mybir as mybir
from concourse.bass_utils import run_bass_kernel_spmd
from concourse.ap import AP

F32 = mybir.dt.float32
BF16 = mybir.dt.bfloat16
AF = mybir.ActivationFunctionType
ALU = mybir.AluOpType
AX = mybir.AxisListType

D = 1024
LC = 256
DFF = 2816
NFC = DFF // 128
PADZ = 16
EPS = 1e-6
GRID_W = 64
LAMBDA_INIT1 = 0.8 - 0.6 * math.exp(-0.3 * 1)
SAME_ENGINE_SYNC = True


def rev(ap):
    a = [list(x) for x in ap.ap]
    st, n = a[-1]
    a[-1] = [-st, n]
    return AP(ap.tensor, ap.offset + st * (n - 1), a)


class Prog:
    def __init__(self, nc, es):
        self.nc = nc
        self.es = es
        self.eng = {'pe': nc.tensor, 'act': nc.scalar, 'dve': nc.vector, 'pool': nc.gpsimd, 'sp': nc.sync}
        self.esem = {e: es.enter_context(nc.semaphore('S_' + e)) for e in ('pe', 'act', 'dve', 'pool')}
        self.ecnt = {e: 0 for e in self.esem}
        self.dsem = {}
        self.dpool = []
        self.nd = 0
        self.waited = {e: {} for e in self.eng}
        self.lastw = {}
        self.rd = {}
        self.ninstr = 0

    def _need(self, reads, writes):
        ev = {}

        def add(e):
            if e is None:
                return
            k, sem, val = e
            if k not in ev or ev[k][1] < val:
                ev[k] = (sem, val)
        for r in reads:
            add(self.lastw.get(r))
        for w in writes:
            add(self.lastw.get(w))
            for e in self.rd.get(w, {}).items():
                add((e[0], e[1][0], e[1][1]))
        return ev

    def _wait(self, e, ev):
        for k, (sem, val) in ev.items():
            if k == 'S_' + e and (e == 'pe' or not SAME_ENGINE_SYNC):
                continue
            if self.waited[e].get(k, 0) < val:
                self.eng[e].wait_ge(sem, val)
                self.waited[e][k] = val
                self.ninstr += 1

    def _commit(self, ev, reads, writes):
        k, sem, val = ev
        for w in writes:
            self.lastw[w] = ev
            self.rd[w] = {}
        for r in reads:
            self.rd.setdefault(r, {})[k] = (sem, val)

    def op(self, e, fn, reads, writes):
        self._wait(e, self._need(reads, writes))
        ins = fn(self.eng[e])
        self.ecnt[e] += 1
        ins.then_inc(self.esem[e], 1)
        self.ninstr += 1
        self._commit(('S_' + e, self.esem[e], self.ecnt[e]), reads, writes)

    def dma(self, q, out, in_, reads, writes, key):
        self._wait(q, self._need(reads, writes))
        if key not in self.dsem:
            if self.dpool:
                self.dsem[key] = self.dpool.pop()
            else:
                nm = 'D%d' % self.nd
                self.nd += 1
                self.dsem[key] = [self.es.enter_context(self.nc.semaphore(nm)), 0, nm]
        d = self.dsem[key]
        ins = self.eng[q].dma_start(out=out, in_=in_)
        d[1] += 16
        ins.then_inc(d[0], 16)
        self.ninstr += 1
        self._commit((d[2], d[0], d[1]), reads, writes)

    def barrier(self):
        ev = {}
        for e in self.esem:
            if self.ecnt[e] > 0:
                ev['S_' + e] = (self.esem[e], self.ecnt[e])
        for k, d in self.dsem.items():
            if d[1] > 0:
                ev[d[2]] = (d[0], d[1])
        for e in self.eng:
            self._wait(e, dict(ev))
        self.lastw = {}
        self.rd = {}
        for k, d in self.dsem.items():
            self.dpool.append(d)
        self.dsem = {}


def keys(name, n):
    return [f"{name}.{i}" for i in range(n)]


def tiles_of(T, w=512):
    out = []
    s = 0
    while s < T:
        n = min(w, T - s)
        out.append((s, n))
        s += n
    return out


class K:
    pass


def build(T, dbg=False, stop_after=None):
    nc = bass.Bass("TRN2", target_bir_lowering=False)
    k = K()
    k.nc = nc
    k.T = T

    def din(name, shape, dt=F32):
        return nc.dram_tensor(name, list(shape), dt, kind="ExternalInput").ap()

    def dscr(name, shape, dt=F32, out=False):
        kind = "ExternalOutput" if (out or dbg) else "Internal"
        return nc.dram_tensor(name, list(shape), dt, kind=kind).ap()

    I = {}
    I['x'] = din('x', [T, D])
    I['ctx'] = din('ctx', [LC, D])
    I['c_fm'] = din('c_fm', [128, 2, 8])
    I['mod_w'] = din('mod_w', [2, D, 6 * D])
    I['modb_fm'] = din('modb_fm', [128, 2, 6, 8])
    I['mod_b'] = din('mod_b', [2, 6 * D])
    I['ng_fm'] = din('ng_fm', [128, 2, 2, 8])
    I['final_g_bc'] = din('final_g_bc', [128, D])
    I['ident'] = din('ident', [128, 128])
    I['ab_w_in'] = din('ab_w_in', [D, 1792])
    I['ab_w_out'] = din('ab_w_out', [D, D])
    I['lru_cw'] = din('lru_cw', [128, 6, 4])
    I['lru_cb'] = din('lru_cb', [128, 6])
    I['lru_bdA'] = din('lru_bdA', [128, 2, 6, 128])
    I['lru_bdX'] = din('lru_bdX', [128, 2, 6, 128])
    I['lru_ba'] = din('lru_ba', [128, 2, 6])
    I['lru_bx'] = din('lru_bx', [128, 2, 6])
    I['lru_lam'] = din('lru_lam', [128, 2, 6])
    I['pool_bd'] = din('pool_bd', [128, 2, 128])
    I['pool_scale'] = din('pool_scale', [128, 2])
    I['pool_invw'] = din('pool_invw', [128, 2])
    I['pool_corr'] = din('pool_corr', [128, 2, 2, 16])
    I['ffn_w_up'] = din('ffn_w_up', [2, D, 2 * DFF])
    I['ffn_w_down'] = din('ffn_w_down', [2, DFF, D])
    I['ffn_cw'] = din('ffn_cw', [128, 2, 2 * NFC, 3])
    I['ffn_cb'] = din('ffn_cb', [128, 2, 2 * NFC])
    I['cd_w_in'] = din('cd_w_in', [D, 2816])
    I['cd_w_sw'] = din('cd_w_sw', [D, 1536])
    I['cd_w_out'] = din('cd_w_out', [D, D])
    I['rope_c'] = din('rope_c', [128, T])
    I['rope_s'] = din('rope_s', [128, T])
    I['diff_l'] = din('diff_l', [1, 4, 64])
    I['subln_g'] = din('subln_g', [128, 1])
    I['conf_w'] = din('conf_w', [128, 2, 31])
    I['conf_b'] = din('conf_b', [128, 2])
    I['conf_lng'] = din('conf_lng', [128, 2])
    I['conf_lnb'] = din('conf_lnb', [128, 2])
    k.I = I

    k.out = nc.dram_tensor('out', [T, D], F32, kind="ExternalOutput").ap()
    S = {}
    for nm, TT in (('l', T), ('c', LC)):
        S['z_' + nm] = dscr('z_' + nm, [1792, PADZ + TT + PADZ])
        S['xa_' + nm] = dscr('xa_' + nm, [768, TT])
        S['hf_' + nm] = dscr('hf_' + nm, [768, TT])
        S['x05_' + nm] = dscr('x05_' + nm, [1 + TT, D])
        S['x1_' + nm] = dscr('x1_' + nm, [1 + TT + 128, D])
    S['x15'] = dscr('x15', [1 + T, D])
    S['x2'] = dscr('x2', [1 + T + 128, D])
    S['qT'] = dscr('qT', [768, T], BF16)
    S['kT'] = dscr('kT', [768, T + LC], BF16)
    S['v'] = dscr('v', [T + LC, 768], BF16)
    S['gl'] = dscr('gl', [256, PADZ + T + PADZ])
    k.S = S

    with ExitStack() as es:
        P = Prog(nc, es)
        k.P = P

        uid = [0]

        def sb(name, shape, dt, st=es):
            uid[0] += 1
            return st.enter_context(nc.sbuf_tensor(f"{name}_s{uid[0]}", list(shape), dt))

        def ps(name, shape, dt, st=es):
            uid[0] += 1
            return st.enter_context(nc.psum_tensor(f"{name}_p{uid[0]}", list(shape), dt))
        k.sb = sb
        k.ps = ps

        ident = sb('ident', [128, 128], BF16)
        P.dma('pool', ident[:], I['ident'], [], ['ident'], 'ident')
        k.ident = ident
        cst = sb('cst', [128, 8], F32)
        P.op('dve', lambda e: e.memset(cst[:, 0:1], EPS), [], ['cst'])
        P.op('dve', lambda e: e.memset(cst[:, 1:2], 1.0), [], ['cst'])
        P.op('dve', lambda e: e.memset(cst[:, 2:3], 0.0), [], ['cst'])
        k.cst = cst
        zero = sb('zero', [128, 512], F32)
        P.op('dve', lambda e: e.memset(zero[:], 0.0), [], ['zero'])
        k.zero = zero
        ones_bf = sb('ones_bf', [128, 128], BF16)
        P.op('dve', lambda e: e.memset(ones_bf[:], 1.0), [], ['ones_bf'])
        k.ones_bf = ones_bf
        ones_f = sb('ones_f', [128, 128], F32)
        P.op('dve', lambda e: e.memset(ones_f[:], 1.0), [], ['ones_f'])
        k.ones_f = ones_f

        modfm = sb('modfm', [128, 2, 2, 4, 8], F32)
        k.modfm = modfm
        k.gbc_d = nc.dram_tensor('gbc_d', [2, 2, 2, 128, D], F32, kind="Internal").ap()

        phase_adaln(k)
        P.barrier()
        if dbg:
            mdbg = nc.dram_tensor('modfm_dbg', [128, 2, 2, 4, 8], F32, kind="ExternalOutput").ap()
            P.dma('sp', mdbg, modfm[:], [], [], 'mdbg')
            gdbg = nc.dram_tensor('gbc_dbg', [2, 2, 2, 128, D], F32, kind="ExternalOutput").ap()
            P.dma('sp', gdbg, k.gbc_d, [], [], 'gdbg')
        if stop_after == 'adaln':
            return finish(k, es)

        layer0(k, stop_after)
        if stop_after is not None and stop_after.startswith('l0'):
            return finish(k, es)
        layer1(k, stop_after)
        return finish(k, es)


def finish(k, es):
    k.P.barrier()
    k.ninstr = k.P.ninstr
    return k


def phase_adaln(k):
    nc, P, I = k.nc, k.P, k.I
    with ExitStack() as st:
        sb = lambda n, s, d: k.sb(n, s, d, st)
        ps = lambda n, s, d: k.ps(n, s, d, st)
        cf = sb('ad_cf', [128, 2, 8], F32)
        P.dma('sp', cf[:], I['c_fm'], [], ['ad_cf'], 'ad_cf')
        sc = sb('ad_sc', [128, 2, 8], F32)
        P.op('act', lambda e: e.activation(out=sc[:], in_=cf[:], func=AF.Silu), ['ad_cf'], ['ad_sc'])
        rep = sb('ad_rep', [128, 2, 8, 128], F32)
        for s_ in range(2):
            for kc in range(8):
                P.op('dve', lambda e, s_=s_, kc=kc: e.tensor_copy(out=rep[:, s_, kc, :], in_=sc[:, s_, kc:kc + 1].to_broadcast([128, 128])),
                     ['ad_sc'], [f'ad_rep.{s_}.{kc}'])
        modb = sb('ad_modb', [128, 2, 6, 8], F32)
        P.dma('sp', modb[:], I['modb_fm'], [], ['ad_modb'], 'ad_modb')
        ng = sb('ad_ng', [128, 2, 2, 8], F32)
        P.dma('sp', ng[:], I['ng_fm'], [], ['ad_ng'], 'ad_ng')
        brow = sb('ad_brow', [1, 2, 6 * D], F32)
        P.dma('sp', brow[:], I['mod_b'].rearrange("(o l) n -> o l n", o=1), [], ['ad_brow'], 'ad_brow')
        wt = [sb(f'ad_w{i}', [128, 6 * D], F32) for i in range(2)]
        gst = sb('ad_gst', [128, 2, D], F32)
        facc = sb('ad_facc', [128, 32, 2], F32)
        pfm = ps('ad_pfm', [128, 32, 2], F32)
        pbc = [ps(f'ad_pbc{i}', [128, 512], F32) for i in range(4)]
        for l in range(2):
            for pss in range(2):
                for kc in range(8):
                    w = wt[kc % 2]
                    wk = f'ad_w{kc % 2}'
                    P.dma('sp', w[:], I['mod_w'][l, kc * 128:(kc + 1) * 128, :], [], [wk], wk)
                    if pss == 0:
                        jmap = [0, 1, 3, 4]
                        for jj, j in enumerate(jmap):
                            for fc in range(8):
                                col = j * D + fc * 128
                                P.op('pe', lambda e, w=w, col=col, jj=jj, fc=fc, kc=kc: e.matmul(
                                    pfm[:, jj * 8 + fc, :], lhsT=w[:, col:col + 128], rhs=sc[:, :, kc],
                                    start=True, stop=True), [wk, 'ad_sc'], ['ad_pfm'])
                        if kc == 0:
                            P.op('dve', lambda e: e.tensor_copy(out=facc[:], in_=pfm[:]), ['ad_pfm'], ['ad_facc'])
                        else:
                            P.op('dve', lambda e: e.tensor_tensor(out=facc[:], in0=pfm[:], in1=facc[:], op=ALU.add), ['ad_pfm', 'ad_facc'], ['ad_facc'])
                    if True:
                        s_ = pss
                        for nt in range(4):
                            gj = 2 if nt < 2 else 5
                            col = gj * D + (nt % 2) * 512
                            P.op('pe', lambda e, w=w, col=col, nt=nt, kc=kc, s_=s_: e.matmul(
                                pbc[nt][:], lhsT=rep[:, s_, kc, :], rhs=w[:, col:col + 512],
                                start=(kc == 0), stop=False), [wk, f'ad_rep.{s_}.{kc}'], [f'ad_pbc{nt}'])
                if pss == 0:
                    jmap = [0, 1, 3, 4]
                    for s_ in range(2):
                        for jj, j in enumerate(jmap):
                            P.op('dve', lambda e, s_=s_, jj=jj, j=j, l=l: e.tensor_tensor(
                                out=k.modfm[:, l, s_, jj, :], in0=facc[:, jj * 8:(jj + 1) * 8, s_], in1=modb[:, l, j, :], op=ALU.add),
                                ['ad_facc', 'ad_modb'], [f'modfm.{l}.{s_}.{jj}'])
                        for jj, which in ((1, 0), (3, 1)):
                            P.op('dve', lambda e, s_=s_, jj=jj, which=which, l=l: e.scalar_tensor_tensor(
                                out=k.modfm[:, l, s_, jj, :], in0=k.modfm[:, l, s_, jj, :], scalar=1.0, in1=ng[:, l, which, :],
                                op0=ALU.add, op1=ALU.mult), [f'modfm.{l}.{s_}.{jj}', 'ad_ng'], [f'modfm.{l}.{s_}.{jj}'])
                if True:
                    s_ = pss
                    for nt in range(4):
                        gj = 2 if nt < 2 else 5
                        col = gj * D + (nt % 2) * 512
                        P.op('pe', lambda e, nt=nt, col=col, l=l: e.matmul(
                            pbc[nt][:], lhsT=k.ones_f[0:1, :], rhs=brow[0:1, l, col:col + 512], start=False, stop=True),
                            ['ones_f', 'ad_brow'], [f'ad_pbc{nt}'])
                        P.op('act', lambda e, nt=nt: e.copy(out=gst[:, nt // 2, (nt % 2) * 512:(nt % 2) * 512 + 512], in_=pbc[nt][:]),
                            [f'ad_pbc{nt}'], [f'ad_gst.{nt}'])
                    for j2 in range(2):
                        P.dma('sp', k.gbc_d[l, s_, j2], gst[:, j2, :], [f'ad_gst.{2 * j2}', f'ad_gst.{2 * j2 + 1}'], [], 'ad_gst')
        P.barrier()


def load_w_bf16(k, name, dst, src_ap, nk, ncols, colblk=2048):
    P = k.P
    for kc in range(nk):
        for c0 in range(0, ncols, colblk):
            c1 = min(ncols, c0 + colblk)
            P.dma('pool', dst[:, kc, c0:c1], src_ap[kc * 128:(kc + 1) * 128, c0:c1], [], [f'{name}.{kc}'], f'{name}.{kc}')


def fold_w_bf16(k, st, name, dst, src_ap, nk, gb_dram):
    P = k.P
    stg = [k.sb(f'{name}_stg{i}', [128, D], F32, st) for i in range(2)]
    gb = k.sb(f'{name}_gb', [128, D], F32, st)
    P.dma('sp', gb[:], gb_dram, [], [f'{name}_gb'], f'{name}_gb')
    gb_ap = gb[:]
    for kc in range(nk):
        s_ = stg[kc % 2]
        sk = f'{name}_stg{kc % 2}'
        P.dma('sp', s_[:], src_ap[kc * 128:(kc + 1) * 128, :], [], [sk], sk)
        P.op('dve', lambda e, s_=s_, kc=kc: e.tensor_tensor(out=dst[:, kc, :], in0=s_[:], in1=gb_ap, op=ALU.mult),
             [sk, f'{name}_gb'], [f'{name}.{kc}'])


def modulate_tile(k, B, src_rows, n, l, s_, jsh, hT, hTname):
    P = k.P
    ng = n // 128
    X, Xn = B['xt'], B['xtname']
    ss = B['ss']
    xn = B['xn']
    for g0 in range(0, ng, 2):
        gg = min(2, ng - g0)
        P.dma('sp', X[:, 0:gg, :], src_rows[g0 * 128:(g0 + gg) * 128, :].rearrange("(g p) f -> p g f", p=128), [], [Xn], Xn)
        P.op('dve', lambda e: e.memset(ss[:, 0:2], 0.0), [], [B['ssname']])
        for g in range(gg):
            P.op('act', lambda e, g=g: e.activation(out=B['junk'][:], in_=X[:, g, :], func=AF.Square, accum_out=ss[:, g:g + 1]),
                 [Xn], [B['ssname'], B['junkname']])
        P.op('act', lambda e, gg=gg: e.activation(out=ss[:, 4:4 + gg], in_=ss[:, 0:gg], func=AF.Sqrt, bias=k.cst[:, 0:1], scale=1.0 / D),
             [B['ssname'], 'cst'], [B['ssname']])
        P.op('dve', lambda e, gg=gg: e.reciprocal(out=ss[:, 8:8 + gg], in_=ss[:, 4:4 + gg]), [B['ssname']], [B['ssname']])
        for g in range(gg):
            eng = 'dve' if g % 2 == 0 else 'pool'
            P.op(eng, lambda e, g=g, g0=g0: e.tensor_scalar(out=xn[:, g0 + g, :], in0=X[:, g, :], scalar1=ss[:, 8 + g:9 + g], scalar2=None, op0=ALU.mult),
                 [Xn, B['ssname']], [f"{B['xnname']}.{g0 + g}"])
    for fc in range(8):
        tp = B['tp'][fc % 2]
        tpn = B['tpname'][fc % 2]
        for g in range(ng):
            P.op('pe', lambda e, g=g, fc=fc, tp=tp: e.transpose(out=tp[:, g * 128:(g + 1) * 128], in_=xn[:, g, fc * 128:(fc + 1) * 128], identity=k.ident[:]),
                 [f"{B['xnname']}.{g}", 'ident'], [tpn])
        P.op('act', lambda e, fc=fc, tp=tp: e.activation(out=hT[:, fc, 0:n], in_=tp[:, 0:n], func=AF.Identity,
                                                          scale=k.modfm[:, l, s_, jsh + 1, fc:fc + 1], bias=k.modfm[:, l, s_, jsh, fc:fc + 1]),
             [tpn], [f'{hTname}.{fc}'])


def mod_bufs(k, st, pfx):
    B = {}
    B['xt'] = k.sb(pfx + 'xt', [128, 2, D], F32, st)
    B['xtname'] = pfx + 'xt'
    B['xn'] = k.sb(pfx + 'xn', [128, 4, D], BF16, st)
    B['xnname'] = pfx + 'xn'
    B['junk'] = k.sb(pfx + 'junk', [128, D], BF16, st)
    B['junkname'] = pfx + 'junk'
    B['ss'] = k.sb(pfx + 'ss', [128, 12], F32, st)
    B['ssname'] = pfx + 'ss'
    B['tp'] = [k.ps(pfx + f'tp{i}', [128, 512], BF16, st) for i in range(2)]
    B['tpname'] = [pfx + f'tp{i}' for i in range(2)]
    return B


def layer0(k, stop_after):
    nc, P, I, S = k.nc, k.P, k.I, k.S
    with ExitStack() as st:
        sb = lambda n, s, d: k.sb(n, s, d, st)
        lp = {}
        for nm, shp in (('lru_cw', [128, 6, 4]), ('lru_cb', [128, 6]), ('lru_ba', [128, 2, 6]), ('lru_bx', [128, 2, 6]),
                        ('lru_lam', [128, 2, 6]), ('pool_scale', [128, 2]), ('pool_invw', [128, 2]), ('pool_corr', [128, 2, 2, 16])):
            lp[nm] = sb('p_' + nm, shp, F32)
            P.dma('sp', lp[nm][:], I[nm], [], ['p_' + nm], 'p_' + nm)
        for nm, shp in (('lru_bdA', [128, 2, 6, 128]), ('lru_bdX', [128, 2, 6, 128]), ('pool_bd', [128, 2, 128])):
            lp[nm] = sb('p_' + nm, shp, BF16)
            P.dma('pool', lp[nm][:], I[nm], [], ['p_' + nm], 'p_' + nm)
        cl = sb('p_cl', [128, 2, 2, 6], F32)
        tmp = sb('p_cltmp', [128, 2, 6], F32)
        P.op('act', lambda e: e.activation(out=tmp[:], in_=lp['lru_lam'][:], func=AF.Exp, scale=-1.0), ['p_lru_lam'], ['p_cltmp'])
        P.op('act', lambda e: e.activation(out=tmp[:], in_=tmp[:], func=AF.Ln, bias=k.cst[:, 1:2], scale=1.0), ['p_cltmp', 'cst'], ['p_cltmp'])
        P.op('dve', lambda e: e.tensor_scalar(out=cl[:, 0, :, :], in0=tmp[:], scalar1=-8.0, scalar2=None, op0=ALU.mult), ['p_cltmp'], ['p_cl'])
        P.op('dve', lambda e: e.tensor_scalar(out=cl[:, 1, :, :], in0=tmp[:], scalar1=-16.0, scalar2=None, op0=ALU.mult), ['p_cltmp'], ['p_cl'])
        lp['cl'] = cl
        stt = sb('p_state', [128, 2, 6], F32)
        P.op('dve', lambda e: e.memset(stt[:], 0.0), [], ['p_state'])
        lp['state'] = stt
        k.lp = lp
        w_in = sb('w_in0', [128, 8, 1792], BF16)
        load_w_bf16(k, 'w_in0', w_in, I['ab_w_in'], 8, 1792, colblk=1792)
        w_out = sb('w_out0', [128, 8, D], BF16)
        k.w_in0, k.w_out0 = w_in, w_out

        for s_, nm, TT, xsrc in ((1, 'c', LC, I['ctx']), (0, 'l', k.T, I['x'])):
            seg = K()
            seg.nm, seg.T, seg.x, seg.set = nm, TT, xsrc, s_
            seg.z, seg.xa, seg.hf, seg.x05, seg.x1 = S['z_' + nm], S['xa_' + nm], S['hf_' + nm], S['x05_' + nm], S['x1_' + nm]
            seg.tiles = tiles_of(TT)
            with ExitStack() as st2:
                fold_w_bf16(k, st2, 'w_out0', w_out, I['ab_w_out'], 8, k.gbc_d[0, s_, 0])
            P.barrier()
            l0_phaseA(k, seg)
            P.barrier()
            if stop_after == 'l0A' and nm == 'l':
                return
            l0_phaseB(k, seg)
            P.barrier()
            if stop_after == 'l0B' and nm == 'l':
                return
            l0_phaseC(k, seg)
            P.barrier()
            if stop_after == 'l0C' and nm == 'l':
                return
    for s_, nm, TT in ((1, 'c', LC), (0, 'l', k.T)):
        ffn_phase(k, 0, s_, S['x05_' + nm], S['x1_' + nm], TT, final=False)
        P.barrier()


def l0_phaseA(k, seg):
    nc, P = k.nc, k.P
    with ExitStack() as st:
        sb = lambda n, s, d: k.sb(n, s, d, st)
        ps = lambda n, s, d: k.ps(n, s, d, st)
        B = mod_bufs(k, st, 'A_')
        hT = sb('A_hT', [128, 8, 512], BF16)
        zt = [sb(f'A_zt{i}', [128, 14, 512], F32) for i in range(2)]
        zp = [ps(f'A_zp{i}', [128, 512], F32) for i in range(4)]
        zv = seg.z.rearrange("(c p) t -> p c t", p=128)
        P.dma('sp', zv[:, :, 0:PADZ], k.zero[:, 0:14 * PADZ].rearrange("p (c t) -> p c t", c=14), ['zero'], [], 'A_zpad')
        P.dma('sp', zv[:, :, PADZ + seg.T:PADZ + seg.T + PADZ], k.zero[:, 0:14 * PADZ].rearrange("p (c t) -> p c t", c=14), ['zero'], [], 'A_zpad')
        for ti, (s, n) in enumerate(seg.tiles):
            modulate_tile(k, B, seg.x[s:s + n, :], n, 0, seg.set, 0, hT, 'A_hT')
            Z = zt[ti % 2]
            Zn = f'A_zt{ti % 2}'
            for mc in range(14):
                zpp = zp[mc % 4]
                for kc in range(8):
                    P.op('pe', lambda e, mc=mc, kc=kc, zpp=zpp: e.matmul(zpp[:, 0:n], lhsT=k.w_in0[:, kc, mc * 128:(mc + 1) * 128], rhs=hT[:, kc, 0:n],
                                                                         start=(kc == 0), stop=(kc == 7)),
                         [f'w_in0.{kc}', f'A_hT.{kc}'], [f'A_zp{mc % 4}'])
                eng = 'act' if mc % 2 == 0 else 'dve'
                if eng == 'act':
                    P.op('act', lambda e, mc=mc, zpp=zpp: e.copy(out=Z[:, mc, 0:n], in_=zpp[:, 0:n]), [f'A_zp{mc % 4}'], [f'{Zn}.{mc}'])
                else:
                    P.op('dve', lambda e, mc=mc, zpp=zpp: e.tensor_copy(out=Z[:, mc, 0:n], in_=zpp[:, 0:n]), [f'A_zp{mc % 4}'], [f'{Zn}.{mc}'])
            P.dma('sp', zv[:, :, PADZ + s:PADZ + s + n], Z[:, :, 0:n], keys(Zn, 14), [], Zn)


def lru_coeffs(k, C, d, n, xa, xab):
    P, lp = k.P, k.lp
    for c in range(6):
        pr, pi = C['pg'][(2 * c) % 4], C['pg'][(2 * c + 1) % 4]
        prn, pin = C['pgname'][(2 * c) % 4], C['pgname'][(2 * c + 1) % 4]
        P.op('pe', lambda e, c=c, pr=pr: e.matmul(pr[:, 0:n], lhsT=lp['lru_bdA'][:, d, c, :], rhs=xab[:, c, 0:n], start=True, stop=True),
             ['p_lru_bdA', f"{C['xabname']}.{c}"], [prn])
        P.op('pe', lambda e, c=c, pi=pi: e.matmul(pi[:, 0:n], lhsT=lp['lru_bdX'][:, d, c, :], rhs=xab[:, c, 0:n], start=True, stop=True),
             ['p_lru_bdX', f"{C['xabname']}.{c}"], [pin])
        P.op('act', lambda e, c=c, pr=pr: e.activation(out=C['r'][:, c, 0:n], in_=pr[:, 0:n], func=AF.Sigmoid, bias=lp['lru_ba'][:, d, c:c + 1], scale=1.0),
             [prn, 'p_lru_ba'], [f"{C['pfx']}r.{c}"])
        P.op('act', lambda e, c=c, pi=pi: e.activation(out=C['ig'][:, c, 0:n], in_=pi[:, 0:n], func=AF.Sigmoid, bias=lp['lru_bx'][:, d, c:c + 1], scale=1.0),
             [pin, 'p_lru_bx'], [f"{C['pfx']}ig.{c}"])
    for c in range(6):
        P.op('act', lambda e, c=c: e.activation(out=C['a'][:, c, 0:n], in_=C['r'][:, c, 0:n], func=AF.Exp, scale=lp['cl'][:, 0, d, c:c + 1]),
             [f"{C['pfx']}r.{c}", 'p_cl'], [f"{C['pfx']}a.{c}"])
        P.op('act', lambda e, c=c: e.activation(out=C['r'][:, c, 0:n], in_=C['r'][:, c, 0:n], func=AF.Exp, scale=lp['cl'][:, 1, d, c:c + 1]),
             [f"{C['pfx']}r.{c}", 'p_cl'], [f"{C['pfx']}r.{c}"])
    for c in range(6):
        P.op('act', lambda e, c=c: e.activation(out=C['r'][:, c, 0:n], in_=C['r'][:, c, 0:n], func=AF.Sqrt, bias=k.cst[:, 1:2], scale=-1.0),
             [f"{C['pfx']}r.{c}", 'cst'], [f"{C['pfx']}r.{c}"])
        P.op('dve', lambda e, c=c: e.tensor_tensor(out=C['ig'][:, c, 0:n], in0=C['ig'][:, c, 0:n], in1=C['r'][:, c, 0:n], op=ALU.mult),
             [f"{C['pfx']}r.{c}", f"{C['pfx']}ig.{c}"], [f"{C['pfx']}ig.{c}"])
        P.op('pool', lambda e, c=c: e.tensor_tensor(out=C['ig'][:, c, 0:n], in0=C['ig'][:, c, 0:n], in1=xa[:, c, 0:n], op=ALU.mult),
             [f"{C['pfx']}ig.{c}", f"{C['xaname']}.{c}"], [f"{C['pfx']}ig.{c}"])


def coeff_bufs(k, st, pfx):
    C = {'pfx': pfx}
    for nm in ('r', 'ig', 'a'):
        C[nm] = k.sb(pfx + nm, [128, 6, 512], F32, st)
    C['pg'] = [k.ps(pfx + f'pg{i}', [128, 512], F32, st) for i in range(4)]
    C['pgname'] = [pfx + f'pg{i}' for i in range(4)]
    return C


def l0_phaseB(k, seg):
    P, lp = k.P, k.lp
    with ExitStack() as st:
        sb = lambda n, s, d: k.sb(n, s, d, st)
        zin = sb('B_zin', [128, 6, 515], F32)
        xa = sb('B_xa', [128, 6, 512], F32)
        xab = sb('B_xab', [128, 6, 512], BF16)
        hf = sb('B_hf', [128, 6, 512], F32)
        C = coeff_bufs(k, st, 'B_')
        C['xabname'], C['xaname'] = 'B_xab', 'B_xa'
        zv = seg.z[0:768, :].rearrange("(c p) t -> p c t", p=128)
        xav = seg.xa.rearrange("(c p) t -> p c t", p=128)
        hfv = seg.hf.rearrange("(c p) t -> p c t", p=128)
        if seg.nm == 'c':
            P.op('dve', lambda e: e.memset(lp['state'][:], 0.0), [], ['p_state'])
        for ti, (s, n) in enumerate(seg.tiles):
            P.dma('sp', zin[:, :, 0:n + 3], zv[:, :, PADZ + s - 2:PADZ + s + n + 1], [], keys('B_zin', 6), 'B_zin')
            for c in range(6):
                P.op('act', lambda e, c=c: e.activation(out=xa[:, c, 0:n], in_=zin[:, c, 0:n], func=AF.Identity,
                                                        scale=lp['lru_cw'][:, c, 0:1], bias=lp['lru_cb'][:, c:c + 1]),
                     [f'B_zin.{c}', 'p_lru_cw', 'p_lru_cb'], [f'B_xa.{c}'])
                for t in range(1, 4):
                    P.op('dve', lambda e, c=c, t=t: e.scalar_tensor_tensor(out=xa[:, c, 0:n], in0=zin[:, c, t:t + n], scalar=lp['lru_cw'][:, c, t:t + 1],
                                                                           in1=xa[:, c, 0:n], op0=ALU.mult, op1=ALU.add),
                         [f'B_zin.{c}', f'B_xa.{c}'], [f'B_xa.{c}'])
                P.op('pool', lambda e, c=c: e.tensor_copy(out=xab[:, c, 0:n], in_=xa[:, c, 0:n]), [f'B_xa.{c}'], [f'B_xab.{c}'])
            P.dma('sp', xav[:, :, s:s + n], xa[:, :, 0:n], keys('B_xa', 6), [], 'B_xa')
            lru_coeffs(k, C, 0, n, xa, xab)
            for c in range(6):
                P.op('dve', lambda e, c=c: e.tensor_tensor_scan(out=hf[:, c, 0:n], data0=C['a'][:, c, 0:n], data1=C['ig'][:, c, 0:n],
                                                                initial=lp['state'][:, 0, c:c + 1], op0=ALU.mult, op1=ALU.add),
                     [f'B_a.{c}', f'B_ig.{c}', 'p_state'], [f'B_hf.{c}'])
                P.op('dve', lambda e, c=c: e.tensor_copy(out=lp['state'][:, 0, c:c + 1], in_=hf[:, c, n - 1:n]), [f'B_hf.{c}'], ['p_state'])
            P.dma('sp', hfv[:, :, s:s + n], hf[:, :, 0:n], keys('B_hf', 6), [], 'B_hf')


def l0_phaseC(k, seg):
    P, lp, I = k.P, k.lp, k.I
    with ExitStack() as st:
        sb = lambda n, s, d: k.sb(n, s, d, st)
        ps = lambda n, s, d: k.ps(n, s, d, st)
        xa = sb('C_xa', [128, 6, 512], F32)
        xab = sb('C_xab', [128, 6, 512], BF16)
        hb = sb('C_hb', [128, 6, 512], F32)
        hf = sb('C_hf', [128, 6, 512], F32)
        ga = sb('C_ga', [128, 6, 512], F32)
        yT = sb('C_yT', [128, 8, 512], BF16)
        zb = sb('C_zb', [128, 2, 527], F32)
        p2 = sb('C_p2', [128, 527], F32)
        p4 = sb('C_p4', [128, 527], F32)
        p8 = sb('C_p8', [128, 527], F32)
        Ssum = sb('C_S', [128, 512], F32)
        dd = sb('C_dd', [128, 2, 512], BF16)
        xt = sb('C_xt', [128, 4, D], F32)
        xo = sb('C_xo', [128, 4, D], F32)
        C = coeff_bufs(k, st, 'C_')
        C['xabname'], C['xaname'] = 'C_xab', 'C_xa'
        po = [ps(f'C_po{i}', [128, 512], F32) for i in range(4)]
        zg = seg.z[768:1536, :].rearrange("(c p) t -> p c t", p=128)
        zbv = seg.z[1536:1792, :].rearrange("(c p) t -> p c t", p=128)
        xav = seg.xa.rearrange("(c p) t -> p c t", p=128)
        hfv = seg.hf.rearrange("(c p) t -> p c t", p=128)
        if seg.nm == 'c':
            P.op('dve', lambda e: e.memset(lp['state'][:, 1, :], 0.0), [], ['p_state'])
        nt_ = len(seg.tiles)
        for ti in range(nt_ - 1, -1, -1):
            s, n = seg.tiles[ti]
            ng = n // 128
            P.dma('sp', xa[:, :, 0:n], xav[:, :, s:s + n], [], keys('C_xa', 6), 'C_xa')
            P.dma('sp', hf[:, :, 0:n], hfv[:, :, s:s + n], [], keys('C_hf', 6), 'C_hf')
            P.dma('sp', ga[:, :, 0:n], zg[:, :, PADZ + s:PADZ + s + n], [], keys('C_ga', 6), 'C_ga')
            P.dma('sp', zb[:, :, 0:n + 15], zbv[:, :, PADZ + s - 8:PADZ + s + n + 7], [], keys('C_zb', 2), 'C_zb')
            P.dma('sp', xt[:, 0:ng, :], seg.x[s:s + n, :].rearrange("(g p) f -> p g f", p=128), [], ['C_xt'], 'C_xt')
            for c in range(6):
                P.op('pool', lambda e, c=c: e.tensor_copy(out=xab[:, c, 0:n], in_=xa[:, c, 0:n]), [f'C_xa.{c}'], [f'C_xab.{c}'])
            lru_coeffs(k, C, 1, n, xa, xab)
            for c in range(6):
                P.op('dve', lambda e, c=c: e.tensor_tensor_scan(out=rev(hb[:, c, 0:n]), data0=rev(C['a'][:, c, 0:n]), data1=rev(C['ig'][:, c, 0:n]),
                                                                initial=lp['state'][:, 1, c:c + 1], op0=ALU.mult, op1=ALU.add),
                     [f'C_a.{c}', f'C_ig.{c}', 'p_state'], [f'C_hb.{c}'])
                P.op('dve', lambda e, c=c: e.tensor_copy(out=lp['state'][:, 1, c:c + 1], in_=hb[:, c, 0:1]), [f'C_hb.{c}'], ['p_state'])
                P.op('act', lambda e, c=c: e.activation(out=ga[:, c, 0:n], in_=ga[:, c, 0:n], func=AF.Gelu_apprx_tanh), [f'C_ga.{c}'], [f'C_ga.{c}'])
                P.op('pool', lambda e, c=c: e.tensor_tensor(out=hb[:, c, 0:n], in0=hb[:, c, 0:n], in1=hf[:, c, 0:n], op=ALU.add),
                     [f'C_hb.{c}', f'C_hf.{c}'], [f'C_hb.{c}'])
                P.op('dve', lambda e, c=c: e.tensor_tensor(out=yT[:, c, 0:n], in0=hb[:, c, 0:n], in1=ga[:, c, 0:n], op=ALU.mult),
                     [f'C_hb.{c}', f'C_ga.{c}'], [f'C_yT.{c}'])
            W = n + 15
            for ch in range(2):
                zc = zb[:, ch, :]
                P.op('dve', lambda e, zc=zc: e.tensor_tensor(out=p2[:, 0:W - 1], in0=zc[:, 0:W - 1], in1=zc[:, 1:W], op=ALU.add),
                     [f'C_zb.{ch}'], ['C_p2'])
                if ch == 0:
                    P.op('dve', lambda e: e.tensor_copy(out=Ssum[0:64, 0:n], in_=p2[0:64, 7:7 + n]), ['C_p2'], ['C_S'])
                    P.op('dve', lambda e: e.tensor_tensor(out=Ssum[64:128, 0:n], in0=p2[64:128, 6:6 + n], in1=p2[64:128, 8:8 + n], op=ALU.add),
                         ['C_p2'], ['C_S'])
                else:
                    P.op('dve', lambda e: e.tensor_tensor(out=p4[:, 0:W - 3], in0=p2[:, 0:W - 3], in1=p2[:, 2:W - 1], op=ALU.add), ['C_p2'], ['C_p4'])
                    P.op('dve', lambda e: e.tensor_tensor(out=Ssum[0:64, 0:n], in0=p4[0:64, 4:4 + n], in1=p4[0:64, 8:8 + n], op=ALU.add),
                         ['C_p4'], ['C_S'])
                    P.op('dve', lambda e: e.tensor_tensor(out=p8[64:128, 0:W - 7], in0=p4[64:128, 0:W - 7], in1=p4[64:128, 4:W - 3], op=ALU.add),
                         ['C_p4'], ['C_p8'])
                    P.op('dve', lambda e: e.tensor_tensor(out=Ssum[64:128, 0:n], in0=p8[64:128, 0:n], in1=p8[64:128, 8:8 + n], op=ALU.add),
                         ['C_p8'], ['C_S'])
                if ti == 0:
                    P.op('dve', lambda e, ch=ch: e.tensor_tensor(out=Ssum[:, 0:16], in0=Ssum[:, 0:16], in1=lp['pool_corr'][:, ch, 0, :], op=ALU.mult),
                         ['C_S', 'p_pool_corr'], ['C_S'])
                if ti == nt_ - 1:
                    P.op('dve', lambda e, ch=ch: e.tensor_tensor(out=Ssum[:, n - 16:n], in0=Ssum[:, n - 16:n], in1=lp['pool_corr'][:, ch, 1, :], op=ALU.mult),
                         ['C_S', 'p_pool_corr'], ['C_S'])
                P.op('dve', lambda e, ch=ch, zc=zc: e.scalar_tensor_tensor(out=dd[:, ch, 0:n], in0=Ssum[:, 0:n], scalar=lp['pool_invw'][:, ch:ch + 1],
                                                                          in1=zc[:, 8:8 + n], op0=ALU.mult, op1=ALU.subtract),
                     ['C_S', f'C_zb.{ch}', 'p_pool_invw'], [f'C_dd.{ch}'])
                pp = po[ch]
                P.op('pe', lambda e, ch=ch, pp=pp: e.matmul(pp[:, 0:n], lhsT=lp['pool_bd'][:, ch, :], rhs=dd[:, ch, 0:n], start=True, stop=True),
                     ['p_pool_bd', f'C_dd.{ch}'], [f'C_po{ch}'])
                P.op('act', lambda e, ch=ch, pp=pp: e.activation(out=yT[:, 6 + ch, 0:n], in_=pp[:, 0:n], func=AF.Identity, scale=lp['pool_scale'][:, ch:ch + 1], bias=k.cst[:, 2:3]),
                     [f'C_po{ch}', 'p_pool_scale', 'cst'], [f'C_yT.{6 + ch}'])
            for g in range(ng):
                for nt in range(2):
                    pp = po[(2 * g + nt) % 4]
                    ppn = f'C_po{(2 * g + nt) % 4}'
                    for kc in range(8):
                        P.op('pe', lambda e, g=g, nt=nt, kc=kc, pp=pp: e.matmul(pp[:, :], lhsT=yT[:, kc, g * 128:(g + 1) * 128], rhs=k.w_out0[:, kc, nt * 512:(nt + 1) * 512],
                                                                               start=(kc == 0), stop=(kc == 7)),
                             [f'C_yT.{kc}', f'w_out0.{kc}'], [ppn])
                    P.op('dve', lambda e, g=g, nt=nt, pp=pp: e.tensor_tensor(out=xo[:, g, nt * 512:(nt + 1) * 512], in0=pp[:, :], in1=xt[:, g, nt * 512:(nt + 1) * 512], op=ALU.add),
                         [ppn, 'C_xt'], [f'C_xo.{g}'])
            P.dma('sp', seg.x05[1 + s:1 + s + n, :].rearrange("(g p) f -> p g f", p=128), xo[:, 0:ng, :], keys('C_xo', 4)[0:ng], [], 'C_xo')


def ffn_phase(k, l, s_, src, dst, TT, final):
    nc, P, I = k.nc, k.P, k.I
    tiles = tiles_of(TT)
    with ExitStack() as st:
        sb = lambda n, s, d: k.sb(n, s, d, st)
        ps = lambda n, s, d: k.ps(n, s, d, st)
        w_up = sb('F_wup', [128, 8, 2 * DFF], BF16)
        load_w_bf16(k, 'F_wup', w_up, I['ffn_w_up'][l], 8, 2 * DFF)
        w_dn = sb('F_wdn', [128, NFC, D], BF16)
        with ExitStack() as st2:
            fold_w_bf16(k, st2, 'F_wdn', w_dn, I['ffn_w_down'][l], NFC, k.gbc_d[l, s_, 1])
            P.barrier()
        cw = sb('F_cw', [128, 2 * NFC, 3], F32)
        cb = sb('F_cb', [128, 2 * NFC], F32)
        P.dma('sp', cw[:], I['ffn_cw'][:, l, :, :], [], ['F_cw'], 'F_cw')
        P.dma('sp', cb[:], I['ffn_cb'][:, l, :], [], ['F_cb'], 'F_cb')
        B = mod_bufs(k, st, 'F_')
        hT = sb('F_hT', [128, 8, 512], BF16)
        gT = sb('F_gT', [128, NFC, 512], BF16)
        prevu = sb('F_prevu', [128, 2 * NFC, 2], F32)
        P.op('dve', lambda e: e.memset(prevu[:], 0.0), [], ['F_prevu'])
        ucat = [sb(f'F_ucat{i}', [128, 516], F32) for i in range(2)]
        acc = [sb(f'F_acc{i}', [128, 512], F32) for i in range(2)]
        xs = sb('F_xs', [128, 1, D], F32)
        xo = xs
        pu = [ps(f'F_pu{i}', [128, 512], F32) for i in range(2)]
        pd = [ps(f'F_pd{i}', [128, 512], F32) for i in range(2)]
        if final:
            fg = sb('F_fg', [128, D], F32)
            P.dma('sp', fg[:], I['final_g_bc'], [], ['F_fg'], 'F_fg')
            fss = sb('F_fss', [128, 12], F32)
            fjunk = B['junk']

        def conv_gate(n, zero_u):
            for c in range(NFC):
                for vi in range(2):
                    cc = c + vi * NFC
                    U, Un = ucat[vi], f'F_ucat{vi}'
                    A_, An = acc[vi], f'F_acc{vi}'
                    P.op('dve', lambda e, U=U, cc=cc: e.tensor_copy(out=U[:, 0:2], in_=prevu[:, cc, :]), ['F_prevu'], [Un])
                    if zero_u:
                        P.op('pool', lambda e, U=U: e.memset(U[:, 2:2 + n], 0.0), [], [Un])
                    else:
                        pp = pu[vi]
                        for kc in range(8):
                            P.op('pe', lambda e, kc=kc, cc=cc, pp=pp: e.matmul(pp[:, 0:n], lhsT=w_up[:, kc, cc * 128:(cc + 1) * 128], rhs=hT[:, kc, 0:n],
                                                                              start=(kc == 0), stop=(kc == 7)),
                                 [f'F_wup.{kc}', f'F_hT.{kc}'], [f'F_pu{vi}'])
                        P.op('act', lambda e, U=U, pp=pp: e.copy(out=U[:, 2:2 + n], in_=pp[:, 0:n]), [f'F_pu{vi}'], [Un])
                    P.op('act', lambda e, U=U, cc=cc: e.copy(out=prevu[:, cc, :], in_=U[:, n:n + 2]), [Un], ['F_prevu'])
                    P.op('act', lambda e, U=U, A_=A_, cc=cc: e.activation(out=A_[:, 0:n], in_=U[:, 1:1 + n], func=AF.Identity,
                                                                         scale=cw[:, cc, 1:2], bias=cb[:, cc:cc + 1]),
                         [Un, 'F_cw', 'F_cb'], [An])
                    P.op('dve', lambda e, U=U, A_=A_, cc=cc: e.scalar_tensor_tensor(out=A_[:, 0:n], in0=U[:, 0:n], scalar=cw[:, cc, 0:1], in1=A_[:, 0:n],
                                                                                   op0=ALU.mult, op1=ALU.add), [Un, An], [An])
                    P.op('dve', lambda e, U=U, A_=A_, cc=cc: e.scalar_tensor_tensor(out=A_[:, 0:n], in0=U[:, 2:2 + n], scalar=cw[:, cc, 2:3], in1=A_[:, 0:n],
                                                                                   op0=ALU.mult, op1=ALU.add), [Un, An], [An])
                P.op('act', lambda e: e.activation(out=acc[1][:, 0:n], in_=acc[1][:, 0:n], func=AF.Silu), ['F_acc1'], ['F_acc1'])
                P.op('pool', lambda e, c=c: e.tensor_tensor(out=gT[:, c, 0:n], in0=acc[0][:, 0:n], in1=acc[1][:, 0:n], op=ALU.mult),
                     ['F_acc0', 'F_acc1'], [f'F_gT.{c}'])

        def down_res(tok0, n, nrows_last=128):
            ng = n // 128
            nr = lambda g: (nrows_last if g == ng - 1 else 128)
            for g_ in range(ng):
                g = 0
                r0 = 1 + tok0 + g_ * 128
                P.dma('sp', xs[0:nr(g_), g, :], src[r0:r0 + nr(g_), :], [], [f'F_xs.{g}'], f'F_xs{g}')
                for nt in range(2):
                    pp = pd[nt]
                    for kc in range(NFC):
                        P.op('pe', lambda e, g_=g_, nt=nt, kc=kc, pp=pp: e.matmul(pp[:, :], lhsT=gT[:, kc, g_ * 128:(g_ + 1) * 128], rhs=w_dn[:, kc, nt * 512:(nt + 1) * 512],
                                                                               start=(kc == 0), stop=(kc == NFC - 1)),
                             [f'F_gT.{kc}', f'F_wdn.{kc}'], [f'F_pd{nt}'])
                    P.op('dve', lambda e, g=g, g_=g_, nt=nt, pp=pp: e.tensor_tensor(out=xo[0:nr(g_), g, nt * 512:(nt + 1) * 512], in0=pp[0:nr(g_), :],
                                                                            in1=xs[0:nr(g_), g, nt * 512:(nt + 1) * 512], op=ALU.add),
                         [f'F_pd{nt}', f'F_xs.{g}'], [f'F_xs.{g}'])
                if final:
                    P.op('dve', lambda e: e.memset(fss[:, 0:1], 0.0), [], ['F_fss'])
                    P.op('act', lambda e, g=g: e.activation(out=fjunk[:], in_=xo[:, g, :], func=AF.Square, accum_out=fss[:, 0:1]),
                         [f'F_xs.{g}'], ['F_fss', 'F_junk'])
                    P.op('act', lambda e: e.activation(out=fss[:, 1:2], in_=fss[:, 0:1], func=AF.Sqrt, bias=k.cst[:, 0:1], scale=1.0 / D),
                         ['F_fss', 'cst'], ['F_fss'])
                    P.op('dve', lambda e: e.reciprocal(out=fss[:, 2:3], in_=fss[:, 1:2]), ['F_fss'], ['F_fss'])
                    P.op('dve', lambda e, g=g: e.scalar_tensor_tensor(out=xo[:, g, :], in0=xo[:, g, :], scalar=fss[:, 2:3], in1=fg[:],
                                                                     op0=ALU.mult, op1=ALU.mult), [f'F_xs.{g}', 'F_fss', 'F_fg'], [f'F_xs.{g}'])
                t0 = tok0 + g_ * 128
                if final:
                    lo = max(t0, 0)
                    hi = min(t0 + nr(g_), TT)
                    if hi > lo:
                        P.dma('sp', k.out[lo:hi, :], xo[lo - t0:hi - t0, g, :], [f'F_xs.{g}'], [], f'F_xs{g}')
                else:
                    P.dma('sp', dst[1 + t0:1 + t0 + nr(g_), :], xo[0:nr(g_), g, :], [f'F_xs.{g}'], [], f'F_xs{g}')

        for ti, (s, n) in enumerate(tiles):
            modulate_tile(k, B, src[1 + s:1 + s + n, :], n, l, s_, 2, hT, 'F_hT')
            conv_gate(n, False)
            down_res(s - 1, n)
        conv_gate(128, True)
        down_res(TT - 1, 128, nrows_last=1)


def layer1(k, stop_after):
    nc, P, I, S = k.nc, k.P, k.I, k.S
    T = k.T
    TK = T + LC
    NKB = TK // 128
    yc_d = nc.dram_tensor('yc_d', [768, T], BF16, kind="Internal").ap()
    with ExitStack() as st:
        sb = lambda n, s, d: k.sb(n, s, d, st)
        ps = lambda n, s, d: k.ps(n, s, d, st)
        w_in = sb('w_in1', [128, 8, 2816], BF16)
        load_w_bf16(k, 'w_in1', w_in, I['cd_w_in'], 8, 2816, colblk=1408)
        w_sw = sb('w_sw1', [128, 8, 1536], BF16)
        load_w_bf16(k, 'w_sw1', w_sw, I['cd_w_sw'], 8, 1536, colblk=1536)
        B = mod_bufs(k, st, 'E_')
        hT = sb('E_hT', [128, 8, 512], BF16)
        qk = sb('E_qk', [128, 12, 512], BF16)
        vt = sb('E_vt', [128, 4, 768], BF16)
        gl = sb('E_gl', [128, 2, 512], F32)
        rc = sb('E_rc', [128, 512], F32)
        rs = sb('E_rs', [128, 512], F32)
        t1 = sb('E_t1', [128, 512], F32)
        t2 = sb('E_t2', [128, 512], F32)
        sg = sb('E_sg', [128, 512], F32)
        pa = [ps(f'E_pa{i}', [128, 512], F32) for i in range(2)]
        pb = [ps(f'E_pb{i}', [128, 512], F32) for i in range(2)]
        glv = S['gl'].rearrange("(c p) t -> p c t", p=128)
        P.dma('sp', glv[:, :, 0:PADZ], k.zero[:, 0:2 * PADZ].rearrange("p (c t) -> p c t", c=2), ['zero'], [], 'E_glpad')
        P.dma('sp', glv[:, :, PADZ + T:PADZ + T + PADZ], k.zero[:, 0:2 * PADZ].rearrange("p (c t) -> p c t", c=2), ['zero'], [], 'E_glpad')
        qTv = S['qT'].rearrange("(c p) t -> p c t", p=128)
        kTv = S['kT'].rearrange("(c p) t -> p c t", p=128)

        def proj_plain(cols0, nch, dst, dstname, d0, n):
            for c in range(nch):
                pp = pa[c % 2]
                for kc in range(8):
                    P.op('pe', lambda e, c=c, kc=kc, pp=pp: e.matmul(pp[:, 0:n], lhsT=w_in[:, kc, cols0 + c * 128:cols0 + (c + 1) * 128], rhs=hT[:, kc, 0:n],
                                                                    start=(kc == 0), stop=(kc == 7)), [f'w_in1.{kc}', f'E_hT.{kc}'], [f'E_pa{c % 2}'])
                P.op('act', lambda e, c=c, pp=pp: e.copy(out=dst[:, d0 + c, 0:n], in_=pp[:, 0:n]), [f'E_pa{c % 2}'], [f'{dstname}.{d0 + c}'])

        def proj_v(n):
            ng = n // 128
            for g in range(ng):
                for (c0, cn, pp, ppn) in ((0, 512, pa[g % 2], f'E_pa{g % 2}'), (512, 256, pb[g % 2], f'E_pb{g % 2}')):
                    for kc in range(8):
                        P.op('pe', lambda e, g=g, kc=kc, pp=pp, c0=c0, cn=cn: e.matmul(pp[:, 0:cn], lhsT=hT[:, kc, g * 128:(g + 1) * 128],
                                                                                       rhs=w_in[:, kc, 1536 + c0:1536 + c0 + cn], start=(kc == 0), stop=(kc == 7)),
                             [f'w_in1.{kc}', f'E_hT.{kc}'], [ppn])
                    P.op('dve', lambda e, g=g, pp=pp, c0=c0, cn=cn: e.tensor_copy(out=vt[:, g, c0:c0 + cn], in_=pp[:, 0:cn]), [ppn], [f'E_vt.{g}'])

        n = LC
        modulate_tile(k, B, S['x1_c'][1:1 + LC, :], n, 1, 1, 0, hT, 'E_hT')
        proj_plain(768, 6, qk, 'E_qk', 6, n)
        P.dma('sp', kTv[:, :, T:T + n], qk[:, 6:12, 0:n], keys('E_qk', 12)[6:12], [], 'E_qk')
        proj_v(n)
        P.dma('sp', S['v'][T:T + n, :].rearrange("(g p) f -> p g f", p=128), vt[:, 0:n // 128, :], keys('E_vt', 4), [], 'E_vt')
        for ti, (s, n) in enumerate(tiles_of(T)):
            modulate_tile(k, B, S['x1_l'][1 + s:1 + s + n, :], n, 1, 0, 0, hT, 'E_hT')
            P.dma('sp', rc[:, 0:n], I['rope_c'][:, s:s + n], [], ['E_rc'], 'E_rc')
            P.dma('sp', rs[:, 0:n], I['rope_s'][:, s:s + n], [], ['E_rs'], 'E_rs')
            for c in range(12):
                pp, pq = pa[c % 2], pb[c % 2]
                for kc in range(8):
                    P.op('pe', lambda e, c=c, kc=kc, pp=pp: e.matmul(pp[:, 0:n], lhsT=w_in[:, kc, c * 128:(c + 1) * 128], rhs=hT[:, kc, 0:n],
                                                                    start=(kc == 0), stop=(kc == 7)), [f'w_in1.{kc}', f'E_hT.{kc}'], [f'E_pa{c % 2}'])
                for kc in range(8):
                    P.op('pe', lambda e, c=c, kc=kc, pq=pq: e.matmul(pq[:, 0:n], lhsT=w_sw[:, kc, c * 128:(c + 1) * 128], rhs=hT[:, kc, 0:n],
                                                                    start=(kc == 0), stop=(kc == 7)), [f'w_sw1.{kc}', f'E_hT.{kc}'], [f'E_pb{c % 2}'])
                P.op('dve', lambda e, pp=pp: e.tensor_tensor(out=t1[:, 0:n], in0=pp[:, 0:n], in1=rc[:, 0:n], op=ALU.mult), [f'E_pa{c % 2}', 'E_rc'], ['E_t1'])
                P.op('dve', lambda e, pq=pq: e.tensor_tensor(out=t2[:, 0:n], in0=pq[:, 0:n], in1=rs[:, 0:n], op=ALU.mult), [f'E_pb{c % 2}', 'E_rs'], ['E_t2'])
                P.op('pool', lambda e, c=c: e.tensor_tensor(out=qk[:, c, 0:n], in0=t1[:, 0:n], in1=t2[:, 0:n], op=ALU.add), ['E_t1', 'E_t2'], [f'E_qk.{c}'])
            P.dma('sp', qTv[:, :, s:s + n], qk[:, 0:6, 0:n], keys('E_qk', 12)[0:6], [], 'E_qk')
            P.dma('sp', kTv[:, :, s:s + n], qk[:, 6:12, 0:n], keys('E_qk', 12)[6:12], [], 'E_qk')
            proj_v(n)
            P.dma('sp', S['v'][s:s + n, :].rearrange("(g p) f -> p g f", p=128), vt[:, 0:n // 128, :], keys('E_vt', 4), [], 'E_vt')
            for c in range(2):
                pp, pq = pa[c % 2], pb[c % 2]
                for (pz, pzn, cols) in ((pp, f'E_pa{c % 2}', 2304 + c * 128), (pq, f'E_pb{c % 2}', 2304 + 256 + c * 128)):
                    for kc in range(8):
                        P.op('pe', lambda e, kc=kc, pz=pz, cols=cols: e.matmul(pz[:, 0:n], lhsT=w_in[:, kc, cols:cols + 128], rhs=hT[:, kc, 0:n],
                                                                              start=(kc == 0), stop=(kc == 7)), [f'w_in1.{kc}', f'E_hT.{kc}'], [pzn])
                P.op('act', lambda e, pq=pq: e.activation(out=sg[:, 0:n], in_=pq[:, 0:n], func=AF.Sigmoid), [f'E_pb{c % 2}'], ['E_sg'])
                P.op('dve', lambda e, c=c, pp=pp: e.tensor_tensor(out=gl[:, c, 0:n], in0=pp[:, 0:n], in1=sg[:, 0:n], op=ALU.mult), [f'E_pa{c % 2}', 'E_sg'], [f'E_gl.{c}'])
            P.dma('sp', glv[:, :, PADZ + s:PADZ + s + n], gl[:, :, 0:n], keys('E_gl', 2), [], 'E_gl')
    P.barrier()
    if stop_after == 'l1E':
        return

    with ExitStack() as st:
        sb = lambda n, s, d: k.sb(n, s, d, st)
        ps = lambda n, s, d: k.ps(n, s, d, st)
        dl = sb('G_dl', [1, 4, 64], F32)
        P.dma('sp', dl[:], I['diff_l'], [], ['G_dl'], 'G_dl')
        sm = sb('G_sm', [1, 8], F32)
        pr_ = sb('G_pr', [1, 2, 64], F32)
        P.op('dve', lambda e: e.tensor_tensor(out=pr_[:, 0, :], in0=dl[:, 0, :], in1=dl[:, 1, :], op=ALU.mult), ['G_dl'], ['G_pr'])
        P.op('dve', lambda e: e.tensor_tensor(out=pr_[:, 1, :], in0=dl[:, 2, :], in1=dl[:, 3, :], op=ALU.mult), ['G_dl'], ['G_pr'])
        P.op('dve', lambda e: e.reduce_sum(out=sm[:, 0:1], in_=pr_[:, 0, :], axis=AX.X), ['G_pr'], ['G_sm'])
        P.op('dve', lambda e: e.reduce_sum(out=sm[:, 1:2], in_=pr_[:, 1, :], axis=AX.X), ['G_pr'], ['G_sm'])
        P.op('act', lambda e: e.activation(out=sm[:, 2:4], in_=sm[:, 0:2], func=AF.Exp), ['G_sm'], ['G_sm'])
        P.op('dve', lambda e: e.tensor_tensor(out=sm[:, 4:5], in0=sm[:, 3:4], in1=sm[:, 2:3], op=ALU.subtract), ['G_sm'], ['G_sm'])
        P.op('dve', lambda e: e.tensor_scalar(out=sm[:, 5:6], in0=sm[:, 4:5], scalar1=-LAMBDA_INIT1, scalar2=None, op0=ALU.add), ['G_sm'], ['G_sm'])
        neglam = sb('G_neglam', [128, 2], F32)
        gsub = sb('G_gsub', [128, 2], F32)
        P.dma('sp', gsub[:, 0:1], I['subln_g'], [], ['G_gsub'], 'G_gsub')
        P.op('dve', lambda e: e.tensor_scalar(out=gsub[:, 1:2], in0=gsub[:, 0:1], scalar1=(1.0 - LAMBDA_INIT1), scalar2=None, op0=ALU.mult), ['G_gsub'], ['G_gsub'])
        pS = [ps(f'G_pS{i}', [128, 2, 512], F32) for i in range(2)]
        po = [ps(f'G_po{i}', [128, 512], F32) for i in range(2)]
        pl = ps('G_pl', [128, 2, 512], F32)
        P.op('pe', lambda e: e.matmul(pl[:, 0, 0:1], lhsT=k.ones_f[0:1, :], rhs=sm[0:1, 5:6], start=True, stop=True), ['ones_f', 'G_sm'], ['G_pl'])
        P.op('dve', lambda e: e.tensor_copy(out=neglam[:, 0:1], in_=pl[:, 0, 0:1]), ['G_pl'], ['G_neglam'])
        kh = [sb(f'G_kh{i}', [128, TK], BF16) for i in range(2)]
        P.op('pool', lambda e: e.memset(kh[0][64:128, :], 0.0), [], ['G_kh0'])
        P.op('pool', lambda e: e.memset(kh[1][0:64, :], 0.0), [], ['G_kh1'])
        vh = sb('G_vh', [128, NKB, 128], BF16)
        qh = sb('G_qh', [128, 512], BF16)
        pT = [sb(f'G_pT{i}', [128, 2, 512], BF16) for i in range(2)]
        accs = [sb(f'G_acc{i}', [128, 2, 512], F32) for i in range(2)]
        rl = sb('G_rl', [128, 2, 512], F32)
        o1 = sb('G_o1', [128, 512], F32)
        o2 = sb('G_o2', [128, 512], F32)
        sq = sb('G_sq', [128, 512], F32)
        ych = sb('G_ych', [128, 512], BF16)
        vv = S['v'].rearrange("(kb p) f -> p kb f", p=128)
        for h in range(6):
            P.dma('sp', kh[0][0:64, :], S['kT'][h * 128:h * 128 + 64, :], [], ['G_kh0'], 'G_kh0')
            P.dma('sp', kh[1][64:128, :], S['kT'][h * 128 + 64:h * 128 + 128, :], [], ['G_kh1'], 'G_kh1')
            for b0 in range(0, NKB, 16):
                b1 = min(NKB, b0 + 16)
                P.dma('sp', vh[:, b0:b1, :], vv[:, b0:b1, h * 128:(h + 1) * 128], [], ['G_vh'], 'G_vh')
            for (s, n) in tiles_of(T):
                P.dma('sp', qh[:, 0:n], S['qT'][h * 128:(h + 1) * 128, s:s + n], [], ['G_qh'], 'G_qh')

                def emit_qk(kb):
                    pp = pS[kb % 2]
                    for comp in range(2):
                        P.op('pe', lambda e, comp=comp, kb=kb, pp=pp: e.matmul(pp[:, comp, 0:n], lhsT=kh[comp][:, kb * 128:(kb + 1) * 128], rhs=qh[:, 0:n], start=True, stop=True),
                             [f'G_kh{comp}', 'G_qh'], [f'G_pS{kb % 2}'])
                emit_qk(0)
                first = [True, True]
                for kb in range(NKB):
                    if kb + 1 < NKB:
                        emit_qk(kb + 1)
                    pp = pS[kb % 2]
                    ppn = f'G_pS{kb % 2}'
                    pt = pT[kb % 2]
                    ptn = f'G_pT{kb % 2}'
                    P.op('act', lambda e, pp=pp, pt=pt: e.activation(out=pt[:, :, 0:n], in_=pp[:, :, 0:n], func=AF.Exp, scale=0.125), [ppn], [ptn])
                    for comp in range(2):
                        P.op('pe', lambda e, comp=comp, kb=kb, pt=pt: e.matmul(po[comp][:, 0:n], lhsT=vh[:, kb, :], rhs=pt[:, comp, 0:n], start=(kb == 0), stop=(kb == NKB - 1)),
                             ['G_vh', ptn], [f'G_po{comp}'])
                    ai = 1 if kb % 3 == 2 else 0
                    ae = 'pool' if ai == 1 else 'dve'
                    ac = accs[ai]
                    acn = f'G_acc{ai}'
                    if first[ai]:
                        first[ai] = False
                        P.op(ae, lambda e, ac=ac, pt=pt: e.tensor_copy(out=ac[:, :, 0:n], in_=pt[:, :, 0:n]), [ptn], [acn])
                    else:
                        P.op(ae, lambda e, ac=ac, pt=pt: e.tensor_tensor(out=ac[:, :, 0:n], in0=ac[:, :, 0:n], in1=pt[:, :, 0:n], op=ALU.add), [ptn, acn], [acn])
                for comp in range(2):
                    for i in range(2):
                        P.op('pe', lambda e, comp=comp, i=i: e.matmul(pl[:, comp, 0:n], lhsT=k.ones_f[:], rhs=accs[i][:, comp, 0:n], start=(i == 0), stop=(i == 1)),
                             ['ones_f', f'G_acc{i}'], ['G_pl'])
                P.op('dve', lambda e: e.reciprocal(out=rl[:, :, 0:n], in_=pl[:, :, 0:n]), ['G_pl'], ['G_rl'])
                P.op('dve', lambda e: e.tensor_tensor(out=o1[:, 0:n], in0=po[0][:, 0:n], in1=rl[:, 0, 0:n], op=ALU.mult), ['G_po0', 'G_rl'], ['G_o1'])
                P.op('dve', lambda e: e.tensor_tensor(out=o2[:, 0:n], in0=po[1][:, 0:n], in1=rl[:, 1, 0:n], op=ALU.mult), ['G_po1', 'G_rl'], ['G_o2'])
                P.op('dve', lambda e: e.scalar_tensor_tensor(out=o1[:, 0:n], in0=o2[:, 0:n], scalar=neglam[:, 0:1], in1=o1[:, 0:n], op0=ALU.mult, op1=ALU.add),
                     ['G_o1', 'G_o2', 'G_neglam'], ['G_o1'])
                P.op('act', lambda e: e.activation(out=sq[:, 0:n], in_=o1[:, 0:n], func=AF.Square), ['G_o1'], ['G_sq'])
                P.op('pe', lambda e: e.matmul(pl[:, 0, 0:n], lhsT=k.ones_f[:], rhs=sq[:, 0:n], start=True, stop=True), ['ones_f', 'G_sq'], ['G_pl'])
                P.op('act', lambda e: e.activation(out=sq[:, 0:n], in_=pl[:, 0, 0:n], func=AF.Sqrt, bias=k.cst[:, 0:1], scale=1.0 / 128), ['G_pl', 'cst'], ['G_sq'])
                P.op('dve', lambda e: e.reciprocal(out=sq[:, 0:n], in_=sq[:, 0:n]), ['G_sq'], ['G_sq'])
                P.op('dve', lambda e: e.scalar_tensor_tensor(out=ych[:, 0:n], in0=o1[:, 0:n], scalar=gsub[:, 1:2], in1=sq[:, 0:n], op0=ALU.mult, op1=ALU.mult),
                     ['G_o1', 'G_sq', 'G_gsub'], ['G_ych'])
                P.dma('sp', yc_d[h * 128:(h + 1) * 128, s:s + n], ych[:, 0:n], ['G_ych'], [], 'G_ych')
    P.barrier()
    if stop_after == 'l1F1':
        return

    with ExitStack() as st:
        sb = lambda n, s, d: k.sb(n, s, d, st)
        ps = lambda n, s, d: k.ps(n, s, d, st)
        w_out = sb('w_out1', [128, 8, D], BF16)
        with ExitStack() as st2:
            fold_w_bf16(k, st2, 'w_out1', w_out, I['cd_w_out'], 8, k.gbc_d[1, 0, 0])
            P.barrier()
        cp = {}
        for nm, shp in (('conf_w', [128, 2, 31]), ('conf_b', [128, 2]), ('conf_lng', [128, 2]), ('conf_lnb', [128, 2])):
            cp[nm] = sb('H_' + nm, shp, F32)
            P.dma('sp', cp[nm][:], I[nm], [], ['H_' + nm], 'H_' + nm)
        yT = sb('H_yT', [128, 8, 512], BF16)
        gin = sb('H_gin', [128, 2, 542], F32)
        ca = sb('H_ca', [128, 512], F32)
        cb_ = sb('H_cb', [128, 512], F32)
        ct = [sb(f'H_ct{i}', [128, 512], F32) for i in range(2)]
        xm = sb('H_xm', [128, 2, 512], F32)
        sq = sb('H_sq', [128, 2, 512], F32)
        rstd = sb('H_rstd', [128, 512], F32)
        xt = sb('H_xt', [128, 4, D], F32)
        xo = sb('H_xo', [128, 4, D], F32)
        pm = ps('H_pm', [128, 512], F32)
        pv = ps('H_pv', [128, 512], F32)
        po = [ps(f'H_po{i}', [128, 512], F32) for i in range(4)]
        glv = S['gl'].rearrange("(c p) t -> p c t", p=128)
        ycv = yc_d.rearrange("(c p) t -> p c t", p=128)
        for (s, n) in tiles_of(T):
            ng = n // 128
            P.dma('sp', yT[:, 0:6, 0:n], ycv[:, :, s:s + n], [], keys('H_yT', 8)[0:6], 'H_yT')
            P.dma('sp', gin[:, :, 0:n + 30], glv[:, :, PADZ + s - 15:PADZ + s + n + 15], [], keys('H_gin', 2), 'H_gin')
            P.dma('sp', xt[:, 0:ng, :], S['x1_l'][1 + s:1 + s + n, :].rearrange("(g p) f -> p g f", p=128), [], ['H_xt'], 'H_xt')
            for c in range(2):
                P.op('act', lambda e, c=c: e.activation(out=ca[:, 0:n], in_=gin[:, c, 0:n], func=AF.Identity, scale=cp['conf_w'][:, c, 0:1], bias=cp['conf_b'][:, c:c + 1]),
                     [f'H_gin.{c}', 'H_conf_w', 'H_conf_b'], ['H_ca'])
                P.op('pool', lambda e, c=c: e.tensor_scalar(out=cb_[:, 0:n], in0=gin[:, c, 1:1 + n], scalar1=cp['conf_w'][:, c, 1:2], scalar2=None, op0=ALU.mult),
                     [f'H_gin.{c}', 'H_conf_w'], ['H_cb'])
                for t in range(2, 31):
                    if t % 2 == 0:
                        P.op('dve', lambda e, c=c, t=t: e.scalar_tensor_tensor(out=ca[:, 0:n], in0=gin[:, c, t:t + n], scalar=cp['conf_w'][:, c, t:t + 1], in1=ca[:, 0:n],
                                                                               op0=ALU.mult, op1=ALU.add), [f'H_gin.{c}', 'H_ca'], ['H_ca'])
                    else:
                        ctt = ct[(t // 2) % 2]
                        ctn = f'H_ct{(t // 2) % 2}'
                        P.op('act', lambda e, c=c, t=t, ctt=ctt: e.activation(out=ctt[:, 0:n], in_=gin[:, c, t:t + n], func=AF.Identity, scale=cp['conf_w'][:, c, t:t + 1], bias=k.cst[:, 2:3]),
                             [f'H_gin.{c}', 'H_conf_w', 'cst'], [ctn])
                        P.op('pool', lambda e, ctt=ctt: e.tensor_tensor(out=cb_[:, 0:n], in0=cb_[:, 0:n], in1=ctt[:, 0:n], op=ALU.add), [ctn, 'H_cb'], ['H_cb'])
                P.op('dve', lambda e, c=c: e.tensor_tensor(out=xm[:, c, 0:n], in0=ca[:, 0:n], in1=cb_[:, 0:n], op=ALU.add), ['H_ca', 'H_cb'], [f'H_xm.{c}'])
            for c in range(2):
                P.op('pe', lambda e, c=c: e.matmul(pm[:, 0:n], lhsT=k.ones_f[:], rhs=xm[:, c, 0:n], start=(c == 0), stop=(c == 1)), ['ones_f', f'H_xm.{c}'], ['H_pm'])
            for c in range(2):
                P.op('dve', lambda e, c=c: e.scalar_tensor_tensor(out=xm[:, c, 0:n], in0=pm[:, 0:n], scalar=-1.0 / 256, in1=xm[:, c, 0:n], op0=ALU.mult, op1=ALU.add),
                     ['H_pm', f'H_xm.{c}'], [f'H_xm.{c}'])
                P.op('act', lambda e, c=c: e.activation(out=sq[:, c, 0:n], in_=xm[:, c, 0:n], func=AF.Square), [f'H_xm.{c}'], [f'H_sq.{c}'])
            for c in range(2):
                P.op('pe', lambda e, c=c: e.matmul(pv[:, 0:n], lhsT=k.ones_f[:], rhs=sq[:, c, 0:n], start=(c == 0), stop=(c == 1)), ['ones_f', f'H_sq.{c}'], ['H_pv'])
            P.op('act', lambda e: e.activation(out=rstd[:, 0:n], in_=pv[:, 0:n], func=AF.Sqrt, bias=k.cst[:, 0:1], scale=1.0 / 256), ['H_pv', 'cst'], ['H_rstd'])
            P.op('dve', lambda e: e.reciprocal(out=rstd[:, 0:n], in_=rstd[:, 0:n]), ['H_rstd'], ['H_rstd'])
            for c in range(2):
                P.op('dve', lambda e, c=c: e.tensor_tensor(out=xm[:, c, 0:n], in0=xm[:, c, 0:n], in1=rstd[:, 0:n], op=ALU.mult), [f'H_xm.{c}', 'H_rstd'], [f'H_xm.{c}'])
                P.op('act', lambda e, c=c: e.activation(out=yT[:, 6 + c, 0:n], in_=xm[:, c, 0:n], func=AF.Silu, scale=cp['conf_lng'][:, c:c + 1], bias=cp['conf_lnb'][:, c:c + 1]),
                     [f'H_xm.{c}', 'H_conf_lng', 'H_conf_lnb'], [f'H_yT.{6 + c}'])
            for g in range(ng):
                for nt in range(2):
                    pp = po[(2 * g + nt) % 4]
                    ppn = f'H_po{(2 * g + nt) % 4}'
                    for kc in range(8):
                        P.op('pe', lambda e, g=g, nt=nt, kc=kc, pp=pp: e.matmul(pp[:, :], lhsT=yT[:, kc, g * 128:(g + 1) * 128], rhs=w_out[:, kc, nt * 512:(nt + 1) * 512],
                                                                               start=(kc == 0), stop=(kc == 7)), [f'H_yT.{kc}', f'w_out1.{kc}'], [ppn])
                    P.op('dve', lambda e, g=g, nt=nt, pp=pp: e.tensor_tensor(out=xo[:, g, nt * 512:(nt + 1) * 512], in0=pp[:, :], in1=xt[:, g, nt * 512:(nt + 1) * 512], op=ALU.add),
                         [ppn, 'H_xt'], [f'H_xo.{g}'])
            P.dma('sp', S['x15'][1 + s:1 + s + n, :].rearrange("(g p) f -> p g f", p=128), xo[:, 0:ng, :], keys('H_xo', 4)[0:ng], [], 'H_xo')
    P.barrier()
    if stop_after == 'l1F2':
        return
    ffn_phase(k, 1, 0, S['x15'], None, T, final=True)
    P.barrier()


def _fm(v, nch):
    return np.ascontiguousarray(np.asarray(v, np.float32).reshape(nch, 128).T)


def prep_shared(inp, T):
    f = lambda a: np.ascontiguousarray(np.asarray(a, np.float32))
    d = {}
    d['mod_w'] = f(inp['mod_w'])
    d['mod_b'] = f(inp['mod_b'])
    d['modb_fm'] = f(np.asarray(inp['mod_b']).reshape(2, 6, 8, 128).transpose(3, 0, 1, 2))
    ng = np.stack([np.asarray(inp['norm_mix_g']), np.asarray(inp['norm_ffn_g'])], axis=1)
    d['ng_fm'] = f(ng.reshape(2, 2, 8, 128).transpose(3, 0, 1, 2))
    d['final_g_bc'] = f(np.broadcast_to(np.asarray(inp['final_g'])[None, :], (128, D)))
    d['ident'] = f(np.eye(128))
    d['ab_w_in'] = f(inp['ab_w_in'][0])
    d['ab_w_out'] = f(inp['ab_w_out'][0])
    d['lru_cw'] = f(np.asarray(inp['lru_conv_w'][0]).reshape(4, 6, 128).transpose(2, 1, 0))
    d['lru_cb'] = _fm(inp['lru_conv_b'][0], 6)
    for nm, src in (('lru_bdA', inp['lru_wa'][0]), ('lru_bdX', inp['lru_wx'][0])):
        src = np.asarray(src)
        bd = np.zeros((128, 2, 6, 128), np.float32)
        for dd in range(2):
            for c in range(6):
                bd[0:64, dd, c, 0:64] = src[dd, 2 * c]
                bd[64:128, dd, c, 64:128] = src[dd, 2 * c + 1]
        d[nm] = bd
    for nm, src in (('lru_ba', inp['lru_ba'][0]), ('lru_bx', inp['lru_bx'][0]), ('lru_lam', inp['lru_lambda'][0])):
        d[nm] = f(np.asarray(src).reshape(2, 6, 128).transpose(2, 0, 1))
    pw = np.asarray(inp['pool_w'][0])
    bd = np.zeros((128, 2, 128), np.float32)
    for ch in range(2):
        bd[0:64, ch, 0:64] = pw[2 * ch]
        bd[64:128, ch, 64:128] = pw[2 * ch + 1]
    d['pool_bd'] = bd
    d['pool_scale'] = _fm(inp['pool_scale'][0], 2)
    invw = np.zeros((128, 2), np.float32)
    corr = np.ones((128, 2, 2, 16), np.float32)
    wins = (2, 4, 8, 16)
    Lbig = 1 << 20
    for g, w in enumerate(wins):
        ch, half = g // 2, g % 2
        psl = slice(64 * half, 64 * half + 64)
        invw[psl, ch] = 1.0 / w
        for i in range(16):
            t = i
            cnt = (t + w - w // 2) - max(t - w // 2, 0)
            corr[psl, ch, 0, i] = float(w) / cnt
            t = Lbig - 16 + i
            cnt = min(t + w - w // 2, Lbig) - (t - w // 2)
            corr[psl, ch, 1, i] = float(w) / cnt
    d['pool_invw'] = invw
    d['pool_corr'] = corr
    d['ffn_w_up'] = f(inp['ffn_w_up'])
    d['ffn_w_down'] = f(inp['ffn_w_down'])
    d['ffn_cw'] = f(np.asarray(inp['ffn_conv_w']).reshape(2, 3, 2 * NFC, 128).transpose(3, 0, 2, 1))
    d['ffn_cb'] = f(np.asarray(inp['ffn_conv_b']).reshape(2, 2 * NFC, 128).transpose(2, 0, 1))
    w_in = np.asarray(inp['cd_w_in'][0], np.float32)
    d['cd_w_in'] = f(w_in)
    qk = w_in[:, :1536].reshape(D, 1536 // 32, 2, 16)
    d['cd_w_sw'] = f(qk[:, :, ::-1, :].reshape(D, 1536))
    d['cd_w_out'] = f(inp['cd_w_out'][0])
    t = np.arange(T)
    row = (t // GRID_W).astype(np.float32)
    col = (t % GRID_W).astype(np.float32)
    inv = (10000.0 ** (-np.arange(16, dtype=np.float32) / 16)).astype(np.float32)
    ang_r = (row[:, None] * inv).astype(np.float32)
    ang_c = (col[:, None] * inv).astype(np.float32)
    rc = np.zeros((128, T), np.float32)
    rs = np.zeros((128, T), np.float32)
    for p in range(128):
        dd = p % 64
        ang = ang_r if dd < 32 else ang_c
        fq = dd % 16
        first = (dd % 32) < 16
        rc[p] = np.cos(ang[:, fq])
        rs[p] = (-1.0 if first else 1.0) * np.sin(ang[:, fq])
    d['rope_c'] = rc
    d['rope_s'] = rs
    d['diff_l'] = f(np.stack([np.asarray(inp['diff_lq1'][0]), np.asarray(inp['diff_lk1'][0]),
                              np.asarray(inp['diff_lq2'][0]), np.asarray(inp['diff_lk2'][0])])[None])
    d['subln_g'] = f(np.asarray(inp['diff_subln_g'][0]).reshape(128, 1))
    d['conf_w'] = f(np.asarray(inp['conf_dw_w'][0]).reshape(31, 2, 128).transpose(2, 1, 0))
    d['conf_b'] = _fm(inp['conf_dw_b'][0], 2)
    d['conf_lng'] = _fm(inp['conf_ln_g'][0], 2)
    d['conf_lnb'] = _fm(inp['conf_ln_b'][0], 2)
    return d


def prep_core(inp, shared, b, T):
    m = dict(shared)
    m['x'] = np.ascontiguousarray(np.asarray(inp['x'][b, :T], np.float32))
    m['ctx'] = np.ascontiguousarray(np.asarray(inp['ctx'][b], np.float32))
    m['c_fm'] = np.ascontiguousarray(np.stack([_fm(inp['c'][b], 8), _fm(inp['c_ctx'], 8)], axis=1))
    return m


_CACHE = {}


def kernel(**inputs):
    T = inputs['x'].shape[1]
    Bn = inputs['x'].shape[0]
    if T not in _CACHE:
        _CACHE[T] = build(T)
    kk = _CACHE[T]
    shared = prep_shared(inputs, T)
    ncores = 8
    in_maps = [prep_core(inputs, shared, c % Bn, T) for c in range(ncores)]
    res = run_bass_kernel_spmd(kk.nc, in_maps, core_ids=list(range(ncores)))
    out = np.stack([np.asarray(res.results[b]['out'], np.float32) for b in range(Bn)], axis=0)
    return out
```

```python
import numpy as np
import math
import os
from contextlib import ExitStack
import concourse.bass as bass
import concourse.mybir as mybir
from concourse.bass_utils import run_bass_kernel_spmd
from concourse.ap import AP

F32 = mybir.dt.float32
BF16 = mybir.dt.bfloat16
AF = mybir.ActivationFunctionType
ALU = mybir.AluOpType
AX = mybir.AxisListType

D = 1024
LC = 256
DFF = 2816
NFC = DFF // 128
PADZ = 16
EPS = 1e-6
GRID_W = 64
LAMBDA_INIT1 = 0.8 - 0.6 * math.exp(-0.3 * 1)
SAME_ENGINE_SYNC = True


def rev(ap):
    a = [list(x) for x in ap.ap]
    st, n = a[-1]
    a[-1] = [-st, n]
    return AP(ap.tensor, ap.offset + st * (n - 1), a)


class Prog:
    def __init__(self, nc, es):
        self.nc = nc
        self.es = es
        self.eng = {'pe': nc.tensor, 'act': nc.scalar, 'dve': nc.vector, 'pool': nc.gpsimd, 'sp': nc.sync}
        self.esem = {e: es.enter_context(nc.semaphore('S_' + e)) for e in ('pe', 'act', 'dve', 'pool')}
        self.ecnt = {e: 0 for e in self.esem}
        self.dsem = {}
        self.dpool = []
        self.nd = 0
        self.waited = {e: {} for e in self.eng}
        self.lastw = {}
        self.rd = {}
        self.ninstr = 0

    def _need(self, reads, writes):
        ev = {}

        def add(e):
            if e is None:
                return
            k, sem, val = e
            if k not in ev or ev[k][1] < val:
                ev[k] = (sem, val)
        for r in reads:
            add(self.lastw.get(r))
        for w in writes:
            add(self.lastw.get(w))
            for e in self.rd.get(w, {}).items():
                add((e[0], e[1][0], e[1][1]))
        return ev

    def _wait(self, e, ev):
        for k, (sem, val) in ev.items():
            if k == 'S_' + e and (e == 'pe' or not SAME_ENGINE_SYNC):
                continue
            if self.waited[e].get(k, 0) < val:
                self.eng[e].wait_ge(sem, val)
                self.waited[e][k] = val
                self.ninstr += 1

    def _commit(self, ev, reads, writes):
        k, sem, val = ev
        for w in writes:
            self.lastw[w] = ev
            self.rd[w] = {}
        for r in reads:
            self.rd.setdefault(r, {})[k] = (sem, val)

    def op(self, e, fn, reads, writes):
        self._wait(e, self._need(reads, writes))
        ins = fn(self.eng[e])
        self.ecnt[e] += 1
        ins.then_inc(self.esem[e], 1)
        self.ninstr += 1
        self._commit(('S_' + e, self.esem[e], self.ecnt[e]), reads, writes)

    def dma(self, q, out, in_, reads, writes, key):
        self._wait(q, self._need(reads, writes))
        if key not in self.dsem:
            if self.dpool:
                self.dsem[key] = self.dpool.pop()
            else:
                nm = 'D%d' % self.nd
                self.nd += 1
                self.dsem[key] = [self.es.enter_context(self.nc.semaphore(nm)), 0, nm]
        d = self.dsem[key]
        ins = self.eng[q].dma_start(out=out, in_=in_)
        d[1] += 16
        ins.then_inc(d[0], 16)
        self.ninstr += 1
        self._commit((d[2], d[0], d[1]), reads, writes)

    def barrier(self):
        ev = {}
        for e in self.esem:
            if self.ecnt[e] > 0:
                ev['S_' + e] = (self.esem[e], self.ecnt[e])
        for k, d in self.dsem.items():
            if d[1] > 0:
                ev[d[2]] = (d[0], d[1])
        for e in self.eng:
            self._wait(e, dict(ev))
        self.lastw = {}
        self.rd = {}
        for k, d in self.dsem.items():
            self.dpool.append(d)
        self.dsem = {}


def keys(name, n):
    return [f"{name}.{i}" for i in range(n)]


def tiles_of(T, w=512):
    out = []
    s = 0
    while s < T:
        n = min(w, T - s)
        out.append((s, n))
        s += n
    return out


class K:
    pass


def build(T, dbg=False, stop_after=None):
    nc = bass.Bass("TRN2", target_bir_lowering=False)
    k = K()
    k.nc = nc
    k.T = T

    def din(name, shape, dt=F32):
        return nc.dram_tensor(name, list(shape), dt, kind="ExternalInput").ap()

    def dscr(name, shape, dt=F32, out=False):
        kind = "ExternalOutput" if (out or dbg) else "Internal"
        return nc.dram_tensor(name, list(shape), dt, kind=kind).ap()

    I = {}
    I['x'] = din('x', [T, D])
    I['ctx'] = din('ctx', [LC, D])
    I['c_fm'] = din('c_fm', [128, 2, 8])
    I['mod_w'] = din('mod_w', [2, D, 6 * D])
    I['modb_fm'] = din('modb_fm', [128, 2, 6, 8])
    I['mod_b'] = din('mod_b', [2, 6 * D])
    I['ng_fm'] = din('ng_fm', [128, 2, 2, 8])
    I['final_g_bc'] = din('final_g_bc', [128, D])
    I['ident'] = din('ident', [128, 128])
    I['ab_w_in'] = din('ab_w_in', [D, 1792])
    I['ab_w_out'] = din('ab_w_out', [D, D])
    I['lru_cw'] = din('lru_cw', [128, 6, 4])
    I['lru_cb'] = din('lru_cb', [128, 6])
    I['lru_bdA'] = din('lru_bdA', [128, 2, 6, 128])
    I['lru_bdX'] = din('lru_bdX', [128, 2, 6, 128])
    I['lru_ba'] = din('lru_ba', [128, 2, 6])
    I['lru_bx'] = din('lru_bx', [128, 2, 6])
    I['lru_lam'] = din('lru_lam', [128, 2, 6])
    I['pool_bd'] = din('pool_bd', [128, 2, 128])
    I['pool_scale'] = din('pool_scale', [128, 2])
    I['pool_invw'] = din('pool_invw', [128, 2])
    I['pool_corr'] = din('pool_corr', [128, 2, 2, 16])
    I['ffn_w_up'] = din('ffn_w_up', [2, D, 2 * DFF])
    I['ffn_w_down'] = din('ffn_w_down', [2, DFF, D])
    I['ffn_cw'] = din('ffn_cw', [128, 2, 2 * NFC, 3])
    I['ffn_cb'] = din('ffn_cb', [128, 2, 2 * NFC])
    I['cd_w_in'] = din('cd_w_in', [D, 2816])
    I['cd_w_sw'] = din('cd_w_sw', [D, 1536])
    I['cd_w_out'] = din('cd_w_out', [D, D])
    I['rope_c'] = din('rope_c', [128, T])
    I['rope_s'] = din('rope_s', [128, T])
    I['diff_l'] = din('diff_l', [1, 4, 64])
    I['subln_g'] = din('subln_g', [128, 1])
    I['conf_w'] = din('conf_w', [128, 2, 31])
    I['conf_b'] = din('conf_b', [128, 2])
    I['conf_lng'] = din('conf_lng', [128, 2])
    I['conf_lnb'] = din('conf_lnb', [128, 2])
    k.I = I

    k.out = nc.dram_tensor('out', [T, D], F32, kind="ExternalOutput").ap()
    S = {}
    for nm, TT in (('l', T), ('c', LC)):
        S['z_' + nm] = dscr('z_' + nm, [1792, PADZ + TT + PADZ])
        S['xa_' + nm] = dscr('xa_' + nm, [768, TT])
        S['hf_' + nm] = dscr('hf_' + nm, [768, TT])
        S['x05_' + nm] = dscr('x05_' + nm, [1 + TT, D])
        S['x1_' + nm] = dscr('x1_' + nm, [1 + TT + 128, D])
    S['x15'] = dscr('x15', [1 + T, D])
    S['x2'] = dscr('x2', [1 + T + 128, D])
    S['qT'] = dscr('qT', [768, T], BF16)
    S['kT'] = dscr('kT', [768, T + LC], BF16)
    S['v'] = dscr('v', [T + LC, 768], BF16)
    S['gl'] = dscr('gl', [256, PADZ + T + PADZ])
    k.S = S

    with ExitStack() as es:
        P = Prog(nc, es)
        k.P = P

        uid = [0]

        def sb(name, shape, dt, st=es):
            uid[0] += 1
            return st.enter_context(nc.sbuf_tensor(f"{name}_s{uid[0]}", list(shape), dt))

        def ps(name, shape, dt, st=es):
            uid[0] += 1
            return st.enter_context(nc.psum_tensor(f"{name}_p{uid[0]}", list(shape), dt))
        k.sb = sb
        k.ps = ps

        ident = sb('ident', [128, 128], BF16)
        P.dma('pool', ident[:], I['ident'], [], ['ident'], 'ident')
        k.ident = ident
        cst = sb('cst', [128, 8], F32)
        P.op('dve', lambda e: e.memset(cst[:, 0:1], EPS), [], ['cst'])
        P.op('dve', lambda e: e.memset(cst[:, 1:2], 1.0), [], ['cst'])
        P.op('dve', lambda e: e.memset(cst[:, 2:3], 0.0), [], ['cst'])
        k.cst = cst
        zero = sb('zero', [128, 512], F32)
        P.op('dve', lambda e: e.memset(zero[:], 0.0), [], ['zero'])
        k.zero = zero
        ones_bf = sb('ones_bf', [128, 128], BF16)
        P.op('dve', lambda e: e.memset(ones_bf[:], 1.0), [], ['ones_bf'])
        k.ones_bf = ones_bf
        ones_f = sb('ones_f', [128, 128], F32)
        P.op('dve', lambda e: e.memset(ones_f[:], 1.0), [], ['ones_f'])
        k.ones_f = ones_f

        modfm = sb('modfm', [128, 2, 2, 4, 8], F32)
        k.modfm = modfm
        k.gbc_d = nc.dram_tensor('gbc_d', [2, 2, 2, 128, D], F32, kind="Internal").ap()

        phase_adaln(k)
        P.barrier()
        if dbg:
            mdbg = nc.dram_tensor('modfm_dbg', [128, 2, 2, 4, 8], F32, kind="ExternalOutput").ap()
            P.dma('sp', mdbg, modfm[:], [], [], 'mdbg')
            gdbg = nc.dram_tensor('gbc_dbg', [2, 2, 2, 128, D], F32, kind="ExternalOutput").ap()
            P.dma('sp', gdbg, k.gbc_d, [], [], 'gdbg')
        if stop_after == 'adaln':
            return finish(k, es)

        layer0(k, stop_after)
        if stop_after is not None and stop_after.startswith('l0'):
            return finish(k, es)
        layer1(k, stop_after)
        return finish(k, es)


def finish(k, es):
    k.P.barrier()
    k.ninstr = k.P.ninstr
    return k


def phase_adaln(k):
    nc, P, I = k.nc, k.P, k.I
    with ExitStack() as st:
        sb = lambda n, s, d: k.sb(n, s, d, st)
        ps = lambda n, s, d: k.ps(n, s, d, st)
        cf = sb('ad_cf', [128, 2, 8], F32)
        P.dma('sp', cf[:], I['c_fm'], [], ['ad_cf'], 'ad_cf')
        sc = sb('ad_sc', [128, 2, 8], F32)
        P.op('act', lambda e: e.activation(out=sc[:], in_=cf[:], func=AF.Silu), ['ad_cf'], ['ad_sc'])
        rep = sb('ad_rep', [128, 2, 8, 128], F32)
        for s_ in range(2):
            for kc in range(8):
                P.op('dve', lambda e, s_=s_, kc=kc: e.tensor_copy(out=rep[:, s_, kc, :], in_=sc[:, s_, kc:kc + 1].to_broadcast([128, 128])),
                     ['ad_sc'], [f'ad_rep.{s_}.{kc}'])
        modb = sb('ad_modb', [128, 2, 6, 8], F32)
        P.dma('sp', modb[:], I['modb_fm'], [], ['ad_modb'], 'ad_modb')
        ng = sb('ad_ng', [128, 2, 2, 8], F32)
        P.dma('sp', ng[:], I['ng_fm'], [], ['ad_ng'], 'ad_ng')
        brow = sb('ad_brow', [1, 2, 6 * D], F32)
        P.dma('sp', brow[:], I['mod_b'].rearrange("(o l) n -> o l n", o=1), [], ['ad_brow'], 'ad_brow')
        wt = [sb(f'ad_w{i}', [128, 6 * D], F32) for i in range(2)]
        gst = sb('ad_gst', [128, 2, D], F32)
        facc = sb('ad_facc', [128, 32, 2], F32)
        pfm = ps('ad_pfm', [128, 32, 2], F32)
        pbc = [ps(f'ad_pbc{i}', [128, 512], F32) for i in range(4)]
        for l in range(2):
            for pss in range(2):
                for kc in range(8):
                    w = wt[kc % 2]
                    wk = f'ad_w{kc % 2}'
                    P.dma('sp', w[:], I['mod_w'][l, kc * 128:(kc + 1) * 128, :], [], [wk], wk)
                    if pss == 0:
                        jmap = [0, 1, 3, 4]
                        for jj, j in enumerate(jmap):
                            for fc in range(8):
                                col = j * D + fc * 128
                                P.op('pe', lambda e, w=w, col=col, jj=jj, fc=fc, kc=kc: e.matmul(
                                    pfm[:, jj * 8 + fc, :], lhsT=w[:, col:col + 128], rhs=sc[:, :, kc],
                                    start=True, stop=True), [wk, 'ad_sc'], ['ad_pfm'])
                        if kc == 0:
                            P.op('dve', lambda e: e.tensor_copy(out=facc[:], in_=pfm[:]), ['ad_pfm'], ['ad_facc'])
                        else:
                            P.op('dve', lambda e: e.tensor_tensor(out=facc[:], in0=pfm[:], in1=facc[:], op=ALU.add), ['ad_pfm', 'ad_facc'], ['ad_facc'])
                    if True:
                        s_ = pss
                        for nt in range(4):
                            gj = 2 if nt < 2 else 5
                            col = gj * D + (nt % 2) * 512
                            P.op('pe', lambda e, w=w, col=col, nt=nt, kc=kc, s_=s_: e.matmul(
                                pbc[nt][:], lhsT=rep[:, s_, kc, :], rhs=w[:, col:col + 512],
                                start=(kc == 0), stop=False), [wk, f'ad_rep.{s_}.{kc}'], [f'ad_pbc{nt}'])
                if pss == 0:
                    jmap = [0, 1, 3, 4]
                    for s_ in range(2):
                        for jj, j in enumerate(jmap):
                            P.op('dve', lambda e, s_=s_, jj=jj, j=j, l=l: e.tensor_tensor(
                                out=k.modfm[:, l, s_, jj, :], in0=facc[:, jj * 8:(jj + 1) * 8, s_], in1=modb[:, l, j, :], op=ALU.add),
                                ['ad_facc', 'ad_modb'], [f'modfm.{l}.{s_}.{jj}'])
                        for jj, which in ((1, 0), (3, 1)):
                            P.op('dve', lambda e, s_=s_, jj=jj, which=which, l=l: e.scalar_tensor_tensor(
                                out=k.modfm[:, l, s_, jj, :], in0=k.modfm[:, l, s_, jj, :], scalar=1.0, in1=ng[:, l, which, :],
                                op0=ALU.add, op1=ALU.mult), [f'modfm.{l}.{s_}.{jj}', 'ad_ng'], [f'modfm.{l}.{s_}.{jj}'])
                if True:
                    s_ = pss
                    for nt in range(4):
                        gj = 2 if nt < 2 else 5
                        col = gj * D + (nt % 2) * 512
                        P.op('pe', lambda e, nt=nt, col=col, l=l: e.matmul(
                            pbc[nt][:], lhsT=k.ones_f[0:1, :], rhs=brow[0:1, l, col:col + 512], start=False, stop=True),
                            ['ones_f', 'ad_brow'], [f'ad_pbc{nt}'])
                        P.op('act', lambda e, nt=nt: e.copy(out=gst[:, nt // 2, (nt % 2) * 512:(nt % 2) * 512 + 512], in_=pbc[nt][:]),
                            [f'ad_pbc{nt}'], [f'ad_gst.{nt}'])
                    for j2 in range(2):
                        P.dma('sp', k.gbc_d[l, s_, j2], gst[:, j2, :], [f'ad_gst.{2 * j2}', f'ad_gst.{2 * j2 + 1}'], [], 'ad_gst')
        P.barrier()


def load_w_bf16(k, name, dst, src_ap, nk, ncols, colblk=2048):
    P = k.P
    for kc in range(nk):
        for c0 in range(0, ncols, colblk):
            c1 = min(ncols, c0 + colblk)
            P.dma('pool', dst[:, kc, c0:c1], src_ap[kc * 128:(kc + 1) * 128, c0:c1], [], [f'{name}.{kc}'], f'{name}.{kc}')


def fold_w_bf16(k, st, name, dst, src_ap, nk, gb_dram):
    P = k.P
    stg = [k.sb(f'{name}_stg{i}', [128, D], F32, st) for i in range(2)]
    gb = k.sb(f'{name}_gb', [128, D], F32, st)
    P.dma('sp', gb[:], gb_dram, [], [f'{name}_gb'], f'{name}_gb')
    gb_ap = gb[:]
    for kc in range(nk):
        s_ = stg[kc % 2]
        sk = f'{name}_stg{kc % 2}'
        P.dma('sp', s_[:], src_ap[kc * 128:(kc + 1) * 128, :], [], [sk], sk)
        P.op('dve', lambda e, s_=s_, kc=kc: e.tensor_tensor(out=dst[:, kc, :], in0=s_[:], in1=gb_ap, op=ALU.mult),
             [sk, f'{name}_gb'], [f'{name}.{kc}'])


def modulate_tile(k, B, src_rows, n, l, s_, jsh, hT, hTname):
    P = k.P
    ng = n // 128
    X, Xn = B['xt'], B['xtname']
    ss = B['ss']
    xn = B['xn']
    for g0 in range(0, ng, 2):
        gg = min(2, ng - g0)
        P.dma('sp', X[:, 0:gg, :], src_rows[g0 * 128:(g0 + gg) * 128, :].rearrange("(g p) f -> p g f", p=128), [], [Xn], Xn)
        P.op('dve', lambda e: e.memset(ss[:, 0:2], 0.0), [], [B['ssname']])
        for g in range(gg):
            P.op('act', lambda e, g=g: e.activation(out=B['junk'][:], in_=X[:, g, :], func=AF.Square, accum_out=ss[:, g:g + 1]),
                 [Xn], [B['ssname'], B['junkname']])
        P.op('act', lambda e, gg=gg: e.activation(out=ss[:, 4:4 + gg], in_=ss[:, 0:gg], func=AF.Sqrt, bias=k.cst[:, 0:1], scale=1.0 / D),
             [B['ssname'], 'cst'], [B['ssname']])
        P.op('dve', lambda e, gg=gg: e.reciprocal(out=ss[:, 8:8 + gg], in_=ss[:, 4:4 + gg]), [B['ssname']], [B['ssname']])
        for g in range(gg):
            eng = 'dve' if g % 2 == 0 else 'pool'
            P.op(eng, lambda e, g=g, g0=g0: e.tensor_scalar(out=xn[:, g0 + g, :], in0=X[:, g, :], scalar1=ss[:, 8 + g:9 + g], scalar2=None, op0=ALU.mult),
                 [Xn, B['ssname']], [f"{B['xnname']}.{g0 + g}"])
    for fc in range(8):
        tp = B['tp'][fc % 2]
        tpn = B['tpname'][fc % 2]
        for g in range(ng):
            P.op('pe', lambda e, g=g, fc=fc, tp=tp: e.transpose(out=tp[:, g * 128:(g + 1) * 128], in_=xn[:, g, fc * 128:(fc + 1) * 128], identity=k.ident[:]),
                 [f"{B['xnname']}.{g}", 'ident'], [tpn])
        P.op('act', lambda e, fc=fc, tp=tp: e.activation(out=hT[:, fc, 0:n], in_=tp[:, 0:n], func=AF.Identity,
                                                          scale=k.modfm[:, l, s_, jsh + 1, fc:fc + 1], bias=k.modfm[:, l, s_, jsh, fc:fc + 1]),
             [tpn], [f'{hTname}.{fc}'])


def mod_bufs(k, st, pfx):
    B = {}
    B['xt'] = k.sb(pfx + 'xt', [128, 2, D], F32, st)
    B['xtname'] = pfx + 'xt'
    B['xn'] = k.sb(pfx + 'xn', [128, 4, D], BF16, st)
    B['xnname'] = pfx + 'xn'
    B['junk'] = k.sb(pfx + 'junk', [128, D], BF16, st)
    B['junkname'] = pfx + 'junk'
    B['ss'] = k.sb(pfx + 'ss', [128, 12], F32, st)
    B['ssname'] = pfx + 'ss'
    B['tp'] = [k.ps(pfx + f'tp{i}', [128, 512], BF16, st) for i in range(2)]
    B['tpname'] = [pfx + f'tp{i}' for i in range(2)]
    return B


def layer0(k, stop_after):
    nc, P, I, S = k.nc, k.P, k.I, k.S
    with ExitStack() as st:
        sb = lambda n, s, d: k.sb(n, s, d, st)
        lp = {}
        for nm, shp in (('lru_cw', [128, 6, 4]), ('lru_cb', [128, 6]), ('lru_ba', [128, 2, 6]), ('lru_bx', [128, 2, 6]),
                        ('lru_lam', [128, 2, 6]), ('pool_scale', [128, 2]), ('pool_invw', [128, 2]), ('pool_corr', [128, 2, 2, 16])):
            lp[nm] = sb('p_' + nm, shp, F32)
            P.dma('sp', lp[nm][:], I[nm], [], ['p_' + nm], 'p_' + nm)
        for nm, shp in (('lru_bdA', [128, 2, 6, 128]), ('lru_bdX', [128, 2, 6, 128]), ('pool_bd', [128, 2, 128])):
            lp[nm] = sb('p_' + nm, shp, BF16)
            P.dma('pool', lp[nm][:], I[nm], [], ['p_' + nm], 'p_' + nm)
        cl = sb('p_cl', [128, 2, 2, 6], F32)
        tmp = sb('p_cltmp', [128, 2, 6], F32)
        P.op('act', lambda e: e.activation(out=tmp[:], in_=lp['lru_lam'][:], func=AF.Exp, scale=-1.0), ['p_lru_lam'], ['p_cltmp'])
        P.op('act', lambda e: e.activation(out=tmp[:], in_=tmp[:], func=AF.Ln, bias=k.cst[:, 1:2], scale=1.0), ['p_cltmp', 'cst'], ['p_cltmp'])
        P.op('dve', lambda e: e.tensor_scalar(out=cl[:, 0, :, :], in0=tmp[:], scalar1=-8.0, scalar2=None, op0=ALU.mult), ['p_cltmp'], ['p_cl'])
        P.op('dve', lambda e: e.tensor_scalar(out=cl[:, 1, :, :], in0=tmp[:], scalar1=-16.0, scalar2=None, op0=ALU.mult), ['p_cltmp'], ['p_cl'])
        lp['cl'] = cl
        stt = sb('p_state', [128, 2, 6], F32)
        P.op('dve', lambda e: e.memset(stt[:], 0.0), [], ['p_state'])
        lp['state'] = stt
        k.lp = lp
        w_in = sb('w_in0', [128, 8, 1792], BF16)
        load_w_bf16(k, 'w_in0', w_in, I['ab_w_in'], 8, 1792, colblk=1792)
        w_out = sb('w_out0', [128, 8, D], BF16)
        k.w_in0, k.w_out0 = w_in, w_out

        for s_, nm, TT, xsrc in ((1, 'c', LC, I['ctx']), (0, 'l', k.T, I['x'])):
            seg = K()
            seg.nm, seg.T, seg.x, seg.set = nm, TT, xsrc, s_
            seg.z, seg.xa, seg.hf, seg.x05, seg.x1 = S['z_' + nm], S['xa_' + nm], S['hf_' + nm], S['x05_' + nm], S['x1_' + nm]
            seg.tiles = tiles_of(TT)
            with ExitStack() as st2:
                fold_w_bf16(k, st2, 'w_out0', w_out, I['ab_w_out'], 8, k.gbc_d[0, s_, 0])
            P.barrier()
            l0_phaseA(k, seg)
            P.barrier()
            if stop_after == 'l0A' and nm == 'l':
                return
            l0_phaseB(k, seg)
            P.barrier()
            if stop_after == 'l0B' and nm == 'l':
                return
            l0_phaseC(k, seg)
            P.barrier()
            if stop_after == 'l0C' and nm == 'l':
                return
    for s_, nm, TT in ((1, 'c', LC), (0, 'l', k.T)):
        ffn_phase(k, 0, s_, S['x05_' + nm], S['x1_' + nm], TT, final=False)
        P.barrier()


def l0_phaseA(k, seg):
    nc, P = k.nc, k.P
    with ExitStack() as st:
        sb = lambda n, s, d: k.sb(n, s, d, st)
        ps = lambda n, s, d: k.ps(n, s, d, st)
        B = mod_bufs(k, st, 'A_')
        hT = sb('A_hT', [128, 8, 512], BF16)
        zt = [sb(f'A_zt{i}', [128, 14, 512], F32) for i in range(2)]
        zp = [ps(f'A_zp{i}', [128, 512], F32) for i in range(4)]
        zv = seg.z.rearrange("(c p) t -> p c t", p=128)
        P.dma('sp', zv[:, :, 0:PADZ], k.zero[:, 0:14 * PADZ].rearrange("p (c t) -> p c t", c=14), ['zero'], [], 'A_zpad')
        P.dma('sp', zv[:, :, PADZ + seg.T:PADZ + seg.T + PADZ], k.zero[:, 0:14 * PADZ].rearrange("p (c t) -> p c t", c=14), ['zero'], [], 'A_zpad')
        for ti, (s, n) in enumerate(seg.tiles):
            modulate_tile(k, B, seg.x[s:s + n, :], n, 0, seg.set, 0, hT, 'A_hT')
            Z = zt[ti % 2]
            Zn = f'A_zt{ti % 2}'
            for mc in range(14):
                zpp = zp[mc % 4]
                for kc in range(8):
                    P.op('pe', lambda e, mc=mc, kc=kc, zpp=zpp: e.matmul(zpp[:, 0:n], lhsT=k.w_in0[:, kc, mc * 128:(mc + 1) * 128], rhs=hT[:, kc, 0:n],
                                                                         start=(kc == 0), stop=(kc == 7)),
                         [f'w_in0.{kc}', f'A_hT.{kc}'], [f'A_zp{mc % 4}'])
                eng = 'act' if mc % 2 == 0 else 'dve'
                if eng == 'act':
                    P.op('act', lambda e, mc=mc, zpp=zpp: e.copy(out=Z[:, mc, 0:n], in_=zpp[:, 0:n]), [f'A_zp{mc % 4}'], [f'{Zn}.{mc}'])
                else:
                    P.op('dve', lambda e, mc=mc, zpp=zpp: e.tensor_copy(out=Z[:, mc, 0:n], in_=zpp[:, 0:n]), [f'A_zp{mc % 4}'], [f'{Zn}.{mc}'])
            P.dma('sp', zv[:, :, PADZ + s:PADZ + s + n], Z[:, :, 0:n], keys(Zn, 14), [], Zn)


def lru_coeffs(k, C, d, n, xa, xab):
    P, lp = k.P, k.lp
    for c in range(6):
        pr, pi = C['pg'][(2 * c) % 4], C['pg'][(2 * c + 1) % 4]
        prn, pin = C['pgname'][(2 * c) % 4], C['pgname'][(2 * c + 1) % 4]
        P.op('pe', lambda e, c=c, pr=pr: e.matmul(pr[:, 0:n], lhsT=lp['lru_bdA'][:, d, c, :], rhs=xab[:, c, 0:n], start=True, stop=True),
             ['p_lru_bdA', f"{C['xabname']}.{c}"], [prn])
        P.op('pe', lambda e, c=c, pi=pi: e.matmul(pi[:, 0:n], lhsT=lp['lru_bdX'][:, d, c, :], rhs=xab[:, c, 0:n], start=True, stop=True),
             ['p_lru_bdX', f"{C['xabname']}.{c}"], [pin])
        P.op('act', lambda e, c=c, pr=pr: e.activation(out=C['r'][:, c, 0:n], in_=pr[:, 0:n], func=AF.Sigmoid, bias=lp['lru_ba'][:, d, c:c + 1], scale=1.0),
             [prn, 'p_lru_ba'], [f"{C['pfx']}r.{c}"])
        P.op('act', lambda e, c=c, pi=pi: e.activation(out=C['ig'][:, c, 0:n], in_=pi[:, 0:n], func=AF.Sigmoid, bias=lp['lru_bx'][:, d, c:c + 1], scale=1.0),
             [pin, 'p_lru_bx'], [f"{C['pfx']}ig.{c}"])
    for c in range(6):
        P.op('act', lambda e, c=c: e.activation(out=C['a'][:, c, 0:n], in_=C['r'][:, c, 0:n], func=AF.Exp, scale=lp['cl'][:, 0, d, c:c + 1]),
             [f"{C['pfx']}r.{c}", 'p_cl'], [f"{C['pfx']}a.{c}"])
        P.op('act', lambda e, c=c: e.activation(out=C['r'][:, c, 0:n], in_=C['r'][:, c, 0:n], func=AF.Exp, scale=lp['cl'][:, 1, d, c:c + 1]),
             [f"{C['pfx']}r.{c}", 'p_cl'], [f"{C['pfx']}r.{c}"])
    for c in range(6):
        P.op('act', lambda e, c=c: e.activation(out=C['r'][:, c, 0:n], in_=C['r'][:, c, 0:n], func=AF.Sqrt, bias=k.cst[:, 1:2], scale=-1.0),
             [f"{C['pfx']}r.{c}", 'cst'], [f"{C['pfx']}r.{c}"])
        P.op('dve', lambda e, c=c: e.tensor_tensor(out=C['ig'][:, c, 0:n], in0=C['ig'][:, c, 0:n], in1=C['r'][:, c, 0:n], op=ALU.mult),
             [f"{C['pfx']}r.{c}", f"{C['pfx']}ig.{c}"], [f"{C['pfx']}ig.{c}"])
        P.op('pool', lambda e, c=c: e.tensor_tensor(out=C['ig'][:, c, 0:n], in0=C['ig'][:, c, 0:n], in1=xa[:, c, 0:n], op=ALU.mult),
             [f"{C['pfx']}ig.{c}", f"{C['xaname']}.{c}"], [f"{C['pfx']}ig.{c}"])


def coeff_bufs(k, st, pfx):
    C = {'pfx': pfx}
    for nm in ('r', 'ig', 'a'):
        C[nm] = k.sb(pfx + nm, [128, 6, 512], F32, st)
    C['pg'] = [k.ps(pfx + f'pg{i}', [128, 512], F32, st) for i in range(4)]
    C['pgname'] = [pfx + f'pg{i}' for i in range(4)]
    return C


def l0_phaseB(k, seg):
    P, lp = k.P, k.lp
    with ExitStack() as st:
        sb = lambda n, s, d: k.sb(n, s, d, st)
        zin = sb('B_zin', [128, 6, 515], F32)
        xa = sb('B_xa', [128, 6, 512], F32)
        xab = sb('B_xab', [128, 6, 512], BF16)
        hf = sb('B_hf', [128, 6, 512], F32)
        C = coeff_bufs(k, st, 'B_')
        C['xabname'], C['xaname'] = 'B_xab', 'B_xa'
        zv = seg.z[0:768, :].rearrange("(c p) t -> p c t", p=128)
        xav = seg.xa.rearrange("(c p) t -> p c t", p=128)
        hfv = seg.hf.rearrange("(c p) t -> p c t", p=128)
        if seg.nm == 'c':
            P.op('dve', lambda e: e.memset(lp['state'][:], 0.0), [], ['p_state'])
        for ti, (s, n) in enumerate(seg.tiles):
            P.dma('sp', zin[:, :, 0:n + 3], zv[:, :, PADZ + s - 2:PADZ + s + n + 1], [], keys('B_zin', 6), 'B_zin')
            for c in range(6):
                P.op('act', lambda e, c=c: e.activation(out=xa[:, c, 0:n], in_=zin[:, c, 0:n], func=AF.Identity,
                                                        scale=lp['lru_cw'][:, c, 0:1], bias=lp['lru_cb'][:, c:c + 1]),
                     [f'B_zin.{c}', 'p_lru_cw', 'p_lru_cb'], [f'B_xa.{c}'])
                for t in range(1, 4):
                    P.op('dve', lambda e, c=c, t=t: e.scalar_tensor_tensor(out=xa[:, c, 0:n], in0=zin[:, c, t:t + n], scalar=lp['lru_cw'][:, c, t:t + 1],
                                                                           in1=xa[:, c, 0:n], op0=ALU.mult, op1=ALU.add),
                         [f'B_zin.{c}', f'B_xa.{c}'], [f'B_xa.{c}'])
                P.op('pool', lambda e, c=c: e.tensor_copy(out=xab[:, c, 0:n], in_=xa[:, c, 0:n]), [f'B_xa.{c}'], [f'B_xab.{c}'])
            P.dma('sp', xav[:, :, s:s + n], xa[:, :, 0:n], keys('B_xa', 6), [], 'B_xa')
            lru_coeffs(k, C, 0, n, xa, xab)
            for c in range(6):
                P.op('dve', lambda e, c=c: e.tensor_tensor_scan(out=hf[:, c, 0:n], data0=C['a'][:, c, 0:n], data1=C['ig'][:, c, 0:n],
                                                                initial=lp['state'][:, 0, c:c + 1], op0=ALU.mult, op1=ALU.add),
                     [f'B_a.{c}', f'B_ig.{c}', 'p_state'], [f'B_hf.{c}'])
                P.op('dve', lambda e, c=c: e.tensor_copy(out=lp['state'][:, 0, c:c + 1], in_=hf[:, c, n - 1:n]), [f'B_hf.{c}'], ['p_state'])
            P.dma('sp', hfv[:, :, s:s + n], hf[:, :, 0:n], keys('B_hf', 6), [], 'B_hf')


def l0_phaseC(k, seg):
    P, lp, I = k.P, k.lp, k.I
    with ExitStack() as st:
        sb = lambda n, s, d: k.sb(n, s, d, st)
        ps = lambda n, s, d: k.ps(n, s, d, st)
        xa = sb('C_xa', [128, 6, 512], F32)
        xab = sb('C_xab', [128, 6, 512], BF16)
        hb = sb('C_hb', [128, 6, 512], F32)
        hf = sb('C_hf', [128, 6, 512], F32)
        ga = sb('C_ga', [128, 6, 512], F32)
        yT = sb('C_yT', [128, 8, 512], BF16)
        zb = sb('C_zb', [128, 2, 527], F32)
        p2 = sb('C_p2', [128, 527], F32)
        p4 = sb('C_p4', [128, 527], F32)
        p8 = sb('C_p8', [128, 527], F32)
        Ssum = sb('C_S', [128, 512], F32)
        dd = sb('C_dd', [128, 2, 512], BF16)
        xt = sb('C_xt', [128, 4, D], F32)
        xo = sb('C_xo', [128, 4, D], F32)
        C = coeff_bufs(k, st, 'C_')
        C['xabname'], C['xaname'] = 'C_xab', 'C_xa'
        po = [ps(f'C_po{i}', [128, 512], F32) for i in range(4)]
        zg = seg.z[768:1536, :].rearrange("(c p) t -> p c t", p=128)
        zbv = seg.z[1536:1792, :].rearrange("(c p) t -> p c t", p=128)
        xav = seg.xa.rearrange("(c p) t -> p c t", p=128)
        hfv = seg.hf.rearrange("(c p) t -> p c t", p=128)
        if seg.nm == 'c':
            P.op('dve', lambda e: e.memset(lp['state'][:, 1, :], 0.0), [], ['p_state'])
        nt_ = len(seg.tiles)
        for ti in range(nt_ - 1, -1, -1):
            s, n = seg.tiles[ti]
            ng = n // 128
            P.dma('sp', xa[:, :, 0:n], xav[:, :, s:s + n], [], keys('C_xa', 6), 'C_xa')
            P.dma('sp', hf[:, :, 0:n], hfv[:, :, s:s + n], [], keys('C_hf', 6), 'C_hf')
            P.dma('sp', ga[:, :, 0:n], zg[:, :, PADZ + s:PADZ + s + n], [], keys('C_ga', 6), 'C_ga')
            P.dma('sp', zb[:, :, 0:n + 15], zbv[:, :, PADZ + s - 8:PADZ + s + n + 7], [], keys('C_zb', 2), 'C_zb')
            P.dma('sp', xt[:, 0:ng, :], seg.x[s:s + n, :].rearrange("(g p) f -> p g f", p=128), [], ['C_xt'], 'C_xt')
            for c in range(6):
                P.op('pool', lambda e, c=c: e.tensor_copy(out=xab[:, c, 0:n], in_=xa[:, c, 0:n]), [f'C_xa.{c}'], [f'C_xab.{c}'])
            lru_coeffs(k, C, 1, n, xa, xab)
            for c in range(6):
                P.op('dve', lambda e, c=c: e.tensor_tensor_scan(out=rev(hb[:, c, 0:n]), data0=rev(C['a'][:, c, 0:n]), data1=rev(C['ig'][:, c, 0:n]),
                                                                initial=lp['state'][:, 1, c:c + 1], op0=ALU.mult, op1=ALU.add),
                     [f'C_a.{c}', f'C_ig.{c}', 'p_state'], [f'C_hb.{c}'])
                P.op('dve', lambda e, c=c: e.tensor_copy(out=lp['state'][:, 1, c:c + 1], in_=hb[:, c, 0:1]), [f'C_hb.{c}'], ['p_state'])
                P.op('act', lambda e, c=c: e.activation(out=ga[:, c, 0:n], in_=ga[:, c, 0:n], func=AF.Gelu_apprx_tanh), [f'C_ga.{c}'], [f'C_ga.{c}'])
                P.op('pool', lambda e, c=c: e.tensor_tensor(out=hb[:, c, 0:n], in0=hb[:, c, 0:n], in1=hf[:, c, 0:n], op=ALU.add),
                     [f'C_hb.{c}', f'C_hf.{c}'], [f'C_hb.{c}'])
                P.op('dve', lambda e, c=c: e.tensor_tensor(out=yT[:, c, 0:n], in0=hb[:, c, 0:n], in1=ga[:, c, 0:n], op=ALU.mult),
                     [f'C_hb.{c}', f'C_ga.{c}'], [f'C_yT.{c}'])
            W = n + 15
            for ch in range(2):
                zc = zb[:, ch, :]
                P.op('dve', lambda e, zc=zc: e.tensor_tensor(out=p2[:, 0:W - 1], in0=zc[:, 0:W - 1], in1=zc[:, 1:W], op=ALU.add),
                     [f'C_zb.{ch}'], ['C_p2'])
                if ch == 0:
                    P.op('dve', lambda e: e.tensor_copy(out=Ssum[0:64, 0:n], in_=p2[0:64, 7:7 + n]), ['C_p2'], ['C_S'])
                    P.op('dve', lambda e: e.tensor_tensor(out=Ssum[64:128, 0:n], in0=p2[64:128, 6:6 + n], in1=p2[64:128, 8:8 + n], op=ALU.add),
                         ['C_p2'], ['C_S'])
                else:
                    P.op('dve', lambda e: e.tensor_tensor(out=p4[:, 0:W - 3], in0=p2[:, 0:W - 3], in1=p2[:, 2:W - 1], op=ALU.add), ['C_p2'], ['C_p4'])
                    P.op('dve', lambda e: e.tensor_tensor(out=Ssum[0:64, 0:n], in0=p4[0:64, 4:4 + n], in1=p4[0:64, 8:8 + n], op=ALU.add),
                         ['C_p4'], ['C_S'])
                    P.op('dve', lambda e: e.tensor_tensor(out=p8[64:128, 0:W - 7], in0=p4[64:128, 0:W - 7], in1=p4[64:128, 4:W - 3], op=ALU.add),
                         ['C_p4'], ['C_p8'])
                    P.op('dve', lambda e: e.tensor_tensor(out=Ssum[64:128, 0:n], in0=p8[64:128, 0:n], in1=p8[64:128, 8:8 + n], op=ALU.add),
                         ['C_p8'], ['C_S'])
                if ti == 0:
                    P.op('dve', lambda e, ch=ch: e.tensor_tensor(out=Ssum[:, 0:16], in0=Ssum[:, 0:16], in1=lp['pool_corr'][:, ch, 0, :], op=ALU.mult),
                         ['C_S', 'p_pool_corr'], ['C_S'])
                if ti == nt_ - 1:
                    P.op('dve', lambda e, ch=ch: e.tensor_tensor(out=Ssum[:, n - 16:n], in0=Ssum[:, n - 16:n], in1=lp['pool_corr'][:, ch, 1, :], op=ALU.mult),
                         ['C_S', 'p_pool_corr'], ['C_S'])
                P.op('dve', lambda e, ch=ch, zc=zc: e.scalar_tensor_tensor(out=dd[:, ch, 0:n], in0=Ssum[:, 0:n], scalar=lp['pool_invw'][:, ch:ch + 1],
                                                                          in1=zc[:, 8:8 + n], op0=ALU.mult, op1=ALU.subtract),
                     ['C_S', f'C_zb.{ch}', 'p_pool_invw'], [f'C_dd.{ch}'])
                pp = po[ch]
                P.op('pe', lambda e, ch=ch, pp=pp: e.matmul(pp[:, 0:n], lhsT=lp['pool_bd'][:, ch, :], rhs=dd[:, ch, 0:n], start=True, stop=True),
                     ['p_pool_bd', f'C_dd.{ch}'], [f'C_po{ch}'])
                P.op('act', lambda e, ch=ch, pp=pp: e.activation(out=yT[:, 6 + ch, 0:n], in_=pp[:, 0:n], func=AF.Identity, scale=lp['pool_scale'][:, ch:ch + 1], bias=k.cst[:, 2:3]),
                     [f'C_po{ch}', 'p_pool_scale', 'cst'], [f'C_yT.{6 + ch}'])
            for g in range(ng):
                for nt in range(2):
                    pp = po[(2 * g + nt) % 4]
                    ppn = f'C_po{(2 * g + nt) % 4}'
                    for kc in range(8):
                        P.op('pe', lambda e, g=g, nt=nt, kc=kc, pp=pp: e.matmul(pp[:, :], lhsT=yT[:, kc, g * 128:(g + 1) * 128], rhs=k.w_out0[:, kc, nt * 512:(nt + 1) * 512],
                                                                               start=(kc == 0), stop=(kc == 7)),
                             [f'C_yT.{kc}', f'w_out0.{kc}'], [ppn])
                    P.op('dve', lambda e, g=g, nt=nt, pp=pp: e.tensor_tensor(out=xo[:, g, nt * 512:(nt + 1) * 512], in0=pp[:, :], in1=xt[:, g, nt * 512:(nt + 1) * 512], op=ALU.add),
                         [ppn, 'C_xt'], [f'C_xo.{g}'])
            P.dma('sp', seg.x05[1 + s:1 + s + n, :].rearrange("(g p) f -> p g f", p=128), xo[:, 0:ng, :], keys('C_xo', 4)[0:ng], [], 'C_xo')


def ffn_phase(k, l, s_, src, dst, TT, final):
    nc, P, I = k.nc, k.P, k.I
    tiles = tiles_of(TT)
    with ExitStack() as st:
        sb = lambda n, s, d: k.sb(n, s, d, st)
        ps = lambda n, s, d: k.ps(n, s, d, st)
        w_up = sb('F_wup', [128, 8, 2 * DFF], BF16)
        load_w_bf16(k, 'F_wup', w_up, I['ffn_w_up'][l], 8, 2 * DFF)
        w_dn = sb('F_wdn', [128, NFC, D], BF16)
        with ExitStack() as st2:
            fold_w_bf16(k, st2, 'F_wdn', w_dn, I['ffn_w_down'][l], NFC, k.gbc_d[l, s_, 1])
            P.barrier()
        cw = sb('F_cw', [128, 2 * NFC, 3], F32)
        cb = sb('F_cb', [128, 2 * NFC], F32)
        P.dma('sp', cw[:], I['ffn_cw'][:, l, :, :], [], ['F_cw'], 'F_cw')
        P.dma('sp', cb[:], I['ffn_cb'][:, l, :], [], ['F_cb'], 'F_cb')
        B = mod_bufs(k, st, 'F_')
        hT = sb('F_hT', [128, 8, 512], BF16)
        gT = sb('F_gT', [128, NFC, 512], BF16)
        prevu = [sb(f'F_prevu{i}', [128, 2 * NFC, 2], F32) for i in range(2)]
        P.op('dve', lambda e: e.memset(prevu[0][:], 0.0), [], keys('F_prevu0', 2 * NFC))
        acc = [sb(f'F_acc{i}', [128, 512], F32) for i in range(4)]
        xs = sb('F_xs', [128, 1, D], F32)
        xo = xs
        pu = [ps(f'F_pu{i}', [128, 512], F32) for i in range(4)]
        pd = [ps(f'F_pd{i}', [128, 512], F32) for i in range(2)]
        if final:
            fg = sb('F_fg', [128, D], F32)
            P.dma('sp', fg[:], I['final_g_bc'], [], ['F_fg'], 'F_fg')
            fss = sb('F_fss', [128, 12], F32)
            fjunk = B['junk']

        def conv_gate(n, zero_u, ti):
            pin, pout = prevu[ti % 2], prevu[(ti + 1) % 2]
            pinn, poutn = f'F_prevu{ti % 2}', f'F_prevu{(ti + 1) % 2}'
            for c in range(NFC):
                q = c % 2
                AA = [acc[2 * q], acc[2 * q + 1]]
                AN = [f'F_acc{2 * q}', f'F_acc{2 * q + 1}']
                PP = [pu[2 * q], pu[2 * q + 1]]
                PN = [f'F_pu{2 * q}', f'F_pu{2 * q + 1}']
                CC = [c, NFC + c]
                if not zero_u:
                    for vi in range(2):
                        for kc in range(8):
                            P.op('pe', lambda e, kc=kc, cc=CC[vi], pp=PP[vi]: e.matmul(pp[:, 0:n], lhsT=w_up[:, kc, cc * 128:(cc + 1) * 128], rhs=hT[:, kc, 0:n],
                                                                                      start=(kc == 0), stop=(kc == 7)),
                                 [f'F_wup.{kc}', f'F_hT.{kc}'], [PN[vi]])
                    for vi in range(2):
                        P.op('act', lambda e, A_=AA[vi], pp=PP[vi], cc=CC[vi]: e.activation(out=A_[:, 0:n], in_=pp[:, 0:n], func=AF.Identity,
                                                                                          scale=cw[:, cc, 2:3], bias=cb[:, cc:cc + 1]),
                             [PN[vi], 'F_cw', 'F_cb'], [AN[vi]])
                    for vi in range(2):
                        if os.environ.get('DBG_SKIP_SAVE'):
                            continue
                        P.op('dve', lambda e, pp=PP[vi], cc=CC[vi]: e.tensor_copy(out=pout[:, cc, :], in_=pp[:, n - 2:n]), [PN[vi]], [f'{poutn}.{CC[vi]}'])
                    for vi in range(2):
                        P.op('dve', lambda e, A_=AA[vi], pp=PP[vi], cc=CC[vi]: e.scalar_tensor_tensor(out=A_[:, 1:n], in0=pp[:, 0:n - 1], scalar=cw[:, cc, 1:2], in1=A_[:, 1:n],
                                                                                                    op0=ALU.mult, op1=ALU.add), [PN[vi], AN[vi]], [AN[vi]])
                    for vi in range(2):
                        P.op('dve', lambda e, A_=AA[vi], pp=PP[vi], cc=CC[vi]: e.scalar_tensor_tensor(out=A_[:, 2:n], in0=pp[:, 0:n - 2], scalar=cw[:, cc, 0:1], in1=A_[:, 2:n],
                                                                                                    op0=ALU.mult, op1=ALU.add), [PN[vi], AN[vi]], [AN[vi]])
                else:
                    for vi in range(2):
                        P.op('act', lambda e, A_=AA[vi], cc=CC[vi]: e.activation(out=A_[:, 0:n], in_=k.zero[:, 0:n], func=AF.Identity,
                                                                                scale=cw[:, cc, 2:3], bias=cb[:, cc:cc + 1]),
                             ['zero', 'F_cw', 'F_cb'], [AN[vi]])
                for vi in range(2):
                    if os.environ.get('DBG_SKIP_TINY'):
                        continue
                    P.op('dve', lambda e, A_=AA[vi], cc=CC[vi]: e.scalar_tensor_tensor(out=A_[:, 0:2], in0=pin[:, cc, :], scalar=cw[:, cc, 0:1], in1=A_[:, 0:2],
                                                                                     op0=ALU.mult, op1=ALU.add), [f'{pinn}.{CC[vi]}', AN[vi]], [AN[vi]])
                for vi in range(2):
                    if os.environ.get('DBG_SKIP_TINY1'):
                        continue
                    P.op('dve', lambda e, A_=AA[vi], cc=CC[vi]: e.scalar_tensor_tensor(out=A_[:, 0:1], in0=pin[:, cc, 1:2], scalar=cw[:, cc, 1:2], in1=A_[:, 0:1],
                                                                                     op0=ALU.mult, op1=ALU.add), [f'{pinn}.{CC[vi]}', AN[vi]], [AN[vi]])
                P.op('act', lambda e, A_=AA[1]: e.activation(out=A_[:, 0:n], in_=A_[:, 0:n], func=AF.Silu), [AN[1]], [AN[1]])
                P.op('pool', lambda e, c=c, A0=AA[0], A1=AA[1]: e.tensor_tensor(out=gT[:, c, 0:n], in0=A0[:, 0:n], in1=A1[:, 0:n], op=ALU.mult),
                     [AN[0], AN[1]], [f'F_gT.{c}'])

        def down_res(tok0, n, nrows_last=128):
            ng = n // 128
            nr = lambda g: (nrows_last if g == ng - 1 else 128)
            for g_ in range(ng):
                g = 0
                r0 = 1 + tok0 + g_ * 128
                P.dma('sp', xs[0:nr(g_), g, :], src[r0:r0 + nr(g_), :], [], [f'F_xs.{g}'], f'F_xs{g}')
                for nt in range(2):
                    pp = pd[nt]
                    for kc in range(NFC):
                        P.op('pe', lambda e, g_=g_, nt=nt, kc=kc, pp=pp: e.matmul(pp[:, :], lhsT=gT[:, kc, g_ * 128:(g_ + 1) * 128], rhs=w_dn[:, kc, nt * 512:(nt + 1) * 512],
                                                                               start=(kc == 0), stop=(kc == NFC - 1)),
                             [f'F_gT.{kc}', f'F_wdn.{kc}'], [f'F_pd{nt}'])
                    P.op('dve', lambda e, g=g, g_=g_, nt=nt, pp=pp: e.tensor_tensor(out=xo[0:nr(g_), g, nt * 512:(nt + 1) * 512], in0=pp[0:nr(g_), :],
                                                                            in1=xs[0:nr(g_), g, nt * 512:(nt + 1) * 512], op=ALU.add),
                         [f'F_pd{nt}', f'F_xs.{g}'], [f'F_xs.{g}'])
                if final:
                    P.op('dve', lambda e: e.memset(fss[:, 0:1], 0.0), [], ['F_fss'])
                    P.op('act', lambda e, g=g: e.activation(out=fjunk[:], in_=xo[:, g, :], func=AF.Square, accum_out=fss[:, 0:1]),
                         [f'F_xs.{g}'], ['F_fss', 'F_junk'])
                    P.op('act', lambda e: e.activation(out=fss[:, 1:2], in_=fss[:, 0:1], func=AF.Sqrt, bias=k.cst[:, 0:1], scale=1.0 / D),
                         ['F_fss', 'cst'], ['F_fss'])
                    P.op('dve', lambda e: e.reciprocal(out=fss[:, 2:3], in_=fss[:, 1:2]), ['F_fss'], ['F_fss'])
                    P.op('dve', lambda e, g=g: e.scalar_tensor_tensor(out=xo[:, g, :], in0=xo[:, g, :], scalar=fss[:, 2:3], in1=fg[:],
                                                                     op0=ALU.mult, op1=ALU.mult), [f'F_xs.{g}', 'F_fss', 'F_fg'], [f'F_xs.{g}'])
                t0 = tok0 + g_ * 128
                if final:
                    lo = max(t0, 0)
                    hi = min(t0 + nr(g_), TT)
                    if hi > lo:
                        P.dma('sp', k.out[lo:hi, :], xo[lo - t0:hi - t0, g, :], [f'F_xs.{g}'], [], f'F_xs{g}')
                else:
                    P.dma('sp', dst[1 + t0:1 + t0 + nr(g_), :], xo[0:nr(g_), g, :], [f'F_xs.{g}'], [], f'F_xs{g}')

        for ti, (s, n) in enumerate(tiles):
            modulate_tile(k, B, src[1 + s:1 + s + n, :], n, l, s_, 2, hT, 'F_hT')
            conv_gate(n, False, ti)
            down_res(s - 1, n)
        conv_gate(128, True, len(tiles))
        down_res(TT - 1, 128, nrows_last=1)


def layer1(k, stop_after):
    nc, P, I, S = k.nc, k.P, k.I, k.S
    T = k.T
    TK = T + LC
    NKB = TK // 128
    yc_d = nc.dram_tensor('yc_d', [768, T], BF16, kind="Internal").ap()
    with ExitStack() as st:
        sb = lambda n, s, d: k.sb(n, s, d, st)
        ps = lambda n, s, d: k.ps(n, s, d, st)
        w_in = sb('w_in1', [128, 8, 2816], BF16)
        load_w_bf16(k, 'w_in1', w_in, I['cd_w_in'], 8, 2816, colblk=1408)
        w_sw = sb('w_sw1', [128, 8, 1536], BF16)
        load_w_bf16(k, 'w_sw1', w_sw, I['cd_w_sw'], 8, 1536, colblk=1536)
        B = mod_bufs(k, st, 'E_')
        hT = sb('E_hT', [128, 8, 512], BF16)
        qk = sb('E_qk', [128, 12, 512], BF16)
        vt = sb('E_vt', [128, 4, 768], BF16)
        gl = sb('E_gl', [128, 2, 512], F32)
        rc = sb('E_rc', [128, 512], F32)
        rs = sb('E_rs', [128, 512], F32)
        t1 = sb('E_t1', [128, 512], F32)
        t2 = sb('E_t2', [128, 512], F32)
        sg = sb('E_sg', [128, 512], F32)
        pa = [ps(f'E_pa{i}', [128, 512], F32) for i in range(2)]
        pb = [ps(f'E_pb{i}', [128, 512], F32) for i in range(2)]
        glv = S['gl'].rearrange("(c p) t -> p c t", p=128)
        P.dma('sp', glv[:, :, 0:PADZ], k.zero[:, 0:2 * PADZ].rearrange("p (c t) -> p c t", c=2), ['zero'], [], 'E_glpad')
        P.dma('sp', glv[:, :, PADZ + T:PADZ + T + PADZ], k.zero[:, 0:2 * PADZ].rearrange("p (c t) -> p c t", c=2), ['zero'], [], 'E_glpad')
        qTv = S['qT'].rearrange("(c p) t -> p c t", p=128)
        kTv = S['kT'].rearrange("(c p) t -> p c t", p=128)

        def proj_plain(cols0, nch, dst, dstname, d0, n):
            for c in range(nch):
                pp = pa[c % 2]
                for kc in range(8):
                    P.op('pe', lambda e, c=c, kc=kc, pp=pp: e.matmul(pp[:, 0:n], lhsT=w_in[:, kc, cols0 + c * 128:cols0 + (c + 1) * 128], rhs=hT[:, kc, 0:n],
                                                                    start=(kc == 0), stop=(kc == 7)), [f'w_in1.{kc}', f'E_hT.{kc}'], [f'E_pa{c % 2}'])
                P.op('act', lambda e, c=c, pp=pp: e.copy(out=dst[:, d0 + c, 0:n], in_=pp[:, 0:n]), [f'E_pa{c % 2}'], [f'{dstname}.{d0 + c}'])

        def proj_v(n):
            ng = n // 128
            for g in range(ng):
                for (c0, cn, pp, ppn) in ((0, 512, pa[g % 2], f'E_pa{g % 2}'), (512, 256, pb[g % 2], f'E_pb{g % 2}')):
                    for kc in range(8):
                        P.op('pe', lambda e, g=g, kc=kc, pp=pp, c0=c0, cn=cn: e.matmul(pp[:, 0:cn], lhsT=hT[:, kc, g * 128:(g + 1) * 128],
                                                                                       rhs=w_in[:, kc, 1536 + c0:1536 + c0 + cn], start=(kc == 0), stop=(kc == 7)),
                             [f'w_in1.{kc}', f'E_hT.{kc}'], [ppn])
                    P.op('dve', lambda e, g=g, pp=pp, c0=c0, cn=cn: e.tensor_copy(out=vt[:, g, c0:c0 + cn], in_=pp[:, 0:cn]), [ppn], [f'E_vt.{g}'])

        n = LC
        modulate_tile(k, B, S['x1_c'][1:1 + LC, :], n, 1, 1, 0, hT, 'E_hT')
        proj_plain(768, 6, qk, 'E_qk', 6, n)
        P.dma('sp', kTv[:, :, T:T + n], qk[:, 6:12, 0:n], keys('E_qk', 12)[6:12], [], 'E_qk')
        proj_v(n)
        P.dma('sp', S['v'][T:T + n, :].rearrange("(g p) f -> p g f", p=128), vt[:, 0:n // 128, :], keys('E_vt', 4), [], 'E_vt')
        for ti, (s, n) in enumerate(tiles_of(T)):
            modulate_tile(k, B, S['x1_l'][1 + s:1 + s + n, :], n, 1, 0, 0, hT, 'E_hT')
            P.dma('sp', rc[:, 0:n], I['rope_c'][:, s:s + n], [], ['E_rc'], 'E_rc')
            P.dma('sp', rs[:, 0:n], I['rope_s'][:, s:s + n], [], ['E_rs'], 'E_rs')
            for c in range(12):
                pp, pq = pa[c % 2], pb[c % 2]
                for kc in range(8):
                    P.op('pe', lambda e, c=c, kc=kc, pp=pp: e.matmul(pp[:, 0:n], lhsT=w_in[:, kc, c * 128:(c + 1) * 128], rhs=hT[:, kc, 0:n],
                                                                    start=(kc == 0), stop=(kc == 7)), [f'w_in1.{kc}', f'E_hT.{kc}'], [f'E_pa{c % 2}'])
                for kc in range(8):
                    P.op('pe', lambda e, c=c, kc=kc, pq=pq: e.matmul(pq[:, 0:n], lhsT=w_sw[:, kc, c * 128:(c + 1) * 128], rhs=hT[:, kc, 0:n],
                                                                    start=(kc == 0), stop=(kc == 7)), [f'w_sw1.{kc}', f'E_hT.{kc}'], [f'E_pb{c % 2}'])
                P.op('dve', lambda e, pp=pp: e.tensor_tensor(out=t1[:, 0:n], in0=pp[:, 0:n], in1=rc[:, 0:n], op=ALU.mult), [f'E_pa{c % 2}', 'E_rc'], ['E_t1'])
                P.op('dve', lambda e, pq=pq: e.tensor_tensor(out=t2[:, 0:n], in0=pq[:, 0:n], in1=rs[:, 0:n], op=ALU.mult), [f'E_pb{c % 2}', 'E_rs'], ['E_t2'])
                P.op('pool', lambda e, c=c: e.tensor_tensor(out=qk[:, c, 0:n], in0=t1[:, 0:n], in1=t2[:, 0:n], op=ALU.add), ['E_t1', 'E_t2'], [f'E_qk.{c}'])
            P.dma('sp', qTv[:, :, s:s + n], qk[:, 0:6, 0:n], keys('E_qk', 12)[0:6], [], 'E_q')
            P.dma('sp', kTv[:, :, s:s + n], qk[:, 6:12, 0:n], keys('E_qk', 12)[6:12], [], 'E_qk')
            proj_v(n)
            P.dma('sp', S['v'][s:s + n, :].rearrange("(g p) f -> p g f", p=128), vt[:, 0:n // 128, :], keys('E_vt', 4), [], 'E_vt')
            for c in range(2):
                pp, pq = pa[c % 2], pb[c % 2]
                for (pz, pzn, cols) in ((pp, f'E_pa{c % 2}', 2304 + c * 128), (pq, f'E_pb{c % 2}', 2304 + 256 + c * 128)):
                    for kc in range(8):
                        P.op('pe', lambda e, kc=kc, pz=pz, cols=cols: e.matmul(pz[:, 0:n], lhsT=w_in[:, kc, cols:cols + 128], rhs=hT[:, kc, 0:n],
                                                                              start=(kc == 0), stop=(kc == 7)), [f'w_in1.{kc}', f'E_hT.{kc}'], [pzn])
                P.op('act', lambda e, pq=pq: e.activation(out=sg[:, 0:n], in_=pq[:, 0:n], func=AF.Sigmoid), [f'E_pb{c % 2}'], ['E_sg'])
                P.op('dve', lambda e, c=c, pp=pp: e.tensor_tensor(out=gl[:, c, 0:n], in0=pp[:, 0:n], in1=sg[:, 0:n], op=ALU.mult), [f'E_pa{c % 2}', 'E_sg'], [f'E_gl.{c}'])
            P.dma('sp', glv[:, :, PADZ + s:PADZ + s + n], gl[:, :, 0:n], keys('E_gl', 2), [], 'E_gl')
    P.barrier()
    if stop_after == 'l1E':
        return

    with ExitStack() as st:
        sb = lambda n, s, d: k.sb(n, s, d, st)
        ps = lambda n, s, d: k.ps(n, s, d, st)
        dl = sb('G_dl', [1, 4, 64], F32)
        P.dma('sp', dl[:], I['diff_l'], [], ['G_dl'], 'G_dl')
        sm = sb('G_sm', [1, 8], F32)
        pr_ = sb('G_pr', [1, 2, 64], F32)
        P.op('dve', lambda e: e.tensor_tensor(out=pr_[:, 0, :], in0=dl[:, 0, :], in1=dl[:, 1, :], op=ALU.mult), ['G_dl'], ['G_pr'])
        P.op('dve', lambda e: e.tensor_tensor(out=pr_[:, 1, :], in0=dl[:, 2, :], in1=dl[:, 3, :], op=ALU.mult), ['G_dl'], ['G_pr'])
        P.op('dve', lambda e: e.reduce_sum(out=sm[:, 0:1], in_=pr_[:, 0, :], axis=AX.X), ['G_pr'], ['G_sm'])
        P.op('dve', lambda e: e.reduce_sum(out=sm[:, 1:2], in_=pr_[:, 1, :], axis=AX.X), ['G_pr'], ['G_sm'])
        P.op('act', lambda e: e.activation(out=sm[:, 2:4], in_=sm[:, 0:2], func=AF.Exp), ['G_sm'], ['G_sm'])
        P.op('dve', lambda e: e.tensor_tensor(out=sm[:, 4:5], in0=sm[:, 3:4], in1=sm[:, 2:3], op=ALU.subtract), ['G_sm'], ['G_sm'])
        P.op('dve', lambda e: e.tensor_scalar(out=sm[:, 5:6], in0=sm[:, 4:5], scalar1=-LAMBDA_INIT1, scalar2=None, op0=ALU.add), ['G_sm'], ['G_sm'])
        neglam = sb('G_neglam', [128, 2], F32)
        gsub = sb('G_gsub', [128, 2], F32)
        P.dma('sp', gsub[:, 0:1], I['subln_g'], [], ['G_gsub'], 'G_gsub')
        P.op('dve', lambda e: e.tensor_scalar(out=gsub[:, 1:2], in0=gsub[:, 0:1], scalar1=(1.0 - LAMBDA_INIT1), scalar2=None, op0=ALU.mult), ['G_gsub'], ['G_gsub'])
        pS = [ps(f'G_pS{i}', [128, 2, 512], F32) for i in range(2)]
        po = [ps(f'G_po{i}', [128, 512], F32) for i in range(2)]
        pl = ps('G_pl', [128, 2, 512], F32)
        P.op('pe', lambda e: e.matmul(pl[:, 0, 0:1], lhsT=k.ones_f[0:1, :], rhs=sm[0:1, 5:6], start=True, stop=True), ['ones_f', 'G_sm'], ['G_pl'])
        P.op('dve', lambda e: e.tensor_copy(out=neglam[:, 0:1], in_=pl[:, 0, 0:1]), ['G_pl'], ['G_neglam'])
        kh = [[sb(f'G_kh{j}_{i}', [128, TK], BF16) for i in range(2)] for j in range(2)]
        for j in range(2):
            P.op('pool', lambda e, j=j: e.memset(kh[j][0][64:128, :], 0.0), [], [f'G_kh{j}_0'])
            P.op('pool', lambda e, j=j: e.memset(kh[j][1][0:64, :], 0.0), [], [f'G_kh{j}_1'])
        vh = [sb(f'G_vh{j}', [128, NKB, 128], BF16) for j in range(2)]
        qh = [sb(f'G_qh{j}', [128, 512], BF16) for j in range(2)]
        pT = [sb(f'G_pT{i}', [128, 2, 512], BF16) for i in range(2)]
        accs = [sb(f'G_acc{i}', [128, 2, 512], F32) for i in range(2)]
        rl = sb('G_rl', [128, 2, 512], F32)
        o1 = sb('G_o1', [128, 512], F32)
        o2 = sb('G_o2', [128, 512], F32)
        sq = sb('G_sq', [128, 512], F32)
        ych = sb('G_ych', [128, 512], BF16)
        vv = S['v'].rearrange("(kb p) f -> p kb f", p=128)
        qtiles = tiles_of(T)

        def load_head(h):
            j = h % 2
            P.dma('sp', kh[j][0][0:64, :], S['kT'][h * 128:h * 128 + 64, :], [], [f'G_kh{j}_0'], f'G_kh{j}_0')
            P.dma('sp', kh[j][1][64:128, :], S['kT'][h * 128 + 64:h * 128 + 128, :], [], [f'G_kh{j}_1'], f'G_kh{j}_1')
            for b0 in range(0, NKB, 16):
                b1 = min(NKB, b0 + 16)
                P.dma('sp', vh[j][:, b0:b1, :], vv[:, b0:b1, h * 128:(h + 1) * 128], [], [f'G_vh{j}'], f'G_vh{j}')

        def load_q(h, ti):
            s, n = qtiles[ti]
            gi = (h * len(qtiles) + ti) % 2
            P.dma('sp', qh[gi][:, 0:n], S['qT'][h * 128:(h + 1) * 128, s:s + n], [], [f'G_qh{gi}'], f'G_qh{gi}')

        load_head(0)
        load_q(0, 0)
        for h in range(6):
            hj = h % 2
            for ti, (s, n) in enumerate(qtiles):
                gi = (h * len(qtiles) + ti) % 2
                qcur = qh[gi]
                qn_ = f'G_qh{gi}'
                if ti + 1 < len(qtiles):
                    load_q(h, ti + 1)
                elif h + 1 < 6:
                    load_q(h + 1, 0)
                if ti == 0 and h + 1 < 6:
                    load_head(h + 1)

                def emit_qk(kb):
                    pp = pS[kb % 2]
                    for comp in range(2):
                        P.op('pe', lambda e, comp=comp, kb=kb, pp=pp: e.matmul(pp[:, comp, 0:n], lhsT=kh[hj][comp][:, kb * 128:(kb + 1) * 128], rhs=qcur[:, 0:n], start=True, stop=True),
                             [f'G_kh{hj}_{comp}', qn_], [f'G_pS{kb % 2}'])
                emit_qk(0)
                first = [True, True]
                for kb in range(NKB):
                    if kb + 1 < NKB:
                        emit_qk(kb + 1)
                    pp = pS[kb % 2]
                    ppn = f'G_pS{kb % 2}'
                    pt = pT[kb % 2]
                    ptn = f'G_pT{kb % 2}'
                    P.op('act', lambda e, pp=pp, pt=pt: e.activation(out=pt[:, :, 0:n], in_=pp[:, :, 0:n], func=AF.Exp, scale=0.125), [ppn], [ptn])
                    for comp in range(2):
                        P.op('pe', lambda e, comp=comp, kb=kb, pt=pt: e.matmul(po[comp][:, 0:n], lhsT=vh[hj][:, kb, :], rhs=pt[:, comp, 0:n], start=(kb == 0), stop=(kb == NKB - 1)),
                             [f'G_vh{hj}', ptn], [f'G_po{comp}'])
                    ai = 1 if kb % 3 == 2 else 0
                    ae = 'pool' if ai == 1 else 'dve'
                    ac = accs[ai]
                    acn = f'G_acc{ai}'
                    if first[ai]:
                        first[ai] = False
                        P.op(ae, lambda e, ac=ac, pt=pt: e.tensor_copy(out=ac[:, :, 0:n], in_=pt[:, :, 0:n]), [ptn], [acn])
                    else:
                        P.op(ae, lambda e, ac=ac, pt=pt: e.tensor_tensor(out=ac[:, :, 0:n], in0=ac[:, :, 0:n], in1=pt[:, :, 0:n], op=ALU.add), [ptn, acn], [acn])
                for comp in range(2):
                    for i in range(2):
                        P.op('pe', lambda e, comp=comp, i=i: e.matmul(pl[:, comp, 0:n], lhsT=k.ones_f[:], rhs=accs[i][:, comp, 0:n], start=(i == 0), stop=(i == 1)),
                             ['ones_f', f'G_acc{i}'], ['G_pl'])
                P.op('dve', lambda e: e.reciprocal(out=rl[:, :, 0:n], in_=pl[:, :, 0:n]), ['G_pl'], ['G_rl'])
                P.op('dve', lambda e: e.tensor_tensor(out=o1[:, 0:n], in0=po[0][:, 0:n], in1=rl[:, 0, 0:n], op=ALU.mult), ['G_po0', 'G_rl'], ['G_o1'])
                P.op('dve', lambda e: e.tensor_tensor(out=o2[:, 0:n], in0=po[1][:, 0:n], in1=rl[:, 1, 0:n], op=ALU.mult), ['G_po1', 'G_rl'], ['G_o2'])
                P.op('dve', lambda e: e.scalar_tensor_tensor(out=o1[:, 0:n], in0=o2[:, 0:n], scalar=neglam[:, 0:1], in1=o1[:, 0:n], op0=ALU.mult, op1=ALU.add),
                     ['G_o1', 'G_o2', 'G_neglam'], ['G_o1'])
                P.op('act', lambda e: e.activation(out=sq[:, 0:n], in_=o1[:, 0:n], func=AF.Square), ['G_o1'], ['G_sq'])
                P.op('pe', lambda e: e.matmul(pl[:, 0, 0:n], lhsT=k.ones_f[:], rhs=sq[:, 0:n], start=True, stop=True), ['ones_f', 'G_sq'], ['G_pl'])
                P.op('act', lambda e: e.activation(out=sq[:, 0:n], in_=pl[:, 0, 0:n], func=AF.Sqrt, bias=k.cst[:, 0:1], scale=1.0 / 128), ['G_pl', 'cst'], ['G_sq'])
                P.op('dve', lambda e: e.reciprocal(out=sq[:, 0:n], in_=sq[:, 0:n]), ['G_sq'], ['G_sq'])
                P.op('dve', lambda e: e.scalar_tensor_tensor(out=ych[:, 0:n], in0=o1[:, 0:n], scalar=gsub[:, 1:2], in1=sq[:, 0:n], op0=ALU.mult, op1=ALU.mult),
                     ['G_o1', 'G_sq', 'G_gsub'], ['G_ych'])
                P.dma('sp', yc_d[h * 128:(h + 1) * 128, s:s + n], ych[:, 0:n], ['G_ych'], [], 'G_ych')
    P.barrier()
    if stop_after == 'l1F1':
        return

    with ExitStack() as st:
        sb = lambda n, s, d: k.sb(n, s, d, st)
        ps = lambda n, s, d: k.ps(n, s, d, st)
        w_out = sb('w_out1', [128, 8, D], BF16)
        with ExitStack() as st2:
            fold_w_bf16(k, st2, 'w_out1', w_out, I['cd_w_out'], 8, k.gbc_d[1, 0, 0])
            P.barrier()
        cp = {}
        for nm, shp in (('conf_w', [128, 2, 31]), ('conf_b', [128, 2]), ('conf_lng', [128, 2]), ('conf_lnb', [128, 2])):
            cp[nm] = sb('H_' + nm, shp, F32)
            P.dma('sp', cp[nm][:], I[nm], [], ['H_' + nm], 'H_' + nm)
        yT = sb('H_yT', [128, 8, 512], BF16)
        gin = sb('H_gin', [128, 2, 542], F32)
        ca = sb('H_ca', [128, 512], F32)
        cb_ = sb('H_cb', [128, 512], F32)
        ct = [sb(f'H_ct{i}', [128, 512], F32) for i in range(2)]
        xm = sb('H_xm', [128, 2, 512], F32)
        sq = sb('H_sq', [128, 2, 512], F32)
        rstd = sb('H_rstd', [128, 512], F32)
        xt = sb('H_xt', [128, 4, D], F32)
        xo = sb('H_xo', [128, 4, D], F32)
        pm = ps('H_pm', [128, 512], F32)
        pv = ps('H_pv', [128, 512], F32)
        po = [ps(f'H_po{i}', [128, 512], F32) for i in range(4)]
        glv = S['gl'].rearrange("(c p) t -> p c t", p=128)
        ycv = yc_d.rearrange("(c p) t -> p c t", p=128)
        for (s, n) in tiles_of(T):
            ng = n // 128
            P.dma('sp', yT[:, 0:6, 0:n], ycv[:, :, s:s + n], [], keys('H_yT', 8)[0:6], 'H_yT')
            P.dma('sp', gin[:, :, 0:n + 30], glv[:, :, PADZ + s - 15:PADZ + s + n + 15], [], keys('H_gin', 2), 'H_gin')
            P.dma('sp', xt[:, 0:ng, :], S['x1_l'][1 + s:1 + s + n, :].rearrange("(g p) f -> p g f", p=128), [], ['H_xt'], 'H_xt')
            for c in range(2):
                P.op('act', lambda e, c=c: e.activation(out=ca[:, 0:n], in_=gin[:, c, 0:n], func=AF.Identity, scale=cp['conf_w'][:, c, 0:1], bias=cp['conf_b'][:, c:c + 1]),
                     [f'H_gin.{c}', 'H_conf_w', 'H_conf_b'], ['H_ca'])
                P.op('pool', lambda e, c=c: e.tensor_scalar(out=cb_[:, 0:n], in0=gin[:, c, 1:1 + n], scalar1=cp['conf_w'][:, c, 1:2], scalar2=None, op0=ALU.mult),
                     [f'H_gin.{c}', 'H_conf_w'], ['H_cb'])
                for t in range(2, 31):
                    if t % 2 == 0:
                        P.op('dve', lambda e, c=c, t=t: e.scalar_tensor_tensor(out=ca[:, 0:n], in0=gin[:, c, t:t + n], scalar=cp['conf_w'][:, c, t:t + 1], in1=ca[:, 0:n],
                                                                               op0=ALU.mult, op1=ALU.add), [f'H_gin.{c}', 'H_ca'], ['H_ca'])
                    else:
                        ctt = ct[(t // 2) % 2]
                        ctn = f'H_ct{(t // 2) % 2}'
                        P.op('act', lambda e, c=c, t=t, ctt=ctt: e.activation(out=ctt[:, 0:n], in_=gin[:, c, t:t + n], func=AF.Identity, scale=cp['conf_w'][:, c, t:t + 1], bias=k.cst[:, 2:3]),
                             [f'H_gin.{c}', 'H_conf_w', 'cst'], [ctn])
                        P.op('pool', lambda e, ctt=ctt: e.tensor_tensor(out=cb_[:, 0:n], in0=cb_[:, 0:n], in1=ctt[:, 0:n], op=ALU.add), [ctn, 'H_cb'], ['H_cb'])
                P.op('dve', lambda e, c=c: e.tensor_tensor(out=xm[:, c, 0:n], in0=ca[:, 0:n], in1=cb_[:, 0:n], op=ALU.add), ['H_ca', 'H_cb'], [f'H_xm.{c}'])
            for c in range(2):
                P.op('pe', lambda e, c=c: e.matmul(pm[:, 0:n], lhsT=k.ones_f[:], rhs=xm[:, c, 0:n], start=(c == 0), stop=(c == 1)), ['ones_f', f'H_xm.{c}'], ['H_pm'])
            for c in range(2):
                P.op('dve', lambda e, c=c: e.scalar_tensor_tensor(out=xm[:, c, 0:n], in0=pm[:, 0:n], scalar=-1.0 / 256, in1=xm[:, c, 0:n], op0=ALU.mult, op1=ALU.add),
                     ['H_pm', f'H_xm.{c}'], [f'H_xm.{c}'])
                P.op('act', lambda e, c=c: e.activation(out=sq[:, c, 0:n], in_=xm[:, c, 0:n], func=AF.Square), [f'H_xm.{c}'], [f'H_sq.{c}'])
            for c in range(2):
                P.op('pe', lambda e, c=c: e.matmul(pv[:, 0:n], lhsT=k.ones_f[:], rhs=sq[:, c, 0:n], start=(c == 0), stop=(c == 1)), ['ones_f', f'H_sq.{c}'], ['H_pv'])
            P.op('act', lambda e: e.activation(out=rstd[:, 0:n], in_=pv[:, 0:n], func=AF.Sqrt, bias=k.cst[:, 0:1], scale=1.0 / 256), ['H_pv', 'cst'], ['H_rstd'])
            P.op('dve', lambda e: e.reciprocal(out=rstd[:, 0:n], in_=rstd[:, 0:n]), ['H_rstd'], ['H_rstd'])
            for c in range(2):
                P.op('dve', lambda e, c=c: e.tensor_tensor(out=xm[:, c, 0:n], in0=xm[:, c, 0:n], in1=rstd[:, 0:n], op=ALU.mult), [f'H_xm.{c}', 'H_rstd'], [f'H_xm.{c}'])
                P.op('act', lambda e, c=c: e.activation(out=yT[:, 6 + c, 0:n], in_=xm[:, c, 0:n], func=AF.Silu, scale=cp['conf_lng'][:, c:c + 1], bias=cp['conf_lnb'][:, c:c + 1]),
                     [f'H_xm.{c}', 'H_conf_lng', 'H_conf_lnb'], [f'H_yT.{6 + c}'])
            for g in range(ng):
                for nt in range(2):
                    pp = po[(2 * g + nt) % 4]
                    ppn = f'H_po{(2 * g + nt) % 4}'
                    for kc in range(8):
                        P.op('pe', lambda e, g=g, nt=nt, kc=kc, pp=pp: e.matmul(pp[:, :], lhsT=yT[:, kc, g * 128:(g + 1) * 128], rhs=w_out[:, kc, nt * 512:(nt + 1) * 512],
                                                                               start=(kc == 0), stop=(kc == 7)), [f'H_yT.{kc}', f'w_out1.{kc}'], [ppn])
                    P.op('dve', lambda e, g=g, nt=nt, pp=pp: e.tensor_tensor(out=xo[:, g, nt * 512:(nt + 1) * 512], in0=pp[:, :], in1=xt[:, g, nt * 512:(nt + 1) * 512], op=ALU.add),
                         [ppn, 'H_xt'], [f'H_xo.{g}'])
            P.dma('sp', S['x15'][1 + s:1 + s + n, :].rearrange("(g p) f -> p g f", p=128), xo[:, 0:ng, :], keys('H_xo', 4)[0:ng], [], 'H_xo')
    P.barrier()
    if stop_after == 'l1F2':
        return
    ffn_phase(k, 1, 0, S['x15'], None, T, final=True)
    P.barrier()


def _fm(v, nch):
    return np.ascontiguousarray(np.asarray(v, np.float32).reshape(nch, 128).T)


def prep_shared(inp, T):
    f = lambda a: np.ascontiguousarray(np.asarray(a, np.float32))
    d = {}
    d['mod_w'] = f(inp['mod_w'])
    d['mod_b'] = f(inp['mod_b'])
    d['modb_fm'] = f(np.asarray(inp['mod_b']).reshape(2, 6, 8, 128).transpose(3, 0, 1, 2))
    ng = np.stack([np.asarray(inp['norm_mix_g']), np.asarray(inp['norm_ffn_g'])], axis=1)
    d['ng_fm'] = f(ng.reshape(2, 2, 8, 128).transpose(3, 0, 1, 2))
    d['final_g_bc'] = f(np.broadcast_to(np.asarray(inp['final_g'])[None, :], (128, D)))
    d['ident'] = f(np.eye(128))
    d['ab_w_in'] = f(inp['ab_w_in'][0])
    d['ab_w_out'] = f(inp['ab_w_out'][0])
    d['lru_cw'] = f(np.asarray(inp['lru_conv_w'][0]).reshape(4, 6, 128).transpose(2, 1, 0))
    d['lru_cb'] = _fm(inp['lru_conv_b'][0], 6)
    for nm, src in (('lru_bdA', inp['lru_wa'][0]), ('lru_bdX', inp['lru_wx'][0])):
        src = np.asarray(src)
        bd = np.zeros((128, 2, 6, 128), np.float32)
        for dd in range(2):
            for c in range(6):
                bd[0:64, dd, c, 0:64] = src[dd, 2 * c]
                bd[64:128, dd, c, 64:128] = src[dd, 2 * c + 1]
        d[nm] = bd
    for nm, src in (('lru_ba', inp['lru_ba'][0]), ('lru_bx', inp['lru_bx'][0]), ('lru_lam', inp['lru_lambda'][0])):
        d[nm] = f(np.asarray(src).reshape(2, 6, 128).transpose(2, 0, 1))
    pw = np.asarray(inp['pool_w'][0])
    bd = np.zeros((128, 2, 128), np.float32)
    for ch in range(2):
        bd[0:64, ch, 0:64] = pw[2 * ch]
        bd[64:128, ch, 64:128] = pw[2 * ch + 1]
    d['pool_bd'] = bd
    d['pool_scale'] = _fm(inp['pool_scale'][0], 2)
    invw = np.zeros((128, 2), np.float32)
    corr = np.ones((128, 2, 2, 16), np.float32)
    wins = (2, 4, 8, 16)
    Lbig = 1 << 20
    for g, w in enumerate(wins):
        ch, half = g // 2, g % 2
        psl = slice(64 * half, 64 * half + 64)
        invw[psl, ch] = 1.0 / w
        for i in range(16):
            t = i
            cnt = (t + w - w // 2) - max(t - w // 2, 0)
            corr[psl, ch, 0, i] = float(w) / cnt
            t = Lbig - 16 + i
            cnt = min(t + w - w // 2, Lbig) - (t - w // 2)
            corr[psl, ch, 1, i] = float(w) / cnt
    d['pool_invw'] = invw
    d['pool_corr'] = corr
    d['ffn_w_up'] = f(inp['ffn_w_up'])
    d['ffn_w_down'] = f(inp['ffn_w_down'])
    d['ffn_cw'] = f(np.asarray(inp['ffn_conv_w']).reshape(2, 3, 2 * NFC, 128).transpose(3, 0, 2, 1))
    d['ffn_cb'] = f(np.asarray(inp['ffn_conv_b']).reshape(2, 2 * NFC, 128).transpose(2, 0, 1))
    w_in = np.asarray(inp['cd_w_in'][0], np.float32)
    d['cd_w_in'] = f(w_in)
    qk = w_in[:, :1536].reshape(D, 1536 // 32, 2, 16)
    d['cd_w_sw'] = f(qk[:, :, ::-1, :].reshape(D, 1536))
    d['cd_w_out'] = f(inp['cd_w_out'][0])
    t = np.arange(T)
    row = (t // GRID_W).astype(np.float32)
    col = (t % GRID_W).astype(np.float32)
    inv = (10000.0 ** (-np.arange(16, dtype=np.float32) / 16)).astype(np.float32)
    ang_r = (row[:, None] * inv).astype(np.float32)
    ang_c = (col[:, None] * inv).astype(np.float32)
    rc = np.zeros((128, T), np.float32)
    rs = np.zeros((128, T), np.float32)
    for p in range(128):
        dd = p % 64
        ang = ang_r if dd < 32 else ang_c
        fq = dd % 16
        first = (dd % 32) < 16
        rc[p] = np.cos(ang[:, fq])
        rs[p] = (-1.0 if first else 1.0) * np.sin(ang[:, fq])
    d['rope_c'] = rc
    d['rope_s'] = rs
    d['diff_l'] = f(np.stack([np.asarray(inp['diff_lq1'][0]), np.asarray(inp['diff_lk1'][0]),
                              np.asarray(inp['diff_lq2'][0]), np.asarray(inp['diff_lk2'][0])])[None])
    d['subln_g'] = f(np.asarray(inp['diff_subln_g'][0]).reshape(128, 1))
    d['conf_w'] = f(np.asarray(inp['conf_dw_w'][0]).reshape(31, 2, 128).transpose(2, 1, 0))
    d['conf_b'] = _fm(inp['conf_dw_b'][0], 2)
    d['conf_lng'] = _fm(inp['conf_ln_g'][0], 2)
    d['conf_lnb'] = _fm(inp['conf_ln_b'][0], 2)
    return d


def prep_core(inp, shared, b, T):
    m = dict(shared)
    m['x'] = np.ascontiguousarray(np.asarray(inp['x'][b, :T], np.float32))
    m['ctx'] = np.ascontiguousarray(np.asarray(inp['ctx'][b], np.float32))
    m['c_fm'] = np.ascontiguousarray(np.stack([_fm(inp['c'][b], 8), _fm(inp['c_ctx'], 8)], axis=1))
    return m


_CACHE = {}


def kernel(**inputs):
    T = inputs['x'].shape[1]
    Bn = inputs['x'].shape[0]
    if T not in _CACHE:
        _CACHE[T] = build(T)
    kk = _CACHE[T]
    shared = prep_shared(inputs, T)
    ncores = 8
    in_maps = [prep_core(inputs, shared, c % Bn, T) for c in range(ncores)]
    res = run_bass_kernel_spmd(kk.nc, in_maps, core_ids=list(range(ncores)))
    out = np.stack([np.asarray(res.results[b]['out'], np.float32) for b in range(Bn)], axis=0)
    return out
```

```python
import numpy as np
import math
import os
from contextlib import ExitStack
import concourse.bass as bass
import concourse.mybir as mybir
from concourse.bass_utils import run_bass_kernel_spmd
from concourse.ap import AP

F32 = mybir.dt.float32
BF16 = mybir.dt.bfloat16
AF = mybir.ActivationFunctionType
ALU = mybir.AluOpType
AX = mybir.AxisListType

D = 1024
LC = 256
DFF = 2816
NFC = DFF // 128
PADZ = 16
EPS = 1e-6
GRID_W = 64
LAMBDA_INIT1 = 0.8 - 0.6 * math.exp(-0.3 * 1)
SAME_ENGINE_SYNC = True


def rev(ap):
    a = [list(x) for x in ap.ap]
    st, n = a[-1]
    a[-1] = [-st, n]
    return AP(ap.tensor, ap.offset + st * (n - 1), a)


class Prog:
    def __init__(self, nc, es):
        self.nc = nc
        self.es = es
        self.eng = {'pe': nc.tensor, 'act': nc.scalar, 'dve': nc.vector, 'pool': nc.gpsimd, 'sp': nc.sync}
        self.esem = {e: es.enter_context(nc.semaphore('S_' + e)) for e in ('pe', 'act', 'dve', 'pool')}
        self.ecnt = {e: 0 for e in self.esem}
        self.dsem = {}
        self.dpool = []
        self.nd = 0
        self.waited = {e: {} for e in self.eng}
        self.lastw = {}
        self.rd = {}
        self.ninstr = 0

    def _need(self, reads, writes):
        ev = {}

        def add(e):
            if e is None:
                return
            k, sem, val = e
            if k not in ev or ev[k][1] < val:
                ev[k] = (sem, val)
        for r in reads:
            add(self.lastw.get(r))
        for w in writes:
            add(self.lastw.get(w))
            for e in self.rd.get(w, {}).items():
                add((e[0], e[1][0], e[1][1]))
        return ev

    def _wait(self, e, ev):
        for k, (sem, val) in ev.items():
            if k == 'S_' + e and (e == 'pe' or not SAME_ENGINE_SYNC):
                continue
            if self.waited[e].get(k, 0) < val:
                self.eng[e].wait_ge(sem, val)
                self.waited[e][k] = val
                self.ninstr += 1

    def _commit(self, ev, reads, writes):
        k, sem, val = ev
        for w in writes:
            self.lastw[w] = ev
            self.rd[w] = {}
        for r in reads:
            self.rd.setdefault(r, {})[k] = (sem, val)

    def op(self, e, fn, reads, writes):
        self._wait(e, self._need(reads, writes))
        ins = fn(self.eng[e])
        self.ecnt[e] += 1
        ins.then_inc(self.esem[e], 1)
        self.ninstr += 1
        self._commit(('S_' + e, self.esem[e], self.ecnt[e]), reads, writes)

    def dma(self, q, out, in_, reads, writes, key):
        self._wait(q, self._need(reads, writes))
        if key not in self.dsem:
            if self.dpool:
                self.dsem[key] = self.dpool.pop()
            else:
                nm = 'D%d' % self.nd
                self.nd += 1
                self.dsem[key] = [self.es.enter_context(self.nc.semaphore(nm)), 0, nm]
        d = self.dsem[key]
        ins = self.eng[q].dma_start(out=out, in_=in_)
        d[1] += 16
        ins.then_inc(d[0], 16)
        self.ninstr += 1
        self._commit((d[2], d[0], d[1]), reads, writes)

    def barrier(self):
        ev = {}
        for e in self.esem:
            if self.ecnt[e] > 0:
                ev['S_' + e] = (self.esem[e], self.ecnt[e])
        for k, d in self.dsem.items():
            if d[1] > 0:
                ev[d[2]] = (d[0], d[1])
        for e in self.eng:
            self._wait(e, dict(ev))
        self.lastw = {}
        self.rd = {}
        for k, d in self.dsem.items():
            self.dpool.append(d)
        self.dsem = {}


def keys(name, n):
    return [f"{name}.{i}" for i in range(n)]


def tiles_of(T, w=512):
    out = []
    s = 0
    while s < T:
        n = min(w, T - s)
        out.append((s, n))
        s += n
    return out


class K:
    pass


def build(T, dbg=False, stop_after=None, half=True):
    nc = bass.Bass("TRN2", target_bir_lowering=False)
    k = K()
    k.nc = nc
    k.T = T
    k.half = half
    k.TL = (T // 2 + 128) if half else T
    k.TO = (T // 2) if half else T

    def din(name, shape, dt=F32):
        return nc.dram_tensor(name, list(shape), dt, kind="ExternalInput").ap()

    def dscr(name, shape, dt=F32, out=False):
        kind = "ExternalOutput" if (out or dbg) else "Internal"
        return nc.dram_tensor(name, list(shape), dt, kind=kind).ap()

    I = {}
    I['x'] = din('x', [T, D])
    I['ctx'] = din('ctx', [LC, D])
    I['c_fm'] = din('c_fm', [128, 2, 8])
    I['mod_w'] = din('mod_w', [2, D, 6 * D])
    I['modb_fm'] = din('modb_fm', [128, 2, 6, 8])
    I['mod_b'] = din('mod_b', [2, 6 * D])
    I['ng_fm'] = din('ng_fm', [128, 2, 2, 8])
    I['final_g_bc'] = din('final_g_bc', [128, D])
    I['ident'] = din('ident', [128, 128])
    I['ab_w_in'] = din('ab_w_in', [D, 1792])
    I['ab_w_out'] = din('ab_w_out', [D, D])
    I['lru_cw'] = din('lru_cw', [128, 6, 5])
    I['pool_fl'] = din('pool_fl', [128, 2])
    I['lru_cb'] = din('lru_cb', [128, 6])
    I['lru_bdA'] = din('lru_bdA', [128, 2, 6, 128])
    I['lru_bdX'] = din('lru_bdX', [128, 2, 6, 128])
    I['lru_ba'] = din('lru_ba', [128, 2, 6])
    I['lru_bx'] = din('lru_bx', [128, 2, 6])
    I['lru_lam'] = din('lru_lam', [128, 2, 6])
    I['pool_bd'] = din('pool_bd', [128, 2, 128])
    I['pool_scale'] = din('pool_scale', [128, 2])
    I['pool_invw'] = din('pool_invw', [128, 2])
    I['pool_corr'] = din('pool_corr', [128, 2, 2, 16])
    I['ffn_w_up'] = din('ffn_w_up', [2, D, 2 * DFF])
    I['ffn_w_down'] = din('ffn_w_down', [2, DFF, D])
    I['ffn_cw'] = din('ffn_cw', [128, 2, 2 * NFC, 3])
    I['ffn_cb'] = din('ffn_cb', [128, 2, 2 * NFC])
    I['cd_w_in'] = din('cd_w_in', [D, 2816])
    I['cd_w_sw'] = din('cd_w_sw', [D, 1536])
    I['cd_w_out'] = din('cd_w_out', [D, D])
    I['rope_c'] = din('rope_c', [128, T])
    I['rope_s'] = din('rope_s', [128, T])
    I['diff_l'] = din('diff_l', [1, 4, 64])
    I['subln_g'] = din('subln_g', [128, 1])
    I['conf_w'] = din('conf_w', [128, 2, 31])
    I['conf_b'] = din('conf_b', [128, 2])
    I['conf_lng'] = din('conf_lng', [128, 2])
    I['conf_lnb'] = din('conf_lnb', [128, 2])
    k.I = I

    k.out = nc.dram_tensor('out', [k.TO, D], F32, kind="ExternalOutput").ap()
    S = {}
    for nm, TT in (('l', T), ('c', LC)):
        S['z_' + nm] = dscr('z_' + nm, [1792, PADZ + TT + PADZ])
        S['xa_' + nm] = dscr('xa_' + nm, [768, TT])
        S['hf_' + nm] = dscr('hf_' + nm, [768, TT])
        S['x05_' + nm] = dscr('x05_' + nm, [1 + TT, D])
        S['x1_' + nm] = dscr('x1_' + nm, [1 + TT + 128, D])
    S['x15'] = dscr('x15', [1 + T, D])
    S['x2'] = dscr('x2', [1 + T + 128, D])
    S['qT'] = dscr('qT', [768, T], BF16)
    S['kT'] = dscr('kT', [768, T + LC], BF16)
    S['v'] = dscr('v', [T + LC, 768], BF16)
    S['gl'] = dscr('gl', [256, PADZ + T + PADZ])
    k.S = S

    with ExitStack() as es:
        P = Prog(nc, es)
        k.P = P

        uid = [0]

        def sb(name, shape, dt, st=es):
            uid[0] += 1
            return st.enter_context(nc.sbuf_tensor(f"{name}_s{uid[0]}", list(shape), dt))

        def ps(name, shape, dt, st=es):
            uid[0] += 1
            return st.enter_context(nc.psum_tensor(f"{name}_p{uid[0]}", list(shape), dt))
        k.sb = sb
        k.ps = ps

        ident = sb('ident', [128, 128], BF16)
        P.dma('pool', ident[:], I['ident'], [], ['ident'], 'ident')
        k.ident = ident
        cst = sb('cst', [128, 8], F32)
        P.op('dve', lambda e: e.memset(cst[:, 0:1], EPS), [], ['cst'])
        P.op('dve', lambda e: e.memset(cst[:, 1:2], 1.0), [], ['cst'])
        P.op('dve', lambda e: e.memset(cst[:, 2:3], 0.0), [], ['cst'])
        k.cst = cst
        zero = sb('zero', [128, 512], F32)
        P.op('dve', lambda e: e.memset(zero[:], 0.0), [], ['zero'])
        k.zero = zero
        ones_bf = sb('ones_bf', [128, 128], BF16)
        P.op('dve', lambda e: e.memset(ones_bf[:], 1.0), [], ['ones_bf'])
        k.ones_bf = ones_bf
        ones_f = sb('ones_f', [128, 128], F32)
        P.op('dve', lambda e: e.memset(ones_f[:], 1.0), [], ['ones_f'])
        k.ones_f = ones_f

        modfm = sb('modfm', [128, 2, 2, 4, 8], F32)
        k.modfm = modfm
        k.gbc_d = nc.dram_tensor('gbc_d', [2, 2, 2, 128, D], F32, kind="Internal").ap()

        phase_adaln(k)
        P.barrier()
        if dbg:
            mdbg = nc.dram_tensor('modfm_dbg', [128, 2, 2, 4, 8], F32, kind="ExternalOutput").ap()
            P.dma('sp', mdbg, modfm[:], [], [], 'mdbg')
            gdbg = nc.dram_tensor('gbc_dbg', [2, 2, 2, 128, D], F32, kind="ExternalOutput").ap()
            P.dma('sp', gdbg, k.gbc_d, [], [], 'gdbg')
        if stop_after == 'adaln':
            return finish(k, es)

        layer0(k, stop_after)
        if stop_after is not None and stop_after.startswith('l0'):
            return finish(k, es)
        layer1(k, stop_after)
        return finish(k, es)


def finish(k, es):
    k.P.barrier()
    k.ninstr = k.P.ninstr
    return k


def phase_adaln(k):
    nc, P, I = k.nc, k.P, k.I
    with ExitStack() as st:
        sb = lambda n, s, d: k.sb(n, s, d, st)
        ps = lambda n, s, d: k.ps(n, s, d, st)
        cf = sb('ad_cf', [128, 2, 8], F32)
        P.dma('sp', cf[:], I['c_fm'], [], ['ad_cf'], 'ad_cf')
        sc = sb('ad_sc', [128, 2, 8], F32)
        P.op('act', lambda e: e.activation(out=sc[:], in_=cf[:], func=AF.Silu), ['ad_cf'], ['ad_sc'])
        rep = sb('ad_rep', [128, 2, 8, 128], F32)
        for s_ in range(2):
            for kc in range(8):
                P.op('dve', lambda e, s_=s_, kc=kc: e.tensor_copy(out=rep[:, s_, kc, :], in_=sc[:, s_, kc:kc + 1].to_broadcast([128, 128])),
                     ['ad_sc'], [f'ad_rep.{s_}.{kc}'])
        modb = sb('ad_modb', [128, 2, 6, 8], F32)
        P.dma('sp', modb[:], I['modb_fm'], [], ['ad_modb'], 'ad_modb')
        ng = sb('ad_ng', [128, 2, 2, 8], F32)
        P.dma('sp', ng[:], I['ng_fm'], [], ['ad_ng'], 'ad_ng')
        brow = sb('ad_brow', [1, 2, 6 * D], F32)
        P.dma('sp', brow[:], I['mod_b'].rearrange("(o l) n -> o l n", o=1), [], ['ad_brow'], 'ad_brow')
        wt = [sb(f'ad_w{i}', [128, 6 * D], F32) for i in range(2)]
        gst = sb('ad_gst', [128, 2, D], F32)
        facc = sb('ad_facc', [128, 32, 2], F32)
        pfm = ps('ad_pfm', [128, 32, 2], F32)
        pbc = [ps(f'ad_pbc{i}', [128, 512], F32) for i in range(4)]
        for l in range(2):
            for pss in range(2):
                for kc in range(8):
                    w = wt[kc % 2]
                    wk = f'ad_w{kc % 2}'
                    P.dma('sp', w[:], I['mod_w'][l, kc * 128:(kc + 1) * 128, :], [], [wk], wk)
                    if pss == 0:
                        jmap = [0, 1, 3, 4]
                        for jj, j in enumerate(jmap):
                            for fc in range(8):
                                col = j * D + fc * 128
                                P.op('pe', lambda e, w=w, col=col, jj=jj, fc=fc, kc=kc: e.matmul(
                                    pfm[:, jj * 8 + fc, :], lhsT=w[:, col:col + 128], rhs=sc[:, :, kc],
                                    start=True, stop=True), [wk, 'ad_sc'], ['ad_pfm'])
                        if kc == 0:
                            P.op('dve', lambda e: e.tensor_copy(out=facc[:], in_=pfm[:]), ['ad_pfm'], ['ad_facc'])
                        else:
                            P.op('dve', lambda e: e.tensor_tensor(out=facc[:], in0=pfm[:], in1=facc[:], op=ALU.add), ['ad_pfm', 'ad_facc'], ['ad_facc'])
                    if True:
                        s_ = pss
                        for nt in range(4):
                            gj = 2 if nt < 2 else 5
                            col = gj * D + (nt % 2) * 512
                            P.op('pe', lambda e, w=w, col=col, nt=nt, kc=kc, s_=s_: e.matmul(
                                pbc[nt][:], lhsT=rep[:, s_, kc, :], rhs=w[:, col:col + 512],
                                start=(kc == 0), stop=False), [wk, f'ad_rep.{s_}.{kc}'], [f'ad_pbc{nt}'])
                if pss == 0:
                    jmap = [0, 1, 3, 4]
                    for s_ in range(2):
                        for jj, j in enumerate(jmap):
                            P.op('dve', lambda e, s_=s_, jj=jj, j=j, l=l: e.tensor_tensor(
                                out=k.modfm[:, l, s_, jj, :], in0=facc[:, jj * 8:(jj + 1) * 8, s_], in1=modb[:, l, j, :], op=ALU.add),
                                ['ad_facc', 'ad_modb'], [f'modfm.{l}.{s_}.{jj}'])
                        for jj, which in ((1, 0), (3, 1)):
                            P.op('dve', lambda e, s_=s_, jj=jj, which=which, l=l: e.scalar_tensor_tensor(
                                out=k.modfm[:, l, s_, jj, :], in0=k.modfm[:, l, s_, jj, :], scalar=1.0, in1=ng[:, l, which, :],
                                op0=ALU.add, op1=ALU.mult), [f'modfm.{l}.{s_}.{jj}', 'ad_ng'], [f'modfm.{l}.{s_}.{jj}'])
                if True:
                    s_ = pss
                    for nt in range(4):
                        gj = 2 if nt < 2 else 5
                        col = gj * D + (nt % 2) * 512
                        P.op('pe', lambda e, nt=nt, col=col, l=l: e.matmul(
                            pbc[nt][:], lhsT=k.ones_f[0:1, :], rhs=brow[0:1, l, col:col + 512], start=False, stop=True),
                            ['ones_f', 'ad_brow'], [f'ad_pbc{nt}'])
                        P.op('act', lambda e, nt=nt: e.copy(out=gst[:, nt // 2, (nt % 2) * 512:(nt % 2) * 512 + 512], in_=pbc[nt][:]),
                            [f'ad_pbc{nt}'], [f'ad_gst.{nt}'])
                    for j2 in range(2):
                        P.dma('sp', k.gbc_d[l, s_, j2], gst[:, j2, :], [f'ad_gst.{2 * j2}', f'ad_gst.{2 * j2 + 1}'], [], 'ad_gst')
        P.barrier()


def load_w_bf16(k, name, dst, src_ap, nk, ncols, colblk=2048):
    P = k.P
    for kc in range(nk):
        for c0 in range(0, ncols, colblk):
            c1 = min(ncols, c0 + colblk)
            P.dma('pool', dst[:, kc, c0:c1], src_ap[kc * 128:(kc + 1) * 128, c0:c1], [], [f'{name}.{kc}'], f'{name}.{kc}')


def fold_w_bf16(k, st, name, dst, src_ap, nk, gb_dram):
    P = k.P
    stg = [k.sb(f'{name}_stg{i}', [128, D], F32, st) for i in range(2)]
    gb = k.sb(f'{name}_gb', [128, D], F32, st)
    P.dma('sp', gb[:], gb_dram, [], [f'{name}_gb'], f'{name}_gb')
    gb_ap = gb[:]
    for kc in range(nk):
        s_ = stg[kc % 2]
        sk = f'{name}_stg{kc % 2}'
        P.dma('sp', s_[:], src_ap[kc * 128:(kc + 1) * 128, :], [], [sk], sk)
        P.op('dve', lambda e, s_=s_, kc=kc: e.tensor_tensor(out=dst[:, kc, :], in0=s_[:], in1=gb_ap, op=ALU.mult),
             [sk, f'{name}_gb'], [f'{name}.{kc}'])


def modulate_tile(k, B, src_rows, n, l, s_, jsh, hT, hTname):
    P = k.P
    ng = n // 128
    X, Xn = B['xt'], B['xtname']
    ss = B['ss']
    xn = B['xn']
    for g0 in range(0, ng, 2):
        gg = min(2, ng - g0)
        P.dma('sp', X[:, 0:gg, :], src_rows[g0 * 128:(g0 + gg) * 128, :].rearrange("(g p) f -> p g f", p=128), [], [Xn], Xn)
        P.op('dve', lambda e: e.memset(ss[:, 0:2], 0.0), [], [B['ssname']])
        for g in range(gg):
            P.op('act', lambda e, g=g: e.activation(out=B['junk'][:], in_=X[:, g, :], func=AF.Square, accum_out=ss[:, g:g + 1]),
                 [Xn], [B['ssname'], B['junkname']])
        P.op('act', lambda e, gg=gg: e.activation(out=ss[:, 4:4 + gg], in_=ss[:, 0:gg], func=AF.Sqrt, bias=k.cst[:, 0:1], scale=1.0 / D),
             [B['ssname'], 'cst'], [B['ssname']])
        P.op('dve', lambda e, gg=gg: e.reciprocal(out=ss[:, 8:8 + gg], in_=ss[:, 4:4 + gg]), [B['ssname']], [B['ssname']])
        for g in range(gg):
            eng = 'dve' if g % 2 == 0 else 'pool'
            P.op(eng, lambda e, g=g, g0=g0: e.tensor_scalar(out=xn[:, g0 + g, :], in0=X[:, g, :], scalar1=ss[:, 8 + g:9 + g], scalar2=None, op0=ALU.mult),
                 [Xn, B['ssname']], [f"{B['xnname']}.{g0 + g}"])
    for fc in range(8):
        tp = B['tp'][fc % 2]
        tpn = B['tpname'][fc % 2]
        for g in range(ng):
            P.op('pe', lambda e, g=g, fc=fc, tp=tp: e.transpose(out=tp[:, g * 128:(g + 1) * 128], in_=xn[:, g, fc * 128:(fc + 1) * 128], identity=k.ident[:]),
                 [f"{B['xnname']}.{g}", 'ident'], [tpn])
        P.op('act', lambda e, fc=fc, tp=tp: e.activation(out=hT[:, fc, 0:n], in_=tp[:, 0:n], func=AF.Identity,
                                                          scale=k.modfm[:, l, s_, jsh + 1, fc:fc + 1], bias=k.modfm[:, l, s_, jsh, fc:fc + 1]),
             [tpn], [f'{hTname}.{fc}'])


def mod_bufs(k, st, pfx):
    B = {}
    B['xt'] = k.sb(pfx + 'xt', [128, 2, D], F32, st)
    B['xtname'] = pfx + 'xt'
    B['xn'] = k.sb(pfx + 'xn', [128, 4, D], BF16, st)
    B['xnname'] = pfx + 'xn'
    B['junk'] = k.sb(pfx + 'junk', [128, D], BF16, st)
    B['junkname'] = pfx + 'junk'
    B['ss'] = k.sb(pfx + 'ss', [128, 12], F32, st)
    B['ssname'] = pfx + 'ss'
    B['tp'] = [k.ps(pfx + f'tp{i}', [128, 512], BF16, st) for i in range(2)]
    B['tpname'] = [pfx + f'tp{i}' for i in range(2)]
    return B


def layer0(k, stop_after):
    nc, P, I, S = k.nc, k.P, k.I, k.S
    with ExitStack() as st:
        sb = lambda n, s, d: k.sb(n, s, d, st)
        lp = {}
        for nm, shp in (('lru_cw', [128, 6, 5]), ('pool_fl', [128, 2]), ('lru_cb', [128, 6]), ('lru_ba', [128, 2, 6]), ('lru_bx', [128, 2, 6]),
                        ('lru_lam', [128, 2, 6]), ('pool_scale', [128, 2]), ('pool_invw', [128, 2]), ('pool_corr', [128, 2, 2, 16])):
            lp[nm] = sb('p_' + nm, shp, F32)
            P.dma('sp', lp[nm][:], I[nm], [], ['p_' + nm], 'p_' + nm)
        for nm, shp in (('lru_bdA', [128, 2, 6, 128]), ('lru_bdX', [128, 2, 6, 128]), ('pool_bd', [128, 2, 128])):
            lp[nm] = sb('p_' + nm, shp, BF16)
            P.dma('pool', lp[nm][:], I[nm], [], ['p_' + nm], 'p_' + nm)
        cl = sb('p_cl', [128, 2, 2, 6], F32)
        tmp = sb('p_cltmp', [128, 2, 6], F32)
        P.op('act', lambda e: e.activation(out=tmp[:], in_=lp['lru_lam'][:], func=AF.Exp, scale=-1.0), ['p_lru_lam'], ['p_cltmp'])
        P.op('act', lambda e: e.activation(out=tmp[:], in_=tmp[:], func=AF.Ln, bias=k.cst[:, 1:2], scale=1.0), ['p_cltmp', 'cst'], ['p_cltmp'])
        P.op('dve', lambda e: e.tensor_scalar(out=cl[:, 0, :, :], in0=tmp[:], scalar1=-8.0, scalar2=None, op0=ALU.mult), ['p_cltmp'], ['p_cl'])
        P.op('dve', lambda e: e.tensor_scalar(out=cl[:, 1, :, :], in0=tmp[:], scalar1=-16.0, scalar2=None, op0=ALU.mult), ['p_cltmp'], ['p_cl'])
        lp['cl'] = cl
        stt = sb('p_state', [128, 2, 6], F32)
        P.op('dve', lambda e: e.memset(stt[:], 0.0), [], ['p_state'])
        lp['state'] = stt
        k.lp = lp
        w_in = sb('w_in0', [128, 8, 1792], BF16)
        load_w_bf16(k, 'w_in0', w_in, I['ab_w_in'], 8, 1792, colblk=1792)
        w_out = sb('w_out0', [128, 8, D], BF16)
        k.w_in0, k.w_out0 = w_in, w_out

        for s_, nm, TT, xsrc in ((1, 'c', LC, I['ctx']), (0, 'l', k.T, I['x'])):
            seg = K()
            seg.nm, seg.T, seg.x, seg.set = nm, TT, xsrc, s_
            seg.z, seg.xa, seg.hf, seg.x05, seg.x1 = S['z_' + nm], S['xa_' + nm], S['hf_' + nm], S['x05_' + nm], S['x1_' + nm]
            seg.tiles = tiles_of(TT)
            with ExitStack() as st2:
                fold_w_bf16(k, st2, 'w_out0', w_out, I['ab_w_out'], 8, k.gbc_d[0, s_, 0])
            P.barrier()
            l0_phaseA(k, seg)
            P.barrier()
            if stop_after == 'l0A' and nm == 'l':
                return
            l0_phaseB(k, seg)
            P.barrier()
            if stop_after == 'l0B' and nm == 'l':
                return
            l0_phaseC(k, seg)
            P.barrier()
            if stop_after == 'l0C' and nm == 'l':
                return
    for s_, nm, TT in ((1, 'c', LC), (0, 'l', k.T)):
        ffn_phase(k, 0, s_, S['x05_' + nm], S['x1_' + nm], TT, final=False)
        P.barrier()


def l0_phaseA(k, seg):
    nc, P = k.nc, k.P
    with ExitStack() as st:
        sb = lambda n, s, d: k.sb(n, s, d, st)
        ps = lambda n, s, d: k.ps(n, s, d, st)
        B = mod_bufs(k, st, 'A_')
        hT = sb('A_hT', [128, 8, 512], BF16)
        zt = [sb(f'A_zt{i}', [128, 14, 512], F32) for i in range(2)]
        zp = [ps(f'A_zp{i}', [128, 512], F32) for i in range(4)]
        zv = seg.z.rearrange("(c p) t -> p c t", p=128)
        P.dma('sp', zv[:, :, 0:PADZ], k.zero[:, 0:14 * PADZ].rearrange("p (c t) -> p c t", c=14), ['zero'], [], 'A_zpad')
        P.dma('sp', zv[:, :, PADZ + seg.T:PADZ + seg.T + PADZ], k.zero[:, 0:14 * PADZ].rearrange("p (c t) -> p c t", c=14), ['zero'], [], 'A_zpad')
        for ti, (s, n) in enumerate(seg.tiles):
            modulate_tile(k, B, seg.x[s:s + n, :], n, 0, seg.set, 0, hT, 'A_hT')
            Z = zt[ti % 2]
            Zn = f'A_zt{ti % 2}'
            for mc in range(14):
                zpp = zp[mc % 4]
                for kc in range(8):
                    P.op('pe', lambda e, mc=mc, kc=kc, zpp=zpp: e.matmul(zpp[:, 0:n], lhsT=k.w_in0[:, kc, mc * 128:(mc + 1) * 128], rhs=hT[:, kc, 0:n],
                                                                         start=(kc == 0), stop=(kc == 7)),
                         [f'w_in0.{kc}', f'A_hT.{kc}'], [f'A_zp{mc % 4}'])
                eng = 'act' if mc % 2 == 0 else 'dve'
                if eng == 'act':
                    P.op('act', lambda e, mc=mc, zpp=zpp: e.copy(out=Z[:, mc, 0:n], in_=zpp[:, 0:n]), [f'A_zp{mc % 4}'], [f'{Zn}.{mc}'])
                else:
                    P.op('dve', lambda e, mc=mc, zpp=zpp: e.tensor_copy(out=Z[:, mc, 0:n], in_=zpp[:, 0:n]), [f'A_zp{mc % 4}'], [f'{Zn}.{mc}'])
            P.dma('sp', zv[:, :, PADZ + s:PADZ + s + n], Z[:, :, 0:n], keys(Zn, 14), [], Zn)


def lru_coeffs(k, C, d, n, xa, xab):
    P, lp = k.P, k.lp
    for c in range(6):
        pr, pi = C['pg'][(2 * c) % 4], C['pg'][(2 * c + 1) % 4]
        prn, pin = C['pgname'][(2 * c) % 4], C['pgname'][(2 * c + 1) % 4]
        P.op('pe', lambda e, c=c, pr=pr: e.matmul(pr[:, 0:n], lhsT=lp['lru_bdA'][:, d, c, :], rhs=xab[:, c, 0:n], start=True, stop=True),
             ['p_lru_bdA', f"{C['xabname']}.{c}"], [prn])
        P.op('pe', lambda e, c=c, pi=pi: e.matmul(pi[:, 0:n], lhsT=lp['lru_bdX'][:, d, c, :], rhs=xab[:, c, 0:n], start=True, stop=True),
             ['p_lru_bdX', f"{C['xabname']}.{c}"], [pin])
        P.op('act', lambda e, c=c, pr=pr: e.activation(out=C['r'][:, c, 0:n], in_=pr[:, 0:n], func=AF.Sigmoid, bias=lp['lru_ba'][:, d, c:c + 1], scale=1.0),
             [prn, 'p_lru_ba'], [f"{C['pfx']}r.{c}"])
        P.op('act', lambda e, c=c, pi=pi: e.activation(out=C['ig'][:, c, 0:n], in_=pi[:, 0:n], func=AF.Sigmoid, bias=lp['lru_bx'][:, d, c:c + 1], scale=1.0),
             [pin, 'p_lru_bx'], [f"{C['pfx']}ig.{c}"])
    for c in range(6):
        P.op('act', lambda e, c=c: e.activation(out=C['a'][:, c, 0:n], in_=C['r'][:, c, 0:n], func=AF.Exp, scale=lp['cl'][:, 0, d, c:c + 1]),
             [f"{C['pfx']}r.{c}", 'p_cl'], [f"{C['pfx']}a.{c}"])
        P.op('act', lambda e, c=c: e.activation(out=C['r'][:, c, 0:n], in_=C['r'][:, c, 0:n], func=AF.Exp, scale=lp['cl'][:, 1, d, c:c + 1]),
             [f"{C['pfx']}r.{c}", 'p_cl'], [f"{C['pfx']}r.{c}"])
    for c in range(6):
        P.op('act', lambda e, c=c: e.activation(out=C['r'][:, c, 0:n], in_=C['r'][:, c, 0:n], func=AF.Sqrt, bias=k.cst[:, 1:2], scale=-1.0),
             [f"{C['pfx']}r.{c}", 'cst'], [f"{C['pfx']}r.{c}"])
        P.op('dve', lambda e, c=c: e.tensor_tensor(out=C['ig'][:, c, 0:n], in0=C['ig'][:, c, 0:n], in1=C['r'][:, c, 0:n], op=ALU.mult),
             [f"{C['pfx']}r.{c}", f"{C['pfx']}ig.{c}"], [f"{C['pfx']}ig.{c}"])
        P.op('pool', lambda e, c=c: e.tensor_tensor(out=C['ig'][:, c, 0:n], in0=C['ig'][:, c, 0:n], in1=xa[:, c, 0:n], op=ALU.mult),
             [f"{C['pfx']}ig.{c}", f"{C['xaname']}.{c}"], [f"{C['pfx']}ig.{c}"])


def coeff_bufs(k, st, pfx):
    C = {'pfx': pfx}
    for nm in ('r', 'ig', 'a'):
        C[nm] = k.sb(pfx + nm, [128, 6, 512], F32, st)
    C['pg'] = [k.ps(pfx + f'pg{i}', [128, 512], F32, st) for i in range(4)]
    C['pgname'] = [pfx + f'pg{i}' for i in range(4)]
    return C


def l0_phaseB(k, seg):
    P, lp = k.P, k.lp
    with ExitStack() as st:
        sb = lambda n, s, d: k.sb(n, s, d, st)
        zin = sb('B_zin', [128, 6, 516], F32)
        xa = sb('B_xa', [128, 6, 512], F32)
        xab = sb('B_xab', [128, 6, 512], BF16)
        hf = sb('B_hf', [128, 6, 512], F32)
        C = coeff_bufs(k, st, 'B_')
        C['xabname'], C['xaname'] = 'B_xab', 'B_xa'
        zv = seg.z[0:768, :].rearrange("(c p) t -> p c t", p=128)
        xav = seg.xa.rearrange("(c p) t -> p c t", p=128)
        hfv = seg.hf.rearrange("(c p) t -> p c t", p=128)
        if seg.nm == 'c':
            P.op('dve', lambda e: e.memset(lp['state'][:], 0.0), [], ['p_state'])
        for ti, (s, n) in enumerate(seg.tiles):
            P.dma('sp', zin[:, :, 0:n + 4], zv[:, :, PADZ + s - 2:PADZ + s + n + 2], [], keys('B_zin', 6), 'B_zin')
            for c in range(6):
                P.op('act', lambda e, c=c: e.activation(out=xa[:, c, 0:n], in_=zin[:, c, 0:n], func=AF.Identity,
                                                        scale=lp['lru_cw'][:, c, 0:1], bias=lp['lru_cb'][:, c:c + 1]),
                     [f'B_zin.{c}', 'p_lru_cw', 'p_lru_cb'], [f'B_xa.{c}'])
                for t in range(1, 5):
                    P.op('dve', lambda e, c=c, t=t: e.scalar_tensor_tensor(out=xa[:, c, 0:n], in0=zin[:, c, t:t + n], scalar=lp['lru_cw'][:, c, t:t + 1],
                                                                           in1=xa[:, c, 0:n], op0=ALU.mult, op1=ALU.add),
                         [f'B_zin.{c}', f'B_xa.{c}'], [f'B_xa.{c}'])
                P.op('pool', lambda e, c=c: e.tensor_copy(out=xab[:, c, 0:n], in_=xa[:, c, 0:n]), [f'B_xa.{c}'], [f'B_xab.{c}'])
            P.dma('sp', xav[:, :, s:s + n], xa[:, :, 0:n], keys('B_xa', 6), [], 'B_xa')
            lru_coeffs(k, C, 0, n, xa, xab)
            for c in range(6):
                P.op('dve', lambda e, c=c: e.tensor_tensor_scan(out=hf[:, c, 0:n], data0=C['a'][:, c, 0:n], data1=C['ig'][:, c, 0:n],
                                                                initial=lp['state'][:, 0, c:c + 1], op0=ALU.mult, op1=ALU.add),
                     [f'B_a.{c}', f'B_ig.{c}', 'p_state'], [f'B_hf.{c}'])
                P.op('dve', lambda e, c=c: e.tensor_copy(out=lp['state'][:, 0, c:c + 1], in_=hf[:, c, n - 1:n]), [f'B_hf.{c}'], ['p_state'])
            P.dma('sp', hfv[:, :, s:s + n], hf[:, :, 0:n], keys('B_hf', 6), [], 'B_hf')


def l0_phaseC(k, seg):
    P, lp, I = k.P, k.lp, k.I
    with ExitStack() as st:
        sb = lambda n, s, d: k.sb(n, s, d, st)
        ps = lambda n, s, d: k.ps(n, s, d, st)
        xa = sb('C_xa', [128, 6, 512], F32)
        xab = sb('C_xab', [128, 6, 512], BF16)
        hb = sb('C_hb', [128, 6, 512], F32)
        hf = sb('C_hf', [128, 6, 512], F32)
        ga = sb('C_ga', [128, 6, 512], F32)
        yT = sb('C_yT', [128, 8, 512], BF16)
        zb = sb('C_zb', [128, 2, 528], F32)
        p2 = sb('C_p2', [128, 528], F32)
        p4 = sb('C_p4', [128, 528], F32)
        p8 = sb('C_p8', [128, 528], F32)
        Qw = sb('C_Qw', [128, 516], F32)
        Ssum = sb('C_S', [128, 512], F32)
        dd = sb('C_dd', [128, 2, 512], BF16)
        xt = sb('C_xt', [128, 4, D], F32)
        xo = sb('C_xo', [128, 4, D], F32)
        C = coeff_bufs(k, st, 'C_')
        C['xabname'], C['xaname'] = 'C_xab', 'C_xa'
        po = [ps(f'C_po{i}', [128, 512], F32) for i in range(4)]
        zg = seg.z[768:1536, :].rearrange("(c p) t -> p c t", p=128)
        zbv = seg.z[1536:1792, :].rearrange("(c p) t -> p c t", p=128)
        xav = seg.xa.rearrange("(c p) t -> p c t", p=128)
        hfv = seg.hf.rearrange("(c p) t -> p c t", p=128)
        if seg.nm == 'c':
            P.op('dve', lambda e: e.memset(lp['state'][:, 1, :], 0.0), [], ['p_state'])
        nt_ = len(seg.tiles)
        for ti in range(nt_ - 1, -1, -1):
            s, n = seg.tiles[ti]
            ng = n // 128
            P.dma('sp', xa[:, :, 0:n], xav[:, :, s:s + n], [], keys('C_xa', 6), 'C_xa')
            P.dma('sp', hf[:, :, 0:n], hfv[:, :, s:s + n], [], keys('C_hf', 6), 'C_hf')
            P.dma('sp', ga[:, :, 0:n], zg[:, :, PADZ + s:PADZ + s + n], [], keys('C_ga', 6), 'C_ga')
            P.dma('sp', zb[:, :, 0:n + 16], zbv[:, :, PADZ + s - 8:PADZ + s + n + 8], [], keys('C_zb', 2), 'C_zb')
            P.dma('sp', xt[:, 0:ng, :], seg.x[s:s + n, :].rearrange("(g p) f -> p g f", p=128), [], ['C_xt'], 'C_xt')
            for c in range(6):
                P.op('pool', lambda e, c=c: e.tensor_copy(out=xab[:, c, 0:n], in_=xa[:, c, 0:n]), [f'C_xa.{c}'], [f'C_xab.{c}'])
            lru_coeffs(k, C, 1, n, xa, xab)
            for c in range(6):
                P.op('dve', lambda e, c=c: e.tensor_tensor_scan(out=rev(hb[:, c, 0:n]), data0=rev(C['a'][:, c, 0:n]), data1=rev(C['ig'][:, c, 0:n]),
                                                                initial=lp['state'][:, 1, c:c + 1], op0=ALU.mult, op1=ALU.add),
                     [f'C_a.{c}', f'C_ig.{c}', 'p_state'], [f'C_hb.{c}'])
                P.op('dve', lambda e, c=c: e.tensor_copy(out=lp['state'][:, 1, c:c + 1], in_=hb[:, c, 0:1]), [f'C_hb.{c}'], ['p_state'])
                P.op('act', lambda e, c=c: e.activation(out=ga[:, c, 0:n], in_=ga[:, c, 0:n], func=AF.Gelu_apprx_tanh), [f'C_ga.{c}'], [f'C_ga.{c}'])
                P.op('pool', lambda e, c=c: e.tensor_tensor(out=hb[:, c, 0:n], in0=hb[:, c, 0:n], in1=hf[:, c, 0:n], op=ALU.add),
                     [f'C_hb.{c}', f'C_hf.{c}'], [f'C_hb.{c}'])
                P.op('dve', lambda e, c=c: e.tensor_tensor(out=yT[:, c, 0:n], in0=hb[:, c, 0:n], in1=ga[:, c, 0:n], op=ALU.mult),
                     [f'C_hb.{c}', f'C_ga.{c}'], [f'C_yT.{c}'])
            W = n + 16
            n1 = n + 1
            for ch in range(2):
                zc = zb[:, ch, :]
                P.op('dve', lambda e, zc=zc: e.tensor_tensor(out=p2[:, 0:W - 1], in0=zc[:, 0:W - 1], in1=zc[:, 1:W], op=ALU.add),
                     [f'C_zb.{ch}'], ['C_p2'])
                if ch == 0:
                    P.op('dve', lambda e: e.tensor_copy(out=Qw[0:64, 0:n1], in_=p2[0:64, 7:7 + n1]), ['C_p2'], ['C_Qw'])
                    P.op('dve', lambda e: e.tensor_tensor(out=Qw[64:128, 0:n1], in0=p2[64:128, 6:6 + n1], in1=p2[64:128, 8:8 + n1], op=ALU.add),
                         ['C_p2'], ['C_Qw'])
                else:
                    P.op('dve', lambda e: e.tensor_tensor(out=p4[:, 0:W - 3], in0=p2[:, 0:W - 3], in1=p2[:, 2:W - 1], op=ALU.add), ['C_p2'], ['C_p4'])
                    P.op('dve', lambda e: e.tensor_tensor(out=Qw[0:64, 0:n1], in0=p4[0:64, 4:4 + n1], in1=p4[0:64, 8:8 + n1], op=ALU.add),
                         ['C_p4'], ['C_Qw'])
                    P.op('dve', lambda e: e.tensor_tensor(out=p8[64:128, 0:W - 7], in0=p4[64:128, 0:W - 7], in1=p4[64:128, 4:W - 3], op=ALU.add),
                         ['C_p4'], ['C_p8'])
                    P.op('dve', lambda e: e.tensor_tensor(out=Qw[64:128, 0:n1], in0=p8[64:128, 0:n1], in1=p8[64:128, 8:8 + n1], op=ALU.add),
                         ['C_p8'], ['C_Qw'])
                P.op('dve', lambda e: e.tensor_scalar(out=Ssum[:, 0:n], in0=Qw[:, 0:n], scalar1=lp['pool_fl'][:, 0:1], scalar2=None, op0=ALU.mult),
                     ['C_Qw', 'p_pool_fl'], ['C_S'])
                P.op('dve', lambda e: e.scalar_tensor_tensor(out=Ssum[:, 0:n], in0=Qw[:, 1:n1], scalar=lp['pool_fl'][:, 1:2], in1=Ssum[:, 0:n],
                                                             op0=ALU.mult, op1=ALU.add), ['C_Qw', 'C_S', 'p_pool_fl'], ['C_S'])
                if ti == 0:
                    P.op('dve', lambda e, ch=ch: e.tensor_tensor(out=Ssum[:, 0:16], in0=Ssum[:, 0:16], in1=lp['pool_corr'][:, ch, 0, :], op=ALU.mult),
                         ['C_S', 'p_pool_corr'], ['C_S'])
                if ti == nt_ - 1:
                    P.op('dve', lambda e, ch=ch: e.tensor_tensor(out=Ssum[:, n - 16:n], in0=Ssum[:, n - 16:n], in1=lp['pool_corr'][:, ch, 1, :], op=ALU.mult),
                         ['C_S', 'p_pool_corr'], ['C_S'])
                P.op('dve', lambda e, ch=ch, zc=zc: e.scalar_tensor_tensor(out=dd[:, ch, 0:n], in0=Ssum[:, 0:n], scalar=lp['pool_invw'][:, ch:ch + 1],
                                                                          in1=zc[:, 8:8 + n], op0=ALU.mult, op1=ALU.subtract),
                     ['C_S', f'C_zb.{ch}', 'p_pool_invw'], [f'C_dd.{ch}'])
                pp = po[ch]
                P.op('pe', lambda e, ch=ch, pp=pp: e.matmul(pp[:, 0:n], lhsT=lp['pool_bd'][:, ch, :], rhs=dd[:, ch, 0:n], start=True, stop=True),
                     ['p_pool_bd', f'C_dd.{ch}'], [f'C_po{ch}'])
                P.op('act', lambda e, ch=ch, pp=pp: e.activation(out=yT[:, 6 + ch, 0:n], in_=pp[:, 0:n], func=AF.Identity, scale=lp['pool_scale'][:, ch:ch + 1], bias=k.cst[:, 2:3]),
                     [f'C_po{ch}', 'p_pool_scale', 'cst'], [f'C_yT.{6 + ch}'])
            for g in range(ng):
                for nt in range(2):
                    pp = po[(2 * g + nt) % 4]
                    ppn = f'C_po{(2 * g + nt) % 4}'
                    for kc in range(8):
                        P.op('pe', lambda e, g=g, nt=nt, kc=kc, pp=pp: e.matmul(pp[:, :], lhsT=yT[:, kc, g * 128:(g + 1) * 128], rhs=k.w_out0[:, kc, nt * 512:(nt + 1) * 512],
                                                                               start=(kc == 0), stop=(kc == 7)),
                             [f'C_yT.{kc}', f'w_out0.{kc}'], [ppn])
                    P.op('dve', lambda e, g=g, nt=nt, pp=pp: e.tensor_tensor(out=xo[:, g, nt * 512:(nt + 1) * 512], in0=pp[:, :], in1=xt[:, g, nt * 512:(nt + 1) * 512], op=ALU.add),
                         [ppn, 'C_xt'], [f'C_xo.{g}'])
            P.dma('sp', seg.x05[1 + s:1 + s + n, :].rearrange("(g p) f -> p g f", p=128), xo[:, 0:ng, :], keys('C_xo', 4)[0:ng], [], 'C_xo')


def ffn_phase(k, l, s_, src, dst, TT, final, flush=True, out_rows=None):
    nc, P, I = k.nc, k.P, k.I
    tiles = tiles_of(TT)
    with ExitStack() as st:
        sb = lambda n, s, d: k.sb(n, s, d, st)
        ps = lambda n, s, d: k.ps(n, s, d, st)
        w_up = sb('F_wup', [128, 8, 2 * DFF], BF16)
        load_w_bf16(k, 'F_wup', w_up, I['ffn_w_up'][l], 8, 2 * DFF)
        w_dn = sb('F_wdn', [128, NFC, D], BF16)
        with ExitStack() as st2:
            fold_w_bf16(k, st2, 'F_wdn', w_dn, I['ffn_w_down'][l], NFC, k.gbc_d[l, s_, 1])
            P.barrier()
        cw = sb('F_cw', [128, 2 * NFC, 3], F32)
        cb = sb('F_cb', [128, 2 * NFC], F32)
        P.dma('sp', cw[:], I['ffn_cw'][:, l, :, :], [], ['F_cw'], 'F_cw')
        P.dma('sp', cb[:], I['ffn_cb'][:, l, :], [], ['F_cb'], 'F_cb')
        B = mod_bufs(k, st, 'F_')
        hT = sb('F_hT', [128, 8, 512], BF16)
        gT = sb('F_gT', [128, NFC, 512], BF16)
        prevu = [sb(f'F_prevu{i}', [128, 2 * NFC, 2], F32) for i in range(2)]
        P.op('dve', lambda e: e.memset(prevu[0][:], 0.0), [], keys('F_prevu0', 2 * NFC))
        acc = [sb(f'F_acc{i}', [128, 512], F32) for i in range(4)]
        xs = sb('F_xs', [128, 2, D], F32)
        xo = xs
        pu = [ps(f'F_pu{i}', [128, 512], F32) for i in range(4)]
        pd = [ps(f'F_pd{i}', [128, 512], F32) for i in range(2)]
        if final:
            fg = sb('F_fg', [128, D], F32)
            P.dma('sp', fg[:], I['final_g_bc'], [], ['F_fg'], 'F_fg')
            fss = sb('F_fss', [128, 12], F32)
            fjunk = B['junk']

        if os.environ.get('DBG_SBUF'):
            print('FFN sbuf remaining', nc.sbuf_bytes_remaining, 'final', final)
        def conv_gate(n, zero_u, ti):
            pin, pout = prevu[ti % 2], prevu[(ti + 1) % 2]
            pinn, poutn = f'F_prevu{ti % 2}', f'F_prevu{(ti + 1) % 2}'
            for c in range(NFC):
                q = c % 2
                AA = [acc[2 * q], acc[2 * q + 1]]
                AN = [f'F_acc{2 * q}', f'F_acc{2 * q + 1}']
                PP = [pu[2 * q], pu[2 * q + 1]]
                PN = [f'F_pu{2 * q}', f'F_pu{2 * q + 1}']
                CC = [c, NFC + c]
                if not zero_u:
                    for vi in range(2):
                        for kc in range(8):
                            P.op('pe', lambda e, kc=kc, cc=CC[vi], pp=PP[vi]: e.matmul(pp[:, 0:n], lhsT=w_up[:, kc, cc * 128:(cc + 1) * 128], rhs=hT[:, kc, 0:n],
                                                                                      start=(kc == 0), stop=(kc == 7)),
                                 [f'F_wup.{kc}', f'F_hT.{kc}'], [PN[vi]])
                    for vi in range(2):
                        P.op('act', lambda e, A_=AA[vi], pp=PP[vi], cc=CC[vi]: e.activation(out=A_[:, 0:n], in_=pp[:, 0:n], func=AF.Identity,
                                                                                          scale=cw[:, cc, 2:3], bias=cb[:, cc:cc + 1]),
                             [PN[vi], 'F_cw', 'F_cb'], [AN[vi]])
                    for vi in range(2):
                        if os.environ.get('DBG_SKIP_SAVE'):
                            continue
                        P.op('dve', lambda e, pp=PP[vi], cc=CC[vi]: e.tensor_copy(out=pout[:, cc, :], in_=pp[:, n - 2:n]), [PN[vi]], [f'{poutn}.{CC[vi]}'])
                    for vi in range(2):
                        P.op('dve', lambda e, A_=AA[vi], pp=PP[vi], cc=CC[vi]: e.scalar_tensor_tensor(out=A_[:, 1:n], in0=pp[:, 0:n - 1], scalar=cw[:, cc, 1:2], in1=A_[:, 1:n],
                                                                                                    op0=ALU.mult, op1=ALU.add), [PN[vi], AN[vi]], [AN[vi]])
                    for vi in range(2):
                        P.op('dve', lambda e, A_=AA[vi], pp=PP[vi], cc=CC[vi]: e.scalar_tensor_tensor(out=A_[:, 2:n], in0=pp[:, 0:n - 2], scalar=cw[:, cc, 0:1], in1=A_[:, 2:n],
                                                                                                    op0=ALU.mult, op1=ALU.add), [PN[vi], AN[vi]], [AN[vi]])
                else:
                    for vi in range(2):
                        P.op('act', lambda e, A_=AA[vi], cc=CC[vi]: e.activation(out=A_[:, 0:n], in_=k.zero[:, 0:n], func=AF.Identity,
                                                                                scale=cw[:, cc, 2:3], bias=cb[:, cc:cc + 1]),
                             ['zero', 'F_cw', 'F_cb'], [AN[vi]])
                for vi in range(2):
                    if os.environ.get('DBG_SKIP_TINY'):
                        continue
                    P.op('dve', lambda e, A_=AA[vi], cc=CC[vi]: e.scalar_tensor_tensor(out=A_[:, 0:2], in0=pin[:, cc, :], scalar=cw[:, cc, 0:1], in1=A_[:, 0:2],
                                                                                     op0=ALU.mult, op1=ALU.add), [f'{pinn}.{CC[vi]}', AN[vi]], [AN[vi]])
                for vi in range(2):
                    if os.environ.get('DBG_SKIP_TINY1'):
                        continue
                    P.op('dve', lambda e, A_=AA[vi], cc=CC[vi]: e.scalar_tensor_tensor(out=A_[:, 0:1], in0=pin[:, cc, 1:2], scalar=cw[:, cc, 1:2], in1=A_[:, 0:1],
                                                                                     op0=ALU.mult, op1=ALU.add), [f'{pinn}.{CC[vi]}', AN[vi]], [AN[vi]])
                P.op('act', lambda e, A_=AA[1]: e.activation(out=A_[:, 0:n], in_=A_[:, 0:n], func=AF.Silu), [AN[1]], [AN[1]])
                P.op('pool', lambda e, c=c, A0=AA[0], A1=AA[1]: e.tensor_tensor(out=gT[:, c, 0:n], in0=A0[:, 0:n], in1=A1[:, 0:n], op=ALU.mult),
                     [AN[0], AN[1]], [f'F_gT.{c}'])

        def down_res(tok0, n, nrows_last=128):
            ng = n // 128
            nr = lambda g: (nrows_last if g == ng - 1 else 128)
            for g_ in range(ng):
                g = g_ % 2
                r0 = 1 + tok0 + g_ * 128
                P.dma('sp', xs[0:nr(g_), g, :], src[r0:r0 + nr(g_), :], [], [f'F_xs.{g}'], f'F_xs{g}')
                for nt in range(2):
                    pp = pd[nt]
                    for kc in range(NFC):
                        P.op('pe', lambda e, g_=g_, nt=nt, kc=kc, pp=pp: e.matmul(pp[:, :], lhsT=gT[:, kc, g_ * 128:(g_ + 1) * 128], rhs=w_dn[:, kc, nt * 512:(nt + 1) * 512],
                                                                               start=(kc == 0), stop=(kc == NFC - 1)),
                             [f'F_gT.{kc}', f'F_wdn.{kc}'], [f'F_pd{nt}'])
                    P.op('dve', lambda e, g=g, g_=g_, nt=nt, pp=pp: e.tensor_tensor(out=xo[0:nr(g_), g, nt * 512:(nt + 1) * 512], in0=pp[0:nr(g_), :],
                                                                            in1=xs[0:nr(g_), g, nt * 512:(nt + 1) * 512], op=ALU.add),
                         [f'F_pd{nt}', f'F_xs.{g}'], [f'F_xs.{g}'])
                if final:
                    P.op('dve', lambda e: e.memset(fss[:, 0:1], 0.0), [], ['F_fss'])
                    P.op('act', lambda e, g=g: e.activation(out=fjunk[:], in_=xo[:, g, :], func=AF.Square, accum_out=fss[:, 0:1]),
                         [f'F_xs.{g}'], ['F_fss', 'F_junk'])
                    P.op('act', lambda e: e.activation(out=fss[:, 1:2], in_=fss[:, 0:1], func=AF.Sqrt, bias=k.cst[:, 0:1], scale=1.0 / D),
                         ['F_fss', 'cst'], ['F_fss'])
                    P.op('dve', lambda e: e.reciprocal(out=fss[:, 2:3], in_=fss[:, 1:2]), ['F_fss'], ['F_fss'])
                    P.op('dve', lambda e, g=g: e.scalar_tensor_tensor(out=xo[:, g, :], in0=xo[:, g, :], scalar=fss[:, 2:3], in1=fg[:],
                                                                     op0=ALU.mult, op1=ALU.mult), [f'F_xs.{g}', 'F_fss', 'F_fg'], [f'F_xs.{g}'])
                t0 = tok0 + g_ * 128
                if final:
                    lo = max(t0, 0)
                    hi = min(t0 + nr(g_), TT if out_rows is None else out_rows)
                    if hi > lo:
                        P.dma('sp', k.out[lo:hi, :], xo[lo - t0:hi - t0, g, :], [f'F_xs.{g}'], [], f'F_xs{g}')
                else:
                    P.dma('sp', dst[1 + t0:1 + t0 + nr(g_), :], xo[0:nr(g_), g, :], [f'F_xs.{g}'], [], f'F_xs{g}')

        for ti, (s, n) in enumerate(tiles):
            modulate_tile(k, B, src[1 + s:1 + s + n, :], n, l, s_, 2, hT, 'F_hT')
            conv_gate(n, False, ti)
            down_res(s - 1, n)
        if flush:
            conv_gate(128, True, len(tiles))
            down_res(TT - 1, 128, nrows_last=1)


def layer1(k, stop_after):
    nc, P, I, S = k.nc, k.P, k.I, k.S
    T = k.T
    TK = T + LC
    NKB = TK // 128
    TL = k.TL
    yc_d = nc.dram_tensor('yc_d', [768, T], BF16, kind="Internal").ap()
    with ExitStack() as st:
        sb = lambda n, s, d: k.sb(n, s, d, st)
        ps = lambda n, s, d: k.ps(n, s, d, st)
        w_in = sb('w_in1', [128, 8, 2816], BF16)
        load_w_bf16(k, 'w_in1', w_in, I['cd_w_in'], 8, 2816, colblk=1408)
        w_sw = sb('w_sw1', [128, 8, 1536], BF16)
        load_w_bf16(k, 'w_sw1', w_sw, I['cd_w_sw'], 8, 1536, colblk=1536)
        B = mod_bufs(k, st, 'E_')
        hT = sb('E_hT', [128, 8, 512], BF16)
        qk = sb('E_qk', [128, 12, 512], BF16)
        vt = sb('E_vt', [128, 4, 768], BF16)
        gl = sb('E_gl', [128, 2, 512], F32)
        rc = sb('E_rc', [128, 512], F32)
        rs = sb('E_rs', [128, 512], F32)
        t1 = sb('E_t1', [128, 512], F32)
        t2 = sb('E_t2', [128, 512], F32)
        sg = sb('E_sg', [128, 512], F32)
        pa = [ps(f'E_pa{i}', [128, 512], F32) for i in range(2)]
        pb = [ps(f'E_pb{i}', [128, 512], F32) for i in range(2)]
        glv = S['gl'].rearrange("(c p) t -> p c t", p=128)
        P.dma('sp', glv[:, :, 0:PADZ], k.zero[:, 0:2 * PADZ].rearrange("p (c t) -> p c t", c=2), ['zero'], [], 'E_glpad')
        P.dma('sp', glv[:, :, PADZ + T:PADZ + T + PADZ], k.zero[:, 0:2 * PADZ].rearrange("p (c t) -> p c t", c=2), ['zero'], [], 'E_glpad')
        qTv = S['qT'].rearrange("(c p) t -> p c t", p=128)
        kTv = S['kT'].rearrange("(c p) t -> p c t", p=128)

        def proj_plain(cols0, nch, dst, dstname, d0, n):
            for c in range(nch):
                pp = pa[c % 2]
                for kc in range(8):
                    P.op('pe', lambda e, c=c, kc=kc, pp=pp: e.matmul(pp[:, 0:n], lhsT=w_in[:, kc, cols0 + c * 128:cols0 + (c + 1) * 128], rhs=hT[:, kc, 0:n],
                                                                    start=(kc == 0), stop=(kc == 7)), [f'w_in1.{kc}', f'E_hT.{kc}'], [f'E_pa{c % 2}'])
                P.op('act', lambda e, c=c, pp=pp: e.copy(out=dst[:, d0 + c, 0:n], in_=pp[:, 0:n]), [f'E_pa{c % 2}'], [f'{dstname}.{d0 + c}'])

        def proj_v(n):
            ng = n // 128
            for g in range(ng):
                for (c0, cn, pp, ppn) in ((0, 512, pa[g % 2], f'E_pa{g % 2}'), (512, 256, pb[g % 2], f'E_pb{g % 2}')):
                    for kc in range(8):
                        P.op('pe', lambda e, g=g, kc=kc, pp=pp, c0=c0, cn=cn: e.matmul(pp[:, 0:cn], lhsT=hT[:, kc, g * 128:(g + 1) * 128],
                                                                                       rhs=w_in[:, kc, 1536 + c0:1536 + c0 + cn], start=(kc == 0), stop=(kc == 7)),
                             [f'w_in1.{kc}', f'E_hT.{kc}'], [ppn])
                    P.op('dve', lambda e, g=g, pp=pp, c0=c0, cn=cn: e.tensor_copy(out=vt[:, g, c0:c0 + cn], in_=pp[:, 0:cn]), [ppn], [f'E_vt.{g}'])

        n = LC
        modulate_tile(k, B, S['x1_c'][1:1 + LC, :], n, 1, 1, 0, hT, 'E_hT')
        proj_plain(768, 6, qk, 'E_qk', 6, n)
        P.dma('sp', kTv[:, :, T:T + n], qk[:, 6:12, 0:n], keys('E_qk', 12)[6:12], [], 'E_qk')
        proj_v(n)
        P.dma('sp', S['v'][T:T + n, :].rearrange("(g p) f -> p g f", p=128), vt[:, 0:n // 128, :], keys('E_vt', 4), [], 'E_vt')
        for ti, (s, n) in enumerate(tiles_of(T)):
            modulate_tile(k, B, S['x1_l'][1 + s:1 + s + n, :], n, 1, 0, 0, hT, 'E_hT')
            P.dma('sp', rc[:, 0:n], I['rope_c'][:, s:s + n], [], ['E_rc'], 'E_rc')
            P.dma('sp', rs[:, 0:n], I['rope_s'][:, s:s + n], [], ['E_rs'], 'E_rs')
            need_q = s < TL + 16
            for c in (range(12) if need_q else range(6, 12)):
                pp, pq = pa[c % 2], pb[c % 2]
                for kc in range(8):
                    P.op('pe', lambda e, c=c, kc=kc, pp=pp: e.matmul(pp[:, 0:n], lhsT=w_in[:, kc, c * 128:(c + 1) * 128], rhs=hT[:, kc, 0:n],
                                                                    start=(kc == 0), stop=(kc == 7)), [f'w_in1.{kc}', f'E_hT.{kc}'], [f'E_pa{c % 2}'])
                for kc in range(8):
                    P.op('pe', lambda e, c=c, kc=kc, pq=pq: e.matmul(pq[:, 0:n], lhsT=w_sw[:, kc, c * 128:(c + 1) * 128], rhs=hT[:, kc, 0:n],
                                                                    start=(kc == 0), stop=(kc == 7)), [f'w_sw1.{kc}', f'E_hT.{kc}'], [f'E_pb{c % 2}'])
                P.op('dve', lambda e, pp=pp: e.tensor_tensor(out=t1[:, 0:n], in0=pp[:, 0:n], in1=rc[:, 0:n], op=ALU.mult), [f'E_pa{c % 2}', 'E_rc'], ['E_t1'])
                P.op('dve', lambda e, pq=pq: e.tensor_tensor(out=t2[:, 0:n], in0=pq[:, 0:n], in1=rs[:, 0:n], op=ALU.mult), [f'E_pb{c % 2}', 'E_rs'], ['E_t2'])
                P.op('pool', lambda e, c=c: e.tensor_tensor(out=qk[:, c, 0:n], in0=t1[:, 0:n], in1=t2[:, 0:n], op=ALU.add), ['E_t1', 'E_t2'], [f'E_qk.{c}'])
            if need_q:
                P.dma('sp', qTv[:, :, s:s + n], qk[:, 0:6, 0:n], keys('E_qk', 12)[0:6], [], 'E_q')
            P.dma('sp', kTv[:, :, s:s + n], qk[:, 6:12, 0:n], keys('E_qk', 12)[6:12], [], 'E_qk')
            proj_v(n)
            P.dma('sp', S['v'][s:s + n, :].rearrange("(g p) f -> p g f", p=128), vt[:, 0:n // 128, :], keys('E_vt', 4), [], 'E_vt')
            for c in (range(2) if need_q else []):
                pp, pq = pa[c % 2], pb[c % 2]
                for (pz, pzn, cols) in ((pp, f'E_pa{c % 2}', 2304 + c * 128), (pq, f'E_pb{c % 2}', 2304 + 256 + c * 128)):
                    for kc in range(8):
                        P.op('pe', lambda e, kc=kc, pz=pz, cols=cols: e.matmul(pz[:, 0:n], lhsT=w_in[:, kc, cols:cols + 128], rhs=hT[:, kc, 0:n],
                                                                              start=(kc == 0), stop=(kc == 7)), [f'w_in1.{kc}', f'E_hT.{kc}'], [pzn])
                P.op('act', lambda e, pq=pq: e.activation(out=sg[:, 0:n], in_=pq[:, 0:n], func=AF.Sigmoid), [f'E_pb{c % 2}'], ['E_sg'])
                P.op('dve', lambda e, c=c, pp=pp: e.tensor_tensor(out=gl[:, c, 0:n], in0=pp[:, 0:n], in1=sg[:, 0:n], op=ALU.mult), [f'E_pa{c % 2}', 'E_sg'], [f'E_gl.{c}'])
            if need_q:
                P.dma('sp', glv[:, :, PADZ + s:PADZ + s + n], gl[:, :, 0:n], keys('E_gl', 2), [], 'E_gl')
    P.barrier()
    if stop_after == 'l1E':
        return

    with ExitStack() as st:
        sb = lambda n, s, d: k.sb(n, s, d, st)
        ps = lambda n, s, d: k.ps(n, s, d, st)
        dl = sb('G_dl', [1, 4, 64], F32)
        P.dma('sp', dl[:], I['diff_l'], [], ['G_dl'], 'G_dl')
        sm = sb('G_sm', [1, 8], F32)
        pr_ = sb('G_pr', [1, 2, 64], F32)
        P.op('dve', lambda e: e.tensor_tensor(out=pr_[:, 0, :], in0=dl[:, 0, :], in1=dl[:, 1, :], op=ALU.mult), ['G_dl'], ['G_pr'])
        P.op('dve', lambda e: e.tensor_tensor(out=pr_[:, 1, :], in0=dl[:, 2, :], in1=dl[:, 3, :], op=ALU.mult), ['G_dl'], ['G_pr'])
        P.op('dve', lambda e: e.reduce_sum(out=sm[:, 0:1], in_=pr_[:, 0, :], axis=AX.X), ['G_pr'], ['G_sm'])
        P.op('dve', lambda e: e.reduce_sum(out=sm[:, 1:2], in_=pr_[:, 1, :], axis=AX.X), ['G_pr'], ['G_sm'])
        P.op('act', lambda e: e.activation(out=sm[:, 2:4], in_=sm[:, 0:2], func=AF.Exp), ['G_sm'], ['G_sm'])
        P.op('dve', lambda e: e.tensor_tensor(out=sm[:, 4:5], in0=sm[:, 3:4], in1=sm[:, 2:3], op=ALU.subtract), ['G_sm'], ['G_sm'])
        P.op('dve', lambda e: e.tensor_scalar(out=sm[:, 5:6], in0=sm[:, 4:5], scalar1=-LAMBDA_INIT1, scalar2=None, op0=ALU.add), ['G_sm'], ['G_sm'])
        neglam = sb('G_neglam', [128, 2], F32)
        gsub = sb('G_gsub', [128, 2], F32)
        P.dma('sp', gsub[:, 0:1], I['subln_g'], [], ['G_gsub'], 'G_gsub')
        P.op('dve', lambda e: e.tensor_scalar(out=gsub[:, 1:2], in0=gsub[:, 0:1], scalar1=(1.0 - LAMBDA_INIT1), scalar2=None, op0=ALU.mult), ['G_gsub'], ['G_gsub'])
        pS = [ps(f'G_pS{i}', [128, 2, 512], F32) for i in range(2)]
        po = [ps(f'G_po{i}', [128, 512], F32) for i in range(2)]
        pl = ps('G_pl', [128, 2, 512], F32)
        P.op('pe', lambda e: e.matmul(pl[:, 0, 0:1], lhsT=k.ones_f[0:1, :], rhs=sm[0:1, 5:6], start=True, stop=True), ['ones_f', 'G_sm'], ['G_pl'])
        P.op('dve', lambda e: e.tensor_copy(out=neglam[:, 0:1], in_=pl[:, 0, 0:1]), ['G_pl'], ['G_neglam'])
        kh = [[sb(f'G_kh{j}_{i}', [128, TK], BF16) for i in range(2)] for j in range(2)]
        for j in range(2):
            P.op('pool', lambda e, j=j: e.memset(kh[j][0][64:128, :], 0.0), [], [f'G_kh{j}_0'])
            P.op('pool', lambda e, j=j: e.memset(kh[j][1][0:64, :], 0.0), [], [f'G_kh{j}_1'])
        vh = [sb(f'G_vh{j}', [128, NKB, 128], BF16) for j in range(2)]
        qh = [sb(f'G_qh{j}', [128, 512], BF16) for j in range(2)]
        pT = [sb(f'G_pT{i}', [128, 2, 512], BF16) for i in range(4)]
        accs = [sb(f'G_acc{i}', [128, 2, 512], F32) for i in range(2)]
        rl = sb('G_rl', [128, 2, 512], F32)
        o1 = sb('G_o1', [128, 512], F32)
        o2 = sb('G_o2', [128, 512], F32)
        sq = sb('G_sq', [128, 512], F32)
        ych = sb('G_ych', [128, 512], BF16)
        vv = S['v'].rearrange("(kb p) f -> p kb f", p=128)
        qtiles = tiles_of(TL)

        def load_head(h):
            j = h % 2
            P.dma('sp', kh[j][0][0:64, :], S['kT'][h * 128:h * 128 + 64, :], [], [f'G_kh{j}_0'], f'G_kh{j}_0')
            P.dma('sp', kh[j][1][64:128, :], S['kT'][h * 128 + 64:h * 128 + 128, :], [], [f'G_kh{j}_1'], f'G_kh{j}_1')
            for b0 in range(0, NKB, 16):
                b1 = min(NKB, b0 + 16)
                P.dma('sp', vh[j][:, b0:b1, :], vv[:, b0:b1, h * 128:(h + 1) * 128], [], [f'G_vh{j}'], f'G_vh{j}')

        def load_q(h, ti):
            s, n = qtiles[ti]
            gi = (h * len(qtiles) + ti) % 2
            P.dma('sp', qh[gi][:, 0:n], S['qT'][h * 128:(h + 1) * 128, s:s + n], [], [f'G_qh{gi}'], f'G_qh{gi}')

        load_head(0)
        load_q(0, 0)
        for h in range(6):
            hj = h % 2
            for ti, (s, n) in enumerate(qtiles):
                gi = (h * len(qtiles) + ti) % 2
                qcur = qh[gi]
                qn_ = f'G_qh{gi}'
                if ti + 1 < len(qtiles):
                    load_q(h, ti + 1)
                elif h + 1 < 6:
                    load_q(h + 1, 0)
                if ti == 0 and h + 1 < 6:
                    load_head(h + 1)

                def emit_qk(kb):
                    pp = pS[kb % 2]
                    for comp in range(2):
                        P.op('pe', lambda e, comp=comp, kb=kb, pp=pp: e.matmul(pp[:, comp, 0:n], lhsT=kh[hj][comp][:, kb * 128:(kb + 1) * 128], rhs=qcur[:, 0:n], start=True, stop=True),
                             [f'G_kh{hj}_{comp}', qn_], [f'G_pS{kb % 2}'])
                emit_qk(0)
                first = [True, True]
                for kb in range(NKB):
                    if kb + 1 < NKB:
                        emit_qk(kb + 1)
                    pp = pS[kb % 2]
                    ppn = f'G_pS{kb % 2}'
                    pt = pT[kb % 4]
                    ptn = f'G_pT{kb % 4}'
                    P.op('act', lambda e, pp=pp, pt=pt: e.activation(out=pt[:, :, 0:n], in_=pp[:, :, 0:n], func=AF.Exp, scale=0.125), [ppn], [ptn])
                    for comp in range(2):
                        P.op('pe', lambda e, comp=comp, kb=kb, pt=pt: e.matmul(po[comp][:, 0:n], lhsT=vh[hj][:, kb, :], rhs=pt[:, comp, 0:n], start=(kb == 0), stop=(kb == NKB - 1)),
                             [f'G_vh{hj}', ptn], [f'G_po{comp}'])
                    ai = 1 if kb % 3 == 2 else 0
                    ae = 'pool' if ai == 1 else 'dve'
                    ac = accs[ai]
                    acn = f'G_acc{ai}'
                    if first[ai]:
                        first[ai] = False
                        P.op(ae, lambda e, ac=ac, pt=pt: e.tensor_copy(out=ac[:, :, 0:n], in_=pt[:, :, 0:n]), [ptn], [acn])
                    else:
                        P.op(ae, lambda e, ac=ac, pt=pt: e.tensor_tensor(out=ac[:, :, 0:n], in0=ac[:, :, 0:n], in1=pt[:, :, 0:n], op=ALU.add), [ptn, acn], [acn])
                for comp in range(2):
                    for i in range(2):
                        P.op('pe', lambda e, comp=comp, i=i: e.matmul(pl[:, comp, 0:n], lhsT=k.ones_f[:], rhs=accs[i][:, comp, 0:n], start=(i == 0), stop=(i == 1)),
                             ['ones_f', f'G_acc{i}'], ['G_pl'])
                P.op('dve', lambda e: e.reciprocal(out=rl[:, :, 0:n], in_=pl[:, :, 0:n]), ['G_pl'], ['G_rl'])
                P.op('dve', lambda e: e.tensor_tensor(out=o1[:, 0:n], in0=po[0][:, 0:n], in1=rl[:, 0, 0:n], op=ALU.mult), ['G_po0', 'G_rl'], ['G_o1'])
                P.op('dve', lambda e: e.tensor_tensor(out=o2[:, 0:n], in0=po[1][:, 0:n], in1=rl[:, 1, 0:n], op=ALU.mult), ['G_po1', 'G_rl'], ['G_o2'])
                P.op('dve', lambda e: e.scalar_tensor_tensor(out=o1[:, 0:n], in0=o2[:, 0:n], scalar=neglam[:, 0:1], in1=o1[:, 0:n], op0=ALU.mult, op1=ALU.add),
                     ['G_o1', 'G_o2', 'G_neglam'], ['G_o1'])
                P.op('act', lambda e: e.activation(out=sq[:, 0:n], in_=o1[:, 0:n], func=AF.Square), ['G_o1'], ['G_sq'])
                P.op('pe', lambda e: e.matmul(pl[:, 0, 0:n], lhsT=k.ones_f[:], rhs=sq[:, 0:n], start=True, stop=True), ['ones_f', 'G_sq'], ['G_pl'])
                P.op('act', lambda e: e.activation(out=sq[:, 0:n], in_=pl[:, 0, 0:n], func=AF.Sqrt, bias=k.cst[:, 0:1], scale=1.0 / 128), ['G_pl', 'cst'], ['G_sq'])
                P.op('dve', lambda e: e.reciprocal(out=sq[:, 0:n], in_=sq[:, 0:n]), ['G_sq'], ['G_sq'])
                P.op('dve', lambda e: e.scalar_tensor_tensor(out=ych[:, 0:n], in0=o1[:, 0:n], scalar=gsub[:, 1:2], in1=sq[:, 0:n], op0=ALU.mult, op1=ALU.mult),
                     ['G_o1', 'G_sq', 'G_gsub'], ['G_ych'])
                P.dma('sp', yc_d[h * 128:(h + 1) * 128, s:s + n], ych[:, 0:n], ['G_ych'], [], 'G_ych')
    P.barrier()
    if stop_after == 'l1F1':
        return

    with ExitStack() as st:
        sb = lambda n, s, d: k.sb(n, s, d, st)
        ps = lambda n, s, d: k.ps(n, s, d, st)
        w_out = sb('w_out1', [128, 8, D], BF16)
        with ExitStack() as st2:
            fold_w_bf16(k, st2, 'w_out1', w_out, I['cd_w_out'], 8, k.gbc_d[1, 0, 0])
            P.barrier()
        cp = {}
        for nm, shp in (('conf_w', [128, 2, 31]), ('conf_b', [128, 2]), ('conf_lng', [128, 2]), ('conf_lnb', [128, 2])):
            cp[nm] = sb('H_' + nm, shp, F32)
            P.dma('sp', cp[nm][:], I[nm], [], ['H_' + nm], 'H_' + nm)
        yT = sb('H_yT', [128, 8, 512], BF16)
        gin = sb('H_gin', [128, 2, 542], F32)
        ca = sb('H_ca', [128, 512], F32)
        cb_ = sb('H_cb', [128, 512], F32)
        ct = [sb(f'H_ct{i}', [128, 512], F32) for i in range(2)]
        xm = sb('H_xm', [128, 2, 512], F32)
        sq = sb('H_sq', [128, 2, 512], F32)
        rstd = sb('H_rstd', [128, 512], F32)
        xt = sb('H_xt', [128, 4, D], F32)
        xo = sb('H_xo', [128, 4, D], F32)
        pm = ps('H_pm', [128, 512], F32)
        pv = ps('H_pv', [128, 512], F32)
        po = [ps(f'H_po{i}', [128, 512], F32) for i in range(4)]
        glv = S['gl'].rearrange("(c p) t -> p c t", p=128)
        ycv = yc_d.rearrange("(c p) t -> p c t", p=128)
        for (s, n) in tiles_of(TL):
            ng = n // 128
            P.dma('sp', yT[:, 0:6, 0:n], ycv[:, :, s:s + n], [], keys('H_yT', 8)[0:6], 'H_yT')
            P.dma('sp', gin[:, :, 0:n + 30], glv[:, :, PADZ + s - 15:PADZ + s + n + 15], [], keys('H_gin', 2), 'H_gin')
            P.dma('sp', xt[:, 0:ng, :], S['x1_l'][1 + s:1 + s + n, :].rearrange("(g p) f -> p g f", p=128), [], ['H_xt'], 'H_xt')
            for c in range(2):
                P.op('act', lambda e, c=c: e.activation(out=ca[:, 0:n], in_=gin[:, c, 0:n], func=AF.Identity, scale=cp['conf_w'][:, c, 0:1], bias=cp['conf_b'][:, c:c + 1]),
                     [f'H_gin.{c}', 'H_conf_w', 'H_conf_b'], ['H_ca'])
                P.op('pool', lambda e, c=c: e.tensor_scalar(out=cb_[:, 0:n], in0=gin[:, c, 1:1 + n], scalar1=cp['conf_w'][:, c, 1:2], scalar2=None, op0=ALU.mult),
                     [f'H_gin.{c}', 'H_conf_w'], ['H_cb'])
                for t in range(2, 31):
                    if t % 2 == 0:
                        P.op('dve', lambda e, c=c, t=t: e.scalar_tensor_tensor(out=ca[:, 0:n], in0=gin[:, c, t:t + n], scalar=cp['conf_w'][:, c, t:t + 1], in1=ca[:, 0:n],
                                                                               op0=ALU.mult, op1=ALU.add), [f'H_gin.{c}', 'H_ca'], ['H_ca'])
                    else:
                        ctt = ct[(t // 2) % 2]
                        ctn = f'H_ct{(t // 2) % 2}'
                        P.op('act', lambda e, c=c, t=t, ctt=ctt: e.activation(out=ctt[:, 0:n], in_=gin[:, c, t:t + n], func=AF.Identity, scale=cp['conf_w'][:, c, t:t + 1], bias=k.cst[:, 2:3]),
                             [f'H_gin.{c}', 'H_conf_w', 'cst'], [ctn])
                        P.op('pool', lambda e, ctt=ctt: e.tensor_tensor(out=cb_[:, 0:n], in0=cb_[:, 0:n], in1=ctt[:, 0:n], op=ALU.add), [ctn, 'H_cb'], ['H_cb'])
                P.op('dve', lambda e, c=c: e.tensor_tensor(out=xm[:, c, 0:n], in0=ca[:, 0:n], in1=cb_[:, 0:n], op=ALU.add), ['H_ca', 'H_cb'], [f'H_xm.{c}'])
            for c in range(2):
                P.op('pe', lambda e, c=c: e.matmul(pm[:, 0:n], lhsT=k.ones_f[:], rhs=xm[:, c, 0:n], start=(c == 0), stop=(c == 1)), ['ones_f', f'H_xm.{c}'], ['H_pm'])
            for c in range(2):
                P.op('dve', lambda e, c=c: e.scalar_tensor_tensor(out=xm[:, c, 0:n], in0=pm[:, 0:n], scalar=-1.0 / 256, in1=xm[:, c, 0:n], op0=ALU.mult, op1=ALU.add),
                     ['H_pm', f'H_xm.{c}'], [f'H_xm.{c}'])
                P.op('act', lambda e, c=c: e.activation(out=sq[:, c, 0:n], in_=xm[:, c, 0:n], func=AF.Square), [f'H_xm.{c}'], [f'H_sq.{c}'])
            for c in range(2):
                P.op('pe', lambda e, c=c: e.matmul(pv[:, 0:n], lhsT=k.ones_f[:], rhs=sq[:, c, 0:n], start=(c == 0), stop=(c == 1)), ['ones_f', f'H_sq.{c}'], ['H_pv'])
            P.op('act', lambda e: e.activation(out=rstd[:, 0:n], in_=pv[:, 0:n], func=AF.Sqrt, bias=k.cst[:, 0:1], scale=1.0 / 256), ['H_pv', 'cst'], ['H_rstd'])
            P.op('dve', lambda e: e.reciprocal(out=rstd[:, 0:n], in_=rstd[:, 0:n]), ['H_rstd'], ['H_rstd'])
            for c in range(2):
                P.op('dve', lambda e, c=c: e.tensor_tensor(out=xm[:, c, 0:n], in0=xm[:, c, 0:n], in1=rstd[:, 0:n], op=ALU.mult), [f'H_xm.{c}', 'H_rstd'], [f'H_xm.{c}'])
                P.op('act', lambda e, c=c: e.activation(out=yT[:, 6 + c, 0:n], in_=xm[:, c, 0:n], func=AF.Silu, scale=cp['conf_lng'][:, c:c + 1], bias=cp['conf_lnb'][:, c:c + 1]),
                     [f'H_xm.{c}', 'H_conf_lng', 'H_conf_lnb'], [f'H_yT.{6 + c}'])
            for g in range(ng):
                for nt in range(2):
                    pp = po[(2 * g + nt) % 4]
                    ppn = f'H_po{(2 * g + nt) % 4}'
                    for kc in range(8):
                        P.op('pe', lambda e, g=g, nt=nt, kc=kc, pp=pp: e.matmul(pp[:, :], lhsT=yT[:, kc, g * 128:(g + 1) * 128], rhs=w_out[:, kc, nt * 512:(nt + 1) * 512],
                                                                               start=(kc == 0), stop=(kc == 7)), [f'H_yT.{kc}', f'w_out1.{kc}'], [ppn])
                    P.op('dve', lambda e, g=g, nt=nt, pp=pp: e.tensor_tensor(out=xo[:, g, nt * 512:(nt + 1) * 512], in0=pp[:, :], in1=xt[:, g, nt * 512:(nt + 1) * 512], op=ALU.add),
                         [ppn, 'H_xt'], [f'H_xo.{g}'])
            P.dma('sp', S['x15'][1 + s:1 + s + n, :].rearrange("(g p) f -> p g f", p=128), xo[:, 0:ng, :], keys('H_xo', 4)[0:ng], [], 'H_xo')
    P.barrier()
    if stop_after == 'l1F2':
        return
    ffn_phase(k, 1, 0, S['x15'], None, TL, final=True, flush=(not k.half), out_rows=k.TO)
    P.barrier()


def _fm(v, nch):
    return np.ascontiguousarray(np.asarray(v, np.float32).reshape(nch, 128).T)


def prep_shared(inp, T):
    f = lambda a: np.ascontiguousarray(np.asarray(a, np.float32))
    d = {}
    d['mod_w'] = f(inp['mod_w'])
    d['mod_b'] = f(inp['mod_b'])
    d['modb_fm'] = f(np.asarray(inp['mod_b']).reshape(2, 6, 8, 128).transpose(3, 0, 1, 2))
    ng = np.stack([np.asarray(inp['norm_mix_g']), np.asarray(inp['norm_ffn_g'])], axis=1)
    d['ng_fm'] = f(ng.reshape(2, 2, 8, 128).transpose(3, 0, 1, 2))
    d['final_g_bc'] = f(np.broadcast_to(np.asarray(inp['final_g'])[None, :], (128, D)))
    d['ident'] = f(np.eye(128))
    d['ab_w_in'] = f(inp['ab_w_in'][0])
    d['ab_w_out'] = f(inp['ab_w_out'][0])
    cw4 = np.asarray(inp['lru_conv_w'][0], np.float32).reshape(4, 6, 128).transpose(2, 1, 0)
    d['lru_cw'] = f(np.concatenate([cw4, np.zeros((128, 6, 1), np.float32)], axis=2))
    d['pool_fl'] = f(np.stack([np.ones(128), np.zeros(128)], axis=1))
    d['lru_cb'] = _fm(inp['lru_conv_b'][0], 6)
    for nm, src in (('lru_bdA', inp['lru_wa'][0]), ('lru_bdX', inp['lru_wx'][0])):
        src = np.asarray(src)
        bd = np.zeros((128, 2, 6, 128), np.float32)
        for dd in range(2):
            for c in range(6):
                bd[0:64, dd, c, 0:64] = src[dd, 2 * c]
                bd[64:128, dd, c, 64:128] = src[dd, 2 * c + 1]
        d[nm] = bd
    for nm, src in (('lru_ba', inp['lru_ba'][0]), ('lru_bx', inp['lru_bx'][0]), ('lru_lam', inp['lru_lambda'][0])):
        d[nm] = f(np.asarray(src).reshape(2, 6, 128).transpose(2, 0, 1))
    pw = np.asarray(inp['pool_w'][0])
    bd = np.zeros((128, 2, 128), np.float32)
    for ch in range(2):
        bd[0:64, ch, 0:64] = pw[2 * ch]
        bd[64:128, ch, 64:128] = pw[2 * ch + 1]
    d['pool_bd'] = bd
    d['pool_scale'] = _fm(inp['pool_scale'][0], 2)
    invw = np.zeros((128, 2), np.float32)
    corr = np.ones((128, 2, 2, 16), np.float32)
    wins = (2, 4, 8, 16)
    Lbig = 1 << 20
    for g, w in enumerate(wins):
        ch, half = g // 2, g % 2
        psl = slice(64 * half, 64 * half + 64)
        invw[psl, ch] = 1.0 / w
        for i in range(16):
            t = i
            cnt = (t + w - w // 2) - max(t - w // 2, 0)
            corr[psl, ch, 0, i] = float(w) / cnt
            t = Lbig - 16 + i
            cnt = min(t + w - w // 2, Lbig) - (t - w // 2)
            corr[psl, ch, 1, i] = float(w) / cnt
    d['pool_invw'] = invw
    d['pool_corr'] = corr
    d['ffn_w_up'] = f(inp['ffn_w_up'])
    d['ffn_w_down'] = f(inp['ffn_w_down'])
    d['ffn_cw'] = f(np.asarray(inp['ffn_conv_w']).reshape(2, 3, 2 * NFC, 128).transpose(3, 0, 2, 1))
    d['ffn_cb'] = f(np.asarray(inp['ffn_conv_b']).reshape(2, 2 * NFC, 128).transpose(2, 0, 1))
    w_in = np.asarray(inp['cd_w_in'][0], np.float32)
    d['cd_w_in'] = f(w_in)
    qk = w_in[:, :1536].reshape(D, 1536 // 32, 2, 16)
    d['cd_w_sw'] = f(qk[:, :, ::-1, :].reshape(D, 1536))
    d['cd_w_out'] = f(inp['cd_w_out'][0])
    t = np.arange(T)
    row = (t // GRID_W).astype(np.float32)
    col = (t % GRID_W).astype(np.float32)
    inv = (10000.0 ** (-np.arange(16, dtype=np.float32) / 16)).astype(np.float32)
    ang_r = (row[:, None] * inv).astype(np.float32)
    ang_c = (col[:, None] * inv).astype(np.float32)
    rc = np.zeros((128, T), np.float32)
    rs = np.zeros((128, T), np.float32)
    for p in range(128):
        dd = p % 64
        ang = ang_r if dd < 32 else ang_c
        fq = dd % 16
        first = (dd % 32) < 16
        rc[p] = np.cos(ang[:, fq])
        rs[p] = (-1.0 if first else 1.0) * np.sin(ang[:, fq])
    d['rope_c'] = rc
    d['rope_s'] = rs
    d['diff_l'] = f(np.stack([np.asarray(inp['diff_lq1'][0]), np.asarray(inp['diff_lk1'][0]),
                              np.asarray(inp['diff_lq2'][0]), np.asarray(inp['diff_lk2'][0])])[None])
    d['subln_g'] = f(np.asarray(inp['diff_subln_g'][0]).reshape(128, 1))
    d['conf_w'] = f(np.asarray(inp['conf_dw_w'][0]).reshape(31, 2, 128).transpose(2, 1, 0))
    d['conf_b'] = _fm(inp['conf_dw_b'][0], 2)
    d['conf_lng'] = _fm(inp['conf_ln_g'][0], 2)
    d['conf_lnb'] = _fm(inp['conf_ln_b'][0], 2)
    return d


def prep_rev(shared, T):
    f = lambda a: np.ascontiguousarray(np.asarray(a, np.float32))
    d = dict(shared)
    cw = shared['lru_cw']
    d['lru_cw'] = f(cw[:, :, ::-1])
    for nm in ('lru_bdA', 'lru_bdX', 'lru_ba', 'lru_bx', 'lru_lam'):
        d[nm] = f(shared[nm][:, ::-1])
    d['pool_fl'] = f(np.stack([np.zeros(128), np.ones(128)], axis=1))
    corr = np.ones((128, 2, 2, 16), np.float32)
    Lbig = 1 << 20
    for g, w in enumerate((2, 4, 8, 16)):
        ch, half = g // 2, g % 2
        psl = slice(64 * half, 64 * half + 64)
        for i in range(16):
            r = i
            cnt = (r + w // 2) - max(r - w // 2 + 1, 0) + 1
            corr[psl, ch, 0, i] = float(w) / cnt
            r = Lbig - 16 + i
            cnt = min(r + w // 2, Lbig - 1) - (r - w // 2 + 1) + 1
            corr[psl, ch, 1, i] = float(w) / cnt
    d['pool_corr'] = corr
    d['ffn_cw'] = f(shared['ffn_cw'][:, :, :, ::-1])
    d['conf_w'] = f(shared['conf_w'][:, :, ::-1])
    d['rope_c'] = f(shared['rope_c'][:, ::-1])
    d['rope_s'] = f(shared['rope_s'][:, ::-1])
    return d


def prep_core(inp, shared, b, T, rev=False):
    m = dict(shared)
    x = np.asarray(inp['x'][b, :T], np.float32)
    ctx = np.asarray(inp['ctx'][b], np.float32)
    if rev:
        x = x[::-1]
        ctx = ctx[::-1]
    m['x'] = np.ascontiguousarray(x)
    m['ctx'] = np.ascontiguousarray(ctx)
    m['c_fm'] = np.ascontiguousarray(np.stack([_fm(inp['c'][b], 8), _fm(inp['c_ctx'], 8)], axis=1))
    return m


_CACHE = {}


def kernel(**inputs):
    T = inputs['x'].shape[1]
    Bn = inputs['x'].shape[0]
    if T not in _CACHE:
        _CACHE[T] = build(T)
    kk = _CACHE[T]
    shared = prep_shared(inputs, T)
    shared_r = prep_rev(shared, T)
    in_maps = [prep_core(inputs, shared, b, T, rev=False) for b in range(Bn)] + \
              [prep_core(inputs, shared_r, b, T, rev=True) for b in range(Bn)]
    res = run_bass_kernel_spmd(kk.nc, in_maps, core_ids=list(range(2 * Bn)))
    out = np.empty((Bn, T, D), np.float32)
    for b in range(Bn):
        out[b, :T // 2] = np.asarray(res.results[b]['out'], np.float32)
        out[b, T // 2:] = np.asarray(res.results[Bn + b]['out'], np.float32)[::-1]
    return out
```

```python
import numpy as np
import math
import os
from contextlib import ExitStack
import concourse.bass as bass
import concourse.mybir as mybir
from concourse.bass_utils import run_bass_kernel_spmd
from concourse.ap import AP

F32 = mybir.dt.float32
BF16 = mybir.dt.bfloat16
AF = mybir.ActivationFunctionType
ALU = mybir.AluOpType
AX = mybir.AxisListType

D = 1024
LC = 256
DFF = 2816
NFC = DFF // 128
PADZ = 16
EPS = 1e-6
GRID_W = 64
LAMBDA_INIT1 = 0.8 - 0.6 * math.exp(-0.3 * 1)
SAME_ENGINE_SYNC = True


def rev(ap):
    a = [list(x) for x in ap.ap]
    st, n = a[-1]
    a[-1] = [-st, n]
    return AP(ap.tensor, ap.offset + st * (n - 1), a)


class Prog:
    def __init__(self, nc, es):
        self.nc = nc
        self.es = es
        self.eng = {'pe': nc.tensor, 'act': nc.scalar, 'dve': nc.vector, 'pool': nc.gpsimd, 'sp': nc.sync}
        self.esem = {e: es.enter_context(nc.semaphore('S_' + e)) for e in ('pe', 'act', 'dve', 'pool')}
        self.ecnt = {e: 0 for e in self.esem}
        self.dsem = {}
        self.dpool = []
        self.nd = 0
        self.waited = {e: {} for e in self.eng}
        self.lastw = {}
        self.rd = {}
        self.ninstr = 0

    def _need(self, reads, writes):
        ev = {}

        def add(e):
            if e is None:
                return
            k, sem, val = e
            if k not in ev or ev[k][1] < val:
                ev[k] = (sem, val)
        for r in reads:
            add(self.lastw.get(r))
        for w in writes:
            add(self.lastw.get(w))
            for e in self.rd.get(w, {}).items():
                add((e[0], e[1][0], e[1][1]))
        return ev

    def _wait(self, e, ev):
        for k, (sem, val) in ev.items():
            if k == 'S_' + e and (e == 'pe' or not SAME_ENGINE_SYNC):
                continue
            if self.waited[e].get(k, 0) < val:
                self.eng[e].wait_ge(sem, val)
                self.waited[e][k] = val
                self.ninstr += 1

    def _commit(self, ev, reads, writes):
        k, sem, val = ev
        for w in writes:
            self.lastw[w] = ev
            self.rd[w] = {}
        for r in reads:
            self.rd.setdefault(r, {})[k] = (sem, val)

    def op(self, e, fn, reads, writes):
        self._wait(e, self._need(reads, writes))
        ins = fn(self.eng[e])
        self.ecnt[e] += 1
        ins.then_inc(self.esem[e], 1)
        self.ninstr += 1
        self._commit(('S_' + e, self.esem[e], self.ecnt[e]), reads, writes)

    def dma(self, q, out, in_, reads, writes, key):
        self._wait(q, self._need(reads, writes))
        if key not in self.dsem:
            if self.dpool:
                self.dsem[key] = self.dpool.pop()
            else:
                nm = 'D%d' % self.nd
                self.nd += 1
                self.dsem[key] = [self.es.enter_context(self.nc.semaphore(nm)), 0, nm]
        d = self.dsem[key]
        ins = self.eng[q].dma_start(out=out, in_=in_)
        d[1] += 16
        ins.then_inc(d[0], 16)
        self.ninstr += 1
        self._commit((d[2], d[0], d[1]), reads, writes)

    def barrier(self):
        ev = {}
        for e in self.esem:
            if self.ecnt[e] > 0:
                ev['S_' + e] = (self.esem[e], self.ecnt[e])
        for k, d in self.dsem.items():
            if d[1] > 0:
                ev[d[2]] = (d[0], d[1])
        for e in self.eng:
            self._wait(e, dict(ev))
        self.lastw = {}
        self.rd = {}
        for k, d in self.dsem.items():
            self.dpool.append(d)
        self.dsem = {}


def keys(name, n):
    return [f"{name}.{i}" for i in range(n)]


def tiles_of(T, w=512):
    out = []
    s = 0
    while s < T:
        n = min(w, T - s)
        out.append((s, n))
        s += n
    return out


class K:
    pass


def build(T, dbg=False, stop_after=None, half=True):
    nc = bass.Bass("TRN2", target_bir_lowering=False)
    k = K()
    k.nc = nc
    k.T = T
    k.half = half
    k.TL = (T // 2 + 128) if half else T
    k.TO = (T // 2) if half else T

    def din(name, shape, dt=F32):
        return nc.dram_tensor(name, list(shape), dt, kind="ExternalInput").ap()

    def dscr(name, shape, dt=F32, out=False):
        kind = "ExternalOutput" if (out or dbg) else "Internal"
        return nc.dram_tensor(name, list(shape), dt, kind=kind).ap()

    I = {}
    I['x'] = din('x', [T, D])
    I['ctx'] = din('ctx', [LC, D])
    I['c_fm'] = din('c_fm', [128, 2, 8])
    I['mod_w'] = din('mod_w', [2, D, 6 * D])
    I['modb_fm'] = din('modb_fm', [128, 2, 6, 8])
    I['mod_b'] = din('mod_b', [2, 6 * D])
    I['ng_fm'] = din('ng_fm', [128, 2, 2, 8])
    I['final_g_bc'] = din('final_g_bc', [128, D])
    I['ident'] = din('ident', [128, 128])
    I['ab_w_in'] = din('ab_w_in', [D, 1792])
    I['ab_w_out'] = din('ab_w_out', [D, D])
    I['lru_cw'] = din('lru_cw', [128, 6, 5])
    I['pool_fl'] = din('pool_fl', [128, 2])
    I['lru_cb'] = din('lru_cb', [128, 6])
    I['lru_bdA'] = din('lru_bdA', [128, 2, 6, 128])
    I['lru_bdX'] = din('lru_bdX', [128, 2, 6, 128])
    I['lru_ba'] = din('lru_ba', [128, 2, 6])
    I['lru_bx'] = din('lru_bx', [128, 2, 6])
    I['lru_lam'] = din('lru_lam', [128, 2, 6])
    I['pool_bd'] = din('pool_bd', [128, 2, 128])
    I['pool_scale'] = din('pool_scale', [128, 2])
    I['pool_invw'] = din('pool_invw', [128, 2])
    I['pool_corr'] = din('pool_corr', [128, 2, 2, 16])
    I['ffn_w_up'] = din('ffn_w_up', [2, D, 2 * DFF])
    I['ffn_w_down'] = din('ffn_w_down', [2, DFF, D])
    I['ffn_cw'] = din('ffn_cw', [128, 2, 2 * NFC, 3])
    I['ffn_cb'] = din('ffn_cb', [128, 2, 2 * NFC])
    I['cd_w_in'] = din('cd_w_in', [D, 2816])
    I['cd_w_sw'] = din('cd_w_sw', [D, 1536])
    I['cd_w_out'] = din('cd_w_out', [D, D])
    I['rope_c'] = din('rope_c', [128, T])
    I['rope_s'] = din('rope_s', [128, T])
    I['diff_l'] = din('diff_l', [1, 4, 64])
    I['subln_g'] = din('subln_g', [128, 1])
    I['conf_w'] = din('conf_w', [128, 2, 31])
    I['conf_b'] = din('conf_b', [128, 2])
    I['conf_lng'] = din('conf_lng', [128, 2])
    I['conf_lnb'] = din('conf_lnb', [128, 2])
    k.I = I

    k.out = nc.dram_tensor('out', [k.TO, D], F32, kind="ExternalOutput").ap()
    S = {}
    for nm, TT in (('l', T), ('c', LC)):
        S['z_' + nm] = dscr('z_' + nm, [1792, PADZ + TT + PADZ])
        S['xa_' + nm] = dscr('xa_' + nm, [768, TT])
        S['hf_' + nm] = dscr('hf_' + nm, [768, TT])
        S['x05_' + nm] = dscr('x05_' + nm, [1 + TT, D])
        S['x1_' + nm] = dscr('x1_' + nm, [1 + TT + 128, D])
    S['x15'] = dscr('x15', [1 + T, D])
    S['x2'] = dscr('x2', [1 + T + 128, D])
    S['qT'] = dscr('qT', [768, T], BF16)
    S['kT'] = dscr('kT', [768, T + LC], BF16)
    S['v'] = dscr('v', [T + LC, 768], BF16)
    S['gl'] = dscr('gl', [256, PADZ + T + PADZ])
    k.S = S

    with ExitStack() as es:
        P = Prog(nc, es)
        k.P = P

        uid = [0]

        def sb(name, shape, dt, st=es):
            uid[0] += 1
            return st.enter_context(nc.sbuf_tensor(f"{name}_s{uid[0]}", list(shape), dt))

        def ps(name, shape, dt, st=es):
            uid[0] += 1
            return st.enter_context(nc.psum_tensor(f"{name}_p{uid[0]}", list(shape), dt))
        k.sb = sb
        k.ps = ps

        ident = sb('ident', [128, 128], BF16)
        P.dma('pool', ident[:], I['ident'], [], ['ident'], 'ident')
        k.ident = ident
        cst = sb('cst', [128, 8], F32)
        P.op('dve', lambda e: e.memset(cst[:, 0:1], EPS), [], ['cst'])
        P.op('dve', lambda e: e.memset(cst[:, 1:2], 1.0), [], ['cst'])
        P.op('dve', lambda e: e.memset(cst[:, 2:3], 0.0), [], ['cst'])
        k.cst = cst
        zero = sb('zero', [128, 512], F32)
        P.op('dve', lambda e: e.memset(zero[:], 0.0), [], ['zero'])
        k.zero = zero
        ones_bf = sb('ones_bf', [128, 128], BF16)
        P.op('dve', lambda e: e.memset(ones_bf[:], 1.0), [], ['ones_bf'])
        k.ones_bf = ones_bf
        ones_f = sb('ones_f', [128, 128], F32)
        P.op('dve', lambda e: e.memset(ones_f[:], 1.0), [], ['ones_f'])
        k.ones_f = ones_f

        modfm = sb('modfm', [128, 2, 2, 4, 8], F32)
        k.modfm = modfm
        k.gbc_d = nc.dram_tensor('gbc_d', [2, 2, 2, 128, D], F32, kind="Internal").ap()

        if os.environ.get('DBG_ONLY_F1'):
            layer1(k, 'l1F1')
            return finish(k, es)
        phase_adaln(k)
        P.barrier()
        if dbg:
            mdbg = nc.dram_tensor('modfm_dbg', [128, 2, 2, 4, 8], F32, kind="ExternalOutput").ap()
            P.dma('sp', mdbg, modfm[:], [], [], 'mdbg')
            gdbg = nc.dram_tensor('gbc_dbg', [2, 2, 2, 128, D], F32, kind="ExternalOutput").ap()
            P.dma('sp', gdbg, k.gbc_d, [], [], 'gdbg')
        if stop_after == 'adaln':
            return finish(k, es)

        layer0(k, stop_after)
        if stop_after is not None and stop_after.startswith('l0'):
            return finish(k, es)
        layer1(k, stop_after)
        return finish(k, es)


def finish(k, es):
    k.P.barrier()
    k.ninstr = k.P.ninstr
    return k


def phase_adaln(k):
    nc, P, I = k.nc, k.P, k.I
    with ExitStack() as st:
        sb = lambda n, s, d: k.sb(n, s, d, st)
        ps = lambda n, s, d: k.ps(n, s, d, st)
        cf = sb('ad_cf', [128, 2, 8], F32)
        P.dma('sp', cf[:], I['c_fm'], [], ['ad_cf'], 'ad_cf')
        sc = sb('ad_sc', [128, 2, 8], F32)
        P.op('act', lambda e: e.activation(out=sc[:], in_=cf[:], func=AF.Silu), ['ad_cf'], ['ad_sc'])
        rep = sb('ad_rep', [128, 2, 8, 128], F32)
        for s_ in range(2):
            for kc in range(8):
                P.op('dve', lambda e, s_=s_, kc=kc: e.tensor_copy(out=rep[:, s_, kc, :], in_=sc[:, s_, kc:kc + 1].to_broadcast([128, 128])),
                     ['ad_sc'], [f'ad_rep.{s_}.{kc}'])
        modb = sb('ad_modb', [128, 2, 6, 8], F32)
        P.dma('sp', modb[:], I['modb_fm'], [], ['ad_modb'], 'ad_modb')
        ng = sb('ad_ng', [128, 2, 2, 8], F32)
        P.dma('sp', ng[:], I['ng_fm'], [], ['ad_ng'], 'ad_ng')
        brow = sb('ad_brow', [1, 2, 6 * D], F32)
        P.dma('sp', brow[:], I['mod_b'].rearrange("(o l) n -> o l n", o=1), [], ['ad_brow'], 'ad_brow')
        wt = [sb(f'ad_w{i}', [128, 6 * D], F32) for i in range(2)]
        gst = sb('ad_gst', [128, 2, D], F32)
        facc = sb('ad_facc', [128, 32, 2], F32)
        pfm = ps('ad_pfm', [128, 32, 2], F32)
        pbc = [ps(f'ad_pbc{i}', [128, 512], F32) for i in range(4)]
        for l in range(2):
            for pss in range(2):
                for kc in range(8):
                    w = wt[kc % 2]
                    wk = f'ad_w{kc % 2}'
                    P.dma('sp', w[:], I['mod_w'][l, kc * 128:(kc + 1) * 128, :], [], [wk], wk)
                    if pss == 0:
                        jmap = [0, 1, 3, 4]
                        for jj, j in enumerate(jmap):
                            for fc in range(8):
                                col = j * D + fc * 128
                                P.op('pe', lambda e, w=w, col=col, jj=jj, fc=fc, kc=kc: e.matmul(
                                    pfm[:, jj * 8 + fc, :], lhsT=w[:, col:col + 128], rhs=sc[:, :, kc],
                                    start=True, stop=True), [wk, 'ad_sc'], ['ad_pfm'])
                        if kc == 0:
                            P.op('dve', lambda e: e.tensor_copy(out=facc[:], in_=pfm[:]), ['ad_pfm'], ['ad_facc'])
                        else:
                            P.op('dve', lambda e: e.tensor_tensor(out=facc[:], in0=pfm[:], in1=facc[:], op=ALU.add), ['ad_pfm', 'ad_facc'], ['ad_facc'])
                    if True:
                        s_ = pss
                        for nt in range(4):
                            gj = 2 if nt < 2 else 5
                            col = gj * D + (nt % 2) * 512
                            P.op('pe', lambda e, w=w, col=col, nt=nt, kc=kc, s_=s_: e.matmul(
                                pbc[nt][:], lhsT=rep[:, s_, kc, :], rhs=w[:, col:col + 512],
                                start=(kc == 0), stop=False), [wk, f'ad_rep.{s_}.{kc}'], [f'ad_pbc{nt}'])
                if pss == 0:
                    jmap = [0, 1, 3, 4]
                    for s_ in range(2):
                        for jj, j in enumerate(jmap):
                            P.op('dve', lambda e, s_=s_, jj=jj, j=j, l=l: e.tensor_tensor(
                                out=k.modfm[:, l, s_, jj, :], in0=facc[:, jj * 8:(jj + 1) * 8, s_], in1=modb[:, l, j, :], op=ALU.add),
                                ['ad_facc', 'ad_modb'], [f'modfm.{l}.{s_}.{jj}'])
                        for jj, which in ((1, 0), (3, 1)):
                            P.op('dve', lambda e, s_=s_, jj=jj, which=which, l=l: e.scalar_tensor_tensor(
                                out=k.modfm[:, l, s_, jj, :], in0=k.modfm[:, l, s_, jj, :], scalar=1.0, in1=ng[:, l, which, :],
                                op0=ALU.add, op1=ALU.mult), [f'modfm.{l}.{s_}.{jj}', 'ad_ng'], [f'modfm.{l}.{s_}.{jj}'])
                if True:
                    s_ = pss
                    for nt in range(4):
                        gj = 2 if nt < 2 else 5
                        col = gj * D + (nt % 2) * 512
                        P.op('pe', lambda e, nt=nt, col=col, l=l: e.matmul(
                            pbc[nt][:], lhsT=k.ones_f[0:1, :], rhs=brow[0:1, l, col:col + 512], start=False, stop=True),
                            ['ones_f', 'ad_brow'], [f'ad_pbc{nt}'])
                        P.op('act', lambda e, nt=nt: e.copy(out=gst[:, nt // 2, (nt % 2) * 512:(nt % 2) * 512 + 512], in_=pbc[nt][:]),
                            [f'ad_pbc{nt}'], [f'ad_gst.{nt}'])
                    for j2 in range(2):
                        P.dma('sp', k.gbc_d[l, s_, j2], gst[:, j2, :], [f'ad_gst.{2 * j2}', f'ad_gst.{2 * j2 + 1}'], [], 'ad_gst')
        P.barrier()


def load_w_bf16(k, name, dst, src_ap, nk, ncols, colblk=2048):
    P = k.P
    for kc in range(nk):
        for c0 in range(0, ncols, colblk):
            c1 = min(ncols, c0 + colblk)
            P.dma('pool', dst[:, kc, c0:c1], src_ap[kc * 128:(kc + 1) * 128, c0:c1], [], [f'{name}.{kc}'], f'{name}.{kc}')


def fold_w_bf16(k, st, name, dst, src_ap, nk, gb_dram):
    P = k.P
    stg = [k.sb(f'{name}_stg{i}', [128, D], F32, st) for i in range(2)]
    gb = k.sb(f'{name}_gb', [128, D], F32, st)
    P.dma('sp', gb[:], gb_dram, [], [f'{name}_gb'], f'{name}_gb')
    gb_ap = gb[:]
    for kc in range(nk):
        s_ = stg[kc % 2]
        sk = f'{name}_stg{kc % 2}'
        P.dma('sp', s_[:], src_ap[kc * 128:(kc + 1) * 128, :], [], [sk], sk)
        P.op('dve', lambda e, s_=s_, kc=kc: e.tensor_tensor(out=dst[:, kc, :], in0=s_[:], in1=gb_ap, op=ALU.mult),
             [sk, f'{name}_gb'], [f'{name}.{kc}'])


def modulate_tile(k, B, src_rows, n, l, s_, jsh, hT, hTname):
    P = k.P
    ng = n // 128
    X, Xn = B['xt'], B['xtname']
    ss = B['ss']
    xn = B['xn']
    for g0 in range(0, ng, 2):
        gg = min(2, ng - g0)
        P.dma('sp', X[:, 0:gg, :], src_rows[g0 * 128:(g0 + gg) * 128, :].rearrange("(g p) f -> p g f", p=128), [], [Xn], Xn)
        P.op('dve', lambda e: e.memset(ss[:, 0:2], 0.0), [], [B['ssname']])
        for g in range(gg):
            P.op('act', lambda e, g=g: e.activation(out=B['junk'][:], in_=X[:, g, :], func=AF.Square, accum_out=ss[:, g:g + 1]),
                 [Xn], [B['ssname'], B['junkname']])
        P.op('act', lambda e, gg=gg: e.activation(out=ss[:, 4:4 + gg], in_=ss[:, 0:gg], func=AF.Sqrt, bias=k.cst[:, 0:1], scale=1.0 / D),
             [B['ssname'], 'cst'], [B['ssname']])
        P.op('dve', lambda e, gg=gg: e.reciprocal(out=ss[:, 8:8 + gg], in_=ss[:, 4:4 + gg]), [B['ssname']], [B['ssname']])
        for g in range(gg):
            eng = 'dve' if g % 2 == 0 else 'pool'
            P.op(eng, lambda e, g=g, g0=g0: e.tensor_scalar(out=xn[:, g0 + g, :], in0=X[:, g, :], scalar1=ss[:, 8 + g:9 + g], scalar2=None, op0=ALU.mult),
                 [Xn, B['ssname']], [f"{B['xnname']}.{g0 + g}"])
    for fc in range(8):
        tp = B['tp'][fc % 2]
        tpn = B['tpname'][fc % 2]
        for g in range(ng):
            P.op('pe', lambda e, g=g, fc=fc, tp=tp: e.transpose(out=tp[:, g * 128:(g + 1) * 128], in_=xn[:, g, fc * 128:(fc + 1) * 128], identity=k.ident[:]),
                 [f"{B['xnname']}.{g}", 'ident'], [tpn])
        P.op('act', lambda e, fc=fc, tp=tp: e.activation(out=hT[:, fc, 0:n], in_=tp[:, 0:n], func=AF.Identity,
                                                          scale=k.modfm[:, l, s_, jsh + 1, fc:fc + 1], bias=k.modfm[:, l, s_, jsh, fc:fc + 1]),
             [tpn], [f'{hTname}.{fc}'])


def mod_bufs(k, st, pfx):
    B = {}
    B['xt'] = k.sb(pfx + 'xt', [128, 2, D], F32, st)
    B['xtname'] = pfx + 'xt'
    B['xn'] = k.sb(pfx + 'xn', [128, 4, D], BF16, st)
    B['xnname'] = pfx + 'xn'
    B['junk'] = k.sb(pfx + 'junk', [128, D], BF16, st)
    B['junkname'] = pfx + 'junk'
    B['ss'] = k.sb(pfx + 'ss', [128, 12], F32, st)
    B['ssname'] = pfx + 'ss'
    B['tp'] = [k.ps(pfx + f'tp{i}', [128, 512], BF16, st) for i in range(2)]
    B['tpname'] = [pfx + f'tp{i}' for i in range(2)]
    return B


def layer0(k, stop_after):
    nc, P, I, S = k.nc, k.P, k.I, k.S
    with ExitStack() as st:
        sb = lambda n, s, d: k.sb(n, s, d, st)
        lp = {}
        for nm, shp in (('lru_cw', [128, 6, 5]), ('pool_fl', [128, 2]), ('lru_cb', [128, 6]), ('lru_ba', [128, 2, 6]), ('lru_bx', [128, 2, 6]),
                        ('lru_lam', [128, 2, 6]), ('pool_scale', [128, 2]), ('pool_invw', [128, 2]), ('pool_corr', [128, 2, 2, 16])):
            lp[nm] = sb('p_' + nm, shp, F32)
            P.dma('sp', lp[nm][:], I[nm], [], ['p_' + nm], 'p_' + nm)
        for nm, shp in (('lru_bdA', [128, 2, 6, 128]), ('lru_bdX', [128, 2, 6, 128]), ('pool_bd', [128, 2, 128])):
            lp[nm] = sb('p_' + nm, shp, BF16)
            P.dma('pool', lp[nm][:], I[nm], [], ['p_' + nm], 'p_' + nm)
        cl = sb('p_cl', [128, 2, 2, 6], F32)
        tmp = sb('p_cltmp', [128, 2, 6], F32)
        P.op('act', lambda e: e.activation(out=tmp[:], in_=lp['lru_lam'][:], func=AF.Exp, scale=-1.0), ['p_lru_lam'], ['p_cltmp'])
        P.op('act', lambda e: e.activation(out=tmp[:], in_=tmp[:], func=AF.Ln, bias=k.cst[:, 1:2], scale=1.0), ['p_cltmp', 'cst'], ['p_cltmp'])
        P.op('dve', lambda e: e.tensor_scalar(out=cl[:, 0, :, :], in0=tmp[:], scalar1=-8.0, scalar2=None, op0=ALU.mult), ['p_cltmp'], ['p_cl'])
        P.op('dve', lambda e: e.tensor_scalar(out=cl[:, 1, :, :], in0=tmp[:], scalar1=-16.0, scalar2=None, op0=ALU.mult), ['p_cltmp'], ['p_cl'])
        lp['cl'] = cl
        stt = sb('p_state', [128, 2, 6], F32)
        P.op('dve', lambda e: e.memset(stt[:], 0.0), [], ['p_state'])
        lp['state'] = stt
        k.lp = lp
        w_in = sb('w_in0', [128, 8, 1792], BF16)
        load_w_bf16(k, 'w_in0', w_in, I['ab_w_in'], 8, 1792, colblk=1792)
        w_out = sb('w_out0', [128, 8, D], BF16)
        k.w_in0, k.w_out0 = w_in, w_out

        for s_, nm, TT, xsrc in ((1, 'c', LC, I['ctx']), (0, 'l', k.T, I['x'])):
            seg = K()
            seg.nm, seg.T, seg.x, seg.set = nm, TT, xsrc, s_
            seg.z, seg.xa, seg.hf, seg.x05, seg.x1 = S['z_' + nm], S['xa_' + nm], S['hf_' + nm], S['x05_' + nm], S['x1_' + nm]
            seg.tiles = tiles_of(TT)
            with ExitStack() as st2:
                fold_w_bf16(k, st2, 'w_out0', w_out, I['ab_w_out'], 8, k.gbc_d[0, s_, 0])
            P.barrier()
            l0_phaseA(k, seg)
            P.barrier()
            if stop_after == 'l0A' and nm == 'l':
                return
            l0_phaseB(k, seg)
            P.barrier()
            if stop_after == 'l0B' and nm == 'l':
                return
            l0_phaseC(k, seg)
            P.barrier()
            if stop_after == 'l0C' and nm == 'l':
                return
    for s_, nm, TT in ((1, 'c', LC), (0, 'l', k.T)):
        ffn_phase(k, 0, s_, S['x05_' + nm], S['x1_' + nm], TT, final=False)
        P.barrier()


def l0_phaseA(k, seg):
    nc, P = k.nc, k.P
    with ExitStack() as st:
        sb = lambda n, s, d: k.sb(n, s, d, st)
        ps = lambda n, s, d: k.ps(n, s, d, st)
        B = mod_bufs(k, st, 'A_')
        hT = sb('A_hT', [128, 8, 512], BF16)
        zt = [sb(f'A_zt{i}', [128, 14, 512], F32) for i in range(2)]
        zp = [ps(f'A_zp{i}', [128, 512], F32) for i in range(4)]
        zv = seg.z.rearrange("(c p) t -> p c t", p=128)
        P.dma('sp', zv[:, :, 0:PADZ], k.zero[:, 0:14 * PADZ].rearrange("p (c t) -> p c t", c=14), ['zero'], [], 'A_zpad')
        P.dma('sp', zv[:, :, PADZ + seg.T:PADZ + seg.T + PADZ], k.zero[:, 0:14 * PADZ].rearrange("p (c t) -> p c t", c=14), ['zero'], [], 'A_zpad')
        for ti, (s, n) in enumerate(seg.tiles):
            modulate_tile(k, B, seg.x[s:s + n, :], n, 0, seg.set, 0, hT, 'A_hT')
            Z = zt[ti % 2]
            Zn = f'A_zt{ti % 2}'
            for mc in range(14):
                zpp = zp[mc % 4]
                for kc in range(8):
                    P.op('pe', lambda e, mc=mc, kc=kc, zpp=zpp: e.matmul(zpp[:, 0:n], lhsT=k.w_in0[:, kc, mc * 128:(mc + 1) * 128], rhs=hT[:, kc, 0:n],
                                                                         start=(kc == 0), stop=(kc == 7)),
                         [f'w_in0.{kc}', f'A_hT.{kc}'], [f'A_zp{mc % 4}'])
                eng = 'act' if mc % 2 == 0 else 'dve'
                if eng == 'act':
                    P.op('act', lambda e, mc=mc, zpp=zpp: e.copy(out=Z[:, mc, 0:n], in_=zpp[:, 0:n]), [f'A_zp{mc % 4}'], [f'{Zn}.{mc}'])
                else:
                    P.op('dve', lambda e, mc=mc, zpp=zpp: e.tensor_copy(out=Z[:, mc, 0:n], in_=zpp[:, 0:n]), [f'A_zp{mc % 4}'], [f'{Zn}.{mc}'])
            P.dma('sp', zv[:, :, PADZ + s:PADZ + s + n], Z[:, :, 0:n], keys(Zn, 14), [], Zn)


def lru_coeffs(k, C, d, n, xa, xab):
    P, lp = k.P, k.lp
    for c in range(6):
        pr, pi = C['pg'][(2 * c) % 4], C['pg'][(2 * c + 1) % 4]
        prn, pin = C['pgname'][(2 * c) % 4], C['pgname'][(2 * c + 1) % 4]
        P.op('pe', lambda e, c=c, pr=pr: e.matmul(pr[:, 0:n], lhsT=lp['lru_bdA'][:, d, c, :], rhs=xab[:, c, 0:n], start=True, stop=True),
             ['p_lru_bdA', f"{C['xabname']}.{c}"], [prn])
        P.op('pe', lambda e, c=c, pi=pi: e.matmul(pi[:, 0:n], lhsT=lp['lru_bdX'][:, d, c, :], rhs=xab[:, c, 0:n], start=True, stop=True),
             ['p_lru_bdX', f"{C['xabname']}.{c}"], [pin])
        P.op('act', lambda e, c=c, pr=pr: e.activation(out=C['r'][:, c, 0:n], in_=pr[:, 0:n], func=AF.Sigmoid, bias=lp['lru_ba'][:, d, c:c + 1], scale=1.0),
             [prn, 'p_lru_ba'], [f"{C['pfx']}r.{c}"])
        P.op('act', lambda e, c=c, pi=pi: e.activation(out=C['ig'][:, c, 0:n], in_=pi[:, 0:n], func=AF.Sigmoid, bias=lp['lru_bx'][:, d, c:c + 1], scale=1.0),
             [pin, 'p_lru_bx'], [f"{C['pfx']}ig.{c}"])
    for c in range(6):
        P.op('act', lambda e, c=c: e.activation(out=C['a'][:, c, 0:n], in_=C['r'][:, c, 0:n], func=AF.Exp, scale=lp['cl'][:, 0, d, c:c + 1]),
             [f"{C['pfx']}r.{c}", 'p_cl'], [f"{C['pfx']}a.{c}"])
        P.op('act', lambda e, c=c: e.activation(out=C['r'][:, c, 0:n], in_=C['r'][:, c, 0:n], func=AF.Exp, scale=lp['cl'][:, 1, d, c:c + 1]),
             [f"{C['pfx']}r.{c}", 'p_cl'], [f"{C['pfx']}r.{c}"])
    for c in range(6):
        P.op('act', lambda e, c=c: e.activation(out=C['r'][:, c, 0:n], in_=C['r'][:, c, 0:n], func=AF.Sqrt, bias=k.cst[:, 1:2], scale=-1.0),
             [f"{C['pfx']}r.{c}", 'cst'], [f"{C['pfx']}r.{c}"])
        P.op('dve', lambda e, c=c: e.tensor_tensor(out=C['ig'][:, c, 0:n], in0=C['ig'][:, c, 0:n], in1=C['r'][:, c, 0:n], op=ALU.mult),
             [f"{C['pfx']}r.{c}", f"{C['pfx']}ig.{c}"], [f"{C['pfx']}ig.{c}"])
        P.op('pool', lambda e, c=c: e.tensor_tensor(out=C['ig'][:, c, 0:n], in0=C['ig'][:, c, 0:n], in1=xa[:, c, 0:n], op=ALU.mult),
             [f"{C['pfx']}ig.{c}", f"{C['xaname']}.{c}"], [f"{C['pfx']}ig.{c}"])


def coeff_bufs(k, st, pfx):
    C = {'pfx': pfx}
    for nm in ('r', 'ig', 'a'):
        C[nm] = k.sb(pfx + nm, [128, 6, 512], F32, st)
    C['pg'] = [k.ps(pfx + f'pg{i}', [128, 512], F32, st) for i in range(4)]
    C['pgname'] = [pfx + f'pg{i}' for i in range(4)]
    return C


def l0_phaseB(k, seg):
    P, lp = k.P, k.lp
    with ExitStack() as st:
        sb = lambda n, s, d: k.sb(n, s, d, st)
        zin = sb('B_zin', [128, 6, 516], F32)
        xa = sb('B_xa', [128, 6, 512], F32)
        xab = sb('B_xab', [128, 6, 512], BF16)
        hf = sb('B_hf', [128, 6, 512], F32)
        C = coeff_bufs(k, st, 'B_')
        C['xabname'], C['xaname'] = 'B_xab', 'B_xa'
        zv = seg.z[0:768, :].rearrange("(c p) t -> p c t", p=128)
        xav = seg.xa.rearrange("(c p) t -> p c t", p=128)
        hfv = seg.hf.rearrange("(c p) t -> p c t", p=128)
        if seg.nm == 'c':
            P.op('dve', lambda e: e.memset(lp['state'][:], 0.0), [], ['p_state'])
        for ti, (s, n) in enumerate(seg.tiles):
            P.dma('sp', zin[:, :, 0:n + 4], zv[:, :, PADZ + s - 2:PADZ + s + n + 2], [], keys('B_zin', 6), 'B_zin')
            for c in range(6):
                P.op('act', lambda e, c=c: e.activation(out=xa[:, c, 0:n], in_=zin[:, c, 0:n], func=AF.Identity,
                                                        scale=lp['lru_cw'][:, c, 0:1], bias=lp['lru_cb'][:, c:c + 1]),
                     [f'B_zin.{c}', 'p_lru_cw', 'p_lru_cb'], [f'B_xa.{c}'])
                for t in range(1, 5):
                    P.op('dve', lambda e, c=c, t=t: e.scalar_tensor_tensor(out=xa[:, c, 0:n], in0=zin[:, c, t:t + n], scalar=lp['lru_cw'][:, c, t:t + 1],
                                                                           in1=xa[:, c, 0:n], op0=ALU.mult, op1=ALU.add),
                         [f'B_zin.{c}', f'B_xa.{c}'], [f'B_xa.{c}'])
                P.op('pool', lambda e, c=c: e.tensor_copy(out=xab[:, c, 0:n], in_=xa[:, c, 0:n]), [f'B_xa.{c}'], [f'B_xab.{c}'])
            P.dma('sp', xav[:, :, s:s + n], xa[:, :, 0:n], keys('B_xa', 6), [], 'B_xa')
            lru_coeffs(k, C, 0, n, xa, xab)
            for c in range(6):
                P.op('dve', lambda e, c=c: e.tensor_tensor_scan(out=hf[:, c, 0:n], data0=C['a'][:, c, 0:n], data1=C['ig'][:, c, 0:n],
                                                                initial=lp['state'][:, 0, c:c + 1], op0=ALU.mult, op1=ALU.add),
                     [f'B_a.{c}', f'B_ig.{c}', 'p_state'], [f'B_hf.{c}'])
                P.op('dve', lambda e, c=c: e.tensor_copy(out=lp['state'][:, 0, c:c + 1], in_=hf[:, c, n - 1:n]), [f'B_hf.{c}'], ['p_state'])
            P.dma('sp', hfv[:, :, s:s + n], hf[:, :, 0:n], keys('B_hf', 6), [], 'B_hf')


def l0_phaseC(k, seg):
    P, lp, I = k.P, k.lp, k.I
    with ExitStack() as st:
        sb = lambda n, s, d: k.sb(n, s, d, st)
        ps = lambda n, s, d: k.ps(n, s, d, st)
        xa = sb('C_xa', [128, 6, 512], F32)
        xab = sb('C_xab', [128, 6, 512], BF16)
        hb = sb('C_hb', [128, 6, 512], F32)
        hf = sb('C_hf', [128, 6, 512], F32)
        ga = sb('C_ga', [128, 6, 512], F32)
        yT = sb('C_yT', [128, 8, 512], BF16)
        zb = sb('C_zb', [128, 2, 528], F32)
        p2 = sb('C_p2', [128, 528], F32)
        p4 = sb('C_p4', [128, 528], F32)
        p8 = sb('C_p8', [128, 528], F32)
        Qw = sb('C_Qw', [128, 516], F32)
        Ssum = sb('C_S', [128, 512], F32)
        dd = sb('C_dd', [128, 2, 512], BF16)
        xt = sb('C_xt', [128, 4, D], F32)
        xo = sb('C_xo', [128, 4, D], F32)
        C = coeff_bufs(k, st, 'C_')
        C['xabname'], C['xaname'] = 'C_xab', 'C_xa'
        po = [ps(f'C_po{i}', [128, 512], F32) for i in range(4)]
        zg = seg.z[768:1536, :].rearrange("(c p) t -> p c t", p=128)
        zbv = seg.z[1536:1792, :].rearrange("(c p) t -> p c t", p=128)
        xav = seg.xa.rearrange("(c p) t -> p c t", p=128)
        hfv = seg.hf.rearrange("(c p) t -> p c t", p=128)
        if seg.nm == 'c':
            P.op('dve', lambda e: e.memset(lp['state'][:, 1, :], 0.0), [], ['p_state'])
        nt_ = len(seg.tiles)
        for ti in range(nt_ - 1, -1, -1):
            s, n = seg.tiles[ti]
            ng = n // 128
            P.dma('sp', xa[:, :, 0:n], xav[:, :, s:s + n], [], keys('C_xa', 6), 'C_xa')
            P.dma('sp', hf[:, :, 0:n], hfv[:, :, s:s + n], [], keys('C_hf', 6), 'C_hf')
            P.dma('sp', ga[:, :, 0:n], zg[:, :, PADZ + s:PADZ + s + n], [], keys('C_ga', 6), 'C_ga')
            P.dma('sp', zb[:, :, 0:n + 16], zbv[:, :, PADZ + s - 8:PADZ + s + n + 8], [], keys('C_zb', 2), 'C_zb')
            P.dma('sp', xt[:, 0:ng, :], seg.x[s:s + n, :].rearrange("(g p) f -> p g f", p=128), [], ['C_xt'], 'C_xt')
            for c in range(6):
                P.op('pool', lambda e, c=c: e.tensor_copy(out=xab[:, c, 0:n], in_=xa[:, c, 0:n]), [f'C_xa.{c}'], [f'C_xab.{c}'])
            lru_coeffs(k, C, 1, n, xa, xab)
            for c in range(6):
                P.op('dve', lambda e, c=c: e.tensor_tensor_scan(out=rev(hb[:, c, 0:n]), data0=rev(C['a'][:, c, 0:n]), data1=rev(C['ig'][:, c, 0:n]),
                                                                initial=lp['state'][:, 1, c:c + 1], op0=ALU.mult, op1=ALU.add),
                     [f'C_a.{c}', f'C_ig.{c}', 'p_state'], [f'C_hb.{c}'])
                P.op('dve', lambda e, c=c: e.tensor_copy(out=lp['state'][:, 1, c:c + 1], in_=hb[:, c, 0:1]), [f'C_hb.{c}'], ['p_state'])
                P.op('act', lambda e, c=c: e.activation(out=ga[:, c, 0:n], in_=ga[:, c, 0:n], func=AF.Gelu_apprx_tanh), [f'C_ga.{c}'], [f'C_ga.{c}'])
                P.op('pool', lambda e, c=c: e.tensor_tensor(out=hb[:, c, 0:n], in0=hb[:, c, 0:n], in1=hf[:, c, 0:n], op=ALU.add),
                     [f'C_hb.{c}', f'C_hf.{c}'], [f'C_hb.{c}'])
                P.op('dve', lambda e, c=c: e.tensor_tensor(out=yT[:, c, 0:n], in0=hb[:, c, 0:n], in1=ga[:, c, 0:n], op=ALU.mult),
                     [f'C_hb.{c}', f'C_ga.{c}'], [f'C_yT.{c}'])
            W = n + 16
            n1 = n + 1
            for ch in range(2):
                zc = zb[:, ch, :]
                P.op('dve', lambda e, zc=zc: e.tensor_tensor(out=p2[:, 0:W - 1], in0=zc[:, 0:W - 1], in1=zc[:, 1:W], op=ALU.add),
                     [f'C_zb.{ch}'], ['C_p2'])
                if ch == 0:
                    P.op('dve', lambda e: e.tensor_copy(out=Qw[0:64, 0:n1], in_=p2[0:64, 7:7 + n1]), ['C_p2'], ['C_Qw'])
                    P.op('dve', lambda e: e.tensor_tensor(out=Qw[64:128, 0:n1], in0=p2[64:128, 6:6 + n1], in1=p2[64:128, 8:8 + n1], op=ALU.add),
                         ['C_p2'], ['C_Qw'])
                else:
                    P.op('dve', lambda e: e.tensor_tensor(out=p4[:, 0:W - 3], in0=p2[:, 0:W - 3], in1=p2[:, 2:W - 1], op=ALU.add), ['C_p2'], ['C_p4'])
                    P.op('dve', lambda e: e.tensor_tensor(out=Qw[0:64, 0:n1], in0=p4[0:64, 4:4 + n1], in1=p4[0:64, 8:8 + n1], op=ALU.add),
                         ['C_p4'], ['C_Qw'])
                    P.op('dve', lambda e: e.tensor_tensor(out=p8[64:128, 0:W - 7], in0=p4[64:128, 0:W - 7], in1=p4[64:128, 4:W - 3], op=ALU.add),
                         ['C_p4'], ['C_p8'])
                    P.op('dve', lambda e: e.tensor_tensor(out=Qw[64:128, 0:n1], in0=p8[64:128, 0:n1], in1=p8[64:128, 8:8 + n1], op=ALU.add),
                         ['C_p8'], ['C_Qw'])
                P.op('dve', lambda e: e.tensor_scalar(out=Ssum[:, 0:n], in0=Qw[:, 0:n], scalar1=lp['pool_fl'][:, 0:1], scalar2=None, op0=ALU.mult),
                     ['C_Qw', 'p_pool_fl'], ['C_S'])
                P.op('dve', lambda e: e.scalar_tensor_tensor(out=Ssum[:, 0:n], in0=Qw[:, 1:n1], scalar=lp['pool_fl'][:, 1:2], in1=Ssum[:, 0:n],
                                                             op0=ALU.mult, op1=ALU.add), ['C_Qw', 'C_S', 'p_pool_fl'], ['C_S'])
                if ti == 0:
                    P.op('dve', lambda e, ch=ch: e.tensor_tensor(out=Ssum[:, 0:16], in0=Ssum[:, 0:16], in1=lp['pool_corr'][:, ch, 0, :], op=ALU.mult),
                         ['C_S', 'p_pool_corr'], ['C_S'])
                if ti == nt_ - 1:
                    P.op('dve', lambda e, ch=ch: e.tensor_tensor(out=Ssum[:, n - 16:n], in0=Ssum[:, n - 16:n], in1=lp['pool_corr'][:, ch, 1, :], op=ALU.mult),
                         ['C_S', 'p_pool_corr'], ['C_S'])
                P.op('dve', lambda e, ch=ch, zc=zc: e.scalar_tensor_tensor(out=dd[:, ch, 0:n], in0=Ssum[:, 0:n], scalar=lp['pool_invw'][:, ch:ch + 1],
                                                                          in1=zc[:, 8:8 + n], op0=ALU.mult, op1=ALU.subtract),
                     ['C_S', f'C_zb.{ch}', 'p_pool_invw'], [f'C_dd.{ch}'])
                pp = po[ch]
                P.op('pe', lambda e, ch=ch, pp=pp: e.matmul(pp[:, 0:n], lhsT=lp['pool_bd'][:, ch, :], rhs=dd[:, ch, 0:n], start=True, stop=True),
                     ['p_pool_bd', f'C_dd.{ch}'], [f'C_po{ch}'])
                P.op('act', lambda e, ch=ch, pp=pp: e.activation(out=yT[:, 6 + ch, 0:n], in_=pp[:, 0:n], func=AF.Identity, scale=lp['pool_scale'][:, ch:ch + 1], bias=k.cst[:, 2:3]),
                     [f'C_po{ch}', 'p_pool_scale', 'cst'], [f'C_yT.{6 + ch}'])
            for g in range(ng):
                for nt in range(2):
                    pp = po[(2 * g + nt) % 4]
                    ppn = f'C_po{(2 * g + nt) % 4}'
                    for kc in range(8):
                        P.op('pe', lambda e, g=g, nt=nt, kc=kc, pp=pp: e.matmul(pp[:, :], lhsT=yT[:, kc, g * 128:(g + 1) * 128], rhs=k.w_out0[:, kc, nt * 512:(nt + 1) * 512],
                                                                               start=(kc == 0), stop=(kc == 7)),
                             [f'C_yT.{kc}', f'w_out0.{kc}'], [ppn])
                    P.op('dve', lambda e, g=g, nt=nt, pp=pp: e.tensor_tensor(out=xo[:, g, nt * 512:(nt + 1) * 512], in0=pp[:, :], in1=xt[:, g, nt * 512:(nt + 1) * 512], op=ALU.add),
                         [ppn, 'C_xt'], [f'C_xo.{g}'])
            P.dma('sp', seg.x05[1 + s:1 + s + n, :].rearrange("(g p) f -> p g f", p=128), xo[:, 0:ng, :], keys('C_xo', 4)[0:ng], [], 'C_xo')


def ffn_phase(k, l, s_, src, dst, TT, final, flush=True, out_rows=None):
    nc, P, I = k.nc, k.P, k.I
    tiles = tiles_of(TT)
    with ExitStack() as st:
        sb = lambda n, s, d: k.sb(n, s, d, st)
        ps = lambda n, s, d: k.ps(n, s, d, st)
        w_up = sb('F_wup', [128, 8, 2 * DFF], BF16)
        load_w_bf16(k, 'F_wup', w_up, I['ffn_w_up'][l], 8, 2 * DFF)
        w_dn = sb('F_wdn', [128, NFC, D], BF16)
        with ExitStack() as st2:
            fold_w_bf16(k, st2, 'F_wdn', w_dn, I['ffn_w_down'][l], NFC, k.gbc_d[l, s_, 1])
            P.barrier()
        cw = sb('F_cw', [128, 2 * NFC, 3], F32)
        cb = sb('F_cb', [128, 2 * NFC], F32)
        P.dma('sp', cw[:], I['ffn_cw'][:, l, :, :], [], ['F_cw'], 'F_cw')
        P.dma('sp', cb[:], I['ffn_cb'][:, l, :], [], ['F_cb'], 'F_cb')
        B = mod_bufs(k, st, 'F_')
        hT = sb('F_hT', [128, 8, 512], BF16)
        gT = sb('F_gT', [128, NFC, 512], BF16)
        prevu = [sb(f'F_prevu{i}', [128, 2 * NFC, 2], F32) for i in range(2)]
        P.op('dve', lambda e: e.memset(prevu[0][:], 0.0), [], keys('F_prevu0', 2 * NFC))
        acc = [sb(f'F_acc{i}', [128, 512], F32) for i in range(4)]
        corr = sb('F_corr', [128, 2 * NFC, 2], F32)
        ctmp = sb('F_ctmp', [128, 2 * NFC], F32)
        xs = sb('F_xs', [128, 2, D], F32)
        xo = xs
        pu = [ps(f'F_pu{i}', [128, 512], F32) for i in range(4)]
        pd = [ps(f'F_pd{i}', [128, 512], F32) for i in range(2)]
        if final:
            fg = sb('F_fg', [128, D], F32)
            P.dma('sp', fg[:], I['final_g_bc'], [], ['F_fg'], 'F_fg')
            fss = sb('F_fss', [128, 12], F32)
            fjunk = B['junk']

        if os.environ.get('DBG_SBUF'):
            print('FFN sbuf remaining', nc.sbuf_bytes_remaining, 'final', final)
        def conv_gate(n, zero_u, ti):
            pin, pout = prevu[ti % 2], prevu[(ti + 1) % 2]
            pinn, poutn = f'F_prevu{ti % 2}', f'F_prevu{(ti + 1) % 2}'
            allin = keys(pinn, 2 * NFC)
            P.op('dve', lambda e: e.tensor_tensor(out=corr[:, :, 0], in0=cw[:, :, 0], in1=pin[:, :, 0], op=ALU.mult), allin + ['F_cw'], ['F_corr'])
            P.op('dve', lambda e: e.tensor_tensor(out=ctmp[:, :], in0=cw[:, :, 1], in1=pin[:, :, 1], op=ALU.mult), allin + ['F_cw'], ['F_ctmp'])
            P.op('dve', lambda e: e.tensor_tensor(out=corr[:, :, 0], in0=corr[:, :, 0], in1=ctmp[:, :], op=ALU.add), ['F_corr', 'F_ctmp'], ['F_corr'])
            P.op('dve', lambda e: e.tensor_tensor(out=corr[:, :, 1], in0=cw[:, :, 0], in1=pin[:, :, 1], op=ALU.mult), allin + ['F_cw', 'F_corr'], ['F_corr'])
            for c in range(NFC):
                q = c % 2
                AA = [acc[2 * q], acc[2 * q + 1]]
                AN = [f'F_acc{2 * q}', f'F_acc{2 * q + 1}']
                PP = [pu[2 * q], pu[2 * q + 1]]
                PN = [f'F_pu{2 * q}', f'F_pu{2 * q + 1}']
                CC = [c, NFC + c]
                if not zero_u:
                    for vi in range(2):
                        for kc in range(8):
                            P.op('pe', lambda e, kc=kc, cc=CC[vi], pp=PP[vi]: e.matmul(pp[:, 0:n], lhsT=w_up[:, kc, cc * 128:(cc + 1) * 128], rhs=hT[:, kc, 0:n],
                                                                                      start=(kc == 0), stop=(kc == 7)),
                                 [f'F_wup.{kc}', f'F_hT.{kc}'], [PN[vi]])
                    for vi in range(2):
                        P.op('act', lambda e, A_=AA[vi], pp=PP[vi], cc=CC[vi]: e.activation(out=A_[:, 0:n], in_=pp[:, 0:n], func=AF.Identity,
                                                                                          scale=cw[:, cc, 2:3], bias=cb[:, cc:cc + 1]),
                             [PN[vi], 'F_cw', 'F_cb'], [AN[vi]])
                    for vi in range(2):
                        if os.environ.get('DBG_SKIP_SAVE'):
                            continue
                        P.op('dve', lambda e, pp=PP[vi], cc=CC[vi]: e.tensor_copy(out=pout[:, cc, :], in_=pp[:, n - 2:n]), [PN[vi]], [f'{poutn}.{CC[vi]}'])
                    for vi in range(2):
                        P.op('dve', lambda e, A_=AA[vi], pp=PP[vi], cc=CC[vi]: e.scalar_tensor_tensor(out=A_[:, 1:n], in0=pp[:, 0:n - 1], scalar=cw[:, cc, 1:2], in1=A_[:, 1:n],
                                                                                                    op0=ALU.mult, op1=ALU.add), [PN[vi], AN[vi]], [AN[vi]])
                    for vi in range(2):
                        P.op('dve', lambda e, A_=AA[vi], pp=PP[vi], cc=CC[vi]: e.scalar_tensor_tensor(out=A_[:, 2:n], in0=pp[:, 0:n - 2], scalar=cw[:, cc, 0:1], in1=A_[:, 2:n],
                                                                                                    op0=ALU.mult, op1=ALU.add), [PN[vi], AN[vi]], [AN[vi]])
                else:
                    for vi in range(2):
                        P.op('act', lambda e, A_=AA[vi], cc=CC[vi]: e.activation(out=A_[:, 0:n], in_=k.zero[:, 0:n], func=AF.Identity,
                                                                                scale=cw[:, cc, 2:3], bias=cb[:, cc:cc + 1]),
                             ['zero', 'F_cw', 'F_cb'], [AN[vi]])
                for vi in range(2):
                    P.op('pool', lambda e, A_=AA[vi], cc=CC[vi]: e.tensor_tensor(out=A_[:, 0:2], in0=A_[:, 0:2], in1=corr[:, cc, :], op=ALU.add),
                         ['F_corr', AN[vi]], [AN[vi]])
                P.op('act', lambda e, A_=AA[1]: e.activation(out=A_[:, 0:n], in_=A_[:, 0:n], func=AF.Silu), [AN[1]], [AN[1]])
                P.op('pool', lambda e, c=c, A0=AA[0], A1=AA[1]: e.tensor_tensor(out=gT[:, c, 0:n], in0=A0[:, 0:n], in1=A1[:, 0:n], op=ALU.mult),
                     [AN[0], AN[1]], [f'F_gT.{c}'])

        def down_res(tok0, n, nrows_last=128):
            ng = n // 128
            nr = lambda g: (nrows_last if g == ng - 1 else 128)
            for g_ in range(ng):
                g = g_ % 2
                r0 = 1 + tok0 + g_ * 128
                P.dma('sp', xs[0:nr(g_), g, :], src[r0:r0 + nr(g_), :], [], [f'F_xs.{g}'], f'F_xs{g}')
                for nt in range(2):
                    pp = pd[nt]
                    for kc in range(NFC):
                        P.op('pe', lambda e, g_=g_, nt=nt, kc=kc, pp=pp: e.matmul(pp[:, :], lhsT=gT[:, kc, g_ * 128:(g_ + 1) * 128], rhs=w_dn[:, kc, nt * 512:(nt + 1) * 512],
                                                                               start=(kc == 0), stop=(kc == NFC - 1)),
                             [f'F_gT.{kc}', f'F_wdn.{kc}'], [f'F_pd{nt}'])
                    P.op('dve', lambda e, g=g, g_=g_, nt=nt, pp=pp: e.tensor_tensor(out=xo[0:nr(g_), g, nt * 512:(nt + 1) * 512], in0=pp[0:nr(g_), :],
                                                                            in1=xs[0:nr(g_), g, nt * 512:(nt + 1) * 512], op=ALU.add),
                         [f'F_pd{nt}', f'F_xs.{g}'], [f'F_xs.{g}'])
                if final:
                    P.op('dve', lambda e: e.memset(fss[:, 0:1], 0.0), [], ['F_fss'])
                    P.op('act', lambda e, g=g: e.activation(out=fjunk[:], in_=xo[:, g, :], func=AF.Square, accum_out=fss[:, 0:1]),
                         [f'F_xs.{g}'], ['F_fss', 'F_junk'])
                    P.op('act', lambda e: e.activation(out=fss[:, 1:2], in_=fss[:, 0:1], func=AF.Sqrt, bias=k.cst[:, 0:1], scale=1.0 / D),
                         ['F_fss', 'cst'], ['F_fss'])
                    P.op('dve', lambda e: e.reciprocal(out=fss[:, 2:3], in_=fss[:, 1:2]), ['F_fss'], ['F_fss'])
                    P.op('dve', lambda e, g=g: e.scalar_tensor_tensor(out=xo[:, g, :], in0=xo[:, g, :], scalar=fss[:, 2:3], in1=fg[:],
                                                                     op0=ALU.mult, op1=ALU.mult), [f'F_xs.{g}', 'F_fss', 'F_fg'], [f'F_xs.{g}'])
                t0 = tok0 + g_ * 128
                if final:
                    lo = max(t0, 0)
                    hi = min(t0 + nr(g_), TT if out_rows is None else out_rows)
                    if hi > lo:
                        P.dma('sp', k.out[lo:hi, :], xo[lo - t0:hi - t0, g, :], [f'F_xs.{g}'], [], f'F_xs{g}')
                else:
                    P.dma('sp', dst[1 + t0:1 + t0 + nr(g_), :], xo[0:nr(g_), g, :], [f'F_xs.{g}'], [], f'F_xs{g}')

        for ti, (s, n) in enumerate(tiles):
            modulate_tile(k, B, src[1 + s:1 + s + n, :], n, l, s_, 2, hT, 'F_hT')
            conv_gate(n, False, ti)
            down_res(s - 1, n)
        if flush:
            conv_gate(128, True, len(tiles))
            down_res(TT - 1, 128, nrows_last=1)


def layer1(k, stop_after):
    nc, P, I, S = k.nc, k.P, k.I, k.S
    T = k.T
    TK = T + LC
    NKB = TK // 128
    TL = k.TL
    yc_d = nc.dram_tensor('yc_d', [768, T], BF16, kind="Internal").ap()
    with ExitStack() as st:
      if not os.environ.get('DBG_ONLY_F1'):
          sb = lambda n, s, d: k.sb(n, s, d, st)
          ps = lambda n, s, d: k.ps(n, s, d, st)
          w_in = sb('w_in1', [128, 8, 2816], BF16)
          load_w_bf16(k, 'w_in1', w_in, I['cd_w_in'], 8, 2816, colblk=1408)
          w_sw = sb('w_sw1', [128, 8, 1536], BF16)
          load_w_bf16(k, 'w_sw1', w_sw, I['cd_w_sw'], 8, 1536, colblk=1536)
          B = mod_bufs(k, st, 'E_')
          hT = sb('E_hT', [128, 8, 512], BF16)
          qk = sb('E_qk', [128, 12, 512], BF16)
          vt = sb('E_vt', [128, 4, 768], BF16)
          gl = sb('E_gl', [128, 2, 512], F32)
          rc = sb('E_rc', [128, 512], F32)
          rs = sb('E_rs', [128, 512], F32)
          t1 = sb('E_t1', [128, 512], F32)
          t2 = sb('E_t2', [128, 512], F32)
          sg = sb('E_sg', [128, 512], F32)
          pa = [ps(f'E_pa{i}', [128, 512], F32) for i in range(2)]
          pb = [ps(f'E_pb{i}', [128, 512], F32) for i in range(2)]
          glv = S['gl'].rearrange("(c p) t -> p c t", p=128)
          P.dma('sp', glv[:, :, 0:PADZ], k.zero[:, 0:2 * PADZ].rearrange("p (c t) -> p c t", c=2), ['zero'], [], 'E_glpad')
          P.dma('sp', glv[:, :, PADZ + T:PADZ + T + PADZ], k.zero[:, 0:2 * PADZ].rearrange("p (c t) -> p c t", c=2), ['zero'], [], 'E_glpad')
          qTv = S['qT'].rearrange("(c p) t -> p c t", p=128)
          kTv = S['kT'].rearrange("(c p) t -> p c t", p=128)

          def proj_plain(cols0, nch, dst, dstname, d0, n):
              for c in range(nch):
                  pp = pa[c % 2]
                  for kc in range(8):
                      P.op('pe', lambda e, c=c, kc=kc, pp=pp: e.matmul(pp[:, 0:n], lhsT=w_in[:, kc, cols0 + c * 128:cols0 + (c + 1) * 128], rhs=hT[:, kc, 0:n],
                                                                      start=(kc == 0), stop=(kc == 7)), [f'w_in1.{kc}', f'E_hT.{kc}'], [f'E_pa{c % 2}'])
                  P.op('act', lambda e, c=c, pp=pp: e.copy(out=dst[:, d0 + c, 0:n], in_=pp[:, 0:n]), [f'E_pa{c % 2}'], [f'{dstname}.{d0 + c}'])

          def proj_v(n):
              ng = n // 128
              for g in range(ng):
                  for (c0, cn, pp, ppn) in ((0, 512, pa[g % 2], f'E_pa{g % 2}'), (512, 256, pb[g % 2], f'E_pb{g % 2}')):
                      for kc in range(8):
                          P.op('pe', lambda e, g=g, kc=kc, pp=pp, c0=c0, cn=cn: e.matmul(pp[:, 0:cn], lhsT=hT[:, kc, g * 128:(g + 1) * 128],
                                                                                         rhs=w_in[:, kc, 1536 + c0:1536 + c0 + cn], start=(kc == 0), stop=(kc == 7)),
                               [f'w_in1.{kc}', f'E_hT.{kc}'], [ppn])
                      P.op('dve', lambda e, g=g, pp=pp, c0=c0, cn=cn: e.tensor_copy(out=vt[:, g, c0:c0 + cn], in_=pp[:, 0:cn]), [ppn], [f'E_vt.{g}'])

          n = LC
          modulate_tile(k, B, S['x1_c'][1:1 + LC, :], n, 1, 1, 0, hT, 'E_hT')
          proj_plain(768, 6, qk, 'E_qk', 6, n)
          P.dma('sp', kTv[:, :, T:T + n], qk[:, 6:12, 0:n], keys('E_qk', 12)[6:12], [], 'E_qk')
          proj_v(n)
          P.dma('sp', S['v'][T:T + n, :].rearrange("(g p) f -> p g f", p=128), vt[:, 0:n // 128, :], keys('E_vt', 4), [], 'E_vt')
          for ti, (s, n) in enumerate(tiles_of(T)):
              modulate_tile(k, B, S['x1_l'][1 + s:1 + s + n, :], n, 1, 0, 0, hT, 'E_hT')
              P.dma('sp', rc[:, 0:n], I['rope_c'][:, s:s + n], [], ['E_rc'], 'E_rc')
              P.dma('sp', rs[:, 0:n], I['rope_s'][:, s:s + n], [], ['E_rs'], 'E_rs')
              need_q = s < TL + 16
              for c in (range(12) if need_q else range(6, 12)):
                  pp, pq = pa[c % 2], pb[c % 2]
                  for kc in range(8):
                      P.op('pe', lambda e, c=c, kc=kc, pp=pp: e.matmul(pp[:, 0:n], lhsT=w_in[:, kc, c * 128:(c + 1) * 128], rhs=hT[:, kc, 0:n],
                                                                      start=(kc == 0), stop=(kc == 7)), [f'w_in1.{kc}', f'E_hT.{kc}'], [f'E_pa{c % 2}'])
                  for kc in range(8):
                      P.op('pe', lambda e, c=c, kc=kc, pq=pq: e.matmul(pq[:, 0:n], lhsT=w_sw[:, kc, c * 128:(c + 1) * 128], rhs=hT[:, kc, 0:n],
                                                                      start=(kc == 0), stop=(kc == 7)), [f'w_sw1.{kc}', f'E_hT.{kc}'], [f'E_pb{c % 2}'])
                  P.op('dve', lambda e, pp=pp: e.tensor_tensor(out=t1[:, 0:n], in0=pp[:, 0:n], in1=rc[:, 0:n], op=ALU.mult), [f'E_pa{c % 2}', 'E_rc'], ['E_t1'])
                  P.op('dve', lambda e, pq=pq: e.tensor_tensor(out=t2[:, 0:n], in0=pq[:, 0:n], in1=rs[:, 0:n], op=ALU.mult), [f'E_pb{c % 2}', 'E_rs'], ['E_t2'])
                  P.op('pool', lambda e, c=c: e.tensor_tensor(out=qk[:, c, 0:n], in0=t1[:, 0:n], in1=t2[:, 0:n], op=ALU.add), ['E_t1', 'E_t2'], [f'E_qk.{c}'])
              if need_q:
                  P.dma('sp', qTv[:, :, s:s + n], qk[:, 0:6, 0:n], keys('E_qk', 12)[0:6], [], 'E_q')
              P.dma('sp', kTv[:, :, s:s + n], qk[:, 6:12, 0:n], keys('E_qk', 12)[6:12], [], 'E_qk')
              proj_v(n)
              P.dma('sp', S['v'][s:s + n, :].rearrange("(g p) f -> p g f", p=128), vt[:, 0:n // 128, :], keys('E_vt', 4), [], 'E_vt')
              for c in (range(2) if need_q else []):
                  pp, pq = pa[c % 2], pb[c % 2]
                  for (pz, pzn, cols) in ((pp, f'E_pa{c % 2}', 2304 + c * 128), (pq, f'E_pb{c % 2}', 2304 + 256 + c * 128)):
                      for kc in range(8):
                          P.op('pe', lambda e, kc=kc, pz=pz, cols=cols: e.matmul(pz[:, 0:n], lhsT=w_in[:, kc, cols:cols + 128], rhs=hT[:, kc, 0:n],
                                                                                start=(kc == 0), stop=(kc == 7)), [f'w_in1.{kc}', f'E_hT.{kc}'], [pzn])
                  P.op('act', lambda e, pq=pq: e.activation(out=sg[:, 0:n], in_=pq[:, 0:n], func=AF.Sigmoid), [f'E_pb{c % 2}'], ['E_sg'])
                  P.op('dve', lambda e, c=c, pp=pp: e.tensor_tensor(out=gl[:, c, 0:n], in0=pp[:, 0:n], in1=sg[:, 0:n], op=ALU.mult), [f'E_pa{c % 2}', 'E_sg'], [f'E_gl.{c}'])
              if need_q:
                  P.dma('sp', glv[:, :, PADZ + s:PADZ + s + n], gl[:, :, 0:n], keys('E_gl', 2), [], 'E_gl')
    P.barrier()
    if stop_after == 'l1E':
        return

    with ExitStack() as st:
        sb = lambda n, s, d: k.sb(n, s, d, st)
        ps = lambda n, s, d: k.ps(n, s, d, st)
        dl = sb('G_dl', [1, 4, 64], F32)
        P.dma('sp', dl[:], I['diff_l'], [], ['G_dl'], 'G_dl')
        sm = sb('G_sm', [1, 8], F32)
        pr_ = sb('G_pr', [1, 2, 64], F32)
        P.op('dve', lambda e: e.tensor_tensor(out=pr_[:, 0, :], in0=dl[:, 0, :], in1=dl[:, 1, :], op=ALU.mult), ['G_dl'], ['G_pr'])
        P.op('dve', lambda e: e.tensor_tensor(out=pr_[:, 1, :], in0=dl[:, 2, :], in1=dl[:, 3, :], op=ALU.mult), ['G_dl'], ['G_pr'])
        P.op('dve', lambda e: e.reduce_sum(out=sm[:, 0:1], in_=pr_[:, 0, :], axis=AX.X), ['G_pr'], ['G_sm'])
        P.op('dve', lambda e: e.reduce_sum(out=sm[:, 1:2], in_=pr_[:, 1, :], axis=AX.X), ['G_pr'], ['G_sm'])
        P.op('act', lambda e: e.activation(out=sm[:, 2:4], in_=sm[:, 0:2], func=AF.Exp), ['G_sm'], ['G_sm'])
        P.op('dve', lambda e: e.tensor_tensor(out=sm[:, 4:5], in0=sm[:, 3:4], in1=sm[:, 2:3], op=ALU.subtract), ['G_sm'], ['G_sm'])
        P.op('dve', lambda e: e.tensor_scalar(out=sm[:, 5:6], in0=sm[:, 4:5], scalar1=-LAMBDA_INIT1, scalar2=None, op0=ALU.add), ['G_sm'], ['G_sm'])
        neglam = sb('G_neglam', [128, 2], F32)
        gsub = sb('G_gsub', [128, 2], F32)
        P.dma('sp', gsub[:, 0:1], I['subln_g'], [], ['G_gsub'], 'G_gsub')
        P.op('dve', lambda e: e.tensor_scalar(out=gsub[:, 1:2], in0=gsub[:, 0:1], scalar1=(1.0 - LAMBDA_INIT1), scalar2=None, op0=ALU.mult), ['G_gsub'], ['G_gsub'])
        pS = [ps(f'G_pS{i}', [128, 2, 512], F32) for i in range(2)]
        po = [ps(f'G_po{i}', [128, 512], F32) for i in range(2)]
        pl = ps('G_pl', [128, 2, 512], F32)
        P.op('pe', lambda e: e.matmul(pl[:, 0, 0:1], lhsT=k.ones_f[0:1, :], rhs=sm[0:1, 5:6], start=True, stop=True), ['ones_f', 'G_sm'], ['G_pl'])
        P.op('dve', lambda e: e.tensor_copy(out=neglam[:, 0:1], in_=pl[:, 0, 0:1]), ['G_pl'], ['G_neglam'])
        kh = [sb(f'G_kh{j}', [128, TK], BF16) for j in range(2)]
        vh = [sb(f'G_vh{j}', [128, NKB, 128], BF16) for j in range(2)]
        qh = [sb(f'G_qh{j}', [128, 512], BF16) for j in range(2)]
        pT = [sb(f'G_pT{i}', [128, 2, 512], BF16) for i in range(4)]
        accs = [sb(f'G_acc{i}', [128, 2, 512], F32) for i in range(2)]
        rl = sb('G_rl', [128, 2, 512], F32)
        o1 = sb('G_o1', [128, 512], F32)
        o2 = sb('G_o2', [128, 512], F32)
        sq = sb('G_sq', [128, 512], F32)
        ych = sb('G_ych', [128, 512], BF16)
        vv = S['v'].rearrange("(kb p) f -> p kb f", p=128)
        qtiles = tiles_of(TL)

        def load_head(h):
            j = h % 2
            P.dma('sp', kh[j][:, :], S['kT'][h * 128:h * 128 + 128, :], [], [f'G_kh{j}'], f'G_kh{j}')
            for b0 in range(0, NKB, 16):
                b1 = min(NKB, b0 + 16)
                P.dma('sp', vh[j][:, b0:b1, :], vv[:, b0:b1, h * 128:(h + 1) * 128], [], [f'G_vh{j}'], f'G_vh{j}')

        def load_q(h, ti):
            s, n = qtiles[ti]
            gi = (h * len(qtiles) + ti) % 2
            P.dma('sp', qh[gi][:, 0:n], S['qT'][h * 128:(h + 1) * 128, s:s + n], [], [f'G_qh{gi}'], f'G_qh{gi}')

        load_head(0)
        load_q(0, 0)
        for h in range(6):
            hj = h % 2
            for ti, (s, n) in enumerate(qtiles):
                gi = (h * len(qtiles) + ti) % 2
                qcur = qh[gi]
                qn_ = f'G_qh{gi}'
                if ti + 1 < len(qtiles):
                    load_q(h, ti + 1)
                elif h + 1 < 6:
                    load_q(h + 1, 0)
                if ti == 0 and h + 1 < 6:
                    load_head(h + 1)

                def emit_qk(kb):
                    pp = pS[kb % 2]
                    for comp in range(2):
                        P.op('pe', lambda e, comp=comp, kb=kb, pp=pp: e.matmul(pp[:, comp, 0:n], lhsT=kh[hj][64 * comp:64 * comp + 64, kb * 128:(kb + 1) * 128], rhs=qcur[64 * comp:64 * comp + 64, 0:n], start=True, stop=True),
                             [f'G_kh{hj}', qn_], [f'G_pS{kb % 2}'])
                emit_qk(0)
                first = [True, True]
                for kb in range(NKB):
                    if kb + 1 < NKB:
                        emit_qk(kb + 1)
                    pp = pS[kb % 2]
                    ppn = f'G_pS{kb % 2}'
                    pt = pT[kb % 4]
                    ptn = f'G_pT{kb % 4}'
                    P.op('act', lambda e, pp=pp, pt=pt: e.activation(out=pt[:, :, 0:n], in_=pp[:, :, 0:n], func=AF.Exp, scale=0.125), [ppn], [ptn])
                    for comp in range(2):
                        P.op('pe', lambda e, comp=comp, kb=kb, pt=pt: e.matmul(po[comp][:, 0:n], lhsT=vh[hj][:, kb, :], rhs=pt[:, comp, 0:n], start=(kb == 0), stop=(kb == NKB - 1)),
                             [f'G_vh{hj}', ptn], [f'G_po{comp}'])
                    ai = 1 if kb % 3 == 2 else 0
                    if os.environ.get('DBG_F1_ADD') == 'dve':
                        ai = 0
                    if os.environ.get('DBG_F1_ADD') == 'none' and kb > 2:
                        continue
                    ae = 'pool' if ai == 1 else 'dve'
                    ac = accs[ai]
                    acn = f'G_acc{ai}'
                    if first[ai]:
                        first[ai] = False
                        P.op(ae, lambda e, ac=ac, pt=pt: e.tensor_copy(out=ac[:, :, 0:n], in_=pt[:, :, 0:n]), [ptn], [acn])
                    else:
                        P.op(ae, lambda e, ac=ac, pt=pt: e.tensor_tensor(out=ac[:, :, 0:n], in0=ac[:, :, 0:n], in1=pt[:, :, 0:n], op=ALU.add), [ptn, acn], [acn])
                for comp in range(2):
                    for i in range(2):
                        P.op('pe', lambda e, comp=comp, i=i: e.matmul(pl[:, comp, 0:n], lhsT=k.ones_f[:], rhs=accs[i][:, comp, 0:n], start=(i == 0), stop=(i == 1)),
                             ['ones_f', f'G_acc{i}'], ['G_pl'])
                P.op('dve', lambda e: e.reciprocal(out=rl[:, :, 0:n], in_=pl[:, :, 0:n]), ['G_pl'], ['G_rl'])
                P.op('dve', lambda e: e.tensor_tensor(out=o1[:, 0:n], in0=po[0][:, 0:n], in1=rl[:, 0, 0:n], op=ALU.mult), ['G_po0', 'G_rl'], ['G_o1'])
                P.op('dve', lambda e: e.tensor_tensor(out=o2[:, 0:n], in0=po[1][:, 0:n], in1=rl[:, 1, 0:n], op=ALU.mult), ['G_po1', 'G_rl'], ['G_o2'])
                P.op('dve', lambda e: e.scalar_tensor_tensor(out=o1[:, 0:n], in0=o2[:, 0:n], scalar=neglam[:, 0:1], in1=o1[:, 0:n], op0=ALU.mult, op1=ALU.add),
                     ['G_o1', 'G_o2', 'G_neglam'], ['G_o1'])
                P.op('act', lambda e: e.activation(out=sq[:, 0:n], in_=o1[:, 0:n], func=AF.Square), ['G_o1'], ['G_sq'])
                P.op('pe', lambda e: e.matmul(pl[:, 0, 0:n], lhsT=k.ones_f[:], rhs=sq[:, 0:n], start=True, stop=True), ['ones_f', 'G_sq'], ['G_pl'])
                P.op('act', lambda e: e.activation(out=sq[:, 0:n], in_=pl[:, 0, 0:n], func=AF.Sqrt, bias=k.cst[:, 0:1], scale=1.0 / 128), ['G_pl', 'cst'], ['G_sq'])
                P.op('dve', lambda e: e.reciprocal(out=sq[:, 0:n], in_=sq[:, 0:n]), ['G_sq'], ['G_sq'])
                P.op('dve', lambda e: e.scalar_tensor_tensor(out=ych[:, 0:n], in0=o1[:, 0:n], scalar=gsub[:, 1:2], in1=sq[:, 0:n], op0=ALU.mult, op1=ALU.mult),
                     ['G_o1', 'G_sq', 'G_gsub'], ['G_ych'])
                P.dma('sp', yc_d[h * 128:(h + 1) * 128, s:s + n], ych[:, 0:n], ['G_ych'], [], 'G_ych')
    P.barrier()
    if stop_after == 'l1F1':
        return

    with ExitStack() as st:
        sb = lambda n, s, d: k.sb(n, s, d, st)
        ps = lambda n, s, d: k.ps(n, s, d, st)
        w_out = sb('w_out1', [128, 8, D], BF16)
        with ExitStack() as st2:
            fold_w_bf16(k, st2, 'w_out1', w_out, I['cd_w_out'], 8, k.gbc_d[1, 0, 0])
            P.barrier()
        cp = {}
        for nm, shp in (('conf_w', [128, 2, 31]), ('conf_b', [128, 2]), ('conf_lng', [128, 2]), ('conf_lnb', [128, 2])):
            cp[nm] = sb('H_' + nm, shp, F32)
            P.dma('sp', cp[nm][:], I[nm], [], ['H_' + nm], 'H_' + nm)
        yT = sb('H_yT', [128, 8, 512], BF16)
        gin = sb('H_gin', [128, 2, 542], F32)
        ca = sb('H_ca', [128, 512], F32)
        cb_ = sb('H_cb', [128, 512], F32)
        ct = [sb(f'H_ct{i}', [128, 512], F32) for i in range(2)]
        xm = sb('H_xm', [128, 2, 512], F32)
        sq = sb('H_sq', [128, 2, 512], F32)
        rstd = sb('H_rstd', [128, 512], F32)
        xt = sb('H_xt', [128, 4, D], F32)
        xo = sb('H_xo', [128, 4, D], F32)
        pm = ps('H_pm', [128, 512], F32)
        pv = ps('H_pv', [128, 512], F32)
        po = [ps(f'H_po{i}', [128, 512], F32) for i in range(4)]
        glv = S['gl'].rearrange("(c p) t -> p c t", p=128)
        ycv = yc_d.rearrange("(c p) t -> p c t", p=128)
        for (s, n) in tiles_of(TL):
            ng = n // 128
            P.dma('sp', yT[:, 0:6, 0:n], ycv[:, :, s:s + n], [], keys('H_yT', 8)[0:6], 'H_yT')
            P.dma('sp', gin[:, :, 0:n + 30], glv[:, :, PADZ + s - 15:PADZ + s + n + 15], [], keys('H_gin', 2), 'H_gin')
            P.dma('sp', xt[:, 0:ng, :], S['x1_l'][1 + s:1 + s + n, :].rearrange("(g p) f -> p g f", p=128), [], ['H_xt'], 'H_xt')
            for c in range(2):
                P.op('act', lambda e, c=c: e.activation(out=ca[:, 0:n], in_=gin[:, c, 0:n], func=AF.Identity, scale=cp['conf_w'][:, c, 0:1], bias=cp['conf_b'][:, c:c + 1]),
                     [f'H_gin.{c}', 'H_conf_w', 'H_conf_b'], ['H_ca'])
                P.op('pool', lambda e, c=c: e.tensor_scalar(out=cb_[:, 0:n], in0=gin[:, c, 1:1 + n], scalar1=cp['conf_w'][:, c, 1:2], scalar2=None, op0=ALU.mult),
                     [f'H_gin.{c}', 'H_conf_w'], ['H_cb'])
                for t in range(2, 31):
                    if t % 2 == 0:
                        P.op('dve', lambda e, c=c, t=t: e.scalar_tensor_tensor(out=ca[:, 0:n], in0=gin[:, c, t:t + n], scalar=cp['conf_w'][:, c, t:t + 1], in1=ca[:, 0:n],
                                                                               op0=ALU.mult, op1=ALU.add), [f'H_gin.{c}', 'H_ca'], ['H_ca'])
                    else:
                        ctt = ct[(t // 2) % 2]
                        ctn = f'H_ct{(t // 2) % 2}'
                        P.op('act', lambda e, c=c, t=t, ctt=ctt: e.activation(out=ctt[:, 0:n], in_=gin[:, c, t:t + n], func=AF.Identity, scale=cp['conf_w'][:, c, t:t + 1], bias=k.cst[:, 2:3]),
                             [f'H_gin.{c}', 'H_conf_w', 'cst'], [ctn])
                        P.op('pool', lambda e, ctt=ctt: e.tensor_tensor(out=cb_[:, 0:n], in0=cb_[:, 0:n], in1=ctt[:, 0:n], op=ALU.add), [ctn, 'H_cb'], ['H_cb'])
                P.op('dve', lambda e, c=c: e.tensor_tensor(out=xm[:, c, 0:n], in0=ca[:, 0:n], in1=cb_[:, 0:n], op=ALU.add), ['H_ca', 'H_cb'], [f'H_xm.{c}'])
            for c in range(2):
                P.op('pe', lambda e, c=c: e.matmul(pm[:, 0:n], lhsT=k.ones_f[:], rhs=xm[:, c, 0:n], start=(c == 0), stop=(c == 1)), ['ones_f', f'H_xm.{c}'], ['H_pm'])
            for c in range(2):
                P.op('dve', lambda e, c=c: e.scalar_tensor_tensor(out=xm[:, c, 0:n], in0=pm[:, 0:n], scalar=-1.0 / 256, in1=xm[:, c, 0:n], op0=ALU.mult, op1=ALU.add),
                     ['H_pm', f'H_xm.{c}'], [f'H_xm.{c}'])
                P.op('act', lambda e, c=c: e.activation(out=sq[:, c, 0:n], in_=xm[:, c, 0:n], func=AF.Square), [f'H_xm.{c}'], [f'H_sq.{c}'])
            for c in range(2):
                P.op('pe', lambda e, c=c: e.matmul(pv[:, 0:n], lhsT=k.ones_f[:], rhs=sq[:, c, 0:n], start=(c == 0), stop=(c == 1)), ['ones_f', f'H_sq.{c}'], ['H_pv'])
            P.op('act', lambda e: e.activation(out=rstd[:, 0:n], in_=pv[:, 0:n], func=AF.Sqrt, bias=k.cst[:, 0:1], scale=1.0 / 256), ['H_pv', 'cst'], ['H_rstd'])
            P.op('dve', lambda e: e.reciprocal(out=rstd[:, 0:n], in_=rstd[:, 0:n]), ['H_rstd'], ['H_rstd'])
            for c in range(2):
                P.op('dve', lambda e, c=c: e.tensor_tensor(out=xm[:, c, 0:n], in0=xm[:, c, 0:n], in1=rstd[:, 0:n], op=ALU.mult), [f'H_xm.{c}', 'H_rstd'], [f'H_xm.{c}'])
                P.op('act', lambda e, c=c: e.activation(out=yT[:, 6 + c, 0:n], in_=xm[:, c, 0:n], func=AF.Silu, scale=cp['conf_lng'][:, c:c + 1], bias=cp['conf_lnb'][:, c:c + 1]),
                     [f'H_xm.{c}', 'H_conf_lng', 'H_conf_lnb'], [f'H_yT.{6 + c}'])
            for g in range(ng):
                for nt in range(2):
                    pp = po[(2 * g + nt) % 4]
                    ppn = f'H_po{(2 * g + nt) % 4}'
                    for kc in range(8):
                        P.op('pe', lambda e, g=g, nt=nt, kc=kc, pp=pp: e.matmul(pp[:, :], lhsT=yT[:, kc, g * 128:(g + 1) * 128], rhs=w_out[:, kc, nt * 512:(nt + 1) * 512],
                                                                               start=(kc == 0), stop=(kc == 7)), [f'H_yT.{kc}', f'w_out1.{kc}'], [ppn])
                    P.op('dve', lambda e, g=g, nt=nt, pp=pp: e.tensor_tensor(out=xo[:, g, nt * 512:(nt + 1) * 512], in0=pp[:, :], in1=xt[:, g, nt * 512:(nt + 1) * 512], op=ALU.add),
                         [ppn, 'H_xt'], [f'H_xo.{g}'])
            P.dma('sp', S['x15'][1 + s:1 + s + n, :].rearrange("(g p) f -> p g f", p=128), xo[:, 0:ng, :], keys('H_xo', 4)[0:ng], [], 'H_xo')
    P.barrier()
    if stop_after == 'l1F2':
        return
    ffn_phase(k, 1, 0, S['x15'], None, TL, final=True, flush=(not k.half), out_rows=k.TO)
    P.barrier()


def _fm(v, nch):
    return np.ascontiguousarray(np.asarray(v, np.float32).reshape(nch, 128).T)


def prep_shared(inp, T):
    f = lambda a: np.ascontiguousarray(np.asarray(a, np.float32))
    d = {}
    d['mod_w'] = f(inp['mod_w'])
    d['mod_b'] = f(inp['mod_b'])
    d['modb_fm'] = f(np.asarray(inp['mod_b']).reshape(2, 6, 8, 128).transpose(3, 0, 1, 2))
    ng = np.stack([np.asarray(inp['norm_mix_g']), np.asarray(inp['norm_ffn_g'])], axis=1)
    d['ng_fm'] = f(ng.reshape(2, 2, 8, 128).transpose(3, 0, 1, 2))
    d['final_g_bc'] = f(np.broadcast_to(np.asarray(inp['final_g'])[None, :], (128, D)))
    d['ident'] = f(np.eye(128))
    d['ab_w_in'] = f(inp['ab_w_in'][0])
    d['ab_w_out'] = f(inp['ab_w_out'][0])
    cw4 = np.asarray(inp['lru_conv_w'][0], np.float32).reshape(4, 6, 128).transpose(2, 1, 0)
    d['lru_cw'] = f(np.concatenate([cw4, np.zeros((128, 6, 1), np.float32)], axis=2))
    d['pool_fl'] = f(np.stack([np.ones(128), np.zeros(128)], axis=1))
    d['lru_cb'] = _fm(inp['lru_conv_b'][0], 6)
    for nm, src in (('lru_bdA', inp['lru_wa'][0]), ('lru_bdX', inp['lru_wx'][0])):
        src = np.asarray(src)
        bd = np.zeros((128, 2, 6, 128), np.float32)
        for dd in range(2):
            for c in range(6):
                bd[0:64, dd, c, 0:64] = src[dd, 2 * c]
                bd[64:128, dd, c, 64:128] = src[dd, 2 * c + 1]
        d[nm] = bd
    for nm, src in (('lru_ba', inp['lru_ba'][0]), ('lru_bx', inp['lru_bx'][0]), ('lru_lam', inp['lru_lambda'][0])):
        d[nm] = f(np.asarray(src).reshape(2, 6, 128).transpose(2, 0, 1))
    pw = np.asarray(inp['pool_w'][0])
    bd = np.zeros((128, 2, 128), np.float32)
    for ch in range(2):
        bd[0:64, ch, 0:64] = pw[2 * ch]
        bd[64:128, ch, 64:128] = pw[2 * ch + 1]
    d['pool_bd'] = bd
    d['pool_scale'] = _fm(inp['pool_scale'][0], 2)
    invw = np.zeros((128, 2), np.float32)
    corr = np.ones((128, 2, 2, 16), np.float32)
    wins = (2, 4, 8, 16)
    Lbig = 1 << 20
    for g, w in enumerate(wins):
        ch, half = g // 2, g % 2
        psl = slice(64 * half, 64 * half + 64)
        invw[psl, ch] = 1.0 / w
        for i in range(16):
            t = i
            cnt = (t + w - w // 2) - max(t - w // 2, 0)
            corr[psl, ch, 0, i] = float(w) / cnt
            t = Lbig - 16 + i
            cnt = min(t + w - w // 2, Lbig) - (t - w // 2)
            corr[psl, ch, 1, i] = float(w) / cnt
    d['pool_invw'] = invw
    d['pool_corr'] = corr
    d['ffn_w_up'] = f(inp['ffn_w_up'])
    d['ffn_w_down'] = f(inp['ffn_w_down'])
    d['ffn_cw'] = f(np.asarray(inp['ffn_conv_w']).reshape(2, 3, 2 * NFC, 128).transpose(3, 0, 2, 1))
    d['ffn_cb'] = f(np.asarray(inp['ffn_conv_b']).reshape(2, 2 * NFC, 128).transpose(2, 0, 1))
    w_in = np.asarray(inp['cd_w_in'][0], np.float32)
    d['cd_w_in'] = f(w_in)
    qk = w_in[:, :1536].reshape(D, 1536 // 32, 2, 16)
    d['cd_w_sw'] = f(qk[:, :, ::-1, :].reshape(D, 1536))
    d['cd_w_out'] = f(inp['cd_w_out'][0])
    t = np.arange(T)
    row = (t // GRID_W).astype(np.float32)
    col = (t % GRID_W).astype(np.float32)
    inv = (10000.0 ** (-np.arange(16, dtype=np.float32) / 16)).astype(np.float32)
    ang_r = (row[:, None] * inv).astype(np.float32)
    ang_c = (col[:, None] * inv).astype(np.float32)
    rc = np.zeros((128, T), np.float32)
    rs = np.zeros((128, T), np.float32)
    for p in range(128):
        dd = p % 64
        ang = ang_r if dd < 32 else ang_c
        fq = dd % 16
        first = (dd % 32) < 16
        rc[p] = np.cos(ang[:, fq])
        rs[p] = (-1.0 if first else 1.0) * np.sin(ang[:, fq])
    d['rope_c'] = rc
    d['rope_s'] = rs
    d['diff_l'] = f(np.stack([np.asarray(inp['diff_lq1'][0]), np.asarray(inp['diff_lk1'][0]),
                              np.asarray(inp['diff_lq2'][0]), np.asarray(inp['diff_lk2'][0])])[None])
    d['subln_g'] = f(np.asarray(inp['diff_subln_g'][0]).reshape(128, 1))
    d['conf_w'] = f(np.asarray(inp['conf_dw_w'][0]).reshape(31, 2, 128).transpose(2, 1, 0))
    d['conf_b'] = _fm(inp['conf_dw_b'][0], 2)
    d['conf_lng'] = _fm(inp['conf_ln_g'][0], 2)
    d['conf_lnb'] = _fm(inp['conf_ln_b'][0], 2)
    return d


def prep_rev(shared, T):
    f = lambda a: np.ascontiguousarray(np.asarray(a, np.float32))
    d = dict(shared)
    cw = shared['lru_cw']
    d['lru_cw'] = f(cw[:, :, ::-1])
    for nm in ('lru_bdA', 'lru_bdX', 'lru_ba', 'lru_bx', 'lru_lam'):
        d[nm] = f(shared[nm][:, ::-1])
    d['pool_fl'] = f(np.stack([np.zeros(128), np.ones(128)], axis=1))
    corr = np.ones((128, 2, 2, 16), np.float32)
    Lbig = 1 << 20
    for g, w in enumerate((2, 4, 8, 16)):
        ch, half = g // 2, g % 2
        psl = slice(64 * half, 64 * half + 64)
        for i in range(16):
            r = i
            cnt = (r + w // 2) - max(r - w // 2 + 1, 0) + 1
            corr[psl, ch, 0, i] = float(w) / cnt
            r = Lbig - 16 + i
            cnt = min(r + w // 2, Lbig - 1) - (r - w // 2 + 1) + 1
            corr[psl, ch, 1, i] = float(w) / cnt
    d['pool_corr'] = corr
    d['ffn_cw'] = f(shared['ffn_cw'][:, :, :, ::-1])
    d['conf_w'] = f(shared['conf_w'][:, :, ::-1])
    d['rope_c'] = f(shared['rope_c'][:, ::-1])
    d['rope_s'] = f(shared['rope_s'][:, ::-1])
    return d


def prep_core(inp, shared, b, T, rev=False):
    m = dict(shared)
    x = np.asarray(inp['x'][b, :T], np.float32)
    ctx = np.asarray(inp['ctx'][b], np.float32)
    if rev:
        x = x[::-1]
        ctx = ctx[::-1]
    m['x'] = np.ascontiguousarray(x)
    m['ctx'] = np.ascontiguousarray(ctx)
    m['c_fm'] = np.ascontiguousarray(np.stack([_fm(inp['c'][b], 8), _fm(inp['c_ctx'], 8)], axis=1))
    return m


_CACHE = {}


def kernel(**inputs):
    T = inputs['x'].shape[1]
    Bn = inputs['x'].shape[0]
    if T not in _CACHE:
        _CACHE[T] = build(T)
    kk = _CACHE[T]
    shared = prep_shared(inputs, T)
    shared_r = prep_rev(shared, T)
    in_maps = [prep_core(inputs, shared, b, T, rev=False) for b in range(Bn)] + \
              [prep_core(inputs, shared_r, b, T, rev=True) for b in range(Bn)]
    res = run_bass_kernel_spmd(kk.nc, in_maps, core_ids=list(range(2 * Bn)))
    out = np.empty((Bn, T, D), np.float32)
    for b in range(Bn):
        out[b, :T // 2] = np.asarray(res.results[b]['out'], np.float32)
        out[b, T // 2:] = np.asarray(res.results[Bn + b]['out'], np.float32)[::-1]
    return out
```

```python
import numpy as np
import math
import os
from contextlib import ExitStack
import concourse.bass as bass
import concourse.mybir as mybir
from concourse.bass_utils import run_bass_kernel_spmd
from concourse.ap import AP

F32 = mybir.dt.float32
BF16 = mybir.dt.bfloat16
AF = mybir.ActivationFunctionType
ALU = mybir.AluOpType
AX = mybir.AxisListType

D = 1024
LC = 256
DFF = 2816
NFC = DFF // 128
PADZ = 16
EPS = 1e-6
GRID_W = 64
LAMBDA_INIT1 = 0.8 - 0.6 * math.exp(-0.3 * 1)
SAME_ENGINE_SYNC = True


def rev(ap):
    a = [list(x) for x in ap.ap]
    st, n = a[-1]
    a[-1] = [-st, n]
    return AP(ap.tensor, ap.offset + st * (n - 1), a)


class Prog:
    def __init__(self, nc, es):
        self.nc = nc
        self.es = es
        self.eng = {'pe': nc.tensor, 'act': nc.scalar, 'dve': nc.vector, 'pool': nc.gpsimd, 'sp': nc.sync}
        self.esem = {e: es.enter_context(nc.semaphore('S_' + e)) for e in ('pe', 'act', 'dve', 'pool')}
        self.ecnt = {e: 0 for e in self.esem}
        self.dsem = {}
        self.dpool = []
        self.nd = 0
        self.waited = {e: {} for e in self.eng}
        self.lastw = {}
        self.rd = {}
        self.ninstr = 0

    def _need(self, reads, writes):
        ev = {}

        def add(e):
            if e is None:
                return
            k, sem, val = e
            if k not in ev or ev[k][1] < val:
                ev[k] = (sem, val)
        for r in reads:
            add(self.lastw.get(r))
        for w in writes:
            add(self.lastw.get(w))
            for e in self.rd.get(w, {}).items():
                add((e[0], e[1][0], e[1][1]))
        return ev

    def _wait(self, e, ev):
        for k, (sem, val) in ev.items():
            if k == 'S_' + e and (e == 'pe' or not SAME_ENGINE_SYNC):
                continue
            if self.waited[e].get(k, 0) < val:
                self.eng[e].wait_ge(sem, val)
                self.waited[e][k] = val
                self.ninstr += 1

    def _commit(self, ev, reads, writes):
        k, sem, val = ev
        for w in writes:
            self.lastw[w] = ev
            self.rd[w] = {}
        for r in reads:
            self.rd.setdefault(r, {})[k] = (sem, val)

    def op(self, e, fn, reads, writes):
        self._wait(e, self._need(reads, writes))
        ins = fn(self.eng[e])
        self.ecnt[e] += 1
        ins.then_inc(self.esem[e], 1)
        self.ninstr += 1
        self._commit(('S_' + e, self.esem[e], self.ecnt[e]), reads, writes)

    def dma(self, q, out, in_, reads, writes, key):
        self._wait(q, self._need(reads, writes))
        if key not in self.dsem:
            if self.dpool:
                self.dsem[key] = self.dpool.pop()
            else:
                nm = 'D%d' % self.nd
                self.nd += 1
                self.dsem[key] = [self.es.enter_context(self.nc.semaphore(nm)), 0, nm]
        d = self.dsem[key]
        ins = self.eng[q].dma_start(out=out, in_=in_)
        d[1] += 16
        ins.then_inc(d[0], 16)
        self.ninstr += 1
        self._commit((d[2], d[0], d[1]), reads, writes)

    def barrier(self):
        ev = {}
        for e in self.esem:
            if self.ecnt[e] > 0:
                ev['S_' + e] = (self.esem[e], self.ecnt[e])
        for k, d in self.dsem.items():
            if d[1] > 0:
                ev[d[2]] = (d[0], d[1])
        for e in self.eng:
            self._wait(e, dict(ev))
        self.lastw = {}
        self.rd = {}
        for k, d in self.dsem.items():
            self.dpool.append(d)
        self.dsem = {}


def keys(name, n):
    return [f"{name}.{i}" for i in range(n)]


def tiles_of(T, w=512):
    out = []
    s = 0
    while s < T:
        n = min(w, T - s)
        out.append((s, n))
        s += n
    return out


class K:
    pass


def build(T, dbg=False, stop_after=None, half=True):
    nc = bass.Bass("TRN2", target_bir_lowering=False)
    k = K()
    k.nc = nc
    k.T = T
    k.half = half
    k.TL = (T // 2 + 128) if half else T
    k.TO = (T // 2) if half else T

    def din(name, shape, dt=F32):
        return nc.dram_tensor(name, list(shape), dt, kind="ExternalInput").ap()

    def dscr(name, shape, dt=F32, out=False):
        kind = "ExternalOutput" if (out or dbg) else "Internal"
        return nc.dram_tensor(name, list(shape), dt, kind=kind).ap()

    I = {}
    I['x'] = din('x', [T, D])
    I['ctx'] = din('ctx', [LC, D])
    I['c_fm'] = din('c_fm', [128, 2, 8])
    I['mod_w'] = din('mod_w', [2, D, 6 * D])
    I['modb_fm'] = din('modb_fm', [128, 2, 6, 8])
    I['mod_b'] = din('mod_b', [2, 6 * D])
    I['ng_fm'] = din('ng_fm', [128, 2, 2, 8])
    I['final_g_bc'] = din('final_g_bc', [128, D])
    I['ident'] = din('ident', [128, 128])
    I['ab_w_in'] = din('ab_w_in', [D, 1792])
    I['ab_w_out'] = din('ab_w_out', [D, D])
    I['lru_cw'] = din('lru_cw', [128, 6, 5])
    I['pool_fl'] = din('pool_fl', [128, 2])
    I['lru_cb'] = din('lru_cb', [128, 6])
    I['lru_bdA'] = din('lru_bdA', [128, 2, 6, 128])
    I['lru_bdX'] = din('lru_bdX', [128, 2, 6, 128])
    I['lru_ba'] = din('lru_ba', [128, 2, 6])
    I['lru_bx'] = din('lru_bx', [128, 2, 6])
    I['lru_lam'] = din('lru_lam', [128, 2, 6])
    I['pool_bd'] = din('pool_bd', [128, 2, 128])
    I['pool_scale'] = din('pool_scale', [128, 2])
    I['pool_invw'] = din('pool_invw', [128, 2])
    I['pool_corr'] = din('pool_corr', [128, 2, 2, 16])
    I['ffn_w_up'] = din('ffn_w_up', [2, D, 2 * DFF])
    I['ffn_w_down'] = din('ffn_w_down', [2, DFF, D])
    I['ffn_cw'] = din('ffn_cw', [128, 2, 2 * NFC, 3])
    I['ffn_cb'] = din('ffn_cb', [128, 2, 2 * NFC])
    I['cd_w_in'] = din('cd_w_in', [D, 2816])
    I['cd_w_sw'] = din('cd_w_sw', [D, 1536])
    I['cd_w_out'] = din('cd_w_out', [D, D])
    I['rope_c'] = din('rope_c', [128, T])
    I['rope_s'] = din('rope_s', [128, T])
    I['diff_l'] = din('diff_l', [1, 4, 64])
    I['subln_g'] = din('subln_g', [128, 1])
    I['conf_w'] = din('conf_w', [128, 2, 31])
    I['conf_b'] = din('conf_b', [128, 2])
    I['conf_lng'] = din('conf_lng', [128, 2])
    I['conf_lnb'] = din('conf_lnb', [128, 2])
    k.I = I

    k.out = nc.dram_tensor('out', [k.TO, D], F32, kind="ExternalOutput").ap()
    S = {}
    for nm, TT in (('l', T), ('c', LC)):
        S['z_' + nm] = dscr('z_' + nm, [1792, PADZ + TT + PADZ])
        S['xa_' + nm] = dscr('xa_' + nm, [768, TT])
        S['hf_' + nm] = dscr('hf_' + nm, [768, TT])
        S['x05_' + nm] = dscr('x05_' + nm, [1 + TT, D])
        S['x1_' + nm] = dscr('x1_' + nm, [1 + TT + 128, D])
    S['x15'] = dscr('x15', [1 + T, D])
    S['x2'] = dscr('x2', [1 + T + 128, D])
    S['qT'] = dscr('qT', [768, T], BF16)
    S['kT'] = dscr('kT', [768, T + LC], BF16)
    S['v'] = dscr('v', [T + LC, 768], BF16)
    S['gl'] = dscr('gl', [256, PADZ + T + PADZ])
    k.S = S

    with ExitStack() as es:
        P = Prog(nc, es)
        k.P = P

        uid = [0]

        def sb(name, shape, dt, st=es):
            uid[0] += 1
            return st.enter_context(nc.sbuf_tensor(f"{name}_s{uid[0]}", list(shape), dt))

        def ps(name, shape, dt, st=es):
            uid[0] += 1
            return st.enter_context(nc.psum_tensor(f"{name}_p{uid[0]}", list(shape), dt))
        k.sb = sb
        k.ps = ps

        ident = sb('ident', [128, 128], BF16)
        P.dma('pool', ident[:], I['ident'], [], ['ident'], 'ident')
        k.ident = ident
        cst = sb('cst', [128, 8], F32)
        P.op('dve', lambda e: e.memset(cst[:, 0:1], EPS), [], ['cst'])
        P.op('dve', lambda e: e.memset(cst[:, 1:2], 1.0), [], ['cst'])
        P.op('dve', lambda e: e.memset(cst[:, 2:3], 0.0), [], ['cst'])
        k.cst = cst
        zero = sb('zero', [128, 512], F32)
        P.op('dve', lambda e: e.memset(zero[:], 0.0), [], ['zero'])
        k.zero = zero
        ones_bf = sb('ones_bf', [128, 128], BF16)
        P.op('dve', lambda e: e.memset(ones_bf[:], 1.0), [], ['ones_bf'])
        k.ones_bf = ones_bf
        ones_f = sb('ones_f', [128, 128], F32)
        P.op('dve', lambda e: e.memset(ones_f[:], 1.0), [], ['ones_f'])
        k.ones_f = ones_f

        modfm = sb('modfm', [128, 2, 2, 4, 8], F32)
        k.modfm = modfm
        k.gbc_d = nc.dram_tensor('gbc_d', [2, 2, 2, 128, D], F32, kind="Internal").ap()

        if os.environ.get('DBG_ONLY_F1'):
            layer1(k, 'l1F1')
            return finish(k, es)
        phase_adaln(k)
        P.barrier()
        if dbg:
            mdbg = nc.dram_tensor('modfm_dbg', [128, 2, 2, 4, 8], F32, kind="ExternalOutput").ap()
            P.dma('sp', mdbg, modfm[:], [], [], 'mdbg')
            gdbg = nc.dram_tensor('gbc_dbg', [2, 2, 2, 128, D], F32, kind="ExternalOutput").ap()
            P.dma('sp', gdbg, k.gbc_d, [], [], 'gdbg')
        if stop_after == 'adaln':
            return finish(k, es)

        layer0(k, stop_after)
        if stop_after is not None and stop_after.startswith('l0'):
            return finish(k, es)
        layer1(k, stop_after)
        return finish(k, es)


def finish(k, es):
    k.P.barrier()
    k.ninstr = k.P.ninstr
    return k


def phase_adaln(k):
    nc, P, I = k.nc, k.P, k.I
    with ExitStack() as st:
        sb = lambda n, s, d: k.sb(n, s, d, st)
        ps = lambda n, s, d: k.ps(n, s, d, st)
        cf = sb('ad_cf', [128, 2, 8], F32)
        P.dma('sp', cf[:], I['c_fm'], [], ['ad_cf'], 'ad_cf')
        sc = sb('ad_sc', [128, 2, 8], F32)
        P.op('act', lambda e: e.activation(out=sc[:], in_=cf[:], func=AF.Silu), ['ad_cf'], ['ad_sc'])
        rep = sb('ad_rep', [128, 2, 8, 128], F32)
        for s_ in range(2):
            for kc in range(8):
                P.op('dve', lambda e, s_=s_, kc=kc: e.tensor_copy(out=rep[:, s_, kc, :], in_=sc[:, s_, kc:kc + 1].to_broadcast([128, 128])),
                     ['ad_sc'], [f'ad_rep.{s_}.{kc}'])
        modb = sb('ad_modb', [128, 2, 6, 8], F32)
        P.dma('sp', modb[:], I['modb_fm'], [], ['ad_modb'], 'ad_modb')
        ng = sb('ad_ng', [128, 2, 2, 8], F32)
        P.dma('sp', ng[:], I['ng_fm'], [], ['ad_ng'], 'ad_ng')
        brow = sb('ad_brow', [1, 2, 6 * D], F32)
        P.dma('sp', brow[:], I['mod_b'].rearrange("(o l) n -> o l n", o=1), [], ['ad_brow'], 'ad_brow')
        wt = [sb(f'ad_w{i}', [128, 6 * D], F32) for i in range(2)]
        gst = sb('ad_gst', [128, 2, D], F32)
        facc = sb('ad_facc', [128, 32, 2], F32)
        pfm = ps('ad_pfm', [128, 32, 2], F32)
        pbc = [ps(f'ad_pbc{i}', [128, 512], F32) for i in range(4)]
        for l in range(2):
            for pss in range(2):
                for kc in range(8):
                    w = wt[kc % 2]
                    wk = f'ad_w{kc % 2}'
                    P.dma('sp', w[:], I['mod_w'][l, kc * 128:(kc + 1) * 128, :], [], [wk], wk)
                    if pss == 0:
                        jmap = [0, 1, 3, 4]
                        for jj, j in enumerate(jmap):
                            for fc in range(8):
                                col = j * D + fc * 128
                                P.op('pe', lambda e, w=w, col=col, jj=jj, fc=fc, kc=kc: e.matmul(
                                    pfm[:, jj * 8 + fc, :], lhsT=w[:, col:col + 128], rhs=sc[:, :, kc],
                                    start=True, stop=True), [wk, 'ad_sc'], ['ad_pfm'])
                        if kc == 0:
                            P.op('dve', lambda e: e.tensor_copy(out=facc[:], in_=pfm[:]), ['ad_pfm'], ['ad_facc'])
                        else:
                            P.op('dve', lambda e: e.tensor_tensor(out=facc[:], in0=pfm[:], in1=facc[:], op=ALU.add), ['ad_pfm', 'ad_facc'], ['ad_facc'])
                    if True:
                        s_ = pss
                        for nt in range(4):
                            gj = 2 if nt < 2 else 5
                            col = gj * D + (nt % 2) * 512
                            P.op('pe', lambda e, w=w, col=col, nt=nt, kc=kc, s_=s_: e.matmul(
                                pbc[nt][:], lhsT=rep[:, s_, kc, :], rhs=w[:, col:col + 512],
                                start=(kc == 0), stop=False), [wk, f'ad_rep.{s_}.{kc}'], [f'ad_pbc{nt}'])
                if pss == 0:
                    jmap = [0, 1, 3, 4]
                    for s_ in range(2):
                        for jj, j in enumerate(jmap):
                            P.op('dve', lambda e, s_=s_, jj=jj, j=j, l=l: e.tensor_tensor(
                                out=k.modfm[:, l, s_, jj, :], in0=facc[:, jj * 8:(jj + 1) * 8, s_], in1=modb[:, l, j, :], op=ALU.add),
                                ['ad_facc', 'ad_modb'], [f'modfm.{l}.{s_}.{jj}'])
                        for jj, which in ((1, 0), (3, 1)):
                            P.op('dve', lambda e, s_=s_, jj=jj, which=which, l=l: e.scalar_tensor_tensor(
                                out=k.modfm[:, l, s_, jj, :], in0=k.modfm[:, l, s_, jj, :], scalar=1.0, in1=ng[:, l, which, :],
                                op0=ALU.add, op1=ALU.mult), [f'modfm.{l}.{s_}.{jj}', 'ad_ng'], [f'modfm.{l}.{s_}.{jj}'])
                if True:
                    s_ = pss
                    for nt in range(4):
                        gj = 2 if nt < 2 else 5
                        col = gj * D + (nt % 2) * 512
                        P.op('pe', lambda e, nt=nt, col=col, l=l: e.matmul(
                            pbc[nt][:], lhsT=k.ones_f[0:1, :], rhs=brow[0:1, l, col:col + 512], start=False, stop=True),
                            ['ones_f', 'ad_brow'], [f'ad_pbc{nt}'])
                        P.op('act', lambda e, nt=nt: e.copy(out=gst[:, nt // 2, (nt % 2) * 512:(nt % 2) * 512 + 512], in_=pbc[nt][:]),
                            [f'ad_pbc{nt}'], [f'ad_gst.{nt}'])
                    for j2 in range(2):
                        P.dma('sp', k.gbc_d[l, s_, j2], gst[:, j2, :], [f'ad_gst.{2 * j2}', f'ad_gst.{2 * j2 + 1}'], [], 'ad_gst')
        P.barrier()


def load_w_bf16(k, name, dst, src_ap, nk, ncols, colblk=2048):
    P = k.P
    for kc in range(nk):
        for c0 in range(0, ncols, colblk):
            c1 = min(ncols, c0 + colblk)
            P.dma('pool', dst[:, kc, c0:c1], src_ap[kc * 128:(kc + 1) * 128, c0:c1], [], [f'{name}.{kc}'], f'{name}.{kc}')


def fold_w_bf16(k, st, name, dst, src_ap, nk, gb_dram):
    P = k.P
    stg = [k.sb(f'{name}_stg{i}', [128, D], F32, st) for i in range(2)]
    gb = k.sb(f'{name}_gb', [128, D], F32, st)
    P.dma('sp', gb[:], gb_dram, [], [f'{name}_gb'], f'{name}_gb')
    gb_ap = gb[:]
    for kc in range(nk):
        s_ = stg[kc % 2]
        sk = f'{name}_stg{kc % 2}'
        P.dma('sp', s_[:], src_ap[kc * 128:(kc + 1) * 128, :], [], [sk], sk)
        P.op('dve', lambda e, s_=s_, kc=kc: e.tensor_tensor(out=dst[:, kc, :], in0=s_[:], in1=gb_ap, op=ALU.mult),
             [sk, f'{name}_gb'], [f'{name}.{kc}'])


def modulate_tile(k, B, src_rows, n, l, s_, jsh, hT, hTname):
    P = k.P
    ng = n // 128
    X, Xn = B['xt'], B['xtname']
    ss = B['ss']
    xn = B['xn']
    for g0 in range(0, ng, 2):
        gg = min(2, ng - g0)
        P.dma('sp', X[:, 0:gg, :], src_rows[g0 * 128:(g0 + gg) * 128, :].rearrange("(g p) f -> p g f", p=128), [], [Xn], Xn)
        P.op('dve', lambda e: e.memset(ss[:, 0:2], 0.0), [], [B['ssname']])
        for g in range(gg):
            P.op('act', lambda e, g=g: e.activation(out=B['junk'][:], in_=X[:, g, :], func=AF.Square, accum_out=ss[:, g:g + 1]),
                 [Xn], [B['ssname'], B['junkname']])
        P.op('act', lambda e, gg=gg: e.activation(out=ss[:, 4:4 + gg], in_=ss[:, 0:gg], func=AF.Sqrt, bias=k.cst[:, 0:1], scale=1.0 / D),
             [B['ssname'], 'cst'], [B['ssname']])
        P.op('dve', lambda e, gg=gg: e.reciprocal(out=ss[:, 8:8 + gg], in_=ss[:, 4:4 + gg]), [B['ssname']], [B['ssname']])
        for g in range(gg):
            if g % 2 == 0:
                P.op('dve', lambda e, g=g, g0=g0: e.tensor_scalar(out=xn[:, g0 + g, :], in0=X[:, g, :], scalar1=ss[:, 8 + g:9 + g], scalar2=None, op0=ALU.mult),
                     [Xn, B['ssname']], [f"{B['xnname']}.{g0 + g}"])
            else:
                P.op('act', lambda e, g=g, g0=g0: e.activation(out=xn[:, g0 + g, :], in_=X[:, g, :], func=AF.Identity, scale=ss[:, 8 + g:9 + g], bias=k.cst[:, 2:3]),
                     [Xn, B['ssname'], 'cst'], [f"{B['xnname']}.{g0 + g}"])
    for fc in range(8):
        tp = B['tp'][fc % 2]
        tpn = B['tpname'][fc % 2]
        for g in range(ng):
            P.op('pe', lambda e, g=g, fc=fc, tp=tp: e.transpose(out=tp[:, g * 128:(g + 1) * 128], in_=xn[:, g, fc * 128:(fc + 1) * 128], identity=k.ident[:]),
                 [f"{B['xnname']}.{g}", 'ident'], [tpn])
        P.op('act', lambda e, fc=fc, tp=tp: e.activation(out=hT[:, fc, 0:n], in_=tp[:, 0:n], func=AF.Identity,
                                                          scale=k.modfm[:, l, s_, jsh + 1, fc:fc + 1], bias=k.modfm[:, l, s_, jsh, fc:fc + 1]),
             [tpn], [f'{hTname}.{fc}'])


def mod_bufs(k, st, pfx):
    B = {}
    B['xt'] = k.sb(pfx + 'xt', [128, 2, D], F32, st)
    B['xtname'] = pfx + 'xt'
    B['xn'] = k.sb(pfx + 'xn', [128, 4, D], BF16, st)
    B['xnname'] = pfx + 'xn'
    B['junk'] = k.sb(pfx + 'junk', [128, D], BF16, st)
    B['junkname'] = pfx + 'junk'
    B['ss'] = k.sb(pfx + 'ss', [128, 12], F32, st)
    B['ssname'] = pfx + 'ss'
    B['tp'] = [k.ps(pfx + f'tp{i}', [128, 512], BF16, st) for i in range(2)]
    B['tpname'] = [pfx + f'tp{i}' for i in range(2)]
    return B


def layer0(k, stop_after):
    nc, P, I, S = k.nc, k.P, k.I, k.S
    with ExitStack() as st:
        sb = lambda n, s, d: k.sb(n, s, d, st)
        lp = {}
        for nm, shp in (('lru_cw', [128, 6, 5]), ('pool_fl', [128, 2]), ('lru_cb', [128, 6]), ('lru_ba', [128, 2, 6]), ('lru_bx', [128, 2, 6]),
                        ('lru_lam', [128, 2, 6]), ('pool_scale', [128, 2]), ('pool_invw', [128, 2]), ('pool_corr', [128, 2, 2, 16])):
            lp[nm] = sb('p_' + nm, shp, F32)
            P.dma('sp', lp[nm][:], I[nm], [], ['p_' + nm], 'p_' + nm)
        for nm, shp in (('lru_bdA', [128, 2, 6, 128]), ('lru_bdX', [128, 2, 6, 128]), ('pool_bd', [128, 2, 128])):
            lp[nm] = sb('p_' + nm, shp, BF16)
            P.dma('pool', lp[nm][:], I[nm], [], ['p_' + nm], 'p_' + nm)
        cl = sb('p_cl', [128, 2, 2, 6], F32)
        tmp = sb('p_cltmp', [128, 2, 6], F32)
        P.op('act', lambda e: e.activation(out=tmp[:], in_=lp['lru_lam'][:], func=AF.Exp, scale=-1.0), ['p_lru_lam'], ['p_cltmp'])
        P.op('act', lambda e: e.activation(out=tmp[:], in_=tmp[:], func=AF.Ln, bias=k.cst[:, 1:2], scale=1.0), ['p_cltmp', 'cst'], ['p_cltmp'])
        P.op('dve', lambda e: e.tensor_scalar(out=cl[:, 0, :, :], in0=tmp[:], scalar1=-8.0, scalar2=None, op0=ALU.mult), ['p_cltmp'], ['p_cl'])
        P.op('dve', lambda e: e.tensor_scalar(out=cl[:, 1, :, :], in0=tmp[:], scalar1=-16.0, scalar2=None, op0=ALU.mult), ['p_cltmp'], ['p_cl'])
        lp['cl'] = cl
        stt = sb('p_state', [128, 2, 6], F32)
        P.op('dve', lambda e: e.memset(stt[:], 0.0), [], keys('p_state0', 6) + keys('p_state1', 6))
        lp['state'] = stt
        k.lp = lp
        w_in = sb('w_in0', [128, 8, 1792], BF16)
        load_w_bf16(k, 'w_in0', w_in, I['ab_w_in'], 8, 1792, colblk=1792)
        w_out = sb('w_out0', [128, 8, D], BF16)
        k.w_in0, k.w_out0 = w_in, w_out

        for s_, nm, TT, xsrc in ((1, 'c', LC, I['ctx']), (0, 'l', k.T, I['x'])):
            seg = K()
            seg.nm, seg.T, seg.x, seg.set = nm, TT, xsrc, s_
            seg.z, seg.xa, seg.hf, seg.x05, seg.x1 = S['z_' + nm], S['xa_' + nm], S['hf_' + nm], S['x05_' + nm], S['x1_' + nm]
            seg.tiles = tiles_of(TT)
            with ExitStack() as st2:
                fold_w_bf16(k, st2, 'w_out0', w_out, I['ab_w_out'], 8, k.gbc_d[0, s_, 0])
            P.barrier()
            l0_phaseA(k, seg)
            P.barrier()
            if stop_after == 'l0A' and nm == 'l':
                return
            l0_phaseB(k, seg)
            P.barrier()
            if stop_after == 'l0B' and nm == 'l':
                return
            l0_phaseC(k, seg)
            P.barrier()
            if stop_after == 'l0C' and nm == 'l':
                return
    for s_, nm, TT in ((1, 'c', LC), (0, 'l', k.T)):
        ffn_phase(k, 0, s_, S['x05_' + nm], S['x1_' + nm], TT, final=False)
        P.barrier()


def l0_phaseA(k, seg):
    nc, P = k.nc, k.P
    with ExitStack() as st:
        sb = lambda n, s, d: k.sb(n, s, d, st)
        ps = lambda n, s, d: k.ps(n, s, d, st)
        B = mod_bufs(k, st, 'A_')
        hT = sb('A_hT', [128, 8, 512], BF16)
        zt = [sb(f'A_zt{i}', [128, 14, 512], F32) for i in range(2)]
        zp = [ps(f'A_zp{i}', [128, 512], F32) for i in range(4)]
        zv = seg.z.rearrange("(c p) t -> p c t", p=128)
        P.dma('sp', zv[:, :, 0:PADZ], k.zero[:, 0:14 * PADZ].rearrange("p (c t) -> p c t", c=14), ['zero'], [], 'A_zpad')
        P.dma('sp', zv[:, :, PADZ + seg.T:PADZ + seg.T + PADZ], k.zero[:, 0:14 * PADZ].rearrange("p (c t) -> p c t", c=14), ['zero'], [], 'A_zpad')
        for ti, (s, n) in enumerate(seg.tiles):
            modulate_tile(k, B, seg.x[s:s + n, :], n, 0, seg.set, 0, hT, 'A_hT')
            Z = zt[ti % 2]
            Zn = f'A_zt{ti % 2}'
            for mc in range(14):
                zpp = zp[mc % 4]
                for kc in range(8):
                    P.op('pe', lambda e, mc=mc, kc=kc, zpp=zpp: e.matmul(zpp[:, 0:n], lhsT=k.w_in0[:, kc, mc * 128:(mc + 1) * 128], rhs=hT[:, kc, 0:n],
                                                                         start=(kc == 0), stop=(kc == 7)),
                         [f'w_in0.{kc}', f'A_hT.{kc}'], [f'A_zp{mc % 4}'])
                eng = 'act' if mc % 2 == 0 else 'dve'
                if eng == 'act':
                    P.op('act', lambda e, mc=mc, zpp=zpp: e.copy(out=Z[:, mc, 0:n], in_=zpp[:, 0:n]), [f'A_zp{mc % 4}'], [f'{Zn}.{mc}'])
                else:
                    P.op('dve', lambda e, mc=mc, zpp=zpp: e.tensor_copy(out=Z[:, mc, 0:n], in_=zpp[:, 0:n]), [f'A_zp{mc % 4}'], [f'{Zn}.{mc}'])
            P.dma('sp', zv[:, :, PADZ + s:PADZ + s + n], Z[:, :, 0:n], keys(Zn, 14), [], Zn)


def lru_coeffs(k, C, d, n, xa, xab):
    P, lp = k.P, k.lp
    for c in range(6):
        pr, pi = C['pg'][(2 * c) % 4], C['pg'][(2 * c + 1) % 4]
        prn, pin = C['pgname'][(2 * c) % 4], C['pgname'][(2 * c + 1) % 4]
        P.op('pe', lambda e, c=c, pr=pr: e.matmul(pr[:, 0:n], lhsT=lp['lru_bdA'][:, d, c, :], rhs=xab[:, c, 0:n], start=True, stop=True),
             ['p_lru_bdA', f"{C['xabname']}.{c}"], [prn])
        P.op('pe', lambda e, c=c, pi=pi: e.matmul(pi[:, 0:n], lhsT=lp['lru_bdX'][:, d, c, :], rhs=xab[:, c, 0:n], start=True, stop=True),
             ['p_lru_bdX', f"{C['xabname']}.{c}"], [pin])
        P.op('act', lambda e, c=c, pr=pr: e.activation(out=C['r'][:, c, 0:n], in_=pr[:, 0:n], func=AF.Sigmoid, bias=lp['lru_ba'][:, d, c:c + 1], scale=1.0),
             [prn, 'p_lru_ba'], [f"{C['pfx']}r.{c}"])
        P.op('act', lambda e, c=c, pi=pi: e.activation(out=C['ig'][:, c, 0:n], in_=pi[:, 0:n], func=AF.Sigmoid, bias=lp['lru_bx'][:, d, c:c + 1], scale=1.0),
             [pin, 'p_lru_bx'], [f"{C['pfx']}ig.{c}"])
    for c in range(6):
        P.op('act', lambda e, c=c: e.activation(out=C['a'][:, c, 0:n], in_=C['r'][:, c, 0:n], func=AF.Exp, scale=lp['cl'][:, 0, d, c:c + 1]),
             [f"{C['pfx']}r.{c}", 'p_cl'], [f"{C['pfx']}a.{c}"])
        P.op('act', lambda e, c=c: e.activation(out=C['r'][:, c, 0:n], in_=C['r'][:, c, 0:n], func=AF.Exp, scale=lp['cl'][:, 1, d, c:c + 1]),
             [f"{C['pfx']}r.{c}", 'p_cl'], [f"{C['pfx']}r.{c}"])
    for c in range(6):
        P.op('act', lambda e, c=c: e.activation(out=C['r'][:, c, 0:n], in_=C['r'][:, c, 0:n], func=AF.Sqrt, bias=k.cst[:, 1:2], scale=-1.0),
             [f"{C['pfx']}r.{c}", 'cst'], [f"{C['pfx']}r.{c}"])
        P.op('dve', lambda e, c=c: e.tensor_tensor(out=C['ig'][:, c, 0:n], in0=C['ig'][:, c, 0:n], in1=C['r'][:, c, 0:n], op=ALU.mult),
             [f"{C['pfx']}r.{c}", f"{C['pfx']}ig.{c}"], [f"{C['pfx']}ig.{c}"])
        P.op('pool', lambda e, c=c: e.tensor_tensor(out=C['ig'][:, c, 0:n], in0=C['ig'][:, c, 0:n], in1=xa[:, c, 0:n], op=ALU.mult),
             [f"{C['pfx']}ig.{c}", f"{C['xaname']}.{c}"], [f"{C['pfx']}ig.{c}"])


def coeff_bufs(k, st, pfx, share=None):
    C = {'pfx': pfx}
    for nm in ('r', 'ig', 'a'):
        C[nm] = k.sb(pfx + nm, [128, 6, 512], F32, st)
    if share is None:
        C['pg'] = [k.ps(pfx + f'pg{i}', [128, 512], F32, st) for i in range(4)]
        C['pgname'] = [pfx + f'pg{i}' for i in range(4)]
    else:
        C['pg'], C['pgname'] = share['pg'], share['pgname']
    return C


def l0_phaseB(k, seg):
    P, lp = k.P, k.lp
    with ExitStack() as st:
        sb = lambda n, s, d: k.sb(n, s, d, st)
        sets = []
        for j in range(2):
            Bf = {}
            Bf['zin'] = sb(f'B{j}_zin', [128, 6, 516], F32)
            Bf['xa'] = sb(f'B{j}_xa', [128, 6, 512], F32)
            Bf['xab'] = sb(f'B{j}_xab', [128, 6, 512], BF16)
            Bf['hf'] = sb('B_hf', [128, 6, 512], F32) if j == 0 else sets[0]['hf']
            C = coeff_bufs(k, st, f'B{j}_', share=(sets[0]['C'] if j == 1 else None))
            C['xabname'], C['xaname'] = f'B{j}_xab', f'B{j}_xa'
            Bf['C'] = C
            sets.append(Bf)
        zv = seg.z[0:768, :].rearrange("(c p) t -> p c t", p=128)
        xav = seg.xa.rearrange("(c p) t -> p c t", p=128)
        hfv = seg.hf.rearrange("(c p) t -> p c t", p=128)
        if seg.nm == 'c':
            P.op('dve', lambda e: e.memset(lp['state'][:], 0.0), [], keys('p_state0', 6) + keys('p_state1', 6))
        for ti, (s, n) in enumerate(seg.tiles):
            j = ti % 2
            Bf = sets[j]
            zin, xa, xab, hf, C = Bf['zin'], Bf['xa'], Bf['xab'], Bf['hf'], Bf['C']
            pf = f'B{j}_'
            P.dma('sp', zin[:, :, 0:n + 4], zv[:, :, PADZ + s - 2:PADZ + s + n + 2], [], keys(pf + 'zin', 6), pf + 'zin')
            for c in range(6):
                P.op('act', lambda e, c=c, xa=xa, zin=zin: e.activation(out=xa[:, c, 0:n], in_=zin[:, c, 0:n], func=AF.Identity,
                                                                        scale=lp['lru_cw'][:, c, 0:1], bias=lp['lru_cb'][:, c:c + 1]),
                     [f'{pf}zin.{c}', 'p_lru_cw', 'p_lru_cb'], [f'{pf}xa.{c}'])
                for t in range(1, 5):
                    P.op('dve', lambda e, c=c, t=t, xa=xa, zin=zin: e.scalar_tensor_tensor(out=xa[:, c, 0:n], in0=zin[:, c, t:t + n], scalar=lp['lru_cw'][:, c, t:t + 1],
                                                                                           in1=xa[:, c, 0:n], op0=ALU.mult, op1=ALU.add),
                         [f'{pf}zin.{c}', f'{pf}xa.{c}'], [f'{pf}xa.{c}'])
                P.op('act', lambda e, c=c, xa=xa, xab=xab: e.copy(out=xab[:, c, 0:n], in_=xa[:, c, 0:n]), [f'{pf}xa.{c}'], [f'{pf}xab.{c}'])
            P.dma('sp', xav[:, :, s:s + n], xa[:, :, 0:n], keys(pf + 'xa', 6), [], pf + 'xa')
            lru_coeffs(k, C, 0, n, xa, xab)
            for c in range(6):
                P.op('dve', lambda e, c=c, hf=hf, C=C: e.tensor_tensor_scan(out=hf[:, c, 0:n], data0=C['a'][:, c, 0:n], data1=C['ig'][:, c, 0:n],
                                                                          initial=lp['state'][:, 0, c:c + 1], op0=ALU.mult, op1=ALU.add),
                     [f'{pf}a.{c}', f'{pf}ig.{c}', f'p_state0.{c}'], [f'B_hf.{c}'])
                P.op('dve', lambda e, c=c, hf=hf: e.tensor_copy(out=lp['state'][:, 0, c:c + 1], in_=hf[:, c, n - 1:n]), [f'B_hf.{c}'], [f'p_state0.{c}'])
            P.dma('sp', hfv[:, :, s:s + n], hf[:, :, 0:n], keys('B_hf', 6), [], 'B_hf')


def l0_phaseC(k, seg):
    P, lp, I = k.P, k.lp, k.I
    with ExitStack() as st:
        sb = lambda n, s, d: k.sb(n, s, d, st)
        ps = lambda n, s, d: k.ps(n, s, d, st)
        xa = sb('C_xa', [128, 6, 512], F32)
        xab = sb('C_xab', [128, 6, 512], BF16)
        hb = sb('C_hb', [128, 6, 512], F32)
        hf = sb('C_hf', [128, 6, 512], F32)
        ga = sb('C_ga', [128, 6, 512], F32)
        yT = sb('C_yT', [128, 8, 512], BF16)
        zb = sb('C_zb', [128, 2, 528], F32)
        p2 = sb('C_p2', [128, 528], F32)
        p4 = sb('C_p4', [128, 528], F32)
        p8 = sb('C_p8', [128, 528], F32)
        Qw = sb('C_Qw', [128, 516], F32)
        Ssum = sb('C_S', [128, 512], F32)
        dd = sb('C_dd', [128, 2, 512], BF16)
        xt = sb('C_xt', [128, 4, D], F32)
        xo = sb('C_xo', [128, 4, D], F32)
        C = coeff_bufs(k, st, 'C_')
        C['xabname'], C['xaname'] = 'C_xab', 'C_xa'
        po = [ps(f'C_po{i}', [128, 512], F32) for i in range(4)]
        zg = seg.z[768:1536, :].rearrange("(c p) t -> p c t", p=128)
        zbv = seg.z[1536:1792, :].rearrange("(c p) t -> p c t", p=128)
        xav = seg.xa.rearrange("(c p) t -> p c t", p=128)
        hfv = seg.hf.rearrange("(c p) t -> p c t", p=128)
        if seg.nm == 'c':
            P.op('dve', lambda e: e.memset(lp['state'][:, 1, :], 0.0), [], keys('p_state1', 6))
        nt_ = len(seg.tiles)
        for ti in range(nt_ - 1, -1, -1):
            s, n = seg.tiles[ti]
            ng = n // 128
            P.dma('sp', xa[:, :, 0:n], xav[:, :, s:s + n], [], keys('C_xa', 6), 'C_xa')
            P.dma('sp', hf[:, :, 0:n], hfv[:, :, s:s + n], [], keys('C_hf', 6), 'C_hf')
            P.dma('sp', ga[:, :, 0:n], zg[:, :, PADZ + s:PADZ + s + n], [], keys('C_ga', 6), 'C_ga')
            P.dma('sp', zb[:, :, 0:n + 16], zbv[:, :, PADZ + s - 8:PADZ + s + n + 8], [], keys('C_zb', 2), 'C_zb')
            P.dma('sp', xt[:, 0:ng, :], seg.x[s:s + n, :].rearrange("(g p) f -> p g f", p=128), [], ['C_xt'], 'C_xt')
            for c in range(6):
                P.op('act', lambda e, c=c: e.copy(out=xab[:, c, 0:n], in_=xa[:, c, 0:n]), [f'C_xa.{c}'], [f'C_xab.{c}'])
            lru_coeffs(k, C, 1, n, xa, xab)
            for c in range(6):
                P.op('dve', lambda e, c=c: e.tensor_tensor_scan(out=rev(hb[:, c, 0:n]), data0=rev(C['a'][:, c, 0:n]), data1=rev(C['ig'][:, c, 0:n]),
                                                                initial=lp['state'][:, 1, c:c + 1], op0=ALU.mult, op1=ALU.add),
                     [f'C_a.{c}', f'C_ig.{c}', f'p_state1.{c}'], [f'C_hb.{c}'])
                P.op('dve', lambda e, c=c: e.tensor_copy(out=lp['state'][:, 1, c:c + 1], in_=hb[:, c, 0:1]), [f'C_hb.{c}'], [f'p_state1.{c}'])
                P.op('act', lambda e, c=c: e.activation(out=ga[:, c, 0:n], in_=ga[:, c, 0:n], func=AF.Gelu_apprx_tanh), [f'C_ga.{c}'], [f'C_ga.{c}'])
                P.op('pool', lambda e, c=c: e.tensor_tensor(out=hb[:, c, 0:n], in0=hb[:, c, 0:n], in1=hf[:, c, 0:n], op=ALU.add),
                     [f'C_hb.{c}', f'C_hf.{c}'], [f'C_hb.{c}'])
                P.op('dve', lambda e, c=c: e.tensor_tensor(out=yT[:, c, 0:n], in0=hb[:, c, 0:n], in1=ga[:, c, 0:n], op=ALU.mult),
                     [f'C_hb.{c}', f'C_ga.{c}'], [f'C_yT.{c}'])
            W = n + 16
            n1 = n + 1
            for ch in range(2):
                zc = zb[:, ch, :]
                P.op('dve', lambda e, zc=zc: e.tensor_tensor(out=p2[:, 0:W - 1], in0=zc[:, 0:W - 1], in1=zc[:, 1:W], op=ALU.add),
                     [f'C_zb.{ch}'], ['C_p2'])
                if ch == 0:
                    P.op('dve', lambda e: e.tensor_copy(out=Qw[0:64, 0:n1], in_=p2[0:64, 7:7 + n1]), ['C_p2'], ['C_Qw'])
                    P.op('dve', lambda e: e.tensor_tensor(out=Qw[64:128, 0:n1], in0=p2[64:128, 6:6 + n1], in1=p2[64:128, 8:8 + n1], op=ALU.add),
                         ['C_p2'], ['C_Qw'])
                else:
                    P.op('dve', lambda e: e.tensor_tensor(out=p4[:, 0:W - 3], in0=p2[:, 0:W - 3], in1=p2[:, 2:W - 1], op=ALU.add), ['C_p2'], ['C_p4'])
                    P.op('dve', lambda e: e.tensor_tensor(out=Qw[0:64, 0:n1], in0=p4[0:64, 4:4 + n1], in1=p4[0:64, 8:8 + n1], op=ALU.add),
                         ['C_p4'], ['C_Qw'])
                    P.op('dve', lambda e: e.tensor_tensor(out=p8[64:128, 0:W - 7], in0=p4[64:128, 0:W - 7], in1=p4[64:128, 4:W - 3], op=ALU.add),
                         ['C_p4'], ['C_p8'])
                    P.op('dve', lambda e: e.tensor_tensor(out=Qw[64:128, 0:n1], in0=p8[64:128, 0:n1], in1=p8[64:128, 8:8 + n1], op=ALU.add),
                         ['C_p8'], ['C_Qw'])
                P.op('dve', lambda e: e.tensor_scalar(out=Ssum[:, 0:n], in0=Qw[:, 0:n], scalar1=lp['pool_fl'][:, 0:1], scalar2=None, op0=ALU.mult),
                     ['C_Qw', 'p_pool_fl'], ['C_S'])
                P.op('dve', lambda e: e.scalar_tensor_tensor(out=Ssum[:, 0:n], in0=Qw[:, 1:n1], scalar=lp['pool_fl'][:, 1:2], in1=Ssum[:, 0:n],
                                                             op0=ALU.mult, op1=ALU.add), ['C_Qw', 'C_S', 'p_pool_fl'], ['C_S'])
                if ti == 0:
                    P.op('dve', lambda e, ch=ch: e.tensor_tensor(out=Ssum[:, 0:16], in0=Ssum[:, 0:16], in1=lp['pool_corr'][:, ch, 0, :], op=ALU.mult),
                         ['C_S', 'p_pool_corr'], ['C_S'])
                if ti == nt_ - 1:
                    P.op('dve', lambda e, ch=ch: e.tensor_tensor(out=Ssum[:, n - 16:n], in0=Ssum[:, n - 16:n], in1=lp['pool_corr'][:, ch, 1, :], op=ALU.mult),
                         ['C_S', 'p_pool_corr'], ['C_S'])
                P.op('dve', lambda e, ch=ch, zc=zc: e.scalar_tensor_tensor(out=dd[:, ch, 0:n], in0=Ssum[:, 0:n], scalar=lp['pool_invw'][:, ch:ch + 1],
                                                                          in1=zc[:, 8:8 + n], op0=ALU.mult, op1=ALU.subtract),
                     ['C_S', f'C_zb.{ch}', 'p_pool_invw'], [f'C_dd.{ch}'])
                pp = po[ch]
                P.op('pe', lambda e, ch=ch, pp=pp: e.matmul(pp[:, 0:n], lhsT=lp['pool_bd'][:, ch, :], rhs=dd[:, ch, 0:n], start=True, stop=True),
                     ['p_pool_bd', f'C_dd.{ch}'], [f'C_po{ch}'])
                P.op('act', lambda e, ch=ch, pp=pp: e.activation(out=yT[:, 6 + ch, 0:n], in_=pp[:, 0:n], func=AF.Identity, scale=lp['pool_scale'][:, ch:ch + 1], bias=k.cst[:, 2:3]),
                     [f'C_po{ch}', 'p_pool_scale', 'cst'], [f'C_yT.{6 + ch}'])
            for g in range(ng):
                for nt in range(2):
                    pp = po[(2 * g + nt) % 4]
                    ppn = f'C_po{(2 * g + nt) % 4}'
                    for kc in range(8):
                        P.op('pe', lambda e, g=g, nt=nt, kc=kc, pp=pp: e.matmul(pp[:, :], lhsT=yT[:, kc, g * 128:(g + 1) * 128], rhs=k.w_out0[:, kc, nt * 512:(nt + 1) * 512],
                                                                               start=(kc == 0), stop=(kc == 7)),
                             [f'C_yT.{kc}', f'w_out0.{kc}'], [ppn])
                    P.op('dve', lambda e, g=g, nt=nt, pp=pp: e.tensor_tensor(out=xo[:, g, nt * 512:(nt + 1) * 512], in0=pp[:, :], in1=xt[:, g, nt * 512:(nt + 1) * 512], op=ALU.add),
                         [ppn, 'C_xt'], [f'C_xo.{g}'])
            P.dma('sp', seg.x05[1 + s:1 + s + n, :].rearrange("(g p) f -> p g f", p=128), xo[:, 0:ng, :], keys('C_xo', 4)[0:ng], [], 'C_xo')


def ffn_phase(k, l, s_, src, dst, TT, final, flush=True, out_rows=None):
    nc, P, I = k.nc, k.P, k.I
    tiles = tiles_of(TT)
    with ExitStack() as st:
        sb = lambda n, s, d: k.sb(n, s, d, st)
        ps = lambda n, s, d: k.ps(n, s, d, st)
        w_up = sb('F_wup', [128, 8, 2 * DFF], BF16)
        load_w_bf16(k, 'F_wup', w_up, I['ffn_w_up'][l], 8, 2 * DFF)
        w_dn = sb('F_wdn', [128, NFC, D], BF16)
        with ExitStack() as st2:
            fold_w_bf16(k, st2, 'F_wdn', w_dn, I['ffn_w_down'][l], NFC, k.gbc_d[l, s_, 1])
            P.barrier()
        cw = sb('F_cw', [128, 2 * NFC, 3], F32)
        cb = sb('F_cb', [128, 2 * NFC], F32)
        P.dma('sp', cw[:], I['ffn_cw'][:, l, :, :], [], ['F_cw'], 'F_cw')
        P.dma('sp', cb[:], I['ffn_cb'][:, l, :], [], ['F_cb'], 'F_cb')
        B = mod_bufs(k, st, 'F_')
        hT = sb('F_hT', [128, 8, 512], BF16)
        gT = sb('F_gT', [128, NFC, 512], BF16)
        prevu = [sb(f'F_prevu{i}', [128, 2 * NFC, 2], F32) for i in range(2)]
        P.op('dve', lambda e: e.memset(prevu[0][:], 0.0), [], keys('F_prevu0', 2 * NFC))
        acc = [sb(f'F_acc{i}', [128, 512], F32) for i in range(4)]
        corr = sb('F_corr', [128, 2 * NFC, 2], F32)
        ctmp = sb('F_ctmp', [128, 2 * NFC], F32)
        xs = sb('F_xs', [128, 2, D], F32)
        xo = xs
        pu = [ps(f'F_pu{i}', [128, 512], F32) for i in range(4)]
        pd = [ps(f'F_pd{i}', [128, 512], F32) for i in range(2)]
        if final:
            fg = sb('F_fg', [128, D], F32)
            P.dma('sp', fg[:], I['final_g_bc'], [], ['F_fg'], 'F_fg')
            fss = sb('F_fss', [128, 12], F32)
            fjunk = B['junk']

        if os.environ.get('DBG_SBUF'):
            print('FFN sbuf remaining', nc.sbuf_bytes_remaining, 'final', final)
        def conv_gate(n, zero_u, ti):
            pin, pout = prevu[ti % 2], prevu[(ti + 1) % 2]
            pinn, poutn = f'F_prevu{ti % 2}', f'F_prevu{(ti + 1) % 2}'
            allin = keys(pinn, 2 * NFC)
            P.op('dve', lambda e: e.tensor_tensor(out=corr[:, :, 0], in0=cw[:, :, 0], in1=pin[:, :, 0], op=ALU.mult), allin + ['F_cw'], ['F_corr'])
            P.op('dve', lambda e: e.tensor_tensor(out=ctmp[:, :], in0=cw[:, :, 1], in1=pin[:, :, 1], op=ALU.mult), allin + ['F_cw'], ['F_ctmp'])
            P.op('dve', lambda e: e.tensor_tensor(out=corr[:, :, 0], in0=corr[:, :, 0], in1=ctmp[:, :], op=ALU.add), ['F_corr', 'F_ctmp'], ['F_corr'])
            P.op('dve', lambda e: e.tensor_tensor(out=corr[:, :, 1], in0=cw[:, :, 0], in1=pin[:, :, 1], op=ALU.mult), allin + ['F_cw', 'F_corr'], ['F_corr'])
            for c in range(NFC):
                q = c % 2
                AA = [acc[2 * q], acc[2 * q + 1]]
                AN = [f'F_acc{2 * q}', f'F_acc{2 * q + 1}']
                PP = [pu[2 * q], pu[2 * q + 1]]
                PN = [f'F_pu{2 * q}', f'F_pu{2 * q + 1}']
                CC = [c, NFC + c]
                if not zero_u:
                    for vi in range(2):
                        for kc in range(8):
                            P.op('pe', lambda e, kc=kc, cc=CC[vi], pp=PP[vi]: e.matmul(pp[:, 0:n], lhsT=w_up[:, kc, cc * 128:(cc + 1) * 128], rhs=hT[:, kc, 0:n],
                                                                                      start=(kc == 0), stop=(kc == 7)),
                                 [f'F_wup.{kc}', f'F_hT.{kc}'], [PN[vi]])
                    for vi in range(2):
                        P.op('act', lambda e, A_=AA[vi], pp=PP[vi], cc=CC[vi]: e.activation(out=A_[:, 0:n], in_=pp[:, 0:n], func=AF.Identity,
                                                                                          scale=cw[:, cc, 2:3], bias=cb[:, cc:cc + 1]),
                             [PN[vi], 'F_cw', 'F_cb'], [AN[vi]])
                    for vi in range(2):
                        if os.environ.get('DBG_SKIP_SAVE'):
                            continue
                        P.op('dve', lambda e, pp=PP[vi], cc=CC[vi]: e.tensor_copy(out=pout[:, cc, :], in_=pp[:, n - 2:n]), [PN[vi]], [f'{poutn}.{CC[vi]}'])
                    for vi in range(2):
                        P.op('dve', lambda e, A_=AA[vi], pp=PP[vi], cc=CC[vi]: e.scalar_tensor_tensor(out=A_[:, 1:n], in0=pp[:, 0:n - 1], scalar=cw[:, cc, 1:2], in1=A_[:, 1:n],
                                                                                                    op0=ALU.mult, op1=ALU.add), [PN[vi], AN[vi]], [AN[vi]])
                    for vi in range(2):
                        P.op('dve', lambda e, A_=AA[vi], pp=PP[vi], cc=CC[vi]: e.scalar_tensor_tensor(out=A_[:, 2:n], in0=pp[:, 0:n - 2], scalar=cw[:, cc, 0:1], in1=A_[:, 2:n],
                                                                                                    op0=ALU.mult, op1=ALU.add), [PN[vi], AN[vi]], [AN[vi]])
                else:
                    for vi in range(2):
                        P.op('act', lambda e, A_=AA[vi], cc=CC[vi]: e.activation(out=A_[:, 0:n], in_=k.zero[:, 0:n], func=AF.Identity,
                                                                                scale=cw[:, cc, 2:3], bias=cb[:, cc:cc + 1]),
                             ['zero', 'F_cw', 'F_cb'], [AN[vi]])
                for vi in range(2):
                    P.op('pool', lambda e, A_=AA[vi], cc=CC[vi]: e.tensor_tensor(out=A_[:, 0:2], in0=A_[:, 0:2], in1=corr[:, cc, :], op=ALU.add),
                         ['F_corr', AN[vi]], [AN[vi]])
                P.op('act', lambda e, A_=AA[1]: e.activation(out=A_[:, 0:n], in_=A_[:, 0:n], func=AF.Silu), [AN[1]], [AN[1]])
                P.op('pool', lambda e, c=c, A0=AA[0], A1=AA[1]: e.tensor_tensor(out=gT[:, c, 0:n], in0=A0[:, 0:n], in1=A1[:, 0:n], op=ALU.mult),
                     [AN[0], AN[1]], [f'F_gT.{c}'])

        def down_res(tok0, n, nrows_last=128):
            ng = n // 128
            nr = lambda g: (nrows_last if g == ng - 1 else 128)
            for g_ in range(ng):
                g = g_ % 2
                r0 = 1 + tok0 + g_ * 128
                P.dma('sp', xs[0:nr(g_), g, :], src[r0:r0 + nr(g_), :], [], [f'F_xs.{g}'], f'F_xs{g}')
                for nt in range(2):
                    pp = pd[nt]
                    for kc in range(NFC):
                        P.op('pe', lambda e, g_=g_, nt=nt, kc=kc, pp=pp: e.matmul(pp[:, :], lhsT=gT[:, kc, g_ * 128:(g_ + 1) * 128], rhs=w_dn[:, kc, nt * 512:(nt + 1) * 512],
                                                                               start=(kc == 0), stop=(kc == NFC - 1)),
                             [f'F_gT.{kc}', f'F_wdn.{kc}'], [f'F_pd{nt}'])
                    P.op('dve', lambda e, g=g, g_=g_, nt=nt, pp=pp: e.tensor_tensor(out=xo[0:nr(g_), g, nt * 512:(nt + 1) * 512], in0=pp[0:nr(g_), :],
                                                                            in1=xs[0:nr(g_), g, nt * 512:(nt + 1) * 512], op=ALU.add),
                         [f'F_pd{nt}', f'F_xs.{g}'], [f'F_xs.{g}'])
                if final:
                    P.op('dve', lambda e: e.memset(fss[:, 0:1], 0.0), [], ['F_fss'])
                    P.op('act', lambda e, g=g: e.activation(out=fjunk[:], in_=xo[:, g, :], func=AF.Square, accum_out=fss[:, 0:1]),
                         [f'F_xs.{g}'], ['F_fss', 'F_junk'])
                    P.op('act', lambda e: e.activation(out=fss[:, 1:2], in_=fss[:, 0:1], func=AF.Sqrt, bias=k.cst[:, 0:1], scale=1.0 / D),
                         ['F_fss', 'cst'], ['F_fss'])
                    P.op('dve', lambda e: e.reciprocal(out=fss[:, 2:3], in_=fss[:, 1:2]), ['F_fss'], ['F_fss'])
                    P.op('dve', lambda e, g=g: e.scalar_tensor_tensor(out=xo[:, g, :], in0=xo[:, g, :], scalar=fss[:, 2:3], in1=fg[:],
                                                                     op0=ALU.mult, op1=ALU.mult), [f'F_xs.{g}', 'F_fss', 'F_fg'], [f'F_xs.{g}'])
                t0 = tok0 + g_ * 128
                if final:
                    lo = max(t0, 0)
                    hi = min(t0 + nr(g_), TT if out_rows is None else out_rows)
                    if hi > lo:
                        P.dma('sp', k.out[lo:hi, :], xo[lo - t0:hi - t0, g, :], [f'F_xs.{g}'], [], f'F_xs{g}')
                else:
                    P.dma('sp', dst[1 + t0:1 + t0 + nr(g_), :], xo[0:nr(g_), g, :], [f'F_xs.{g}'], [], f'F_xs{g}')

        for ti, (s, n) in enumerate(tiles):
            modulate_tile(k, B, src[1 + s:1 + s + n, :], n, l, s_, 2, hT, 'F_hT')
            conv_gate(n, False, ti)
            down_res(s - 1, n)
        if flush:
            conv_gate(128, True, len(tiles))
            down_res(TT - 1, 128, nrows_last=1)


def layer1(k, stop_after):
    nc, P, I, S = k.nc, k.P, k.I, k.S
    T = k.T
    TK = T + LC
    NKB = TK // 128
    TL = k.TL
    yc_d = nc.dram_tensor('yc_d', [768, T], BF16, kind="Internal").ap()
    with ExitStack() as st:
      if not os.environ.get('DBG_ONLY_F1'):
          sb = lambda n, s, d: k.sb(n, s, d, st)
          ps = lambda n, s, d: k.ps(n, s, d, st)
          w_in = sb('w_in1', [128, 8, 2816], BF16)
          load_w_bf16(k, 'w_in1', w_in, I['cd_w_in'], 8, 2816, colblk=1408)
          w_sw = sb('w_sw1', [128, 8, 1536], BF16)
          load_w_bf16(k, 'w_sw1', w_sw, I['cd_w_sw'], 8, 1536, colblk=1536)
          B = mod_bufs(k, st, 'E_')
          hT = sb('E_hT', [128, 8, 512], BF16)
          qk = sb('E_qk', [128, 12, 512], BF16)
          vt = sb('E_vt', [128, 4, 768], BF16)
          gl = sb('E_gl', [128, 2, 512], F32)
          rc = sb('E_rc', [128, 512], F32)
          rs = sb('E_rs', [128, 512], F32)
          t1 = sb('E_t1', [128, 512], F32)
          t2 = sb('E_t2', [128, 512], F32)
          sg = sb('E_sg', [128, 512], F32)
          pa = [ps(f'E_pa{i}', [128, 512], F32) for i in range(2)]
          pb = [ps(f'E_pb{i}', [128, 512], F32) for i in range(2)]
          glv = S['gl'].rearrange("(c p) t -> p c t", p=128)
          P.dma('sp', glv[:, :, 0:PADZ], k.zero[:, 0:2 * PADZ].rearrange("p (c t) -> p c t", c=2), ['zero'], [], 'E_glpad')
          P.dma('sp', glv[:, :, PADZ + T:PADZ + T + PADZ], k.zero[:, 0:2 * PADZ].rearrange("p (c t) -> p c t", c=2), ['zero'], [], 'E_glpad')
          qTv = S['qT'].rearrange("(c p) t -> p c t", p=128)
          kTv = S['kT'].rearrange("(c p) t -> p c t", p=128)

          def proj_plain(cols0, nch, dst, dstname, d0, n):
              for c in range(nch):
                  pp = pa[c % 2]
                  for kc in range(8):
                      P.op('pe', lambda e, c=c, kc=kc, pp=pp: e.matmul(pp[:, 0:n], lhsT=w_in[:, kc, cols0 + c * 128:cols0 + (c + 1) * 128], rhs=hT[:, kc, 0:n],
                                                                      start=(kc == 0), stop=(kc == 7)), [f'w_in1.{kc}', f'E_hT.{kc}'], [f'E_pa{c % 2}'])
                  P.op('act', lambda e, c=c, pp=pp: e.copy(out=dst[:, d0 + c, 0:n], in_=pp[:, 0:n]), [f'E_pa{c % 2}'], [f'{dstname}.{d0 + c}'])

          def proj_v(n):
              ng = n // 128
              for g in range(ng):
                  for (c0, cn, pp, ppn) in ((0, 512, pa[g % 2], f'E_pa{g % 2}'), (512, 256, pb[g % 2], f'E_pb{g % 2}')):
                      for kc in range(8):
                          P.op('pe', lambda e, g=g, kc=kc, pp=pp, c0=c0, cn=cn: e.matmul(pp[:, 0:cn], lhsT=hT[:, kc, g * 128:(g + 1) * 128],
                                                                                         rhs=w_in[:, kc, 1536 + c0:1536 + c0 + cn], start=(kc == 0), stop=(kc == 7)),
                               [f'w_in1.{kc}', f'E_hT.{kc}'], [ppn])
                      P.op('dve', lambda e, g=g, pp=pp, c0=c0, cn=cn: e.tensor_copy(out=vt[:, g, c0:c0 + cn], in_=pp[:, 0:cn]), [ppn], [f'E_vt.{g}'])

          n = LC
          modulate_tile(k, B, S['x1_c'][1:1 + LC, :], n, 1, 1, 0, hT, 'E_hT')
          proj_plain(768, 6, qk, 'E_qk', 6, n)
          P.dma('sp', kTv[:, :, T:T + n], qk[:, 6:12, 0:n], keys('E_qk', 12)[6:12], [], 'E_qk')
          proj_v(n)
          P.dma('sp', S['v'][T:T + n, :].rearrange("(g p) f -> p g f", p=128), vt[:, 0:n // 128, :], keys('E_vt', 4), [], 'E_vt')
          for ti, (s, n) in enumerate(tiles_of(T)):
              modulate_tile(k, B, S['x1_l'][1 + s:1 + s + n, :], n, 1, 0, 0, hT, 'E_hT')
              P.dma('sp', rc[:, 0:n], I['rope_c'][:, s:s + n], [], ['E_rc'], 'E_rc')
              P.dma('sp', rs[:, 0:n], I['rope_s'][:, s:s + n], [], ['E_rs'], 'E_rs')
              need_q = s < TL + 16
              for c in (range(12) if need_q else range(6, 12)):
                  pp, pq = pa[c % 2], pb[c % 2]
                  for kc in range(8):
                      P.op('pe', lambda e, c=c, kc=kc, pp=pp: e.matmul(pp[:, 0:n], lhsT=w_in[:, kc, c * 128:(c + 1) * 128], rhs=hT[:, kc, 0:n],
                                                                      start=(kc == 0), stop=(kc == 7)), [f'w_in1.{kc}', f'E_hT.{kc}'], [f'E_pa{c % 2}'])
                  for kc in range(8):
                      P.op('pe', lambda e, c=c, kc=kc, pq=pq: e.matmul(pq[:, 0:n], lhsT=w_sw[:, kc, c * 128:(c + 1) * 128], rhs=hT[:, kc, 0:n],
                                                                      start=(kc == 0), stop=(kc == 7)), [f'w_sw1.{kc}', f'E_hT.{kc}'], [f'E_pb{c % 2}'])
                  P.op('dve', lambda e, pp=pp: e.tensor_tensor(out=t1[:, 0:n], in0=pp[:, 0:n], in1=rc[:, 0:n], op=ALU.mult), [f'E_pa{c % 2}', 'E_rc'], ['E_t1'])
                  P.op('dve', lambda e, pq=pq: e.tensor_tensor(out=t2[:, 0:n], in0=pq[:, 0:n], in1=rs[:, 0:n], op=ALU.mult), [f'E_pb{c % 2}', 'E_rs'], ['E_t2'])
                  P.op('pool', lambda e, c=c: e.tensor_tensor(out=qk[:, c, 0:n], in0=t1[:, 0:n], in1=t2[:, 0:n], op=ALU.add), ['E_t1', 'E_t2'], [f'E_qk.{c}'])
              if need_q:
                  P.dma('sp', qTv[:, :, s:s + n], qk[:, 0:6, 0:n], keys('E_qk', 12)[0:6], [], 'E_q')
              P.dma('sp', kTv[:, :, s:s + n], qk[:, 6:12, 0:n], keys('E_qk', 12)[6:12], [], 'E_qk')
              proj_v(n)
              P.dma('sp', S['v'][s:s + n, :].rearrange("(g p) f -> p g f", p=128), vt[:, 0:n // 128, :], keys('E_vt', 4), [], 'E_vt')
              for c in (range(2) if need_q else []):
                  pp, pq = pa[c % 2], pb[c % 2]
                  for (pz, pzn, cols) in ((pp, f'E_pa{c % 2}', 2304 + c * 128), (pq, f'E_pb{c % 2}', 2304 + 256 + c * 128)):
                      for kc in range(8):
                          P.op('pe', lambda e, kc=kc, pz=pz, cols=cols: e.matmul(pz[:, 0:n], lhsT=w_in[:, kc, cols:cols + 128], rhs=hT[:, kc, 0:n],
                                                                                start=(kc == 0), stop=(kc == 7)), [f'w_in1.{kc}', f'E_hT.{kc}'], [pzn])
                  P.op('act', lambda e, pq=pq: e.activation(out=sg[:, 0:n], in_=pq[:, 0:n], func=AF.Sigmoid), [f'E_pb{c % 2}'], ['E_sg'])
                  P.op('dve', lambda e, c=c, pp=pp: e.tensor_tensor(out=gl[:, c, 0:n], in0=pp[:, 0:n], in1=sg[:, 0:n], op=ALU.mult), [f'E_pa{c % 2}', 'E_sg'], [f'E_gl.{c}'])
              if need_q:
                  P.dma('sp', glv[:, :, PADZ + s:PADZ + s + n], gl[:, :, 0:n], keys('E_gl', 2), [], 'E_gl')
    P.barrier()
    if stop_after == 'l1E':
        return

    with ExitStack() as st:
        sb = lambda n, s, d: k.sb(n, s, d, st)
        ps = lambda n, s, d: k.ps(n, s, d, st)
        dl = sb('G_dl', [1, 4, 64], F32)
        P.dma('sp', dl[:], I['diff_l'], [], ['G_dl'], 'G_dl')
        sm = sb('G_sm', [1, 8], F32)
        pr_ = sb('G_pr', [1, 2, 64], F32)
        P.op('dve', lambda e: e.tensor_tensor(out=pr_[:, 0, :], in0=dl[:, 0, :], in1=dl[:, 1, :], op=ALU.mult), ['G_dl'], ['G_pr'])
        P.op('dve', lambda e: e.tensor_tensor(out=pr_[:, 1, :], in0=dl[:, 2, :], in1=dl[:, 3, :], op=ALU.mult), ['G_dl'], ['G_pr'])
        P.op('dve', lambda e: e.reduce_sum(out=sm[:, 0:1], in_=pr_[:, 0, :], axis=AX.X), ['G_pr'], ['G_sm'])
        P.op('dve', lambda e: e.reduce_sum(out=sm[:, 1:2], in_=pr_[:, 1, :], axis=AX.X), ['G_pr'], ['G_sm'])
        P.op('act', lambda e: e.activation(out=sm[:, 2:4], in_=sm[:, 0:2], func=AF.Exp), ['G_sm'], ['G_sm'])
        P.op('dve', lambda e: e.tensor_tensor(out=sm[:, 4:5], in0=sm[:, 3:4], in1=sm[:, 2:3], op=ALU.subtract), ['G_sm'], ['G_sm'])
        P.op('dve', lambda e: e.tensor_scalar(out=sm[:, 5:6], in0=sm[:, 4:5], scalar1=-LAMBDA_INIT1, scalar2=None, op0=ALU.add), ['G_sm'], ['G_sm'])
        neglam = sb('G_neglam', [128, 2], F32)
        gsub = sb('G_gsub', [128, 2], F32)
        P.dma('sp', gsub[:, 0:1], I['subln_g'], [], ['G_gsub'], 'G_gsub')
        P.op('dve', lambda e: e.tensor_scalar(out=gsub[:, 1:2], in0=gsub[:, 0:1], scalar1=(1.0 - LAMBDA_INIT1), scalar2=None, op0=ALU.mult), ['G_gsub'], ['G_gsub'])
        pS = [ps(f'G_pS{i}', [128, 2, 512], F32) for i in range(2)]
        po = [ps(f'G_po{i}', [128, 512], F32) for i in range(2)]
        pl = ps('G_pl', [128, 2, 512], F32)
        P.op('pe', lambda e: e.matmul(pl[:, 0, 0:1], lhsT=k.ones_f[0:1, :], rhs=sm[0:1, 5:6], start=True, stop=True), ['ones_f', 'G_sm'], ['G_pl'])
        P.op('dve', lambda e: e.tensor_copy(out=neglam[:, 0:1], in_=pl[:, 0, 0:1]), ['G_pl'], ['G_neglam'])
        kh = [sb(f'G_kh{j}', [128, TK], BF16) for j in range(2)]
        vh = [sb(f'G_vh{j}', [128, NKB, 128], BF16) for j in range(2)]
        qh = [sb(f'G_qh{j}', [128, 512], BF16) for j in range(2)]
        pT = [sb(f'G_pT{i}', [128, 2, 512], BF16) for i in range(4)]
        accs = [sb(f'G_acc{i}', [128, 2, 512], F32) for i in range(2)]
        rl = sb('G_rl', [128, 2, 512], F32)
        o1 = sb('G_o1', [128, 512], F32)
        o2 = sb('G_o2', [128, 512], F32)
        sq = sb('G_sq', [128, 512], F32)
        ych = sb('G_ych', [128, 512], BF16)
        vv = S['v'].rearrange("(kb p) f -> p kb f", p=128)
        qtiles = tiles_of(TL)

        def load_head(h):
            j = h % 2
            P.dma('sp', kh[j][:, :], S['kT'][h * 128:h * 128 + 128, :], [], [f'G_kh{j}'], f'G_kh{j}')
            for b0 in range(0, NKB, 16):
                b1 = min(NKB, b0 + 16)
                P.dma('sp', vh[j][:, b0:b1, :], vv[:, b0:b1, h * 128:(h + 1) * 128], [], [f'G_vh{j}'], f'G_vh{j}')

        def load_q(h, ti):
            s, n = qtiles[ti]
            gi = (h * len(qtiles) + ti) % 2
            P.dma('sp', qh[gi][:, 0:n], S['qT'][h * 128:(h + 1) * 128, s:s + n], [], [f'G_qh{gi}'], f'G_qh{gi}')

        load_head(0)
        load_q(0, 0)
        for h in range(6):
            hj = h % 2
            for ti, (s, n) in enumerate(qtiles):
                gi = (h * len(qtiles) + ti) % 2
                qcur = qh[gi]
                qn_ = f'G_qh{gi}'
                if ti + 1 < len(qtiles):
                    load_q(h, ti + 1)
                elif h + 1 < 6:
                    load_q(h + 1, 0)
                if ti == 0 and h + 1 < 6:
                    load_head(h + 1)

                def emit_qk(kb):
                    pp = pS[kb % 2]
                    for comp in range(2):
                        P.op('pe', lambda e, comp=comp, kb=kb, pp=pp: e.matmul(pp[:, comp, 0:n], lhsT=kh[hj][64 * comp:64 * comp + 64, kb * 128:(kb + 1) * 128], rhs=qcur[64 * comp:64 * comp + 64, 0:n], start=True, stop=True),
                             [f'G_kh{hj}', qn_], [f'G_pS{kb % 2}'])
                emit_qk(0)
                first = [True, True]
                for kb in range(NKB):
                    if kb + 1 < NKB:
                        emit_qk(kb + 1)
                    pp = pS[kb % 2]
                    ppn = f'G_pS{kb % 2}'
                    pt = pT[kb % 4]
                    ptn = f'G_pT{kb % 4}'
                    P.op('act', lambda e, pp=pp, pt=pt: e.activation(out=pt[:, :, 0:n], in_=pp[:, :, 0:n], func=AF.Exp, scale=0.125), [ppn], [ptn])
                    for comp in range(2):
                        P.op('pe', lambda e, comp=comp, kb=kb, pt=pt: e.matmul(po[comp][:, 0:n], lhsT=vh[hj][:, kb, :], rhs=pt[:, comp, 0:n], start=(kb == 0), stop=(kb == NKB - 1)),
                             [f'G_vh{hj}', ptn], [f'G_po{comp}'])
                    ai = kb % 2
                    if os.environ.get('DBG_F1_ADD') == 'dve':
                        ai = 0
                    if os.environ.get('DBG_F1_ADD') == 'none' and kb > 2:
                        continue
                    ae = 'pool' if ai == 1 else 'dve'
                    ac = accs[ai]
                    acn = f'G_acc{ai}'
                    if first[ai]:
                        first[ai] = False
                        P.op(ae, lambda e, ac=ac, pt=pt: e.tensor_copy(out=ac[:, :, 0:n], in_=pt[:, :, 0:n]), [ptn], [acn])
                    else:
                        P.op(ae, lambda e, ac=ac, pt=pt: e.tensor_tensor(out=ac[:, :, 0:n], in0=ac[:, :, 0:n], in1=pt[:, :, 0:n], op=ALU.add), [ptn, acn], [acn])
                for comp in range(2):
                    for i in range(2):
                        P.op('pe', lambda e, comp=comp, i=i: e.matmul(pl[:, comp, 0:n], lhsT=k.ones_f[:], rhs=accs[i][:, comp, 0:n], start=(i == 0), stop=(i == 1)),
                             ['ones_f', f'G_acc{i}'], ['G_pl'])
                P.op('dve', lambda e: e.reciprocal(out=rl[:, :, 0:n], in_=pl[:, :, 0:n]), ['G_pl'], ['G_rl'])
                P.op('dve', lambda e: e.tensor_tensor(out=o1[:, 0:n], in0=po[0][:, 0:n], in1=rl[:, 0, 0:n], op=ALU.mult), ['G_po0', 'G_rl'], ['G_o1'])
                P.op('dve', lambda e: e.tensor_tensor(out=o2[:, 0:n], in0=po[1][:, 0:n], in1=rl[:, 1, 0:n], op=ALU.mult), ['G_po1', 'G_rl'], ['G_o2'])
                P.op('dve', lambda e: e.scalar_tensor_tensor(out=o1[:, 0:n], in0=o2[:, 0:n], scalar=neglam[:, 0:1], in1=o1[:, 0:n], op0=ALU.mult, op1=ALU.add),
                     ['G_o1', 'G_o2', 'G_neglam'], ['G_o1'])
                P.op('act', lambda e: e.activation(out=sq[:, 0:n], in_=o1[:, 0:n], func=AF.Square), ['G_o1'], ['G_sq'])
                P.op('pe', lambda e: e.matmul(pl[:, 0, 0:n], lhsT=k.ones_f[:], rhs=sq[:, 0:n], start=True, stop=True), ['ones_f', 'G_sq'], ['G_pl'])
                P.op('act', lambda e: e.activation(out=sq[:, 0:n], in_=pl[:, 0, 0:n], func=AF.Sqrt, bias=k.cst[:, 0:1], scale=1.0 / 128), ['G_pl', 'cst'], ['G_sq'])
                P.op('dve', lambda e: e.reciprocal(out=sq[:, 0:n], in_=sq[:, 0:n]), ['G_sq'], ['G_sq'])
                P.op('dve', lambda e: e.scalar_tensor_tensor(out=ych[:, 0:n], in0=o1[:, 0:n], scalar=gsub[:, 1:2], in1=sq[:, 0:n], op0=ALU.mult, op1=ALU.mult),
                     ['G_o1', 'G_sq', 'G_gsub'], ['G_ych'])
                P.dma('sp', yc_d[h * 128:(h + 1) * 128, s:s + n], ych[:, 0:n], ['G_ych'], [], 'G_ych')
    P.barrier()
    if stop_after == 'l1F1':
        return

    with ExitStack() as st:
        sb = lambda n, s, d: k.sb(n, s, d, st)
        ps = lambda n, s, d: k.ps(n, s, d, st)
        w_out = sb('w_out1', [128, 8, D], BF16)
        with ExitStack() as st2:
            fold_w_bf16(k, st2, 'w_out1', w_out, I['cd_w_out'], 8, k.gbc_d[1, 0, 0])
            P.barrier()
        cp = {}
        for nm, shp in (('conf_w', [128, 2, 31]), ('conf_b', [128, 2]), ('conf_lng', [128, 2]), ('conf_lnb', [128, 2])):
            cp[nm] = sb('H_' + nm, shp, F32)
            P.dma('sp', cp[nm][:], I[nm], [], ['H_' + nm], 'H_' + nm)
        yT = sb('H_yT', [128, 8, 512], BF16)
        gin = sb('H_gin', [128, 2, 542], F32)
        ca = sb('H_ca', [128, 512], F32)
        cb_ = sb('H_cb', [128, 512], F32)
        ct = [sb(f'H_ct{i}', [128, 512], F32) for i in range(2)]
        xm = sb('H_xm', [128, 2, 512], F32)
        sq = sb('H_sq', [128, 2, 512], F32)
        rstd = sb('H_rstd', [128, 512], F32)
        xt = sb('H_xt', [128, 4, D], F32)
        xo = sb('H_xo', [128, 4, D], F32)
        pm = ps('H_pm', [128, 512], F32)
        pv = ps('H_pv', [128, 512], F32)
        po = [ps(f'H_po{i}', [128, 512], F32) for i in range(4)]
        glv = S['gl'].rearrange("(c p) t -> p c t", p=128)
        ycv = yc_d.rearrange("(c p) t -> p c t", p=128)
        for (s, n) in tiles_of(TL):
            ng = n // 128
            P.dma('sp', yT[:, 0:6, 0:n], ycv[:, :, s:s + n], [], keys('H_yT', 8)[0:6], 'H_yT')
            P.dma('sp', gin[:, :, 0:n + 30], glv[:, :, PADZ + s - 15:PADZ + s + n + 15], [], keys('H_gin', 2), 'H_gin')
            P.dma('sp', xt[:, 0:ng, :], S['x1_l'][1 + s:1 + s + n, :].rearrange("(g p) f -> p g f", p=128), [], ['H_xt'], 'H_xt')
            for c in range(2):
                P.op('act', lambda e, c=c: e.activation(out=ca[:, 0:n], in_=gin[:, c, 0:n], func=AF.Identity, scale=cp['conf_w'][:, c, 0:1], bias=cp['conf_b'][:, c:c + 1]),
                     [f'H_gin.{c}', 'H_conf_w', 'H_conf_b'], ['H_ca'])
                P.op('pool', lambda e, c=c: e.tensor_scalar(out=cb_[:, 0:n], in0=gin[:, c, 1:1 + n], scalar1=cp['conf_w'][:, c, 1:2], scalar2=None, op0=ALU.mult),
                     [f'H_gin.{c}', 'H_conf_w'], ['H_cb'])
                for t in range(2, 31):
                    if t % 2 == 0:
                        P.op('dve', lambda e, c=c, t=t: e.scalar_tensor_tensor(out=ca[:, 0:n], in0=gin[:, c, t:t + n], scalar=cp['conf_w'][:, c, t:t + 1], in1=ca[:, 0:n],
                                                                               op0=ALU.mult, op1=ALU.add), [f'H_gin.{c}', 'H_ca'], ['H_ca'])
                    else:
                        ctt = ct[(t // 2) % 2]
                        ctn = f'H_ct{(t // 2) % 2}'
                        P.op('act', lambda e, c=c, t=t, ctt=ctt: e.activation(out=ctt[:, 0:n], in_=gin[:, c, t:t + n], func=AF.Identity, scale=cp['conf_w'][:, c, t:t + 1], bias=k.cst[:, 2:3]),
                             [f'H_gin.{c}', 'H_conf_w', 'cst'], [ctn])
                        P.op('pool', lambda e, ctt=ctt: e.tensor_tensor(out=cb_[:, 0:n], in0=cb_[:, 0:n], in1=ctt[:, 0:n], op=ALU.add), [ctn, 'H_cb'], ['H_cb'])
                P.op('dve', lambda e, c=c: e.tensor_tensor(out=xm[:, c, 0:n], in0=ca[:, 0:n], in1=cb_[:, 0:n], op=ALU.add), ['H_ca', 'H_cb'], [f'H_xm.{c}'])
            for c in range(2):
                P.op('pe', lambda e, c=c: e.matmul(pm[:, 0:n], lhsT=k.ones_f[:], rhs=xm[:, c, 0:n], start=(c == 0), stop=(c == 1)), ['ones_f', f'H_xm.{c}'], ['H_pm'])
            for c in range(2):
                P.op('dve', lambda e, c=c: e.scalar_tensor_tensor(out=xm[:, c, 0:n], in0=pm[:, 0:n], scalar=-1.0 / 256, in1=xm[:, c, 0:n], op0=ALU.mult, op1=ALU.add),
                     ['H_pm', f'H_xm.{c}'], [f'H_xm.{c}'])
                P.op('act', lambda e, c=c: e.activation(out=sq[:, c, 0:n], in_=xm[:, c, 0:n], func=AF.Square), [f'H_xm.{c}'], [f'H_sq.{c}'])
            for c in range(2):
                P.op('pe', lambda e, c=c: e.matmul(pv[:, 0:n], lhsT=k.ones_f[:], rhs=sq[:, c, 0:n], start=(c == 0), stop=(c == 1)), ['ones_f', f'H_sq.{c}'], ['H_pv'])
            P.op('act', lambda e: e.activation(out=rstd[:, 0:n], in_=pv[:, 0:n], func=AF.Sqrt, bias=k.cst[:, 0:1], scale=1.0 / 256), ['H_pv', 'cst'], ['H_rstd'])
            P.op('dve', lambda e: e.reciprocal(out=rstd[:, 0:n], in_=rstd[:, 0:n]), ['H_rstd'], ['H_rstd'])
            for c in range(2):
                P.op('dve', lambda e, c=c: e.tensor_tensor(out=xm[:, c, 0:n], in0=xm[:, c, 0:n], in1=rstd[:, 0:n], op=ALU.mult), [f'H_xm.{c}', 'H_rstd'], [f'H_xm.{c}'])
                P.op('act', lambda e, c=c: e.activation(out=yT[:, 6 + c, 0:n], in_=xm[:, c, 0:n], func=AF.Silu, scale=cp['conf_lng'][:, c:c + 1], bias=cp['conf_lnb'][:, c:c + 1]),
                     [f'H_xm.{c}', 'H_conf_lng', 'H_conf_lnb'], [f'H_yT.{6 + c}'])
            for g in range(ng):
                for nt in range(2):
                    pp = po[(2 * g + nt) % 4]
                    ppn = f'H_po{(2 * g + nt) % 4}'
                    for kc in range(8):
                        P.op('pe', lambda e, g=g, nt=nt, kc=kc, pp=pp: e.matmul(pp[:, :], lhsT=yT[:, kc, g * 128:(g + 1) * 128], rhs=w_out[:, kc, nt * 512:(nt + 1) * 512],
                                                                               start=(kc == 0), stop=(kc == 7)), [f'H_yT.{kc}', f'w_out1.{kc}'], [ppn])
                    P.op('dve', lambda e, g=g, nt=nt, pp=pp: e.tensor_tensor(out=xo[:, g, nt * 512:(nt + 1) * 512], in0=pp[:, :], in1=xt[:, g, nt * 512:(nt + 1) * 512], op=ALU.add),
                         [ppn, 'H_xt'], [f'H_xo.{g}'])
            P.dma('sp', S['x15'][1 + s:1 + s + n, :].rearrange("(g p) f -> p g f", p=128), xo[:, 0:ng, :], keys('H_xo', 4)[0:ng], [], 'H_xo')
    P.barrier()
    if stop_after == 'l1F2':
        return
    ffn_phase(k, 1, 0, S['x15'], None, TL, final=True, flush=(not k.half), out_rows=k.TO)
    P.barrier()


def _fm(v, nch):
    return np.ascontiguousarray(np.asarray(v, np.float32).reshape(nch, 128).T)


def prep_shared(inp, T):
    f = lambda a: np.ascontiguousarray(np.asarray(a, np.float32))
    d = {}
    d['mod_w'] = f(inp['mod_w'])
    d['mod_b'] = f(inp['mod_b'])
    d['modb_fm'] = f(np.asarray(inp['mod_b']).reshape(2, 6, 8, 128).transpose(3, 0, 1, 2))
    ng = np.stack([np.asarray(inp['norm_mix_g']), np.asarray(inp['norm_ffn_g'])], axis=1)
    d['ng_fm'] = f(ng.reshape(2, 2, 8, 128).transpose(3, 0, 1, 2))
    d['final_g_bc'] = f(np.broadcast_to(np.asarray(inp['final_g'])[None, :], (128, D)))
    d['ident'] = f(np.eye(128))
    d['ab_w_in'] = f(inp['ab_w_in'][0])
    d['ab_w_out'] = f(inp['ab_w_out'][0])
    cw4 = np.asarray(inp['lru_conv_w'][0], np.float32).reshape(4, 6, 128).transpose(2, 1, 0)
    d['lru_cw'] = f(np.concatenate([cw4, np.zeros((128, 6, 1), np.float32)], axis=2))
    d['pool_fl'] = f(np.stack([np.ones(128), np.zeros(128)], axis=1))
    d['lru_cb'] = _fm(inp['lru_conv_b'][0], 6)
    for nm, src in (('lru_bdA', inp['lru_wa'][0]), ('lru_bdX', inp['lru_wx'][0])):
        src = np.asarray(src)
        bd = np.zeros((128, 2, 6, 128), np.float32)
        for dd in range(2):
            for c in range(6):
                bd[0:64, dd, c, 0:64] = src[dd, 2 * c]
                bd[64:128, dd, c, 64:128] = src[dd, 2 * c + 1]
        d[nm] = bd
    for nm, src in (('lru_ba', inp['lru_ba'][0]), ('lru_bx', inp['lru_bx'][0]), ('lru_lam', inp['lru_lambda'][0])):
        d[nm] = f(np.asarray(src).reshape(2, 6, 128).transpose(2, 0, 1))
    pw = np.asarray(inp['pool_w'][0])
    bd = np.zeros((128, 2, 128), np.float32)
    for ch in range(2):
        bd[0:64, ch, 0:64] = pw[2 * ch]
        bd[64:128, ch, 64:128] = pw[2 * ch + 1]
    d['pool_bd'] = bd
    d['pool_scale'] = _fm(inp['pool_scale'][0], 2)
    invw = np.zeros((128, 2), np.float32)
    corr = np.ones((128, 2, 2, 16), np.float32)
    wins = (2, 4, 8, 16)
    Lbig = 1 << 20
    for g, w in enumerate(wins):
        ch, half = g // 2, g % 2
        psl = slice(64 * half, 64 * half + 64)
        invw[psl, ch] = 1.0 / w
        for i in range(16):
            t = i
            cnt = (t + w - w // 2) - max(t - w // 2, 0)
            corr[psl, ch, 0, i] = float(w) / cnt
            t = Lbig - 16 + i
            cnt = min(t + w - w // 2, Lbig) - (t - w // 2)
            corr[psl, ch, 1, i] = float(w) / cnt
    d['pool_invw'] = invw
    d['pool_corr'] = corr
    d['ffn_w_up'] = f(inp['ffn_w_up'])
    d['ffn_w_down'] = f(inp['ffn_w_down'])
    d['ffn_cw'] = f(np.asarray(inp['ffn_conv_w']).reshape(2, 3, 2 * NFC, 128).transpose(3, 0, 2, 1))
    d['ffn_cb'] = f(np.asarray(inp['ffn_conv_b']).reshape(2, 2 * NFC, 128).transpose(2, 0, 1))
    w_in = np.asarray(inp['cd_w_in'][0], np.float32)
    d['cd_w_in'] = f(w_in)
    qk = w_in[:, :1536].reshape(D, 1536 // 32, 2, 16)
    d['cd_w_sw'] = f(qk[:, :, ::-1, :].reshape(D, 1536))
    d['cd_w_out'] = f(inp['cd_w_out'][0])
    t = np.arange(T)
    row = (t // GRID_W).astype(np.float32)
    col = (t % GRID_W).astype(np.float32)
    inv = (10000.0 ** (-np.arange(16, dtype=np.float32) / 16)).astype(np.float32)
    ang_r = (row[:, None] * inv).astype(np.float32)
    ang_c = (col[:, None] * inv).astype(np.float32)
    rc = np.zeros((128, T), np.float32)
    rs = np.zeros((128, T), np.float32)
    for p in range(128):
        dd = p % 64
        ang = ang_r if dd < 32 else ang_c
        fq = dd % 16
        first = (dd % 32) < 16
        rc[p] = np.cos(ang[:, fq])
        rs[p] = (-1.0 if first else 1.0) * np.sin(ang[:, fq])
    d['rope_c'] = rc
    d['rope_s'] = rs
    d['diff_l'] = f(np.stack([np.asarray(inp['diff_lq1'][0]), np.asarray(inp['diff_lk1'][0]),
                              np.asarray(inp['diff_lq2'][0]), np.asarray(inp['diff_lk2'][0])])[None])
    d['subln_g'] = f(np.asarray(inp['diff_subln_g'][0]).reshape(128, 1))
    d['conf_w'] = f(np.asarray(inp['conf_dw_w'][0]).reshape(31, 2, 128).transpose(2, 1, 0))
    d['conf_b'] = _fm(inp['conf_dw_b'][0], 2)
    d['conf_lng'] = _fm(inp['conf_ln_g'][0], 2)
    d['conf_lnb'] = _fm(inp['conf_ln_b'][0], 2)
    return d


def prep_rev(shared, T):
    f = lambda a: np.ascontiguousarray(np.asarray(a, np.float32))
    d = dict(shared)
    cw = shared['lru_cw']
    d['lru_cw'] = f(cw[:, :, ::-1])
    for nm in ('lru_bdA', 'lru_bdX', 'lru_ba', 'lru_bx', 'lru_lam'):
        d[nm] = f(shared[nm][:, ::-1])
    d['pool_fl'] = f(np.stack([np.zeros(128), np.ones(128)], axis=1))
    corr = np.ones((128, 2, 2, 16), np.float32)
    Lbig = 1 << 20
    for g, w in enumerate((2, 4, 8, 16)):
        ch, half = g // 2, g % 2
        psl = slice(64 * half, 64 * half + 64)
        for i in range(16):
            r = i
            cnt = (r + w // 2) - max(r - w // 2 + 1, 0) + 1
            corr[psl, ch, 0, i] = float(w) / cnt
            r = Lbig - 16 + i
            cnt = min(r + w // 2, Lbig - 1) - (r - w // 2 + 1) + 1
            corr[psl, ch, 1, i] = float(w) / cnt
    d['pool_corr'] = corr
    d['ffn_cw'] = f(shared['ffn_cw'][:, :, :, ::-1])
    d['conf_w'] = f(shared['conf_w'][:, :, ::-1])
    d['rope_c'] = f(shared['rope_c'][:, ::-1])
    d['rope_s'] = f(shared['rope_s'][:, ::-1])
    return d


def prep_core(inp, shared, b, T, rev=False):
    m = dict(shared)
    x = np.asarray(inp['x'][b, :T], np.float32)
    ctx = np.asarray(inp['ctx'][b], np.float32)
    if rev:
        x = x[::-1]
        ctx = ctx[::-1]
    m['x'] = np.ascontiguousarray(x)
    m['ctx'] = np.ascontiguousarray(ctx)
    m['c_fm'] = np.ascontiguousarray(np.stack([_fm(inp['c'][b], 8), _fm(inp['c_ctx'], 8)], axis=1))
    return m


_CACHE = {}


def kernel(**inputs):
    T = inputs['x'].shape[1]
    Bn = inputs['x'].shape[0]
    if T not in _CACHE:
        _CACHE[T] = build(T)
    kk = _CACHE[T]
    shared = prep_shared(inputs, T)
    shared_r = prep_rev(shared, T)
    in_maps = [prep_core(inputs, shared, b, T, rev=False) for b in range(Bn)] + \
              [prep_core(inputs, shared_r, b, T, rev=True) for b in range(Bn)]
    res = run_bass_kernel_spmd(kk.nc, in_maps, core_ids=list(range(2 * Bn)))
    out = np.empty((Bn, T, D), np.float32)
    for b in range(Bn):
        out[b, :T // 2] = np.asarray(res.results[b]['out'], np.float32)
        out[b, T // 2:] = np.asarray(res.results[Bn + b]['out'], np.float32)[::-1]
    return out
```

```python
import numpy as np
import math
import os
from contextlib import ExitStack
import concourse.bass as bass
import concourse.mybir as mybir
from concourse.bass_utils import run_bass_kernel_spmd
from concourse.ap import AP

F32 = mybir.dt.float32
BF16 = mybir.dt.bfloat16
AF = mybir.ActivationFunctionType
ALU = mybir.AluOpType
AX = mybir.AxisListType

D = 1024
LC = 256
DFF = 2816
NFC = DFF // 128
PADZ = 16
EPS = 1e-6
GRID_W = 64
LAMBDA_INIT1 = 0.8 - 0.6 * math.exp(-0.3 * 1)
SAME_ENGINE_SYNC = True


def rev(ap):
    a = [list(x) for x in ap.ap]
    st, n = a[-1]
    a[-1] = [-st, n]
    return AP(ap.tensor, ap.offset + st * (n - 1), a)


class Prog:
    def __init__(self, nc, es):
        self.nc = nc
        self.es = es
        self.eng = {'pe': nc.tensor, 'act': nc.scalar, 'dve': nc.vector, 'pool': nc.gpsimd, 'sp': nc.sync}
        self.esem = {e: es.enter_context(nc.semaphore('S_' + e)) for e in ('pe', 'act', 'dve', 'pool')}
        self.ecnt = {e: 0 for e in self.esem}
        self.dsem = {}
        self.dpool = []
        self.nd = 0
        self.waited = {e: {} for e in self.eng}
        self.lastw = {}
        self.rd = {}
        self.ninstr = 0

    def _need(self, reads, writes):
        ev = {}

        def add(e):
            if e is None:
                return
            k, sem, val = e
            if k not in ev or ev[k][1] < val:
                ev[k] = (sem, val)
        for r in reads:
            add(self.lastw.get(r))
        for w in writes:
            add(self.lastw.get(w))
            for e in self.rd.get(w, {}).items():
                add((e[0], e[1][0], e[1][1]))
        return ev

    def _wait(self, e, ev):
        for k, (sem, val) in ev.items():
            if k == 'S_' + e and (e == 'pe' or not SAME_ENGINE_SYNC):
                continue
            if self.waited[e].get(k, 0) < val:
                self.eng[e].wait_ge(sem, val)
                self.waited[e][k] = val
                self.ninstr += 1

    def _commit(self, ev, reads, writes):
        k, sem, val = ev
        for w in writes:
            self.lastw[w] = ev
            self.rd[w] = {}
        for r in reads:
            self.rd.setdefault(r, {})[k] = (sem, val)

    def op(self, e, fn, reads, writes):
        self._wait(e, self._need(reads, writes))
        ins = fn(self.eng[e])
        self.ecnt[e] += 1
        ins.then_inc(self.esem[e], 1)
        self.ninstr += 1
        self._commit(('S_' + e, self.esem[e], self.ecnt[e]), reads, writes)

    def dma(self, q, out, in_, reads, writes, key):
        self._wait(q, self._need(reads, writes))
        if key not in self.dsem:
            if self.dpool:
                self.dsem[key] = self.dpool.pop()
            else:
                nm = 'D%d' % self.nd
                self.nd += 1
                self.dsem[key] = [self.es.enter_context(self.nc.semaphore(nm)), 0, nm]
        d = self.dsem[key]
        ins = self.eng[q].dma_start(out=out, in_=in_)
        d[1] += 16
        ins.then_inc(d[0], 16)
        self.ninstr += 1
        self._commit((d[2], d[0], d[1]), reads, writes)

    def barrier(self):
        ev = {}
        for e in self.esem:
            if self.ecnt[e] > 0:
                ev['S_' + e] = (self.esem[e], self.ecnt[e])
        for k, d in self.dsem.items():
            if d[1] > 0:
                ev[d[2]] = (d[0], d[1])
        for e in self.eng:
            self._wait(e, dict(ev))
        self.lastw = {}
        self.rd = {}
        for k, d in self.dsem.items():
            self.dpool.append(d)
        self.dsem = {}


def keys(name, n):
    return [f"{name}.{i}" for i in range(n)]


def tiles_of(T, w=512):
    out = []
    s = 0
    while s < T:
        n = min(w, T - s)
        out.append((s, n))
        s += n
    return out


class K:
    pass


def build(T, dbg=False, stop_after=None, half=True):
    nc = bass.Bass("TRN2", target_bir_lowering=False)
    k = K()
    k.nc = nc
    k.T = T
    k.half = half
    k.TL = (T // 2 + 128) if half else T
    k.TO = (T // 2) if half else T

    def din(name, shape, dt=F32):
        return nc.dram_tensor(name, list(shape), dt, kind="ExternalInput").ap()

    def dscr(name, shape, dt=F32, out=False):
        kind = "ExternalOutput" if (out or dbg) else "Internal"
        return nc.dram_tensor(name, list(shape), dt, kind=kind).ap()

    I = {}
    I['x'] = din('x', [T, D])
    I['ctx'] = din('ctx', [LC, D])
    I['c_fm'] = din('c_fm', [128, 2, 8])
    I['mod_w'] = din('mod_w', [2, D, 6 * D])
    I['modb_fm'] = din('modb_fm', [128, 2, 6, 8])
    I['mod_b'] = din('mod_b', [2, 6 * D])
    I['ng_fm'] = din('ng_fm', [128, 2, 2, 8])
    I['final_g_bc'] = din('final_g_bc', [128, D])
    I['ident'] = din('ident', [128, 128])
    I['ab_w_in'] = din('ab_w_in', [D, 1792])
    I['ab_w_out'] = din('ab_w_out', [D, D])
    I['lru_cw'] = din('lru_cw', [128, 6, 5])
    I['pool_fl'] = din('pool_fl', [128, 2])
    I['lru_cb'] = din('lru_cb', [128, 6])
    I['lru_bdA'] = din('lru_bdA', [128, 2, 6, 128])
    I['lru_bdX'] = din('lru_bdX', [128, 2, 6, 128])
    I['lru_ba'] = din('lru_ba', [128, 2, 6])
    I['lru_bx'] = din('lru_bx', [128, 2, 6])
    I['lru_lam'] = din('lru_lam', [128, 2, 6])
    I['pool_bd'] = din('pool_bd', [128, 2, 128])
    I['pool_scale'] = din('pool_scale', [128, 2])
    I['pool_invw'] = din('pool_invw', [128, 2])
    I['pool_corr'] = din('pool_corr', [128, 2, 2, 16])
    I['ffn_w_up'] = din('ffn_w_up', [2, D, 2 * DFF])
    I['ffn_w_down'] = din('ffn_w_down', [2, DFF, D])
    I['ffn_cw'] = din('ffn_cw', [128, 2, 2 * NFC, 3])
    I['ffn_cb'] = din('ffn_cb', [128, 2, 2 * NFC])
    I['cd_w_in'] = din('cd_w_in', [D, 2816])
    I['cd_w_sw'] = din('cd_w_sw', [D, 1536])
    I['cd_w_out'] = din('cd_w_out', [D, D])
    I['rope_c'] = din('rope_c', [128, T])
    I['rope_s'] = din('rope_s', [128, T])
    I['diff_l'] = din('diff_l', [1, 4, 64])
    I['subln_g'] = din('subln_g', [128, 1])
    I['conf_w'] = din('conf_w', [128, 2, 31])
    I['conf_b'] = din('conf_b', [128, 2])
    I['conf_lng'] = din('conf_lng', [128, 2])
    I['conf_lnb'] = din('conf_lnb', [128, 2])
    k.I = I

    k.out = nc.dram_tensor('out', [k.TO, D], F32, kind="ExternalOutput").ap()
    S = {}
    for nm, TT in (('l', T), ('c', LC)):
        S['z_' + nm] = dscr('z_' + nm, [1792, PADZ + TT + PADZ])
        S['xa_' + nm] = dscr('xa_' + nm, [768, TT])
        S['hf_' + nm] = dscr('hf_' + nm, [768, TT])
        S['x05_' + nm] = dscr('x05_' + nm, [1 + TT, D])
        S['x1_' + nm] = dscr('x1_' + nm, [1 + TT + 128, D])
    S['x15'] = dscr('x15', [1 + T, D])
    S['x2'] = dscr('x2', [1 + T + 128, D])
    S['qT'] = dscr('qT', [768, T], BF16)
    S['kT'] = dscr('kT', [768, T + LC], BF16)
    S['v'] = dscr('v', [T + LC, 768], BF16)
    S['gl'] = dscr('gl', [256, PADZ + T + PADZ])
    k.S = S

    with ExitStack() as es:
        P = Prog(nc, es)
        k.P = P

        uid = [0]

        def sb(name, shape, dt, st=es):
            uid[0] += 1
            return st.enter_context(nc.sbuf_tensor(f"{name}_s{uid[0]}", list(shape), dt))

        def ps(name, shape, dt, st=es):
            uid[0] += 1
            return st.enter_context(nc.psum_tensor(f"{name}_p{uid[0]}", list(shape), dt))
        k.sb = sb
        k.ps = ps

        ident = sb('ident', [128, 128], BF16)
        P.dma('pool', ident[:], I['ident'], [], ['ident'], 'ident')
        k.ident = ident
        cst = sb('cst', [128, 8], F32)
        P.op('dve', lambda e: e.memset(cst[:, 0:1], EPS), [], ['cst'])
        P.op('dve', lambda e: e.memset(cst[:, 1:2], 1.0), [], ['cst'])
        P.op('dve', lambda e: e.memset(cst[:, 2:3], 0.0), [], ['cst'])
        k.cst = cst
        zero = sb('zero', [128, 512], F32)
        P.op('dve', lambda e: e.memset(zero[:], 0.0), [], ['zero'])
        k.zero = zero
        ones_bf = sb('ones_bf', [128, 128], BF16)
        P.op('dve', lambda e: e.memset(ones_bf[:], 1.0), [], ['ones_bf'])
        k.ones_bf = ones_bf
        ones_f = sb('ones_f', [128, 128], F32)
        P.op('dve', lambda e: e.memset(ones_f[:], 1.0), [], ['ones_f'])
        k.ones_f = ones_f

        modfm = sb('modfm', [128, 2, 2, 4, 8], F32)
        k.modfm = modfm
        k.gbc_d = nc.dram_tensor('gbc_d', [2, 2, 2, 128, D], F32, kind="Internal").ap()

        if os.environ.get('DBG_ONLY_F1'):
            layer1(k, 'l1F1')
            return finish(k, es)
        phase_adaln(k)
        P.barrier()
        if dbg:
            mdbg = nc.dram_tensor('modfm_dbg', [128, 2, 2, 4, 8], F32, kind="ExternalOutput").ap()
            P.dma('sp', mdbg, modfm[:], [], [], 'mdbg')
            gdbg = nc.dram_tensor('gbc_dbg', [2, 2, 2, 128, D], F32, kind="ExternalOutput").ap()
            P.dma('sp', gdbg, k.gbc_d, [], [], 'gdbg')
        if stop_after == 'adaln':
            return finish(k, es)

        layer0(k, stop_after)
        if stop_after is not None and stop_after.startswith('l0'):
            return finish(k, es)
        layer1(k, stop_after)
        return finish(k, es)


def finish(k, es):
    k.P.barrier()
    k.ninstr = k.P.ninstr
    return k


def phase_adaln(k):
    nc, P, I = k.nc, k.P, k.I
    with ExitStack() as st:
        sb = lambda n, s, d: k.sb(n, s, d, st)
        ps = lambda n, s, d: k.ps(n, s, d, st)
        cf = sb('ad_cf', [128, 2, 8], F32)
        P.dma('sp', cf[:], I['c_fm'], [], ['ad_cf'], 'ad_cf')
        sc = sb('ad_sc', [128, 2, 8], F32)
        P.op('act', lambda e: e.activation(out=sc[:], in_=cf[:], func=AF.Silu), ['ad_cf'], ['ad_sc'])
        rep = sb('ad_rep', [128, 2, 8, 128], F32)
        for s_ in range(2):
            for kc in range(8):
                P.op('dve', lambda e, s_=s_, kc=kc: e.tensor_copy(out=rep[:, s_, kc, :], in_=sc[:, s_, kc:kc + 1].to_broadcast([128, 128])),
                     ['ad_sc'], [f'ad_rep.{s_}.{kc}'])
        modb = sb('ad_modb', [128, 2, 6, 8], F32)
        P.dma('sp', modb[:], I['modb_fm'], [], ['ad_modb'], 'ad_modb')
        ng = sb('ad_ng', [128, 2, 2, 8], F32)
        P.dma('sp', ng[:], I['ng_fm'], [], ['ad_ng'], 'ad_ng')
        brow = sb('ad_brow', [1, 2, 6 * D], F32)
        P.dma('sp', brow[:], I['mod_b'].rearrange("(o l) n -> o l n", o=1), [], ['ad_brow'], 'ad_brow')
        wt = [sb(f'ad_w{i}', [128, 6 * D], F32) for i in range(2)]
        gst = sb('ad_gst', [128, 2, D], F32)
        facc = sb('ad_facc', [128, 32, 2], F32)
        pfm = ps('ad_pfm', [128, 32, 2], F32)
        pbc = [ps(f'ad_pbc{i}', [128, 512], F32) for i in range(4)]
        for l in range(2):
            for pss in range(2):
                for kc in range(8):
                    w = wt[kc % 2]
                    wk = f'ad_w{kc % 2}'
                    P.dma('sp', w[:], I['mod_w'][l, kc * 128:(kc + 1) * 128, :], [], [wk], wk)
                    if pss == 0:
                        jmap = [0, 1, 3, 4]
                        for jj, j in enumerate(jmap):
                            for fc in range(8):
                                col = j * D + fc * 128
                                P.op('pe', lambda e, w=w, col=col, jj=jj, fc=fc, kc=kc: e.matmul(
                                    pfm[:, jj * 8 + fc, :], lhsT=w[:, col:col + 128], rhs=sc[:, :, kc],
                                    start=True, stop=True), [wk, 'ad_sc'], ['ad_pfm'])
                        if kc == 0:
                            P.op('dve', lambda e: e.tensor_copy(out=facc[:], in_=pfm[:]), ['ad_pfm'], ['ad_facc'])
                        else:
                            P.op('dve', lambda e: e.tensor_tensor(out=facc[:], in0=pfm[:], in1=facc[:], op=ALU.add), ['ad_pfm', 'ad_facc'], ['ad_facc'])
                    if True:
                        s_ = pss
                        for nt in range(4):
                            gj = 2 if nt < 2 else 5
                            col = gj * D + (nt % 2) * 512
                            P.op('pe', lambda e, w=w, col=col, nt=nt, kc=kc, s_=s_: e.matmul(
                                pbc[nt][:], lhsT=rep[:, s_, kc, :], rhs=w[:, col:col + 512],
                                start=(kc == 0), stop=False), [wk, f'ad_rep.{s_}.{kc}'], [f'ad_pbc{nt}'])
                if pss == 0:
                    jmap = [0, 1, 3, 4]
                    for s_ in range(2):
                        for jj, j in enumerate(jmap):
                            P.op('dve', lambda e, s_=s_, jj=jj, j=j, l=l: e.tensor_tensor(
                                out=k.modfm[:, l, s_, jj, :], in0=facc[:, jj * 8:(jj + 1) * 8, s_], in1=modb[:, l, j, :], op=ALU.add),
                                ['ad_facc', 'ad_modb'], [f'modfm.{l}.{s_}.{jj}'])
                        for jj, which in ((1, 0), (3, 1)):
                            P.op('dve', lambda e, s_=s_, jj=jj, which=which, l=l: e.scalar_tensor_tensor(
                                out=k.modfm[:, l, s_, jj, :], in0=k.modfm[:, l, s_, jj, :], scalar=1.0, in1=ng[:, l, which, :],
                                op0=ALU.add, op1=ALU.mult), [f'modfm.{l}.{s_}.{jj}', 'ad_ng'], [f'modfm.{l}.{s_}.{jj}'])
                if True:
                    s_ = pss
                    for nt in range(4):
                        gj = 2 if nt < 2 else 5
                        col = gj * D + (nt % 2) * 512
                        P.op('pe', lambda e, nt=nt, col=col, l=l: e.matmul(
                            pbc[nt][:], lhsT=k.ones_f[0:1, :], rhs=brow[0:1, l, col:col + 512], start=False, stop=True),
                            ['ones_f', 'ad_brow'], [f'ad_pbc{nt}'])
                        P.op('act', lambda e, nt=nt: e.copy(out=gst[:, nt // 2, (nt % 2) * 512:(nt % 2) * 512 + 512], in_=pbc[nt][:]),
                            [f'ad_pbc{nt}'], [f'ad_gst.{nt}'])
                    for j2 in range(2):
                        P.dma('sp', k.gbc_d[l, s_, j2], gst[:, j2, :], [f'ad_gst.{2 * j2}', f'ad_gst.{2 * j2 + 1}'], [], 'ad_gst')
        P.barrier()


def load_w_bf16(k, name, dst, src_ap, nk, ncols, colblk=2048):
    P = k.P
    for kc in range(nk):
        for c0 in range(0, ncols, colblk):
            c1 = min(ncols, c0 + colblk)
            P.dma('pool', dst[:, kc, c0:c1], src_ap[kc * 128:(kc + 1) * 128, c0:c1], [], [f'{name}.{kc}'], f'{name}.{kc}')


def fold_w_bf16(k, st, name, dst, src_ap, nk, gb_dram):
    P = k.P
    stg = [k.sb(f'{name}_stg{i}', [128, D], F32, st) for i in range(2)]
    gb = k.sb(f'{name}_gb', [128, D], F32, st)
    P.dma('sp', gb[:], gb_dram, [], [f'{name}_gb'], f'{name}_gb')
    gb_ap = gb[:]
    for kc in range(nk):
        s_ = stg[kc % 2]
        sk = f'{name}_stg{kc % 2}'
        P.dma('sp', s_[:], src_ap[kc * 128:(kc + 1) * 128, :], [], [sk], sk)
        P.op('dve', lambda e, s_=s_, kc=kc: e.tensor_tensor(out=dst[:, kc, :], in0=s_[:], in1=gb_ap, op=ALU.mult),
             [sk, f'{name}_gb'], [f'{name}.{kc}'])


def modulate_tile(k, B, src_rows, n, l, s_, jsh, hT, hTname):
    P = k.P
    ng = n // 128
    X, Xn = B['xt'], B['xtname']
    ss = B['ss']
    xn = B['xn']
    for g0 in range(0, ng, 2):
        gg = min(2, ng - g0)
        P.dma('sp', X[:, 0:gg, :], src_rows[g0 * 128:(g0 + gg) * 128, :].rearrange("(g p) f -> p g f", p=128), [], [Xn], Xn)
        P.op('dve', lambda e: e.memset(ss[:, 0:2], 0.0), [], [B['ssname']])
        for g in range(gg):
            P.op('act', lambda e, g=g: e.activation(out=B['junk'][:], in_=X[:, g, :], func=AF.Square, accum_out=ss[:, g:g + 1]),
                 [Xn], [B['ssname'], B['junkname']])
        P.op('act', lambda e, gg=gg: e.activation(out=ss[:, 4:4 + gg], in_=ss[:, 0:gg], func=AF.Sqrt, bias=k.cst[:, 0:1], scale=1.0 / D),
             [B['ssname'], 'cst'], [B['ssname']])
        P.op('dve', lambda e, gg=gg: e.reciprocal(out=ss[:, 8:8 + gg], in_=ss[:, 4:4 + gg]), [B['ssname']], [B['ssname']])
        for g in range(gg):
            if g % 2 == 0:
                P.op('dve', lambda e, g=g, g0=g0: e.tensor_scalar(out=xn[:, g0 + g, :], in0=X[:, g, :], scalar1=ss[:, 8 + g:9 + g], scalar2=None, op0=ALU.mult),
                     [Xn, B['ssname']], [f"{B['xnname']}.{g0 + g}"])
            else:
                P.op('act', lambda e, g=g, g0=g0: e.activation(out=xn[:, g0 + g, :], in_=X[:, g, :], func=AF.Identity, scale=ss[:, 8 + g:9 + g], bias=k.cst[:, 2:3]),
                     [Xn, B['ssname'], 'cst'], [f"{B['xnname']}.{g0 + g}"])
    for fc in range(8):
        tp = B['tp'][fc % 2]
        tpn = B['tpname'][fc % 2]
        for g in range(ng):
            P.op('pe', lambda e, g=g, fc=fc, tp=tp: e.transpose(out=tp[:, g * 128:(g + 1) * 128], in_=xn[:, g, fc * 128:(fc + 1) * 128], identity=k.ident[:]),
                 [f"{B['xnname']}.{g}", 'ident'], [tpn])
        P.op('act', lambda e, fc=fc, tp=tp: e.activation(out=hT[:, fc, 0:n], in_=tp[:, 0:n], func=AF.Identity,
                                                          scale=k.modfm[:, l, s_, jsh + 1, fc:fc + 1], bias=k.modfm[:, l, s_, jsh, fc:fc + 1]),
             [tpn], [f'{hTname}.{fc}'])


def mod_bufs(k, st, pfx):
    B = {}
    B['xt'] = k.sb(pfx + 'xt', [128, 2, D], F32, st)
    B['xtname'] = pfx + 'xt'
    B['xn'] = k.sb(pfx + 'xn', [128, 4, D], BF16, st)
    B['xnname'] = pfx + 'xn'
    B['junk'] = k.sb(pfx + 'junk', [128, D], BF16, st)
    B['junkname'] = pfx + 'junk'
    B['ss'] = k.sb(pfx + 'ss', [128, 12], F32, st)
    B['ssname'] = pfx + 'ss'
    B['tp'] = [k.ps(pfx + f'tp{i}', [128, 512], BF16, st) for i in range(2)]
    B['tpname'] = [pfx + f'tp{i}' for i in range(2)]
    return B


def layer0(k, stop_after):
    nc, P, I, S = k.nc, k.P, k.I, k.S
    with ExitStack() as st:
        sb = lambda n, s, d: k.sb(n, s, d, st)
        lp = {}
        for nm, shp in (('lru_cw', [128, 6, 5]), ('pool_fl', [128, 2]), ('lru_cb', [128, 6]), ('lru_ba', [128, 2, 6]), ('lru_bx', [128, 2, 6]),
                        ('lru_lam', [128, 2, 6]), ('pool_scale', [128, 2]), ('pool_invw', [128, 2]), ('pool_corr', [128, 2, 2, 16])):
            lp[nm] = sb('p_' + nm, shp, F32)
            P.dma('sp', lp[nm][:], I[nm], [], ['p_' + nm], 'p_' + nm)
        for nm, shp in (('lru_bdA', [128, 2, 6, 128]), ('lru_bdX', [128, 2, 6, 128]), ('pool_bd', [128, 2, 128])):
            lp[nm] = sb('p_' + nm, shp, BF16)
            P.dma('pool', lp[nm][:], I[nm], [], ['p_' + nm], 'p_' + nm)
        cl = sb('p_cl', [128, 2, 2, 6], F32)
        tmp = sb('p_cltmp', [128, 2, 6], F32)
        P.op('act', lambda e: e.activation(out=tmp[:], in_=lp['lru_lam'][:], func=AF.Exp, scale=-1.0), ['p_lru_lam'], ['p_cltmp'])
        P.op('act', lambda e: e.activation(out=tmp[:], in_=tmp[:], func=AF.Ln, bias=k.cst[:, 1:2], scale=1.0), ['p_cltmp', 'cst'], ['p_cltmp'])
        P.op('dve', lambda e: e.tensor_scalar(out=cl[:, 0, :, :], in0=tmp[:], scalar1=-8.0, scalar2=None, op0=ALU.mult), ['p_cltmp'], ['p_cl'])
        P.op('dve', lambda e: e.tensor_scalar(out=cl[:, 1, :, :], in0=tmp[:], scalar1=-16.0, scalar2=None, op0=ALU.mult), ['p_cltmp'], ['p_cl'])
        lp['cl'] = cl
        stt = sb('p_state', [128, 2, 6], F32)
        P.op('dve', lambda e: e.memset(stt[:], 0.0), [], keys('p_state0', 6) + keys('p_state1', 6))
        lp['state'] = stt
        k.lp = lp
        w_in = sb('w_in0', [128, 8, 1792], BF16)
        load_w_bf16(k, 'w_in0', w_in, I['ab_w_in'], 8, 1792, colblk=1792)
        w_out = sb('w_out0', [128, 8, D], BF16)
        k.w_in0, k.w_out0 = w_in, w_out

        for s_, nm, TT, xsrc in ((1, 'c', LC, I['ctx']), (0, 'l', k.T, I['x'])):
            seg = K()
            seg.nm, seg.T, seg.x, seg.set = nm, TT, xsrc, s_
            seg.z, seg.xa, seg.hf, seg.x05, seg.x1 = S['z_' + nm], S['xa_' + nm], S['hf_' + nm], S['x05_' + nm], S['x1_' + nm]
            seg.tiles = tiles_of(TT)
            with ExitStack() as st2:
                fold_w_bf16(k, st2, 'w_out0', w_out, I['ab_w_out'], 8, k.gbc_d[0, s_, 0])
            P.barrier()
            l0_phaseA(k, seg)
            P.barrier()
            if stop_after == 'l0A' and nm == 'l':
                return
            l0_phaseB(k, seg)
            P.barrier()
            if stop_after == 'l0B' and nm == 'l':
                return
            l0_phaseC(k, seg)
            P.barrier()
            if stop_after == 'l0C' and nm == 'l':
                return
    for s_, nm, TT in ((1, 'c', LC), (0, 'l', k.T)):
        ffn_phase(k, 0, s_, S['x05_' + nm], S['x1_' + nm], TT, final=False)
        P.barrier()


def l0_phaseA(k, seg):
    nc, P = k.nc, k.P
    with ExitStack() as st:
        sb = lambda n, s, d: k.sb(n, s, d, st)
        ps = lambda n, s, d: k.ps(n, s, d, st)
        B = mod_bufs(k, st, 'A_')
        hT = sb('A_hT', [128, 8, 512], BF16)
        zt = [sb(f'A_zt{i}', [128, 14, 512], F32) for i in range(2)]
        zp = [ps(f'A_zp{i}', [128, 512], F32) for i in range(4)]
        zv = seg.z.rearrange("(c p) t -> p c t", p=128)
        P.dma('sp', zv[:, :, 0:PADZ], k.zero[:, 0:14 * PADZ].rearrange("p (c t) -> p c t", c=14), ['zero'], [], 'A_zpad')
        P.dma('sp', zv[:, :, PADZ + seg.T:PADZ + seg.T + PADZ], k.zero[:, 0:14 * PADZ].rearrange("p (c t) -> p c t", c=14), ['zero'], [], 'A_zpad')
        for ti, (s, n) in enumerate(seg.tiles):
            modulate_tile(k, B, seg.x[s:s + n, :], n, 0, seg.set, 0, hT, 'A_hT')
            Z = zt[ti % 2]
            Zn = f'A_zt{ti % 2}'
            for mc in range(14):
                zpp = zp[mc % 4]
                for kc in range(8):
                    P.op('pe', lambda e, mc=mc, kc=kc, zpp=zpp: e.matmul(zpp[:, 0:n], lhsT=k.w_in0[:, kc, mc * 128:(mc + 1) * 128], rhs=hT[:, kc, 0:n],
                                                                         start=(kc == 0), stop=(kc == 7)),
                         [f'w_in0.{kc}', f'A_hT.{kc}'], [f'A_zp{mc % 4}'])
                eng = 'act' if mc % 2 == 0 else 'dve'
                if eng == 'act':
                    P.op('act', lambda e, mc=mc, zpp=zpp: e.copy(out=Z[:, mc, 0:n], in_=zpp[:, 0:n]), [f'A_zp{mc % 4}'], [f'{Zn}.{mc}'])
                else:
                    P.op('dve', lambda e, mc=mc, zpp=zpp: e.tensor_copy(out=Z[:, mc, 0:n], in_=zpp[:, 0:n]), [f'A_zp{mc % 4}'], [f'{Zn}.{mc}'])
            P.dma('sp', zv[:, :, PADZ + s:PADZ + s + n], Z[:, :, 0:n], keys(Zn, 14), [], Zn)


def lru_coeffs(k, C, d, n, xa, xab):
    P, lp = k.P, k.lp
    for c in range(6):
        pr, pi = C['pg'][(2 * c) % 4], C['pg'][(2 * c + 1) % 4]
        prn, pin = C['pgname'][(2 * c) % 4], C['pgname'][(2 * c + 1) % 4]
        P.op('pe', lambda e, c=c, pr=pr: e.matmul(pr[:, 0:n], lhsT=lp['lru_bdA'][:, d, c, :], rhs=xab[:, c, 0:n], start=True, stop=True),
             ['p_lru_bdA', f"{C['xabname']}.{c}"], [prn])
        P.op('pe', lambda e, c=c, pi=pi: e.matmul(pi[:, 0:n], lhsT=lp['lru_bdX'][:, d, c, :], rhs=xab[:, c, 0:n], start=True, stop=True),
             ['p_lru_bdX', f"{C['xabname']}.{c}"], [pin])
        P.op('act', lambda e, c=c, pr=pr: e.activation(out=C['r'][:, c, 0:n], in_=pr[:, 0:n], func=AF.Sigmoid, bias=lp['lru_ba'][:, d, c:c + 1], scale=1.0),
             [prn, 'p_lru_ba'], [f"{C['pfx']}r.{c}"])
        P.op('act', lambda e, c=c, pi=pi: e.activation(out=C['ig'][:, c, 0:n], in_=pi[:, 0:n], func=AF.Sigmoid, bias=lp['lru_bx'][:, d, c:c + 1], scale=1.0),
             [pin, 'p_lru_bx'], [f"{C['pfx']}ig.{c}"])
    for c in range(6):
        P.op('act', lambda e, c=c: e.activation(out=C['a'][:, c, 0:n], in_=C['r'][:, c, 0:n], func=AF.Exp, scale=lp['cl'][:, 0, d, c:c + 1]),
             [f"{C['pfx']}r.{c}", 'p_cl'], [f"{C['pfx']}a.{c}"])
        P.op('act', lambda e, c=c: e.activation(out=C['r'][:, c, 0:n], in_=C['r'][:, c, 0:n], func=AF.Exp, scale=lp['cl'][:, 1, d, c:c + 1]),
             [f"{C['pfx']}r.{c}", 'p_cl'], [f"{C['pfx']}r.{c}"])
    for c in range(6):
        P.op('act', lambda e, c=c: e.activation(out=C['r'][:, c, 0:n], in_=C['r'][:, c, 0:n], func=AF.Sqrt, bias=k.cst[:, 1:2], scale=-1.0),
             [f"{C['pfx']}r.{c}", 'cst'], [f"{C['pfx']}r.{c}"])
        P.op('dve', lambda e, c=c: e.tensor_tensor(out=C['ig'][:, c, 0:n], in0=C['ig'][:, c, 0:n], in1=C['r'][:, c, 0:n], op=ALU.mult),
             [f"{C['pfx']}r.{c}", f"{C['pfx']}ig.{c}"], [f"{C['pfx']}ig.{c}"])
        P.op('pool', lambda e, c=c: e.tensor_tensor(out=C['ig'][:, c, 0:n], in0=C['ig'][:, c, 0:n], in1=xa[:, c, 0:n], op=ALU.mult),
             [f"{C['pfx']}ig.{c}", f"{C['xaname']}.{c}"], [f"{C['pfx']}ig.{c}"])


def coeff_bufs(k, st, pfx, share=None):
    C = {'pfx': pfx}
    for nm in ('r', 'ig', 'a'):
        C[nm] = k.sb(pfx + nm, [128, 6, 512], F32, st)
    if share is None:
        C['pg'] = [k.ps(pfx + f'pg{i}', [128, 512], F32, st) for i in range(4)]
        C['pgname'] = [pfx + f'pg{i}' for i in range(4)]
    else:
        C['pg'], C['pgname'] = share['pg'], share['pgname']
    return C


def l0_phaseB(k, seg):
    P, lp = k.P, k.lp
    with ExitStack() as st:
        sb = lambda n, s, d: k.sb(n, s, d, st)
        sets = []
        for j in range(2):
            Bf = {}
            Bf['zin'] = sb(f'B{j}_zin', [128, 6, 516], F32)
            Bf['xa'] = sb(f'B{j}_xa', [128, 6, 512], F32)
            Bf['xab'] = sb(f'B{j}_xab', [128, 6, 512], BF16)
            Bf['hf'] = sb('B_hf', [128, 6, 512], F32) if j == 0 else sets[0]['hf']
            C = coeff_bufs(k, st, f'B{j}_', share=(sets[0]['C'] if j == 1 else None))
            C['xabname'], C['xaname'] = f'B{j}_xab', f'B{j}_xa'
            Bf['C'] = C
            sets.append(Bf)
        zv = seg.z[0:768, :].rearrange("(c p) t -> p c t", p=128)
        xav = seg.xa.rearrange("(c p) t -> p c t", p=128)
        hfv = seg.hf.rearrange("(c p) t -> p c t", p=128)
        if seg.nm == 'c':
            P.op('dve', lambda e: e.memset(lp['state'][:], 0.0), [], keys('p_state0', 6) + keys('p_state1', 6))
        for ti, (s, n) in enumerate(seg.tiles):
            j = ti % 2
            Bf = sets[j]
            zin, xa, xab, hf, C = Bf['zin'], Bf['xa'], Bf['xab'], Bf['hf'], Bf['C']
            pf = f'B{j}_'
            P.dma('sp', zin[:, :, 0:n + 4], zv[:, :, PADZ + s - 2:PADZ + s + n + 2], [], keys(pf + 'zin', 6), pf + 'zin')
            for c in range(6):
                P.op('act', lambda e, c=c, xa=xa, zin=zin: e.activation(out=xa[:, c, 0:n], in_=zin[:, c, 0:n], func=AF.Identity,
                                                                        scale=lp['lru_cw'][:, c, 0:1], bias=lp['lru_cb'][:, c:c + 1]),
                     [f'{pf}zin.{c}', 'p_lru_cw', 'p_lru_cb'], [f'{pf}xa.{c}'])
                for t in range(1, 5):
                    P.op('dve', lambda e, c=c, t=t, xa=xa, zin=zin: e.scalar_tensor_tensor(out=xa[:, c, 0:n], in0=zin[:, c, t:t + n], scalar=lp['lru_cw'][:, c, t:t + 1],
                                                                                           in1=xa[:, c, 0:n], op0=ALU.mult, op1=ALU.add),
                         [f'{pf}zin.{c}', f'{pf}xa.{c}'], [f'{pf}xa.{c}'])
                P.op('act', lambda e, c=c, xa=xa, xab=xab: e.copy(out=xab[:, c, 0:n], in_=xa[:, c, 0:n]), [f'{pf}xa.{c}'], [f'{pf}xab.{c}'])
            P.dma('sp', xav[:, :, s:s + n], xa[:, :, 0:n], keys(pf + 'xa', 6), [], pf + 'xa')
            lru_coeffs(k, C, 0, n, xa, xab)
            for c in range(6):
                P.op('dve', lambda e, c=c, hf=hf, C=C: e.tensor_tensor_scan(out=hf[:, c, 0:n], data0=C['a'][:, c, 0:n], data1=C['ig'][:, c, 0:n],
                                                                          initial=lp['state'][:, 0, c:c + 1], op0=ALU.mult, op1=ALU.add),
                     [f'{pf}a.{c}', f'{pf}ig.{c}', f'p_state0.{c}'], [f'B_hf.{c}'])
                P.op('dve', lambda e, c=c, hf=hf: e.tensor_copy(out=lp['state'][:, 0, c:c + 1], in_=hf[:, c, n - 1:n]), [f'B_hf.{c}'], [f'p_state0.{c}'])
            P.dma('sp', hfv[:, :, s:s + n], hf[:, :, 0:n], keys('B_hf', 6), [], 'B_hf')


def l0_phaseC(k, seg):
    P, lp, I = k.P, k.lp, k.I
    with ExitStack() as st:
        sb = lambda n, s, d: k.sb(n, s, d, st)
        ps = lambda n, s, d: k.ps(n, s, d, st)
        xa = sb('C_xa', [128, 6, 512], F32)
        xab = sb('C_xab', [128, 6, 512], BF16)
        hb = sb('C_hb', [128, 6, 512], F32)
        hf = sb('C_hf', [128, 6, 512], F32)
        ga = sb('C_ga', [128, 6, 512], F32)
        yT = sb('C_yT', [128, 8, 512], BF16)
        zb = sb('C_zb', [128, 2, 528], F32)
        p2 = sb('C_p2', [128, 528], F32)
        p4 = sb('C_p4', [128, 528], F32)
        p8 = sb('C_p8', [128, 528], F32)
        Qw = sb('C_Qw', [128, 516], F32)
        Ssum = sb('C_S', [128, 512], F32)
        dd = sb('C_dd', [128, 2, 512], BF16)
        xt = sb('C_xt', [128, 4, D], F32)
        xo = sb('C_xo', [128, 4, D], F32)
        C = coeff_bufs(k, st, 'C_')
        C['xabname'], C['xaname'] = 'C_xab', 'C_xa'
        po = [ps(f'C_po{i}', [128, 512], F32) for i in range(4)]
        zg = seg.z[768:1536, :].rearrange("(c p) t -> p c t", p=128)
        zbv = seg.z[1536:1792, :].rearrange("(c p) t -> p c t", p=128)
        xav = seg.xa.rearrange("(c p) t -> p c t", p=128)
        hfv = seg.hf.rearrange("(c p) t -> p c t", p=128)
        if seg.nm == 'c':
            P.op('dve', lambda e: e.memset(lp['state'][:, 1, :], 0.0), [], keys('p_state1', 6))
        nt_ = len(seg.tiles)
        for ti in range(nt_ - 1, -1, -1):
            s, n = seg.tiles[ti]
            ng = n // 128
            P.dma('sp', xa[:, :, 0:n], xav[:, :, s:s + n], [], keys('C_xa', 6), 'C_xa')
            P.dma('sp', hf[:, :, 0:n], hfv[:, :, s:s + n], [], keys('C_hf', 6), 'C_hf')
            P.dma('sp', ga[:, :, 0:n], zg[:, :, PADZ + s:PADZ + s + n], [], keys('C_ga', 6), 'C_ga')
            P.dma('sp', zb[:, :, 0:n + 16], zbv[:, :, PADZ + s - 8:PADZ + s + n + 8], [], keys('C_zb', 2), 'C_zb')
            P.dma('sp', xt[:, 0:ng, :], seg.x[s:s + n, :].rearrange("(g p) f -> p g f", p=128), [], ['C_xt'], 'C_xt')
            W = n + 16
            n1 = n + 1
            for ch in range(2):
                zc = zb[:, ch, :]
                P.op('dve', lambda e, zc=zc: e.tensor_tensor(out=p2[:, 0:W - 1], in0=zc[:, 0:W - 1], in1=zc[:, 1:W], op=ALU.add),
                     [f'C_zb.{ch}'], ['C_p2'])
                if ch == 0:
                    P.op('dve', lambda e: e.tensor_copy(out=Qw[0:64, 0:n1], in_=p2[0:64, 7:7 + n1]), ['C_p2'], ['C_Qw'])
                    P.op('dve', lambda e: e.tensor_tensor(out=Qw[64:128, 0:n1], in0=p2[64:128, 6:6 + n1], in1=p2[64:128, 8:8 + n1], op=ALU.add),
                         ['C_p2'], ['C_Qw'])
                else:
                    P.op('dve', lambda e: e.tensor_tensor(out=p4[:, 0:W - 3], in0=p2[:, 0:W - 3], in1=p2[:, 2:W - 1], op=ALU.add), ['C_p2'], ['C_p4'])
                    P.op('dve', lambda e: e.tensor_tensor(out=Qw[0:64, 0:n1], in0=p4[0:64, 4:4 + n1], in1=p4[0:64, 8:8 + n1], op=ALU.add),
                         ['C_p4'], ['C_Qw'])
                    P.op('dve', lambda e: e.tensor_tensor(out=p8[64:128, 0:W - 7], in0=p4[64:128, 0:W - 7], in1=p4[64:128, 4:W - 3], op=ALU.add),
                         ['C_p4'], ['C_p8'])
                    P.op('dve', lambda e: e.tensor_tensor(out=Qw[64:128, 0:n1], in0=p8[64:128, 0:n1], in1=p8[64:128, 8:8 + n1], op=ALU.add),
                         ['C_p8'], ['C_Qw'])
                P.op('dve', lambda e: e.tensor_scalar(out=Ssum[:, 0:n], in0=Qw[:, 0:n], scalar1=lp['pool_fl'][:, 0:1], scalar2=None, op0=ALU.mult),
                     ['C_Qw', 'p_pool_fl'], ['C_S'])
                P.op('dve', lambda e: e.scalar_tensor_tensor(out=Ssum[:, 0:n], in0=Qw[:, 1:n1], scalar=lp['pool_fl'][:, 1:2], in1=Ssum[:, 0:n],
                                                             op0=ALU.mult, op1=ALU.add), ['C_Qw', 'C_S', 'p_pool_fl'], ['C_S'])
                if ti == 0:
                    P.op('dve', lambda e, ch=ch: e.tensor_tensor(out=Ssum[:, 0:16], in0=Ssum[:, 0:16], in1=lp['pool_corr'][:, ch, 0, :], op=ALU.mult),
                         ['C_S', 'p_pool_corr'], ['C_S'])
                if ti == nt_ - 1:
                    P.op('dve', lambda e, ch=ch: e.tensor_tensor(out=Ssum[:, n - 16:n], in0=Ssum[:, n - 16:n], in1=lp['pool_corr'][:, ch, 1, :], op=ALU.mult),
                         ['C_S', 'p_pool_corr'], ['C_S'])
                P.op('dve', lambda e, ch=ch, zc=zc: e.scalar_tensor_tensor(out=dd[:, ch, 0:n], in0=Ssum[:, 0:n], scalar=lp['pool_invw'][:, ch:ch + 1],
                                                                          in1=zc[:, 8:8 + n], op0=ALU.mult, op1=ALU.subtract),
                     ['C_S', f'C_zb.{ch}', 'p_pool_invw'], [f'C_dd.{ch}'])
                pp = po[ch]
                P.op('pe', lambda e, ch=ch, pp=pp: e.matmul(pp[:, 0:n], lhsT=lp['pool_bd'][:, ch, :], rhs=dd[:, ch, 0:n], start=True, stop=True),
                     ['p_pool_bd', f'C_dd.{ch}'], [f'C_po{ch}'])
                P.op('act', lambda e, ch=ch, pp=pp: e.activation(out=yT[:, 6 + ch, 0:n], in_=pp[:, 0:n], func=AF.Identity, scale=lp['pool_scale'][:, ch:ch + 1], bias=k.cst[:, 2:3]),
                     [f'C_po{ch}', 'p_pool_scale', 'cst'], [f'C_yT.{6 + ch}'])
            for c in range(6):
                P.op('act', lambda e, c=c: e.copy(out=xab[:, c, 0:n], in_=xa[:, c, 0:n]), [f'C_xa.{c}'], [f'C_xab.{c}'])
            lru_coeffs(k, C, 1, n, xa, xab)
            for c in range(6):
                P.op('dve', lambda e, c=c: e.tensor_tensor_scan(out=rev(hb[:, c, 0:n]), data0=rev(C['a'][:, c, 0:n]), data1=rev(C['ig'][:, c, 0:n]),
                                                                initial=lp['state'][:, 1, c:c + 1], op0=ALU.mult, op1=ALU.add),
                     [f'C_a.{c}', f'C_ig.{c}', f'p_state1.{c}'], [f'C_hb.{c}'])
                P.op('dve', lambda e, c=c: e.tensor_copy(out=lp['state'][:, 1, c:c + 1], in_=hb[:, c, 0:1]), [f'C_hb.{c}'], [f'p_state1.{c}'])
                P.op('act', lambda e, c=c: e.activation(out=ga[:, c, 0:n], in_=ga[:, c, 0:n], func=AF.Gelu_apprx_tanh), [f'C_ga.{c}'], [f'C_ga.{c}'])
                P.op('pool', lambda e, c=c: e.tensor_tensor(out=hb[:, c, 0:n], in0=hb[:, c, 0:n], in1=hf[:, c, 0:n], op=ALU.add),
                     [f'C_hb.{c}', f'C_hf.{c}'], [f'C_hb.{c}'])
                P.op('dve', lambda e, c=c: e.tensor_tensor(out=yT[:, c, 0:n], in0=hb[:, c, 0:n], in1=ga[:, c, 0:n], op=ALU.mult),
                     [f'C_hb.{c}', f'C_ga.{c}'], [f'C_yT.{c}'])
            for g in range(ng):
                for nt in range(2):
                    pp = po[(2 * g + nt) % 4]
                    ppn = f'C_po{(2 * g + nt) % 4}'
                    for kc in range(8):
                        P.op('pe', lambda e, g=g, nt=nt, kc=kc, pp=pp: e.matmul(pp[:, :], lhsT=yT[:, kc, g * 128:(g + 1) * 128], rhs=k.w_out0[:, kc, nt * 512:(nt + 1) * 512],
                                                                               start=(kc == 0), stop=(kc == 7)),
                             [f'C_yT.{kc}', f'w_out0.{kc}'], [ppn])
                    P.op('dve', lambda e, g=g, nt=nt, pp=pp: e.tensor_tensor(out=xo[:, g, nt * 512:(nt + 1) * 512], in0=pp[:, :], in1=xt[:, g, nt * 512:(nt + 1) * 512], op=ALU.add),
                         [ppn, 'C_xt'], [f'C_xo.{g}'])
            P.dma('sp', seg.x05[1 + s:1 + s + n, :].rearrange("(g p) f -> p g f", p=128), xo[:, 0:ng, :], keys('C_xo', 4)[0:ng], [], 'C_xo')


def ffn_phase(k, l, s_, src, dst, TT, final, flush=True, out_rows=None):
    nc, P, I = k.nc, k.P, k.I
    tiles = tiles_of(TT)
    with ExitStack() as st:
        sb = lambda n, s, d: k.sb(n, s, d, st)
        ps = lambda n, s, d: k.ps(n, s, d, st)
        w_up = sb('F_wup', [128, 8, 2 * DFF], BF16)
        load_w_bf16(k, 'F_wup', w_up, I['ffn_w_up'][l], 8, 2 * DFF)
        w_dn = sb('F_wdn', [128, NFC, D], BF16)
        with ExitStack() as st2:
            fold_w_bf16(k, st2, 'F_wdn', w_dn, I['ffn_w_down'][l], NFC, k.gbc_d[l, s_, 1])
            P.barrier()
        cw = sb('F_cw', [128, 2 * NFC, 3], F32)
        cb = sb('F_cb', [128, 2 * NFC], F32)
        P.dma('sp', cw[:], I['ffn_cw'][:, l, :, :], [], ['F_cw'], 'F_cw')
        P.dma('sp', cb[:], I['ffn_cb'][:, l, :], [], ['F_cb'], 'F_cb')
        B = mod_bufs(k, st, 'F_')
        hT = sb('F_hT', [128, 8, 512], BF16)
        gT = sb('F_gT', [128, NFC, 512], BF16)
        prevu = [sb(f'F_prevu{i}', [128, 2 * NFC, 2], F32) for i in range(2)]
        P.op('dve', lambda e: e.memset(prevu[0][:], 0.0), [], keys('F_prevu0', 2 * NFC))
        acc = [sb(f'F_acc{i}', [128, 512], F32) for i in range(4)]
        corr = sb('F_corr', [128, 2 * NFC, 2], F32)
        ctmp = sb('F_ctmp', [128, 2 * NFC], F32)
        xs = sb('F_xs', [128, 2, D], F32)
        xo = xs
        pu = [ps(f'F_pu{i}', [128, 512], F32) for i in range(4)]
        pd = [ps(f'F_pd{i}', [128, 512], F32) for i in range(2)]
        if final:
            fg = sb('F_fg', [128, D], F32)
            P.dma('sp', fg[:], I['final_g_bc'], [], ['F_fg'], 'F_fg')
            fss = sb('F_fss', [128, 12], F32)
            fjunk = B['junk']

        if os.environ.get('DBG_SBUF'):
            print('FFN sbuf remaining', nc.sbuf_bytes_remaining, 'final', final)
        def conv_gate(n, zero_u, ti):
            pin, pout = prevu[ti % 2], prevu[(ti + 1) % 2]
            pinn, poutn = f'F_prevu{ti % 2}', f'F_prevu{(ti + 1) % 2}'
            allin = keys(pinn, 2 * NFC)
            P.op('dve', lambda e: e.tensor_tensor(out=corr[:, :, 0], in0=cw[:, :, 0], in1=pin[:, :, 0], op=ALU.mult), allin + ['F_cw'], ['F_corr'])
            P.op('dve', lambda e: e.tensor_tensor(out=ctmp[:, :], in0=cw[:, :, 1], in1=pin[:, :, 1], op=ALU.mult), allin + ['F_cw'], ['F_ctmp'])
            P.op('dve', lambda e: e.tensor_tensor(out=corr[:, :, 0], in0=corr[:, :, 0], in1=ctmp[:, :], op=ALU.add), ['F_corr', 'F_ctmp'], ['F_corr'])
            P.op('dve', lambda e: e.tensor_tensor(out=corr[:, :, 1], in0=cw[:, :, 0], in1=pin[:, :, 1], op=ALU.mult), allin + ['F_cw', 'F_corr'], ['F_corr'])
            for c in range(NFC):
                q = c % 2
                AA = [acc[2 * q], acc[2 * q + 1]]
                AN = [f'F_acc{2 * q}', f'F_acc{2 * q + 1}']
                PP = [pu[2 * q], pu[2 * q + 1]]
                PN = [f'F_pu{2 * q}', f'F_pu{2 * q + 1}']
                CC = [c, NFC + c]
                if not zero_u:
                    for vi in range(2):
                        for kc in range(8):
                            P.op('pe', lambda e, kc=kc, cc=CC[vi], pp=PP[vi]: e.matmul(pp[:, 0:n], lhsT=w_up[:, kc, cc * 128:(cc + 1) * 128], rhs=hT[:, kc, 0:n],
                                                                                      start=(kc == 0), stop=(kc == 7)),
                                 [f'F_wup.{kc}', f'F_hT.{kc}'], [PN[vi]])
                    for vi in range(2):
                        P.op('act', lambda e, A_=AA[vi], pp=PP[vi], cc=CC[vi]: e.activation(out=A_[:, 0:n], in_=pp[:, 0:n], func=AF.Identity,
                                                                                          scale=cw[:, cc, 2:3], bias=cb[:, cc:cc + 1]),
                             [PN[vi], 'F_cw', 'F_cb'], [AN[vi]])
                    for vi in range(2):
                        if os.environ.get('DBG_SKIP_SAVE'):
                            continue
                        P.op('dve', lambda e, pp=PP[vi], cc=CC[vi]: e.tensor_copy(out=pout[:, cc, :], in_=pp[:, n - 2:n]), [PN[vi]], [f'{poutn}.{CC[vi]}'])
                    for vi in range(2):
                        P.op('dve', lambda e, A_=AA[vi], pp=PP[vi], cc=CC[vi]: e.scalar_tensor_tensor(out=A_[:, 1:n], in0=pp[:, 0:n - 1], scalar=cw[:, cc, 1:2], in1=A_[:, 1:n],
                                                                                                    op0=ALU.mult, op1=ALU.add), [PN[vi], AN[vi]], [AN[vi]])
                    for vi in range(2):
                        P.op('dve', lambda e, A_=AA[vi], pp=PP[vi], cc=CC[vi]: e.scalar_tensor_tensor(out=A_[:, 2:n], in0=pp[:, 0:n - 2], scalar=cw[:, cc, 0:1], in1=A_[:, 2:n],
                                                                                                    op0=ALU.mult, op1=ALU.add), [PN[vi], AN[vi]], [AN[vi]])
                else:
                    for vi in range(2):
                        P.op('act', lambda e, A_=AA[vi], cc=CC[vi]: e.activation(out=A_[:, 0:n], in_=k.zero[:, 0:n], func=AF.Identity,
                                                                                scale=cw[:, cc, 2:3], bias=cb[:, cc:cc + 1]),
                             ['zero', 'F_cw', 'F_cb'], [AN[vi]])
                for vi in range(2):
                    P.op('pool', lambda e, A_=AA[vi], cc=CC[vi]: e.tensor_tensor(out=A_[:, 0:2], in0=A_[:, 0:2], in1=corr[:, cc, :], op=ALU.add),
                         ['F_corr', AN[vi]], [AN[vi]])
                P.op('act', lambda e, A_=AA[1]: e.activation(out=A_[:, 0:n], in_=A_[:, 0:n], func=AF.Silu), [AN[1]], [AN[1]])
                P.op('pool', lambda e, c=c, A0=AA[0], A1=AA[1]: e.tensor_tensor(out=gT[:, c, 0:n], in0=A0[:, 0:n], in1=A1[:, 0:n], op=ALU.mult),
                     [AN[0], AN[1]], [f'F_gT.{c}'])

        def down_res(tok0, n, nrows_last=128):
            ng = n // 128
            nr = lambda g: (nrows_last if g == ng - 1 else 128)
            for g_ in range(ng):
                g = g_ % 2
                r0 = 1 + tok0 + g_ * 128
                P.dma('sp', xs[0:nr(g_), g, :], src[r0:r0 + nr(g_), :], [], [f'F_xs.{g}'], f'F_xs{g}')
                for nt in range(2):
                    pp = pd[nt]
                    for kc in range(NFC):
                        P.op('pe', lambda e, g_=g_, nt=nt, kc=kc, pp=pp: e.matmul(pp[:, :], lhsT=gT[:, kc, g_ * 128:(g_ + 1) * 128], rhs=w_dn[:, kc, nt * 512:(nt + 1) * 512],
                                                                               start=(kc == 0), stop=(kc == NFC - 1)),
                             [f'F_gT.{kc}', f'F_wdn.{kc}'], [f'F_pd{nt}'])
                    P.op('dve', lambda e, g=g, g_=g_, nt=nt, pp=pp: e.tensor_tensor(out=xo[0:nr(g_), g, nt * 512:(nt + 1) * 512], in0=pp[0:nr(g_), :],
                                                                            in1=xs[0:nr(g_), g, nt * 512:(nt + 1) * 512], op=ALU.add),
                         [f'F_pd{nt}', f'F_xs.{g}'], [f'F_xs.{g}'])
                if final:
                    P.op('dve', lambda e: e.memset(fss[:, 0:1], 0.0), [], ['F_fss'])
                    P.op('act', lambda e, g=g: e.activation(out=fjunk[:], in_=xo[:, g, :], func=AF.Square, accum_out=fss[:, 0:1]),
                         [f'F_xs.{g}'], ['F_fss', 'F_junk'])
                    P.op('act', lambda e: e.activation(out=fss[:, 1:2], in_=fss[:, 0:1], func=AF.Sqrt, bias=k.cst[:, 0:1], scale=1.0 / D),
                         ['F_fss', 'cst'], ['F_fss'])
                    P.op('dve', lambda e: e.reciprocal(out=fss[:, 2:3], in_=fss[:, 1:2]), ['F_fss'], ['F_fss'])
                    P.op('dve', lambda e, g=g: e.scalar_tensor_tensor(out=xo[:, g, :], in0=xo[:, g, :], scalar=fss[:, 2:3], in1=fg[:],
                                                                     op0=ALU.mult, op1=ALU.mult), [f'F_xs.{g}', 'F_fss', 'F_fg'], [f'F_xs.{g}'])
                t0 = tok0 + g_ * 128
                if final:
                    lo = max(t0, 0)
                    hi = min(t0 + nr(g_), TT if out_rows is None else out_rows)
                    if hi > lo:
                        P.dma('sp', k.out[lo:hi, :], xo[lo - t0:hi - t0, g, :], [f'F_xs.{g}'], [], f'F_xs{g}')
                else:
                    P.dma('sp', dst[1 + t0:1 + t0 + nr(g_), :], xo[0:nr(g_), g, :], [f'F_xs.{g}'], [], f'F_xs{g}')

        for ti, (s, n) in enumerate(tiles):
            modulate_tile(k, B, src[1 + s:1 + s + n, :], n, l, s_, 2, hT, 'F_hT')
            conv_gate(n, False, ti)
            down_res(s - 1, n)
        if flush:
            conv_gate(128, True, len(tiles))
            down_res(TT - 1, 128, nrows_last=1)


def layer1(k, stop_after):
    nc, P, I, S = k.nc, k.P, k.I, k.S
    T = k.T
    TK = T + LC
    NKB = TK // 128
    TL = k.TL
    yc_d = nc.dram_tensor('yc_d', [768, T], BF16, kind="Internal").ap()
    with ExitStack() as st:
      if not os.environ.get('DBG_ONLY_F1'):
          sb = lambda n, s, d: k.sb(n, s, d, st)
          ps = lambda n, s, d: k.ps(n, s, d, st)
          w_in = sb('w_in1', [128, 8, 2816], BF16)
          load_w_bf16(k, 'w_in1', w_in, I['cd_w_in'], 8, 2816, colblk=1408)
          w_sw = sb('w_sw1', [128, 8, 1536], BF16)
          load_w_bf16(k, 'w_sw1', w_sw, I['cd_w_sw'], 8, 1536, colblk=1536)
          B = mod_bufs(k, st, 'E_')
          hT = sb('E_hT', [128, 8, 512], BF16)
          qk = sb('E_qk', [128, 12, 512], BF16)
          vt = sb('E_vt', [128, 4, 768], BF16)
          gl = sb('E_gl', [128, 2, 512], F32)
          rc = sb('E_rc', [128, 512], F32)
          rs = sb('E_rs', [128, 512], F32)
          t1 = sb('E_t1', [128, 512], F32)
          t2 = sb('E_t2', [128, 512], F32)
          sg = sb('E_sg', [128, 512], F32)
          pa = [ps(f'E_pa{i}', [128, 512], F32) for i in range(2)]
          pb = [ps(f'E_pb{i}', [128, 512], F32) for i in range(2)]
          glv = S['gl'].rearrange("(c p) t -> p c t", p=128)
          P.dma('sp', glv[:, :, 0:PADZ], k.zero[:, 0:2 * PADZ].rearrange("p (c t) -> p c t", c=2), ['zero'], [], 'E_glpad')
          P.dma('sp', glv[:, :, PADZ + T:PADZ + T + PADZ], k.zero[:, 0:2 * PADZ].rearrange("p (c t) -> p c t", c=2), ['zero'], [], 'E_glpad')
          qTv = S['qT'].rearrange("(c p) t -> p c t", p=128)
          kTv = S['kT'].rearrange("(c p) t -> p c t", p=128)

          def proj_plain(cols0, nch, dst, dstname, d0, n):
              for c in range(nch):
                  pp = pa[c % 2]
                  for kc in range(8):
                      P.op('pe', lambda e, c=c, kc=kc, pp=pp: e.matmul(pp[:, 0:n], lhsT=w_in[:, kc, cols0 + c * 128:cols0 + (c + 1) * 128], rhs=hT[:, kc, 0:n],
                                                                      start=(kc == 0), stop=(kc == 7)), [f'w_in1.{kc}', f'E_hT.{kc}'], [f'E_pa{c % 2}'])
                  P.op('act', lambda e, c=c, pp=pp: e.copy(out=dst[:, d0 + c, 0:n], in_=pp[:, 0:n]), [f'E_pa{c % 2}'], [f'{dstname}.{d0 + c}'])

          def proj_v(n):
              ng = n // 128
              for g in range(ng):
                  for (c0, cn, pp, ppn) in ((0, 512, pa[g % 2], f'E_pa{g % 2}'), (512, 256, pb[g % 2], f'E_pb{g % 2}')):
                      for kc in range(8):
                          P.op('pe', lambda e, g=g, kc=kc, pp=pp, c0=c0, cn=cn: e.matmul(pp[:, 0:cn], lhsT=hT[:, kc, g * 128:(g + 1) * 128],
                                                                                         rhs=w_in[:, kc, 1536 + c0:1536 + c0 + cn], start=(kc == 0), stop=(kc == 7)),
                               [f'w_in1.{kc}', f'E_hT.{kc}'], [ppn])
                      P.op('dve', lambda e, g=g, pp=pp, c0=c0, cn=cn: e.tensor_copy(out=vt[:, g, c0:c0 + cn], in_=pp[:, 0:cn]), [ppn], [f'E_vt.{g}'])

          n = LC
          modulate_tile(k, B, S['x1_c'][1:1 + LC, :], n, 1, 1, 0, hT, 'E_hT')
          proj_plain(768, 6, qk, 'E_qk', 6, n)
          P.dma('sp', kTv[:, :, T:T + n], qk[:, 6:12, 0:n], keys('E_qk', 12)[6:12], [], 'E_qk')
          proj_v(n)
          P.dma('sp', S['v'][T:T + n, :].rearrange("(g p) f -> p g f", p=128), vt[:, 0:n // 128, :], keys('E_vt', 4), [], 'E_vt')
          for ti, (s, n) in enumerate(tiles_of(T)):
              modulate_tile(k, B, S['x1_l'][1 + s:1 + s + n, :], n, 1, 0, 0, hT, 'E_hT')
              P.dma('sp', rc[:, 0:n], I['rope_c'][:, s:s + n], [], ['E_rc'], 'E_rc')
              P.dma('sp', rs[:, 0:n], I['rope_s'][:, s:s + n], [], ['E_rs'], 'E_rs')
              need_q = s < TL + 16
              for c in (range(12) if need_q else range(6, 12)):
                  pp, pq = pa[c % 2], pb[c % 2]
                  for kc in range(8):
                      P.op('pe', lambda e, c=c, kc=kc, pp=pp: e.matmul(pp[:, 0:n], lhsT=w_in[:, kc, c * 128:(c + 1) * 128], rhs=hT[:, kc, 0:n],
                                                                      start=(kc == 0), stop=(kc == 7)), [f'w_in1.{kc}', f'E_hT.{kc}'], [f'E_pa{c % 2}'])
                  for kc in range(8):
                      P.op('pe', lambda e, c=c, kc=kc, pq=pq: e.matmul(pq[:, 0:n], lhsT=w_sw[:, kc, c * 128:(c + 1) * 128], rhs=hT[:, kc, 0:n],
                                                                      start=(kc == 0), stop=(kc == 7)), [f'w_sw1.{kc}', f'E_hT.{kc}'], [f'E_pb{c % 2}'])
                  P.op('dve', lambda e, pp=pp: e.tensor_tensor(out=t1[:, 0:n], in0=pp[:, 0:n], in1=rc[:, 0:n], op=ALU.mult), [f'E_pa{c % 2}', 'E_rc'], ['E_t1'])
                  P.op('dve', lambda e, pq=pq: e.tensor_tensor(out=t2[:, 0:n], in0=pq[:, 0:n], in1=rs[:, 0:n], op=ALU.mult), [f'E_pb{c % 2}', 'E_rs'], ['E_t2'])
                  P.op('pool', lambda e, c=c: e.tensor_tensor(out=qk[:, c, 0:n], in0=t1[:, 0:n], in1=t2[:, 0:n], op=ALU.add), ['E_t1', 'E_t2'], [f'E_qk.{c}'])
              if need_q:
                  P.dma('sp', qTv[:, :, s:s + n], qk[:, 0:6, 0:n], keys('E_qk', 12)[0:6], [], 'E_q')
              P.dma('sp', kTv[:, :, s:s + n], qk[:, 6:12, 0:n], keys('E_qk', 12)[6:12], [], 'E_qk')
              proj_v(n)
              P.dma('sp', S['v'][s:s + n, :].rearrange("(g p) f -> p g f", p=128), vt[:, 0:n // 128, :], keys('E_vt', 4), [], 'E_vt')
              for c in (range(2) if need_q else []):
                  pp, pq = pa[c % 2], pb[c % 2]
                  for (pz, pzn, cols) in ((pp, f'E_pa{c % 2}', 2304 + c * 128), (pq, f'E_pb{c % 2}', 2304 + 256 + c * 128)):
                      for kc in range(8):
                          P.op('pe', lambda e, kc=kc, pz=pz, cols=cols: e.matmul(pz[:, 0:n], lhsT=w_in[:, kc, cols:cols + 128], rhs=hT[:, kc, 0:n],
                                                                                start=(kc == 0), stop=(kc == 7)), [f'w_in1.{kc}', f'E_hT.{kc}'], [pzn])
                  P.op('act', lambda e, pq=pq: e.activation(out=sg[:, 0:n], in_=pq[:, 0:n], func=AF.Sigmoid), [f'E_pb{c % 2}'], ['E_sg'])
                  P.op('dve', lambda e, c=c, pp=pp: e.tensor_tensor(out=gl[:, c, 0:n], in0=pp[:, 0:n], in1=sg[:, 0:n], op=ALU.mult), [f'E_pa{c % 2}', 'E_sg'], [f'E_gl.{c}'])
              if need_q:
                  P.dma('sp', glv[:, :, PADZ + s:PADZ + s + n], gl[:, :, 0:n], keys('E_gl', 2), [], 'E_gl')
    P.barrier()
    if stop_after == 'l1E':
        return

    with ExitStack() as st:
        sb = lambda n, s, d: k.sb(n, s, d, st)
        ps = lambda n, s, d: k.ps(n, s, d, st)
        dl = sb('G_dl', [1, 4, 64], F32)
        P.dma('sp', dl[:], I['diff_l'], [], ['G_dl'], 'G_dl')
        sm = sb('G_sm', [1, 8], F32)
        pr_ = sb('G_pr', [1, 2, 64], F32)
        P.op('dve', lambda e: e.tensor_tensor(out=pr_[:, 0, :], in0=dl[:, 0, :], in1=dl[:, 1, :], op=ALU.mult), ['G_dl'], ['G_pr'])
        P.op('dve', lambda e: e.tensor_tensor(out=pr_[:, 1, :], in0=dl[:, 2, :], in1=dl[:, 3, :], op=ALU.mult), ['G_dl'], ['G_pr'])
        P.op('dve', lambda e: e.reduce_sum(out=sm[:, 0:1], in_=pr_[:, 0, :], axis=AX.X), ['G_pr'], ['G_sm'])
        P.op('dve', lambda e: e.reduce_sum(out=sm[:, 1:2], in_=pr_[:, 1, :], axis=AX.X), ['G_pr'], ['G_sm'])
        P.op('act', lambda e: e.activation(out=sm[:, 2:4], in_=sm[:, 0:2], func=AF.Exp), ['G_sm'], ['G_sm'])
        P.op('dve', lambda e: e.tensor_tensor(out=sm[:, 4:5], in0=sm[:, 3:4], in1=sm[:, 2:3], op=ALU.subtract), ['G_sm'], ['G_sm'])
        P.op('dve', lambda e: e.tensor_scalar(out=sm[:, 5:6], in0=sm[:, 4:5], scalar1=-LAMBDA_INIT1, scalar2=None, op0=ALU.add), ['G_sm'], ['G_sm'])
        neglam = sb('G_neglam', [128, 2], F32)
        gsub = sb('G_gsub', [128, 2], F32)
        P.dma('sp', gsub[:, 0:1], I['subln_g'], [], ['G_gsub'], 'G_gsub')
        P.op('dve', lambda e: e.tensor_scalar(out=gsub[:, 1:2], in0=gsub[:, 0:1], scalar1=(1.0 - LAMBDA_INIT1), scalar2=None, op0=ALU.mult), ['G_gsub'], ['G_gsub'])
        pS = [ps(f'G_pS{i}', [128, 2, 512], F32) for i in range(2)]
        po = [ps(f'G_po{i}', [128, 512], F32) for i in range(2)]
        pl = ps('G_pl', [128, 2, 512], F32)
        P.op('pe', lambda e: e.matmul(pl[:, 0, 0:1], lhsT=k.ones_f[0:1, :], rhs=sm[0:1, 5:6], start=True, stop=True), ['ones_f', 'G_sm'], ['G_pl0', 'G_pl1'])
        P.op('dve', lambda e: e.tensor_copy(out=neglam[:, 0:1], in_=pl[:, 0, 0:1]), ['G_pl0', 'G_pl1'], ['G_neglam'])
        kh = [sb(f'G_kh{j}', [128, TK], BF16) for j in range(2)]
        vh = [sb(f'G_vh{j}', [128, NKB, 128], BF16) for j in range(2)]
        qh = [sb(f'G_qh{j}', [128, 512], BF16) for j in range(2)]
        pT = [sb(f'G_pT{i}', [128, 2, 512], BF16) for i in range(4)]
        accs = [sb(f'G_acc{i}', [128, 2, 512], F32) for i in range(2)]
        rl = sb('G_rl', [128, 2, 512], F32)
        o1 = sb('G_o1', [128, 512], F32)
        o2 = sb('G_o2', [128, 512], F32)
        sq = sb('G_sq', [128, 512], F32)
        ych = sb('G_ych', [128, 512], BF16)
        vv = S['v'].rearrange("(kb p) f -> p kb f", p=128)
        qtiles = tiles_of(TL)

        def load_head(h):
            j = h % 2
            P.dma('sp', kh[j][:, :], S['kT'][h * 128:h * 128 + 128, :], [], [f'G_kh{j}'], f'G_kh{j}')
            for b0 in range(0, NKB, 16):
                b1 = min(NKB, b0 + 16)
                P.dma('sp', vh[j][:, b0:b1, :], vv[:, b0:b1, h * 128:(h + 1) * 128], [], [f'G_vh{j}'], f'G_vh{j}')

        def load_q(h, ti):
            s, n = qtiles[ti]
            gi = (h * len(qtiles) + ti) % 2
            P.dma('sp', qh[gi][:, 0:n], S['qT'][h * 128:(h + 1) * 128, s:s + n], [], [f'G_qh{gi}'], f'G_qh{gi}')

        load_head(0)
        load_q(0, 0)
        for h in range(6):
            hj = h % 2
            for ti, (s, n) in enumerate(qtiles):
                gi = (h * len(qtiles) + ti) % 2
                qcur = qh[gi]
                qn_ = f'G_qh{gi}'
                if ti + 1 < len(qtiles):
                    load_q(h, ti + 1)
                elif h + 1 < 6:
                    load_q(h + 1, 0)
                if ti == 0 and h + 1 < 6:
                    load_head(h + 1)

                def emit_qk(kb):
                    pp = pS[kb % 2]
                    for comp in range(2):
                        P.op('pe', lambda e, comp=comp, kb=kb, pp=pp: e.matmul(pp[:, comp, 0:n], lhsT=kh[hj][64 * comp:64 * comp + 64, kb * 128:(kb + 1) * 128], rhs=qcur[64 * comp:64 * comp + 64, 0:n], start=True, stop=True),
                             [f'G_kh{hj}', qn_], [f'G_pS{kb % 2}'])
                emit_qk(0)
                first = [True, True]
                for kb in range(NKB):
                    if kb + 1 < NKB:
                        emit_qk(kb + 1)
                    pp = pS[kb % 2]
                    ppn = f'G_pS{kb % 2}'
                    pt = pT[kb % 4]
                    ptn = f'G_pT{kb % 4}'
                    P.op('act', lambda e, pp=pp, pt=pt: e.activation(out=pt[:, :, 0:n], in_=pp[:, :, 0:n], func=AF.Exp, scale=0.125), [ppn], [ptn])
                    for comp in range(2):
                        P.op('pe', lambda e, comp=comp, kb=kb, pt=pt: e.matmul(po[comp][:, 0:n], lhsT=vh[hj][:, kb, :], rhs=pt[:, comp, 0:n], start=(kb == 0), stop=(kb == NKB - 1)),
                             [f'G_vh{hj}', ptn], [f'G_po{comp}'])
                    P.op('pe', lambda e, kb=kb, pt=pt: e.matmul(pl[:, 0, 0:n], lhsT=k.ones_bf[:], rhs=pt[:, 0, 0:n], start=(kb == 0), stop=(kb == NKB - 1)),
                         ['ones_bf', ptn], ['G_pl0'])
                    ac = accs[1]
                    if kb == 0:
                        P.op('dve', lambda e, ac=ac, pt=pt: e.tensor_copy(out=ac[:, 1, 0:n], in_=pt[:, 1, 0:n]), [ptn], ['G_acc1'])
                    else:
                        P.op('dve', lambda e, ac=ac, pt=pt: e.tensor_tensor(out=ac[:, 1, 0:n], in0=ac[:, 1, 0:n], in1=pt[:, 1, 0:n], op=ALU.add), [ptn, 'G_acc1'], ['G_acc1'])
                P.op('pe', lambda e: e.matmul(pl[:, 1, 0:n], lhsT=k.ones_f[:], rhs=accs[1][:, 1, 0:n], start=True, stop=True), ['ones_f', 'G_acc1'], ['G_pl1'])
                P.op('dve', lambda e: e.reciprocal(out=rl[:, :, 0:n], in_=pl[:, :, 0:n]), ['G_pl0', 'G_pl1'], ['G_rl'])
                P.op('dve', lambda e: e.tensor_tensor(out=o1[:, 0:n], in0=po[0][:, 0:n], in1=rl[:, 0, 0:n], op=ALU.mult), ['G_po0', 'G_rl'], ['G_o1'])
                P.op('dve', lambda e: e.tensor_tensor(out=o2[:, 0:n], in0=po[1][:, 0:n], in1=rl[:, 1, 0:n], op=ALU.mult), ['G_po1', 'G_rl'], ['G_o2'])
                P.op('dve', lambda e: e.scalar_tensor_tensor(out=o1[:, 0:n], in0=o2[:, 0:n], scalar=neglam[:, 0:1], in1=o1[:, 0:n], op0=ALU.mult, op1=ALU.add),
                     ['G_o1', 'G_o2', 'G_neglam'], ['G_o1'])
                P.op('act', lambda e: e.activation(out=sq[:, 0:n], in_=o1[:, 0:n], func=AF.Square), ['G_o1'], ['G_sq'])
                P.op('pe', lambda e: e.matmul(pl[:, 0, 0:n], lhsT=k.ones_f[:], rhs=sq[:, 0:n], start=True, stop=True), ['ones_f', 'G_sq'], ['G_pl0', 'G_pl1'])
                P.op('act', lambda e: e.activation(out=sq[:, 0:n], in_=pl[:, 0, 0:n], func=AF.Sqrt, bias=k.cst[:, 0:1], scale=1.0 / 128), ['G_pl0', 'G_pl1', 'cst'], ['G_sq'])
                P.op('dve', lambda e: e.reciprocal(out=sq[:, 0:n], in_=sq[:, 0:n]), ['G_sq'], ['G_sq'])
                P.op('dve', lambda e: e.scalar_tensor_tensor(out=ych[:, 0:n], in0=o1[:, 0:n], scalar=gsub[:, 1:2], in1=sq[:, 0:n], op0=ALU.mult, op1=ALU.mult),
                     ['G_o1', 'G_sq', 'G_gsub'], ['G_ych'])
                P.dma('sp', yc_d[h * 128:(h + 1) * 128, s:s + n], ych[:, 0:n], ['G_ych'], [], 'G_ych')
    P.barrier()
    if stop_after == 'l1F1':
        return

    with ExitStack() as st:
        sb = lambda n, s, d: k.sb(n, s, d, st)
        ps = lambda n, s, d: k.ps(n, s, d, st)
        w_out = sb('w_out1', [128, 8, D], BF16)
        with ExitStack() as st2:
            fold_w_bf16(k, st2, 'w_out1', w_out, I['cd_w_out'], 8, k.gbc_d[1, 0, 0])
            P.barrier()
        cp = {}
        for nm, shp in (('conf_w', [128, 2, 31]), ('conf_b', [128, 2]), ('conf_lng', [128, 2]), ('conf_lnb', [128, 2])):
            cp[nm] = sb('H_' + nm, shp, F32)
            P.dma('sp', cp[nm][:], I[nm], [], ['H_' + nm], 'H_' + nm)
        yT = sb('H_yT', [128, 8, 512], BF16)
        gin = sb('H_gin', [128, 2, 542], F32)
        ca = sb('H_ca', [128, 512], F32)
        cb_ = sb('H_cb', [128, 512], F32)
        ct = [sb(f'H_ct{i}', [128, 512], F32) for i in range(2)]
        xm = sb('H_xm', [128, 2, 512], F32)
        sq = sb('H_sq', [128, 2, 512], F32)
        rstd = sb('H_rstd', [128, 512], F32)
        xt = sb('H_xt', [128, 4, D], F32)
        xo = sb('H_xo', [128, 4, D], F32)
        pm = ps('H_pm', [128, 512], F32)
        pv = ps('H_pv', [128, 512], F32)
        po = [ps(f'H_po{i}', [128, 512], F32) for i in range(4)]
        glv = S['gl'].rearrange("(c p) t -> p c t", p=128)
        ycv = yc_d.rearrange("(c p) t -> p c t", p=128)
        for (s, n) in tiles_of(TL):
            ng = n // 128
            P.dma('sp', yT[:, 0:6, 0:n], ycv[:, :, s:s + n], [], keys('H_yT', 8)[0:6], 'H_yT')
            P.dma('sp', gin[:, :, 0:n + 30], glv[:, :, PADZ + s - 15:PADZ + s + n + 15], [], keys('H_gin', 2), 'H_gin')
            P.dma('sp', xt[:, 0:ng, :], S['x1_l'][1 + s:1 + s + n, :].rearrange("(g p) f -> p g f", p=128), [], ['H_xt'], 'H_xt')
            for c in range(2):
                P.op('act', lambda e, c=c: e.activation(out=ca[:, 0:n], in_=gin[:, c, 0:n], func=AF.Identity, scale=cp['conf_w'][:, c, 0:1], bias=cp['conf_b'][:, c:c + 1]),
                     [f'H_gin.{c}', 'H_conf_w', 'H_conf_b'], ['H_ca'])
                P.op('pool', lambda e, c=c: e.tensor_scalar(out=cb_[:, 0:n], in0=gin[:, c, 1:1 + n], scalar1=cp['conf_w'][:, c, 1:2], scalar2=None, op0=ALU.mult),
                     [f'H_gin.{c}', 'H_conf_w'], ['H_cb'])
                for t in range(2, 31):
                    if t % 2 == 0:
                        P.op('dve', lambda e, c=c, t=t: e.scalar_tensor_tensor(out=ca[:, 0:n], in0=gin[:, c, t:t + n], scalar=cp['conf_w'][:, c, t:t + 1], in1=ca[:, 0:n],
                                                                               op0=ALU.mult, op1=ALU.add), [f'H_gin.{c}', 'H_ca'], ['H_ca'])
                    else:
                        ctt = ct[(t // 2) % 2]
                        ctn = f'H_ct{(t // 2) % 2}'
                        P.op('act', lambda e, c=c, t=t, ctt=ctt: e.activation(out=ctt[:, 0:n], in_=gin[:, c, t:t + n], func=AF.Identity, scale=cp['conf_w'][:, c, t:t + 1], bias=k.cst[:, 2:3]),
                             [f'H_gin.{c}', 'H_conf_w', 'cst'], [ctn])
                        P.op('pool', lambda e, ctt=ctt: e.tensor_tensor(out=cb_[:, 0:n], in0=cb_[:, 0:n], in1=ctt[:, 0:n], op=ALU.add), [ctn, 'H_cb'], ['H_cb'])
                P.op('dve', lambda e, c=c: e.tensor_tensor(out=xm[:, c, 0:n], in0=ca[:, 0:n], in1=cb_[:, 0:n], op=ALU.add), ['H_ca', 'H_cb'], [f'H_xm.{c}'])
            for c in range(2):
                P.op('pe', lambda e, c=c: e.matmul(pm[:, 0:n], lhsT=k.ones_f[:], rhs=xm[:, c, 0:n], start=(c == 0), stop=(c == 1)), ['ones_f', f'H_xm.{c}'], ['H_pm'])
            for c in range(2):
                P.op('dve', lambda e, c=c: e.scalar_tensor_tensor(out=xm[:, c, 0:n], in0=pm[:, 0:n], scalar=-1.0 / 256, in1=xm[:, c, 0:n], op0=ALU.mult, op1=ALU.add),
                     ['H_pm', f'H_xm.{c}'], [f'H_xm.{c}'])
                P.op('act', lambda e, c=c: e.activation(out=sq[:, c, 0:n], in_=xm[:, c, 0:n], func=AF.Square), [f'H_xm.{c}'], [f'H_sq.{c}'])
            for c in range(2):
                P.op('pe', lambda e, c=c: e.matmul(pv[:, 0:n], lhsT=k.ones_f[:], rhs=sq[:, c, 0:n], start=(c == 0), stop=(c == 1)), ['ones_f', f'H_sq.{c}'], ['H_pv'])
            P.op('act', lambda e: e.activation(out=rstd[:, 0:n], in_=pv[:, 0:n], func=AF.Sqrt, bias=k.cst[:, 0:1], scale=1.0 / 256), ['H_pv', 'cst'], ['H_rstd'])
            P.op('dve', lambda e: e.reciprocal(out=rstd[:, 0:n], in_=rstd[:, 0:n]), ['H_rstd'], ['H_rstd'])
            for c in range(2):
                P.op('dve', lambda e, c=c: e.tensor_tensor(out=xm[:, c, 0:n], in0=xm[:, c, 0:n], in1=rstd[:, 0:n], op=ALU.mult), [f'H_xm.{c}', 'H_rstd'], [f'H_xm.{c}'])
                P.op('act', lambda e, c=c: e.activation(out=yT[:, 6 + c, 0:n], in_=xm[:, c, 0:n], func=AF.Silu, scale=cp['conf_lng'][:, c:c + 1], bias=cp['conf_lnb'][:, c:c + 1]),
                     [f'H_xm.{c}', 'H_conf_lng', 'H_conf_lnb'], [f'H_yT.{6 + c}'])
            for g in range(ng):
                for nt in range(2):
                    pp = po[(2 * g + nt) % 4]
                    ppn = f'H_po{(2 * g + nt) % 4}'
                    for kc in range(8):
                        P.op('pe', lambda e, g=g, nt=nt, kc=kc, pp=pp: e.matmul(pp[:, :], lhsT=yT[:, kc, g * 128:(g + 1) * 128], rhs=w_out[:, kc, nt * 512:(nt + 1) * 512],
                                                                               start=(kc == 0), stop=(kc == 7)), [f'H_yT.{kc}', f'w_out1.{kc}'], [ppn])
                    P.op('dve', lambda e, g=g, nt=nt, pp=pp: e.tensor_tensor(out=xo[:, g, nt * 512:(nt + 1) * 512], in0=pp[:, :], in1=xt[:, g, nt * 512:(nt + 1) * 512], op=ALU.add),
                         [ppn, 'H_xt'], [f'H_xo.{g}'])
            P.dma('sp', S['x15'][1 + s:1 + s + n, :].rearrange("(g p) f -> p g f", p=128), xo[:, 0:ng, :], keys('H_xo', 4)[0:ng], [], 'H_xo')
    P.barrier()
    if stop_after == 'l1F2':
        return
    ffn_phase(k, 1, 0, S['x15'], None, TL, final=True, flush=(not k.half), out_rows=k.TO)
    P.barrier()


def _fm(v, nch):
    return np.ascontiguousarray(np.asarray(v, np.float32).reshape(nch, 128).T)


def prep_shared(inp, T):
    f = lambda a: np.ascontiguousarray(np.asarray(a, np.float32))
    d = {}
    d['mod_w'] = f(inp['mod_w'])
    d['mod_b'] = f(inp['mod_b'])
    d['modb_fm'] = f(np.asarray(inp['mod_b']).reshape(2, 6, 8, 128).transpose(3, 0, 1, 2))
    ng = np.stack([np.asarray(inp['norm_mix_g']), np.asarray(inp['norm_ffn_g'])], axis=1)
    d['ng_fm'] = f(ng.reshape(2, 2, 8, 128).transpose(3, 0, 1, 2))
    d['final_g_bc'] = f(np.broadcast_to(np.asarray(inp['final_g'])[None, :], (128, D)))
    d['ident'] = f(np.eye(128))
    d['ab_w_in'] = f(inp['ab_w_in'][0])
    d['ab_w_out'] = f(inp['ab_w_out'][0])
    cw4 = np.asarray(inp['lru_conv_w'][0], np.float32).reshape(4, 6, 128).transpose(2, 1, 0)
    d['lru_cw'] = f(np.concatenate([cw4, np.zeros((128, 6, 1), np.float32)], axis=2))
    d['pool_fl'] = f(np.stack([np.ones(128), np.zeros(128)], axis=1))
    d['lru_cb'] = _fm(inp['lru_conv_b'][0], 6)
    for nm, src in (('lru_bdA', inp['lru_wa'][0]), ('lru_bdX', inp['lru_wx'][0])):
        src = np.asarray(src)
        bd = np.zeros((128, 2, 6, 128), np.float32)
        for dd in range(2):
            for c in range(6):
                bd[0:64, dd, c, 0:64] = src[dd, 2 * c]
                bd[64:128, dd, c, 64:128] = src[dd, 2 * c + 1]
        d[nm] = bd
    for nm, src in (('lru_ba', inp['lru_ba'][0]), ('lru_bx', inp['lru_bx'][0]), ('lru_lam', inp['lru_lambda'][0])):
        d[nm] = f(np.asarray(src).reshape(2, 6, 128).transpose(2, 0, 1))
    pw = np.asarray(inp['pool_w'][0])
    bd = np.zeros((128, 2, 128), np.float32)
    for ch in range(2):
        bd[0:64, ch, 0:64] = pw[2 * ch]
        bd[64:128, ch, 64:128] = pw[2 * ch + 1]
    d['pool_bd'] = bd
    d['pool_scale'] = _fm(inp['pool_scale'][0], 2)
    invw = np.zeros((128, 2), np.float32)
    corr = np.ones((128, 2, 2, 16), np.float32)
    wins = (2, 4, 8, 16)
    Lbig = 1 << 20
    for g, w in enumerate(wins):
        ch, half = g // 2, g % 2
        psl = slice(64 * half, 64 * half + 64)
        invw[psl, ch] = 1.0 / w
        for i in range(16):
            t = i
            cnt = (t + w - w // 2) - max(t - w // 2, 0)
            corr[psl, ch, 0, i] = float(w) / cnt
            t = Lbig - 16 + i
            cnt = min(t + w - w // 2, Lbig) - (t - w // 2)
            corr[psl, ch, 1, i] = float(w) / cnt
    d['pool_invw'] = invw
    d['pool_corr'] = corr
    d['ffn_w_up'] = f(inp['ffn_w_up'])
    d['ffn_w_down'] = f(inp['ffn_w_down'])
    d['ffn_cw'] = f(np.asarray(inp['ffn_conv_w']).reshape(2, 3, 2 * NFC, 128).transpose(3, 0, 2, 1))
    d['ffn_cb'] = f(np.asarray(inp['ffn_conv_b']).reshape(2, 2 * NFC, 128).transpose(2, 0, 1))
    w_in = np.asarray(inp['cd_w_in'][0], np.float32)
    d['cd_w_in'] = f(w_in)
    qk = w_in[:, :1536].reshape(D, 1536 // 32, 2, 16)
    d['cd_w_sw'] = f(qk[:, :, ::-1, :].reshape(D, 1536))
    d['cd_w_out'] = f(inp['cd_w_out'][0])
    t = np.arange(T)
    row = (t // GRID_W).astype(np.float32)
    col = (t % GRID_W).astype(np.float32)
    inv = (10000.0 ** (-np.arange(16, dtype=np.float32) / 16)).astype(np.float32)
    ang_r = (row[:, None] * inv).astype(np.float32)
    ang_c = (col[:, None] * inv).astype(np.float32)
    rc = np.zeros((128, T), np.float32)
    rs = np.zeros((128, T), np.float32)
    for p in range(128):
        dd = p % 64
        ang = ang_r if dd < 32 else ang_c
        fq = dd % 16
        first = (dd % 32) < 16
        rc[p] = np.cos(ang[:, fq])
        rs[p] = (-1.0 if first else 1.0) * np.sin(ang[:, fq])
    d['rope_c'] = rc
    d['rope_s'] = rs
    d['diff_l'] = f(np.stack([np.asarray(inp['diff_lq1'][0]), np.asarray(inp['diff_lk1'][0]),
                              np.asarray(inp['diff_lq2'][0]), np.asarray(inp['diff_lk2'][0])])[None])
    d['subln_g'] = f(np.asarray(inp['diff_subln_g'][0]).reshape(128, 1))
    d['conf_w'] = f(np.asarray(inp['conf_dw_w'][0]).reshape(31, 2, 128).transpose(2, 1, 0))
    d['conf_b'] = _fm(inp['conf_dw_b'][0], 2)
    d['conf_lng'] = _fm(inp['conf_ln_g'][0], 2)
    d['conf_lnb'] = _fm(inp['conf_ln_b'][0], 2)
    return d


def prep_rev(shared, T):
    f = lambda a: np.ascontiguousarray(np.asarray(a, np.float32))
    d = dict(shared)
    cw = shared['lru_cw']
    d['lru_cw'] = f(cw[:, :, ::-1])
    for nm in ('lru_bdA', 'lru_bdX', 'lru_ba', 'lru_bx', 'lru_lam'):
        d[nm] = f(shared[nm][:, ::-1])
    d['pool_fl'] = f(np.stack([np.zeros(128), np.ones(128)], axis=1))
    corr = np.ones((128, 2, 2, 16), np.float32)
    Lbig = 1 << 20
    for g, w in enumerate((2, 4, 8, 16)):
        ch, half = g // 2, g % 2
        psl = slice(64 * half, 64 * half + 64)
        for i in range(16):
            r = i
            cnt = (r + w // 2) - max(r - w // 2 + 1, 0) + 1
            corr[psl, ch, 0, i] = float(w) / cnt
            r = Lbig - 16 + i
            cnt = min(r + w // 2, Lbig - 1) - (r - w // 2 + 1) + 1
            corr[psl, ch, 1, i] = float(w) / cnt
    d['pool_corr'] = corr
    d['ffn_cw'] = f(shared['ffn_cw'][:, :, :, ::-1])
    d['conf_w'] = f(shared['conf_w'][:, :, ::-1])
    d['rope_c'] = f(shared['rope_c'][:, ::-1])
    d['rope_s'] = f(shared['rope_s'][:, ::-1])
    return d


def prep_core(inp, shared, b, T, rev=False):
    m = dict(shared)
    x = np.asarray(inp['x'][b, :T], np.float32)
    ctx = np.asarray(inp['ctx'][b], np.float32)
    if rev:
        x = x[::-1]
        ctx = ctx[::-1]
    m['x'] = np.ascontiguousarray(x)
    m['ctx'] = np.ascontiguousarray(ctx)
    m['c_fm'] = np.ascontiguousarray(np.stack([_fm(inp['c'][b], 8), _fm(inp['c_ctx'], 8)], axis=1))
    return m


_CACHE = {}


def kernel(**inputs):
    T = inputs['x'].shape[1]
    Bn = inputs['x'].shape[0]
    if T not in _CACHE:
        _CACHE[T] = build(T)
    kk = _CACHE[T]
    shared = prep_shared(inputs, T)
    shared_r = prep_rev(shared, T)
    in_maps = [prep_core(inputs, shared, b, T, rev=False) for b in range(Bn)] + \
              [prep_core(inputs, shared_r, b, T, rev=True) for b in range(Bn)]
    res = run_bass_kernel_spmd(kk.nc, in_maps, core_ids=list(range(2 * Bn)))
    out = np.empty((Bn, T, D), np.float32)
    for b in range(Bn):
        out[b, :T // 2] = np.asarray(res.results[b]['out'], np.float32)
        out[b, T // 2:] = np.asarray(res.results[Bn + b]['out'], np.float32)[::-1]
    return out
```

```python
import numpy as np
import math
import os
from contextlib import ExitStack
import concourse.bass as bass
import concourse.mybir as mybir
from concourse.bass_utils import run_bass_kernel_spmd
from concourse.ap import AP

F32 = mybir.dt.float32
BF16 = mybir.dt.bfloat16
AF = mybir.ActivationFunctionType
ALU = mybir.AluOpType
AX = mybir.AxisListType

D = 1024
LC = 256
DFF = 2816
NFC = DFF // 128
PADZ = 16
EPS = 1e-6
GRID_W = 64
LAMBDA_INIT1 = 0.8 - 0.6 * math.exp(-0.3 * 1)
SAME_ENGINE_SYNC = True


def rev(ap):
    a = [list(x) for x in ap.ap]
    st, n = a[-1]
    a[-1] = [-st, n]
    return AP(ap.tensor, ap.offset + st * (n - 1), a)


class Prog:
    def __init__(self, nc, es):
        self.nc = nc
        self.es = es
        self.eng = {'pe': nc.tensor, 'act': nc.scalar, 'dve': nc.vector, 'pool': nc.gpsimd, 'sp': nc.sync}
        self.esem = {e: es.enter_context(nc.semaphore('S_' + e)) for e in ('pe', 'act', 'dve', 'pool')}
        self.ecnt = {e: 0 for e in self.esem}
        self.dsem = {}
        self.dpool = []
        self.nd = 0
        self.waited = {e: {} for e in self.eng}
        self.lastw = {}
        self.rd = {}
        self.ninstr = 0

    def _need(self, reads, writes):
        ev = {}

        def add(e):
            if e is None:
                return
            k, sem, val = e
            if k not in ev or ev[k][1] < val:
                ev[k] = (sem, val)
        for r in reads:
            add(self.lastw.get(r))
        for w in writes:
            add(self.lastw.get(w))
            for e in self.rd.get(w, {}).items():
                add((e[0], e[1][0], e[1][1]))
        return ev

    def _wait(self, e, ev):
        for k, (sem, val) in ev.items():
            if k == 'S_' + e and (e == 'pe' or not SAME_ENGINE_SYNC):
                continue
            if self.waited[e].get(k, 0) < val:
                self.eng[e].wait_ge(sem, val)
                self.waited[e][k] = val
                self.ninstr += 1

    def _commit(self, ev, reads, writes):
        k, sem, val = ev
        for w in writes:
            self.lastw[w] = ev
            self.rd[w] = {}
        for r in reads:
            self.rd.setdefault(r, {})[k] = (sem, val)

    def op(self, e, fn, reads, writes):
        self._wait(e, self._need(reads, writes))
        ins = fn(self.eng[e])
        self.ecnt[e] += 1
        ins.then_inc(self.esem[e], 1)
        self.ninstr += 1
        self._commit(('S_' + e, self.esem[e], self.ecnt[e]), reads, writes)

    def dma(self, q, out, in_, reads, writes, key):
        self._wait(q, self._need(reads, writes))
        if key not in self.dsem:
            if self.dpool:
                self.dsem[key] = self.dpool.pop()
            else:
                nm = 'D%d' % self.nd
                self.nd += 1
                self.dsem[key] = [self.es.enter_context(self.nc.semaphore(nm)), 0, nm]
        d = self.dsem[key]
        ins = self.eng[q].dma_start(out=out, in_=in_)
        d[1] += 16
        ins.then_inc(d[0], 16)
        self.ninstr += 1
        self._commit((d[2], d[0], d[1]), reads, writes)

    def barrier(self):
        ev = {}
        for e in self.esem:
            if self.ecnt[e] > 0:
                ev['S_' + e] = (self.esem[e], self.ecnt[e])
        for k, d in self.dsem.items():
            if d[1] > 0:
                ev[d[2]] = (d[0], d[1])
        for e in self.eng:
            self._wait(e, dict(ev))
        self.lastw = {}
        self.rd = {}
        for k, d in self.dsem.items():
            self.dpool.append(d)
        self.dsem = {}


def keys(name, n):
    return [f"{name}.{i}" for i in range(n)]


def tiles_of(T, w=512):
    out = []
    s = 0
    while s < T:
        n = min(w, T - s)
        out.append((s, n))
        s += n
    return out


class K:
    pass


def build(T, dbg=False, stop_after=None, half=True):
    nc = bass.Bass("TRN2", target_bir_lowering=False)
    k = K()
    k.nc = nc
    k.T = T
    k.half = half
    k.TL = (T // 2 + 128) if half else T
    k.TO = (T // 2) if half else T

    def din(name, shape, dt=F32):
        return nc.dram_tensor(name, list(shape), dt, kind="ExternalInput").ap()

    def dscr(name, shape, dt=F32, out=False):
        kind = "ExternalOutput" if (out or dbg) else "Internal"
        return nc.dram_tensor(name, list(shape), dt, kind=kind).ap()

    I = {}
    I['x'] = din('x', [T, D])
    I['ctx'] = din('ctx', [LC, D])
    I['c_fm'] = din('c_fm', [128, 2, 8])
    I['mod_w'] = din('mod_w', [2, D, 6 * D])
    I['modb_fm'] = din('modb_fm', [128, 2, 6, 8])
    I['mod_b'] = din('mod_b', [2, 6 * D])
    I['ng_fm'] = din('ng_fm', [128, 2, 2, 8])
    I['final_g_bc'] = din('final_g_bc', [128, D])
    I['ident'] = din('ident', [128, 128])
    I['ab_w_in'] = din('ab_w_in', [D, 1792])
    I['ab_w_out'] = din('ab_w_out', [D, D])
    I['lru_cw'] = din('lru_cw', [128, 6, 5])
    I['pool_fl'] = din('pool_fl', [128, 2])
    I['lru_cb'] = din('lru_cb', [128, 6])
    I['lru_bdA'] = din('lru_bdA', [128, 2, 6, 128])
    I['lru_bdX'] = din('lru_bdX', [128, 2, 6, 128])
    I['lru_ba'] = din('lru_ba', [128, 2, 6])
    I['lru_bx'] = din('lru_bx', [128, 2, 6])
    I['lru_lam'] = din('lru_lam', [128, 2, 6])
    I['pool_bd'] = din('pool_bd', [128, 2, 128])
    I['pool_scale'] = din('pool_scale', [128, 2])
    I['pool_invw'] = din('pool_invw', [128, 2])
    I['pool_corr'] = din('pool_corr', [128, 2, 2, 16])
    I['ffn_w_up'] = din('ffn_w_up', [2, D, 2 * DFF])
    I['ffn_w_down'] = din('ffn_w_down', [2, DFF, D])
    I['ffn_cw'] = din('ffn_cw', [128, 2, 2 * NFC, 3])
    I['ffn_cb'] = din('ffn_cb', [128, 2, 2 * NFC])
    I['cd_w_in'] = din('cd_w_in', [D, 2816])
    I['cd_w_sw'] = din('cd_w_sw', [D, 1536])
    I['cd_w_out'] = din('cd_w_out', [D, D])
    I['rope_c'] = din('rope_c', [128, T])
    I['rope_s'] = din('rope_s', [128, T])
    I['diff_l'] = din('diff_l', [1, 4, 64])
    I['subln_g'] = din('subln_g', [128, 1])
    I['conf_w'] = din('conf_w', [128, 2, 31])
    I['conf_b'] = din('conf_b', [128, 2])
    I['conf_lng'] = din('conf_lng', [128, 2])
    I['conf_lnb'] = din('conf_lnb', [128, 2])
    k.I = I

    k.out = nc.dram_tensor('out', [k.TO, D], F32, kind="ExternalOutput").ap()
    S = {}
    for nm, TT in (('l', T), ('c', LC)):
        S['z_' + nm] = dscr('z_' + nm, [1792, PADZ + TT + PADZ])
        S['xa_' + nm] = dscr('xa_' + nm, [768, TT])
        S['hf_' + nm] = dscr('hf_' + nm, [768, TT])
        S['x05_' + nm] = dscr('x05_' + nm, [1 + TT, D])
        S['x1_' + nm] = dscr('x1_' + nm, [1 + TT + 128, D])
    S['x15'] = dscr('x15', [1 + T, D])
    S['x2'] = dscr('x2', [1 + T + 128, D])
    S['qT'] = dscr('qT', [768, T], BF16)
    S['kT'] = dscr('kT', [768, T + LC], BF16)
    S['v'] = dscr('v', [T + LC, 768], BF16)
    S['gl'] = dscr('gl', [256, PADZ + T + PADZ])
    k.S = S

    with ExitStack() as es:
        P = Prog(nc, es)
        k.P = P

        uid = [0]

        def sb(name, shape, dt, st=es):
            uid[0] += 1
            return st.enter_context(nc.sbuf_tensor(f"{name}_s{uid[0]}", list(shape), dt))

        def ps(name, shape, dt, st=es):
            uid[0] += 1
            return st.enter_context(nc.psum_tensor(f"{name}_p{uid[0]}", list(shape), dt))
        k.sb = sb
        k.ps = ps

        ident = sb('ident', [128, 128], BF16)
        P.dma('pool', ident[:], I['ident'], [], ['ident'], 'ident')
        k.ident = ident
        cst = sb('cst', [128, 8], F32)
        P.op('dve', lambda e: e.memset(cst[:, 0:1], EPS), [], ['cst'])
        P.op('dve', lambda e: e.memset(cst[:, 1:2], 1.0), [], ['cst'])
        P.op('dve', lambda e: e.memset(cst[:, 2:3], 0.0), [], ['cst'])
        k.cst = cst
        zero = sb('zero', [128, 512], F32)
        P.op('dve', lambda e: e.memset(zero[:], 0.0), [], ['zero'])
        k.zero = zero
        ones_bf = sb('ones_bf', [128, 128], BF16)
        P.op('dve', lambda e: e.memset(ones_bf[:], 1.0), [], ['ones_bf'])
        k.ones_bf = ones_bf
        ones_f = sb('ones_f', [128, 128], F32)
        P.op('dve', lambda e: e.memset(ones_f[:], 1.0), [], ['ones_f'])
        k.ones_f = ones_f

        modfm = sb('modfm', [128, 2, 2, 4, 8], F32)
        k.modfm = modfm
        k.gbc_d = nc.dram_tensor('gbc_d', [2, 2, 2, 128, D], F32, kind="Internal").ap()

        if os.environ.get('DBG_ONLY_F1'):
            layer1(k, 'l1F1')
            return finish(k, es)
        phase_adaln(k)
        P.barrier()
        if dbg:
            mdbg = nc.dram_tensor('modfm_dbg', [128, 2, 2, 4, 8], F32, kind="ExternalOutput").ap()
            P.dma('sp', mdbg, modfm[:], [], [], 'mdbg')
            gdbg = nc.dram_tensor('gbc_dbg', [2, 2, 2, 128, D], F32, kind="ExternalOutput").ap()
            P.dma('sp', gdbg, k.gbc_d, [], [], 'gdbg')
        if stop_after == 'adaln':
            return finish(k, es)

        layer0(k, stop_after)
        if stop_after is not None and stop_after.startswith('l0'):
            return finish(k, es)
        layer1(k, stop_after)
        return finish(k, es)


def finish(k, es):
    k.P.barrier()
    k.ninstr = k.P.ninstr
    return k


def phase_adaln(k):
    nc, P, I = k.nc, k.P, k.I
    with ExitStack() as st:
        sb = lambda n, s, d: k.sb(n, s, d, st)
        ps = lambda n, s, d: k.ps(n, s, d, st)
        cf = sb('ad_cf', [128, 2, 8], F32)
        P.dma('sp', cf[:], I['c_fm'], [], ['ad_cf'], 'ad_cf')
        sc = sb('ad_sc', [128, 2, 8], F32)
        P.op('act', lambda e: e.activation(out=sc[:], in_=cf[:], func=AF.Silu), ['ad_cf'], ['ad_sc'])
        rep = sb('ad_rep', [128, 2, 8, 128], F32)
        for s_ in range(2):
            for kc in range(8):
                P.op('dve', lambda e, s_=s_, kc=kc: e.tensor_copy(out=rep[:, s_, kc, :], in_=sc[:, s_, kc:kc + 1].to_broadcast([128, 128])),
                     ['ad_sc'], [f'ad_rep.{s_}.{kc}'])
        modb = sb('ad_modb', [128, 2, 6, 8], F32)
        P.dma('sp', modb[:], I['modb_fm'], [], ['ad_modb'], 'ad_modb')
        ng = sb('ad_ng', [128, 2, 2, 8], F32)
        P.dma('sp', ng[:], I['ng_fm'], [], ['ad_ng'], 'ad_ng')
        brow = sb('ad_brow', [1, 2, 6 * D], F32)
        P.dma('sp', brow[:], I['mod_b'].rearrange("(o l) n -> o l n", o=1), [], ['ad_brow'], 'ad_brow')
        wt = [sb(f'ad_w{i}', [128, 6 * D], F32) for i in range(2)]
        gst = sb('ad_gst', [128, 2, D], F32)
        facc = sb('ad_facc', [128, 32, 2], F32)
        pfm = ps('ad_pfm', [128, 32, 2], F32)
        pbc = [ps(f'ad_pbc{i}', [128, 512], F32) for i in range(4)]
        for l in range(2):
            for pss in range(2):
                for kc in range(8):
                    w = wt[kc % 2]
                    wk = f'ad_w{kc % 2}'
                    P.dma('sp', w[:], I['mod_w'][l, kc * 128:(kc + 1) * 128, :], [], [wk], wk)
                    if pss == 0:
                        jmap = [0, 1, 3, 4]
                        for jj, j in enumerate(jmap):
                            for fc in range(8):
                                col = j * D + fc * 128
                                P.op('pe', lambda e, w=w, col=col, jj=jj, fc=fc, kc=kc: e.matmul(
                                    pfm[:, jj * 8 + fc, :], lhsT=w[:, col:col + 128], rhs=sc[:, :, kc],
                                    start=True, stop=True), [wk, 'ad_sc'], ['ad_pfm'])
                        if kc == 0:
                            P.op('dve', lambda e: e.tensor_copy(out=facc[:], in_=pfm[:]), ['ad_pfm'], ['ad_facc'])
                        else:
                            P.op('dve', lambda e: e.tensor_tensor(out=facc[:], in0=pfm[:], in1=facc[:], op=ALU.add), ['ad_pfm', 'ad_facc'], ['ad_facc'])
                    if True:
                        s_ = pss
                        for nt in range(4):
                            gj = 2 if nt < 2 else 5
                            col = gj * D + (nt % 2) * 512
                            P.op('pe', lambda e, w=w, col=col, nt=nt, kc=kc, s_=s_: e.matmul(
                                pbc[nt][:], lhsT=rep[:, s_, kc, :], rhs=w[:, col:col + 512],
                                start=(kc == 0), stop=False), [wk, f'ad_rep.{s_}.{kc}'], [f'ad_pbc{nt}'])
                if pss == 0:
                    jmap = [0, 1, 3, 4]
                    for s_ in range(2):
                        for jj, j in enumerate(jmap):
                            P.op('dve', lambda e, s_=s_, jj=jj, j=j, l=l: e.tensor_tensor(
                                out=k.modfm[:, l, s_, jj, :], in0=facc[:, jj * 8:(jj + 1) * 8, s_], in1=modb[:, l, j, :], op=ALU.add),
                                ['ad_facc', 'ad_modb'], [f'modfm.{l}.{s_}.{jj}'])
                        for jj, which in ((1, 0), (3, 1)):
                            P.op('dve', lambda e, s_=s_, jj=jj, which=which, l=l: e.scalar_tensor_tensor(
                                out=k.modfm[:, l, s_, jj, :], in0=k.modfm[:, l, s_, jj, :], scalar=1.0, in1=ng[:, l, which, :],
                                op0=ALU.add, op1=ALU.mult), [f'modfm.{l}.{s_}.{jj}', 'ad_ng'], [f'modfm.{l}.{s_}.{jj}'])
                if True:
                    s_ = pss
                    for nt in range(4):
                        gj = 2 if nt < 2 else 5
                        col = gj * D + (nt % 2) * 512
                        P.op('pe', lambda e, nt=nt, col=col, l=l: e.matmul(
                            pbc[nt][:], lhsT=k.ones_f[0:1, :], rhs=brow[0:1, l, col:col + 512], start=False, stop=True),
                            ['ones_f', 'ad_brow'], [f'ad_pbc{nt}'])
                        P.op('act', lambda e, nt=nt: e.copy(out=gst[:, nt // 2, (nt % 2) * 512:(nt % 2) * 512 + 512], in_=pbc[nt][:]),
                            [f'ad_pbc{nt}'], [f'ad_gst.{nt}'])
                    for j2 in range(2):
                        P.dma('sp', k.gbc_d[l, s_, j2], gst[:, j2, :], [f'ad_gst.{2 * j2}', f'ad_gst.{2 * j2 + 1}'], [], 'ad_gst')
        P.barrier()


def load_w_bf16(k, name, dst, src_ap, nk, ncols, colblk=2048):
    P = k.P
    for kc in range(nk):
        for c0 in range(0, ncols, colblk):
            c1 = min(ncols, c0 + colblk)
            P.dma('pool', dst[:, kc, c0:c1], src_ap[kc * 128:(kc + 1) * 128, c0:c1], [], [f'{name}.{kc}'], f'{name}.{kc}')


def fold_w_bf16(k, st, name, dst, src_ap, nk, gb_dram):
    P = k.P
    stg = [k.sb(f'{name}_stg{i}', [128, D], F32, st) for i in range(2)]
    gb = k.sb(f'{name}_gb', [128, D], F32, st)
    P.dma('sp', gb[:], gb_dram, [], [f'{name}_gb'], f'{name}_gb')
    gb_ap = gb[:]
    for kc in range(nk):
        s_ = stg[kc % 2]
        sk = f'{name}_stg{kc % 2}'
        P.dma('sp', s_[:], src_ap[kc * 128:(kc + 1) * 128, :], [], [sk], sk)
        P.op('dve', lambda e, s_=s_, kc=kc: e.tensor_tensor(out=dst[:, kc, :], in0=s_[:], in1=gb_ap, op=ALU.mult),
             [sk, f'{name}_gb'], [f'{name}.{kc}'])


def modulate_tile(k, B, src_rows, n, l, s_, jsh, hT, hTname):
    P = k.P
    ng = n // 128
    X, Xn = B['xt'], B['xtname']
    ss = B['ss']
    xn = B['xn']
    for g0 in range(0, ng, 2):
        gg = min(2, ng - g0)
        P.dma('sp', X[:, 0:gg, :], src_rows[g0 * 128:(g0 + gg) * 128, :].rearrange("(g p) f -> p g f", p=128), [], [Xn], Xn)
        P.op('dve', lambda e: e.memset(ss[:, 0:2], 0.0), [], [B['ssname']])
        for g in range(gg):
            P.op('act', lambda e, g=g: e.activation(out=B['junk'][:], in_=X[:, g, :], func=AF.Square, accum_out=ss[:, g:g + 1]),
                 [Xn], [B['ssname'], B['junkname']])
        P.op('act', lambda e, gg=gg: e.activation(out=ss[:, 4:4 + gg], in_=ss[:, 0:gg], func=AF.Sqrt, bias=k.cst[:, 0:1], scale=1.0 / D),
             [B['ssname'], 'cst'], [B['ssname']])
        P.op('dve', lambda e, gg=gg: e.reciprocal(out=ss[:, 8:8 + gg], in_=ss[:, 4:4 + gg]), [B['ssname']], [B['ssname']])
        for g in range(gg):
            if g % 2 == 0:
                P.op('dve', lambda e, g=g, g0=g0: e.tensor_scalar(out=xn[:, g0 + g, :], in0=X[:, g, :], scalar1=ss[:, 8 + g:9 + g], scalar2=None, op0=ALU.mult),
                     [Xn, B['ssname']], [f"{B['xnname']}.{g0 + g}"])
            else:
                P.op('act', lambda e, g=g, g0=g0: e.activation(out=xn[:, g0 + g, :], in_=X[:, g, :], func=AF.Identity, scale=ss[:, 8 + g:9 + g], bias=k.cst[:, 2:3]),
                     [Xn, B['ssname'], 'cst'], [f"{B['xnname']}.{g0 + g}"])
    for fc in range(8):
        tp = B['tp'][fc % 2]
        tpn = B['tpname'][fc % 2]
        for g in range(ng):
            P.op('pe', lambda e, g=g, fc=fc, tp=tp: e.transpose(out=tp[:, g * 128:(g + 1) * 128], in_=xn[:, g, fc * 128:(fc + 1) * 128], identity=k.ident[:]),
                 [f"{B['xnname']}.{g}", 'ident'], [tpn])
        P.op('act', lambda e, fc=fc, tp=tp: e.activation(out=hT[:, fc, 0:n], in_=tp[:, 0:n], func=AF.Identity,
                                                          scale=k.modfm[:, l, s_, jsh + 1, fc:fc + 1], bias=k.modfm[:, l, s_, jsh, fc:fc + 1]),
             [tpn], [f'{hTname}.{fc}'])


def mod_bufs(k, st, pfx):
    B = {}
    B['xt'] = k.sb(pfx + 'xt', [128, 2, D], F32, st)
    B['xtname'] = pfx + 'xt'
    B['xn'] = k.sb(pfx + 'xn', [128, 4, D], BF16, st)
    B['xnname'] = pfx + 'xn'
    B['junk'] = k.sb(pfx + 'junk', [128, D], BF16, st)
    B['junkname'] = pfx + 'junk'
    B['ss'] = k.sb(pfx + 'ss', [128, 12], F32, st)
    B['ssname'] = pfx + 'ss'
    B['tp'] = [k.ps(pfx + f'tp{i}', [128, 512], BF16, st) for i in range(2)]
    B['tpname'] = [pfx + f'tp{i}' for i in range(2)]
    return B


def layer0(k, stop_after):
    nc, P, I, S = k.nc, k.P, k.I, k.S
    with ExitStack() as st:
        sb = lambda n, s, d: k.sb(n, s, d, st)
        lp = {}
        for nm, shp in (('lru_cw', [128, 6, 5]), ('pool_fl', [128, 2]), ('lru_cb', [128, 6]), ('lru_ba', [128, 2, 6]), ('lru_bx', [128, 2, 6]),
                        ('lru_lam', [128, 2, 6]), ('pool_scale', [128, 2]), ('pool_invw', [128, 2]), ('pool_corr', [128, 2, 2, 16])):
            lp[nm] = sb('p_' + nm, shp, F32)
            P.dma('sp', lp[nm][:], I[nm], [], ['p_' + nm], 'p_' + nm)
        for nm, shp in (('lru_bdA', [128, 2, 6, 128]), ('lru_bdX', [128, 2, 6, 128]), ('pool_bd', [128, 2, 128])):
            lp[nm] = sb('p_' + nm, shp, BF16)
            P.dma('pool', lp[nm][:], I[nm], [], ['p_' + nm], 'p_' + nm)
        cl = sb('p_cl', [128, 2, 2, 6], F32)
        tmp = sb('p_cltmp', [128, 2, 6], F32)
        P.op('act', lambda e: e.activation(out=tmp[:], in_=lp['lru_lam'][:], func=AF.Exp, scale=-1.0), ['p_lru_lam'], ['p_cltmp'])
        P.op('act', lambda e: e.activation(out=tmp[:], in_=tmp[:], func=AF.Ln, bias=k.cst[:, 1:2], scale=1.0), ['p_cltmp', 'cst'], ['p_cltmp'])
        P.op('dve', lambda e: e.tensor_scalar(out=cl[:, 0, :, :], in0=tmp[:], scalar1=-8.0, scalar2=None, op0=ALU.mult), ['p_cltmp'], ['p_cl'])
        P.op('dve', lambda e: e.tensor_scalar(out=cl[:, 1, :, :], in0=tmp[:], scalar1=-16.0, scalar2=None, op0=ALU.mult), ['p_cltmp'], ['p_cl'])
        lp['cl'] = cl
        stt = sb('p_state', [128, 2, 6], F32)
        P.op('dve', lambda e: e.memset(stt[:], 0.0), [], keys('p_state0', 6) + keys('p_state1', 6))
        lp['state'] = stt
        k.lp = lp
        w_in = sb('w_in0', [128, 8, 1792], BF16)
        load_w_bf16(k, 'w_in0', w_in, I['ab_w_in'], 8, 1792, colblk=1792)
        w_out = sb('w_out0', [128, 8, D], BF16)
        k.w_in0, k.w_out0 = w_in, w_out

        for s_, nm, TT, xsrc in ((1, 'c', LC, I['ctx']), (0, 'l', k.T, I['x'])):
            seg = K()
            seg.nm, seg.T, seg.x, seg.set = nm, TT, xsrc, s_
            seg.z, seg.xa, seg.hf, seg.x05, seg.x1 = S['z_' + nm], S['xa_' + nm], S['hf_' + nm], S['x05_' + nm], S['x1_' + nm]
            seg.tiles = tiles_of(TT)
            with ExitStack() as st2:
                fold_w_bf16(k, st2, 'w_out0', w_out, I['ab_w_out'], 8, k.gbc_d[0, s_, 0])
            P.barrier()
            l0_phaseA(k, seg)
            P.barrier()
            if stop_after == 'l0A' and nm == 'l':
                return
            l0_phaseB(k, seg)
            P.barrier()
            if stop_after == 'l0B' and nm == 'l':
                return
            l0_phaseC(k, seg)
            P.barrier()
            if stop_after == 'l0C' and nm == 'l':
                return
    for s_, nm, TT in ((1, 'c', LC), (0, 'l', k.T)):
        ffn_phase(k, 0, s_, S['x05_' + nm], S['x1_' + nm], TT, final=False)
        P.barrier()


def l0_phaseA(k, seg):
    nc, P = k.nc, k.P
    with ExitStack() as st:
        sb = lambda n, s, d: k.sb(n, s, d, st)
        ps = lambda n, s, d: k.ps(n, s, d, st)
        B = mod_bufs(k, st, 'A_')
        hT = sb('A_hT', [128, 8, 512], BF16)
        zt = [sb(f'A_zt{i}', [128, 14, 512], F32) for i in range(2)]
        zp = [ps(f'A_zp{i}', [128, 512], F32) for i in range(4)]
        zv = seg.z.rearrange("(c p) t -> p c t", p=128)
        P.dma('sp', zv[:, :, 0:PADZ], k.zero[:, 0:14 * PADZ].rearrange("p (c t) -> p c t", c=14), ['zero'], [], 'A_zpad')
        P.dma('sp', zv[:, :, PADZ + seg.T:PADZ + seg.T + PADZ], k.zero[:, 0:14 * PADZ].rearrange("p (c t) -> p c t", c=14), ['zero'], [], 'A_zpad')
        for ti, (s, n) in enumerate(seg.tiles):
            modulate_tile(k, B, seg.x[s:s + n, :], n, 0, seg.set, 0, hT, 'A_hT')
            Z = zt[ti % 2]
            Zn = f'A_zt{ti % 2}'
            for mc in range(14):
                zpp = zp[mc % 4]
                for kc in range(8):
                    P.op('pe', lambda e, mc=mc, kc=kc, zpp=zpp: e.matmul(zpp[:, 0:n], lhsT=k.w_in0[:, kc, mc * 128:(mc + 1) * 128], rhs=hT[:, kc, 0:n],
                                                                         start=(kc == 0), stop=(kc == 7)),
                         [f'w_in0.{kc}', f'A_hT.{kc}'], [f'A_zp{mc % 4}'])
                eng = 'act' if mc % 2 == 0 else 'dve'
                if eng == 'act':
                    P.op('act', lambda e, mc=mc, zpp=zpp: e.copy(out=Z[:, mc, 0:n], in_=zpp[:, 0:n]), [f'A_zp{mc % 4}'], [f'{Zn}.{mc}'])
                else:
                    P.op('dve', lambda e, mc=mc, zpp=zpp: e.tensor_copy(out=Z[:, mc, 0:n], in_=zpp[:, 0:n]), [f'A_zp{mc % 4}'], [f'{Zn}.{mc}'])
            P.dma('sp', zv[:, :, PADZ + s:PADZ + s + n], Z[:, :, 0:n], keys(Zn, 14), [], Zn)


def lru_coeffs(k, C, d, n, xa, xab):
    P, lp = k.P, k.lp
    for c in range(6):
        pr, pi = C['pg'][(2 * c) % 4], C['pg'][(2 * c + 1) % 4]
        prn, pin = C['pgname'][(2 * c) % 4], C['pgname'][(2 * c + 1) % 4]
        P.op('pe', lambda e, c=c, pr=pr: e.matmul(pr[:, 0:n], lhsT=lp['lru_bdA'][:, d, c, :], rhs=xab[:, c, 0:n], start=True, stop=True),
             ['p_lru_bdA', f"{C['xabname']}.{c}"], [prn])
        P.op('pe', lambda e, c=c, pi=pi: e.matmul(pi[:, 0:n], lhsT=lp['lru_bdX'][:, d, c, :], rhs=xab[:, c, 0:n], start=True, stop=True),
             ['p_lru_bdX', f"{C['xabname']}.{c}"], [pin])
        P.op('act', lambda e, c=c, pr=pr: e.activation(out=C['r'][:, c, 0:n], in_=pr[:, 0:n], func=AF.Sigmoid, bias=lp['lru_ba'][:, d, c:c + 1], scale=1.0),
             [prn, 'p_lru_ba'], [f"{C['pfx']}r.{c}"])
        P.op('act', lambda e, c=c, pi=pi: e.activation(out=C['ig'][:, c, 0:n], in_=pi[:, 0:n], func=AF.Sigmoid, bias=lp['lru_bx'][:, d, c:c + 1], scale=1.0),
             [pin, 'p_lru_bx'], [f"{C['pfx']}ig.{c}"])
    for c in range(6):
        P.op('act', lambda e, c=c: e.activation(out=C['a'][:, c, 0:n], in_=C['r'][:, c, 0:n], func=AF.Exp, scale=lp['cl'][:, 0, d, c:c + 1]),
             [f"{C['pfx']}r.{c}", 'p_cl'], [f"{C['pfx']}a.{c}"])
        P.op('act', lambda e, c=c: e.activation(out=C['r'][:, c, 0:n], in_=C['r'][:, c, 0:n], func=AF.Exp, scale=lp['cl'][:, 1, d, c:c + 1]),
             [f"{C['pfx']}r.{c}", 'p_cl'], [f"{C['pfx']}r.{c}"])
    for c in range(6):
        P.op('act', lambda e, c=c: e.activation(out=C['r'][:, c, 0:n], in_=C['r'][:, c, 0:n], func=AF.Sqrt, bias=k.cst[:, 1:2], scale=-1.0),
             [f"{C['pfx']}r.{c}", 'cst'], [f"{C['pfx']}r.{c}"])
        P.op('dve', lambda e, c=c: e.tensor_tensor(out=C['ig'][:, c, 0:n], in0=C['ig'][:, c, 0:n], in1=C['r'][:, c, 0:n], op=ALU.mult),
             [f"{C['pfx']}r.{c}", f"{C['pfx']}ig.{c}"], [f"{C['pfx']}ig.{c}"])
        P.op('pool', lambda e, c=c: e.tensor_tensor(out=C['ig'][:, c, 0:n], in0=C['ig'][:, c, 0:n], in1=xa[:, c, 0:n], op=ALU.mult),
             [f"{C['pfx']}ig.{c}", f"{C['xaname']}.{c}"], [f"{C['pfx']}ig.{c}"])


def coeff_bufs(k, st, pfx, share=None):
    C = {'pfx': pfx}
    for nm in ('r', 'ig', 'a'):
        C[nm] = k.sb(pfx + nm, [128, 6, 512], F32, st)
    if share is None:
        C['pg'] = [k.ps(pfx + f'pg{i}', [128, 512], F32, st) for i in range(4)]
        C['pgname'] = [pfx + f'pg{i}' for i in range(4)]
    else:
        C['pg'], C['pgname'] = share['pg'], share['pgname']
    return C


def l0_phaseB(k, seg):
    P, lp = k.P, k.lp
    with ExitStack() as st:
        sb = lambda n, s, d: k.sb(n, s, d, st)
        sets = []
        for j in range(2):
            Bf = {}
            Bf['zin'] = sb(f'B{j}_zin', [128, 6, 516], F32)
            Bf['xa'] = sb(f'B{j}_xa', [128, 6, 512], F32)
            Bf['xab'] = sb(f'B{j}_xab', [128, 6, 512], BF16)
            Bf['hf'] = sb('B_hf', [128, 6, 512], F32) if j == 0 else sets[0]['hf']
            C = coeff_bufs(k, st, f'B{j}_', share=(sets[0]['C'] if j == 1 else None))
            C['xabname'], C['xaname'] = f'B{j}_xab', f'B{j}_xa'
            Bf['C'] = C
            sets.append(Bf)
        zv = seg.z[0:768, :].rearrange("(c p) t -> p c t", p=128)
        xav = seg.xa.rearrange("(c p) t -> p c t", p=128)
        hfv = seg.hf.rearrange("(c p) t -> p c t", p=128)
        if seg.nm == 'c':
            P.op('dve', lambda e: e.memset(lp['state'][:], 0.0), [], keys('p_state0', 6) + keys('p_state1', 6))
        for ti, (s, n) in enumerate(seg.tiles):
            j = ti % 2
            Bf = sets[j]
            zin, xa, xab, hf, C = Bf['zin'], Bf['xa'], Bf['xab'], Bf['hf'], Bf['C']
            pf = f'B{j}_'
            P.dma('sp', zin[:, :, 0:n + 4], zv[:, :, PADZ + s - 2:PADZ + s + n + 2], [], keys(pf + 'zin', 6), pf + 'zin')
            for c in range(6):
                P.op('act', lambda e, c=c, xa=xa, zin=zin: e.activation(out=xa[:, c, 0:n], in_=zin[:, c, 0:n], func=AF.Identity,
                                                                        scale=lp['lru_cw'][:, c, 0:1], bias=lp['lru_cb'][:, c:c + 1]),
                     [f'{pf}zin.{c}', 'p_lru_cw', 'p_lru_cb'], [f'{pf}xa.{c}'])
            for t in range(1, 5):
                for c in range(6):
                    P.op('dve', lambda e, c=c, t=t, xa=xa, zin=zin: e.scalar_tensor_tensor(out=xa[:, c, 0:n], in0=zin[:, c, t:t + n], scalar=lp['lru_cw'][:, c, t:t + 1],
                                                                                           in1=xa[:, c, 0:n], op0=ALU.mult, op1=ALU.add),
                         [f'{pf}zin.{c}', f'{pf}xa.{c}'], [f'{pf}xa.{c}'])
            for c in range(6):
                P.op('act', lambda e, c=c, xa=xa, xab=xab: e.copy(out=xab[:, c, 0:n], in_=xa[:, c, 0:n]), [f'{pf}xa.{c}'], [f'{pf}xab.{c}'])
            P.dma('sp', xav[:, :, s:s + n], xa[:, :, 0:n], keys(pf + 'xa', 6), [], pf + 'xa')
            lru_coeffs(k, C, 0, n, xa, xab)
            for c in range(6):
                P.op('dve', lambda e, c=c, hf=hf, C=C: e.tensor_tensor_scan(out=hf[:, c, 0:n], data0=C['a'][:, c, 0:n], data1=C['ig'][:, c, 0:n],
                                                                          initial=lp['state'][:, 0, c:c + 1], op0=ALU.mult, op1=ALU.add),
                     [f'{pf}a.{c}', f'{pf}ig.{c}', f'p_state0.{c}'], [f'B_hf.{c}'])
                P.op('dve', lambda e, c=c, hf=hf: e.tensor_copy(out=lp['state'][:, 0, c:c + 1], in_=hf[:, c, n - 1:n]), [f'B_hf.{c}'], [f'p_state0.{c}'])
            P.dma('sp', hfv[:, :, s:s + n], hf[:, :, 0:n], keys('B_hf', 6), [], 'B_hf')


def l0_phaseC(k, seg):
    P, lp, I = k.P, k.lp, k.I
    with ExitStack() as st:
        sb = lambda n, s, d: k.sb(n, s, d, st)
        ps = lambda n, s, d: k.ps(n, s, d, st)
        xa = sb('C_xa', [128, 6, 512], F32)
        xab = sb('C_xab', [128, 6, 512], BF16)
        hb = sb('C_hb', [128, 6, 512], F32)
        hf = sb('C_hf', [128, 6, 512], F32)
        ga = sb('C_ga', [128, 6, 512], F32)
        yT = sb('C_yT', [128, 8, 512], BF16)
        zb = sb('C_zb', [128, 2, 528], F32)
        p2 = sb('C_p2', [128, 528], F32)
        p4 = sb('C_p4', [128, 528], F32)
        p8 = sb('C_p8', [128, 528], F32)
        Qw = sb('C_Qw', [128, 516], F32)
        Ssum = sb('C_S', [128, 512], F32)
        dd = sb('C_dd', [128, 2, 512], BF16)
        xt = sb('C_xt', [128, 4, D], F32)
        xo = sb('C_xo', [128, 4, D], F32)
        C = coeff_bufs(k, st, 'C_')
        C['xabname'], C['xaname'] = 'C_xab', 'C_xa'
        po = [ps(f'C_po{i}', [128, 512], F32) for i in range(4)]
        zg = seg.z[768:1536, :].rearrange("(c p) t -> p c t", p=128)
        zbv = seg.z[1536:1792, :].rearrange("(c p) t -> p c t", p=128)
        xav = seg.xa.rearrange("(c p) t -> p c t", p=128)
        hfv = seg.hf.rearrange("(c p) t -> p c t", p=128)
        if seg.nm == 'c':
            P.op('dve', lambda e: e.memset(lp['state'][:, 1, :], 0.0), [], keys('p_state1', 6))
        nt_ = len(seg.tiles)
        for ti in range(nt_ - 1, -1, -1):
            s, n = seg.tiles[ti]
            ng = n // 128
            P.dma('sp', xa[:, :, 0:n], xav[:, :, s:s + n], [], keys('C_xa', 6), 'C_xa')
            P.dma('sp', hf[:, :, 0:n], hfv[:, :, s:s + n], [], keys('C_hf', 6), 'C_hf')
            P.dma('sp', ga[:, :, 0:n], zg[:, :, PADZ + s:PADZ + s + n], [], keys('C_ga', 6), 'C_ga')
            P.dma('sp', zb[:, :, 0:n + 16], zbv[:, :, PADZ + s - 8:PADZ + s + n + 8], [], keys('C_zb', 2), 'C_zb')
            P.dma('sp', xt[:, 0:ng, :], seg.x[s:s + n, :].rearrange("(g p) f -> p g f", p=128), [], ['C_xt'], 'C_xt')
            W = n + 16
            n1 = n + 1
            for ch in range(2):
                zc = zb[:, ch, :]
                P.op('dve', lambda e, zc=zc: e.tensor_tensor(out=p2[:, 0:W - 1], in0=zc[:, 0:W - 1], in1=zc[:, 1:W], op=ALU.add),
                     [f'C_zb.{ch}'], ['C_p2'])
                if ch == 0:
                    P.op('dve', lambda e: e.tensor_copy(out=Qw[0:64, 0:n1], in_=p2[0:64, 7:7 + n1]), ['C_p2'], ['C_Qw'])
                    P.op('dve', lambda e: e.tensor_tensor(out=Qw[64:128, 0:n1], in0=p2[64:128, 6:6 + n1], in1=p2[64:128, 8:8 + n1], op=ALU.add),
                         ['C_p2'], ['C_Qw'])
                else:
                    P.op('dve', lambda e: e.tensor_tensor(out=p4[:, 0:W - 3], in0=p2[:, 0:W - 3], in1=p2[:, 2:W - 1], op=ALU.add), ['C_p2'], ['C_p4'])
                    P.op('dve', lambda e: e.tensor_tensor(out=Qw[0:64, 0:n1], in0=p4[0:64, 4:4 + n1], in1=p4[0:64, 8:8 + n1], op=ALU.add),
                         ['C_p4'], ['C_Qw'])
                    P.op('dve', lambda e: e.tensor_tensor(out=p8[64:128, 0:W - 7], in0=p4[64:128, 0:W - 7], in1=p4[64:128, 4:W - 3], op=ALU.add),
                         ['C_p4'], ['C_p8'])
                    P.op('dve', lambda e: e.tensor_tensor(out=Qw[64:128, 0:n1], in0=p8[64:128, 0:n1], in1=p8[64:128, 8:8 + n1], op=ALU.add),
                         ['C_p8'], ['C_Qw'])
                P.op('dve', lambda e: e.tensor_scalar(out=Ssum[:, 0:n], in0=Qw[:, 0:n], scalar1=lp['pool_fl'][:, 0:1], scalar2=None, op0=ALU.mult),
                     ['C_Qw', 'p_pool_fl'], ['C_S'])
                P.op('dve', lambda e: e.scalar_tensor_tensor(out=Ssum[:, 0:n], in0=Qw[:, 1:n1], scalar=lp['pool_fl'][:, 1:2], in1=Ssum[:, 0:n],
                                                             op0=ALU.mult, op1=ALU.add), ['C_Qw', 'C_S', 'p_pool_fl'], ['C_S'])
                if ti == 0:
                    P.op('dve', lambda e, ch=ch: e.tensor_tensor(out=Ssum[:, 0:16], in0=Ssum[:, 0:16], in1=lp['pool_corr'][:, ch, 0, :], op=ALU.mult),
                         ['C_S', 'p_pool_corr'], ['C_S'])
                if ti == nt_ - 1:
                    P.op('dve', lambda e, ch=ch: e.tensor_tensor(out=Ssum[:, n - 16:n], in0=Ssum[:, n - 16:n], in1=lp['pool_corr'][:, ch, 1, :], op=ALU.mult),
                         ['C_S', 'p_pool_corr'], ['C_S'])
                P.op('dve', lambda e, ch=ch, zc=zc: e.scalar_tensor_tensor(out=dd[:, ch, 0:n], in0=Ssum[:, 0:n], scalar=lp['pool_invw'][:, ch:ch + 1],
                                                                          in1=zc[:, 8:8 + n], op0=ALU.mult, op1=ALU.subtract),
                     ['C_S', f'C_zb.{ch}', 'p_pool_invw'], [f'C_dd.{ch}'])
                pp = po[ch]
                P.op('pe', lambda e, ch=ch, pp=pp: e.matmul(pp[:, 0:n], lhsT=lp['pool_bd'][:, ch, :], rhs=dd[:, ch, 0:n], start=True, stop=True),
                     ['p_pool_bd', f'C_dd.{ch}'], [f'C_po{ch}'])
                P.op('act', lambda e, ch=ch, pp=pp: e.activation(out=yT[:, 6 + ch, 0:n], in_=pp[:, 0:n], func=AF.Identity, scale=lp['pool_scale'][:, ch:ch + 1], bias=k.cst[:, 2:3]),
                     [f'C_po{ch}', 'p_pool_scale', 'cst'], [f'C_yT.{6 + ch}'])
            for c in range(6):
                P.op('act', lambda e, c=c: e.copy(out=xab[:, c, 0:n], in_=xa[:, c, 0:n]), [f'C_xa.{c}'], [f'C_xab.{c}'])
            lru_coeffs(k, C, 1, n, xa, xab)
            for c in range(6):
                P.op('act', lambda e, c=c: e.activation(out=ga[:, c, 0:n], in_=ga[:, c, 0:n], func=AF.Gelu_apprx_tanh), [f'C_ga.{c}'], [f'C_ga.{c}'])
            for c in range(6):
                P.op('dve', lambda e, c=c: e.tensor_tensor_scan(out=rev(hb[:, c, 0:n]), data0=rev(C['a'][:, c, 0:n]), data1=rev(C['ig'][:, c, 0:n]),
                                                                initial=lp['state'][:, 1, c:c + 1], op0=ALU.mult, op1=ALU.add),
                     [f'C_a.{c}', f'C_ig.{c}', f'p_state1.{c}'], [f'C_hb.{c}'])
                P.op('dve', lambda e, c=c: e.tensor_copy(out=lp['state'][:, 1, c:c + 1], in_=hb[:, c, 0:1]), [f'C_hb.{c}'], [f'p_state1.{c}'])
                P.op('pool', lambda e, c=c: e.tensor_tensor(out=hb[:, c, 0:n], in0=hb[:, c, 0:n], in1=hf[:, c, 0:n], op=ALU.add),
                     [f'C_hb.{c}', f'C_hf.{c}'], [f'C_hb.{c}'])
                P.op('dve', lambda e, c=c: e.tensor_tensor(out=yT[:, c, 0:n], in0=hb[:, c, 0:n], in1=ga[:, c, 0:n], op=ALU.mult),
                     [f'C_hb.{c}', f'C_ga.{c}'], [f'C_yT.{c}'])
            for g in range(ng):
                for nt in range(2):
                    pp = po[(2 * g + nt) % 4]
                    ppn = f'C_po{(2 * g + nt) % 4}'
                    for kc in range(8):
                        P.op('pe', lambda e, g=g, nt=nt, kc=kc, pp=pp: e.matmul(pp[:, :], lhsT=yT[:, kc, g * 128:(g + 1) * 128], rhs=k.w_out0[:, kc, nt * 512:(nt + 1) * 512],
                                                                               start=(kc == 0), stop=(kc == 7)),
                             [f'C_yT.{kc}', f'w_out0.{kc}'], [ppn])
                    P.op('dve', lambda e, g=g, nt=nt, pp=pp: e.tensor_tensor(out=xo[:, g, nt * 512:(nt + 1) * 512], in0=pp[:, :], in1=xt[:, g, nt * 512:(nt + 1) * 512], op=ALU.add),
                         [ppn, 'C_xt'], [f'C_xo.{g}'])
            P.dma('sp', seg.x05[1 + s:1 + s + n, :].rearrange("(g p) f -> p g f", p=128), xo[:, 0:ng, :], keys('C_xo', 4)[0:ng], [], 'C_xo')


def ffn_phase(k, l, s_, src, dst, TT, final, flush=True, out_rows=None):
    nc, P, I = k.nc, k.P, k.I
    tiles = tiles_of(TT)
    with ExitStack() as st:
        sb = lambda n, s, d: k.sb(n, s, d, st)
        ps = lambda n, s, d: k.ps(n, s, d, st)
        w_up = sb('F_wup', [128, 8, 2 * DFF], BF16)
        load_w_bf16(k, 'F_wup', w_up, I['ffn_w_up'][l], 8, 2 * DFF)
        w_dn = sb('F_wdn', [128, NFC, D], BF16)
        with ExitStack() as st2:
            fold_w_bf16(k, st2, 'F_wdn', w_dn, I['ffn_w_down'][l], NFC, k.gbc_d[l, s_, 1])
            P.barrier()
        cw = sb('F_cw', [128, 2 * NFC, 3], F32)
        cb = sb('F_cb', [128, 2 * NFC], F32)
        P.dma('sp', cw[:], I['ffn_cw'][:, l, :, :], [], ['F_cw'], 'F_cw')
        P.dma('sp', cb[:], I['ffn_cb'][:, l, :], [], ['F_cb'], 'F_cb')
        B = mod_bufs(k, st, 'F_')
        hT = sb('F_hT', [128, 8, 512], BF16)
        gT = sb('F_gT', [128, NFC, 512], BF16)
        prevu = [sb(f'F_prevu{i}', [128, 2 * NFC, 2], F32) for i in range(2)]
        P.op('dve', lambda e: e.memset(prevu[0][:], 0.0), [], keys('F_prevu0', 2 * NFC))
        acc = [sb(f'F_acc{i}', [128, 512], F32) for i in range(4)]
        corr = sb('F_corr', [128, 2 * NFC, 2], F32)
        ctmp = sb('F_ctmp', [128, 2 * NFC], F32)
        xs = sb('F_xs', [128, 2, D], F32)
        xo = xs
        pu = [ps(f'F_pu{i}', [128, 512], F32) for i in range(4)]
        pd = [ps(f'F_pd{i}', [128, 512], F32) for i in range(2)]
        if final:
            fg = sb('F_fg', [128, D], F32)
            P.dma('sp', fg[:], I['final_g_bc'], [], ['F_fg'], 'F_fg')
            fss = sb('F_fss', [128, 12], F32)
            fjunk = B['junk']

        if os.environ.get('DBG_SBUF'):
            print('FFN sbuf remaining', nc.sbuf_bytes_remaining, 'final', final)
        def conv_gate(n, zero_u, ti):
            pin, pout = prevu[ti % 2], prevu[(ti + 1) % 2]
            pinn, poutn = f'F_prevu{ti % 2}', f'F_prevu{(ti + 1) % 2}'
            allin = keys(pinn, 2 * NFC)
            P.op('dve', lambda e: e.tensor_tensor(out=corr[:, :, 0], in0=cw[:, :, 0], in1=pin[:, :, 0], op=ALU.mult), allin + ['F_cw'], ['F_corr'])
            P.op('dve', lambda e: e.tensor_tensor(out=ctmp[:, :], in0=cw[:, :, 1], in1=pin[:, :, 1], op=ALU.mult), allin + ['F_cw'], ['F_ctmp'])
            P.op('dve', lambda e: e.tensor_tensor(out=corr[:, :, 0], in0=corr[:, :, 0], in1=ctmp[:, :], op=ALU.add), ['F_corr', 'F_ctmp'], ['F_corr'])
            P.op('dve', lambda e: e.tensor_tensor(out=corr[:, :, 1], in0=cw[:, :, 0], in1=pin[:, :, 1], op=ALU.mult), allin + ['F_cw', 'F_corr'], ['F_corr'])
            for c in range(NFC):
                q = c % 2
                AA = [acc[2 * q], acc[2 * q + 1]]
                AN = [f'F_acc{2 * q}', f'F_acc{2 * q + 1}']
                PP = [pu[2 * q], pu[2 * q + 1]]
                PN = [f'F_pu{2 * q}', f'F_pu{2 * q + 1}']
                CC = [c, NFC + c]
                if not zero_u:
                    for vi in range(2):
                        for kc in range(8):
                            P.op('pe', lambda e, kc=kc, cc=CC[vi], pp=PP[vi]: e.matmul(pp[:, 0:n], lhsT=w_up[:, kc, cc * 128:(cc + 1) * 128], rhs=hT[:, kc, 0:n],
                                                                                      start=(kc == 0), stop=(kc == 7)),
                                 [f'F_wup.{kc}', f'F_hT.{kc}'], [PN[vi]])
                    for vi in range(2):
                        P.op('act', lambda e, A_=AA[vi], pp=PP[vi], cc=CC[vi]: e.activation(out=A_[:, 0:n], in_=pp[:, 0:n], func=AF.Identity,
                                                                                          scale=cw[:, cc, 2:3], bias=cb[:, cc:cc + 1]),
                             [PN[vi], 'F_cw', 'F_cb'], [AN[vi]])
                    for vi in range(2):
                        if os.environ.get('DBG_SKIP_SAVE'):
                            continue
                        P.op('dve', lambda e, pp=PP[vi], cc=CC[vi]: e.tensor_copy(out=pout[:, cc, :], in_=pp[:, n - 2:n]), [PN[vi]], [f'{poutn}.{CC[vi]}'])
                    for vi in range(2):
                        P.op('dve', lambda e, A_=AA[vi], pp=PP[vi], cc=CC[vi]: e.scalar_tensor_tensor(out=A_[:, 1:n], in0=pp[:, 0:n - 1], scalar=cw[:, cc, 1:2], in1=A_[:, 1:n],
                                                                                                    op0=ALU.mult, op1=ALU.add), [PN[vi], AN[vi]], [AN[vi]])
                    for vi in range(2):
                        P.op('dve', lambda e, A_=AA[vi], pp=PP[vi], cc=CC[vi]: e.scalar_tensor_tensor(out=A_[:, 2:n], in0=pp[:, 0:n - 2], scalar=cw[:, cc, 0:1], in1=A_[:, 2:n],
                                                                                                    op0=ALU.mult, op1=ALU.add), [PN[vi], AN[vi]], [AN[vi]])
                else:
                    for vi in range(2):
                        P.op('act', lambda e, A_=AA[vi], cc=CC[vi]: e.activation(out=A_[:, 0:n], in_=k.zero[:, 0:n], func=AF.Identity,
                                                                                scale=cw[:, cc, 2:3], bias=cb[:, cc:cc + 1]),
                             ['zero', 'F_cw', 'F_cb'], [AN[vi]])
                for vi in range(2):
                    P.op('pool', lambda e, A_=AA[vi], cc=CC[vi]: e.tensor_tensor(out=A_[:, 0:2], in0=A_[:, 0:2], in1=corr[:, cc, :], op=ALU.add),
                         ['F_corr', AN[vi]], [AN[vi]])
                P.op('act', lambda e, A_=AA[1]: e.activation(out=A_[:, 0:n], in_=A_[:, 0:n], func=AF.Silu), [AN[1]], [AN[1]])
                P.op('pool', lambda e, c=c, A0=AA[0], A1=AA[1]: e.tensor_tensor(out=gT[:, c, 0:n], in0=A0[:, 0:n], in1=A1[:, 0:n], op=ALU.mult),
                     [AN[0], AN[1]], [f'F_gT.{c}'])

        def down_res(tok0, n, nrows_last=128):
            ng = n // 128
            nr = lambda g: (nrows_last if g == ng - 1 else 128)
            for g_ in range(ng):
                g = g_ % 2
                r0 = 1 + tok0 + g_ * 128
                P.dma('sp', xs[0:nr(g_), g, :], src[r0:r0 + nr(g_), :], [], [f'F_xs.{g}'], f'F_xs{g}')
                for nt in range(2):
                    pp = pd[nt]
                    for kc in range(NFC):
                        P.op('pe', lambda e, g_=g_, nt=nt, kc=kc, pp=pp: e.matmul(pp[:, :], lhsT=gT[:, kc, g_ * 128:(g_ + 1) * 128], rhs=w_dn[:, kc, nt * 512:(nt + 1) * 512],
                                                                               start=(kc == 0), stop=(kc == NFC - 1)),
                             [f'F_gT.{kc}', f'F_wdn.{kc}'], [f'F_pd{nt}'])
                    P.op('dve', lambda e, g=g, g_=g_, nt=nt, pp=pp: e.tensor_tensor(out=xo[0:nr(g_), g, nt * 512:(nt + 1) * 512], in0=pp[0:nr(g_), :],
                                                                            in1=xs[0:nr(g_), g, nt * 512:(nt + 1) * 512], op=ALU.add),
                         [f'F_pd{nt}', f'F_xs.{g}'], [f'F_xs.{g}'])
                if final:
                    P.op('dve', lambda e: e.memset(fss[:, 0:1], 0.0), [], ['F_fss'])
                    P.op('act', lambda e, g=g: e.activation(out=fjunk[:], in_=xo[:, g, :], func=AF.Square, accum_out=fss[:, 0:1]),
                         [f'F_xs.{g}'], ['F_fss', 'F_junk'])
                    P.op('act', lambda e: e.activation(out=fss[:, 1:2], in_=fss[:, 0:1], func=AF.Sqrt, bias=k.cst[:, 0:1], scale=1.0 / D),
                         ['F_fss', 'cst'], ['F_fss'])
                    P.op('dve', lambda e: e.reciprocal(out=fss[:, 2:3], in_=fss[:, 1:2]), ['F_fss'], ['F_fss'])
                    P.op('dve', lambda e, g=g: e.scalar_tensor_tensor(out=xo[:, g, :], in0=xo[:, g, :], scalar=fss[:, 2:3], in1=fg[:],
                                                                     op0=ALU.mult, op1=ALU.mult), [f'F_xs.{g}', 'F_fss', 'F_fg'], [f'F_xs.{g}'])
                t0 = tok0 + g_ * 128
                if final:
                    lo = max(t0, 0)
                    hi = min(t0 + nr(g_), TT if out_rows is None else out_rows)
                    if hi > lo:
                        P.dma('sp', k.out[lo:hi, :], xo[lo - t0:hi - t0, g, :], [f'F_xs.{g}'], [], f'F_xs{g}')
                else:
                    P.dma('sp', dst[1 + t0:1 + t0 + nr(g_), :], xo[0:nr(g_), g, :], [f'F_xs.{g}'], [], f'F_xs{g}')

        for ti, (s, n) in enumerate(tiles):
            modulate_tile(k, B, src[1 + s:1 + s + n, :], n, l, s_, 2, hT, 'F_hT')
            conv_gate(n, False, ti)
            down_res(s - 1, n)
        if flush:
            conv_gate(128, True, len(tiles))
            down_res(TT - 1, 128, nrows_last=1)


def layer1(k, stop_after):
    nc, P, I, S = k.nc, k.P, k.I, k.S
    T = k.T
    TK = T + LC
    NKB = TK // 128
    TL = k.TL
    yc_d = nc.dram_tensor('yc_d', [768, T], BF16, kind="Internal").ap()
    with ExitStack() as st:
      if not os.environ.get('DBG_ONLY_F1'):
          sb = lambda n, s, d: k.sb(n, s, d, st)
          ps = lambda n, s, d: k.ps(n, s, d, st)
          w_in = sb('w_in1', [128, 8, 2816], BF16)
          load_w_bf16(k, 'w_in1', w_in, I['cd_w_in'], 8, 2816, colblk=1408)
          w_sw = sb('w_sw1', [128, 8, 1536], BF16)
          load_w_bf16(k, 'w_sw1', w_sw, I['cd_w_sw'], 8, 1536, colblk=1536)
          B = mod_bufs(k, st, 'E_')
          hT = sb('E_hT', [128, 8, 512], BF16)
          qk = sb('E_qk', [128, 12, 512], BF16)
          vt = sb('E_vt', [128, 4, 768], BF16)
          gl = sb('E_gl', [128, 2, 512], F32)
          rc = sb('E_rc', [128, 512], F32)
          rs = sb('E_rs', [128, 512], F32)
          t1 = sb('E_t1', [128, 512], F32)
          t2 = sb('E_t2', [128, 512], F32)
          sg = sb('E_sg', [128, 512], F32)
          pa = [ps(f'E_pa{i}', [128, 512], F32) for i in range(2)]
          pb = [ps(f'E_pb{i}', [128, 512], F32) for i in range(2)]
          glv = S['gl'].rearrange("(c p) t -> p c t", p=128)
          P.dma('sp', glv[:, :, 0:PADZ], k.zero[:, 0:2 * PADZ].rearrange("p (c t) -> p c t", c=2), ['zero'], [], 'E_glpad')
          P.dma('sp', glv[:, :, PADZ + T:PADZ + T + PADZ], k.zero[:, 0:2 * PADZ].rearrange("p (c t) -> p c t", c=2), ['zero'], [], 'E_glpad')
          qTv = S['qT'].rearrange("(c p) t -> p c t", p=128)
          kTv = S['kT'].rearrange("(c p) t -> p c t", p=128)

          def proj_plain(cols0, nch, dst, dstname, d0, n):
              for c in range(nch):
                  pp = pa[c % 2]
                  for kc in range(8):
                      P.op('pe', lambda e, c=c, kc=kc, pp=pp: e.matmul(pp[:, 0:n], lhsT=w_in[:, kc, cols0 + c * 128:cols0 + (c + 1) * 128], rhs=hT[:, kc, 0:n],
                                                                      start=(kc == 0), stop=(kc == 7)), [f'w_in1.{kc}', f'E_hT.{kc}'], [f'E_pa{c % 2}'])
                  P.op('act', lambda e, c=c, pp=pp: e.copy(out=dst[:, d0 + c, 0:n], in_=pp[:, 0:n]), [f'E_pa{c % 2}'], [f'{dstname}.{d0 + c}'])

          def proj_v(n):
              ng = n // 128
              for g in range(ng):
                  for (c0, cn, pp, ppn) in ((0, 512, pa[g % 2], f'E_pa{g % 2}'), (512, 256, pb[g % 2], f'E_pb{g % 2}')):
                      for kc in range(8):
                          P.op('pe', lambda e, g=g, kc=kc, pp=pp, c0=c0, cn=cn: e.matmul(pp[:, 0:cn], lhsT=hT[:, kc, g * 128:(g + 1) * 128],
                                                                                         rhs=w_in[:, kc, 1536 + c0:1536 + c0 + cn], start=(kc == 0), stop=(kc == 7)),
                               [f'w_in1.{kc}', f'E_hT.{kc}'], [ppn])
                      P.op('dve', lambda e, g=g, pp=pp, c0=c0, cn=cn: e.tensor_copy(out=vt[:, g, c0:c0 + cn], in_=pp[:, 0:cn]), [ppn], [f'E_vt.{g}'])

          n = LC
          modulate_tile(k, B, S['x1_c'][1:1 + LC, :], n, 1, 1, 0, hT, 'E_hT')
          proj_plain(768, 6, qk, 'E_qk', 6, n)
          P.dma('sp', kTv[:, :, T:T + n], qk[:, 6:12, 0:n], keys('E_qk', 12)[6:12], [], 'E_qk')
          proj_v(n)
          P.dma('sp', S['v'][T:T + n, :].rearrange("(g p) f -> p g f", p=128), vt[:, 0:n // 128, :], keys('E_vt', 4), [], 'E_vt')
          for ti, (s, n) in enumerate(tiles_of(T)):
              modulate_tile(k, B, S['x1_l'][1 + s:1 + s + n, :], n, 1, 0, 0, hT, 'E_hT')
              P.dma('sp', rc[:, 0:n], I['rope_c'][:, s:s + n], [], ['E_rc'], 'E_rc')
              P.dma('sp', rs[:, 0:n], I['rope_s'][:, s:s + n], [], ['E_rs'], 'E_rs')
              need_q = s < TL + 16
              for c in (range(12) if need_q else range(6, 12)):
                  pp, pq = pa[c % 2], pb[c % 2]
                  for kc in range(8):
                      P.op('pe', lambda e, c=c, kc=kc, pp=pp: e.matmul(pp[:, 0:n], lhsT=w_in[:, kc, c * 128:(c + 1) * 128], rhs=hT[:, kc, 0:n],
                                                                      start=(kc == 0), stop=(kc == 7)), [f'w_in1.{kc}', f'E_hT.{kc}'], [f'E_pa{c % 2}'])
                  for kc in range(8):
                      P.op('pe', lambda e, c=c, kc=kc, pq=pq: e.matmul(pq[:, 0:n], lhsT=w_sw[:, kc, c * 128:(c + 1) * 128], rhs=hT[:, kc, 0:n],
                                                                      start=(kc == 0), stop=(kc == 7)), [f'w_sw1.{kc}', f'E_hT.{kc}'], [f'E_pb{c % 2}'])
                  P.op('dve', lambda e, pp=pp: e.tensor_tensor(out=t1[:, 0:n], in0=pp[:, 0:n], in1=rc[:, 0:n], op=ALU.mult), [f'E_pa{c % 2}', 'E_rc'], ['E_t1'])
                  P.op('dve', lambda e, pq=pq: e.tensor_tensor(out=t2[:, 0:n], in0=pq[:, 0:n], in1=rs[:, 0:n], op=ALU.mult), [f'E_pb{c % 2}', 'E_rs'], ['E_t2'])
                  P.op('pool', lambda e, c=c: e.tensor_tensor(out=qk[:, c, 0:n], in0=t1[:, 0:n], in1=t2[:, 0:n], op=ALU.add), ['E_t1', 'E_t2'], [f'E_qk.{c}'])
              if need_q:
                  P.dma('sp', qTv[:, :, s:s + n], qk[:, 0:6, 0:n], keys('E_qk', 12)[0:6], [], 'E_q')
              P.dma('sp', kTv[:, :, s:s + n], qk[:, 6:12, 0:n], keys('E_qk', 12)[6:12], [], 'E_qk')
              proj_v(n)
              P.dma('sp', S['v'][s:s + n, :].rearrange("(g p) f -> p g f", p=128), vt[:, 0:n // 128, :], keys('E_vt', 4), [], 'E_vt')
              for c in (range(2) if need_q else []):
                  pp, pq = pa[c % 2], pb[c % 2]
                  for (pz, pzn, cols) in ((pp, f'E_pa{c % 2}', 2304 + c * 128), (pq, f'E_pb{c % 2}', 2304 + 256 + c * 128)):
                      for kc in range(8):
                          P.op('pe', lambda e, kc=kc, pz=pz, cols=cols: e.matmul(pz[:, 0:n], lhsT=w_in[:, kc, cols:cols + 128], rhs=hT[:, kc, 0:n],
                                                                                start=(kc == 0), stop=(kc == 7)), [f'w_in1.{kc}', f'E_hT.{kc}'], [pzn])
                  P.op('act', lambda e, pq=pq: e.activation(out=sg[:, 0:n], in_=pq[:, 0:n], func=AF.Sigmoid), [f'E_pb{c % 2}'], ['E_sg'])
                  P.op('dve', lambda e, c=c, pp=pp: e.tensor_tensor(out=gl[:, c, 0:n], in0=pp[:, 0:n], in1=sg[:, 0:n], op=ALU.mult), [f'E_pa{c % 2}', 'E_sg'], [f'E_gl.{c}'])
              if need_q:
                  P.dma('sp', glv[:, :, PADZ + s:PADZ + s + n], gl[:, :, 0:n], keys('E_gl', 2), [], 'E_gl')
    P.barrier()
    if stop_after == 'l1E':
        return

    with ExitStack() as st:
        sb = lambda n, s, d: k.sb(n, s, d, st)
        ps = lambda n, s, d: k.ps(n, s, d, st)
        dl = sb('G_dl', [1, 4, 64], F32)
        P.dma('sp', dl[:], I['diff_l'], [], ['G_dl'], 'G_dl')
        sm = sb('G_sm', [1, 8], F32)
        pr_ = sb('G_pr', [1, 2, 64], F32)
        P.op('dve', lambda e: e.tensor_tensor(out=pr_[:, 0, :], in0=dl[:, 0, :], in1=dl[:, 1, :], op=ALU.mult), ['G_dl'], ['G_pr'])
        P.op('dve', lambda e: e.tensor_tensor(out=pr_[:, 1, :], in0=dl[:, 2, :], in1=dl[:, 3, :], op=ALU.mult), ['G_dl'], ['G_pr'])
        P.op('dve', lambda e: e.reduce_sum(out=sm[:, 0:1], in_=pr_[:, 0, :], axis=AX.X), ['G_pr'], ['G_sm'])
        P.op('dve', lambda e: e.reduce_sum(out=sm[:, 1:2], in_=pr_[:, 1, :], axis=AX.X), ['G_pr'], ['G_sm'])
        P.op('act', lambda e: e.activation(out=sm[:, 2:4], in_=sm[:, 0:2], func=AF.Exp), ['G_sm'], ['G_sm'])
        P.op('dve', lambda e: e.tensor_tensor(out=sm[:, 4:5], in0=sm[:, 3:4], in1=sm[:, 2:3], op=ALU.subtract), ['G_sm'], ['G_sm'])
        P.op('dve', lambda e: e.tensor_scalar(out=sm[:, 5:6], in0=sm[:, 4:5], scalar1=-LAMBDA_INIT1, scalar2=None, op0=ALU.add), ['G_sm'], ['G_sm'])
        neglam = sb('G_neglam', [128, 2], F32)
        gsub = sb('G_gsub', [128, 2], F32)
        P.dma('sp', gsub[:, 0:1], I['subln_g'], [], ['G_gsub'], 'G_gsub')
        P.op('dve', lambda e: e.tensor_scalar(out=gsub[:, 1:2], in0=gsub[:, 0:1], scalar1=(1.0 - LAMBDA_INIT1), scalar2=None, op0=ALU.mult), ['G_gsub'], ['G_gsub'])
        pS = [ps(f'G_pS{i}', [128, 2, 512], F32) for i in range(2)]
        po = [ps(f'G_po{i}', [128, 512], F32) for i in range(2)]
        pl = ps('G_pl', [128, 2, 512], F32)
        P.op('pe', lambda e: e.matmul(pl[:, 0, 0:1], lhsT=k.ones_f[0:1, :], rhs=sm[0:1, 5:6], start=True, stop=True), ['ones_f', 'G_sm'], ['G_pl0', 'G_pl1'])
        P.op('dve', lambda e: e.tensor_copy(out=neglam[:, 0:1], in_=pl[:, 0, 0:1]), ['G_pl0', 'G_pl1'], ['G_neglam'])
        kh = [sb(f'G_kh{j}', [128, TK], BF16) for j in range(2)]
        vh = [sb(f'G_vh{j}', [128, NKB, 128], BF16) for j in range(2)]
        qh = [sb(f'G_qh{j}', [128, 512], BF16) for j in range(2)]
        pT = [sb(f'G_pT{i}', [128, 2, 512], BF16) for i in range(4)]
        accs = [sb(f'G_acc{i}', [128, 2, 512], F32) for i in range(2)]
        rl = sb('G_rl', [128, 2, 512], F32)
        o1 = sb('G_o1', [128, 512], F32)
        o2 = sb('G_o2', [128, 512], F32)
        sq = sb('G_sq', [128, 512], F32)
        ych = sb('G_ych', [128, 512], BF16)
        vv = S['v'].rearrange("(kb p) f -> p kb f", p=128)
        qtiles = tiles_of(TL)

        def load_head(h):
            j = h % 2
            P.dma('sp', kh[j][:, :], S['kT'][h * 128:h * 128 + 128, :], [], [f'G_kh{j}'], f'G_kh{j}')
            for b0 in range(0, NKB, 16):
                b1 = min(NKB, b0 + 16)
                P.dma('sp', vh[j][:, b0:b1, :], vv[:, b0:b1, h * 128:(h + 1) * 128], [], [f'G_vh{j}'], f'G_vh{j}')

        def load_q(h, ti):
            s, n = qtiles[ti]
            gi = (h * len(qtiles) + ti) % 2
            P.dma('sp', qh[gi][:, 0:n], S['qT'][h * 128:(h + 1) * 128, s:s + n], [], [f'G_qh{gi}'], f'G_qh{gi}')

        load_head(0)
        load_q(0, 0)
        for h in range(6):
            hj = h % 2
            for ti, (s, n) in enumerate(qtiles):
                gi = (h * len(qtiles) + ti) % 2
                qcur = qh[gi]
                qn_ = f'G_qh{gi}'
                if ti + 1 < len(qtiles):
                    load_q(h, ti + 1)
                elif h + 1 < 6:
                    load_q(h + 1, 0)
                if ti == 0 and h + 1 < 6:
                    load_head(h + 1)

                def emit_qk(kb):
                    pp = pS[kb % 2]
                    for comp in range(2):
                        P.op('pe', lambda e, comp=comp, kb=kb, pp=pp: e.matmul(pp[:, comp, 0:n], lhsT=kh[hj][64 * comp:64 * comp + 64, kb * 128:(kb + 1) * 128], rhs=qcur[64 * comp:64 * comp + 64, 0:n], start=True, stop=True),
                             [f'G_kh{hj}', qn_], [f'G_pS{kb % 2}'])
                emit_qk(0)
                first = [True, True]
                for kb in range(NKB):
                    if kb + 1 < NKB:
                        emit_qk(kb + 1)
                    pp = pS[kb % 2]
                    ppn = f'G_pS{kb % 2}'
                    pt = pT[kb % 4]
                    ptn = f'G_pT{kb % 4}'
                    P.op('act', lambda e, pp=pp, pt=pt: e.activation(out=pt[:, :, 0:n], in_=pp[:, :, 0:n], func=AF.Exp, scale=0.125), [ppn], [ptn])
                    for comp in range(2):
                        P.op('pe', lambda e, comp=comp, kb=kb, pt=pt: e.matmul(po[comp][:, 0:n], lhsT=vh[hj][:, kb, :], rhs=pt[:, comp, 0:n], start=(kb == 0), stop=(kb == NKB - 1)),
                             [f'G_vh{hj}', ptn], [f'G_po{comp}'])
                    P.op('pe', lambda e, kb=kb, pt=pt: e.matmul(pl[:, 0, 0:n], lhsT=k.ones_bf[:], rhs=pt[:, 0, 0:n], start=(kb == 0), stop=(kb == NKB - 1)),
                         ['ones_bf', ptn], ['G_pl0'])
                    ac = accs[1]
                    if kb == 0:
                        P.op('dve', lambda e, ac=ac, pt=pt: e.tensor_copy(out=ac[:, 1, 0:n], in_=pt[:, 1, 0:n]), [ptn], ['G_acc1'])
                    else:
                        P.op('dve', lambda e, ac=ac, pt=pt: e.tensor_tensor(out=ac[:, 1, 0:n], in0=ac[:, 1, 0:n], in1=pt[:, 1, 0:n], op=ALU.add), [ptn, 'G_acc1'], ['G_acc1'])
                P.op('pe', lambda e: e.matmul(pl[:, 1, 0:n], lhsT=k.ones_f[:], rhs=accs[1][:, 1, 0:n], start=True, stop=True), ['ones_f', 'G_acc1'], ['G_pl1'])
                P.op('dve', lambda e: e.reciprocal(out=rl[:, :, 0:n], in_=pl[:, :, 0:n]), ['G_pl0', 'G_pl1'], ['G_rl'])
                P.op('dve', lambda e: e.tensor_tensor(out=o1[:, 0:n], in0=po[0][:, 0:n], in1=rl[:, 0, 0:n], op=ALU.mult), ['G_po0', 'G_rl'], ['G_o1'])
                P.op('dve', lambda e: e.tensor_tensor(out=o2[:, 0:n], in0=po[1][:, 0:n], in1=rl[:, 1, 0:n], op=ALU.mult), ['G_po1', 'G_rl'], ['G_o2'])
                P.op('dve', lambda e: e.scalar_tensor_tensor(out=o1[:, 0:n], in0=o2[:, 0:n], scalar=neglam[:, 0:1], in1=o1[:, 0:n], op0=ALU.mult, op1=ALU.add),
                     ['G_o1', 'G_o2', 'G_neglam'], ['G_o1'])
                P.op('act', lambda e: e.activation(out=sq[:, 0:n], in_=o1[:, 0:n], func=AF.Square), ['G_o1'], ['G_sq'])
                P.op('pe', lambda e: e.matmul(pl[:, 0, 0:n], lhsT=k.ones_f[:], rhs=sq[:, 0:n], start=True, stop=True), ['ones_f', 'G_sq'], ['G_pl0', 'G_pl1'])
                P.op('act', lambda e: e.activation(out=sq[:, 0:n], in_=pl[:, 0, 0:n], func=AF.Sqrt, bias=k.cst[:, 0:1], scale=1.0 / 128), ['G_pl0', 'G_pl1', 'cst'], ['G_sq'])
                P.op('dve', lambda e: e.reciprocal(out=sq[:, 0:n], in_=sq[:, 0:n]), ['G_sq'], ['G_sq'])
                P.op('dve', lambda e: e.scalar_tensor_tensor(out=ych[:, 0:n], in0=o1[:, 0:n], scalar=gsub[:, 1:2], in1=sq[:, 0:n], op0=ALU.mult, op1=ALU.mult),
                     ['G_o1', 'G_sq', 'G_gsub'], ['G_ych'])
                P.dma('sp', yc_d[h * 128:(h + 1) * 128, s:s + n], ych[:, 0:n], ['G_ych'], [], 'G_ych')
    P.barrier()
    if stop_after == 'l1F1':
        return

    with ExitStack() as st:
        sb = lambda n, s, d: k.sb(n, s, d, st)
        ps = lambda n, s, d: k.ps(n, s, d, st)
        w_out = sb('w_out1', [128, 8, D], BF16)
        with ExitStack() as st2:
            fold_w_bf16(k, st2, 'w_out1', w_out, I['cd_w_out'], 8, k.gbc_d[1, 0, 0])
            P.barrier()
        cp = {}
        for nm, shp in (('conf_w', [128, 2, 31]), ('conf_b', [128, 2]), ('conf_lng', [128, 2]), ('conf_lnb', [128, 2])):
            cp[nm] = sb('H_' + nm, shp, F32)
            P.dma('sp', cp[nm][:], I[nm], [], ['H_' + nm], 'H_' + nm)
        yT = sb('H_yT', [128, 8, 512], BF16)
        gin = sb('H_gin', [128, 2, 542], F32)
        ca = sb('H_ca', [128, 512], F32)
        cb_ = sb('H_cb', [128, 512], F32)
        ct = [sb(f'H_ct{i}', [128, 512], F32) for i in range(2)]
        xm = sb('H_xm', [128, 2, 512], F32)
        sq = sb('H_sq', [128, 2, 512], F32)
        rstd = sb('H_rstd', [128, 512], F32)
        xt = sb('H_xt', [128, 4, D], F32)
        xo = sb('H_xo', [128, 4, D], F32)
        pm = ps('H_pm', [128, 512], F32)
        pv = ps('H_pv', [128, 512], F32)
        po = [ps(f'H_po{i}', [128, 512], F32) for i in range(4)]
        glv = S['gl'].rearrange("(c p) t -> p c t", p=128)
        ycv = yc_d.rearrange("(c p) t -> p c t", p=128)
        for (s, n) in tiles_of(TL):
            ng = n // 128
            P.dma('sp', yT[:, 0:6, 0:n], ycv[:, :, s:s + n], [], keys('H_yT', 8)[0:6], 'H_yT')
            P.dma('sp', gin[:, :, 0:n + 30], glv[:, :, PADZ + s - 15:PADZ + s + n + 15], [], keys('H_gin', 2), 'H_gin')
            P.dma('sp', xt[:, 0:ng, :], S['x1_l'][1 + s:1 + s + n, :].rearrange("(g p) f -> p g f", p=128), [], ['H_xt'], 'H_xt')
            for c in range(2):
                P.op('act', lambda e, c=c: e.activation(out=ca[:, 0:n], in_=gin[:, c, 0:n], func=AF.Identity, scale=cp['conf_w'][:, c, 0:1], bias=cp['conf_b'][:, c:c + 1]),
                     [f'H_gin.{c}', 'H_conf_w', 'H_conf_b'], ['H_ca'])
                P.op('pool', lambda e, c=c: e.tensor_scalar(out=cb_[:, 0:n], in0=gin[:, c, 1:1 + n], scalar1=cp['conf_w'][:, c, 1:2], scalar2=None, op0=ALU.mult),
                     [f'H_gin.{c}', 'H_conf_w'], ['H_cb'])
                for t in range(2, 31):
                    if t % 2 == 0:
                        P.op('dve', lambda e, c=c, t=t: e.scalar_tensor_tensor(out=ca[:, 0:n], in0=gin[:, c, t:t + n], scalar=cp['conf_w'][:, c, t:t + 1], in1=ca[:, 0:n],
                                                                               op0=ALU.mult, op1=ALU.add), [f'H_gin.{c}', 'H_ca'], ['H_ca'])
                    else:
                        ctt = ct[(t // 2) % 2]
                        ctn = f'H_ct{(t // 2) % 2}'
                        P.op('act', lambda e, c=c, t=t, ctt=ctt: e.activation(out=ctt[:, 0:n], in_=gin[:, c, t:t + n], func=AF.Identity, scale=cp['conf_w'][:, c, t:t + 1], bias=k.cst[:, 2:3]),
                             [f'H_gin.{c}', 'H_conf_w', 'cst'], [ctn])
                        P.op('pool', lambda e, ctt=ctt: e.tensor_tensor(out=cb_[:, 0:n], in0=cb_[:, 0:n], in1=ctt[:, 0:n], op=ALU.add), [ctn, 'H_cb'], ['H_cb'])
                P.op('dve', lambda e, c=c: e.tensor_tensor(out=xm[:, c, 0:n], in0=ca[:, 0:n], in1=cb_[:, 0:n], op=ALU.add), ['H_ca', 'H_cb'], [f'H_xm.{c}'])
            for c in range(2):
                P.op('pe', lambda e, c=c: e.matmul(pm[:, 0:n], lhsT=k.ones_f[:], rhs=xm[:, c, 0:n], start=(c == 0), stop=(c == 1)), ['ones_f', f'H_xm.{c}'], ['H_pm'])
            for c in range(2):
                P.op('dve', lambda e, c=c: e.scalar_tensor_tensor(out=xm[:, c, 0:n], in0=pm[:, 0:n], scalar=-1.0 / 256, in1=xm[:, c, 0:n], op0=ALU.mult, op1=ALU.add),
                     ['H_pm', f'H_xm.{c}'], [f'H_xm.{c}'])
                P.op('act', lambda e, c=c: e.activation(out=sq[:, c, 0:n], in_=xm[:, c, 0:n], func=AF.Square), [f'H_xm.{c}'], [f'H_sq.{c}'])
            for c in range(2):
                P.op('pe', lambda e, c=c: e.matmul(pv[:, 0:n], lhsT=k.ones_f[:], rhs=sq[:, c, 0:n], start=(c == 0), stop=(c == 1)), ['ones_f', f'H_sq.{c}'], ['H_pv'])
            P.op('act', lambda e: e.activation(out=rstd[:, 0:n], in_=pv[:, 0:n], func=AF.Sqrt, bias=k.cst[:, 0:1], scale=1.0 / 256), ['H_pv', 'cst'], ['H_rstd'])
            P.op('dve', lambda e: e.reciprocal(out=rstd[:, 0:n], in_=rstd[:, 0:n]), ['H_rstd'], ['H_rstd'])
            for c in range(2):
                P.op('dve', lambda e, c=c: e.tensor_tensor(out=xm[:, c, 0:n], in0=xm[:, c, 0:n], in1=rstd[:, 0:n], op=ALU.mult), [f'H_xm.{c}', 'H_rstd'], [f'H_xm.{c}'])
                P.op('act', lambda e, c=c: e.activation(out=yT[:, 6 + c, 0:n], in_=xm[:, c, 0:n], func=AF.Silu, scale=cp['conf_lng'][:, c:c + 1], bias=cp['conf_lnb'][:, c:c + 1]),
                     [f'H_xm.{c}', 'H_conf_lng', 'H_conf_lnb'], [f'H_yT.{6 + c}'])
            for g in range(ng):
                for nt in range(2):
                    pp = po[(2 * g + nt) % 4]
                    ppn = f'H_po{(2 * g + nt) % 4}'
                    for kc in range(8):
                        P.op('pe', lambda e, g=g, nt=nt, kc=kc, pp=pp: e.matmul(pp[:, :], lhsT=yT[:, kc, g * 128:(g + 1) * 128], rhs=w_out[:, kc, nt * 512:(nt + 1) * 512],
                                                                               start=(kc == 0), stop=(kc == 7)), [f'H_yT.{kc}', f'w_out1.{kc}'], [ppn])
                    P.op('dve', lambda e, g=g, nt=nt, pp=pp: e.tensor_tensor(out=xo[:, g, nt * 512:(nt + 1) * 512], in0=pp[:, :], in1=xt[:, g, nt * 512:(nt + 1) * 512], op=ALU.add),
                         [ppn, 'H_xt'], [f'H_xo.{g}'])
            P.dma('sp', S['x15'][1 + s:1 + s + n, :].rearrange("(g p) f -> p g f", p=128), xo[:, 0:ng, :], keys('H_xo', 4)[0:ng], [], 'H_xo')
    P.barrier()
    if stop_after == 'l1F2':
        return
    ffn_phase(k, 1, 0, S['x15'], None, TL, final=True, flush=(not k.half), out_rows=k.TO)
    P.barrier()


def _fm(v, nch):
    return np.ascontiguousarray(np.asarray(v, np.float32).reshape(nch, 128).T)


def prep_shared(inp, T):
    f = lambda a: np.ascontiguousarray(np.asarray(a, np.float32))
    d = {}
    d['mod_w'] = f(inp['mod_w'])
    d['mod_b'] = f(inp['mod_b'])
    d['modb_fm'] = f(np.asarray(inp['mod_b']).reshape(2, 6, 8, 128).transpose(3, 0, 1, 2))
    ng = np.stack([np.asarray(inp['norm_mix_g']), np.asarray(inp['norm_ffn_g'])], axis=1)
    d['ng_fm'] = f(ng.reshape(2, 2, 8, 128).transpose(3, 0, 1, 2))
    d['final_g_bc'] = f(np.broadcast_to(np.asarray(inp['final_g'])[None, :], (128, D)))
    d['ident'] = f(np.eye(128))
    d['ab_w_in'] = f(inp['ab_w_in'][0])
    d['ab_w_out'] = f(inp['ab_w_out'][0])
    cw4 = np.asarray(inp['lru_conv_w'][0], np.float32).reshape(4, 6, 128).transpose(2, 1, 0)
    d['lru_cw'] = f(np.concatenate([cw4, np.zeros((128, 6, 1), np.float32)], axis=2))
    d['pool_fl'] = f(np.stack([np.ones(128), np.zeros(128)], axis=1))
    d['lru_cb'] = _fm(inp['lru_conv_b'][0], 6)
    for nm, src in (('lru_bdA', inp['lru_wa'][0]), ('lru_bdX', inp['lru_wx'][0])):
        src = np.asarray(src)
        bd = np.zeros((128, 2, 6, 128), np.float32)
        for dd in range(2):
            for c in range(6):
                bd[0:64, dd, c, 0:64] = src[dd, 2 * c]
                bd[64:128, dd, c, 64:128] = src[dd, 2 * c + 1]
        d[nm] = bd
    for nm, src in (('lru_ba', inp['lru_ba'][0]), ('lru_bx', inp['lru_bx'][0]), ('lru_lam', inp['lru_lambda'][0])):
        d[nm] = f(np.asarray(src).reshape(2, 6, 128).transpose(2, 0, 1))
    pw = np.asarray(inp['pool_w'][0])
    bd = np.zeros((128, 2, 128), np.float32)
    for ch in range(2):
        bd[0:64, ch, 0:64] = pw[2 * ch]
        bd[64:128, ch, 64:128] = pw[2 * ch + 1]
    d['pool_bd'] = bd
    d['pool_scale'] = _fm(inp['pool_scale'][0], 2)
    invw = np.zeros((128, 2), np.float32)
    corr = np.ones((128, 2, 2, 16), np.float32)
    wins = (2, 4, 8, 16)
    Lbig = 1 << 20
    for g, w in enumerate(wins):
        ch, half = g // 2, g % 2
        psl = slice(64 * half, 64 * half + 64)
        invw[psl, ch] = 1.0 / w
        for i in range(16):
            t = i
            cnt = (t + w - w // 2) - max(t - w // 2, 0)
            corr[psl, ch, 0, i] = float(w) / cnt
            t = Lbig - 16 + i
            cnt = min(t + w - w // 2, Lbig) - (t - w // 2)
            corr[psl, ch, 1, i] = float(w) / cnt
    d['pool_invw'] = invw
    d['pool_corr'] = corr
    d['ffn_w_up'] = f(inp['ffn_w_up'])
    d['ffn_w_down'] = f(inp['ffn_w_down'])
    d['ffn_cw'] = f(np.asarray(inp['ffn_conv_w']).reshape(2, 3, 2 * NFC, 128).transpose(3, 0, 2, 1))
    d['ffn_cb'] = f(np.asarray(inp['ffn_conv_b']).reshape(2, 2 * NFC, 128).transpose(2, 0, 1))
    w_in = np.asarray(inp['cd_w_in'][0], np.float32)
    d['cd_w_in'] = f(w_in)
    qk = w_in[:, :1536].reshape(D, 1536 // 32, 2, 16)
    d['cd_w_sw'] = f(qk[:, :, ::-1, :].reshape(D, 1536))
    d['cd_w_out'] = f(inp['cd_w_out'][0])
    t = np.arange(T)
    row = (t // GRID_W).astype(np.float32)
    col = (t % GRID_W).astype(np.float32)
    inv = (10000.0 ** (-np.arange(16, dtype=np.float32) / 16)).astype(np.float32)
    ang_r = (row[:, None] * inv).astype(np.float32)
    ang_c = (col[:, None] * inv).astype(np.float32)
    rc = np.zeros((128, T), np.float32)
    rs = np.zeros((128, T), np.float32)
    for p in range(128):
        dd = p % 64
        ang = ang_r if dd < 32 else ang_c
        fq = dd % 16
        first = (dd % 32) < 16
        rc[p] = np.cos(ang[:, fq])
        rs[p] = (-1.0 if first else 1.0) * np.sin(ang[:, fq])
    d['rope_c'] = rc
    d['rope_s'] = rs
    d['diff_l'] = f(np.stack([np.asarray(inp['diff_lq1'][0]), np.asarray(inp['diff_lk1'][0]),
                              np.asarray(inp['diff_lq2'][0]), np.asarray(inp['diff_lk2'][0])])[None])
    d['subln_g'] = f(np.asarray(inp['diff_subln_g'][0]).reshape(128, 1))
    d['conf_w'] = f(np.asarray(inp['conf_dw_w'][0]).reshape(31, 2, 128).transpose(2, 1, 0))
    d['conf_b'] = _fm(inp['conf_dw_b'][0], 2)
    d['conf_lng'] = _fm(inp['conf_ln_g'][0], 2)
    d['conf_lnb'] = _fm(inp['conf_ln_b'][0], 2)
    return d


def prep_rev(shared, T):
    f = lambda a: np.ascontiguousarray(np.asarray(a, np.float32))
    d = dict(shared)
    cw = shared['lru_cw']
    d['lru_cw'] = f(cw[:, :, ::-1])
    for nm in ('lru_bdA', 'lru_bdX', 'lru_ba', 'lru_bx', 'lru_lam'):
        d[nm] = f(shared[nm][:, ::-1])
    d['pool_fl'] = f(np.stack([np.zeros(128), np.ones(128)], axis=1))
    corr = np.ones((128, 2, 2, 16), np.float32)
    Lbig = 1 << 20
    for g, w in enumerate((2, 4, 8, 16)):
        ch, half = g // 2, g % 2
        psl = slice(64 * half, 64 * half + 64)
        for i in range(16):
            r = i
            cnt = (r + w // 2) - max(r - w // 2 + 1, 0) + 1
            corr[psl, ch, 0, i] = float(w) / cnt
            r = Lbig - 16 + i
            cnt = min(r + w // 2, Lbig - 1) - (r - w // 2 + 1) + 1
            corr[psl, ch, 1, i] = float(w) / cnt
    d['pool_corr'] = corr
    d['ffn_cw'] = f(shared['ffn_cw'][:, :, :, ::-1])
    d['conf_w'] = f(shared['conf_w'][:, :, ::-1])
    d['rope_c'] = f(shared['rope_c'][:, ::-1])
    d['rope_s'] = f(shared['rope_s'][:, ::-1])
    return d


def prep_core(inp, shared, b, T, rev=False):
    m = dict(shared)
    x = np.asarray(inp['x'][b, :T], np.float32)
    ctx = np.asarray(inp['ctx'][b], np.float32)
    if rev:
        x = x[::-1]
        ctx = ctx[::-1]
    m['x'] = np.ascontiguousarray(x)
    m['ctx'] = np.ascontiguousarray(ctx)
    m['c_fm'] = np.ascontiguousarray(np.stack([_fm(inp['c'][b], 8), _fm(inp['c_ctx'], 8)], axis=1))
    return m


_CACHE = {}


def kernel(**inputs):
    T = inputs['x'].shape[1]
    Bn = inputs['x'].shape[0]
    if T not in _CACHE:
        _CACHE[T] = build(T)
    kk = _CACHE[T]
    shared = prep_shared(inputs, T)
    shared_r = prep_rev(shared, T)
    in_maps = [prep_core(inputs, shared, b, T, rev=False) for b in range(Bn)] + \
              [prep_core(inputs, shared_r, b, T, rev=True) for b in range(Bn)]
    res = run_bass_kernel_spmd(kk.nc, in_maps, core_ids=list(range(2 * Bn)))
    out = np.empty((Bn, T, D), np.float32)
    for b in range(Bn):
        out[b, :T // 2] = np.asarray(res.results[b]['out'], np.float32)
        out[b, T // 2:] = np.asarray(res.results[Bn + b]['out'], np.float32)[::-1]
    return out
```

```python
import numpy as np
import math
import os
from contextlib import ExitStack
import concourse.bass as bass
import concourse.mybir as mybir
from concourse.bass_utils import run_bass_kernel_spmd
from concourse.ap import AP

F32 = mybir.dt.float32
BF16 = mybir.dt.bfloat16
AF = mybir.ActivationFunctionType
ALU = mybir.AluOpType
AX = mybir.AxisListType

D = 1024
LC = 256
DFF = 2816
NFC = DFF // 128
PADZ = 16
EPS = 1e-6
GRID_W = 64
LAMBDA_INIT1 = 0.8 - 0.6 * math.exp(-0.3 * 1)
SAME_ENGINE_SYNC = True


def rev(ap):
    a = [list(x) for x in ap.ap]
    st, n = a[-1]
    a[-1] = [-st, n]
    return AP(ap.tensor, ap.offset + st * (n - 1), a)


class Prog:
    def __init__(self, nc, es):
        self.nc = nc
        self.es = es
        self.eng = {'pe': nc.tensor, 'act': nc.scalar, 'dve': nc.vector, 'pool': nc.gpsimd, 'sp': nc.sync}
        self.esem = {e: es.enter_context(nc.semaphore('S_' + e)) for e in ('pe', 'act', 'dve', 'pool')}
        self.ecnt = {e: 0 for e in self.esem}
        self.dsem = {}
        self.dpool = []
        self.nd = 0
        self.waited = {e: {} for e in self.eng}
        self.lastw = {}
        self.rd = {}
        self.ninstr = 0

    def _need(self, reads, writes):
        ev = {}

        def add(e):
            if e is None:
                return
            k, sem, val = e
            if k not in ev or ev[k][1] < val:
                ev[k] = (sem, val)
        for r in reads:
            add(self.lastw.get(r))
        for w in writes:
            add(self.lastw.get(w))
            for e in self.rd.get(w, {}).items():
                add((e[0], e[1][0], e[1][1]))
        return ev

    def _wait(self, e, ev):
        for k, (sem, val) in ev.items():
            if k == 'S_' + e and (e == 'pe' or not SAME_ENGINE_SYNC):
                continue
            if self.waited[e].get(k, 0) < val:
                self.eng[e].wait_ge(sem, val)
                self.waited[e][k] = val
                self.ninstr += 1

    def _commit(self, ev, reads, writes):
        k, sem, val = ev
        for w in writes:
            self.lastw[w] = ev
            self.rd[w] = {}
        for r in reads:
            self.rd.setdefault(r, {})[k] = (sem, val)

    def op(self, e, fn, reads, writes):
        self._wait(e, self._need(reads, writes))
        ins = fn(self.eng[e])
        self.ecnt[e] += 1
        ins.then_inc(self.esem[e], 1)
        self.ninstr += 1
        self._commit(('S_' + e, self.esem[e], self.ecnt[e]), reads, writes)

    def dma(self, q, out, in_, reads, writes, key):
        self._wait(q, self._need(reads, writes))
        if key not in self.dsem:
            if self.dpool:
                self.dsem[key] = self.dpool.pop()
            else:
                nm = 'D%d' % self.nd
                self.nd += 1
                self.dsem[key] = [self.es.enter_context(self.nc.semaphore(nm)), 0, nm]
        d = self.dsem[key]
        ins = self.eng[q].dma_start(out=out, in_=in_)
        d[1] += 16
        ins.then_inc(d[0], 16)
        self.ninstr += 1
        self._commit((d[2], d[0], d[1]), reads, writes)

    def barrier(self):
        ev = {}
        for e in self.esem:
            if self.ecnt[e] > 0:
                ev['S_' + e] = (self.esem[e], self.ecnt[e])
        for k, d in self.dsem.items():
            if d[1] > 0:
                ev[d[2]] = (d[0], d[1])
        for e in self.eng:
            self._wait(e, dict(ev))
        self.lastw = {}
        self.rd = {}
        for k, d in self.dsem.items():
            self.dpool.append(d)
        self.dsem = {}


def keys(name, n):
    return [f"{name}.{i}" for i in range(n)]


def tiles_of(T, w=512):
    out = []
    s = 0
    while s < T:
        n = min(w, T - s)
        out.append((s, n))
        s += n
    return out


class K:
    pass


def build(T, dbg=False, stop_after=None, half=True):
    nc = bass.Bass("TRN2", target_bir_lowering=False)
    k = K()
    k.nc = nc
    k.T = T
    k.half = half
    k.TL = (T // 2 + 128) if half else T
    k.TO = (T // 2) if half else T

    def din(name, shape, dt=F32):
        return nc.dram_tensor(name, list(shape), dt, kind="ExternalInput").ap()

    def dscr(name, shape, dt=F32, out=False):
        kind = "ExternalOutput" if (out or dbg) else "Internal"
        return nc.dram_tensor(name, list(shape), dt, kind=kind).ap()

    I = {}
    I['x'] = din('x', [T, D])
    I['ctx'] = din('ctx', [LC, D])
    I['c_fm'] = din('c_fm', [128, 2, 8])
    I['mod_w'] = din('mod_w', [2, D, 6 * D])
    I['modb_fm'] = din('modb_fm', [128, 2, 6, 8])
    I['mod_b'] = din('mod_b', [2, 6 * D])
    I['ng_fm'] = din('ng_fm', [128, 2, 2, 8])
    I['final_g_bc'] = din('final_g_bc', [128, D])
    I['ident'] = din('ident', [128, 128])
    I['ab_w_in'] = din('ab_w_in', [D, 1792])
    I['ab_w_out'] = din('ab_w_out', [D, D])
    I['lru_cw'] = din('lru_cw', [128, 6, 5])
    I['pool_fl'] = din('pool_fl', [128, 2])
    I['lru_cb'] = din('lru_cb', [128, 6])
    I['lru_bdA'] = din('lru_bdA', [128, 2, 6, 128])
    I['lru_bdX'] = din('lru_bdX', [128, 2, 6, 128])
    I['lru_ba'] = din('lru_ba', [128, 2, 6])
    I['lru_bx'] = din('lru_bx', [128, 2, 6])
    I['lru_lam'] = din('lru_lam', [128, 2, 6])
    I['pool_bd'] = din('pool_bd', [128, 2, 128])
    I['pool_scale'] = din('pool_scale', [128, 2])
    I['pool_invw'] = din('pool_invw', [128, 2])
    I['pool_corr'] = din('pool_corr', [128, 2, 2, 16])
    I['ffn_w_up'] = din('ffn_w_up', [2, D, 2 * DFF])
    I['ffn_w_down'] = din('ffn_w_down', [2, DFF, D])
    I['ffn_cw'] = din('ffn_cw', [128, 2, 2 * NFC, 3])
    I['ffn_cb'] = din('ffn_cb', [128, 2, 2 * NFC])
    I['cd_w_in'] = din('cd_w_in', [D, 2816])
    I['cd_w_sw'] = din('cd_w_sw', [D, 1536])
    I['cd_w_out'] = din('cd_w_out', [D, D])
    I['rope_c'] = din('rope_c', [128, T])
    I['rope_s'] = din('rope_s', [128, T])
    I['diff_l'] = din('diff_l', [1, 4, 64])
    I['subln_g'] = din('subln_g', [128, 1])
    I['conf_w'] = din('conf_w', [128, 2, 31])
    I['conf_b'] = din('conf_b', [128, 2])
    I['conf_lng'] = din('conf_lng', [128, 2])
    I['conf_lnb'] = din('conf_lnb', [128, 2])
    k.I = I

    k.out = nc.dram_tensor('out', [k.TO, D], F32, kind="ExternalOutput").ap()
    S = {}
    for nm, TT in (('l', T), ('c', LC)):
        S['z_' + nm] = dscr('z_' + nm, [1792, PADZ + TT + PADZ])
        S['xa_' + nm] = dscr('xa_' + nm, [768, TT])
        S['hf_' + nm] = dscr('hf_' + nm, [768, TT])
        S['x05_' + nm] = dscr('x05_' + nm, [1 + TT, D])
        S['x1_' + nm] = dscr('x1_' + nm, [1 + TT + 128, D])
    S['x15'] = dscr('x15', [1 + T, D])
    S['x2'] = dscr('x2', [1 + T + 128, D])
    S['qT'] = dscr('qT', [768, T], BF16)
    S['kT'] = dscr('kT', [768, T + LC], BF16)
    S['v'] = dscr('v', [T + LC, 768], BF16)
    S['gl'] = dscr('gl', [256, PADZ + T + PADZ])
    k.S = S

    with ExitStack() as es:
        P = Prog(nc, es)
        k.P = P

        uid = [0]

        def sb(name, shape, dt, st=es):
            uid[0] += 1
            return st.enter_context(nc.sbuf_tensor(f"{name}_s{uid[0]}", list(shape), dt))

        def ps(name, shape, dt, st=es):
            uid[0] += 1
            return st.enter_context(nc.psum_tensor(f"{name}_p{uid[0]}", list(shape), dt))
        k.sb = sb
        k.ps = ps

        ident = sb('ident', [128, 128], BF16)
        P.dma('pool', ident[:], I['ident'], [], ['ident'], 'ident')
        k.ident = ident
        cst = sb('cst', [128, 8], F32)
        P.op('dve', lambda e: e.memset(cst[:, 0:1], EPS), [], ['cst'])
        P.op('dve', lambda e: e.memset(cst[:, 1:2], 1.0), [], ['cst'])
        P.op('dve', lambda e: e.memset(cst[:, 2:3], 0.0), [], ['cst'])
        k.cst = cst
        zero = sb('zero', [128, 512], F32)
        P.op('dve', lambda e: e.memset(zero[:], 0.0), [], ['zero'])
        k.zero = zero
        ones_bf = sb('ones_bf', [128, 128], BF16)
        P.op('dve', lambda e: e.memset(ones_bf[:], 1.0), [], ['ones_bf'])
        k.ones_bf = ones_bf
        ones_f = sb('ones_f', [128, 128], F32)
        P.op('dve', lambda e: e.memset(ones_f[:], 1.0), [], ['ones_f'])
        k.ones_f = ones_f

        modfm = sb('modfm', [128, 2, 2, 4, 8], F32)
        k.modfm = modfm
        k.gbc_d = nc.dram_tensor('gbc_d', [2, 2, 2, 128, D], F32, kind="Internal").ap()

        if os.environ.get('DBG_ONLY_F1'):
            layer1(k, 'l1F1')
            return finish(k, es)
        phase_adaln(k)
        P.barrier()
        if dbg:
            mdbg = nc.dram_tensor('modfm_dbg', [128, 2, 2, 4, 8], F32, kind="ExternalOutput").ap()
            P.dma('sp', mdbg, modfm[:], [], [], 'mdbg')
            gdbg = nc.dram_tensor('gbc_dbg', [2, 2, 2, 128, D], F32, kind="ExternalOutput").ap()
            P.dma('sp', gdbg, k.gbc_d, [], [], 'gdbg')
        if stop_after == 'adaln':
            return finish(k, es)

        layer0(k, stop_after)
        if stop_after is not None and stop_after.startswith('l0'):
            return finish(k, es)
        layer1(k, stop_after)
        return finish(k, es)


def finish(k, es):
    k.P.barrier()
    k.ninstr = k.P.ninstr
    return k


def phase_adaln(k):
    nc, P, I = k.nc, k.P, k.I
    with ExitStack() as st:
        sb = lambda n, s, d: k.sb(n, s, d, st)
        ps = lambda n, s, d: k.ps(n, s, d, st)
        cf = sb('ad_cf', [128, 2, 8], F32)
        P.dma('sp', cf[:], I['c_fm'], [], ['ad_cf'], 'ad_cf')
        sc = sb('ad_sc', [128, 2, 8], F32)
        P.op('act', lambda e: e.activation(out=sc[:], in_=cf[:], func=AF.Silu), ['ad_cf'], ['ad_sc'])
        rep = sb('ad_rep', [128, 2, 8, 128], F32)
        for s_ in range(2):
            for kc in range(8):
                P.op('dve', lambda e, s_=s_, kc=kc: e.tensor_copy(out=rep[:, s_, kc, :], in_=sc[:, s_, kc:kc + 1].to_broadcast([128, 128])),
                     ['ad_sc'], [f'ad_rep.{s_}.{kc}'])
        modb = sb('ad_modb', [128, 2, 6, 8], F32)
        P.dma('sp', modb[:], I['modb_fm'], [], ['ad_modb'], 'ad_modb')
        ng = sb('ad_ng', [128, 2, 2, 8], F32)
        P.dma('sp', ng[:], I['ng_fm'], [], ['ad_ng'], 'ad_ng')
        brow = sb('ad_brow', [1, 2, 6 * D], F32)
        P.dma('sp', brow[:], I['mod_b'].rearrange("(o l) n -> o l n", o=1), [], ['ad_brow'], 'ad_brow')
        wt = [sb(f'ad_w{i}', [128, 6 * D], F32) for i in range(2)]
        gst = sb('ad_gst', [128, 2, D], F32)
        facc = sb('ad_facc', [128, 32, 2], F32)
        pfm = ps('ad_pfm', [128, 32, 2], F32)
        pbc = [ps(f'ad_pbc{i}', [128, 512], F32) for i in range(4)]
        for l in range(2):
            for pss in range(2):
                for kc in range(8):
                    w = wt[kc % 2]
                    wk = f'ad_w{kc % 2}'
                    P.dma('sp', w[:], I['mod_w'][l, kc * 128:(kc + 1) * 128, :], [], [wk], wk)
                    if pss == 0:
                        jmap = [0, 1, 3, 4]
                        for jj, j in enumerate(jmap):
                            for fc in range(8):
                                col = j * D + fc * 128
                                P.op('pe', lambda e, w=w, col=col, jj=jj, fc=fc, kc=kc: e.matmul(
                                    pfm[:, jj * 8 + fc, :], lhsT=w[:, col:col + 128], rhs=sc[:, :, kc],
                                    start=True, stop=True), [wk, 'ad_sc'], ['ad_pfm'])
                        if kc == 0:
                            P.op('dve', lambda e: e.tensor_copy(out=facc[:], in_=pfm[:]), ['ad_pfm'], ['ad_facc'])
                        else:
                            P.op('dve', lambda e: e.tensor_tensor(out=facc[:], in0=pfm[:], in1=facc[:], op=ALU.add), ['ad_pfm', 'ad_facc'], ['ad_facc'])
                    if True:
                        s_ = pss
                        for nt in range(4):
                            gj = 2 if nt < 2 else 5
                            col = gj * D + (nt % 2) * 512
                            P.op('pe', lambda e, w=w, col=col, nt=nt, kc=kc, s_=s_: e.matmul(
                                pbc[nt][:], lhsT=rep[:, s_, kc, :], rhs=w[:, col:col + 512],
                                start=(kc == 0), stop=False), [wk, f'ad_rep.{s_}.{kc}'], [f'ad_pbc{nt}'])
                if pss == 0:
                    jmap = [0, 1, 3, 4]
                    for s_ in range(2):
                        for jj, j in enumerate(jmap):
                            P.op('dve', lambda e, s_=s_, jj=jj, j=j, l=l: e.tensor_tensor(
                                out=k.modfm[:, l, s_, jj, :], in0=facc[:, jj * 8:(jj + 1) * 8, s_], in1=modb[:, l, j, :], op=ALU.add),
                                ['ad_facc', 'ad_modb'], [f'modfm.{l}.{s_}.{jj}'])
                        for jj, which in ((1, 0), (3, 1)):
                            P.op('dve', lambda e, s_=s_, jj=jj, which=which, l=l: e.scalar_tensor_tensor(
                                out=k.modfm[:, l, s_, jj, :], in0=k.modfm[:, l, s_, jj, :], scalar=1.0, in1=ng[:, l, which, :],
                                op0=ALU.add, op1=ALU.mult), [f'modfm.{l}.{s_}.{jj}', 'ad_ng'], [f'modfm.{l}.{s_}.{jj}'])
                if True:
                    s_ = pss
                    for nt in range(4):
                        gj = 2 if nt < 2 else 5
                        col = gj * D + (nt % 2) * 512
                        P.op('pe', lambda e, nt=nt, col=col, l=l: e.matmul(
                            pbc[nt][:], lhsT=k.ones_f[0:1, :], rhs=brow[0:1, l, col:col + 512], start=False, stop=True),
                            ['ones_f', 'ad_brow'], [f'ad_pbc{nt}'])
                        P.op('act', lambda e, nt=nt: e.copy(out=gst[:, nt // 2, (nt % 2) * 512:(nt % 2) * 512 + 512], in_=pbc[nt][:]),
                            [f'ad_pbc{nt}'], [f'ad_gst.{nt}'])
                    for j2 in range(2):
                        P.dma('sp', k.gbc_d[l, s_, j2], gst[:, j2, :], [f'ad_gst.{2 * j2}', f'ad_gst.{2 * j2 + 1}'], [], 'ad_gst')
        P.barrier()


def load_w_bf16(k, name, dst, src_ap, nk, ncols, colblk=2048):
    P = k.P
    for kc in range(nk):
        for c0 in range(0, ncols, colblk):
            c1 = min(ncols, c0 + colblk)
            P.dma('pool', dst[:, kc, c0:c1], src_ap[kc * 128:(kc + 1) * 128, c0:c1], [], [f'{name}.{kc}'], f'{name}.{kc}')


def fold_w_bf16(k, st, name, dst, src_ap, nk, gb_dram):
    P = k.P
    stg = [k.sb(f'{name}_stg{i}', [128, D], F32, st) for i in range(2)]
    gb = k.sb(f'{name}_gb', [128, D], F32, st)
    P.dma('sp', gb[:], gb_dram, [], [f'{name}_gb'], f'{name}_gb')
    gb_ap = gb[:]
    for kc in range(nk):
        s_ = stg[kc % 2]
        sk = f'{name}_stg{kc % 2}'
        P.dma('sp', s_[:], src_ap[kc * 128:(kc + 1) * 128, :], [], [sk], sk)
        P.op('dve', lambda e, s_=s_, kc=kc: e.tensor_tensor(out=dst[:, kc, :], in0=s_[:], in1=gb_ap, op=ALU.mult),
             [sk, f'{name}_gb'], [f'{name}.{kc}'])


def modulate_tile(k, B, src_rows, n, l, s_, jsh, hT, hTname, part=0):
    P = k.P
    ng = n // 128
    X, Xn = B['xt'], B['xtname']
    ss = B['ss']
    xn = B['xn']
    for g0 in (range(0, ng, 2) if part in (0, 1) else []):
        gg = min(2, ng - g0)
        P.dma('sp', X[:, 0:gg, :], src_rows[g0 * 128:(g0 + gg) * 128, :].rearrange("(g p) f -> p g f", p=128), [], [Xn], Xn)
        P.op('dve', lambda e: e.memset(ss[:, 0:2], 0.0), [], [B['ssname']])
        for g in range(gg):
            P.op('act', lambda e, g=g: e.activation(out=B['junk'][:], in_=X[:, g, :], func=AF.Square, accum_out=ss[:, g:g + 1]),
                 [Xn], [B['ssname'], B['junkname']])
        P.op('act', lambda e, gg=gg: e.activation(out=ss[:, 4:4 + gg], in_=ss[:, 0:gg], func=AF.Sqrt, bias=k.cst[:, 0:1], scale=1.0 / D),
             [B['ssname'], 'cst'], [B['ssname']])
        P.op('dve', lambda e, gg=gg: e.reciprocal(out=ss[:, 8:8 + gg], in_=ss[:, 4:4 + gg]), [B['ssname']], [B['ssname']])
        for g in range(gg):
            if g % 2 == 0:
                P.op('dve', lambda e, g=g, g0=g0: e.tensor_scalar(out=xn[:, g0 + g, :], in0=X[:, g, :], scalar1=ss[:, 8 + g:9 + g], scalar2=None, op0=ALU.mult),
                     [Xn, B['ssname']], [f"{B['xnname']}.{g0 + g}"])
            else:
                P.op('act', lambda e, g=g, g0=g0: e.activation(out=xn[:, g0 + g, :], in_=X[:, g, :], func=AF.Identity, scale=ss[:, 8 + g:9 + g], bias=k.cst[:, 2:3]),
                     [Xn, B['ssname'], 'cst'], [f"{B['xnname']}.{g0 + g}"])
    if part == 1:
        return
    for fc in range(8):
        tp = B['tp'][fc % 2]
        tpn = B['tpname'][fc % 2]
        for g in range(ng):
            P.op('pe', lambda e, g=g, fc=fc, tp=tp: e.transpose(out=tp[:, g * 128:(g + 1) * 128], in_=xn[:, g, fc * 128:(fc + 1) * 128], identity=k.ident[:]),
                 [f"{B['xnname']}.{g}", 'ident'], [tpn])
        P.op('act', lambda e, fc=fc, tp=tp: e.activation(out=hT[:, fc, 0:n], in_=tp[:, 0:n], func=AF.Identity,
                                                          scale=k.modfm[:, l, s_, jsh + 1, fc:fc + 1], bias=k.modfm[:, l, s_, jsh, fc:fc + 1]),
             [tpn], [f'{hTname}.{fc}'])


def mod_bufs(k, st, pfx):
    B = {}
    B['xt'] = k.sb(pfx + 'xt', [128, 2, D], F32, st)
    B['xtname'] = pfx + 'xt'
    B['xn'] = k.sb(pfx + 'xn', [128, 4, D], BF16, st)
    B['xnname'] = pfx + 'xn'
    B['junk'] = k.sb(pfx + 'junk', [128, D], BF16, st)
    B['junkname'] = pfx + 'junk'
    B['ss'] = k.sb(pfx + 'ss', [128, 12], F32, st)
    B['ssname'] = pfx + 'ss'
    B['tp'] = [k.ps(pfx + f'tp{i}', [128, 512], BF16, st) for i in range(2)]
    B['tpname'] = [pfx + f'tp{i}' for i in range(2)]
    return B


def layer0(k, stop_after):
    nc, P, I, S = k.nc, k.P, k.I, k.S
    with ExitStack() as st:
        sb = lambda n, s, d: k.sb(n, s, d, st)
        lp = {}
        for nm, shp in (('lru_cw', [128, 6, 5]), ('pool_fl', [128, 2]), ('lru_cb', [128, 6]), ('lru_ba', [128, 2, 6]), ('lru_bx', [128, 2, 6]),
                        ('lru_lam', [128, 2, 6]), ('pool_scale', [128, 2]), ('pool_invw', [128, 2]), ('pool_corr', [128, 2, 2, 16])):
            lp[nm] = sb('p_' + nm, shp, F32)
            P.dma('sp', lp[nm][:], I[nm], [], ['p_' + nm], 'p_' + nm)
        for nm, shp in (('lru_bdA', [128, 2, 6, 128]), ('lru_bdX', [128, 2, 6, 128]), ('pool_bd', [128, 2, 128])):
            lp[nm] = sb('p_' + nm, shp, BF16)
            P.dma('pool', lp[nm][:], I[nm], [], ['p_' + nm], 'p_' + nm)
        cl = sb('p_cl', [128, 2, 2, 6], F32)
        tmp = sb('p_cltmp', [128, 2, 6], F32)
        P.op('act', lambda e: e.activation(out=tmp[:], in_=lp['lru_lam'][:], func=AF.Exp, scale=-1.0), ['p_lru_lam'], ['p_cltmp'])
        P.op('act', lambda e: e.activation(out=tmp[:], in_=tmp[:], func=AF.Ln, bias=k.cst[:, 1:2], scale=1.0), ['p_cltmp', 'cst'], ['p_cltmp'])
        P.op('dve', lambda e: e.tensor_scalar(out=cl[:, 0, :, :], in0=tmp[:], scalar1=-8.0, scalar2=None, op0=ALU.mult), ['p_cltmp'], ['p_cl'])
        P.op('dve', lambda e: e.tensor_scalar(out=cl[:, 1, :, :], in0=tmp[:], scalar1=-16.0, scalar2=None, op0=ALU.mult), ['p_cltmp'], ['p_cl'])
        lp['cl'] = cl
        stt = sb('p_state', [128, 2, 6], F32)
        P.op('dve', lambda e: e.memset(stt[:], 0.0), [], keys('p_state0', 6) + keys('p_state1', 6))
        lp['state'] = stt
        k.lp = lp
        w_in = sb('w_in0', [128, 8, 1792], BF16)
        load_w_bf16(k, 'w_in0', w_in, I['ab_w_in'], 8, 1792, colblk=1792)
        w_out = sb('w_out0', [128, 8, D], BF16)
        k.w_in0, k.w_out0 = w_in, w_out

        for s_, nm, TT, xsrc in ((1, 'c', LC, I['ctx']), (0, 'l', k.T, I['x'])):
            seg = K()
            seg.nm, seg.T, seg.x, seg.set = nm, TT, xsrc, s_
            seg.z, seg.xa, seg.hf, seg.x05, seg.x1 = S['z_' + nm], S['xa_' + nm], S['hf_' + nm], S['x05_' + nm], S['x1_' + nm]
            seg.tiles = tiles_of(TT)
            with ExitStack() as st2:
                fold_w_bf16(k, st2, 'w_out0', w_out, I['ab_w_out'], 8, k.gbc_d[0, s_, 0])
            P.barrier()
            l0_phaseA(k, seg)
            P.barrier()
            if stop_after == 'l0A' and nm == 'l':
                return
            l0_phaseB(k, seg)
            P.barrier()
            if stop_after == 'l0B' and nm == 'l':
                return
            l0_phaseC(k, seg)
            P.barrier()
            if stop_after == 'l0C' and nm == 'l':
                return
    for s_, nm, TT in ((1, 'c', LC), (0, 'l', k.T)):
        ffn_phase(k, 0, s_, S['x05_' + nm], S['x1_' + nm], TT, final=False)
        P.barrier()


def l0_phaseA(k, seg):
    nc, P = k.nc, k.P
    with ExitStack() as st:
        sb = lambda n, s, d: k.sb(n, s, d, st)
        ps = lambda n, s, d: k.ps(n, s, d, st)
        B = mod_bufs(k, st, 'A_')
        hT = sb('A_hT', [128, 8, 512], BF16)
        zt = [sb(f'A_zt{i}', [128, 14, 512], F32) for i in range(2)]
        zp = [ps(f'A_zp{i}', [128, 512], F32) for i in range(4)]
        zv = seg.z.rearrange("(c p) t -> p c t", p=128)
        P.dma('sp', zv[:, :, 0:PADZ], k.zero[:, 0:14 * PADZ].rearrange("p (c t) -> p c t", c=14), ['zero'], [], 'A_zpad')
        P.dma('sp', zv[:, :, PADZ + seg.T:PADZ + seg.T + PADZ], k.zero[:, 0:14 * PADZ].rearrange("p (c t) -> p c t", c=14), ['zero'], [], 'A_zpad')
        modulate_tile(k, B, seg.x[seg.tiles[0][0]:seg.tiles[0][0] + seg.tiles[0][1], :], seg.tiles[0][1], 0, seg.set, 0, hT, 'A_hT', part=1)
        for ti, (s, n) in enumerate(seg.tiles):
            modulate_tile(k, B, seg.x[s:s + n, :], n, 0, seg.set, 0, hT, 'A_hT', part=2)
            Z = zt[ti % 2]
            Zn = f'A_zt{ti % 2}'
            for mc in range(14):
                if mc == 6 and ti + 1 < len(seg.tiles):
                    s2, n2 = seg.tiles[ti + 1]
                    modulate_tile(k, B, seg.x[s2:s2 + n2, :], n2, 0, seg.set, 0, hT, 'A_hT', part=1)
                zpp = zp[mc % 4]
                for kc in range(8):
                    P.op('pe', lambda e, mc=mc, kc=kc, zpp=zpp: e.matmul(zpp[:, 0:n], lhsT=k.w_in0[:, kc, mc * 128:(mc + 1) * 128], rhs=hT[:, kc, 0:n],
                                                                         start=(kc == 0), stop=(kc == 7)),
                         [f'w_in0.{kc}', f'A_hT.{kc}'], [f'A_zp{mc % 4}'])
                eng = 'act' if mc % 2 == 0 else 'dve'
                if eng == 'act':
                    P.op('act', lambda e, mc=mc, zpp=zpp: e.copy(out=Z[:, mc, 0:n], in_=zpp[:, 0:n]), [f'A_zp{mc % 4}'], [f'{Zn}.{mc}'])
                else:
                    P.op('dve', lambda e, mc=mc, zpp=zpp: e.tensor_copy(out=Z[:, mc, 0:n], in_=zpp[:, 0:n]), [f'A_zp{mc % 4}'], [f'{Zn}.{mc}'])
            P.dma('sp', zv[:, :, PADZ + s:PADZ + s + n], Z[:, :, 0:n], keys(Zn, 14), [], Zn)


def lru_coeffs(k, C, d, n, xa, xab):
    P, lp = k.P, k.lp
    for c in range(6):
        pr, pi = C['pg'][(2 * c) % 4], C['pg'][(2 * c + 1) % 4]
        prn, pin = C['pgname'][(2 * c) % 4], C['pgname'][(2 * c + 1) % 4]
        P.op('pe', lambda e, c=c, pr=pr: e.matmul(pr[:, 0:n], lhsT=lp['lru_bdA'][:, d, c, :], rhs=xab[:, c, 0:n], start=True, stop=True),
             ['p_lru_bdA', f"{C['xabname']}.{c}"], [prn])
        P.op('pe', lambda e, c=c, pi=pi: e.matmul(pi[:, 0:n], lhsT=lp['lru_bdX'][:, d, c, :], rhs=xab[:, c, 0:n], start=True, stop=True),
             ['p_lru_bdX', f"{C['xabname']}.{c}"], [pin])
        P.op('act', lambda e, c=c, pr=pr: e.activation(out=C['r'][:, c, 0:n], in_=pr[:, 0:n], func=AF.Sigmoid, bias=lp['lru_ba'][:, d, c:c + 1], scale=1.0),
             [prn, 'p_lru_ba'], [f"{C['pfx']}r.{c}"])
        P.op('act', lambda e, c=c, pi=pi: e.activation(out=C['ig'][:, c, 0:n], in_=pi[:, 0:n], func=AF.Sigmoid, bias=lp['lru_bx'][:, d, c:c + 1], scale=1.0),
             [pin, 'p_lru_bx'], [f"{C['pfx']}ig.{c}"])
    for c in range(6):
        P.op('act', lambda e, c=c: e.activation(out=C['a'][:, c, 0:n], in_=C['r'][:, c, 0:n], func=AF.Exp, scale=lp['cl'][:, 0, d, c:c + 1]),
             [f"{C['pfx']}r.{c}", 'p_cl'], [f"{C['pfx']}a.{c}"])
        P.op('act', lambda e, c=c: e.activation(out=C['r'][:, c, 0:n], in_=C['r'][:, c, 0:n], func=AF.Exp, scale=lp['cl'][:, 1, d, c:c + 1]),
             [f"{C['pfx']}r.{c}", 'p_cl'], [f"{C['pfx']}r.{c}"])
    for c in range(6):
        P.op('act', lambda e, c=c: e.activation(out=C['r'][:, c, 0:n], in_=C['r'][:, c, 0:n], func=AF.Sqrt, bias=k.cst[:, 1:2], scale=-1.0),
             [f"{C['pfx']}r.{c}", 'cst'], [f"{C['pfx']}r.{c}"])
        P.op('dve', lambda e, c=c: e.tensor_tensor(out=C['ig'][:, c, 0:n], in0=C['ig'][:, c, 0:n], in1=C['r'][:, c, 0:n], op=ALU.mult),
             [f"{C['pfx']}r.{c}", f"{C['pfx']}ig.{c}"], [f"{C['pfx']}ig.{c}"])
        P.op('pool', lambda e, c=c: e.tensor_tensor(out=C['ig'][:, c, 0:n], in0=C['ig'][:, c, 0:n], in1=xa[:, c, 0:n], op=ALU.mult),
             [f"{C['pfx']}ig.{c}", f"{C['xaname']}.{c}"], [f"{C['pfx']}ig.{c}"])


def coeff_bufs(k, st, pfx, share=None):
    C = {'pfx': pfx}
    for nm in ('r', 'ig', 'a'):
        C[nm] = k.sb(pfx + nm, [128, 6, 512], F32, st)
    if share is None:
        C['pg'] = [k.ps(pfx + f'pg{i}', [128, 512], F32, st) for i in range(4)]
        C['pgname'] = [pfx + f'pg{i}' for i in range(4)]
    else:
        C['pg'], C['pgname'] = share['pg'], share['pgname']
    return C


def l0_phaseB(k, seg):
    P, lp = k.P, k.lp
    with ExitStack() as st:
        sb = lambda n, s, d: k.sb(n, s, d, st)
        sets = []
        for j in range(2):
            Bf = {}
            Bf['zin'] = sb(f'B{j}_zin', [128, 6, 516], F32)
            Bf['xa'] = sb(f'B{j}_xa', [128, 6, 512], F32)
            Bf['xab'] = sb(f'B{j}_xab', [128, 6, 512], BF16)
            Bf['hf'] = sb('B_hf', [128, 6, 512], F32) if j == 0 else sets[0]['hf']
            C = coeff_bufs(k, st, f'B{j}_', share=(sets[0]['C'] if j == 1 else None))
            C['xabname'], C['xaname'] = f'B{j}_xab', f'B{j}_xa'
            Bf['C'] = C
            sets.append(Bf)
        zv = seg.z[0:768, :].rearrange("(c p) t -> p c t", p=128)
        xav = seg.xa.rearrange("(c p) t -> p c t", p=128)
        hfv = seg.hf.rearrange("(c p) t -> p c t", p=128)
        if seg.nm == 'c':
            P.op('dve', lambda e: e.memset(lp['state'][:], 0.0), [], keys('p_state0', 6) + keys('p_state1', 6))
        for ti, (s, n) in enumerate(seg.tiles):
            j = ti % 2
            Bf = sets[j]
            zin, xa, xab, hf, C = Bf['zin'], Bf['xa'], Bf['xab'], Bf['hf'], Bf['C']
            pf = f'B{j}_'
            P.dma('sp', zin[:, :, 0:n + 4], zv[:, :, PADZ + s - 2:PADZ + s + n + 2], [], keys(pf + 'zin', 6), pf + 'zin')
            for c in range(6):
                P.op('act', lambda e, c=c, xa=xa, zin=zin: e.activation(out=xa[:, c, 0:n], in_=zin[:, c, 0:n], func=AF.Identity,
                                                                        scale=lp['lru_cw'][:, c, 0:1], bias=lp['lru_cb'][:, c:c + 1]),
                     [f'{pf}zin.{c}', 'p_lru_cw', 'p_lru_cb'], [f'{pf}xa.{c}'])
                for t in range(1, 5):
                    P.op('dve', lambda e, c=c, t=t, xa=xa, zin=zin: e.scalar_tensor_tensor(out=xa[:, c, 0:n], in0=zin[:, c, t:t + n], scalar=lp['lru_cw'][:, c, t:t + 1],
                                                                                           in1=xa[:, c, 0:n], op0=ALU.mult, op1=ALU.add),
                         [f'{pf}zin.{c}', f'{pf}xa.{c}'], [f'{pf}xa.{c}'])
                P.op('act', lambda e, c=c, xa=xa, xab=xab: e.copy(out=xab[:, c, 0:n], in_=xa[:, c, 0:n]), [f'{pf}xa.{c}'], [f'{pf}xab.{c}'])
            P.dma('sp', xav[:, :, s:s + n], xa[:, :, 0:n], keys(pf + 'xa', 6), [], pf + 'xa')
            lru_coeffs(k, C, 0, n, xa, xab)
            for c in range(6):
                P.op('dve', lambda e, c=c, hf=hf, C=C: e.tensor_tensor_scan(out=hf[:, c, 0:n], data0=C['a'][:, c, 0:n], data1=C['ig'][:, c, 0:n],
                                                                          initial=lp['state'][:, 0, c:c + 1], op0=ALU.mult, op1=ALU.add),
                     [f'{pf}a.{c}', f'{pf}ig.{c}', f'p_state0.{c}'], [f'B_hf.{c}'])
                P.op('dve', lambda e, c=c, hf=hf: e.tensor_copy(out=lp['state'][:, 0, c:c + 1], in_=hf[:, c, n - 1:n]), [f'B_hf.{c}'], [f'p_state0.{c}'])
            P.dma('sp', hfv[:, :, s:s + n], hf[:, :, 0:n], keys('B_hf', 6), [], 'B_hf')


def l0_phaseC(k, seg):
    P, lp, I = k.P, k.lp, k.I
    with ExitStack() as st:
        sb = lambda n, s, d: k.sb(n, s, d, st)
        ps = lambda n, s, d: k.ps(n, s, d, st)
        xa = sb('C_xa', [128, 6, 512], F32)
        xab = sb('C_xab', [128, 6, 512], BF16)
        hb = sb('C_hb', [128, 6, 512], F32)
        hf = sb('C_hf', [128, 6, 512], F32)
        ga = sb('C_ga', [128, 6, 512], F32)
        yT = sb('C_yT', [128, 8, 512], BF16)
        zb = sb('C_zb', [128, 2, 528], F32)
        p2 = sb('C_p2', [128, 528], F32)
        p4 = sb('C_p4', [128, 528], F32)
        p8 = sb('C_p8', [128, 528], F32)
        Qw = sb('C_Qw', [128, 516], F32)
        Ssum = sb('C_S', [128, 512], F32)
        dd = sb('C_dd', [128, 2, 512], BF16)
        xt = sb('C_xt', [128, 4, D], F32)
        xo = sb('C_xo', [128, 4, D], F32)
        C = coeff_bufs(k, st, 'C_')
        C['xabname'], C['xaname'] = 'C_xab', 'C_xa'
        po = [ps(f'C_po{i}', [128, 512], F32) for i in range(4)]
        zg = seg.z[768:1536, :].rearrange("(c p) t -> p c t", p=128)
        zbv = seg.z[1536:1792, :].rearrange("(c p) t -> p c t", p=128)
        xav = seg.xa.rearrange("(c p) t -> p c t", p=128)
        hfv = seg.hf.rearrange("(c p) t -> p c t", p=128)
        if seg.nm == 'c':
            P.op('dve', lambda e: e.memset(lp['state'][:, 1, :], 0.0), [], keys('p_state1', 6))
        nt_ = len(seg.tiles)
        for ti in range(nt_ - 1, -1, -1):
            s, n = seg.tiles[ti]
            ng = n // 128
            P.dma('sp', xa[:, :, 0:n], xav[:, :, s:s + n], [], keys('C_xa', 6), 'C_xa')
            P.dma('sp', hf[:, :, 0:n], hfv[:, :, s:s + n], [], keys('C_hf', 6), 'C_hf')
            P.dma('sp', ga[:, :, 0:n], zg[:, :, PADZ + s:PADZ + s + n], [], keys('C_ga', 6), 'C_ga')
            P.dma('sp', zb[:, :, 0:n + 16], zbv[:, :, PADZ + s - 8:PADZ + s + n + 8], [], keys('C_zb', 2), 'C_zb')
            P.dma('sp', xt[:, 0:ng, :], seg.x[s:s + n, :].rearrange("(g p) f -> p g f", p=128), [], ['C_xt'], 'C_xt')
            W = n + 16
            n1 = n + 1
            for ch in range(2):
                zc = zb[:, ch, :]
                P.op('dve', lambda e, zc=zc: e.tensor_tensor(out=p2[:, 0:W - 1], in0=zc[:, 0:W - 1], in1=zc[:, 1:W], op=ALU.add),
                     [f'C_zb.{ch}'], ['C_p2'])
                if ch == 0:
                    P.op('dve', lambda e: e.tensor_copy(out=Qw[0:64, 0:n1], in_=p2[0:64, 7:7 + n1]), ['C_p2'], ['C_Qw'])
                    P.op('dve', lambda e: e.tensor_tensor(out=Qw[64:128, 0:n1], in0=p2[64:128, 6:6 + n1], in1=p2[64:128, 8:8 + n1], op=ALU.add),
                         ['C_p2'], ['C_Qw'])
                else:
                    P.op('dve', lambda e: e.tensor_tensor(out=p4[:, 0:W - 3], in0=p2[:, 0:W - 3], in1=p2[:, 2:W - 1], op=ALU.add), ['C_p2'], ['C_p4'])
                    P.op('dve', lambda e: e.tensor_tensor(out=Qw[0:64, 0:n1], in0=p4[0:64, 4:4 + n1], in1=p4[0:64, 8:8 + n1], op=ALU.add),
                         ['C_p4'], ['C_Qw'])
                    P.op('dve', lambda e: e.tensor_tensor(out=p8[64:128, 0:W - 7], in0=p4[64:128, 0:W - 7], in1=p4[64:128, 4:W - 3], op=ALU.add),
                         ['C_p4'], ['C_p8'])
                    P.op('dve', lambda e: e.tensor_tensor(out=Qw[64:128, 0:n1], in0=p8[64:128, 0:n1], in1=p8[64:128, 8:8 + n1], op=ALU.add),
                         ['C_p8'], ['C_Qw'])
                P.op('dve', lambda e: e.tensor_scalar(out=Ssum[:, 0:n], in0=Qw[:, 0:n], scalar1=lp['pool_fl'][:, 0:1], scalar2=None, op0=ALU.mult),
                     ['C_Qw', 'p_pool_fl'], ['C_S'])
                P.op('dve', lambda e: e.scalar_tensor_tensor(out=Ssum[:, 0:n], in0=Qw[:, 1:n1], scalar=lp['pool_fl'][:, 1:2], in1=Ssum[:, 0:n],
                                                             op0=ALU.mult, op1=ALU.add), ['C_Qw', 'C_S', 'p_pool_fl'], ['C_S'])
                if ti == 0:
                    P.op('dve', lambda e, ch=ch: e.tensor_tensor(out=Ssum[:, 0:16], in0=Ssum[:, 0:16], in1=lp['pool_corr'][:, ch, 0, :], op=ALU.mult),
                         ['C_S', 'p_pool_corr'], ['C_S'])
                if ti == nt_ - 1:
                    P.op('dve', lambda e, ch=ch: e.tensor_tensor(out=Ssum[:, n - 16:n], in0=Ssum[:, n - 16:n], in1=lp['pool_corr'][:, ch, 1, :], op=ALU.mult),
                         ['C_S', 'p_pool_corr'], ['C_S'])
                P.op('dve', lambda e, ch=ch, zc=zc: e.scalar_tensor_tensor(out=dd[:, ch, 0:n], in0=Ssum[:, 0:n], scalar=lp['pool_invw'][:, ch:ch + 1],
                                                                          in1=zc[:, 8:8 + n], op0=ALU.mult, op1=ALU.subtract),
                     ['C_S', f'C_zb.{ch}', 'p_pool_invw'], [f'C_dd.{ch}'])
                pp = po[ch]
                P.op('pe', lambda e, ch=ch, pp=pp: e.matmul(pp[:, 0:n], lhsT=lp['pool_bd'][:, ch, :], rhs=dd[:, ch, 0:n], start=True, stop=True),
                     ['p_pool_bd', f'C_dd.{ch}'], [f'C_po{ch}'])
                P.op('act', lambda e, ch=ch, pp=pp: e.activation(out=yT[:, 6 + ch, 0:n], in_=pp[:, 0:n], func=AF.Identity, scale=lp['pool_scale'][:, ch:ch + 1], bias=k.cst[:, 2:3]),
                     [f'C_po{ch}', 'p_pool_scale', 'cst'], [f'C_yT.{6 + ch}'])
            for c in range(6):
                P.op('act', lambda e, c=c: e.copy(out=xab[:, c, 0:n], in_=xa[:, c, 0:n]), [f'C_xa.{c}'], [f'C_xab.{c}'])
            lru_coeffs(k, C, 1, n, xa, xab)
            for c in range(6):
                P.op('dve', lambda e, c=c: e.tensor_tensor_scan(out=rev(hb[:, c, 0:n]), data0=rev(C['a'][:, c, 0:n]), data1=rev(C['ig'][:, c, 0:n]),
                                                                initial=lp['state'][:, 1, c:c + 1], op0=ALU.mult, op1=ALU.add),
                     [f'C_a.{c}', f'C_ig.{c}', f'p_state1.{c}'], [f'C_hb.{c}'])
                P.op('dve', lambda e, c=c: e.tensor_copy(out=lp['state'][:, 1, c:c + 1], in_=hb[:, c, 0:1]), [f'C_hb.{c}'], [f'p_state1.{c}'])
                P.op('act', lambda e, c=c: e.activation(out=ga[:, c, 0:n], in_=ga[:, c, 0:n], func=AF.Gelu_apprx_tanh), [f'C_ga.{c}'], [f'C_ga.{c}'])
                P.op('pool', lambda e, c=c: e.tensor_tensor(out=hb[:, c, 0:n], in0=hb[:, c, 0:n], in1=hf[:, c, 0:n], op=ALU.add),
                     [f'C_hb.{c}', f'C_hf.{c}'], [f'C_hb.{c}'])
                P.op('dve', lambda e, c=c: e.tensor_tensor(out=yT[:, c, 0:n], in0=hb[:, c, 0:n], in1=ga[:, c, 0:n], op=ALU.mult),
                     [f'C_hb.{c}', f'C_ga.{c}'], [f'C_yT.{c}'])
            for g in range(ng):
                for nt in range(2):
                    pp = po[(2 * g + nt) % 4]
                    ppn = f'C_po{(2 * g + nt) % 4}'
                    for kc in range(8):
                        P.op('pe', lambda e, g=g, nt=nt, kc=kc, pp=pp: e.matmul(pp[:, :], lhsT=yT[:, kc, g * 128:(g + 1) * 128], rhs=k.w_out0[:, kc, nt * 512:(nt + 1) * 512],
                                                                               start=(kc == 0), stop=(kc == 7)),
                             [f'C_yT.{kc}', f'w_out0.{kc}'], [ppn])
                    P.op('dve', lambda e, g=g, nt=nt, pp=pp: e.tensor_tensor(out=xo[:, g, nt * 512:(nt + 1) * 512], in0=pp[:, :], in1=xt[:, g, nt * 512:(nt + 1) * 512], op=ALU.add),
                         [ppn, 'C_xt'], [f'C_xo.{g}'])
            P.dma('sp', seg.x05[1 + s:1 + s + n, :].rearrange("(g p) f -> p g f", p=128), xo[:, 0:ng, :], keys('C_xo', 4)[0:ng], [], 'C_xo')


def ffn_phase(k, l, s_, src, dst, TT, final, flush=True, out_rows=None):
    nc, P, I = k.nc, k.P, k.I
    tiles = tiles_of(TT)
    with ExitStack() as st:
        sb = lambda n, s, d: k.sb(n, s, d, st)
        ps = lambda n, s, d: k.ps(n, s, d, st)
        w_up = sb('F_wup', [128, 8, 2 * DFF], BF16)
        load_w_bf16(k, 'F_wup', w_up, I['ffn_w_up'][l], 8, 2 * DFF)
        w_dn = sb('F_wdn', [128, NFC, D], BF16)
        with ExitStack() as st2:
            fold_w_bf16(k, st2, 'F_wdn', w_dn, I['ffn_w_down'][l], NFC, k.gbc_d[l, s_, 1])
            P.barrier()
        cw = sb('F_cw', [128, 2 * NFC, 3], F32)
        cb = sb('F_cb', [128, 2 * NFC], F32)
        P.dma('sp', cw[:], I['ffn_cw'][:, l, :, :], [], ['F_cw'], 'F_cw')
        P.dma('sp', cb[:], I['ffn_cb'][:, l, :], [], ['F_cb'], 'F_cb')
        B = mod_bufs(k, st, 'F_')
        hT = sb('F_hT', [128, 8, 512], BF16)
        gT = sb('F_gT', [128, NFC, 512], BF16)
        prevu = [sb(f'F_prevu{i}', [128, 2 * NFC, 2], F32) for i in range(2)]
        P.op('dve', lambda e: e.memset(prevu[0][:], 0.0), [], keys('F_prevu0', 2 * NFC))
        acc = [sb(f'F_acc{i}', [128, 512], F32) for i in range(4)]
        corr = sb('F_corr', [128, 2 * NFC, 2], F32)
        ctmp = sb('F_ctmp', [128, 2 * NFC], F32)
        xs = sb('F_xs', [128, 2, D], F32)
        xo = xs
        pu = [ps(f'F_pu{i}', [128, 512], F32) for i in range(4)]
        pd = [ps(f'F_pd{i}', [128, 512], F32) for i in range(2)]
        if final:
            fg = sb('F_fg', [128, D], F32)
            P.dma('sp', fg[:], I['final_g_bc'], [], ['F_fg'], 'F_fg')
            fss = sb('F_fss', [128, 12], F32)
            fjunk = B['junk']

        if os.environ.get('DBG_SBUF'):
            print('FFN sbuf remaining', nc.sbuf_bytes_remaining, 'final', final)
        def conv_gate(n, zero_u, ti):
            pin, pout = prevu[ti % 2], prevu[(ti + 1) % 2]
            pinn, poutn = f'F_prevu{ti % 2}', f'F_prevu{(ti + 1) % 2}'
            allin = keys(pinn, 2 * NFC)
            P.op('dve', lambda e: e.tensor_tensor(out=corr[:, :, 0], in0=cw[:, :, 0], in1=pin[:, :, 0], op=ALU.mult), allin + ['F_cw'], ['F_corr'])
            P.op('dve', lambda e: e.tensor_tensor(out=ctmp[:, :], in0=cw[:, :, 1], in1=pin[:, :, 1], op=ALU.mult), allin + ['F_cw'], ['F_ctmp'])
            P.op('dve', lambda e: e.tensor_tensor(out=corr[:, :, 0], in0=corr[:, :, 0], in1=ctmp[:, :], op=ALU.add), ['F_corr', 'F_ctmp'], ['F_corr'])
            P.op('dve', lambda e: e.tensor_tensor(out=corr[:, :, 1], in0=cw[:, :, 0], in1=pin[:, :, 1], op=ALU.mult), allin + ['F_cw', 'F_corr'], ['F_corr'])
            for c in range(NFC):
                q = c % 2
                AA = [acc[2 * q], acc[2 * q + 1]]
                AN = [f'F_acc{2 * q}', f'F_acc{2 * q + 1}']
                PP = [pu[2 * q], pu[2 * q + 1]]
                PN = [f'F_pu{2 * q}', f'F_pu{2 * q + 1}']
                CC = [c, NFC + c]
                if not zero_u:
                    for vi in range(2):
                        for kc in range(8):
                            P.op('pe', lambda e, kc=kc, cc=CC[vi], pp=PP[vi]: e.matmul(pp[:, 0:n], lhsT=w_up[:, kc, cc * 128:(cc + 1) * 128], rhs=hT[:, kc, 0:n],
                                                                                      start=(kc == 0), stop=(kc == 7)),
                                 [f'F_wup.{kc}', f'F_hT.{kc}'], [PN[vi]])
                    for vi in range(2):
                        P.op('act', lambda e, A_=AA[vi], pp=PP[vi], cc=CC[vi]: e.activation(out=A_[:, 0:n], in_=pp[:, 0:n], func=AF.Identity,
                                                                                          scale=cw[:, cc, 2:3], bias=cb[:, cc:cc + 1]),
                             [PN[vi], 'F_cw', 'F_cb'], [AN[vi]])
                    for vi in range(2):
                        if os.environ.get('DBG_SKIP_SAVE'):
                            continue
                        P.op('dve', lambda e, pp=PP[vi], cc=CC[vi]: e.tensor_copy(out=pout[:, cc, :], in_=pp[:, n - 2:n]), [PN[vi]], [f'{poutn}.{CC[vi]}'])
                    for vi in range(2):
                        P.op('dve', lambda e, A_=AA[vi], pp=PP[vi], cc=CC[vi]: e.scalar_tensor_tensor(out=A_[:, 1:n], in0=pp[:, 0:n - 1], scalar=cw[:, cc, 1:2], in1=A_[:, 1:n],
                                                                                                    op0=ALU.mult, op1=ALU.add), [PN[vi], AN[vi]], [AN[vi]])
                    for vi in range(2):
                        P.op('dve', lambda e, A_=AA[vi], pp=PP[vi], cc=CC[vi]: e.scalar_tensor_tensor(out=A_[:, 2:n], in0=pp[:, 0:n - 2], scalar=cw[:, cc, 0:1], in1=A_[:, 2:n],
                                                                                                    op0=ALU.mult, op1=ALU.add), [PN[vi], AN[vi]], [AN[vi]])
                else:
                    for vi in range(2):
                        P.op('act', lambda e, A_=AA[vi], cc=CC[vi]: e.activation(out=A_[:, 0:n], in_=k.zero[:, 0:n], func=AF.Identity,
                                                                                scale=cw[:, cc, 2:3], bias=cb[:, cc:cc + 1]),
                             ['zero', 'F_cw', 'F_cb'], [AN[vi]])
                for vi in range(2):
                    P.op('pool', lambda e, A_=AA[vi], cc=CC[vi]: e.tensor_tensor(out=A_[:, 0:2], in0=A_[:, 0:2], in1=corr[:, cc, :], op=ALU.add),
                         ['F_corr', AN[vi]], [AN[vi]])
                P.op('act', lambda e, A_=AA[1]: e.activation(out=A_[:, 0:n], in_=A_[:, 0:n], func=AF.Silu), [AN[1]], [AN[1]])
                P.op('pool', lambda e, c=c, A0=AA[0], A1=AA[1]: e.tensor_tensor(out=gT[:, c, 0:n], in0=A0[:, 0:n], in1=A1[:, 0:n], op=ALU.mult),
                     [AN[0], AN[1]], [f'F_gT.{c}'])

        def down_res(tok0, n, nrows_last=128):
            ng = n // 128
            nr = lambda g: (nrows_last if g == ng - 1 else 128)
            for g_ in range(ng):
                g = g_ % 2
                r0 = 1 + tok0 + g_ * 128
                P.dma('sp', xs[0:nr(g_), g, :], src[r0:r0 + nr(g_), :], [], [f'F_xs.{g}'], f'F_xs{g}')
                for nt in range(2):
                    pp = pd[nt]
                    for kc in range(NFC):
                        P.op('pe', lambda e, g_=g_, nt=nt, kc=kc, pp=pp: e.matmul(pp[:, :], lhsT=gT[:, kc, g_ * 128:(g_ + 1) * 128], rhs=w_dn[:, kc, nt * 512:(nt + 1) * 512],
                                                                               start=(kc == 0), stop=(kc == NFC - 1)),
                             [f'F_gT.{kc}', f'F_wdn.{kc}'], [f'F_pd{nt}'])
                    P.op('dve', lambda e, g=g, g_=g_, nt=nt, pp=pp: e.tensor_tensor(out=xo[0:nr(g_), g, nt * 512:(nt + 1) * 512], in0=pp[0:nr(g_), :],
                                                                            in1=xs[0:nr(g_), g, nt * 512:(nt + 1) * 512], op=ALU.add),
                         [f'F_pd{nt}', f'F_xs.{g}'], [f'F_xs.{g}'])
                if final:
                    P.op('dve', lambda e: e.memset(fss[:, 0:1], 0.0), [], ['F_fss'])
                    P.op('act', lambda e, g=g: e.activation(out=fjunk[:], in_=xo[:, g, :], func=AF.Square, accum_out=fss[:, 0:1]),
                         [f'F_xs.{g}'], ['F_fss', 'F_junk'])
                    P.op('act', lambda e: e.activation(out=fss[:, 1:2], in_=fss[:, 0:1], func=AF.Sqrt, bias=k.cst[:, 0:1], scale=1.0 / D),
                         ['F_fss', 'cst'], ['F_fss'])
                    P.op('dve', lambda e: e.reciprocal(out=fss[:, 2:3], in_=fss[:, 1:2]), ['F_fss'], ['F_fss'])
                    P.op('dve', lambda e, g=g: e.scalar_tensor_tensor(out=xo[:, g, :], in0=xo[:, g, :], scalar=fss[:, 2:3], in1=fg[:],
                                                                     op0=ALU.mult, op1=ALU.mult), [f'F_xs.{g}', 'F_fss', 'F_fg'], [f'F_xs.{g}'])
                t0 = tok0 + g_ * 128
                if final:
                    lo = max(t0, 0)
                    hi = min(t0 + nr(g_), TT if out_rows is None else out_rows)
                    if hi > lo:
                        P.dma('sp', k.out[lo:hi, :], xo[lo - t0:hi - t0, g, :], [f'F_xs.{g}'], [], f'F_xs{g}')
                else:
                    P.dma('sp', dst[1 + t0:1 + t0 + nr(g_), :], xo[0:nr(g_), g, :], [f'F_xs.{g}'], [], f'F_xs{g}')

        modulate_tile(k, B, src[1 + tiles[0][0]:1 + tiles[0][0] + tiles[0][1], :], tiles[0][1], l, s_, 2, hT, 'F_hT', part=1)
        for ti, (s, n) in enumerate(tiles):
            modulate_tile(k, B, src[1 + s:1 + s + n, :], n, l, s_, 2, hT, 'F_hT', part=2)
            conv_gate(n, False, ti)
            if ti + 1 < len(tiles):
                s2, n2 = tiles[ti + 1]
                modulate_tile(k, B, src[1 + s2:1 + s2 + n2, :], n2, l, s_, 2, hT, 'F_hT', part=1)
            down_res(s - 1, n)
        if flush:
            conv_gate(128, True, len(tiles))
            down_res(TT - 1, 128, nrows_last=1)


def layer1(k, stop_after):
    nc, P, I, S = k.nc, k.P, k.I, k.S
    T = k.T
    TK = T + LC
    NKB = TK // 128
    TL = k.TL
    yc_d = nc.dram_tensor('yc_d', [768, T], BF16, kind="Internal").ap()
    with ExitStack() as st:
      if not os.environ.get('DBG_ONLY_F1'):
          sb = lambda n, s, d: k.sb(n, s, d, st)
          ps = lambda n, s, d: k.ps(n, s, d, st)
          w_in = sb('w_in1', [128, 8, 2816], BF16)
          load_w_bf16(k, 'w_in1', w_in, I['cd_w_in'], 8, 2816, colblk=1408)
          w_sw = sb('w_sw1', [128, 8, 1536], BF16)
          load_w_bf16(k, 'w_sw1', w_sw, I['cd_w_sw'], 8, 1536, colblk=1536)
          B = mod_bufs(k, st, 'E_')
          hT = sb('E_hT', [128, 8, 512], BF16)
          qk = sb('E_qk', [128, 12, 512], BF16)
          vt = sb('E_vt', [128, 4, 768], BF16)
          gl = sb('E_gl', [128, 2, 512], F32)
          rc = sb('E_rc', [128, 512], F32)
          rs = sb('E_rs', [128, 512], F32)
          t1 = sb('E_t1', [128, 512], F32)
          t2 = sb('E_t2', [128, 512], F32)
          sg = sb('E_sg', [128, 512], F32)
          pa = [ps(f'E_pa{i}', [128, 512], F32) for i in range(2)]
          pb = [ps(f'E_pb{i}', [128, 512], F32) for i in range(2)]
          glv = S['gl'].rearrange("(c p) t -> p c t", p=128)
          P.dma('sp', glv[:, :, 0:PADZ], k.zero[:, 0:2 * PADZ].rearrange("p (c t) -> p c t", c=2), ['zero'], [], 'E_glpad')
          P.dma('sp', glv[:, :, PADZ + T:PADZ + T + PADZ], k.zero[:, 0:2 * PADZ].rearrange("p (c t) -> p c t", c=2), ['zero'], [], 'E_glpad')
          qTv = S['qT'].rearrange("(c p) t -> p c t", p=128)
          kTv = S['kT'].rearrange("(c p) t -> p c t", p=128)

          def proj_plain(cols0, nch, dst, dstname, d0, n):
              for c in range(nch):
                  pp = pa[c % 2]
                  for kc in range(8):
                      P.op('pe', lambda e, c=c, kc=kc, pp=pp: e.matmul(pp[:, 0:n], lhsT=w_in[:, kc, cols0 + c * 128:cols0 + (c + 1) * 128], rhs=hT[:, kc, 0:n],
                                                                      start=(kc == 0), stop=(kc == 7)), [f'w_in1.{kc}', f'E_hT.{kc}'], [f'E_pa{c % 2}'])
                  P.op('act', lambda e, c=c, pp=pp: e.copy(out=dst[:, d0 + c, 0:n], in_=pp[:, 0:n]), [f'E_pa{c % 2}'], [f'{dstname}.{d0 + c}'])

          def proj_v(n):
              ng = n // 128
              for g in range(ng):
                  for (c0, cn, pp, ppn) in ((0, 512, pa[g % 2], f'E_pa{g % 2}'), (512, 256, pb[g % 2], f'E_pb{g % 2}')):
                      for kc in range(8):
                          P.op('pe', lambda e, g=g, kc=kc, pp=pp, c0=c0, cn=cn: e.matmul(pp[:, 0:cn], lhsT=hT[:, kc, g * 128:(g + 1) * 128],
                                                                                         rhs=w_in[:, kc, 1536 + c0:1536 + c0 + cn], start=(kc == 0), stop=(kc == 7)),
                               [f'w_in1.{kc}', f'E_hT.{kc}'], [ppn])
                      P.op('dve', lambda e, g=g, pp=pp, c0=c0, cn=cn: e.tensor_copy(out=vt[:, g, c0:c0 + cn], in_=pp[:, 0:cn]), [ppn], [f'E_vt.{g}'])

          n = LC
          modulate_tile(k, B, S['x1_c'][1:1 + LC, :], n, 1, 1, 0, hT, 'E_hT')
          proj_plain(768, 6, qk, 'E_qk', 6, n)
          P.dma('sp', kTv[:, :, T:T + n], qk[:, 6:12, 0:n], keys('E_qk', 12)[6:12], [], 'E_qk')
          proj_v(n)
          P.dma('sp', S['v'][T:T + n, :].rearrange("(g p) f -> p g f", p=128), vt[:, 0:n // 128, :], keys('E_vt', 4), [], 'E_vt')
          for ti, (s, n) in enumerate(tiles_of(T)):
              modulate_tile(k, B, S['x1_l'][1 + s:1 + s + n, :], n, 1, 0, 0, hT, 'E_hT')
              P.dma('sp', rc[:, 0:n], I['rope_c'][:, s:s + n], [], ['E_rc'], 'E_rc')
              P.dma('sp', rs[:, 0:n], I['rope_s'][:, s:s + n], [], ['E_rs'], 'E_rs')
              need_q = s < TL + 16
              for c in (range(12) if need_q else range(6, 12)):
                  pp, pq = pa[c % 2], pb[c % 2]
                  for kc in range(8):
                      P.op('pe', lambda e, c=c, kc=kc, pp=pp: e.matmul(pp[:, 0:n], lhsT=w_in[:, kc, c * 128:(c + 1) * 128], rhs=hT[:, kc, 0:n],
                                                                      start=(kc == 0), stop=(kc == 7)), [f'w_in1.{kc}', f'E_hT.{kc}'], [f'E_pa{c % 2}'])
                  for kc in range(8):
                      P.op('pe', lambda e, c=c, kc=kc, pq=pq: e.matmul(pq[:, 0:n], lhsT=w_sw[:, kc, c * 128:(c + 1) * 128], rhs=hT[:, kc, 0:n],
                                                                      start=(kc == 0), stop=(kc == 7)), [f'w_sw1.{kc}', f'E_hT.{kc}'], [f'E_pb{c % 2}'])
                  P.op('dve', lambda e, pp=pp: e.tensor_tensor(out=t1[:, 0:n], in0=pp[:, 0:n], in1=rc[:, 0:n], op=ALU.mult), [f'E_pa{c % 2}', 'E_rc'], ['E_t1'])
                  P.op('dve', lambda e, pq=pq: e.tensor_tensor(out=t2[:, 0:n], in0=pq[:, 0:n], in1=rs[:, 0:n], op=ALU.mult), [f'E_pb{c % 2}', 'E_rs'], ['E_t2'])
                  P.op('pool', lambda e, c=c: e.tensor_tensor(out=qk[:, c, 0:n], in0=t1[:, 0:n], in1=t2[:, 0:n], op=ALU.add), ['E_t1', 'E_t2'], [f'E_qk.{c}'])
              if need_q:
                  P.dma('sp', qTv[:, :, s:s + n], qk[:, 0:6, 0:n], keys('E_qk', 12)[0:6], [], 'E_q')
              P.dma('sp', kTv[:, :, s:s + n], qk[:, 6:12, 0:n], keys('E_qk', 12)[6:12], [], 'E_qk')
              proj_v(n)
              P.dma('sp', S['v'][s:s + n, :].rearrange("(g p) f -> p g f", p=128), vt[:, 0:n // 128, :], keys('E_vt', 4), [], 'E_vt')
              for c in (range(2) if need_q else []):
                  pp, pq = pa[c % 2], pb[c % 2]
                  for (pz, pzn, cols) in ((pp, f'E_pa{c % 2}', 2304 + c * 128), (pq, f'E_pb{c % 2}', 2304 + 256 + c * 128)):
                      for kc in range(8):
                          P.op('pe', lambda e, kc=kc, pz=pz, cols=cols: e.matmul(pz[:, 0:n], lhsT=w_in[:, kc, cols:cols + 128], rhs=hT[:, kc, 0:n],
                                                                                start=(kc == 0), stop=(kc == 7)), [f'w_in1.{kc}', f'E_hT.{kc}'], [pzn])
                  P.op('act', lambda e, pq=pq: e.activation(out=sg[:, 0:n], in_=pq[:, 0:n], func=AF.Sigmoid), [f'E_pb{c % 2}'], ['E_sg'])
                  P.op('dve', lambda e, c=c, pp=pp: e.tensor_tensor(out=gl[:, c, 0:n], in0=pp[:, 0:n], in1=sg[:, 0:n], op=ALU.mult), [f'E_pa{c % 2}', 'E_sg'], [f'E_gl.{c}'])
              if need_q:
                  P.dma('sp', glv[:, :, PADZ + s:PADZ + s + n], gl[:, :, 0:n], keys('E_gl', 2), [], 'E_gl')
    P.barrier()
    if stop_after == 'l1E':
        return

    with ExitStack() as st:
        sb = lambda n, s, d: k.sb(n, s, d, st)
        ps = lambda n, s, d: k.ps(n, s, d, st)
        dl = sb('G_dl', [1, 4, 64], F32)
        P.dma('sp', dl[:], I['diff_l'], [], ['G_dl'], 'G_dl')
        sm = sb('G_sm', [1, 8], F32)
        pr_ = sb('G_pr', [1, 2, 64], F32)
        P.op('dve', lambda e: e.tensor_tensor(out=pr_[:, 0, :], in0=dl[:, 0, :], in1=dl[:, 1, :], op=ALU.mult), ['G_dl'], ['G_pr'])
        P.op('dve', lambda e: e.tensor_tensor(out=pr_[:, 1, :], in0=dl[:, 2, :], in1=dl[:, 3, :], op=ALU.mult), ['G_dl'], ['G_pr'])
        P.op('dve', lambda e: e.reduce_sum(out=sm[:, 0:1], in_=pr_[:, 0, :], axis=AX.X), ['G_pr'], ['G_sm'])
        P.op('dve', lambda e: e.reduce_sum(out=sm[:, 1:2], in_=pr_[:, 1, :], axis=AX.X), ['G_pr'], ['G_sm'])
        P.op('act', lambda e: e.activation(out=sm[:, 2:4], in_=sm[:, 0:2], func=AF.Exp), ['G_sm'], ['G_sm'])
        P.op('dve', lambda e: e.tensor_tensor(out=sm[:, 4:5], in0=sm[:, 3:4], in1=sm[:, 2:3], op=ALU.subtract), ['G_sm'], ['G_sm'])
        P.op('dve', lambda e: e.tensor_scalar(out=sm[:, 5:6], in0=sm[:, 4:5], scalar1=-LAMBDA_INIT1, scalar2=None, op0=ALU.add), ['G_sm'], ['G_sm'])
        neglam = sb('G_neglam', [128, 2], F32)
        gsub = sb('G_gsub', [128, 2], F32)
        P.dma('sp', gsub[:, 0:1], I['subln_g'], [], ['G_gsub'], 'G_gsub')
        P.op('dve', lambda e: e.tensor_scalar(out=gsub[:, 1:2], in0=gsub[:, 0:1], scalar1=(1.0 - LAMBDA_INIT1), scalar2=None, op0=ALU.mult), ['G_gsub'], ['G_gsub'])
        pS = [ps(f'G_pS{i}', [128, 2, 512], F32) for i in range(2)]
        po = [ps(f'G_po{i}', [128, 512], F32) for i in range(2)]
        pl = ps('G_pl', [128, 2, 512], F32)
        P.op('pe', lambda e: e.matmul(pl[:, 0, 0:1], lhsT=k.ones_f[0:1, :], rhs=sm[0:1, 5:6], start=True, stop=True), ['ones_f', 'G_sm'], ['G_pl0', 'G_pl1'])
        P.op('dve', lambda e: e.tensor_copy(out=neglam[:, 0:1], in_=pl[:, 0, 0:1]), ['G_pl0', 'G_pl1'], ['G_neglam'])
        kh = [sb(f'G_kh{j}', [128, TK], BF16) for j in range(2)]
        vh = [sb(f'G_vh{j}', [128, NKB, 128], BF16) for j in range(2)]
        qh = [sb(f'G_qh{j}', [128, 512], BF16) for j in range(2)]
        pT = [sb(f'G_pT{i}', [128, 2, 512], BF16) for i in range(4)]
        accs = [sb(f'G_acc{i}', [128, 2, 512], F32) for i in range(2)]
        rl = sb('G_rl', [128, 2, 512], F32)
        o1 = sb('G_o1', [128, 512], F32)
        o2 = sb('G_o2', [128, 512], F32)
        sq = sb('G_sq', [128, 512], F32)
        ych = sb('G_ych', [128, 512], BF16)
        vv = S['v'].rearrange("(kb p) f -> p kb f", p=128)
        qtiles = tiles_of(TL)

        def load_head(h):
            j = h % 2
            P.dma('sp', kh[j][:, :], S['kT'][h * 128:h * 128 + 128, :], [], [f'G_kh{j}'], f'G_kh{j}')
            for b0 in range(0, NKB, 16):
                b1 = min(NKB, b0 + 16)
                P.dma('sp', vh[j][:, b0:b1, :], vv[:, b0:b1, h * 128:(h + 1) * 128], [], [f'G_vh{j}'], f'G_vh{j}')

        def load_q(h, ti):
            s, n = qtiles[ti]
            gi = (h * len(qtiles) + ti) % 2
            P.dma('sp', qh[gi][:, 0:n], S['qT'][h * 128:(h + 1) * 128, s:s + n], [], [f'G_qh{gi}'], f'G_qh{gi}')

        load_head(0)
        load_q(0, 0)
        for h in range(6):
            hj = h % 2
            for ti, (s, n) in enumerate(qtiles):
                gi = (h * len(qtiles) + ti) % 2
                qcur = qh[gi]
                qn_ = f'G_qh{gi}'
                if ti + 1 < len(qtiles):
                    load_q(h, ti + 1)
                elif h + 1 < 6:
                    load_q(h + 1, 0)
                if ti == 0 and h + 1 < 6:
                    load_head(h + 1)

                def emit_qk(kb):
                    pp = pS[kb % 2]
                    for comp in range(2):
                        P.op('pe', lambda e, comp=comp, kb=kb, pp=pp: e.matmul(pp[:, comp, 0:n], lhsT=kh[hj][64 * comp:64 * comp + 64, kb * 128:(kb + 1) * 128], rhs=qcur[64 * comp:64 * comp + 64, 0:n], start=True, stop=True),
                             [f'G_kh{hj}', qn_], [f'G_pS{kb % 2}'])
                emit_qk(0)
                first = [True, True]
                for kb in range(NKB):
                    if kb + 1 < NKB:
                        emit_qk(kb + 1)
                    pp = pS[kb % 2]
                    ppn = f'G_pS{kb % 2}'
                    pt = pT[kb % 4]
                    ptn = f'G_pT{kb % 4}'
                    P.op('act', lambda e, pp=pp, pt=pt: e.activation(out=pt[:, :, 0:n], in_=pp[:, :, 0:n], func=AF.Exp, scale=0.125), [ppn], [ptn])
                    for comp in range(2):
                        P.op('pe', lambda e, comp=comp, kb=kb, pt=pt: e.matmul(po[comp][:, 0:n], lhsT=vh[hj][:, kb, :], rhs=pt[:, comp, 0:n], start=(kb == 0), stop=(kb == NKB - 1)),
                             [f'G_vh{hj}', ptn], [f'G_po{comp}'])
                    P.op('pe', lambda e, kb=kb, pt=pt: e.matmul(pl[:, 0, 0:n], lhsT=k.ones_bf[:], rhs=pt[:, 0, 0:n], start=(kb == 0), stop=(kb == NKB - 1)),
                         ['ones_bf', ptn], ['G_pl0'])
                    ac = accs[1]
                    if kb == 0:
                        P.op('dve', lambda e, ac=ac, pt=pt: e.tensor_copy(out=ac[:, 1, 0:n], in_=pt[:, 1, 0:n]), [ptn], ['G_acc1'])
                    else:
                        P.op('dve', lambda e, ac=ac, pt=pt: e.tensor_tensor(out=ac[:, 1, 0:n], in0=ac[:, 1, 0:n], in1=pt[:, 1, 0:n], op=ALU.add), [ptn, 'G_acc1'], ['G_acc1'])
                P.op('pe', lambda e: e.matmul(pl[:, 1, 0:n], lhsT=k.ones_f[:], rhs=accs[1][:, 1, 0:n], start=True, stop=True), ['ones_f', 'G_acc1'], ['G_pl1'])
                P.op('dve', lambda e: e.reciprocal(out=rl[:, :, 0:n], in_=pl[:, :, 0:n]), ['G_pl0', 'G_pl1'], ['G_rl'])
                P.op('dve', lambda e: e.tensor_tensor(out=o1[:, 0:n], in0=po[0][:, 0:n], in1=rl[:, 0, 0:n], op=ALU.mult), ['G_po0', 'G_rl'], ['G_o1'])
                P.op('dve', lambda e: e.tensor_tensor(out=o2[:, 0:n], in0=po[1][:, 0:n], in1=rl[:, 1, 0:n], op=ALU.mult), ['G_po1', 'G_rl'], ['G_o2'])
                P.op('dve', lambda e: e.scalar_tensor_tensor(out=o1[:, 0:n], in0=o2[:, 0:n], scalar=neglam[:, 0:1], in1=o1[:, 0:n], op0=ALU.mult, op1=ALU.add),
                     ['G_o1', 'G_o2', 'G_neglam'], ['G_o1'])
                P.op('act', lambda e: e.activation(out=sq[:, 0:n], in_=o1[:, 0:n], func=AF.Square), ['G_o1'], ['G_sq'])
                P.op('pe', lambda e: e.matmul(pl[:, 0, 0:n], lhsT=k.ones_f[:], rhs=sq[:, 0:n], start=True, stop=True), ['ones_f', 'G_sq'], ['G_pl0', 'G_pl1'])
                P.op('act', lambda e: e.activation(out=sq[:, 0:n], in_=pl[:, 0, 0:n], func=AF.Sqrt, bias=k.cst[:, 0:1], scale=1.0 / 128), ['G_pl0', 'G_pl1', 'cst'], ['G_sq'])
                P.op('dve', lambda e: e.reciprocal(out=sq[:, 0:n], in_=sq[:, 0:n]), ['G_sq'], ['G_sq'])
                P.op('dve', lambda e: e.scalar_tensor_tensor(out=ych[:, 0:n], in0=o1[:, 0:n], scalar=gsub[:, 1:2], in1=sq[:, 0:n], op0=ALU.mult, op1=ALU.mult),
                     ['G_o1', 'G_sq', 'G_gsub'], ['G_ych'])
                P.dma('sp', yc_d[h * 128:(h + 1) * 128, s:s + n], ych[:, 0:n], ['G_ych'], [], 'G_ych')
    P.barrier()
    if stop_after == 'l1F1':
        return

    with ExitStack() as st:
        sb = lambda n, s, d: k.sb(n, s, d, st)
        ps = lambda n, s, d: k.ps(n, s, d, st)
        w_out = sb('w_out1', [128, 8, D], BF16)
        with ExitStack() as st2:
            fold_w_bf16(k, st2, 'w_out1', w_out, I['cd_w_out'], 8, k.gbc_d[1, 0, 0])
            P.barrier()
        cp = {}
        for nm, shp in (('conf_w', [128, 2, 31]), ('conf_b', [128, 2]), ('conf_lng', [128, 2]), ('conf_lnb', [128, 2])):
            cp[nm] = sb('H_' + nm, shp, F32)
            P.dma('sp', cp[nm][:], I[nm], [], ['H_' + nm], 'H_' + nm)
        yT = sb('H_yT', [128, 8, 512], BF16)
        gin = sb('H_gin', [128, 2, 542], F32)
        ca = sb('H_ca', [128, 512], F32)
        cb_ = sb('H_cb', [128, 512], F32)
        ct = [sb(f'H_ct{i}', [128, 512], F32) for i in range(2)]
        xm = sb('H_xm', [128, 2, 512], F32)
        sq = sb('H_sq', [128, 2, 512], F32)
        rstd = sb('H_rstd', [128, 512], F32)
        xt = sb('H_xt', [128, 4, D], F32)
        xo = sb('H_xo', [128, 4, D], F32)
        pm = ps('H_pm', [128, 512], F32)
        pv = ps('H_pv', [128, 512], F32)
        po = [ps(f'H_po{i}', [128, 512], F32) for i in range(4)]
        glv = S['gl'].rearrange("(c p) t -> p c t", p=128)
        ycv = yc_d.rearrange("(c p) t -> p c t", p=128)
        for (s, n) in tiles_of(TL):
            ng = n // 128
            P.dma('sp', yT[:, 0:6, 0:n], ycv[:, :, s:s + n], [], keys('H_yT', 8)[0:6], 'H_yT')
            P.dma('sp', gin[:, :, 0:n + 30], glv[:, :, PADZ + s - 15:PADZ + s + n + 15], [], keys('H_gin', 2), 'H_gin')
            P.dma('sp', xt[:, 0:ng, :], S['x1_l'][1 + s:1 + s + n, :].rearrange("(g p) f -> p g f", p=128), [], ['H_xt'], 'H_xt')
            for c in range(2):
                P.op('act', lambda e, c=c: e.activation(out=ca[:, 0:n], in_=gin[:, c, 0:n], func=AF.Identity, scale=cp['conf_w'][:, c, 0:1], bias=cp['conf_b'][:, c:c + 1]),
                     [f'H_gin.{c}', 'H_conf_w', 'H_conf_b'], ['H_ca'])
                P.op('pool', lambda e, c=c: e.tensor_scalar(out=cb_[:, 0:n], in0=gin[:, c, 1:1 + n], scalar1=cp['conf_w'][:, c, 1:2], scalar2=None, op0=ALU.mult),
                     [f'H_gin.{c}', 'H_conf_w'], ['H_cb'])
                for t in range(2, 31):
                    if t % 2 == 0:
                        P.op('dve', lambda e, c=c, t=t: e.scalar_tensor_tensor(out=ca[:, 0:n], in0=gin[:, c, t:t + n], scalar=cp['conf_w'][:, c, t:t + 1], in1=ca[:, 0:n],
                                                                               op0=ALU.mult, op1=ALU.add), [f'H_gin.{c}', 'H_ca'], ['H_ca'])
                    else:
                        ctt = ct[(t // 2) % 2]
                        ctn = f'H_ct{(t // 2) % 2}'
                        P.op('act', lambda e, c=c, t=t, ctt=ctt: e.activation(out=ctt[:, 0:n], in_=gin[:, c, t:t + n], func=AF.Identity, scale=cp['conf_w'][:, c, t:t + 1], bias=k.cst[:, 2:3]),
                             [f'H_gin.{c}', 'H_conf_w', 'cst'], [ctn])
                        P.op('pool', lambda e, ctt=ctt: e.tensor_tensor(out=cb_[:, 0:n], in0=cb_[:, 0:n], in1=ctt[:, 0:n], op=ALU.add), [ctn, 'H_cb'], ['H_cb'])
                P.op('dve', lambda e, c=c: e.tensor_tensor(out=xm[:, c, 0:n], in0=ca[:, 0:n], in1=cb_[:, 0:n], op=ALU.add), ['H_ca', 'H_cb'], [f'H_xm.{c}'])
            for c in range(2):
                P.op('pe', lambda e, c=c: e.matmul(pm[:, 0:n], lhsT=k.ones_f[:], rhs=xm[:, c, 0:n], start=(c == 0), stop=(c == 1)), ['ones_f', f'H_xm.{c}'], ['H_pm'])
            for c in range(2):
                P.op('dve', lambda e, c=c: e.scalar_tensor_tensor(out=xm[:, c, 0:n], in0=pm[:, 0:n], scalar=-1.0 / 256, in1=xm[:, c, 0:n], op0=ALU.mult, op1=ALU.add),
                     ['H_pm', f'H_xm.{c}'], [f'H_xm.{c}'])
                P.op('act', lambda e, c=c: e.activation(out=sq[:, c, 0:n], in_=xm[:, c, 0:n], func=AF.Square), [f'H_xm.{c}'], [f'H_sq.{c}'])
            for c in range(2):
                P.op('pe', lambda e, c=c: e.matmul(pv[:, 0:n], lhsT=k.ones_f[:], rhs=sq[:, c, 0:n], start=(c == 0), stop=(c == 1)), ['ones_f', f'H_sq.{c}'], ['H_pv'])
            P.op('act', lambda e: e.activation(out=rstd[:, 0:n], in_=pv[:, 0:n], func=AF.Sqrt, bias=k.cst[:, 0:1], scale=1.0 / 256), ['H_pv', 'cst'], ['H_rstd'])
            P.op('dve', lambda e: e.reciprocal(out=rstd[:, 0:n], in_=rstd[:, 0:n]), ['H_rstd'], ['H_rstd'])
            for c in range(2):
                P.op('dve', lambda e, c=c: e.tensor_tensor(out=xm[:, c, 0:n], in0=xm[:, c, 0:n], in1=rstd[:, 0:n], op=ALU.mult), [f'H_xm.{c}', 'H_rstd'], [f'H_xm.{c}'])
                P.op('act', lambda e, c=c: e.activation(out=yT[:, 6 + c, 0:n], in_=xm[:, c, 0:n], func=AF.Silu, scale=cp['conf_lng'][:, c:c + 1], bias=cp['conf_lnb'][:, c:c + 1]),
                     [f'H_xm.{c}', 'H_conf_lng', 'H_conf_lnb'], [f'H_yT.{6 + c}'])
            for g in range(ng):
                for nt in range(2):
                    pp = po[(2 * g + nt) % 4]
                    ppn = f'H_po{(2 * g + nt) % 4}'
                    for kc in range(8):
                        P.op('pe', lambda e, g=g, nt=nt, kc=kc, pp=pp: e.matmul(pp[:, :], lhsT=yT[:, kc, g * 128:(g + 1) * 128], rhs=w_out[:, kc, nt * 512:(nt + 1) * 512],
                                                                               start=(kc == 0), stop=(kc == 7)), [f'H_yT.{kc}', f'w_out1.{kc}'], [ppn])
                    P.op('dve', lambda e, g=g, nt=nt, pp=pp: e.tensor_tensor(out=xo[:, g, nt * 512:(nt + 1) * 512], in0=pp[:, :], in1=xt[:, g, nt * 512:(nt + 1) * 512], op=ALU.add),
                         [ppn, 'H_xt'], [f'H_xo.{g}'])
            P.dma('sp', S['x15'][1 + s:1 + s + n, :].rearrange("(g p) f -> p g f", p=128), xo[:, 0:ng, :], keys('H_xo', 4)[0:ng], [], 'H_xo')
    P.barrier()
    if stop_after == 'l1F2':
        return
    ffn_phase(k, 1, 0, S['x15'], None, TL, final=True, flush=(not k.half), out_rows=k.TO)
    P.barrier()


def _fm(v, nch):
    return np.ascontiguousarray(np.asarray(v, np.float32).reshape(nch, 128).T)


def prep_shared(inp, T):
    f = lambda a: np.ascontiguousarray(np.asarray(a, np.float32))
    d = {}
    d['mod_w'] = f(inp['mod_w'])
    d['mod_b'] = f(inp['mod_b'])
    d['modb_fm'] = f(np.asarray(inp['mod_b']).reshape(2, 6, 8, 128).transpose(3, 0, 1, 2))
    ng = np.stack([np.asarray(inp['norm_mix_g']), np.asarray(inp['norm_ffn_g'])], axis=1)
    d['ng_fm'] = f(ng.reshape(2, 2, 8, 128).transpose(3, 0, 1, 2))
    d['final_g_bc'] = f(np.broadcast_to(np.asarray(inp['final_g'])[None, :], (128, D)))
    d['ident'] = f(np.eye(128))
    d['ab_w_in'] = f(inp['ab_w_in'][0])
    d['ab_w_out'] = f(inp['ab_w_out'][0])
    cw4 = np.asarray(inp['lru_conv_w'][0], np.float32).reshape(4, 6, 128).transpose(2, 1, 0)
    d['lru_cw'] = f(np.concatenate([cw4, np.zeros((128, 6, 1), np.float32)], axis=2))
    d['pool_fl'] = f(np.stack([np.ones(128), np.zeros(128)], axis=1))
    d['lru_cb'] = _fm(inp['lru_conv_b'][0], 6)
    for nm, src in (('lru_bdA', inp['lru_wa'][0]), ('lru_bdX', inp['lru_wx'][0])):
        src = np.asarray(src)
        bd = np.zeros((128, 2, 6, 128), np.float32)
        for dd in range(2):
            for c in range(6):
                bd[0:64, dd, c, 0:64] = src[dd, 2 * c]
                bd[64:128, dd, c, 64:128] = src[dd, 2 * c + 1]
        d[nm] = bd
    for nm, src in (('lru_ba', inp['lru_ba'][0]), ('lru_bx', inp['lru_bx'][0]), ('lru_lam', inp['lru_lambda'][0])):
        d[nm] = f(np.asarray(src).reshape(2, 6, 128).transpose(2, 0, 1))
    pw = np.asarray(inp['pool_w'][0])
    bd = np.zeros((128, 2, 128), np.float32)
    for ch in range(2):
        bd[0:64, ch, 0:64] = pw[2 * ch]
        bd[64:128, ch, 64:128] = pw[2 * ch + 1]
    d['pool_bd'] = bd
    d['pool_scale'] = _fm(inp['pool_scale'][0], 2)
    invw = np.zeros((128, 2), np.float32)
    corr = np.ones((128, 2, 2, 16), np.float32)
    wins = (2, 4, 8, 16)
    Lbig = 1 << 20
    for g, w in enumerate(wins):
        ch, half = g // 2, g % 2
        psl = slice(64 * half, 64 * half + 64)
        invw[psl, ch] = 1.0 / w
        for i in range(16):
            t = i
            cnt = (t + w - w // 2) - max(t - w // 2, 0)
            corr[psl, ch, 0, i] = float(w) / cnt
            t = Lbig - 16 + i
            cnt = min(t + w - w // 2, Lbig) - (t - w // 2)
            corr[psl, ch, 1, i] = float(w) / cnt
    d['pool_invw'] = invw
    d['pool_corr'] = corr
    d['ffn_w_up'] = f(inp['ffn_w_up'])
    d['ffn_w_down'] = f(inp['ffn_w_down'])
    d['ffn_cw'] = f(np.asarray(inp['ffn_conv_w']).reshape(2, 3, 2 * NFC, 128).transpose(3, 0, 2, 1))
    d['ffn_cb'] = f(np.asarray(inp['ffn_conv_b']).reshape(2, 2 * NFC, 128).transpose(2, 0, 1))
    w_in = np.asarray(inp['cd_w_in'][0], np.float32)
    d['cd_w_in'] = f(w_in)
    qk = w_in[:, :1536].reshape(D, 1536 // 32, 2, 16)
    d['cd_w_sw'] = f(qk[:, :, ::-1, :].reshape(D, 1536))
    d['cd_w_out'] = f(inp['cd_w_out'][0])
    t = np.arange(T)
    row = (t // GRID_W).astype(np.float32)
    col = (t % GRID_W).astype(np.float32)
    inv = (10000.0 ** (-np.arange(16, dtype=np.float32) / 16)).astype(np.float32)
    ang_r = (row[:, None] * inv).astype(np.float32)
    ang_c = (col[:, None] * inv).astype(np.float32)
    rc = np.zeros((128, T), np.float32)
    rs = np.zeros((128, T), np.float32)
    for p in range(128):
        dd = p % 64
        ang = ang_r if dd < 32 else ang_c
        fq = dd % 16
        first = (dd % 32) < 16
        rc[p] = np.cos(ang[:, fq])
        rs[p] = (-1.0 if first else 1.0) * np.sin(ang[:, fq])
    d['rope_c'] = rc
    d['rope_s'] = rs
    d['diff_l'] = f(np.stack([np.asarray(inp['diff_lq1'][0]), np.asarray(inp['diff_lk1'][0]),
                              np.asarray(inp['diff_lq2'][0]), np.asarray(inp['diff_lk2'][0])])[None])
    d['subln_g'] = f(np.asarray(inp['diff_subln_g'][0]).reshape(128, 1))
    d['conf_w'] = f(np.asarray(inp['conf_dw_w'][0]).reshape(31, 2, 128).transpose(2, 1, 0))
    d['conf_b'] = _fm(inp['conf_dw_b'][0], 2)
    d['conf_lng'] = _fm(inp['conf_ln_g'][0], 2)
    d['conf_lnb'] = _fm(inp['conf_ln_b'][0], 2)
    return d


def prep_rev(shared, T):
    f = lambda a: np.ascontiguousarray(np.asarray(a, np.float32))
    d = dict(shared)
    cw = shared['lru_cw']
    d['lru_cw'] = f(cw[:, :, ::-1])
    for nm in ('lru_bdA', 'lru_bdX', 'lru_ba', 'lru_bx', 'lru_lam'):
        d[nm] = f(shared[nm][:, ::-1])
    d['pool_fl'] = f(np.stack([np.zeros(128), np.ones(128)], axis=1))
    corr = np.ones((128, 2, 2, 16), np.float32)
    Lbig = 1 << 20
    for g, w in enumerate((2, 4, 8, 16)):
        ch, half = g // 2, g % 2
        psl = slice(64 * half, 64 * half + 64)
        for i in range(16):
            r = i
            cnt = (r + w // 2) - max(r - w // 2 + 1, 0) + 1
            corr[psl, ch, 0, i] = float(w) / cnt
            r = Lbig - 16 + i
            cnt = min(r + w // 2, Lbig - 1) - (r - w // 2 + 1) + 1
            corr[psl, ch, 1, i] = float(w) / cnt
    d['pool_corr'] = corr
    d['ffn_cw'] = f(shared['ffn_cw'][:, :, :, ::-1])
    d['conf_w'] = f(shared['conf_w'][:, :, ::-1])
    d['rope_c'] = f(shared['rope_c'][:, ::-1])
    d['rope_s'] = f(shared['rope_s'][:, ::-1])
    return d


def prep_core(inp, shared, b, T, rev=False):
    m = dict(shared)
    x = np.asarray(inp['x'][b, :T], np.float32)
    ctx = np.asarray(inp['ctx'][b], np.float32)
    if rev:
        x = x[::-1]
        ctx = ctx[::-1]
    m['x'] = np.ascontiguousarray(x)
    m['ctx'] = np.ascontiguousarray(ctx)
    m['c_fm'] = np.ascontiguousarray(np.stack([_fm(inp['c'][b], 8), _fm(inp['c_ctx'], 8)], axis=1))
    return m


_CACHE = {}


def kernel(**inputs):
    T = inputs['x'].shape[1]
    Bn = inputs['x'].shape[0]
    if T not in _CACHE:
        _CACHE[T] = build(T)
    kk = _CACHE[T]
    shared = prep_shared(inputs, T)
    shared_r = prep_rev(shared, T)
    in_maps = [prep_core(inputs, shared, b, T, rev=False) for b in range(Bn)] + \
              [prep_core(inputs, shared_r, b, T, rev=True) for b in range(Bn)]
    res = run_bass_kernel_spmd(kk.nc, in_maps, core_ids=list(range(2 * Bn)))
    out = np.empty((Bn, T, D), np.float32)
    for b in range(Bn):
        out[b, :T // 2] = np.asarray(res.results[b]['out'], np.float32)
        out[b, T // 2:] = np.asarray(res.results[Bn + b]['out'], np.float32)[::-1]
    return out
```

```python
import numpy as np
import math
import os
from contextlib import ExitStack
import concourse.bass as bass
import concourse.mybir as mybir
from concourse.bass_utils import run_bass_kernel_spmd
from concourse.ap import AP

F32 = mybir.dt.float32
BF16 = mybir.dt.bfloat16
AF = mybir.ActivationFunctionType
ALU = mybir.AluOpType
AX = mybir.AxisListType

D = 1024
LC = 256
DFF = 2816
NFC = DFF // 128
PADZ = 16
EPS = 1e-6
GRID_W = 64
LAMBDA_INIT1 = 0.8 - 0.6 * math.exp(-0.3 * 1)
SAME_ENGINE_SYNC = True


def rev(ap):
    a = [list(x) for x in ap.ap]
    st, n = a[-1]
    a[-1] = [-st, n]
    return AP(ap.tensor, ap.offset + st * (n - 1), a)


class Prog:
    def __init__(self, nc, es):
        self.nc = nc
        self.es = es
        self.eng = {'pe': nc.tensor, 'act': nc.scalar, 'dve': nc.vector, 'pool': nc.gpsimd, 'sp': nc.sync}
        self.esem = {e: es.enter_context(nc.semaphore('S_' + e)) for e in ('pe', 'act', 'dve', 'pool')}
        self.ecnt = {e: 0 for e in self.esem}
        self.dsem = {}
        self.dpool = []
        self.nd = 0
        self.waited = {e: {} for e in self.eng}
        self.lastw = {}
        self.rd = {}
        self.ninstr = 0

    def _need(self, reads, writes):
        ev = {}

        def add(e):
            if e is None:
                return
            k, sem, val = e
            if k not in ev or ev[k][1] < val:
                ev[k] = (sem, val)
        for r in reads:
            add(self.lastw.get(r))
        for w in writes:
            add(self.lastw.get(w))
            for e in self.rd.get(w, {}).items():
                add((e[0], e[1][0], e[1][1]))
        return ev

    def _wait(self, e, ev):
        for k, (sem, val) in ev.items():
            if k == 'S_' + e and (e == 'pe' or not SAME_ENGINE_SYNC):
                continue
            if self.waited[e].get(k, 0) < val:
                self.eng[e].wait_ge(sem, val)
                self.waited[e][k] = val
                self.ninstr += 1

    def _commit(self, ev, reads, writes):
        k, sem, val = ev
        for w in writes:
            self.lastw[w] = ev
            self.rd[w] = {}
        for r in reads:
            self.rd.setdefault(r, {})[k] = (sem, val)

    def op(self, e, fn, reads, writes):
        self._wait(e, self._need(reads, writes))
        ins = fn(self.eng[e])
        self.ecnt[e] += 1
        ins.then_inc(self.esem[e], 1)
        self.ninstr += 1
        self._commit(('S_' + e, self.esem[e], self.ecnt[e]), reads, writes)

    def dma(self, q, out, in_, reads, writes, key):
        self._wait(q, self._need(reads, writes))
        if key not in self.dsem:
            if self.dpool:
                self.dsem[key] = self.dpool.pop()
            else:
                nm = 'D%d' % self.nd
                self.nd += 1
                self.dsem[key] = [self.es.enter_context(self.nc.semaphore(nm)), 0, nm]
        d = self.dsem[key]
        ins = self.eng[q].dma_start(out=out, in_=in_)
        d[1] += 16
        ins.then_inc(d[0], 16)
        self.ninstr += 1
        self._commit((d[2], d[0], d[1]), reads, writes)

    def barrier(self):
        ev = {}
        for e in self.esem:
            if self.ecnt[e] > 0:
                ev['S_' + e] = (self.esem[e], self.ecnt[e])
        for k, d in self.dsem.items():
            if d[1] > 0:
                ev[d[2]] = (d[0], d[1])
        for e in self.eng:
            self._wait(e, dict(ev))
        self.lastw = {}
        self.rd = {}
        for k, d in self.dsem.items():
            self.dpool.append(d)
        self.dsem = {}


def keys(name, n):
    return [f"{name}.{i}" for i in range(n)]


def tiles_of(T, w=512):
    out = []
    s = 0
    while s < T:
        n = min(w, T - s)
        out.append((s, n))
        s += n
    return out


class K:
    pass


def build(T, dbg=False, stop_after=None, half=True):
    nc = bass.Bass("TRN2", target_bir_lowering=False)
    k = K()
    k.nc = nc
    k.T = T
    k.half = half
    k.TL = (T // 2 + 128) if half else T
    k.TO = (T // 2) if half else T

    def din(name, shape, dt=F32):
        return nc.dram_tensor(name, list(shape), dt, kind="ExternalInput").ap()

    def dscr(name, shape, dt=F32, out=False):
        kind = "ExternalOutput" if (out or dbg) else "Internal"
        return nc.dram_tensor(name, list(shape), dt, kind=kind).ap()

    I = {}
    I['x'] = din('x', [T, D])
    I['ctx'] = din('ctx', [LC, D])
    I['c_fm'] = din('c_fm', [128, 2, 8])
    I['mod_w'] = din('mod_w', [2, D, 6 * D])
    I['modb_fm'] = din('modb_fm', [128, 2, 6, 8])
    I['mod_b'] = din('mod_b', [2, 6 * D])
    I['ng_fm'] = din('ng_fm', [128, 2, 2, 8])
    I['final_g_bc'] = din('final_g_bc', [128, D])
    I['ident'] = din('ident', [128, 128])
    I['ab_w_in'] = din('ab_w_in', [D, 1792])
    I['ab_w_out'] = din('ab_w_out', [D, D])
    I['lru_cw'] = din('lru_cw', [128, 6, 5])
    I['pool_fl'] = din('pool_fl', [128, 2])
    I['lru_cb'] = din('lru_cb', [128, 6])
    I['lru_bdA'] = din('lru_bdA', [128, 2, 6, 128])
    I['lru_bdX'] = din('lru_bdX', [128, 2, 6, 128])
    I['lru_ba'] = din('lru_ba', [128, 2, 6])
    I['lru_bx'] = din('lru_bx', [128, 2, 6])
    I['lru_lam'] = din('lru_lam', [128, 2, 6])
    I['pool_bd'] = din('pool_bd', [128, 2, 128])
    I['pool_scale'] = din('pool_scale', [128, 2])
    I['pool_invw'] = din('pool_invw', [128, 2])
    I['pool_corr'] = din('pool_corr', [128, 2, 2, 16])
    I['ffn_w_up'] = din('ffn_w_up', [2, D, 2 * DFF])
    I['ffn_w_down'] = din('ffn_w_down', [2, DFF, D])
    I['ffn_cw'] = din('ffn_cw', [128, 2, 2 * NFC, 3])
    I['ffn_cb'] = din('ffn_cb', [128, 2, 2 * NFC])
    I['cd_w_in'] = din('cd_w_in', [D, 2816])
    I['cd_w_sw'] = din('cd_w_sw', [D, 1536])
    I['cd_w_out'] = din('cd_w_out', [D, D])
    I['rope_c'] = din('rope_c', [128, T])
    I['rope_s'] = din('rope_s', [128, T])
    I['diff_l'] = din('diff_l', [1, 4, 64])
    I['subln_g'] = din('subln_g', [128, 1])
    I['conf_w'] = din('conf_w', [128, 2, 31])
    I['conf_b'] = din('conf_b', [128, 2])
    I['conf_lng'] = din('conf_lng', [128, 2])
    I['conf_lnb'] = din('conf_lnb', [128, 2])
    k.I = I

    k.out = nc.dram_tensor('out', [k.TO, D], F32, kind="ExternalOutput").ap()
    S = {}
    for nm, TT in (('l', T), ('c', LC)):
        S['z_' + nm] = dscr('z_' + nm, [1792, PADZ + TT + PADZ])
        S['xa_' + nm] = dscr('xa_' + nm, [768, TT])
        S['hf_' + nm] = dscr('hf_' + nm, [768, TT])
        S['x05_' + nm] = dscr('x05_' + nm, [1 + TT, D])
        S['x1_' + nm] = dscr('x1_' + nm, [1 + TT + 128, D])
    S['x15'] = dscr('x15', [1 + T, D])
    S['x2'] = dscr('x2', [1 + T + 128, D])
    S['qT'] = dscr('qT', [768, T], BF16)
    S['kT'] = dscr('kT', [768, T + LC], BF16)
    S['v'] = dscr('v', [T + LC, 768], BF16)
    S['gl'] = dscr('gl', [256, PADZ + T + PADZ])
    k.S = S

    with ExitStack() as es:
        P = Prog(nc, es)
        k.P = P

        uid = [0]

        def sb(name, shape, dt, st=es):
            uid[0] += 1
            return st.enter_context(nc.sbuf_tensor(f"{name}_s{uid[0]}", list(shape), dt))

        def ps(name, shape, dt, st=es):
            uid[0] += 1
            return st.enter_context(nc.psum_tensor(f"{name}_p{uid[0]}", list(shape), dt))
        k.sb = sb
        k.ps = ps

        ident = sb('ident', [128, 128], BF16)
        P.dma('pool', ident[:], I['ident'], [], ['ident'], 'ident')
        k.ident = ident
        cst = sb('cst', [128, 8], F32)
        P.op('dve', lambda e: e.memset(cst[:, 0:1], EPS), [], ['cst'])
        P.op('dve', lambda e: e.memset(cst[:, 1:2], 1.0), [], ['cst'])
        P.op('dve', lambda e: e.memset(cst[:, 2:3], 0.0), [], ['cst'])
        k.cst = cst
        zero = sb('zero', [128, 512], F32)
        P.op('dve', lambda e: e.memset(zero[:], 0.0), [], ['zero'])
        k.zero = zero
        ones_bf = sb('ones_bf', [128, 128], BF16)
        P.op('dve', lambda e: e.memset(ones_bf[:], 1.0), [], ['ones_bf'])
        k.ones_bf = ones_bf
        ones_f = sb('ones_f', [128, 128], F32)
        P.op('dve', lambda e: e.memset(ones_f[:], 1.0), [], ['ones_f'])
        k.ones_f = ones_f

        modfm = sb('modfm', [128, 2, 2, 4, 8], F32)
        k.modfm = modfm
        k.gbc_d = nc.dram_tensor('gbc_d', [2, 2, 2, 128, D], F32, kind="Internal").ap()

        if os.environ.get('DBG_ONLY_F1'):
            layer1(k, 'l1F1')
            return finish(k, es)
        phase_adaln(k)
        P.barrier()
        if dbg:
            mdbg = nc.dram_tensor('modfm_dbg', [128, 2, 2, 4, 8], F32, kind="ExternalOutput").ap()
            P.dma('sp', mdbg, modfm[:], [], [], 'mdbg')
            gdbg = nc.dram_tensor('gbc_dbg', [2, 2, 2, 128, D], F32, kind="ExternalOutput").ap()
            P.dma('sp', gdbg, k.gbc_d, [], [], 'gdbg')
        if stop_after == 'adaln':
            return finish(k, es)

        layer0(k, stop_after)
        if stop_after is not None and stop_after.startswith('l0'):
            return finish(k, es)
        layer1(k, stop_after)
        return finish(k, es)


def finish(k, es):
    k.P.barrier()
    k.ninstr = k.P.ninstr
    return k


def phase_adaln(k):
    nc, P, I = k.nc, k.P, k.I
    with ExitStack() as st:
        sb = lambda n, s, d: k.sb(n, s, d, st)
        ps = lambda n, s, d: k.ps(n, s, d, st)
        cf = sb('ad_cf', [128, 2, 8], F32)
        P.dma('sp', cf[:], I['c_fm'], [], ['ad_cf'], 'ad_cf')
        sc = sb('ad_sc', [128, 2, 8], F32)
        P.op('act', lambda e: e.activation(out=sc[:], in_=cf[:], func=AF.Silu), ['ad_cf'], ['ad_sc'])
        rep = sb('ad_rep', [128, 8, 128], F32)
        for s_ in range(2):
            for kc in range(8):
                P.op('dve', lambda e, s_=s_, kc=kc: e.tensor_copy(out=rep[:, kc, 64 * s_:64 * s_ + 64], in_=sc[:, s_, kc:kc + 1].to_broadcast([128, 64])),
                     ['ad_sc'], [f'ad_rep.{kc}'])
        modb = sb('ad_modb', [128, 2, 6, 8], F32)
        P.dma('sp', modb[:], I['modb_fm'], [], ['ad_modb'], 'ad_modb')
        ng = sb('ad_ng', [128, 2, 2, 8], F32)
        P.dma('sp', ng[:], I['ng_fm'], [], ['ad_ng'], 'ad_ng')
        brow = sb('ad_brow', [1, 2, 6 * D], F32)
        P.dma('sp', brow[:], I['mod_b'].rearrange("(o l) n -> o l n", o=1), [], ['ad_brow'], 'ad_brow')
        wt = [sb(f'ad_w{i}', [128, 6 * D], F32) for i in range(2)]
        gst = sb('ad_gst', [128, 2, D], F32)
        facc = sb('ad_facc', [128, 32, 2], F32)
        pfm = ps('ad_pfm', [128, 32, 2], F32)
        pbc = [ps(f'ad_pbc{i}', [128, 512], F32) for i in range(4)]
        for l in range(2):
            for pss in range(1):
                for kc in range(8):
                    w = wt[kc % 2]
                    wk = f'ad_w{kc % 2}'
                    P.dma('sp', w[:], I['mod_w'][l, kc * 128:(kc + 1) * 128, :], [], [wk], wk)
                    if pss == 0:
                        jmap = [0, 1, 3, 4]
                        for jj, j in enumerate(jmap):
                            for fc in range(8):
                                col = j * D + fc * 128
                                P.op('pe', lambda e, w=w, col=col, jj=jj, fc=fc, kc=kc: e.matmul(
                                    pfm[:, jj * 8 + fc, :], lhsT=w[:, col:col + 128], rhs=sc[:, :, kc],
                                    start=True, stop=True), [wk, 'ad_sc'], ['ad_pfm'])
                        if kc == 0:
                            P.op('dve', lambda e: e.tensor_copy(out=facc[:], in_=pfm[:]), ['ad_pfm'], ['ad_facc'])
                        else:
                            P.op('dve', lambda e: e.tensor_tensor(out=facc[:], in0=pfm[:], in1=facc[:], op=ALU.add), ['ad_pfm', 'ad_facc'], ['ad_facc'])
                    if True:
                        for nt in range(4):
                            gj = 2 if nt < 2 else 5
                            col = gj * D + (nt % 2) * 512
                            P.op('pe', lambda e, w=w, col=col, nt=nt, kc=kc: e.matmul(
                                pbc[nt][:], lhsT=rep[:, kc, :], rhs=w[:, col:col + 512],
                                start=(kc == 0), stop=False), [wk, f'ad_rep.{kc}'], [f'ad_pbc{nt}'])
                if pss == 0:
                    jmap = [0, 1, 3, 4]
                    for s_ in range(2):
                        for jj, j in enumerate(jmap):
                            P.op('dve', lambda e, s_=s_, jj=jj, j=j, l=l: e.tensor_tensor(
                                out=k.modfm[:, l, s_, jj, :], in0=facc[:, jj * 8:(jj + 1) * 8, s_], in1=modb[:, l, j, :], op=ALU.add),
                                ['ad_facc', 'ad_modb'], [f'modfm.{l}.{s_}.{jj}'])
                        for jj, which in ((1, 0), (3, 1)):
                            P.op('dve', lambda e, s_=s_, jj=jj, which=which, l=l: e.scalar_tensor_tensor(
                                out=k.modfm[:, l, s_, jj, :], in0=k.modfm[:, l, s_, jj, :], scalar=1.0, in1=ng[:, l, which, :],
                                op0=ALU.add, op1=ALU.mult), [f'modfm.{l}.{s_}.{jj}', 'ad_ng'], [f'modfm.{l}.{s_}.{jj}'])
                if True:
                    s_ = pss
                    for nt in range(4):
                        gj = 2 if nt < 2 else 5
                        col = gj * D + (nt % 2) * 512
                        P.op('pe', lambda e, nt=nt, col=col, l=l: e.matmul(
                            pbc[nt][:], lhsT=k.ones_f[0:1, :], rhs=brow[0:1, l, col:col + 512], start=False, stop=True),
                            ['ones_f', 'ad_brow'], [f'ad_pbc{nt}'])
                        P.op('act', lambda e, nt=nt: e.copy(out=gst[:, nt // 2, (nt % 2) * 512:(nt % 2) * 512 + 512], in_=pbc[nt][:]),
                            [f'ad_pbc{nt}'], [f'ad_gst.{nt}'])
                    for j2 in range(2):
                        for s2 in range(2):
                            for hh in range(2):
                                P.dma('sp', k.gbc_d[l, s2, j2, 64 * hh:64 * hh + 64, :], gst[64 * s2:64 * s2 + 64, j2, :],
                                      [f'ad_gst.{2 * j2}', f'ad_gst.{2 * j2 + 1}'], [], 'ad_gst')
        P.barrier()


def load_w_bf16(k, name, dst, src_ap, nk, ncols, colblk=2048):
    P = k.P
    for kc in range(nk):
        for c0 in range(0, ncols, colblk):
            c1 = min(ncols, c0 + colblk)
            P.dma('pool', dst[:, kc, c0:c1], src_ap[kc * 128:(kc + 1) * 128, c0:c1], [], [f'{name}.{kc}'], f'{name}.{kc}')


def fold_w_bf16(k, st, name, dst, src_ap, nk, gb_dram):
    P = k.P
    stg = [k.sb(f'{name}_stg{i}', [128, D], F32, st) for i in range(2)]
    gb = k.sb(f'{name}_gb', [128, D], F32, st)
    P.dma('sp', gb[:], gb_dram, [], [f'{name}_gb'], f'{name}_gb')
    gb_ap = gb[:]
    for kc in range(nk):
        s_ = stg[kc % 2]
        sk = f'{name}_stg{kc % 2}'
        P.dma('sp', s_[:], src_ap[kc * 128:(kc + 1) * 128, :], [], [sk], sk)
        P.op('dve', lambda e, s_=s_, kc=kc: e.tensor_tensor(out=dst[:, kc, :], in0=s_[:], in1=gb_ap, op=ALU.mult),
             [sk, f'{name}_gb'], [f'{name}.{kc}'])


def modulate_tile(k, B, src_rows, n, l, s_, jsh, hT, hTname, part=0):
    P = k.P
    ng = n // 128
    X, Xn = B['xt'], B['xtname']
    ss = B['ss']
    xn = B['xn']
    for g0 in (range(0, ng, 2) if part in (0, 1) else []):
        gg = min(2, ng - g0)
        P.dma('sp', X[:, 0:gg, :], src_rows[g0 * 128:(g0 + gg) * 128, :].rearrange("(g p) f -> p g f", p=128), [], [Xn], Xn)
        P.op('dve', lambda e: e.memset(ss[:, 0:2], 0.0), [], [B['ssname']])
        for g in range(gg):
            P.op('act', lambda e, g=g: e.activation(out=B['junk'][:], in_=X[:, g, :], func=AF.Square, accum_out=ss[:, g:g + 1]),
                 [Xn], [B['ssname'], B['junkname']])
        P.op('act', lambda e, gg=gg: e.activation(out=ss[:, 4:4 + gg], in_=ss[:, 0:gg], func=AF.Sqrt, bias=k.cst[:, 0:1], scale=1.0 / D),
             [B['ssname'], 'cst'], [B['ssname']])
        P.op('dve', lambda e, gg=gg: e.reciprocal(out=ss[:, 8:8 + gg], in_=ss[:, 4:4 + gg]), [B['ssname']], [B['ssname']])
        for g in range(gg):
            if g % 2 == 0:
                P.op('dve', lambda e, g=g, g0=g0: e.tensor_scalar(out=xn[:, g0 + g, :], in0=X[:, g, :], scalar1=ss[:, 8 + g:9 + g], scalar2=None, op0=ALU.mult),
                     [Xn, B['ssname']], [f"{B['xnname']}.{g0 + g}"])
            else:
                P.op('act', lambda e, g=g, g0=g0: e.activation(out=xn[:, g0 + g, :], in_=X[:, g, :], func=AF.Identity, scale=ss[:, 8 + g:9 + g], bias=k.cst[:, 2:3]),
                     [Xn, B['ssname'], 'cst'], [f"{B['xnname']}.{g0 + g}"])
    if part == 1:
        return
    for fc in range(8):
        tp = B['tp'][fc % 2]
        tpn = B['tpname'][fc % 2]
        for g in range(ng):
            P.op('pe', lambda e, g=g, fc=fc, tp=tp: e.transpose(out=tp[:, g * 128:(g + 1) * 128], in_=xn[:, g, fc * 128:(fc + 1) * 128], identity=k.ident[:]),
                 [f"{B['xnname']}.{g}", 'ident'], [tpn])
        P.op('act', lambda e, fc=fc, tp=tp: e.activation(out=hT[:, fc, 0:n], in_=tp[:, 0:n], func=AF.Identity,
                                                          scale=k.modfm[:, l, s_, jsh + 1, fc:fc + 1], bias=k.modfm[:, l, s_, jsh, fc:fc + 1]),
             [tpn], [f'{hTname}.{fc}'])


def mod_bufs(k, st, pfx):
    B = {}
    B['xt'] = k.sb(pfx + 'xt', [128, 2, D], F32, st)
    B['xtname'] = pfx + 'xt'
    B['xn'] = k.sb(pfx + 'xn', [128, 4, D], BF16, st)
    B['xnname'] = pfx + 'xn'
    B['junk'] = k.sb(pfx + 'junk', [128, D], BF16, st)
    B['junkname'] = pfx + 'junk'
    B['ss'] = k.sb(pfx + 'ss', [128, 12], F32, st)
    B['ssname'] = pfx + 'ss'
    B['tp'] = [k.ps(pfx + f'tp{i}', [128, 512], BF16, st) for i in range(2)]
    B['tpname'] = [pfx + f'tp{i}' for i in range(2)]
    return B


def layer0(k, stop_after):
    nc, P, I, S = k.nc, k.P, k.I, k.S
    with ExitStack() as st:
        sb = lambda n, s, d: k.sb(n, s, d, st)
        lp = {}
        for nm, shp in (('lru_cw', [128, 6, 5]), ('pool_fl', [128, 2]), ('lru_cb', [128, 6]), ('lru_ba', [128, 2, 6]), ('lru_bx', [128, 2, 6]),
                        ('lru_lam', [128, 2, 6]), ('pool_scale', [128, 2]), ('pool_invw', [128, 2]), ('pool_corr', [128, 2, 2, 16])):
            lp[nm] = sb('p_' + nm, shp, F32)
            P.dma('sp', lp[nm][:], I[nm], [], ['p_' + nm], 'p_' + nm)
        for nm, shp in (('lru_bdA', [128, 2, 6, 128]), ('lru_bdX', [128, 2, 6, 128]), ('pool_bd', [128, 2, 128])):
            lp[nm] = sb('p_' + nm, shp, BF16)
            P.dma('pool', lp[nm][:], I[nm], [], ['p_' + nm], 'p_' + nm)
        cl = sb('p_cl', [128, 2, 2, 6], F32)
        tmp = sb('p_cltmp', [128, 2, 6], F32)
        P.op('act', lambda e: e.activation(out=tmp[:], in_=lp['lru_lam'][:], func=AF.Exp, scale=-1.0), ['p_lru_lam'], ['p_cltmp'])
        P.op('act', lambda e: e.activation(out=tmp[:], in_=tmp[:], func=AF.Ln, bias=k.cst[:, 1:2], scale=1.0), ['p_cltmp', 'cst'], ['p_cltmp'])
        P.op('dve', lambda e: e.tensor_scalar(out=cl[:, 0, :, :], in0=tmp[:], scalar1=-8.0, scalar2=None, op0=ALU.mult), ['p_cltmp'], ['p_cl'])
        P.op('dve', lambda e: e.tensor_scalar(out=cl[:, 1, :, :], in0=tmp[:], scalar1=-16.0, scalar2=None, op0=ALU.mult), ['p_cltmp'], ['p_cl'])
        lp['cl'] = cl
        stt = sb('p_state', [128, 2, 6], F32)
        P.op('dve', lambda e: e.memset(stt[:], 0.0), [], keys('p_state0', 6) + keys('p_state1', 6))
        lp['state'] = stt
        k.lp = lp
        w_in = sb('w_in0', [128, 8, 1792], BF16)
        load_w_bf16(k, 'w_in0', w_in, I['ab_w_in'], 8, 1792, colblk=1792)
        w_out = sb('w_out0', [128, 8, D], BF16)
        k.w_in0, k.w_out0 = w_in, w_out

        for s_, nm, TT, xsrc in ((1, 'c', LC, I['ctx']), (0, 'l', k.T, I['x'])):
            seg = K()
            seg.nm, seg.T, seg.x, seg.set = nm, TT, xsrc, s_
            seg.z, seg.xa, seg.hf, seg.x05, seg.x1 = S['z_' + nm], S['xa_' + nm], S['hf_' + nm], S['x05_' + nm], S['x1_' + nm]
            seg.tiles = tiles_of(TT)
            with ExitStack() as st2:
                fold_w_bf16(k, st2, 'w_out0', w_out, I['ab_w_out'], 8, k.gbc_d[0, s_, 0])
            P.barrier()
            l0_phaseA(k, seg)
            P.barrier()
            if stop_after == 'l0A' and nm == 'l':
                return
            l0_phaseB(k, seg)
            P.barrier()
            if stop_after == 'l0B' and nm == 'l':
                return
            l0_phaseC(k, seg)
            P.barrier()
            if stop_after == 'l0C' and nm == 'l':
                return
    for s_, nm, TT in ((1, 'c', LC), (0, 'l', k.T)):
        ffn_phase(k, 0, s_, S['x05_' + nm], S['x1_' + nm], TT, final=False)
        P.barrier()


def l0_phaseA(k, seg):
    nc, P = k.nc, k.P
    with ExitStack() as st:
        sb = lambda n, s, d: k.sb(n, s, d, st)
        ps = lambda n, s, d: k.ps(n, s, d, st)
        B = mod_bufs(k, st, 'A_')
        hT = sb('A_hT', [128, 8, 512], BF16)
        zt = [sb(f'A_zt{i}', [128, 14, 512], F32) for i in range(2)]
        zp = [ps(f'A_zp{i}', [128, 512], F32) for i in range(4)]
        zv = seg.z.rearrange("(c p) t -> p c t", p=128)
        P.dma('sp', zv[:, :, 0:PADZ], k.zero[:, 0:14 * PADZ].rearrange("p (c t) -> p c t", c=14), ['zero'], [], 'A_zpad')
        P.dma('sp', zv[:, :, PADZ + seg.T:PADZ + seg.T + PADZ], k.zero[:, 0:14 * PADZ].rearrange("p (c t) -> p c t", c=14), ['zero'], [], 'A_zpad')
        modulate_tile(k, B, seg.x[seg.tiles[0][0]:seg.tiles[0][0] + seg.tiles[0][1], :], seg.tiles[0][1], 0, seg.set, 0, hT, 'A_hT', part=1)
        for ti, (s, n) in enumerate(seg.tiles):
            modulate_tile(k, B, seg.x[s:s + n, :], n, 0, seg.set, 0, hT, 'A_hT', part=2)
            Z = zt[ti % 2]
            Zn = f'A_zt{ti % 2}'
            for mc in range(14):
                if mc == 6 and ti + 1 < len(seg.tiles):
                    s2, n2 = seg.tiles[ti + 1]
                    modulate_tile(k, B, seg.x[s2:s2 + n2, :], n2, 0, seg.set, 0, hT, 'A_hT', part=1)
                zpp = zp[mc % 4]
                for kc in range(8):
                    P.op('pe', lambda e, mc=mc, kc=kc, zpp=zpp: e.matmul(zpp[:, 0:n], lhsT=k.w_in0[:, kc, mc * 128:(mc + 1) * 128], rhs=hT[:, kc, 0:n],
                                                                         start=(kc == 0), stop=(kc == 7)),
                         [f'w_in0.{kc}', f'A_hT.{kc}'], [f'A_zp{mc % 4}'])
                eng = 'act' if mc % 2 == 0 else 'dve'
                if eng == 'act':
                    P.op('act', lambda e, mc=mc, zpp=zpp: e.copy(out=Z[:, mc, 0:n], in_=zpp[:, 0:n]), [f'A_zp{mc % 4}'], [f'{Zn}.{mc}'])
                else:
                    P.op('dve', lambda e, mc=mc, zpp=zpp: e.tensor_copy(out=Z[:, mc, 0:n], in_=zpp[:, 0:n]), [f'A_zp{mc % 4}'], [f'{Zn}.{mc}'])
            P.dma('sp', zv[:, :, PADZ + s:PADZ + s + n], Z[:, :, 0:n], keys(Zn, 14), [], Zn)


def lru_coeffs(k, C, d, n, xa, xab):
    P, lp = k.P, k.lp
    for c in range(6):
        pr, pi = C['pg'][(2 * c) % 4], C['pg'][(2 * c + 1) % 4]
        prn, pin = C['pgname'][(2 * c) % 4], C['pgname'][(2 * c + 1) % 4]
        P.op('pe', lambda e, c=c, pr=pr: e.matmul(pr[:, 0:n], lhsT=lp['lru_bdA'][:, d, c, :], rhs=xab[:, c, 0:n], start=True, stop=True),
             ['p_lru_bdA', f"{C['xabname']}.{c}"], [prn])
        P.op('pe', lambda e, c=c, pi=pi: e.matmul(pi[:, 0:n], lhsT=lp['lru_bdX'][:, d, c, :], rhs=xab[:, c, 0:n], start=True, stop=True),
             ['p_lru_bdX', f"{C['xabname']}.{c}"], [pin])
        P.op('act', lambda e, c=c, pr=pr: e.activation(out=C['r'][:, c, 0:n], in_=pr[:, 0:n], func=AF.Sigmoid, bias=lp['lru_ba'][:, d, c:c + 1], scale=1.0),
             [prn, 'p_lru_ba'], [f"{C['pfx']}r.{c}"])
        P.op('act', lambda e, c=c, pi=pi: e.activation(out=C['ig'][:, c, 0:n], in_=pi[:, 0:n], func=AF.Sigmoid, bias=lp['lru_bx'][:, d, c:c + 1], scale=1.0),
             [pin, 'p_lru_bx'], [f"{C['pfx']}ig.{c}"])
    for c in range(6):
        P.op('act', lambda e, c=c: e.activation(out=C['a'][:, c, 0:n], in_=C['r'][:, c, 0:n], func=AF.Exp, scale=lp['cl'][:, 0, d, c:c + 1]),
             [f"{C['pfx']}r.{c}", 'p_cl'], [f"{C['pfx']}a.{c}"])
        P.op('act', lambda e, c=c: e.activation(out=C['r'][:, c, 0:n], in_=C['r'][:, c, 0:n], func=AF.Exp, scale=lp['cl'][:, 1, d, c:c + 1]),
             [f"{C['pfx']}r.{c}", 'p_cl'], [f"{C['pfx']}r.{c}"])
    for c in range(6):
        P.op('act', lambda e, c=c: e.activation(out=C['r'][:, c, 0:n], in_=C['r'][:, c, 0:n], func=AF.Sqrt, bias=k.cst[:, 1:2], scale=-1.0),
             [f"{C['pfx']}r.{c}", 'cst'], [f"{C['pfx']}r.{c}"])
        P.op('dve', lambda e, c=c: e.tensor_tensor(out=C['ig'][:, c, 0:n], in0=C['ig'][:, c, 0:n], in1=C['r'][:, c, 0:n], op=ALU.mult),
             [f"{C['pfx']}r.{c}", f"{C['pfx']}ig.{c}"], [f"{C['pfx']}ig.{c}"])
        P.op('pool', lambda e, c=c: e.tensor_tensor(out=C['ig'][:, c, 0:n], in0=C['ig'][:, c, 0:n], in1=xa[:, c, 0:n], op=ALU.mult),
             [f"{C['pfx']}ig.{c}", f"{C['xaname']}.{c}"], [f"{C['pfx']}ig.{c}"])


def coeff_bufs(k, st, pfx, share=None):
    C = {'pfx': pfx}
    for nm in ('r', 'ig', 'a'):
        C[nm] = k.sb(pfx + nm, [128, 6, 512], F32, st)
    if share is None:
        C['pg'] = [k.ps(pfx + f'pg{i}', [128, 512], F32, st) for i in range(4)]
        C['pgname'] = [pfx + f'pg{i}' for i in range(4)]
    else:
        C['pg'], C['pgname'] = share['pg'], share['pgname']
    return C


def l0_phaseB(k, seg):
    P, lp = k.P, k.lp
    with ExitStack() as st:
        sb = lambda n, s, d: k.sb(n, s, d, st)
        sets = []
        for j in range(2):
            Bf = {}
            Bf['zin'] = sb(f'B{j}_zin', [128, 6, 516], F32)
            Bf['xa'] = sb(f'B{j}_xa', [128, 6, 512], F32)
            Bf['xab'] = sb(f'B{j}_xab', [128, 6, 512], BF16)
            Bf['hf'] = sb('B_hf', [128, 6, 512], F32) if j == 0 else sets[0]['hf']
            C = coeff_bufs(k, st, f'B{j}_', share=(sets[0]['C'] if j == 1 else None))
            C['xabname'], C['xaname'] = f'B{j}_xab', f'B{j}_xa'
            Bf['C'] = C
            sets.append(Bf)
        zv = seg.z[0:768, :].rearrange("(c p) t -> p c t", p=128)
        xav = seg.xa.rearrange("(c p) t -> p c t", p=128)
        hfv = seg.hf.rearrange("(c p) t -> p c t", p=128)
        if seg.nm == 'c':
            P.op('dve', lambda e: e.memset(lp['state'][:], 0.0), [], keys('p_state0', 6) + keys('p_state1', 6))
        for ti, (s, n) in enumerate(seg.tiles):
            j = ti % 2
            Bf = sets[j]
            zin, xa, xab, hf, C = Bf['zin'], Bf['xa'], Bf['xab'], Bf['hf'], Bf['C']
            pf = f'B{j}_'
            P.dma('sp', zin[:, :, 0:n + 4], zv[:, :, PADZ + s - 2:PADZ + s + n + 2], [], keys(pf + 'zin', 6), pf + 'zin')
            for c in range(6):
                P.op('act', lambda e, c=c, xa=xa, zin=zin: e.activation(out=xa[:, c, 0:n], in_=zin[:, c, 0:n], func=AF.Identity,
                                                                        scale=lp['lru_cw'][:, c, 0:1], bias=lp['lru_cb'][:, c:c + 1]),
                     [f'{pf}zin.{c}', 'p_lru_cw', 'p_lru_cb'], [f'{pf}xa.{c}'])
                for t in range(1, 5):
                    P.op('dve', lambda e, c=c, t=t, xa=xa, zin=zin: e.scalar_tensor_tensor(out=xa[:, c, 0:n], in0=zin[:, c, t:t + n], scalar=lp['lru_cw'][:, c, t:t + 1],
                                                                                           in1=xa[:, c, 0:n], op0=ALU.mult, op1=ALU.add),
                         [f'{pf}zin.{c}', f'{pf}xa.{c}'], [f'{pf}xa.{c}'])
                P.op('act', lambda e, c=c, xa=xa, xab=xab: e.copy(out=xab[:, c, 0:n], in_=xa[:, c, 0:n]), [f'{pf}xa.{c}'], [f'{pf}xab.{c}'])
            P.dma('sp', xav[:, :, s:s + n], xa[:, :, 0:n], keys(pf + 'xa', 6), [], pf + 'xa')
            lru_coeffs(k, C, 0, n, xa, xab)
            for c in range(6):
                P.op('dve', lambda e, c=c, hf=hf, C=C: e.tensor_tensor_scan(out=hf[:, c, 0:n], data0=C['a'][:, c, 0:n], data1=C['ig'][:, c, 0:n],
                                                                          initial=lp['state'][:, 0, c:c + 1], op0=ALU.mult, op1=ALU.add),
                     [f'{pf}a.{c}', f'{pf}ig.{c}', f'p_state0.{c}'], [f'B_hf.{c}'])
                P.op('dve', lambda e, c=c, hf=hf: e.tensor_copy(out=lp['state'][:, 0, c:c + 1], in_=hf[:, c, n - 1:n]), [f'B_hf.{c}'], [f'p_state0.{c}'])
            P.dma('sp', hfv[:, :, s:s + n], hf[:, :, 0:n], keys('B_hf', 6), [], 'B_hf')


def l0_phaseC(k, seg):
    P, lp, I = k.P, k.lp, k.I
    with ExitStack() as st:
        sb = lambda n, s, d: k.sb(n, s, d, st)
        ps = lambda n, s, d: k.ps(n, s, d, st)
        xa = sb('C_xa', [128, 6, 512], F32)
        xab = sb('C_xab', [128, 6, 512], BF16)
        hb = sb('C_hb', [128, 6, 512], F32)
        hf = sb('C_hf', [128, 6, 512], F32)
        ga = sb('C_ga', [128, 6, 512], F32)
        yT = sb('C_yT', [128, 8, 512], BF16)
        zb = sb('C_zb', [128, 2, 528], F32)
        p2 = sb('C_p2', [128, 528], F32)
        p4 = sb('C_p4', [128, 528], F32)
        p8 = sb('C_p8', [128, 528], F32)
        Qw = sb('C_Qw', [128, 516], F32)
        Ssum = sb('C_S', [128, 512], F32)
        dd = sb('C_dd', [128, 2, 512], BF16)
        xt = sb('C_xt', [128, 4, D], F32)
        xo = sb('C_xo', [128, 4, D], F32)
        C = coeff_bufs(k, st, 'C_')
        C['xabname'], C['xaname'] = 'C_xab', 'C_xa'
        po = [ps(f'C_po{i}', [128, 512], F32) for i in range(4)]
        zg = seg.z[768:1536, :].rearrange("(c p) t -> p c t", p=128)
        zbv = seg.z[1536:1792, :].rearrange("(c p) t -> p c t", p=128)
        xav = seg.xa.rearrange("(c p) t -> p c t", p=128)
        hfv = seg.hf.rearrange("(c p) t -> p c t", p=128)
        if seg.nm == 'c':
            P.op('dve', lambda e: e.memset(lp['state'][:, 1, :], 0.0), [], keys('p_state1', 6))
        nt_ = len(seg.tiles)
        for ti in range(nt_ - 1, -1, -1):
            s, n = seg.tiles[ti]
            ng = n // 128
            P.dma('sp', xa[:, :, 0:n], xav[:, :, s:s + n], [], keys('C_xa', 6), 'C_xa')
            P.dma('sp', hf[:, :, 0:n], hfv[:, :, s:s + n], [], keys('C_hf', 6), 'C_hf')
            P.dma('sp', ga[:, :, 0:n], zg[:, :, PADZ + s:PADZ + s + n], [], keys('C_ga', 6), 'C_ga')
            P.dma('sp', zb[:, :, 0:n + 16], zbv[:, :, PADZ + s - 8:PADZ + s + n + 8], [], keys('C_zb', 2), 'C_zb')
            P.dma('sp', xt[:, 0:ng, :], seg.x[s:s + n, :].rearrange("(g p) f -> p g f", p=128), [], ['C_xt'], 'C_xt')
            W = n + 16
            n1 = n + 1
            for ch in range(2):
                zc = zb[:, ch, :]
                P.op('dve', lambda e, zc=zc: e.tensor_tensor(out=p2[:, 0:W - 1], in0=zc[:, 0:W - 1], in1=zc[:, 1:W], op=ALU.add),
                     [f'C_zb.{ch}'], ['C_p2'])
                if ch == 0:
                    P.op('dve', lambda e: e.tensor_copy(out=Qw[0:64, 0:n1], in_=p2[0:64, 7:7 + n1]), ['C_p2'], ['C_Qw'])
                    P.op('dve', lambda e: e.tensor_tensor(out=Qw[64:128, 0:n1], in0=p2[64:128, 6:6 + n1], in1=p2[64:128, 8:8 + n1], op=ALU.add),
                         ['C_p2'], ['C_Qw'])
                else:
                    P.op('dve', lambda e: e.tensor_tensor(out=p4[:, 0:W - 3], in0=p2[:, 0:W - 3], in1=p2[:, 2:W - 1], op=ALU.add), ['C_p2'], ['C_p4'])
                    P.op('dve', lambda e: e.tensor_tensor(out=Qw[0:64, 0:n1], in0=p4[0:64, 4:4 + n1], in1=p4[0:64, 8:8 + n1], op=ALU.add),
                         ['C_p4'], ['C_Qw'])
                    P.op('dve', lambda e: e.tensor_tensor(out=p8[64:128, 0:W - 7], in0=p4[64:128, 0:W - 7], in1=p4[64:128, 4:W - 3], op=ALU.add),
                         ['C_p4'], ['C_p8'])
                    P.op('dve', lambda e: e.tensor_tensor(out=Qw[64:128, 0:n1], in0=p8[64:128, 0:n1], in1=p8[64:128, 8:8 + n1], op=ALU.add),
                         ['C_p8'], ['C_Qw'])
                P.op('dve', lambda e: e.tensor_scalar(out=Ssum[:, 0:n], in0=Qw[:, 0:n], scalar1=lp['pool_fl'][:, 0:1], scalar2=None, op0=ALU.mult),
                     ['C_Qw', 'p_pool_fl'], ['C_S'])
                P.op('dve', lambda e: e.scalar_tensor_tensor(out=Ssum[:, 0:n], in0=Qw[:, 1:n1], scalar=lp['pool_fl'][:, 1:2], in1=Ssum[:, 0:n],
                                                             op0=ALU.mult, op1=ALU.add), ['C_Qw', 'C_S', 'p_pool_fl'], ['C_S'])
                if ti == 0:
                    P.op('dve', lambda e, ch=ch: e.tensor_tensor(out=Ssum[:, 0:16], in0=Ssum[:, 0:16], in1=lp['pool_corr'][:, ch, 0, :], op=ALU.mult),
                         ['C_S', 'p_pool_corr'], ['C_S'])
                if ti == nt_ - 1:
                    P.op('dve', lambda e, ch=ch: e.tensor_tensor(out=Ssum[:, n - 16:n], in0=Ssum[:, n - 16:n], in1=lp['pool_corr'][:, ch, 1, :], op=ALU.mult),
                         ['C_S', 'p_pool_corr'], ['C_S'])
                P.op('dve', lambda e, ch=ch, zc=zc: e.scalar_tensor_tensor(out=dd[:, ch, 0:n], in0=Ssum[:, 0:n], scalar=lp['pool_invw'][:, ch:ch + 1],
                                                                          in1=zc[:, 8:8 + n], op0=ALU.mult, op1=ALU.subtract),
                     ['C_S', f'C_zb.{ch}', 'p_pool_invw'], [f'C_dd.{ch}'])
                pp = po[ch]
                P.op('pe', lambda e, ch=ch, pp=pp: e.matmul(pp[:, 0:n], lhsT=lp['pool_bd'][:, ch, :], rhs=dd[:, ch, 0:n], start=True, stop=True),
                     ['p_pool_bd', f'C_dd.{ch}'], [f'C_po{ch}'])
                P.op('act', lambda e, ch=ch, pp=pp: e.activation(out=yT[:, 6 + ch, 0:n], in_=pp[:, 0:n], func=AF.Identity, scale=lp['pool_scale'][:, ch:ch + 1], bias=k.cst[:, 2:3]),
                     [f'C_po{ch}', 'p_pool_scale', 'cst'], [f'C_yT.{6 + ch}'])
            for c in range(6):
                P.op('act', lambda e, c=c: e.copy(out=xab[:, c, 0:n], in_=xa[:, c, 0:n]), [f'C_xa.{c}'], [f'C_xab.{c}'])
            lru_coeffs(k, C, 1, n, xa, xab)
            for c in range(6):
                P.op('dve', lambda e, c=c: e.tensor_tensor_scan(out=rev(hb[:, c, 0:n]), data0=rev(C['a'][:, c, 0:n]), data1=rev(C['ig'][:, c, 0:n]),
                                                                initial=lp['state'][:, 1, c:c + 1], op0=ALU.mult, op1=ALU.add),
                     [f'C_a.{c}', f'C_ig.{c}', f'p_state1.{c}'], [f'C_hb.{c}'])
                P.op('dve', lambda e, c=c: e.tensor_copy(out=lp['state'][:, 1, c:c + 1], in_=hb[:, c, 0:1]), [f'C_hb.{c}'], [f'p_state1.{c}'])
                P.op('act', lambda e, c=c: e.activation(out=ga[:, c, 0:n], in_=ga[:, c, 0:n], func=AF.Gelu_apprx_tanh), [f'C_ga.{c}'], [f'C_ga.{c}'])
                P.op('pool', lambda e, c=c: e.tensor_tensor(out=hb[:, c, 0:n], in0=hb[:, c, 0:n], in1=hf[:, c, 0:n], op=ALU.add),
                     [f'C_hb.{c}', f'C_hf.{c}'], [f'C_hb.{c}'])
                P.op('dve', lambda e, c=c: e.tensor_tensor(out=yT[:, c, 0:n], in0=hb[:, c, 0:n], in1=ga[:, c, 0:n], op=ALU.mult),
                     [f'C_hb.{c}', f'C_ga.{c}'], [f'C_yT.{c}'])
            for g in range(ng):
                for nt in range(2):
                    pp = po[(2 * g + nt) % 4]
                    ppn = f'C_po{(2 * g + nt) % 4}'
                    for kc in range(8):
                        P.op('pe', lambda e, g=g, nt=nt, kc=kc, pp=pp: e.matmul(pp[:, :], lhsT=yT[:, kc, g * 128:(g + 1) * 128], rhs=k.w_out0[:, kc, nt * 512:(nt + 1) * 512],
                                                                               start=(kc == 0), stop=(kc == 7)),
                             [f'C_yT.{kc}', f'w_out0.{kc}'], [ppn])
                    P.op('dve', lambda e, g=g, nt=nt, pp=pp: e.tensor_tensor(out=xo[:, g, nt * 512:(nt + 1) * 512], in0=pp[:, :], in1=xt[:, g, nt * 512:(nt + 1) * 512], op=ALU.add),
                         [ppn, 'C_xt'], [f'C_xo.{g}'])
            P.dma('sp', seg.x05[1 + s:1 + s + n, :].rearrange("(g p) f -> p g f", p=128), xo[:, 0:ng, :], keys('C_xo', 4)[0:ng], [], 'C_xo')


def ffn_phase(k, l, s_, src, dst, TT, final, flush=True, out_rows=None):
    nc, P, I = k.nc, k.P, k.I
    tiles = tiles_of(TT)
    with ExitStack() as st:
        sb = lambda n, s, d: k.sb(n, s, d, st)
        ps = lambda n, s, d: k.ps(n, s, d, st)
        w_up = sb('F_wup', [128, 8, 2 * DFF], BF16)
        load_w_bf16(k, 'F_wup', w_up, I['ffn_w_up'][l], 8, 2 * DFF)
        w_dn = sb('F_wdn', [128, NFC, D], BF16)
        with ExitStack() as st2:
            fold_w_bf16(k, st2, 'F_wdn', w_dn, I['ffn_w_down'][l], NFC, k.gbc_d[l, s_, 1])
            P.barrier()
        cw = sb('F_cw', [128, 2 * NFC, 3], F32)
        cb = sb('F_cb', [128, 2 * NFC], F32)
        P.dma('sp', cw[:], I['ffn_cw'][:, l, :, :], [], ['F_cw'], 'F_cw')
        P.dma('sp', cb[:], I['ffn_cb'][:, l, :], [], ['F_cb'], 'F_cb')
        B = mod_bufs(k, st, 'F_')
        hT = sb('F_hT', [128, 8, 512], BF16)
        gT = sb('F_gT', [128, NFC, 512], BF16)
        prevu = [sb(f'F_prevu{i}', [128, 2 * NFC, 2], F32) for i in range(2)]
        P.op('dve', lambda e: e.memset(prevu[0][:], 0.0), [], keys('F_prevu0', 2 * NFC))
        acc = [sb(f'F_acc{i}', [128, 512], F32) for i in range(4)]
        corr = sb('F_corr', [128, 2 * NFC, 2], F32)
        ctmp = sb('F_ctmp', [128, 2 * NFC], F32)
        xs = sb('F_xs', [128, 2, D], F32)
        xo = xs
        pu = [ps(f'F_pu{i}', [128, 512], F32) for i in range(4)]
        pd = [ps(f'F_pd{i}', [128, 512], F32) for i in range(2)]
        if final:
            fg = sb('F_fg', [128, D], F32)
            P.dma('sp', fg[:], I['final_g_bc'], [], ['F_fg'], 'F_fg')
            fss = sb('F_fss', [128, 12], F32)
            fjunk = B['junk']

        if os.environ.get('DBG_SBUF'):
            print('FFN sbuf remaining', nc.sbuf_bytes_remaining, 'final', final)
        def conv_gate(n, zero_u, ti):
            pin, pout = prevu[ti % 2], prevu[(ti + 1) % 2]
            pinn, poutn = f'F_prevu{ti % 2}', f'F_prevu{(ti + 1) % 2}'
            allin = keys(pinn, 2 * NFC)
            P.op('dve', lambda e: e.tensor_tensor(out=corr[:, :, 0], in0=cw[:, :, 0], in1=pin[:, :, 0], op=ALU.mult), allin + ['F_cw'], ['F_corr'])
            P.op('dve', lambda e: e.tensor_tensor(out=ctmp[:, :], in0=cw[:, :, 1], in1=pin[:, :, 1], op=ALU.mult), allin + ['F_cw'], ['F_ctmp'])
            P.op('dve', lambda e: e.tensor_tensor(out=corr[:, :, 0], in0=corr[:, :, 0], in1=ctmp[:, :], op=ALU.add), ['F_corr', 'F_ctmp'], ['F_corr'])
            P.op('dve', lambda e: e.tensor_tensor(out=corr[:, :, 1], in0=cw[:, :, 0], in1=pin[:, :, 1], op=ALU.mult), allin + ['F_cw', 'F_corr'], ['F_corr'])
            for c in range(NFC):
                q = c % 2
                AA = [acc[2 * q], acc[2 * q + 1]]
                AN = [f'F_acc{2 * q}', f'F_acc{2 * q + 1}']
                PP = [pu[2 * q], pu[2 * q + 1]]
                PN = [f'F_pu{2 * q}', f'F_pu{2 * q + 1}']
                CC = [c, NFC + c]
                if not zero_u:
                    for vi in range(2):
                        for kc in range(8):
                            P.op('pe', lambda e, kc=kc, cc=CC[vi], pp=PP[vi]: e.matmul(pp[:, 0:n], lhsT=w_up[:, kc, cc * 128:(cc + 1) * 128], rhs=hT[:, kc, 0:n],
                                                                                      start=(kc == 0), stop=(kc == 7)),
                                 [f'F_wup.{kc}', f'F_hT.{kc}'], [PN[vi]])
                    for vi in range(2):
                        P.op('act', lambda e, A_=AA[vi], pp=PP[vi], cc=CC[vi]: e.activation(out=A_[:, 0:n], in_=pp[:, 0:n], func=AF.Identity,
                                                                                          scale=cw[:, cc, 2:3], bias=cb[:, cc:cc + 1]),
                             [PN[vi], 'F_cw', 'F_cb'], [AN[vi]])
                    for vi in range(2):
                        if os.environ.get('DBG_SKIP_SAVE'):
                            continue
                        P.op('dve', lambda e, pp=PP[vi], cc=CC[vi]: e.tensor_copy(out=pout[:, cc, :], in_=pp[:, n - 2:n]), [PN[vi]], [f'{poutn}.{CC[vi]}'])
                    for vi in range(2):
                        P.op('dve', lambda e, A_=AA[vi], pp=PP[vi], cc=CC[vi]: e.scalar_tensor_tensor(out=A_[:, 1:n], in0=pp[:, 0:n - 1], scalar=cw[:, cc, 1:2], in1=A_[:, 1:n],
                                                                                                    op0=ALU.mult, op1=ALU.add), [PN[vi], AN[vi]], [AN[vi]])
                    for vi in range(2):
                        P.op('dve', lambda e, A_=AA[vi], pp=PP[vi], cc=CC[vi]: e.scalar_tensor_tensor(out=A_[:, 2:n], in0=pp[:, 0:n - 2], scalar=cw[:, cc, 0:1], in1=A_[:, 2:n],
                                                                                                    op0=ALU.mult, op1=ALU.add), [PN[vi], AN[vi]], [AN[vi]])
                else:
                    for vi in range(2):
                        P.op('act', lambda e, A_=AA[vi], cc=CC[vi]: e.activation(out=A_[:, 0:n], in_=k.zero[:, 0:n], func=AF.Identity,
                                                                                scale=cw[:, cc, 2:3], bias=cb[:, cc:cc + 1]),
                             ['zero', 'F_cw', 'F_cb'], [AN[vi]])
                for vi in range(2):
                    P.op('pool', lambda e, A_=AA[vi], cc=CC[vi]: e.tensor_tensor(out=A_[:, 0:2], in0=A_[:, 0:2], in1=corr[:, cc, :], op=ALU.add),
                         ['F_corr', AN[vi]], [AN[vi]])
                P.op('act', lambda e, A_=AA[1]: e.activation(out=A_[:, 0:n], in_=A_[:, 0:n], func=AF.Silu), [AN[1]], [AN[1]])
                P.op('pool', lambda e, c=c, A0=AA[0], A1=AA[1]: e.tensor_tensor(out=gT[:, c, 0:n], in0=A0[:, 0:n], in1=A1[:, 0:n], op=ALU.mult),
                     [AN[0], AN[1]], [f'F_gT.{c}'])

        def down_res(tok0, n, nrows_last=128):
            ng = n // 128
            nr = lambda g: (nrows_last if g == ng - 1 else 128)
            for g_ in range(ng):
                g = g_ % 2
                r0 = 1 + tok0 + g_ * 128
                P.dma('sp', xs[0:nr(g_), g, :], src[r0:r0 + nr(g_), :], [], [f'F_xs.{g}'], f'F_xs{g}')
                for nt in range(2):
                    pp = pd[nt]
                    for kc in range(NFC):
                        P.op('pe', lambda e, g_=g_, nt=nt, kc=kc, pp=pp: e.matmul(pp[:, :], lhsT=gT[:, kc, g_ * 128:(g_ + 1) * 128], rhs=w_dn[:, kc, nt * 512:(nt + 1) * 512],
                                                                               start=(kc == 0), stop=(kc == NFC - 1)),
                             [f'F_gT.{kc}', f'F_wdn.{kc}'], [f'F_pd{nt}'])
                    P.op('dve', lambda e, g=g, g_=g_, nt=nt, pp=pp: e.tensor_tensor(out=xo[0:nr(g_), g, nt * 512:(nt + 1) * 512], in0=pp[0:nr(g_), :],
                                                                            in1=xs[0:nr(g_), g, nt * 512:(nt + 1) * 512], op=ALU.add),
                         [f'F_pd{nt}', f'F_xs.{g}'], [f'F_xs.{g}'])
                if final:
                    P.op('dve', lambda e: e.memset(fss[:, 0:1], 0.0), [], ['F_fss'])
                    P.op('act', lambda e, g=g: e.activation(out=fjunk[:], in_=xo[:, g, :], func=AF.Square, accum_out=fss[:, 0:1]),
                         [f'F_xs.{g}'], ['F_fss', 'F_junk'])
                    P.op('act', lambda e: e.activation(out=fss[:, 1:2], in_=fss[:, 0:1], func=AF.Sqrt, bias=k.cst[:, 0:1], scale=1.0 / D),
                         ['F_fss', 'cst'], ['F_fss'])
                    P.op('dve', lambda e: e.reciprocal(out=fss[:, 2:3], in_=fss[:, 1:2]), ['F_fss'], ['F_fss'])
                    P.op('dve', lambda e, g=g: e.scalar_tensor_tensor(out=xo[:, g, :], in0=xo[:, g, :], scalar=fss[:, 2:3], in1=fg[:],
                                                                     op0=ALU.mult, op1=ALU.mult), [f'F_xs.{g}', 'F_fss', 'F_fg'], [f'F_xs.{g}'])
                t0 = tok0 + g_ * 128
                if final:
                    lo = max(t0, 0)
                    hi = min(t0 + nr(g_), TT if out_rows is None else out_rows)
                    if hi > lo:
                        P.dma('sp', k.out[lo:hi, :], xo[lo - t0:hi - t0, g, :], [f'F_xs.{g}'], [], f'F_xs{g}')
                else:
                    P.dma('sp', dst[1 + t0:1 + t0 + nr(g_), :], xo[0:nr(g_), g, :], [f'F_xs.{g}'], [], f'F_xs{g}')

        modulate_tile(k, B, src[1 + tiles[0][0]:1 + tiles[0][0] + tiles[0][1], :], tiles[0][1], l, s_, 2, hT, 'F_hT', part=1)
        for ti, (s, n) in enumerate(tiles):
            modulate_tile(k, B, src[1 + s:1 + s + n, :], n, l, s_, 2, hT, 'F_hT', part=2)
            conv_gate(n, False, ti)
            if ti + 1 < len(tiles):
                s2, n2 = tiles[ti + 1]
                modulate_tile(k, B, src[1 + s2:1 + s2 + n2, :], n2, l, s_, 2, hT, 'F_hT', part=1)
            down_res(s - 1, n)
        if flush:
            conv_gate(128, True, len(tiles))
            down_res(TT - 1, 128, nrows_last=1)


def layer1(k, stop_after):
    nc, P, I, S = k.nc, k.P, k.I, k.S
    T = k.T
    TK = T + LC
    NKB = TK // 128
    TL = k.TL
    yc_d = nc.dram_tensor('yc_d', [768, T], BF16, kind="Internal").ap()
    with ExitStack() as st:
      if not os.environ.get('DBG_ONLY_F1'):
          sb = lambda n, s, d: k.sb(n, s, d, st)
          ps = lambda n, s, d: k.ps(n, s, d, st)
          w_in = sb('w_in1', [128, 8, 2816], BF16)
          load_w_bf16(k, 'w_in1', w_in, I['cd_w_in'], 8, 2816, colblk=1408)
          w_sw = sb('w_sw1', [128, 8, 1536], BF16)
          load_w_bf16(k, 'w_sw1', w_sw, I['cd_w_sw'], 8, 1536, colblk=1536)
          B = mod_bufs(k, st, 'E_')
          hT = sb('E_hT', [128, 8, 512], BF16)
          qk = sb('E_qk', [128, 12, 512], BF16)
          vt = sb('E_vt', [128, 4, 768], BF16)
          gl = sb('E_gl', [128, 2, 512], F32)
          rc = sb('E_rc', [128, 512], F32)
          rs = sb('E_rs', [128, 512], F32)
          t1 = sb('E_t1', [128, 512], F32)
          t2 = sb('E_t2', [128, 512], F32)
          sg = sb('E_sg', [128, 512], F32)
          pa = [ps(f'E_pa{i}', [128, 512], F32) for i in range(2)]
          pb = [ps(f'E_pb{i}', [128, 512], F32) for i in range(2)]
          glv = S['gl'].rearrange("(c p) t -> p c t", p=128)
          P.dma('sp', glv[:, :, 0:PADZ], k.zero[:, 0:2 * PADZ].rearrange("p (c t) -> p c t", c=2), ['zero'], [], 'E_glpad')
          P.dma('sp', glv[:, :, PADZ + T:PADZ + T + PADZ], k.zero[:, 0:2 * PADZ].rearrange("p (c t) -> p c t", c=2), ['zero'], [], 'E_glpad')
          qTv = S['qT'].rearrange("(c p) t -> p c t", p=128)
          kTv = S['kT'].rearrange("(c p) t -> p c t", p=128)

          def proj_plain(cols0, nch, dst, dstname, d0, n):
              for c in range(nch):
                  pp = pa[c % 2]
                  for kc in range(8):
                      P.op('pe', lambda e, c=c, kc=kc, pp=pp: e.matmul(pp[:, 0:n], lhsT=w_in[:, kc, cols0 + c * 128:cols0 + (c + 1) * 128], rhs=hT[:, kc, 0:n],
                                                                      start=(kc == 0), stop=(kc == 7)), [f'w_in1.{kc}', f'E_hT.{kc}'], [f'E_pa{c % 2}'])
                  P.op('act', lambda e, c=c, pp=pp: e.copy(out=dst[:, d0 + c, 0:n], in_=pp[:, 0:n]), [f'E_pa{c % 2}'], [f'{dstname}.{d0 + c}'])

          def proj_v(n):
              ng = n // 128
              for g in range(ng):
                  for (c0, cn, pp, ppn) in ((0, 512, pa[g % 2], f'E_pa{g % 2}'), (512, 256, pb[g % 2], f'E_pb{g % 2}')):
                      for kc in range(8):
                          P.op('pe', lambda e, g=g, kc=kc, pp=pp, c0=c0, cn=cn: e.matmul(pp[:, 0:cn], lhsT=hT[:, kc, g * 128:(g + 1) * 128],
                                                                                         rhs=w_in[:, kc, 1536 + c0:1536 + c0 + cn], start=(kc == 0), stop=(kc == 7)),
                               [f'w_in1.{kc}', f'E_hT.{kc}'], [ppn])
                      P.op('dve', lambda e, g=g, pp=pp, c0=c0, cn=cn: e.tensor_copy(out=vt[:, g, c0:c0 + cn], in_=pp[:, 0:cn]), [ppn], [f'E_vt.{g}'])

          n = LC
          modulate_tile(k, B, S['x1_c'][1:1 + LC, :], n, 1, 1, 0, hT, 'E_hT')
          proj_plain(768, 6, qk, 'E_qk', 6, n)
          P.dma('sp', kTv[:, :, T:T + n], qk[:, 6:12, 0:n], keys('E_qk', 12)[6:12], [], 'E_qk')
          proj_v(n)
          P.dma('sp', S['v'][T:T + n, :].rearrange("(g p) f -> p g f", p=128), vt[:, 0:n // 128, :], keys('E_vt', 4), [], 'E_vt')
          etiles = tiles_of(T)
          modulate_tile(k, B, S['x1_l'][1 + etiles[0][0]:1 + etiles[0][0] + etiles[0][1], :], etiles[0][1], 1, 0, 0, hT, 'E_hT', part=1)
          for ti, (s, n) in enumerate(etiles):
              modulate_tile(k, B, S['x1_l'][1 + s:1 + s + n, :], n, 1, 0, 0, hT, 'E_hT', part=2)
              P.dma('sp', rc[:, 0:n], I['rope_c'][:, s:s + n], [], ['E_rc'], 'E_rc')
              P.dma('sp', rs[:, 0:n], I['rope_s'][:, s:s + n], [], ['E_rs'], 'E_rs')
              need_q = s < TL + 16
              for c in (range(12) if need_q else range(6, 12)):
                  pp, pq = pa[c % 2], pb[c % 2]
                  for kc in range(8):
                      P.op('pe', lambda e, c=c, kc=kc, pp=pp: e.matmul(pp[:, 0:n], lhsT=w_in[:, kc, c * 128:(c + 1) * 128], rhs=hT[:, kc, 0:n],
                                                                      start=(kc == 0), stop=(kc == 7)), [f'w_in1.{kc}', f'E_hT.{kc}'], [f'E_pa{c % 2}'])
                  for kc in range(8):
                      P.op('pe', lambda e, c=c, kc=kc, pq=pq: e.matmul(pq[:, 0:n], lhsT=w_sw[:, kc, c * 128:(c + 1) * 128], rhs=hT[:, kc, 0:n],
                                                                      start=(kc == 0), stop=(kc == 7)), [f'w_sw1.{kc}', f'E_hT.{kc}'], [f'E_pb{c % 2}'])
                  P.op('dve', lambda e, pp=pp: e.tensor_tensor(out=t1[:, 0:n], in0=pp[:, 0:n], in1=rc[:, 0:n], op=ALU.mult), [f'E_pa{c % 2}', 'E_rc'], ['E_t1'])
                  P.op('dve', lambda e, pq=pq: e.tensor_tensor(out=t2[:, 0:n], in0=pq[:, 0:n], in1=rs[:, 0:n], op=ALU.mult), [f'E_pb{c % 2}', 'E_rs'], ['E_t2'])
                  P.op('pool', lambda e, c=c: e.tensor_tensor(out=qk[:, c, 0:n], in0=t1[:, 0:n], in1=t2[:, 0:n], op=ALU.add), ['E_t1', 'E_t2'], [f'E_qk.{c}'])
              if ti + 1 < len(etiles):
                  s2, n2 = etiles[ti + 1]
                  modulate_tile(k, B, S['x1_l'][1 + s2:1 + s2 + n2, :], n2, 1, 0, 0, hT, 'E_hT', part=1)
              if need_q:
                  P.dma('sp', qTv[:, :, s:s + n], qk[:, 0:6, 0:n], keys('E_qk', 12)[0:6], [], 'E_q')
              P.dma('sp', kTv[:, :, s:s + n], qk[:, 6:12, 0:n], keys('E_qk', 12)[6:12], [], 'E_qk')
              proj_v(n)
              P.dma('sp', S['v'][s:s + n, :].rearrange("(g p) f -> p g f", p=128), vt[:, 0:n // 128, :], keys('E_vt', 4), [], 'E_vt')
              for c in (range(2) if need_q else []):
                  pp, pq = pa[c % 2], pb[c % 2]
                  for (pz, pzn, cols) in ((pp, f'E_pa{c % 2}', 2304 + c * 128), (pq, f'E_pb{c % 2}', 2304 + 256 + c * 128)):
                      for kc in range(8):
                          P.op('pe', lambda e, kc=kc, pz=pz, cols=cols: e.matmul(pz[:, 0:n], lhsT=w_in[:, kc, cols:cols + 128], rhs=hT[:, kc, 0:n],
                                                                                start=(kc == 0), stop=(kc == 7)), [f'w_in1.{kc}', f'E_hT.{kc}'], [pzn])
                  P.op('act', lambda e, pq=pq: e.activation(out=sg[:, 0:n], in_=pq[:, 0:n], func=AF.Sigmoid), [f'E_pb{c % 2}'], ['E_sg'])
                  P.op('dve', lambda e, c=c, pp=pp: e.tensor_tensor(out=gl[:, c, 0:n], in0=pp[:, 0:n], in1=sg[:, 0:n], op=ALU.mult), [f'E_pa{c % 2}', 'E_sg'], [f'E_gl.{c}'])
              if need_q:
                  P.dma('sp', glv[:, :, PADZ + s:PADZ + s + n], gl[:, :, 0:n], keys('E_gl', 2), [], 'E_gl')
    P.barrier()
    if stop_after == 'l1E':
        return

    with ExitStack() as st:
        sb = lambda n, s, d: k.sb(n, s, d, st)
        ps = lambda n, s, d: k.ps(n, s, d, st)
        dl = sb('G_dl', [1, 4, 64], F32)
        P.dma('sp', dl[:], I['diff_l'], [], ['G_dl'], 'G_dl')
        sm = sb('G_sm', [1, 8], F32)
        pr_ = sb('G_pr', [1, 2, 64], F32)
        P.op('dve', lambda e: e.tensor_tensor(out=pr_[:, 0, :], in0=dl[:, 0, :], in1=dl[:, 1, :], op=ALU.mult), ['G_dl'], ['G_pr'])
        P.op('dve', lambda e: e.tensor_tensor(out=pr_[:, 1, :], in0=dl[:, 2, :], in1=dl[:, 3, :], op=ALU.mult), ['G_dl'], ['G_pr'])
        P.op('dve', lambda e: e.reduce_sum(out=sm[:, 0:1], in_=pr_[:, 0, :], axis=AX.X), ['G_pr'], ['G_sm'])
        P.op('dve', lambda e: e.reduce_sum(out=sm[:, 1:2], in_=pr_[:, 1, :], axis=AX.X), ['G_pr'], ['G_sm'])
        P.op('act', lambda e: e.activation(out=sm[:, 2:4], in_=sm[:, 0:2], func=AF.Exp), ['G_sm'], ['G_sm'])
        P.op('dve', lambda e: e.tensor_tensor(out=sm[:, 4:5], in0=sm[:, 3:4], in1=sm[:, 2:3], op=ALU.subtract), ['G_sm'], ['G_sm'])
        P.op('dve', lambda e: e.tensor_scalar(out=sm[:, 5:6], in0=sm[:, 4:5], scalar1=-LAMBDA_INIT1, scalar2=None, op0=ALU.add), ['G_sm'], ['G_sm'])
        neglam = sb('G_neglam', [128, 2], F32)
        gsub = sb('G_gsub', [128, 2], F32)
        P.dma('sp', gsub[:, 0:1], I['subln_g'], [], ['G_gsub'], 'G_gsub')
        P.op('dve', lambda e: e.tensor_scalar(out=gsub[:, 1:2], in0=gsub[:, 0:1], scalar1=(1.0 - LAMBDA_INIT1), scalar2=None, op0=ALU.mult), ['G_gsub'], ['G_gsub'])
        pS = [ps(f'G_pS{i}', [128, 2, 512], F32) for i in range(2)]
        po = [ps(f'G_po{i}', [128, 512], F32) for i in range(2)]
        pl = ps('G_pl', [128, 2, 512], F32)
        P.op('pe', lambda e: e.matmul(pl[:, 0, 0:1], lhsT=k.ones_f[0:1, :], rhs=sm[0:1, 5:6], start=True, stop=True), ['ones_f', 'G_sm'], ['G_pl0', 'G_pl1'])
        P.op('dve', lambda e: e.tensor_copy(out=neglam[:, 0:1], in_=pl[:, 0, 0:1]), ['G_pl0', 'G_pl1'], ['G_neglam'])
        kh = [sb(f'G_kh{j}', [128, TK], BF16) for j in range(2)]
        vh = [sb(f'G_vh{j}', [128, NKB, 128], BF16) for j in range(2)]
        qh = [sb(f'G_qh{j}', [128, 512], BF16) for j in range(2)]
        pT = [sb(f'G_pT{i}', [128, 2, 512], BF16) for i in range(4)]
        accs = [sb(f'G_acc{i}', [128, 2, 512], F32) for i in range(2)]
        rl = sb('G_rl', [128, 2, 512], F32)
        o1 = sb('G_o1', [128, 512], F32)
        o2 = sb('G_o2', [128, 512], F32)
        sq = sb('G_sq', [128, 512], F32)
        ych = sb('G_ych', [128, 512], BF16)
        vv = S['v'].rearrange("(kb p) f -> p kb f", p=128)
        qtiles = tiles_of(TL)

        def load_head(h):
            j = h % 2
            P.dma('sp', kh[j][:, :], S['kT'][h * 128:h * 128 + 128, :], [], [f'G_kh{j}'], f'G_kh{j}')
            for b0 in range(0, NKB, 16):
                b1 = min(NKB, b0 + 16)
                P.dma('sp', vh[j][:, b0:b1, :], vv[:, b0:b1, h * 128:(h + 1) * 128], [], [f'G_vh{j}'], f'G_vh{j}')

        def load_q(h, ti):
            s, n = qtiles[ti]
            gi = (h * len(qtiles) + ti) % 2
            P.dma('sp', qh[gi][:, 0:n], S['qT'][h * 128:(h + 1) * 128, s:s + n], [], [f'G_qh{gi}'], f'G_qh{gi}')

        load_head(0)
        load_q(0, 0)
        for h in range(6):
            hj = h % 2
            for ti, (s, n) in enumerate(qtiles):
                gi = (h * len(qtiles) + ti) % 2
                qcur = qh[gi]
                qn_ = f'G_qh{gi}'
                if ti + 1 < len(qtiles):
                    load_q(h, ti + 1)
                elif h + 1 < 6:
                    load_q(h + 1, 0)
                if ti == 0 and h + 1 < 6:
                    load_head(h + 1)

                def emit_qk(kb):
                    pp = pS[kb % 2]
                    for comp in range(2):
                        P.op('pe', lambda e, comp=comp, kb=kb, pp=pp: e.matmul(pp[:, comp, 0:n], lhsT=kh[hj][64 * comp:64 * comp + 64, kb * 128:(kb + 1) * 128], rhs=qcur[64 * comp:64 * comp + 64, 0:n], start=True, stop=True),
                             [f'G_kh{hj}', qn_], [f'G_pS{kb % 2}'])
                emit_qk(0)
                first = [True, True]
                for kb in range(NKB):
                    if kb + 1 < NKB:
                        emit_qk(kb + 1)
                    pp = pS[kb % 2]
                    ppn = f'G_pS{kb % 2}'
                    pt = pT[kb % 4]
                    ptn = f'G_pT{kb % 4}'
                    P.op('act', lambda e, pp=pp, pt=pt: e.activation(out=pt[:, :, 0:n], in_=pp[:, :, 0:n], func=AF.Exp, scale=0.125), [ppn], [ptn])
                    for comp in range(2):
                        P.op('pe', lambda e, comp=comp, kb=kb, pt=pt: e.matmul(po[comp][:, 0:n], lhsT=vh[hj][:, kb, :], rhs=pt[:, comp, 0:n], start=(kb == 0), stop=(kb == NKB - 1)),
                             [f'G_vh{hj}', ptn], [f'G_po{comp}'])
                    P.op('pe', lambda e, kb=kb, pt=pt: e.matmul(pl[:, 0, 0:n], lhsT=k.ones_bf[:], rhs=pt[:, 0, 0:n], start=(kb == 0), stop=(kb == NKB - 1)),
                         ['ones_bf', ptn], ['G_pl0'])
                    ac = accs[1]
                    if kb == 0:
                        P.op('dve', lambda e, ac=ac, pt=pt: e.tensor_copy(out=ac[:, 1, 0:n], in_=pt[:, 1, 0:n]), [ptn], ['G_acc1'])
                    else:
                        P.op('dve', lambda e, ac=ac, pt=pt: e.tensor_tensor(out=ac[:, 1, 0:n], in0=ac[:, 1, 0:n], in1=pt[:, 1, 0:n], op=ALU.add), [ptn, 'G_acc1'], ['G_acc1'])
                P.op('pe', lambda e: e.matmul(pl[:, 1, 0:n], lhsT=k.ones_f[:], rhs=accs[1][:, 1, 0:n], start=True, stop=True), ['ones_f', 'G_acc1'], ['G_pl1'])
                P.op('dve', lambda e: e.reciprocal(out=rl[:, :, 0:n], in_=pl[:, :, 0:n]), ['G_pl0', 'G_pl1'], ['G_rl'])
                P.op('dve', lambda e: e.tensor_tensor(out=o1[:, 0:n], in0=po[0][:, 0:n], in1=rl[:, 0, 0:n], op=ALU.mult), ['G_po0', 'G_rl'], ['G_o1'])
                P.op('dve', lambda e: e.tensor_tensor(out=o2[:, 0:n], in0=po[1][:, 0:n], in1=rl[:, 1, 0:n], op=ALU.mult), ['G_po1', 'G_rl'], ['G_o2'])
                P.op('dve', lambda e: e.scalar_tensor_tensor(out=o1[:, 0:n], in0=o2[:, 0:n], scalar=neglam[:, 0:1], in1=o1[:, 0:n], op0=ALU.mult, op1=ALU.add),
                     ['G_o1', 'G_o2', 'G_neglam'], ['G_o1'])
                P.op('act', lambda e: e.activation(out=sq[:, 0:n], in_=o1[:, 0:n], func=AF.Square), ['G_o1'], ['G_sq'])
                P.op('pe', lambda e: e.matmul(pl[:, 0, 0:n], lhsT=k.ones_f[:], rhs=sq[:, 0:n], start=True, stop=True), ['ones_f', 'G_sq'], ['G_pl0', 'G_pl1'])
                P.op('act', lambda e: e.activation(out=sq[:, 0:n], in_=pl[:, 0, 0:n], func=AF.Sqrt, bias=k.cst[:, 0:1], scale=1.0 / 128), ['G_pl0', 'G_pl1', 'cst'], ['G_sq'])
                P.op('dve', lambda e: e.reciprocal(out=sq[:, 0:n], in_=sq[:, 0:n]), ['G_sq'], ['G_sq'])
                P.op('dve', lambda e: e.scalar_tensor_tensor(out=ych[:, 0:n], in0=o1[:, 0:n], scalar=gsub[:, 1:2], in1=sq[:, 0:n], op0=ALU.mult, op1=ALU.mult),
                     ['G_o1', 'G_sq', 'G_gsub'], ['G_ych'])
                P.dma('sp', yc_d[h * 128:(h + 1) * 128, s:s + n], ych[:, 0:n], ['G_ych'], [], 'G_ych')
    P.barrier()
    if stop_after == 'l1F1':
        return

    with ExitStack() as st:
        sb = lambda n, s, d: k.sb(n, s, d, st)
        ps = lambda n, s, d: k.ps(n, s, d, st)
        w_out = sb('w_out1', [128, 8, D], BF16)
        with ExitStack() as st2:
            fold_w_bf16(k, st2, 'w_out1', w_out, I['cd_w_out'], 8, k.gbc_d[1, 0, 0])
            P.barrier()
        cp = {}
        for nm, shp in (('conf_w', [128, 2, 31]), ('conf_b', [128, 2]), ('conf_lng', [128, 2]), ('conf_lnb', [128, 2])):
            cp[nm] = sb('H_' + nm, shp, F32)
            P.dma('sp', cp[nm][:], I[nm], [], ['H_' + nm], 'H_' + nm)
        yT = sb('H_yT', [128, 8, 512], BF16)
        gin = sb('H_gin', [128, 2, 542], F32)
        ca = sb('H_ca', [128, 512], F32)
        cb_ = sb('H_cb', [128, 512], F32)
        ct = [sb(f'H_ct{i}', [128, 512], F32) for i in range(2)]
        xm = sb('H_xm', [128, 2, 512], F32)
        sq = sb('H_sq', [128, 2, 512], F32)
        rstd = sb('H_rstd', [128, 512], F32)
        xt = sb('H_xt', [128, 4, D], F32)
        xo = sb('H_xo', [128, 4, D], F32)
        pm = ps('H_pm', [128, 512], F32)
        pv = ps('H_pv', [128, 512], F32)
        po = [ps(f'H_po{i}', [128, 512], F32) for i in range(4)]
        glv = S['gl'].rearrange("(c p) t -> p c t", p=128)
        ycv = yc_d.rearrange("(c p) t -> p c t", p=128)
        for (s, n) in tiles_of(TL):
            ng = n // 128
            P.dma('sp', yT[:, 0:6, 0:n], ycv[:, :, s:s + n], [], keys('H_yT', 8)[0:6], 'H_yT')
            P.dma('sp', gin[:, :, 0:n + 30], glv[:, :, PADZ + s - 15:PADZ + s + n + 15], [], keys('H_gin', 2), 'H_gin')
            P.dma('sp', xt[:, 0:ng, :], S['x1_l'][1 + s:1 + s + n, :].rearrange("(g p) f -> p g f", p=128), [], ['H_xt'], 'H_xt')
            for c in range(2):
                P.op('act', lambda e, c=c: e.activation(out=ca[:, 0:n], in_=gin[:, c, 0:n], func=AF.Identity, scale=cp['conf_w'][:, c, 0:1], bias=cp['conf_b'][:, c:c + 1]),
                     [f'H_gin.{c}', 'H_conf_w', 'H_conf_b'], ['H_ca'])
                P.op('pool', lambda e, c=c: e.tensor_scalar(out=cb_[:, 0:n], in0=gin[:, c, 1:1 + n], scalar1=cp['conf_w'][:, c, 1:2], scalar2=None, op0=ALU.mult),
                     [f'H_gin.{c}', 'H_conf_w'], ['H_cb'])
                for t in range(2, 31):
                    if t % 2 == 0:
                        P.op('dve', lambda e, c=c, t=t: e.scalar_tensor_tensor(out=ca[:, 0:n], in0=gin[:, c, t:t + n], scalar=cp['conf_w'][:, c, t:t + 1], in1=ca[:, 0:n],
                                                                               op0=ALU.mult, op1=ALU.add), [f'H_gin.{c}', 'H_ca'], ['H_ca'])
                    else:
                        ctt = ct[(t // 2) % 2]
                        ctn = f'H_ct{(t // 2) % 2}'
                        P.op('act', lambda e, c=c, t=t, ctt=ctt: e.activation(out=ctt[:, 0:n], in_=gin[:, c, t:t + n], func=AF.Identity, scale=cp['conf_w'][:, c, t:t + 1], bias=k.cst[:, 2:3]),
                             [f'H_gin.{c}', 'H_conf_w', 'cst'], [ctn])
                        P.op('pool', lambda e, ctt=ctt: e.tensor_tensor(out=cb_[:, 0:n], in0=cb_[:, 0:n], in1=ctt[:, 0:n], op=ALU.add), [ctn, 'H_cb'], ['H_cb'])
                P.op('dve', lambda e, c=c: e.tensor_tensor(out=xm[:, c, 0:n], in0=ca[:, 0:n], in1=cb_[:, 0:n], op=ALU.add), ['H_ca', 'H_cb'], [f'H_xm.{c}'])
            for c in range(2):
                P.op('pe', lambda e, c=c: e.matmul(pm[:, 0:n], lhsT=k.ones_f[:], rhs=xm[:, c, 0:n], start=(c == 0), stop=(c == 1)), ['ones_f', f'H_xm.{c}'], ['H_pm'])
            for c in range(2):
                P.op('dve', lambda e, c=c: e.scalar_tensor_tensor(out=xm[:, c, 0:n], in0=pm[:, 0:n], scalar=-1.0 / 256, in1=xm[:, c, 0:n], op0=ALU.mult, op1=ALU.add),
                     ['H_pm', f'H_xm.{c}'], [f'H_xm.{c}'])
                P.op('act', lambda e, c=c: e.activation(out=sq[:, c, 0:n], in_=xm[:, c, 0:n], func=AF.Square), [f'H_xm.{c}'], [f'H_sq.{c}'])
            for c in range(2):
                P.op('pe', lambda e, c=c: e.matmul(pv[:, 0:n], lhsT=k.ones_f[:], rhs=sq[:, c, 0:n], start=(c == 0), stop=(c == 1)), ['ones_f', f'H_sq.{c}'], ['H_pv'])
            P.op('act', lambda e: e.activation(out=rstd[:, 0:n], in_=pv[:, 0:n], func=AF.Sqrt, bias=k.cst[:, 0:1], scale=1.0 / 256), ['H_pv', 'cst'], ['H_rstd'])
            P.op('dve', lambda e: e.reciprocal(out=rstd[:, 0:n], in_=rstd[:, 0:n]), ['H_rstd'], ['H_rstd'])
            for c in range(2):
                P.op('dve', lambda e, c=c: e.tensor_tensor(out=xm[:, c, 0:n], in0=xm[:, c, 0:n], in1=rstd[:, 0:n], op=ALU.mult), [f'H_xm.{c}', 'H_rstd'], [f'H_xm.{c}'])
                P.op('act', lambda e, c=c: e.activation(out=yT[:, 6 + c, 0:n], in_=xm[:, c, 0:n], func=AF.Silu, scale=cp['conf_lng'][:, c:c + 1], bias=cp['conf_lnb'][:, c:c + 1]),
                     [f'H_xm.{c}', 'H_conf_lng', 'H_conf_lnb'], [f'H_yT.{6 + c}'])
            for g in range(ng):
                for nt in range(2):
                    pp = po[(2 * g + nt) % 4]
                    ppn = f'H_po{(2 * g + nt) % 4}'
                    for kc in range(8):
                        P.op('pe', lambda e, g=g, nt=nt, kc=kc, pp=pp: e.matmul(pp[:, :], lhsT=yT[:, kc, g * 128:(g + 1) * 128], rhs=w_out[:, kc, nt * 512:(nt + 1) * 512],
                                                                               start=(kc == 0), stop=(kc == 7)), [f'H_yT.{kc}', f'w_out1.{kc}'], [ppn])
                    P.op('dve', lambda e, g=g, nt=nt, pp=pp: e.tensor_tensor(out=xo[:, g, nt * 512:(nt + 1) * 512], in0=pp[:, :], in1=xt[:, g, nt * 512:(nt + 1) * 512], op=ALU.add),
                         [ppn, 'H_xt'], [f'H_xo.{g}'])
            P.dma('sp', S['x15'][1 + s:1 + s + n, :].rearrange("(g p) f -> p g f", p=128), xo[:, 0:ng, :], keys('H_xo', 4)[0:ng], [], 'H_xo')
    P.barrier()
    if stop_after == 'l1F2':
        return
    ffn_phase(k, 1, 0, S['x15'], None, TL, final=True, flush=(not k.half), out_rows=k.TO)
    P.barrier()


def _fm(v, nch):
    return np.ascontiguousarray(np.asarray(v, np.float32).reshape(nch, 128).T)


def prep_shared(inp, T):
    f = lambda a: np.ascontiguousarray(np.asarray(a, np.float32))
    d = {}
    d['mod_w'] = f(inp['mod_w'])
    d['mod_b'] = f(inp['mod_b'])
    d['modb_fm'] = f(np.asarray(inp['mod_b']).reshape(2, 6, 8, 128).transpose(3, 0, 1, 2))
    ng = np.stack([np.asarray(inp['norm_mix_g']), np.asarray(inp['norm_ffn_g'])], axis=1)
    d['ng_fm'] = f(ng.reshape(2, 2, 8, 128).transpose(3, 0, 1, 2))
    d['final_g_bc'] = f(np.broadcast_to(np.asarray(inp['final_g'])[None, :], (128, D)))
    d['ident'] = f(np.eye(128))
    d['ab_w_in'] = f(inp['ab_w_in'][0])
    d['ab_w_out'] = f(inp['ab_w_out'][0])
    cw4 = np.asarray(inp['lru_conv_w'][0], np.float32).reshape(4, 6, 128).transpose(2, 1, 0)
    d['lru_cw'] = f(np.concatenate([cw4, np.zeros((128, 6, 1), np.float32)], axis=2))
    d['pool_fl'] = f(np.stack([np.ones(128), np.zeros(128)], axis=1))
    d['lru_cb'] = _fm(inp['lru_conv_b'][0], 6)
    for nm, src in (('lru_bdA', inp['lru_wa'][0]), ('lru_bdX', inp['lru_wx'][0])):
        src = np.asarray(src)
        bd = np.zeros((128, 2, 6, 128), np.float32)
        for dd in range(2):
            for c in range(6):
                bd[0:64, dd, c, 0:64] = src[dd, 2 * c]
                bd[64:128, dd, c, 64:128] = src[dd, 2 * c + 1]
        d[nm] = bd
    for nm, src in (('lru_ba', inp['lru_ba'][0]), ('lru_bx', inp['lru_bx'][0]), ('lru_lam', inp['lru_lambda'][0])):
        d[nm] = f(np.asarray(src).reshape(2, 6, 128).transpose(2, 0, 1))
    pw = np.asarray(inp['pool_w'][0])
    bd = np.zeros((128, 2, 128), np.float32)
    for ch in range(2):
        bd[0:64, ch, 0:64] = pw[2 * ch]
        bd[64:128, ch, 64:128] = pw[2 * ch + 1]
    d['pool_bd'] = bd
    d['pool_scale'] = _fm(inp['pool_scale'][0], 2)
    invw = np.zeros((128, 2), np.float32)
    corr = np.ones((128, 2, 2, 16), np.float32)
    wins = (2, 4, 8, 16)
    Lbig = 1 << 20
    for g, w in enumerate(wins):
        ch, half = g // 2, g % 2
        psl = slice(64 * half, 64 * half + 64)
        invw[psl, ch] = 1.0 / w
        for i in range(16):
            t = i
            cnt = (t + w - w // 2) - max(t - w // 2, 0)
            corr[psl, ch, 0, i] = float(w) / cnt
            t = Lbig - 16 + i
            cnt = min(t + w - w // 2, Lbig) - (t - w // 2)
            corr[psl, ch, 1, i] = float(w) / cnt
    d['pool_invw'] = invw
    d['pool_corr'] = corr
    d['ffn_w_up'] = f(inp['ffn_w_up'])
    d['ffn_w_down'] = f(inp['ffn_w_down'])
    d['ffn_cw'] = f(np.asarray(inp['ffn_conv_w']).reshape(2, 3, 2 * NFC, 128).transpose(3, 0, 2, 1))
    d['ffn_cb'] = f(np.asarray(inp['ffn_conv_b']).reshape(2, 2 * NFC, 128).transpose(2, 0, 1))
    w_in = np.asarray(inp['cd_w_in'][0], np.float32)
    d['cd_w_in'] = f(w_in)
    qk = w_in[:, :1536].reshape(D, 1536 // 32, 2, 16)
    d['cd_w_sw'] = f(qk[:, :, ::-1, :].reshape(D, 1536))
    d['cd_w_out'] = f(inp['cd_w_out'][0])
    t = np.arange(T)
    row = (t // GRID_W).astype(np.float32)
    col = (t % GRID_W).astype(np.float32)
    inv = (10000.0 ** (-np.arange(16, dtype=np.float32) / 16)).astype(np.float32)
    ang_r = (row[:, None] * inv).astype(np.float32)
    ang_c = (col[:, None] * inv).astype(np.float32)
    rc = np.zeros((128, T), np.float32)
    rs = np.zeros((128, T), np.float32)
    for p in range(128):
        dd = p % 64
        ang = ang_r if dd < 32 else ang_c
        fq = dd % 16
        first = (dd % 32) < 16
        rc[p] = np.cos(ang[:, fq])
        rs[p] = (-1.0 if first else 1.0) * np.sin(ang[:, fq])
    d['rope_c'] = rc
    d['rope_s'] = rs
    d['diff_l'] = f(np.stack([np.asarray(inp['diff_lq1'][0]), np.asarray(inp['diff_lk1'][0]),
                              np.asarray(inp['diff_lq2'][0]), np.asarray(inp['diff_lk2'][0])])[None])
    d['subln_g'] = f(np.asarray(inp['diff_subln_g'][0]).reshape(128, 1))
    d['conf_w'] = f(np.asarray(inp['conf_dw_w'][0]).reshape(31, 2, 128).transpose(2, 1, 0))
    d['conf_b'] = _fm(inp['conf_dw_b'][0], 2)
    d['conf_lng'] = _fm(inp['conf_ln_g'][0], 2)
    d['conf_lnb'] = _fm(inp['conf_ln_b'][0], 2)
    return d


def prep_rev(shared, T):
    f = lambda a: np.ascontiguousarray(np.asarray(a, np.float32))
    d = dict(shared)
    cw = shared['lru_cw']
    d['lru_cw'] = f(cw[:, :, ::-1])
    for nm in ('lru_bdA', 'lru_bdX', 'lru_ba', 'lru_bx', 'lru_lam'):
        d[nm] = f(shared[nm][:, ::-1])
    d['pool_fl'] = f(np.stack([np.zeros(128), np.ones(128)], axis=1))
    corr = np.ones((128, 2, 2, 16), np.float32)
    Lbig = 1 << 20
    for g, w in enumerate((2, 4, 8, 16)):
        ch, half = g // 2, g % 2
        psl = slice(64 * half, 64 * half + 64)
        for i in range(16):
            r = i
            cnt = (r + w // 2) - max(r - w // 2 + 1, 0) + 1
            corr[psl, ch, 0, i] = float(w) / cnt
            r = Lbig - 16 + i
            cnt = min(r + w // 2, Lbig - 1) - (r - w // 2 + 1) + 1
            corr[psl, ch, 1, i] = float(w) / cnt
    d['pool_corr'] = corr
    d['ffn_cw'] = f(shared['ffn_cw'][:, :, :, ::-1])
    d['conf_w'] = f(shared['conf_w'][:, :, ::-1])
    d['rope_c'] = f(shared['rope_c'][:, ::-1])
    d['rope_s'] = f(shared['rope_s'][:, ::-1])
    return d


def prep_core(inp, shared, b, T, rev=False):
    m = dict(shared)
    x = np.asarray(inp['x'][b, :T], np.float32)
    ctx = np.asarray(inp['ctx'][b], np.float32)
    if rev:
        x = x[::-1]
        ctx = ctx[::-1]
    m['x'] = np.ascontiguousarray(x)
    m['ctx'] = np.ascontiguousarray(ctx)
    m['c_fm'] = np.ascontiguousarray(np.stack([_fm(inp['c'][b], 8), _fm(inp['c_ctx'], 8)], axis=1))
    return m


_CACHE = {}


def kernel(**inputs):
    T = inputs['x'].shape[1]
    Bn = inputs['x'].shape[0]
    if T not in _CACHE:
        _CACHE[T] = build(T)
    kk = _CACHE[T]
    shared = prep_shared(inputs, T)
    shared_r = prep_rev(shared, T)
    in_maps = [prep_core(inputs, shared, b, T, rev=False) for b in range(Bn)] + \
              [prep_core(inputs, shared_r, b, T, rev=True) for b in range(Bn)]
    res = run_bass_kernel_spmd(kk.nc, in_maps, core_ids=list(range(2 * Bn)))
    out = np.empty((Bn, T, D), np.float32)
    for b in range(Bn):
        out[b, :T // 2] = np.asarray(res.results[b]['out'], np.float32)
        out[b, T // 2:] = np.asarray(res.results[Bn + b]['out'], np.float32)[::-1]
    return out
```

```python
import numpy as np
import math
import os
from contextlib import ExitStack
import concourse.bass as bass
import concourse.mybir as mybir
from concourse.bass_utils import run_bass_kernel_spmd
from concourse.ap import AP

F32 = mybir.dt.float32
BF16 = mybir.dt.bfloat16
AF = mybir.ActivationFunctionType
ALU = mybir.AluOpType
AX = mybir.AxisListType

D = 1024
LC = 256
DFF = 2816
NFC = DFF // 128
PADZ = 16
EPS = 1e-6
GRID_W = 64
LAMBDA_INIT1 = 0.8 - 0.6 * math.exp(-0.3 * 1)
SAME_ENGINE_SYNC = True


def rev(ap):
    a = [list(x) for x in ap.ap]
    st, n = a[-1]
    a[-1] = [-st, n]
    return AP(ap.tensor, ap.offset + st * (n - 1), a)


class Prog:
    def __init__(self, nc, es):
        self.nc = nc
        self.es = es
        self.eng = {'pe': nc.tensor, 'act': nc.scalar, 'dve': nc.vector, 'pool': nc.gpsimd, 'sp': nc.sync}
        self.esem = {e: es.enter_context(nc.semaphore('S_' + e)) for e in ('pe', 'act', 'dve', 'pool')}
        self.ecnt = {e: 0 for e in self.esem}
        self.dsem = {}
        self.dpool = []
        self.nd = 0
        self.waited = {e: {} for e in self.eng}
        self.lastw = {}
        self.rd = {}
        self.ninstr = 0

    def _need(self, reads, writes):
        ev = {}

        def add(e):
            if e is None:
                return
            k, sem, val = e
            if k not in ev or ev[k][1] < val:
                ev[k] = (sem, val)
        for r in reads:
            add(self.lastw.get(r))
        for w in writes:
            add(self.lastw.get(w))
            for e in self.rd.get(w, {}).items():
                add((e[0], e[1][0], e[1][1]))
        return ev

    def _wait(self, e, ev):
        for k, (sem, val) in ev.items():
            if k == 'S_' + e and (e == 'pe' or not SAME_ENGINE_SYNC):
                continue
            if self.waited[e].get(k, 0) < val:
                self.eng[e].wait_ge(sem, val)
                self.waited[e][k] = val
                self.ninstr += 1

    def _commit(self, ev, reads, writes):
        k, sem, val = ev
        for w in writes:
            self.lastw[w] = ev
            self.rd[w] = {}
        for r in reads:
            self.rd.setdefault(r, {})[k] = (sem, val)

    def op(self, e, fn, reads, writes):
        self._wait(e, self._need(reads, writes))
        ins = fn(self.eng[e])
        self.ecnt[e] += 1
        ins.then_inc(self.esem[e], 1)
        self.ninstr += 1
        self._commit(('S_' + e, self.esem[e], self.ecnt[e]), reads, writes)

    def dma(self, q, out, in_, reads, writes, key):
        self._wait(q, self._need(reads, writes))
        if key not in self.dsem:
            if self.dpool:
                self.dsem[key] = self.dpool.pop()
            else:
                nm = 'D%d' % self.nd
                self.nd += 1
                self.dsem[key] = [self.es.enter_context(self.nc.semaphore(nm)), 0, nm]
        d = self.dsem[key]
        ins = self.eng[q].dma_start(out=out, in_=in_)
        d[1] += 16
        ins.then_inc(d[0], 16)
        self.ninstr += 1
        self._commit((d[2], d[0], d[1]), reads, writes)

    def barrier(self):
        ev = {}
        for e in self.esem:
            if self.ecnt[e] > 0:
                ev['S_' + e] = (self.esem[e], self.ecnt[e])
        for k, d in self.dsem.items():
            if d[1] > 0:
                ev[d[2]] = (d[0], d[1])
        for e in self.eng:
            self._wait(e, dict(ev))
        self.lastw = {}
        self.rd = {}
        for k, d in self.dsem.items():
            self.dpool.append(d)
        self.dsem = {}


def keys(name, n):
    return [f"{name}.{i}" for i in range(n)]


def tiles_of(T, w=512):
    out = []
    s = 0
    while s < T:
        n = min(w, T - s)
        out.append((s, n))
        s += n
    return out


class K:
    pass


def build(T, dbg=False, stop_after=None, half=True):
    nc = bass.Bass("TRN2", target_bir_lowering=False)
    k = K()
    k.nc = nc
    k.T = T
    k.half = half
    k.TL = (T // 2 + 128) if half else T
    k.TO = (T // 2) if half else T

    def din(name, shape, dt=F32):
        return nc.dram_tensor(name, list(shape), dt, kind="ExternalInput").ap()

    def dscr(name, shape, dt=F32, out=False):
        kind = "ExternalOutput" if (out or dbg) else "Internal"
        return nc.dram_tensor(name, list(shape), dt, kind=kind).ap()

    I = {}
    I['x'] = din('x', [T, D])
    I['ctx'] = din('ctx', [LC, D])
    I['c_fm'] = din('c_fm', [128, 2, 8])
    I['mod_w'] = din('mod_w', [2, D, 6 * D])
    I['modb_fm'] = din('modb_fm', [128, 2, 6, 8])
    I['mod_b'] = din('mod_b', [2, 6 * D])
    I['ng_fm'] = din('ng_fm', [128, 2, 2, 8])
    I['final_g_bc'] = din('final_g_bc', [128, D])
    I['ident'] = din('ident', [128, 128])
    I['ab_w_in'] = din('ab_w_in', [D, 1792])
    I['ab_w_out'] = din('ab_w_out', [D, D])
    I['lru_cw'] = din('lru_cw', [128, 6, 5])
    I['pool_fl'] = din('pool_fl', [128, 2])
    I['lru_cb'] = din('lru_cb', [128, 6])
    I['lru_bdA'] = din('lru_bdA', [128, 2, 6, 128])
    I['lru_bdX'] = din('lru_bdX', [128, 2, 6, 128])
    I['lru_ba'] = din('lru_ba', [128, 2, 6])
    I['lru_bx'] = din('lru_bx', [128, 2, 6])
    I['lru_lam'] = din('lru_lam', [128, 2, 6])
    I['pool_bd'] = din('pool_bd', [128, 2, 128])
    I['pool_scale'] = din('pool_scale', [128, 2])
    I['pool_invw'] = din('pool_invw', [128, 2])
    I['pool_corr'] = din('pool_corr', [128, 2, 2, 16])
    I['ffn_w_up'] = din('ffn_w_up', [2, D, 2 * DFF])
    I['ffn_w_down'] = din('ffn_w_down', [2, DFF, D])
    I['ffn_cw'] = din('ffn_cw', [128, 2, 2 * NFC, 3])
    I['ffn_cb'] = din('ffn_cb', [128, 2, 2 * NFC])
    I['cd_w_in'] = din('cd_w_in', [D, 2816])
    I['cd_w_sw'] = din('cd_w_sw', [D, 1536])
    I['cd_w_out'] = din('cd_w_out', [D, D])
    I['rope_c'] = din('rope_c', [128, T])
    I['rope_s'] = din('rope_s', [128, T])
    I['diff_l'] = din('diff_l', [1, 4, 64])
    I['subln_g'] = din('subln_g', [128, 1])
    I['conf_w'] = din('conf_w', [128, 2, 31])
    I['conf_b'] = din('conf_b', [128, 2])
    I['conf_lng'] = din('conf_lng', [128, 2])
    I['conf_lnb'] = din('conf_lnb', [128, 2])
    k.I = I

    k.out = nc.dram_tensor('out', [k.TO, D], F32, kind="ExternalOutput").ap()
    S = {}
    for nm, TT in (('l', T), ('c', LC)):
        S['z_' + nm] = dscr('z_' + nm, [1792, PADZ + TT + PADZ])
        S['xa_' + nm] = dscr('xa_' + nm, [768, TT])
        S['hf_' + nm] = dscr('hf_' + nm, [768, TT])
        S['x05_' + nm] = dscr('x05_' + nm, [1 + TT, D])
        S['x1_' + nm] = dscr('x1_' + nm, [1 + TT + 128, D])
    S['x15'] = dscr('x15', [1 + T, D])
    S['x2'] = dscr('x2', [1 + T + 128, D])
    S['qT'] = dscr('qT', [768, T], BF16)
    S['kT'] = dscr('kT', [768, T + LC], BF16)
    S['v'] = dscr('v', [T + LC, 768], BF16)
    S['gl'] = dscr('gl', [256, PADZ + T + PADZ])
    k.S = S

    with ExitStack() as es:
        P = Prog(nc, es)
        k.P = P

        uid = [0]

        def sb(name, shape, dt, st=es):
            uid[0] += 1
            return st.enter_context(nc.sbuf_tensor(f"{name}_s{uid[0]}", list(shape), dt))

        def ps(name, shape, dt, st=es):
            uid[0] += 1
            return st.enter_context(nc.psum_tensor(f"{name}_p{uid[0]}", list(shape), dt))
        k.sb = sb
        k.ps = ps

        ident = sb('ident', [128, 128], BF16)
        P.dma('pool', ident[:], I['ident'], [], ['ident'], 'ident')
        k.ident = ident
        cst = sb('cst', [128, 8], F32)
        P.op('dve', lambda e: e.memset(cst[:, 0:1], EPS), [], ['cst'])
        P.op('dve', lambda e: e.memset(cst[:, 1:2], 1.0), [], ['cst'])
        P.op('dve', lambda e: e.memset(cst[:, 2:3], 0.0), [], ['cst'])
        k.cst = cst
        zero = sb('zero', [128, 512], F32)
        P.op('dve', lambda e: e.memset(zero[:], 0.0), [], ['zero'])
        k.zero = zero
        ones_bf = sb('ones_bf', [128, 128], BF16)
        P.op('dve', lambda e: e.memset(ones_bf[:], 1.0), [], ['ones_bf'])
        k.ones_bf = ones_bf
        ones_f = sb('ones_f', [128, 128], F32)
        P.op('dve', lambda e: e.memset(ones_f[:], 1.0), [], ['ones_f'])
        k.ones_f = ones_f

        modfm = sb('modfm', [128, 2, 2, 4, 8], F32)
        k.modfm = modfm
        k.gbc_d = nc.dram_tensor('gbc_d', [2, 2, 2, 128, D], F32, kind="Internal").ap()

        if os.environ.get('DBG_ONLY_F1'):
            layer1(k, 'l1F1')
            return finish(k, es)
        phase_adaln(k)
        P.barrier()
        if dbg:
            mdbg = nc.dram_tensor('modfm_dbg', [128, 2, 2, 4, 8], F32, kind="ExternalOutput").ap()
            P.dma('sp', mdbg, modfm[:], [], [], 'mdbg')
            gdbg = nc.dram_tensor('gbc_dbg', [2, 2, 2, 128, D], F32, kind="ExternalOutput").ap()
            P.dma('sp', gdbg, k.gbc_d, [], [], 'gdbg')
        if stop_after == 'adaln':
            return finish(k, es)

        layer0(k, stop_after)
        if stop_after is not None and stop_after.startswith('l0'):
            return finish(k, es)
        layer1(k, stop_after)
        return finish(k, es)


def finish(k, es):
    k.P.barrier()
    k.ninstr = k.P.ninstr
    return k


def phase_adaln(k):
    nc, P, I = k.nc, k.P, k.I
    with ExitStack() as st:
        sb = lambda n, s, d: k.sb(n, s, d, st)
        ps = lambda n, s, d: k.ps(n, s, d, st)
        cf = sb('ad_cf', [128, 2, 8], F32)
        P.dma('sp', cf[:], I['c_fm'], [], ['ad_cf'], 'ad_cf')
        sc = sb('ad_sc', [128, 2, 8], F32)
        P.op('act', lambda e: e.activation(out=sc[:], in_=cf[:], func=AF.Silu), ['ad_cf'], ['ad_sc'])
        rep = sb('ad_rep', [128, 8, 128], F32)
        for s_ in range(2):
            for kc in range(8):
                P.op('dve', lambda e, s_=s_, kc=kc: e.tensor_copy(out=rep[:, kc, 64 * s_:64 * s_ + 64], in_=sc[:, s_, kc:kc + 1].to_broadcast([128, 64])),
                     ['ad_sc'], [f'ad_rep.{kc}'])
        modb = sb('ad_modb', [128, 2, 6, 8], F32)
        P.dma('sp', modb[:], I['modb_fm'], [], ['ad_modb'], 'ad_modb')
        ng = sb('ad_ng', [128, 2, 2, 8], F32)
        P.dma('sp', ng[:], I['ng_fm'], [], ['ad_ng'], 'ad_ng')
        brow = sb('ad_brow', [1, 2, 6 * D], F32)
        P.dma('sp', brow[:], I['mod_b'].rearrange("(o l) n -> o l n", o=1), [], ['ad_brow'], 'ad_brow')
        wt = [sb(f'ad_w{i}', [128, 6 * D], F32) for i in range(2)]
        gst = sb('ad_gst', [128, 2, D], F32)
        facc = sb('ad_facc', [128, 32, 2], F32)
        pfm = ps('ad_pfm', [128, 32, 2], F32)
        pbc = [ps(f'ad_pbc{i}', [128, 512], F32) for i in range(4)]
        for l in range(2):
            for pss in range(1):
                for kc in range(8):
                    w = wt[kc % 2]
                    wk = f'ad_w{kc % 2}'
                    P.dma('sp', w[:], I['mod_w'][l, kc * 128:(kc + 1) * 128, :], [], [wk], wk)
                    if pss == 0:
                        jmap = [0, 1, 3, 4]
                        for jj, j in enumerate(jmap):
                            for fc in range(8):
                                col = j * D + fc * 128
                                P.op('pe', lambda e, w=w, col=col, jj=jj, fc=fc, kc=kc: e.matmul(
                                    pfm[:, jj * 8 + fc, :], lhsT=w[:, col:col + 128], rhs=sc[:, :, kc],
                                    start=True, stop=True), [wk, 'ad_sc'], ['ad_pfm'])
                        if kc == 0:
                            P.op('dve', lambda e: e.tensor_copy(out=facc[:], in_=pfm[:]), ['ad_pfm'], ['ad_facc'])
                        else:
                            P.op('dve', lambda e: e.tensor_tensor(out=facc[:], in0=pfm[:], in1=facc[:], op=ALU.add), ['ad_pfm', 'ad_facc'], ['ad_facc'])
                    if True:
                        for nt in range(4):
                            gj = 2 if nt < 2 else 5
                            col = gj * D + (nt % 2) * 512
                            P.op('pe', lambda e, w=w, col=col, nt=nt, kc=kc: e.matmul(
                                pbc[nt][:], lhsT=rep[:, kc, :], rhs=w[:, col:col + 512],
                                start=(kc == 0), stop=False), [wk, f'ad_rep.{kc}'], [f'ad_pbc{nt}'])
                if pss == 0:
                    jmap = [0, 1, 3, 4]
                    for s_ in range(2):
                        for jj, j in enumerate(jmap):
                            P.op('dve', lambda e, s_=s_, jj=jj, j=j, l=l: e.tensor_tensor(
                                out=k.modfm[:, l, s_, jj, :], in0=facc[:, jj * 8:(jj + 1) * 8, s_], in1=modb[:, l, j, :], op=ALU.add),
                                ['ad_facc', 'ad_modb'], [f'modfm.{l}.{s_}.{jj}'])
                        for jj, which in ((1, 0), (3, 1)):
                            P.op('dve', lambda e, s_=s_, jj=jj, which=which, l=l: e.scalar_tensor_tensor(
                                out=k.modfm[:, l, s_, jj, :], in0=k.modfm[:, l, s_, jj, :], scalar=1.0, in1=ng[:, l, which, :],
                                op0=ALU.add, op1=ALU.mult), [f'modfm.{l}.{s_}.{jj}', 'ad_ng'], [f'modfm.{l}.{s_}.{jj}'])
                if True:
                    s_ = pss
                    for nt in range(4):
                        gj = 2 if nt < 2 else 5
                        col = gj * D + (nt % 2) * 512
                        P.op('pe', lambda e, nt=nt, col=col, l=l: e.matmul(
                            pbc[nt][:], lhsT=k.ones_f[0:1, :], rhs=brow[0:1, l, col:col + 512], start=False, stop=True),
                            ['ones_f', 'ad_brow'], [f'ad_pbc{nt}'])
                        P.op('act', lambda e, nt=nt: e.copy(out=gst[:, nt // 2, (nt % 2) * 512:(nt % 2) * 512 + 512], in_=pbc[nt][:]),
                            [f'ad_pbc{nt}'], [f'ad_gst.{nt}'])
                    for j2 in range(2):
                        for s2 in range(2):
                            for hh in range(2):
                                P.dma('sp', k.gbc_d[l, s2, j2, 64 * hh:64 * hh + 64, :], gst[64 * s2:64 * s2 + 64, j2, :],
                                      [f'ad_gst.{2 * j2}', f'ad_gst.{2 * j2 + 1}'], [], 'ad_gst')
        P.barrier()


def load_w_bf16(k, name, dst, src_ap, nk, ncols, colblk=2048):
    P = k.P
    for kc in range(nk):
        for c0 in range(0, ncols, colblk):
            c1 = min(ncols, c0 + colblk)
            P.dma('pool', dst[:, kc, c0:c1], src_ap[kc * 128:(kc + 1) * 128, c0:c1], [], [f'{name}.{kc}'], f'{name}.{kc}')


def fold_w_bf16(k, st, name, dst, src_ap, nk, gb_dram):
    P = k.P
    stg = [k.sb(f'{name}_stg{i}', [128, D], F32, st) for i in range(2)]
    gb = k.sb(f'{name}_gb', [128, D], F32, st)
    P.dma('sp', gb[:], gb_dram, [], [f'{name}_gb'], f'{name}_gb')
    gb_ap = gb[:]
    for kc in range(nk):
        s_ = stg[kc % 2]
        sk = f'{name}_stg{kc % 2}'
        P.dma('sp', s_[:], src_ap[kc * 128:(kc + 1) * 128, :], [], [sk], sk)
        P.op('dve', lambda e, s_=s_, kc=kc: e.tensor_tensor(out=dst[:, kc, :], in0=s_[:], in1=gb_ap, op=ALU.mult),
             [sk, f'{name}_gb'], [f'{name}.{kc}'])


def modulate_tile(k, B, src_rows, n, l, s_, jsh, hT, hTname, part=0):
    P = k.P
    ng = n // 128
    X, Xn = B['xt'], B['xtname']
    ss = B['ss']
    xn = B['xn']
    for g0 in (range(0, ng, 2) if part in (0, 1) else []):
        gg = min(2, ng - g0)
        P.dma('sp', X[:, 0:gg, :], src_rows[g0 * 128:(g0 + gg) * 128, :].rearrange("(g p) f -> p g f", p=128), [], [Xn], Xn)
        P.op('dve', lambda e: e.memset(ss[:, 0:2], 0.0), [], [B['ssname']])
        for g in range(gg):
            P.op('act', lambda e, g=g: e.activation(out=B['junk'][:], in_=X[:, g, :], func=AF.Square, accum_out=ss[:, g:g + 1]),
                 [Xn], [B['ssname'], B['junkname']])
        P.op('act', lambda e, gg=gg: e.activation(out=ss[:, 4:4 + gg], in_=ss[:, 0:gg], func=AF.Sqrt, bias=k.cst[:, 0:1], scale=1.0 / D),
             [B['ssname'], 'cst'], [B['ssname']])
        P.op('dve', lambda e, gg=gg: e.reciprocal(out=ss[:, 8:8 + gg], in_=ss[:, 4:4 + gg]), [B['ssname']], [B['ssname']])
        for g in range(gg):
            if g % 2 == 0:
                P.op('dve', lambda e, g=g, g0=g0: e.tensor_scalar(out=xn[:, g0 + g, :], in0=X[:, g, :], scalar1=ss[:, 8 + g:9 + g], scalar2=None, op0=ALU.mult),
                     [Xn, B['ssname']], [f"{B['xnname']}.{g0 + g}"])
            else:
                P.op('act', lambda e, g=g, g0=g0: e.activation(out=xn[:, g0 + g, :], in_=X[:, g, :], func=AF.Identity, scale=ss[:, 8 + g:9 + g], bias=k.cst[:, 2:3]),
                     [Xn, B['ssname'], 'cst'], [f"{B['xnname']}.{g0 + g}"])
    if part == 1:
        return
    for fc in range(8):
        tp = B['tp'][fc % 2]
        tpn = B['tpname'][fc % 2]
        for g in range(ng):
            P.op('pe', lambda e, g=g, fc=fc, tp=tp: e.transpose(out=tp[:, g * 128:(g + 1) * 128], in_=xn[:, g, fc * 128:(fc + 1) * 128], identity=k.ident[:]),
                 [f"{B['xnname']}.{g}", 'ident'], [tpn])
        P.op('act', lambda e, fc=fc, tp=tp: e.activation(out=hT[:, fc, 0:n], in_=tp[:, 0:n], func=AF.Identity,
                                                          scale=k.modfm[:, l, s_, jsh + 1, fc:fc + 1], bias=k.modfm[:, l, s_, jsh, fc:fc + 1]),
             [tpn], [f'{hTname}.{fc}'])


def mod_bufs(k, st, pfx):
    B = {}
    B['xt'] = k.sb(pfx + 'xt', [128, 2, D], F32, st)
    B['xtname'] = pfx + 'xt'
    B['xn'] = k.sb(pfx + 'xn', [128, 4, D], BF16, st)
    B['xnname'] = pfx + 'xn'
    B['junk'] = k.sb(pfx + 'junk', [128, D], BF16, st)
    B['junkname'] = pfx + 'junk'
    B['ss'] = k.sb(pfx + 'ss', [128, 12], F32, st)
    B['ssname'] = pfx + 'ss'
    B['tp'] = [k.ps(pfx + f'tp{i}', [128, 512], BF16, st) for i in range(2)]
    B['tpname'] = [pfx + f'tp{i}' for i in range(2)]
    return B


def layer0(k, stop_after):
    nc, P, I, S = k.nc, k.P, k.I, k.S
    with ExitStack() as st:
        sb = lambda n, s, d: k.sb(n, s, d, st)
        lp = {}
        for nm, shp in (('lru_cw', [128, 6, 5]), ('pool_fl', [128, 2]), ('lru_cb', [128, 6]), ('lru_ba', [128, 2, 6]), ('lru_bx', [128, 2, 6]),
                        ('lru_lam', [128, 2, 6]), ('pool_scale', [128, 2]), ('pool_invw', [128, 2]), ('pool_corr', [128, 2, 2, 16])):
            lp[nm] = sb('p_' + nm, shp, F32)
            P.dma('sp', lp[nm][:], I[nm], [], ['p_' + nm], 'p_' + nm)
        for nm, shp in (('lru_bdA', [128, 2, 6, 128]), ('lru_bdX', [128, 2, 6, 128]), ('pool_bd', [128, 2, 128])):
            lp[nm] = sb('p_' + nm, shp, BF16)
            P.dma('pool', lp[nm][:], I[nm], [], ['p_' + nm], 'p_' + nm)
        cl = sb('p_cl', [128, 2, 2, 6], F32)
        tmp = sb('p_cltmp', [128, 2, 6], F32)
        P.op('act', lambda e: e.activation(out=tmp[:], in_=lp['lru_lam'][:], func=AF.Exp, scale=-1.0), ['p_lru_lam'], ['p_cltmp'])
        P.op('act', lambda e: e.activation(out=tmp[:], in_=tmp[:], func=AF.Ln, bias=k.cst[:, 1:2], scale=1.0), ['p_cltmp', 'cst'], ['p_cltmp'])
        P.op('dve', lambda e: e.tensor_scalar(out=cl[:, 0, :, :], in0=tmp[:], scalar1=-8.0, scalar2=None, op0=ALU.mult), ['p_cltmp'], ['p_cl'])
        P.op('dve', lambda e: e.tensor_scalar(out=cl[:, 1, :, :], in0=tmp[:], scalar1=-16.0, scalar2=None, op0=ALU.mult), ['p_cltmp'], ['p_cl'])
        lp['cl'] = cl
        stt = sb('p_state', [128, 2, 6], F32)
        P.op('dve', lambda e: e.memset(stt[:], 0.0), [], keys('p_state0', 6) + keys('p_state1', 6))
        lp['state'] = stt
        k.lp = lp
        w_in = sb('w_in0', [128, 8, 1792], BF16)
        load_w_bf16(k, 'w_in0', w_in, I['ab_w_in'], 8, 1792, colblk=1792)
        w_out = sb('w_out0', [128, 8, D], BF16)
        k.w_in0, k.w_out0 = w_in, w_out

        for s_, nm, TT, xsrc in ((1, 'c', LC, I['ctx']), (0, 'l', k.T, I['x'])):
            seg = K()
            seg.nm, seg.T, seg.x, seg.set = nm, TT, xsrc, s_
            seg.z, seg.xa, seg.hf, seg.x05, seg.x1 = S['z_' + nm], S['xa_' + nm], S['hf_' + nm], S['x05_' + nm], S['x1_' + nm]
            seg.tiles = tiles_of(TT)
            with ExitStack() as st2:
                fold_w_bf16(k, st2, 'w_out0', w_out, I['ab_w_out'], 8, k.gbc_d[0, s_, 0])
            P.barrier()
            l0_phaseA(k, seg)
            P.barrier()
            if stop_after == 'l0A' and nm == 'l':
                return
            l0_phaseB(k, seg)
            P.barrier()
            if stop_after == 'l0B' and nm == 'l':
                return
            l0_phaseC(k, seg)
            P.barrier()
            if stop_after == 'l0C' and nm == 'l':
                return
    for s_, nm, TT in ((1, 'c', LC), (0, 'l', k.T)):
        ffn_phase(k, 0, s_, S['x05_' + nm], S['x1_' + nm], TT, final=False)
        P.barrier()


def l0_phaseA(k, seg):
    nc, P = k.nc, k.P
    with ExitStack() as st:
        sb = lambda n, s, d: k.sb(n, s, d, st)
        ps = lambda n, s, d: k.ps(n, s, d, st)
        B = mod_bufs(k, st, 'A_')
        hT = sb('A_hT', [128, 8, 512], BF16)
        zt = [sb(f'A_zt{i}', [128, 14, 512], F32) for i in range(2)]
        zp = [ps(f'A_zp{i}', [128, 512], F32) for i in range(4)]
        zv = seg.z.rearrange("(c p) t -> p c t", p=128)
        P.dma('sp', zv[:, :, 0:PADZ], k.zero[:, 0:14 * PADZ].rearrange("p (c t) -> p c t", c=14), ['zero'], [], 'A_zpad')
        P.dma('sp', zv[:, :, PADZ + seg.T:PADZ + seg.T + PADZ], k.zero[:, 0:14 * PADZ].rearrange("p (c t) -> p c t", c=14), ['zero'], [], 'A_zpad')
        modulate_tile(k, B, seg.x[seg.tiles[0][0]:seg.tiles[0][0] + seg.tiles[0][1], :], seg.tiles[0][1], 0, seg.set, 0, hT, 'A_hT', part=1)
        for ti, (s, n) in enumerate(seg.tiles):
            modulate_tile(k, B, seg.x[s:s + n, :], n, 0, seg.set, 0, hT, 'A_hT', part=2)
            Z = zt[ti % 2]
            Zn = f'A_zt{ti % 2}'
            for mc in range(14):
                if mc == 6 and ti + 1 < len(seg.tiles):
                    s2, n2 = seg.tiles[ti + 1]
                    modulate_tile(k, B, seg.x[s2:s2 + n2, :], n2, 0, seg.set, 0, hT, 'A_hT', part=1)
                zpp = zp[mc % 4]
                for kc in range(8):
                    P.op('pe', lambda e, mc=mc, kc=kc, zpp=zpp: e.matmul(zpp[:, 0:n], lhsT=k.w_in0[:, kc, mc * 128:(mc + 1) * 128], rhs=hT[:, kc, 0:n],
                                                                         start=(kc == 0), stop=(kc == 7)),
                         [f'w_in0.{kc}', f'A_hT.{kc}'], [f'A_zp{mc % 4}'])
                eng = 'act' if mc % 2 == 0 else 'dve'
                if eng == 'act':
                    P.op('act', lambda e, mc=mc, zpp=zpp: e.copy(out=Z[:, mc, 0:n], in_=zpp[:, 0:n]), [f'A_zp{mc % 4}'], [f'{Zn}.{mc}'])
                else:
                    P.op('dve', lambda e, mc=mc, zpp=zpp: e.tensor_copy(out=Z[:, mc, 0:n], in_=zpp[:, 0:n]), [f'A_zp{mc % 4}'], [f'{Zn}.{mc}'])
            P.dma('sp', zv[:, :, PADZ + s:PADZ + s + n], Z[:, :, 0:n], keys(Zn, 14), [], Zn)


def lru_coeffs(k, C, d, n, xa, xab):
    P, lp = k.P, k.lp
    for c in range(6):
        pr, pi = C['pg'][(2 * c) % 4], C['pg'][(2 * c + 1) % 4]
        prn, pin = C['pgname'][(2 * c) % 4], C['pgname'][(2 * c + 1) % 4]
        P.op('pe', lambda e, c=c, pr=pr: e.matmul(pr[:, 0:n], lhsT=lp['lru_bdA'][:, d, c, :], rhs=xab[:, c, 0:n], start=True, stop=True),
             ['p_lru_bdA', f"{C['xabname']}.{c}"], [prn])
        P.op('pe', lambda e, c=c, pi=pi: e.matmul(pi[:, 0:n], lhsT=lp['lru_bdX'][:, d, c, :], rhs=xab[:, c, 0:n], start=True, stop=True),
             ['p_lru_bdX', f"{C['xabname']}.{c}"], [pin])
        P.op('act', lambda e, c=c, pr=pr: e.activation(out=C['r'][:, c, 0:n], in_=pr[:, 0:n], func=AF.Sigmoid, bias=lp['lru_ba'][:, d, c:c + 1], scale=1.0),
             [prn, 'p_lru_ba'], [f"{C['pfx']}r.{c}"])
        P.op('act', lambda e, c=c, pi=pi: e.activation(out=C['ig'][:, c, 0:n], in_=pi[:, 0:n], func=AF.Sigmoid, bias=lp['lru_bx'][:, d, c:c + 1], scale=1.0),
             [pin, 'p_lru_bx'], [f"{C['pfx']}ig.{c}"])
    for c in range(6):
        P.op('act', lambda e, c=c: e.activation(out=C['a'][:, c, 0:n], in_=C['r'][:, c, 0:n], func=AF.Exp, scale=lp['cl'][:, 0, d, c:c + 1]),
             [f"{C['pfx']}r.{c}", 'p_cl'], [f"{C['pfx']}a.{c}"])
        P.op('act', lambda e, c=c: e.activation(out=C['r'][:, c, 0:n], in_=C['r'][:, c, 0:n], func=AF.Exp, scale=lp['cl'][:, 1, d, c:c + 1]),
             [f"{C['pfx']}r.{c}", 'p_cl'], [f"{C['pfx']}r.{c}"])
    for c in range(6):
        P.op('act', lambda e, c=c: e.activation(out=C['r'][:, c, 0:n], in_=C['r'][:, c, 0:n], func=AF.Sqrt, bias=k.cst[:, 1:2], scale=-1.0),
             [f"{C['pfx']}r.{c}", 'cst'], [f"{C['pfx']}r.{c}"])
        P.op('dve', lambda e, c=c: e.tensor_tensor(out=C['ig'][:, c, 0:n], in0=C['ig'][:, c, 0:n], in1=C['r'][:, c, 0:n], op=ALU.mult),
             [f"{C['pfx']}r.{c}", f"{C['pfx']}ig.{c}"], [f"{C['pfx']}ig.{c}"])
        P.op('pool', lambda e, c=c: e.tensor_tensor(out=C['ig'][:, c, 0:n], in0=C['ig'][:, c, 0:n], in1=xa[:, c, 0:n], op=ALU.mult),
             [f"{C['pfx']}ig.{c}", f"{C['xaname']}.{c}"], [f"{C['pfx']}ig.{c}"])


def coeff_bufs(k, st, pfx, share=None):
    C = {'pfx': pfx}
    for nm in ('r', 'ig', 'a'):
        C[nm] = k.sb(pfx + nm, [128, 6, 512], F32, st)
    if share is None:
        C['pg'] = [k.ps(pfx + f'pg{i}', [128, 512], F32, st) for i in range(4)]
        C['pgname'] = [pfx + f'pg{i}' for i in range(4)]
    else:
        C['pg'], C['pgname'] = share['pg'], share['pgname']
    return C


def l0_phaseB(k, seg):
    P, lp = k.P, k.lp
    with ExitStack() as st:
        sb = lambda n, s, d: k.sb(n, s, d, st)
        sets = []
        for j in range(2):
            Bf = {}
            Bf['zin'] = sb(f'B{j}_zin', [128, 6, 516], F32)
            Bf['xa'] = sb(f'B{j}_xa', [128, 6, 512], F32)
            Bf['xab'] = sb(f'B{j}_xab', [128, 6, 512], BF16)
            Bf['hf'] = sb('B_hf', [128, 6, 512], F32) if j == 0 else sets[0]['hf']
            C = coeff_bufs(k, st, f'B{j}_', share=(sets[0]['C'] if j == 1 else None))
            C['xabname'], C['xaname'] = f'B{j}_xab', f'B{j}_xa'
            Bf['C'] = C
            sets.append(Bf)
        zv = seg.z[0:768, :].rearrange("(c p) t -> p c t", p=128)
        xav = seg.xa.rearrange("(c p) t -> p c t", p=128)
        hfv = seg.hf.rearrange("(c p) t -> p c t", p=128)
        if seg.nm == 'c':
            P.op('dve', lambda e: e.memset(lp['state'][:], 0.0), [], keys('p_state0', 6) + keys('p_state1', 6))
        for ti, (s, n) in enumerate(seg.tiles):
            j = ti % 2
            Bf = sets[j]
            zin, xa, xab, hf, C = Bf['zin'], Bf['xa'], Bf['xab'], Bf['hf'], Bf['C']
            pf = f'B{j}_'
            P.dma('sp', zin[:, :, 0:n + 4], zv[:, :, PADZ + s - 2:PADZ + s + n + 2], [], keys(pf + 'zin', 6), pf + 'zin')
            for c in range(6):
                P.op('act', lambda e, c=c, xa=xa, zin=zin: e.activation(out=xa[:, c, 0:n], in_=zin[:, c, 0:n], func=AF.Identity,
                                                                        scale=lp['lru_cw'][:, c, 0:1], bias=lp['lru_cb'][:, c:c + 1]),
                     [f'{pf}zin.{c}', 'p_lru_cw', 'p_lru_cb'], [f'{pf}xa.{c}'])
                for t in range(1, 5):
                    P.op('dve', lambda e, c=c, t=t, xa=xa, zin=zin: e.scalar_tensor_tensor(out=xa[:, c, 0:n], in0=zin[:, c, t:t + n], scalar=lp['lru_cw'][:, c, t:t + 1],
                                                                                           in1=xa[:, c, 0:n], op0=ALU.mult, op1=ALU.add),
                         [f'{pf}zin.{c}', f'{pf}xa.{c}'], [f'{pf}xa.{c}'])
                P.op('act', lambda e, c=c, xa=xa, xab=xab: e.copy(out=xab[:, c, 0:n], in_=xa[:, c, 0:n]), [f'{pf}xa.{c}'], [f'{pf}xab.{c}'])
            P.dma('sp', xav[:, :, s:s + n], xa[:, :, 0:n], keys(pf + 'xa', 6), [], pf + 'xa')
            lru_coeffs(k, C, 0, n, xa, xab)
            for c in range(6):
                P.op('dve', lambda e, c=c, hf=hf, C=C: e.tensor_tensor_scan(out=hf[:, c, 0:n], data0=C['a'][:, c, 0:n], data1=C['ig'][:, c, 0:n],
                                                                          initial=lp['state'][:, 0, c:c + 1], op0=ALU.mult, op1=ALU.add),
                     [f'{pf}a.{c}', f'{pf}ig.{c}', f'p_state0.{c}'], [f'B_hf.{c}'])
                P.op('dve', lambda e, c=c, hf=hf: e.tensor_copy(out=lp['state'][:, 0, c:c + 1], in_=hf[:, c, n - 1:n]), [f'B_hf.{c}'], [f'p_state0.{c}'])
            P.dma('sp', hfv[:, :, s:s + n], hf[:, :, 0:n], keys('B_hf', 6), [], 'B_hf')


def l0_phaseC(k, seg):
    P, lp, I = k.P, k.lp, k.I
    with ExitStack() as st:
        sb = lambda n, s, d: k.sb(n, s, d, st)
        ps = lambda n, s, d: k.ps(n, s, d, st)
        xa = sb('C_xa', [128, 6, 512], F32)
        xab = sb('C_xab', [128, 6, 512], BF16)
        hb = sb('C_hb', [128, 6, 512], F32)
        hf = sb('C_hf', [128, 6, 512], F32)
        ga = sb('C_ga', [128, 6, 512], F32)
        yT = sb('C_yT', [128, 8, 512], BF16)
        zb = sb('C_zb', [128, 2, 528], F32)
        p2 = sb('C_p2', [128, 528], F32)
        p4 = sb('C_p4', [128, 528], F32)
        p8 = sb('C_p8', [128, 528], F32)
        Qw = sb('C_Qw', [128, 516], F32)
        Ssum = sb('C_S', [128, 512], F32)
        dd = sb('C_dd', [128, 2, 512], BF16)
        xt = sb('C_xt', [128, 4, D], F32)
        xo = sb('C_xo', [128, 4, D], F32)
        C = coeff_bufs(k, st, 'C_')
        C['xabname'], C['xaname'] = 'C_xab', 'C_xa'
        po = [ps(f'C_po{i}', [128, 512], F32) for i in range(4)]
        zg = seg.z[768:1536, :].rearrange("(c p) t -> p c t", p=128)
        zbv = seg.z[1536:1792, :].rearrange("(c p) t -> p c t", p=128)
        xav = seg.xa.rearrange("(c p) t -> p c t", p=128)
        hfv = seg.hf.rearrange("(c p) t -> p c t", p=128)
        if seg.nm == 'c':
            P.op('dve', lambda e: e.memset(lp['state'][:, 1, :], 0.0), [], keys('p_state1', 6))
        nt_ = len(seg.tiles)
        for ti in range(nt_ - 1, -1, -1):
            s, n = seg.tiles[ti]
            ng = n // 128
            P.dma('sp', xa[:, :, 0:n], xav[:, :, s:s + n], [], keys('C_xa', 6), 'C_xa')
            P.dma('sp', hf[:, :, 0:n], hfv[:, :, s:s + n], [], keys('C_hf', 6), 'C_hf')
            P.dma('sp', ga[:, :, 0:n], zg[:, :, PADZ + s:PADZ + s + n], [], keys('C_ga', 6), 'C_ga')
            P.dma('sp', zb[:, :, 0:n + 16], zbv[:, :, PADZ + s - 8:PADZ + s + n + 8], [], keys('C_zb', 2), 'C_zb')
            P.dma('sp', xt[:, 0:ng, :], seg.x[s:s + n, :].rearrange("(g p) f -> p g f", p=128), [], ['C_xt'], 'C_xt')
            W = n + 16
            n1 = n + 1
            for ch in range(2):
                zc = zb[:, ch, :]
                P.op('dve', lambda e, zc=zc: e.tensor_tensor(out=p2[:, 0:W - 1], in0=zc[:, 0:W - 1], in1=zc[:, 1:W], op=ALU.add),
                     [f'C_zb.{ch}'], ['C_p2'])
                if ch == 0:
                    P.op('dve', lambda e: e.tensor_copy(out=Qw[0:64, 0:n1], in_=p2[0:64, 7:7 + n1]), ['C_p2'], ['C_Qw'])
                    P.op('dve', lambda e: e.tensor_tensor(out=Qw[64:128, 0:n1], in0=p2[64:128, 6:6 + n1], in1=p2[64:128, 8:8 + n1], op=ALU.add),
                         ['C_p2'], ['C_Qw'])
                else:
                    P.op('dve', lambda e: e.tensor_tensor(out=p4[:, 0:W - 3], in0=p2[:, 0:W - 3], in1=p2[:, 2:W - 1], op=ALU.add), ['C_p2'], ['C_p4'])
                    P.op('dve', lambda e: e.tensor_tensor(out=Qw[0:64, 0:n1], in0=p4[0:64, 4:4 + n1], in1=p4[0:64, 8:8 + n1], op=ALU.add),
                         ['C_p4'], ['C_Qw'])
                    P.op('dve', lambda e: e.tensor_tensor(out=p8[64:128, 0:W - 7], in0=p4[64:128, 0:W - 7], in1=p4[64:128, 4:W - 3], op=ALU.add),
                         ['C_p4'], ['C_p8'])
                    P.op('dve', lambda e: e.tensor_tensor(out=Qw[64:128, 0:n1], in0=p8[64:128, 0:n1], in1=p8[64:128, 8:8 + n1], op=ALU.add),
                         ['C_p8'], ['C_Qw'])
                P.op('dve', lambda e: e.tensor_scalar(out=Ssum[:, 0:n], in0=Qw[:, 0:n], scalar1=lp['pool_fl'][:, 0:1], scalar2=None, op0=ALU.mult),
                     ['C_Qw', 'p_pool_fl'], ['C_S'])
                P.op('dve', lambda e: e.scalar_tensor_tensor(out=Ssum[:, 0:n], in0=Qw[:, 1:n1], scalar=lp['pool_fl'][:, 1:2], in1=Ssum[:, 0:n],
                                                             op0=ALU.mult, op1=ALU.add), ['C_Qw', 'C_S', 'p_pool_fl'], ['C_S'])
                if ti == 0:
                    P.op('dve', lambda e, ch=ch: e.tensor_tensor(out=Ssum[:, 0:16], in0=Ssum[:, 0:16], in1=lp['pool_corr'][:, ch, 0, :], op=ALU.mult),
                         ['C_S', 'p_pool_corr'], ['C_S'])
                if ti == nt_ - 1:
                    P.op('dve', lambda e, ch=ch: e.tensor_tensor(out=Ssum[:, n - 16:n], in0=Ssum[:, n - 16:n], in1=lp['pool_corr'][:, ch, 1, :], op=ALU.mult),
                         ['C_S', 'p_pool_corr'], ['C_S'])
                P.op('dve', lambda e, ch=ch, zc=zc: e.scalar_tensor_tensor(out=dd[:, ch, 0:n], in0=Ssum[:, 0:n], scalar=lp['pool_invw'][:, ch:ch + 1],
                                                                          in1=zc[:, 8:8 + n], op0=ALU.mult, op1=ALU.subtract),
                     ['C_S', f'C_zb.{ch}', 'p_pool_invw'], [f'C_dd.{ch}'])
                pp = po[ch]
                P.op('pe', lambda e, ch=ch, pp=pp: e.matmul(pp[:, 0:n], lhsT=lp['pool_bd'][:, ch, :], rhs=dd[:, ch, 0:n], start=True, stop=True),
                     ['p_pool_bd', f'C_dd.{ch}'], [f'C_po{ch}'])
                P.op('act', lambda e, ch=ch, pp=pp: e.activation(out=yT[:, 6 + ch, 0:n], in_=pp[:, 0:n], func=AF.Identity, scale=lp['pool_scale'][:, ch:ch + 1], bias=k.cst[:, 2:3]),
                     [f'C_po{ch}', 'p_pool_scale', 'cst'], [f'C_yT.{6 + ch}'])
            for c in range(6):
                P.op('act', lambda e, c=c: e.copy(out=xab[:, c, 0:n], in_=xa[:, c, 0:n]), [f'C_xa.{c}'], [f'C_xab.{c}'])
            lru_coeffs(k, C, 1, n, xa, xab)
            for c in range(6):
                P.op('dve', lambda e, c=c: e.tensor_tensor_scan(out=rev(hb[:, c, 0:n]), data0=rev(C['a'][:, c, 0:n]), data1=rev(C['ig'][:, c, 0:n]),
                                                                initial=lp['state'][:, 1, c:c + 1], op0=ALU.mult, op1=ALU.add),
                     [f'C_a.{c}', f'C_ig.{c}', f'p_state1.{c}'], [f'C_hb.{c}'])
                P.op('dve', lambda e, c=c: e.tensor_copy(out=lp['state'][:, 1, c:c + 1], in_=hb[:, c, 0:1]), [f'C_hb.{c}'], [f'p_state1.{c}'])
                P.op('act', lambda e, c=c: e.activation(out=ga[:, c, 0:n], in_=ga[:, c, 0:n], func=AF.Gelu_apprx_tanh), [f'C_ga.{c}'], [f'C_ga.{c}'])
                P.op('pool', lambda e, c=c: e.tensor_tensor(out=hb[:, c, 0:n], in0=hb[:, c, 0:n], in1=hf[:, c, 0:n], op=ALU.add),
                     [f'C_hb.{c}', f'C_hf.{c}'], [f'C_hb.{c}'])
                P.op('dve', lambda e, c=c: e.tensor_tensor(out=yT[:, c, 0:n], in0=hb[:, c, 0:n], in1=ga[:, c, 0:n], op=ALU.mult),
                     [f'C_hb.{c}', f'C_ga.{c}'], [f'C_yT.{c}'])
            for g in range(ng):
                for nt in range(2):
                    pp = po[(2 * g + nt) % 4]
                    ppn = f'C_po{(2 * g + nt) % 4}'
                    for kc in range(8):
                        P.op('pe', lambda e, g=g, nt=nt, kc=kc, pp=pp: e.matmul(pp[:, :], lhsT=yT[:, kc, g * 128:(g + 1) * 128], rhs=k.w_out0[:, kc, nt * 512:(nt + 1) * 512],
                                                                               start=(kc == 0), stop=(kc == 7)),
                             [f'C_yT.{kc}', f'w_out0.{kc}'], [ppn])
                    P.op('dve', lambda e, g=g, nt=nt, pp=pp: e.tensor_tensor(out=xo[:, g, nt * 512:(nt + 1) * 512], in0=pp[:, :], in1=xt[:, g, nt * 512:(nt + 1) * 512], op=ALU.add),
                         [ppn, 'C_xt'], [f'C_xo.{g}'])
            P.dma('sp', seg.x05[1 + s:1 + s + n, :].rearrange("(g p) f -> p g f", p=128), xo[:, 0:ng, :], keys('C_xo', 4)[0:ng], [], 'C_xo')


def ffn_phase(k, l, s_, src, dst, TT, final, flush=True, out_rows=None):
    nc, P, I = k.nc, k.P, k.I
    tiles = tiles_of(TT)
    with ExitStack() as st:
        sb = lambda n, s, d: k.sb(n, s, d, st)
        ps = lambda n, s, d: k.ps(n, s, d, st)
        w_up = sb('F_wup', [128, 8, 2 * DFF], BF16)
        load_w_bf16(k, 'F_wup', w_up, I['ffn_w_up'][l], 8, 2 * DFF)
        w_dn = sb('F_wdn', [128, NFC, D], BF16)
        with ExitStack() as st2:
            fold_w_bf16(k, st2, 'F_wdn', w_dn, I['ffn_w_down'][l], NFC, k.gbc_d[l, s_, 1])
            P.barrier()
        cw = sb('F_cw', [128, 2 * NFC, 3], F32)
        cb = sb('F_cb', [128, 2 * NFC], F32)
        P.dma('sp', cw[:], I['ffn_cw'][:, l, :, :], [], ['F_cw'], 'F_cw')
        P.dma('sp', cb[:], I['ffn_cb'][:, l, :], [], ['F_cb'], 'F_cb')
        B = mod_bufs(k, st, 'F_')
        hT = sb('F_hT', [128, 8, 512], BF16)
        gT = sb('F_gT', [128, NFC, 512], BF16)
        prevu = [sb(f'F_prevu{i}', [128, 2 * NFC, 2], F32) for i in range(2)]
        P.op('dve', lambda e: e.memset(prevu[0][:], 0.0), [], keys('F_prevu0', 2 * NFC))
        acc = [sb(f'F_acc{i}', [128, 512], F32) for i in range(4)]
        corr = sb('F_corr', [128, 2 * NFC, 2], F32)
        ctmp = sb('F_ctmp', [128, 2 * NFC], F32)
        xs = sb('F_xs', [128, 2, D], F32)
        xo = xs
        pu = [ps(f'F_pu{i}', [128, 512], F32) for i in range(4)]
        pd = [ps(f'F_pd{i}', [128, 512], F32) for i in range(2)]
        if final:
            fg = sb('F_fg', [128, D], F32)
            P.dma('sp', fg[:], I['final_g_bc'], [], ['F_fg'], 'F_fg')
            fss = sb('F_fss', [128, 12], F32)
            fjunk = B['junk']

        if os.environ.get('DBG_SBUF'):
            print('FFN sbuf remaining', nc.sbuf_bytes_remaining, 'final', final)
        def conv_gate(n, zero_u, ti):
            pin, pout = prevu[ti % 2], prevu[(ti + 1) % 2]
            pinn, poutn = f'F_prevu{ti % 2}', f'F_prevu{(ti + 1) % 2}'
            allin = keys(pinn, 2 * NFC)
            P.op('dve', lambda e: e.tensor_tensor(out=corr[:, :, 0], in0=cw[:, :, 0], in1=pin[:, :, 0], op=ALU.mult), allin + ['F_cw'], ['F_corr'])
            P.op('dve', lambda e: e.tensor_tensor(out=ctmp[:, :], in0=cw[:, :, 1], in1=pin[:, :, 1], op=ALU.mult), allin + ['F_cw'], ['F_ctmp'])
            P.op('dve', lambda e: e.tensor_tensor(out=corr[:, :, 0], in0=corr[:, :, 0], in1=ctmp[:, :], op=ALU.add), ['F_corr', 'F_ctmp'], ['F_corr'])
            P.op('dve', lambda e: e.tensor_tensor(out=corr[:, :, 1], in0=cw[:, :, 0], in1=pin[:, :, 1], op=ALU.mult), allin + ['F_cw', 'F_corr'], ['F_corr'])
            for c in range(NFC):
                q = c % 2
                AA = [acc[2 * q], acc[2 * q + 1]]
                AN = [f'F_acc{2 * q}', f'F_acc{2 * q + 1}']
                PP = [pu[2 * q], pu[2 * q + 1]]
                PN = [f'F_pu{2 * q}', f'F_pu{2 * q + 1}']
                CC = [c, NFC + c]
                if not zero_u:
                    for vi in range(2):
                        for kc in range(8):
                            P.op('pe', lambda e, kc=kc, cc=CC[vi], pp=PP[vi]: e.matmul(pp[:, 0:n], lhsT=w_up[:, kc, cc * 128:(cc + 1) * 128], rhs=hT[:, kc, 0:n],
                                                                                      start=(kc == 0), stop=(kc == 7)),
                                 [f'F_wup.{kc}', f'F_hT.{kc}'], [PN[vi]])
                    for vi in range(2):
                        P.op('act', lambda e, A_=AA[vi], pp=PP[vi], cc=CC[vi]: e.activation(out=A_[:, 0:n], in_=pp[:, 0:n], func=AF.Identity,
                                                                                          scale=cw[:, cc, 2:3], bias=cb[:, cc:cc + 1]),
                             [PN[vi], 'F_cw', 'F_cb'], [AN[vi]])
                    for vi in range(2):
                        if os.environ.get('DBG_SKIP_SAVE'):
                            continue
                        P.op('dve', lambda e, pp=PP[vi], cc=CC[vi]: e.tensor_copy(out=pout[:, cc, :], in_=pp[:, n - 2:n]), [PN[vi]], [f'{poutn}.{CC[vi]}'])
                    for vi in range(2):
                        P.op('dve', lambda e, A_=AA[vi], pp=PP[vi], cc=CC[vi]: e.scalar_tensor_tensor(out=A_[:, 1:n], in0=pp[:, 0:n - 1], scalar=cw[:, cc, 1:2], in1=A_[:, 1:n],
                                                                                                    op0=ALU.mult, op1=ALU.add), [PN[vi], AN[vi]], [AN[vi]])
                    for vi in range(2):
                        P.op('dve', lambda e, A_=AA[vi], pp=PP[vi], cc=CC[vi]: e.scalar_tensor_tensor(out=A_[:, 2:n], in0=pp[:, 0:n - 2], scalar=cw[:, cc, 0:1], in1=A_[:, 2:n],
                                                                                                    op0=ALU.mult, op1=ALU.add), [PN[vi], AN[vi]], [AN[vi]])
                else:
                    for vi in range(2):
                        P.op('act', lambda e, A_=AA[vi], cc=CC[vi]: e.activation(out=A_[:, 0:n], in_=k.zero[:, 0:n], func=AF.Identity,
                                                                                scale=cw[:, cc, 2:3], bias=cb[:, cc:cc + 1]),
                             ['zero', 'F_cw', 'F_cb'], [AN[vi]])
                for vi in range(2):
                    P.op('pool', lambda e, A_=AA[vi], cc=CC[vi]: e.tensor_tensor(out=A_[:, 0:2], in0=A_[:, 0:2], in1=corr[:, cc, :], op=ALU.add),
                         ['F_corr', AN[vi]], [AN[vi]])
                P.op('act', lambda e, A_=AA[1]: e.activation(out=A_[:, 0:n], in_=A_[:, 0:n], func=AF.Silu), [AN[1]], [AN[1]])
                P.op('pool', lambda e, c=c, A0=AA[0], A1=AA[1]: e.tensor_tensor(out=gT[:, c, 0:n], in0=A0[:, 0:n], in1=A1[:, 0:n], op=ALU.mult),
                     [AN[0], AN[1]], [f'F_gT.{c}'])

        def down_res(tok0, n, nrows_last=128):
            ng = n // 128
            nr = lambda g: (nrows_last if g == ng - 1 else 128)
            for g_ in range(ng):
                g = g_ % 2
                r0 = 1 + tok0 + g_ * 128
                P.dma('sp', xs[0:nr(g_), g, :], src[r0:r0 + nr(g_), :], [], [f'F_xs.{g}'], f'F_xs{g}')
                for nt in range(2):
                    pp = pd[nt]
                    for kc in range(NFC):
                        P.op('pe', lambda e, g_=g_, nt=nt, kc=kc, pp=pp: e.matmul(pp[:, :], lhsT=gT[:, kc, g_ * 128:(g_ + 1) * 128], rhs=w_dn[:, kc, nt * 512:(nt + 1) * 512],
                                                                               start=(kc == 0), stop=(kc == NFC - 1)),
                             [f'F_gT.{kc}', f'F_wdn.{kc}'], [f'F_pd{nt}'])
                    P.op('dve', lambda e, g=g, g_=g_, nt=nt, pp=pp: e.tensor_tensor(out=xo[0:nr(g_), g, nt * 512:(nt + 1) * 512], in0=pp[0:nr(g_), :],
                                                                            in1=xs[0:nr(g_), g, nt * 512:(nt + 1) * 512], op=ALU.add),
                         [f'F_pd{nt}', f'F_xs.{g}'], [f'F_xs.{g}'])
                if final:
                    P.op('dve', lambda e: e.memset(fss[:, 0:1], 0.0), [], ['F_fss'])
                    P.op('act', lambda e, g=g: e.activation(out=fjunk[:], in_=xo[:, g, :], func=AF.Square, accum_out=fss[:, 0:1]),
                         [f'F_xs.{g}'], ['F_fss', 'F_junk'])
                    P.op('act', lambda e: e.activation(out=fss[:, 1:2], in_=fss[:, 0:1], func=AF.Sqrt, bias=k.cst[:, 0:1], scale=1.0 / D),
                         ['F_fss', 'cst'], ['F_fss'])
                    P.op('dve', lambda e: e.reciprocal(out=fss[:, 2:3], in_=fss[:, 1:2]), ['F_fss'], ['F_fss'])
                    P.op('dve', lambda e, g=g: e.scalar_tensor_tensor(out=xo[:, g, :], in0=xo[:, g, :], scalar=fss[:, 2:3], in1=fg[:],
                                                                     op0=ALU.mult, op1=ALU.mult), [f'F_xs.{g}', 'F_fss', 'F_fg'], [f'F_xs.{g}'])
                t0 = tok0 + g_ * 128
                if final:
                    lo = max(t0, 0)
                    hi = min(t0 + nr(g_), TT if out_rows is None else out_rows)
                    if hi > lo:
                        P.dma('sp', k.out[lo:hi, :], xo[lo - t0:hi - t0, g, :], [f'F_xs.{g}'], [], f'F_xs{g}')
                else:
                    P.dma('sp', dst[1 + t0:1 + t0 + nr(g_), :], xo[0:nr(g_), g, :], [f'F_xs.{g}'], [], f'F_xs{g}')

        modulate_tile(k, B, src[1 + tiles[0][0]:1 + tiles[0][0] + tiles[0][1], :], tiles[0][1], l, s_, 2, hT, 'F_hT', part=1)
        for ti, (s, n) in enumerate(tiles):
            modulate_tile(k, B, src[1 + s:1 + s + n, :], n, l, s_, 2, hT, 'F_hT', part=2)
            conv_gate(n, False, ti)
            if ti + 1 < len(tiles):
                s2, n2 = tiles[ti + 1]
                modulate_tile(k, B, src[1 + s2:1 + s2 + n2, :], n2, l, s_, 2, hT, 'F_hT', part=1)
            down_res(s - 1, n)
        if flush:
            conv_gate(128, True, len(tiles))
            down_res(TT - 1, 128, nrows_last=1)


def layer1(k, stop_after):
    nc, P, I, S = k.nc, k.P, k.I, k.S
    T = k.T
    TK = T + LC
    NKB = TK // 128
    TL = k.TL
    yc_d = nc.dram_tensor('yc_d', [768, T], BF16, kind="Internal").ap()
    with ExitStack() as st:
      if not os.environ.get('DBG_ONLY_F1'):
          sb = lambda n, s, d: k.sb(n, s, d, st)
          ps = lambda n, s, d: k.ps(n, s, d, st)
          w_in = sb('w_in1', [128, 8, 2816], BF16)
          load_w_bf16(k, 'w_in1', w_in, I['cd_w_in'], 8, 2816, colblk=1408)
          w_sw = sb('w_sw1', [128, 8, 1536], BF16)
          load_w_bf16(k, 'w_sw1', w_sw, I['cd_w_sw'], 8, 1536, colblk=1536)
          B = mod_bufs(k, st, 'E_')
          hT = sb('E_hT', [128, 8, 512], BF16)
          qk = sb('E_qk', [128, 12, 512], BF16)
          vt = sb('E_vt', [128, 4, 768], BF16)
          gl = sb('E_gl', [128, 2, 512], F32)
          rc = sb('E_rc', [128, 512], F32)
          rs = sb('E_rs', [128, 512], F32)
          t1 = sb('E_t1', [128, 512], F32)
          t2 = sb('E_t2', [128, 512], F32)
          sg = sb('E_sg', [128, 512], F32)
          pa = [ps(f'E_pa{i}', [128, 512], F32) for i in range(2)]
          pb = [ps(f'E_pb{i}', [128, 512], F32) for i in range(2)]
          glv = S['gl'].rearrange("(c p) t -> p c t", p=128)
          P.dma('sp', glv[:, :, 0:PADZ], k.zero[:, 0:2 * PADZ].rearrange("p (c t) -> p c t", c=2), ['zero'], [], 'E_glpad')
          P.dma('sp', glv[:, :, PADZ + T:PADZ + T + PADZ], k.zero[:, 0:2 * PADZ].rearrange("p (c t) -> p c t", c=2), ['zero'], [], 'E_glpad')
          qTv = S['qT'].rearrange("(c p) t -> p c t", p=128)
          kTv = S['kT'].rearrange("(c p) t -> p c t", p=128)

          def proj_plain(cols0, nch, dst, dstname, d0, n):
              for c in range(nch):
                  pp = pa[c % 2]
                  for kc in range(8):
                      P.op('pe', lambda e, c=c, kc=kc, pp=pp: e.matmul(pp[:, 0:n], lhsT=w_in[:, kc, cols0 + c * 128:cols0 + (c + 1) * 128], rhs=hT[:, kc, 0:n],
                                                                      start=(kc == 0), stop=(kc == 7)), [f'w_in1.{kc}', f'E_hT.{kc}'], [f'E_pa{c % 2}'])
                  P.op('act', lambda e, c=c, pp=pp: e.copy(out=dst[:, d0 + c, 0:n], in_=pp[:, 0:n]), [f'E_pa{c % 2}'], [f'{dstname}.{d0 + c}'])

          def proj_v(n):
              ng = n // 128
              for g in range(ng):
                  for (c0, cn, pp, ppn) in ((0, 512, pa[g % 2], f'E_pa{g % 2}'), (512, 256, pb[g % 2], f'E_pb{g % 2}')):
                      for kc in range(8):
                          P.op('pe', lambda e, g=g, kc=kc, pp=pp, c0=c0, cn=cn: e.matmul(pp[:, 0:cn], lhsT=hT[:, kc, g * 128:(g + 1) * 128],
                                                                                         rhs=w_in[:, kc, 1536 + c0:1536 + c0 + cn], start=(kc == 0), stop=(kc == 7)),
                               [f'w_in1.{kc}', f'E_hT.{kc}'], [ppn])
                      P.op('dve', lambda e, g=g, pp=pp, c0=c0, cn=cn: e.tensor_copy(out=vt[:, g, c0:c0 + cn], in_=pp[:, 0:cn]), [ppn], [f'E_vt.{g}'])

          n = LC
          modulate_tile(k, B, S['x1_c'][1:1 + LC, :], n, 1, 1, 0, hT, 'E_hT')
          proj_plain(768, 6, qk, 'E_qk', 6, n)
          P.dma('sp', kTv[:, :, T:T + n], qk[:, 6:12, 0:n], keys('E_qk', 12)[6:12], [], 'E_qk')
          proj_v(n)
          P.dma('sp', S['v'][T:T + n, :].rearrange("(g p) f -> p g f", p=128), vt[:, 0:n // 128, :], keys('E_vt', 4), [], 'E_vt')
          etiles = tiles_of(T)
          modulate_tile(k, B, S['x1_l'][1 + etiles[0][0]:1 + etiles[0][0] + etiles[0][1], :], etiles[0][1], 1, 0, 0, hT, 'E_hT', part=1)
          for ti, (s, n) in enumerate(etiles):
              modulate_tile(k, B, S['x1_l'][1 + s:1 + s + n, :], n, 1, 0, 0, hT, 'E_hT', part=2)
              P.dma('sp', rc[:, 0:n], I['rope_c'][:, s:s + n], [], ['E_rc'], 'E_rc')
              P.dma('sp', rs[:, 0:n], I['rope_s'][:, s:s + n], [], ['E_rs'], 'E_rs')
              need_q = s < TL + 16
              for c in (range(12) if need_q else range(6, 12)):
                  pp, pq = pa[c % 2], pb[c % 2]
                  for kc in range(8):
                      P.op('pe', lambda e, c=c, kc=kc, pp=pp: e.matmul(pp[:, 0:n], lhsT=w_in[:, kc, c * 128:(c + 1) * 128], rhs=hT[:, kc, 0:n],
                                                                      start=(kc == 0), stop=(kc == 7)), [f'w_in1.{kc}', f'E_hT.{kc}'], [f'E_pa{c % 2}'])
                  for kc in range(8):
                      P.op('pe', lambda e, c=c, kc=kc, pq=pq: e.matmul(pq[:, 0:n], lhsT=w_sw[:, kc, c * 128:(c + 1) * 128], rhs=hT[:, kc, 0:n],
                                                                      start=(kc == 0), stop=(kc == 7)), [f'w_sw1.{kc}', f'E_hT.{kc}'], [f'E_pb{c % 2}'])
                  P.op('dve', lambda e, pp=pp: e.tensor_tensor(out=t1[:, 0:n], in0=pp[:, 0:n], in1=rc[:, 0:n], op=ALU.mult), [f'E_pa{c % 2}', 'E_rc'], ['E_t1'])
                  P.op('dve', lambda e, pq=pq: e.tensor_tensor(out=t2[:, 0:n], in0=pq[:, 0:n], in1=rs[:, 0:n], op=ALU.mult), [f'E_pb{c % 2}', 'E_rs'], ['E_t2'])
                  P.op('pool', lambda e, c=c: e.tensor_tensor(out=qk[:, c, 0:n], in0=t1[:, 0:n], in1=t2[:, 0:n], op=ALU.add), ['E_t1', 'E_t2'], [f'E_qk.{c}'])
              if ti + 1 < len(etiles):
                  s2, n2 = etiles[ti + 1]
                  modulate_tile(k, B, S['x1_l'][1 + s2:1 + s2 + n2, :], n2, 1, 0, 0, hT, 'E_hT', part=1)
              if need_q:
                  P.dma('sp', qTv[:, :, s:s + n], qk[:, 0:6, 0:n], keys('E_qk', 12)[0:6], [], 'E_q')
              P.dma('sp', kTv[:, :, s:s + n], qk[:, 6:12, 0:n], keys('E_qk', 12)[6:12], [], 'E_qk')
              proj_v(n)
              P.dma('sp', S['v'][s:s + n, :].rearrange("(g p) f -> p g f", p=128), vt[:, 0:n // 128, :], keys('E_vt', 4), [], 'E_vt')
              for c in (range(2) if need_q else []):
                  pp, pq = pa[c % 2], pb[c % 2]
                  for (pz, pzn, cols) in ((pp, f'E_pa{c % 2}', 2304 + c * 128), (pq, f'E_pb{c % 2}', 2304 + 256 + c * 128)):
                      for kc in range(8):
                          P.op('pe', lambda e, kc=kc, pz=pz, cols=cols: e.matmul(pz[:, 0:n], lhsT=w_in[:, kc, cols:cols + 128], rhs=hT[:, kc, 0:n],
                                                                                start=(kc == 0), stop=(kc == 7)), [f'w_in1.{kc}', f'E_hT.{kc}'], [pzn])
                  P.op('act', lambda e, pq=pq: e.activation(out=sg[:, 0:n], in_=pq[:, 0:n], func=AF.Sigmoid), [f'E_pb{c % 2}'], ['E_sg'])
                  P.op('dve', lambda e, c=c, pp=pp: e.tensor_tensor(out=gl[:, c, 0:n], in0=pp[:, 0:n], in1=sg[:, 0:n], op=ALU.mult), [f'E_pa{c % 2}', 'E_sg'], [f'E_gl.{c}'])
              if need_q:
                  P.dma('sp', glv[:, :, PADZ + s:PADZ + s + n], gl[:, :, 0:n], keys('E_gl', 2), [], 'E_gl')
    P.barrier()
    if stop_after == 'l1E':
        return

    with ExitStack() as st:
        sb = lambda n, s, d: k.sb(n, s, d, st)
        ps = lambda n, s, d: k.ps(n, s, d, st)
        dl = sb('G_dl', [1, 4, 64], F32)
        P.dma('sp', dl[:], I['diff_l'], [], ['G_dl'], 'G_dl')
        sm = sb('G_sm', [1, 8], F32)
        pr_ = sb('G_pr', [1, 2, 64], F32)
        P.op('dve', lambda e: e.tensor_tensor(out=pr_[:, 0, :], in0=dl[:, 0, :], in1=dl[:, 1, :], op=ALU.mult), ['G_dl'], ['G_pr'])
        P.op('dve', lambda e: e.tensor_tensor(out=pr_[:, 1, :], in0=dl[:, 2, :], in1=dl[:, 3, :], op=ALU.mult), ['G_dl'], ['G_pr'])
        P.op('dve', lambda e: e.reduce_sum(out=sm[:, 0:1], in_=pr_[:, 0, :], axis=AX.X), ['G_pr'], ['G_sm'])
        P.op('dve', lambda e: e.reduce_sum(out=sm[:, 1:2], in_=pr_[:, 1, :], axis=AX.X), ['G_pr'], ['G_sm'])
        P.op('act', lambda e: e.activation(out=sm[:, 2:4], in_=sm[:, 0:2], func=AF.Exp), ['G_sm'], ['G_sm'])
        P.op('dve', lambda e: e.tensor_tensor(out=sm[:, 4:5], in0=sm[:, 3:4], in1=sm[:, 2:3], op=ALU.subtract), ['G_sm'], ['G_sm'])
        P.op('dve', lambda e: e.tensor_scalar(out=sm[:, 5:6], in0=sm[:, 4:5], scalar1=-LAMBDA_INIT1, scalar2=None, op0=ALU.add), ['G_sm'], ['G_sm'])
        neglam = sb('G_neglam', [128, 2], F32)
        gsub = sb('G_gsub', [128, 2], F32)
        P.dma('sp', gsub[:, 0:1], I['subln_g'], [], ['G_gsub'], 'G_gsub')
        P.op('dve', lambda e: e.tensor_scalar(out=gsub[:, 1:2], in0=gsub[:, 0:1], scalar1=(1.0 - LAMBDA_INIT1), scalar2=None, op0=ALU.mult), ['G_gsub'], ['G_gsub'])
        pS = [ps(f'G_pS{i}', [128, 2, 512], F32) for i in range(2)]
        po = [ps(f'G_po{i}', [128, 512], F32) for i in range(2)]
        pl = ps('G_pl', [128, 2, 512], F32)
        P.op('pe', lambda e: e.matmul(pl[:, 0, 0:1], lhsT=k.ones_f[0:1, :], rhs=sm[0:1, 5:6], start=True, stop=True), ['ones_f', 'G_sm'], ['G_pl0', 'G_pl1'])
        P.op('dve', lambda e: e.tensor_copy(out=neglam[:, 0:1], in_=pl[:, 0, 0:1]), ['G_pl0', 'G_pl1'], ['G_neglam'])
        kh = [sb(f'G_kh{j}', [128, TK], BF16) for j in range(2)]
        vh = [sb(f'G_vh{j}', [128, NKB, 128], BF16) for j in range(2)]
        qh = [sb(f'G_qh{j}', [128, 512], BF16) for j in range(2)]
        pT = [sb(f'G_pT{i}', [128, 2, 512], BF16) for i in range(4)]
        accs = [sb(f'G_acc{i}', [128, 2, 512], F32) for i in range(2)]
        rl = sb('G_rl', [128, 2, 512], F32)
        o1 = sb('G_o1', [128, 512], F32)
        o2 = sb('G_o2', [128, 512], F32)
        sq = sb('G_sq', [128, 512], F32)
        ych = sb('G_ych', [128, 512], BF16)
        vv = S['v'].rearrange("(kb p) f -> p kb f", p=128)
        qtiles = tiles_of(TL)

        def load_head(h):
            j = h % 2
            P.dma('sp', kh[j][:, :], S['kT'][h * 128:h * 128 + 128, :], [], [f'G_kh{j}'], f'G_kh{j}')
            for b0 in range(0, NKB, 16):
                b1 = min(NKB, b0 + 16)
                P.dma('sp', vh[j][:, b0:b1, :], vv[:, b0:b1, h * 128:(h + 1) * 128], [], [f'G_vh{j}'], f'G_vh{j}')

        def load_q(h, ti):
            s, n = qtiles[ti]
            gi = (h * len(qtiles) + ti) % 2
            P.dma('sp', qh[gi][:, 0:n], S['qT'][h * 128:(h + 1) * 128, s:s + n], [], [f'G_qh{gi}'], f'G_qh{gi}')

        load_head(0)
        load_q(0, 0)
        for h in range(6):
            hj = h % 2
            for ti, (s, n) in enumerate(qtiles):
                gi = (h * len(qtiles) + ti) % 2
                qcur = qh[gi]
                qn_ = f'G_qh{gi}'
                if ti + 1 < len(qtiles):
                    load_q(h, ti + 1)
                elif h + 1 < 6:
                    load_q(h + 1, 0)
                if ti == 0 and h + 1 < 6:
                    load_head(h + 1)

                def emit_qk(kb):
                    pp = pS[kb % 2]
                    for comp in range(2):
                        P.op('pe', lambda e, comp=comp, kb=kb, pp=pp: e.matmul(pp[:, comp, 0:n], lhsT=kh[hj][64 * comp:64 * comp + 64, kb * 128:(kb + 1) * 128], rhs=qcur[64 * comp:64 * comp + 64, 0:n], start=True, stop=True),
                             [f'G_kh{hj}', qn_], [f'G_pS{kb % 2}'])
                emit_qk(0)
                first = [True, True]
                for kb in range(NKB):
                    if kb + 1 < NKB:
                        emit_qk(kb + 1)
                    pp = pS[kb % 2]
                    ppn = f'G_pS{kb % 2}'
                    pt = pT[kb % 4]
                    ptn = f'G_pT{kb % 4}'
                    P.op('act', lambda e, pp=pp, pt=pt: e.activation(out=pt[:, :, 0:n], in_=pp[:, :, 0:n], func=AF.Exp, scale=0.125), [ppn], [ptn])
                    for comp in range(2):
                        P.op('pe', lambda e, comp=comp, kb=kb, pt=pt: e.matmul(po[comp][:, 0:n], lhsT=vh[hj][:, kb, :], rhs=pt[:, comp, 0:n], start=(kb == 0), stop=(kb == NKB - 1)),
                             [f'G_vh{hj}', ptn], [f'G_po{comp}'])
                    ac = accs[1]
                    if kb % 3 == 2:
                        if kb == 2:
                            P.op('dve', lambda e, ac=ac, pt=pt: e.tensor_copy(out=ac[:, 0, 0:n], in_=pt[:, 0, 0:n]), [ptn], ['G_acc0'])
                        else:
                            P.op('dve', lambda e, ac=ac, pt=pt: e.tensor_tensor(out=ac[:, 0, 0:n], in0=ac[:, 0, 0:n], in1=pt[:, 0, 0:n], op=ALU.add), [ptn, 'G_acc0'], ['G_acc0'])
                    else:
                        P.op('pe', lambda e, kb=kb, pt=pt: e.matmul(pl[:, 0, 0:n], lhsT=k.ones_bf[:], rhs=pt[:, 0, 0:n], start=(kb == 0), stop=False),
                             ['ones_bf', ptn], ['G_pl0'])
                    if kb == 0:
                        P.op('dve', lambda e, ac=ac, pt=pt: e.tensor_copy(out=ac[:, 1, 0:n], in_=pt[:, 1, 0:n]), [ptn], ['G_acc1'])
                    else:
                        P.op('dve', lambda e, ac=ac, pt=pt: e.tensor_tensor(out=ac[:, 1, 0:n], in0=ac[:, 1, 0:n], in1=pt[:, 1, 0:n], op=ALU.add), [ptn, 'G_acc1'], ['G_acc1'])
                P.op('pe', lambda e: e.matmul(pl[:, 0, 0:n], lhsT=k.ones_f[:], rhs=accs[1][:, 0, 0:n], start=False, stop=True), ['ones_f', 'G_acc0'], ['G_pl0'])
                P.op('pe', lambda e: e.matmul(pl[:, 1, 0:n], lhsT=k.ones_f[:], rhs=accs[1][:, 1, 0:n], start=True, stop=True), ['ones_f', 'G_acc1'], ['G_pl1'])
                P.op('dve', lambda e: e.reciprocal(out=rl[:, :, 0:n], in_=pl[:, :, 0:n]), ['G_pl0', 'G_pl1'], ['G_rl'])
                P.op('dve', lambda e: e.tensor_tensor(out=o1[:, 0:n], in0=po[0][:, 0:n], in1=rl[:, 0, 0:n], op=ALU.mult), ['G_po0', 'G_rl'], ['G_o1'])
                P.op('dve', lambda e: e.tensor_tensor(out=o2[:, 0:n], in0=po[1][:, 0:n], in1=rl[:, 1, 0:n], op=ALU.mult), ['G_po1', 'G_rl'], ['G_o2'])
                P.op('dve', lambda e: e.scalar_tensor_tensor(out=o1[:, 0:n], in0=o2[:, 0:n], scalar=neglam[:, 0:1], in1=o1[:, 0:n], op0=ALU.mult, op1=ALU.add),
                     ['G_o1', 'G_o2', 'G_neglam'], ['G_o1'])
                P.op('act', lambda e: e.activation(out=sq[:, 0:n], in_=o1[:, 0:n], func=AF.Square), ['G_o1'], ['G_sq'])
                P.op('pe', lambda e: e.matmul(pl[:, 0, 0:n], lhsT=k.ones_f[:], rhs=sq[:, 0:n], start=True, stop=True), ['ones_f', 'G_sq'], ['G_pl0', 'G_pl1'])
                P.op('act', lambda e: e.activation(out=sq[:, 0:n], in_=pl[:, 0, 0:n], func=AF.Sqrt, bias=k.cst[:, 0:1], scale=1.0 / 128), ['G_pl0', 'G_pl1', 'cst'], ['G_sq'])
                P.op('dve', lambda e: e.reciprocal(out=sq[:, 0:n], in_=sq[:, 0:n]), ['G_sq'], ['G_sq'])
                P.op('dve', lambda e: e.scalar_tensor_tensor(out=ych[:, 0:n], in0=o1[:, 0:n], scalar=gsub[:, 1:2], in1=sq[:, 0:n], op0=ALU.mult, op1=ALU.mult),
                     ['G_o1', 'G_sq', 'G_gsub'], ['G_ych'])
                P.dma('sp', yc_d[h * 128:(h + 1) * 128, s:s + n], ych[:, 0:n], ['G_ych'], [], 'G_ych')
    P.barrier()
    if stop_after == 'l1F1':
        return

    with ExitStack() as st:
        sb = lambda n, s, d: k.sb(n, s, d, st)
        ps = lambda n, s, d: k.ps(n, s, d, st)
        w_out = sb('w_out1', [128, 8, D], BF16)
        with ExitStack() as st2:
            fold_w_bf16(k, st2, 'w_out1', w_out, I['cd_w_out'], 8, k.gbc_d[1, 0, 0])
            P.barrier()
        cp = {}
        for nm, shp in (('conf_w', [128, 2, 31]), ('conf_b', [128, 2]), ('conf_lng', [128, 2]), ('conf_lnb', [128, 2])):
            cp[nm] = sb('H_' + nm, shp, F32)
            P.dma('sp', cp[nm][:], I[nm], [], ['H_' + nm], 'H_' + nm)
        yT = sb('H_yT', [128, 8, 512], BF16)
        gin = sb('H_gin', [128, 2, 542], F32)
        ca = sb('H_ca', [128, 512], F32)
        cb_ = sb('H_cb', [128, 512], F32)
        ct = [sb(f'H_ct{i}', [128, 512], F32) for i in range(2)]
        xm = sb('H_xm', [128, 2, 512], F32)
        sq = sb('H_sq', [128, 2, 512], F32)
        rstd = sb('H_rstd', [128, 512], F32)
        xt = sb('H_xt', [128, 4, D], F32)
        xo = sb('H_xo', [128, 4, D], F32)
        pm = ps('H_pm', [128, 512], F32)
        pv = ps('H_pv', [128, 512], F32)
        po = [ps(f'H_po{i}', [128, 512], F32) for i in range(4)]
        glv = S['gl'].rearrange("(c p) t -> p c t", p=128)
        ycv = yc_d.rearrange("(c p) t -> p c t", p=128)
        for (s, n) in tiles_of(TL):
            ng = n // 128
            P.dma('sp', yT[:, 0:6, 0:n], ycv[:, :, s:s + n], [], keys('H_yT', 8)[0:6], 'H_yT')
            P.dma('sp', gin[:, :, 0:n + 30], glv[:, :, PADZ + s - 15:PADZ + s + n + 15], [], keys('H_gin', 2), 'H_gin')
            P.dma('sp', xt[:, 0:ng, :], S['x1_l'][1 + s:1 + s + n, :].rearrange("(g p) f -> p g f", p=128), [], ['H_xt'], 'H_xt')
            for c in range(2):
                P.op('act', lambda e, c=c: e.activation(out=ca[:, 0:n], in_=gin[:, c, 0:n], func=AF.Identity, scale=cp['conf_w'][:, c, 0:1], bias=cp['conf_b'][:, c:c + 1]),
                     [f'H_gin.{c}', 'H_conf_w', 'H_conf_b'], ['H_ca'])
                P.op('pool', lambda e, c=c: e.tensor_scalar(out=cb_[:, 0:n], in0=gin[:, c, 1:1 + n], scalar1=cp['conf_w'][:, c, 1:2], scalar2=None, op0=ALU.mult),
                     [f'H_gin.{c}', 'H_conf_w'], ['H_cb'])
                for t in range(2, 31):
                    if t % 2 == 0:
                        P.op('dve', lambda e, c=c, t=t: e.scalar_tensor_tensor(out=ca[:, 0:n], in0=gin[:, c, t:t + n], scalar=cp['conf_w'][:, c, t:t + 1], in1=ca[:, 0:n],
                                                                               op0=ALU.mult, op1=ALU.add), [f'H_gin.{c}', 'H_ca'], ['H_ca'])
                    else:
                        ctt = ct[(t // 2) % 2]
                        ctn = f'H_ct{(t // 2) % 2}'
                        P.op('act', lambda e, c=c, t=t, ctt=ctt: e.activation(out=ctt[:, 0:n], in_=gin[:, c, t:t + n], func=AF.Identity, scale=cp['conf_w'][:, c, t:t + 1], bias=k.cst[:, 2:3]),
                             [f'H_gin.{c}', 'H_conf_w', 'cst'], [ctn])
                        P.op('pool', lambda e, ctt=ctt: e.tensor_tensor(out=cb_[:, 0:n], in0=cb_[:, 0:n], in1=ctt[:, 0:n], op=ALU.add), [ctn, 'H_cb'], ['H_cb'])
                P.op('dve', lambda e, c=c: e.tensor_tensor(out=xm[:, c, 0:n], in0=ca[:, 0:n], in1=cb_[:, 0:n], op=ALU.add), ['H_ca', 'H_cb'], [f'H_xm.{c}'])
            for c in range(2):
                P.op('pe', lambda e, c=c: e.matmul(pm[:, 0:n], lhsT=k.ones_f[:], rhs=xm[:, c, 0:n], start=(c == 0), stop=(c == 1)), ['ones_f', f'H_xm.{c}'], ['H_pm'])
            for c in range(2):
                P.op('dve', lambda e, c=c: e.scalar_tensor_tensor(out=xm[:, c, 0:n], in0=pm[:, 0:n], scalar=-1.0 / 256, in1=xm[:, c, 0:n], op0=ALU.mult, op1=ALU.add),
                     ['H_pm', f'H_xm.{c}'], [f'H_xm.{c}'])
                P.op('act', lambda e, c=c: e.activation(out=sq[:, c, 0:n], in_=xm[:, c, 0:n], func=AF.Square), [f'H_xm.{c}'], [f'H_sq.{c}'])
            for c in range(2):
                P.op('pe', lambda e, c=c: e.matmul(pv[:, 0:n], lhsT=k.ones_f[:], rhs=sq[:, c, 0:n], start=(c == 0), stop=(c == 1)), ['ones_f', f'H_sq.{c}'], ['H_pv'])
            P.op('act', lambda e: e.activation(out=rstd[:, 0:n], in_=pv[:, 0:n], func=AF.Sqrt, bias=k.cst[:, 0:1], scale=1.0 / 256), ['H_pv', 'cst'], ['H_rstd'])
            P.op('dve', lambda e: e.reciprocal(out=rstd[:, 0:n], in_=rstd[:, 0:n]), ['H_rstd'], ['H_rstd'])
            for c in range(2):
                P.op('dve', lambda e, c=c: e.tensor_tensor(out=xm[:, c, 0:n], in0=xm[:, c, 0:n], in1=rstd[:, 0:n], op=ALU.mult), [f'H_xm.{c}', 'H_rstd'], [f'H_xm.{c}'])
                P.op('act', lambda e, c=c: e.activation(out=yT[:, 6 + c, 0:n], in_=xm[:, c, 0:n], func=AF.Silu, scale=cp['conf_lng'][:, c:c + 1], bias=cp['conf_lnb'][:, c:c + 1]),
                     [f'H_xm.{c}', 'H_conf_lng', 'H_conf_lnb'], [f'H_yT.{6 + c}'])
            for g in range(ng):
                for nt in range(2):
                    pp = po[(2 * g + nt) % 4]
                    ppn = f'H_po{(2 * g + nt) % 4}'
                    for kc in range(8):
                        P.op('pe', lambda e, g=g, nt=nt, kc=kc, pp=pp: e.matmul(pp[:, :], lhsT=yT[:, kc, g * 128:(g + 1) * 128], rhs=w_out[:, kc, nt * 512:(nt + 1) * 512],
                                                                               start=(kc == 0), stop=(kc == 7)), [f'H_yT.{kc}', f'w_out1.{kc}'], [ppn])
                    P.op('dve', lambda e, g=g, nt=nt, pp=pp: e.tensor_tensor(out=xo[:, g, nt * 512:(nt + 1) * 512], in0=pp[:, :], in1=xt[:, g, nt * 512:(nt + 1) * 512], op=ALU.add),
                         [ppn, 'H_xt'], [f'H_xo.{g}'])
            P.dma('sp', S['x15'][1 + s:1 + s + n, :].rearrange("(g p) f -> p g f", p=128), xo[:, 0:ng, :], keys('H_xo', 4)[0:ng], [], 'H_xo')
    P.barrier()
    if stop_after == 'l1F2':
        return
    ffn_phase(k, 1, 0, S['x15'], None, TL, final=True, flush=(not k.half), out_rows=k.TO)
    P.barrier()


def _fm(v, nch):
    return np.ascontiguousarray(np.asarray(v, np.float32).reshape(nch, 128).T)


def prep_shared(inp, T):
    f = lambda a: np.ascontiguousarray(np.asarray(a, np.float32))
    d = {}
    d['mod_w'] = f(inp['mod_w'])
    d['mod_b'] = f(inp['mod_b'])
    d['modb_fm'] = f(np.asarray(inp['mod_b']).reshape(2, 6, 8, 128).transpose(3, 0, 1, 2))
    ng = np.stack([np.asarray(inp['norm_mix_g']), np.asarray(inp['norm_ffn_g'])], axis=1)
    d['ng_fm'] = f(ng.reshape(2, 2, 8, 128).transpose(3, 0, 1, 2))
    d['final_g_bc'] = f(np.broadcast_to(np.asarray(inp['final_g'])[None, :], (128, D)))
    d['ident'] = f(np.eye(128))
    d['ab_w_in'] = f(inp['ab_w_in'][0])
    d['ab_w_out'] = f(inp['ab_w_out'][0])
    cw4 = np.asarray(inp['lru_conv_w'][0], np.float32).reshape(4, 6, 128).transpose(2, 1, 0)
    d['lru_cw'] = f(np.concatenate([cw4, np.zeros((128, 6, 1), np.float32)], axis=2))
    d['pool_fl'] = f(np.stack([np.ones(128), np.zeros(128)], axis=1))
    d['lru_cb'] = _fm(inp['lru_conv_b'][0], 6)
    for nm, src in (('lru_bdA', inp['lru_wa'][0]), ('lru_bdX', inp['lru_wx'][0])):
        src = np.asarray(src)
        bd = np.zeros((128, 2, 6, 128), np.float32)
        for dd in range(2):
            for c in range(6):
                bd[0:64, dd, c, 0:64] = src[dd, 2 * c]
                bd[64:128, dd, c, 64:128] = src[dd, 2 * c + 1]
        d[nm] = bd
    for nm, src in (('lru_ba', inp['lru_ba'][0]), ('lru_bx', inp['lru_bx'][0]), ('lru_lam', inp['lru_lambda'][0])):
        d[nm] = f(np.asarray(src).reshape(2, 6, 128).transpose(2, 0, 1))
    pw = np.asarray(inp['pool_w'][0])
    bd = np.zeros((128, 2, 128), np.float32)
    for ch in range(2):
        bd[0:64, ch, 0:64] = pw[2 * ch]
        bd[64:128, ch, 64:128] = pw[2 * ch + 1]
    d['pool_bd'] = bd
    d['pool_scale'] = _fm(inp['pool_scale'][0], 2)
    invw = np.zeros((128, 2), np.float32)
    corr = np.ones((128, 2, 2, 16), np.float32)
    wins = (2, 4, 8, 16)
    Lbig = 1 << 20
    for g, w in enumerate(wins):
        ch, half = g // 2, g % 2
        psl = slice(64 * half, 64 * half + 64)
        invw[psl, ch] = 1.0 / w
        for i in range(16):
            t = i
            cnt = (t + w - w // 2) - max(t - w // 2, 0)
            corr[psl, ch, 0, i] = float(w) / cnt
            t = Lbig - 16 + i
            cnt = min(t + w - w // 2, Lbig) - (t - w // 2)
            corr[psl, ch, 1, i] = float(w) / cnt
    d['pool_invw'] = invw
    d['pool_corr'] = corr
    d['ffn_w_up'] = f(inp['ffn_w_up'])
    d['ffn_w_down'] = f(inp['ffn_w_down'])
    d['ffn_cw'] = f(np.asarray(inp['ffn_conv_w']).reshape(2, 3, 2 * NFC, 128).transpose(3, 0, 2, 1))
    d['ffn_cb'] = f(np.asarray(inp['ffn_conv_b']).reshape(2, 2 * NFC, 128).transpose(2, 0, 1))
    w_in = np.asarray(inp['cd_w_in'][0], np.float32)
    d['cd_w_in'] = f(w_in)
    qk = w_in[:, :1536].reshape(D, 1536 // 32, 2, 16)
    d['cd_w_sw'] = f(qk[:, :, ::-1, :].reshape(D, 1536))
    d['cd_w_out'] = f(inp['cd_w_out'][0])
    t = np.arange(T)
    row = (t // GRID_W).astype(np.float32)
    col = (t % GRID_W).astype(np.float32)
    inv = (10000.0 ** (-np.arange(16, dtype=np.float32) / 16)).astype(np.float32)
    ang_r = (row[:, None] * inv).astype(np.float32)
    ang_c = (col[:, None] * inv).astype(np.float32)
    rc = np.zeros((128, T), np.float32)
    rs = np.zeros((128, T), np.float32)
    for p in range(128):
        dd = p % 64
        ang = ang_r if dd < 32 else ang_c
        fq = dd % 16
        first = (dd % 32) < 16
        rc[p] = np.cos(ang[:, fq])
        rs[p] = (-1.0 if first else 1.0) * np.sin(ang[:, fq])
    d['rope_c'] = rc
    d['rope_s'] = rs
    d['diff_l'] = f(np.stack([np.asarray(inp['diff_lq1'][0]), np.asarray(inp['diff_lk1'][0]),
                              np.asarray(inp['diff_lq2'][0]), np.asarray(inp['diff_lk2'][0])])[None])
    d['subln_g'] = f(np.asarray(inp['diff_subln_g'][0]).reshape(128, 1))
    d['conf_w'] = f(np.asarray(inp['conf_dw_w'][0]).reshape(31, 2, 128).transpose(2, 1, 0))
    d['conf_b'] = _fm(inp['conf_dw_b'][0], 2)
    d['conf_lng'] = _fm(inp['conf_ln_g'][0], 2)
    d['conf_lnb'] = _fm(inp['conf_ln_b'][0], 2)
    return d


def prep_rev(shared, T):
    f = lambda a: np.ascontiguousarray(np.asarray(a, np.float32))
    d = dict(shared)
    cw = shared['lru_cw']
    d['lru_cw'] = f(cw[:, :, ::-1])
    for nm in ('lru_bdA', 'lru_bdX', 'lru_ba', 'lru_bx', 'lru_lam'):
        d[nm] = f(shared[nm][:, ::-1])
    d['pool_fl'] = f(np.stack([np.zeros(128), np.ones(128)], axis=1))
    corr = np.ones((128, 2, 2, 16), np.float32)
    Lbig = 1 << 20
    for g, w in enumerate((2, 4, 8, 16)):
        ch, half = g // 2, g % 2
        psl = slice(64 * half, 64 * half + 64)
        for i in range(16):
            r = i
            cnt = (r + w // 2) - max(r - w // 2 + 1, 0) + 1
            corr[psl, ch, 0, i] = float(w) / cnt
            r = Lbig - 16 + i
            cnt = min(r + w // 2, Lbig - 1) - (r - w // 2 + 1) + 1
            corr[psl, ch, 1, i] = float(w) / cnt
    d['pool_corr'] = corr
    d['ffn_cw'] = f(shared['ffn_cw'][:, :, :, ::-1])
    d['conf_w'] = f(shared['conf_w'][:, :, ::-1])
    d['rope_c'] = f(shared['rope_c'][:, ::-1])
    d['rope_s'] = f(shared['rope_s'][:, ::-1])
    return d


def prep_core(inp, shared, b, T, rev=False):
    m = dict(shared)
    x = np.asarray(inp['x'][b, :T], np.float32)
    ctx = np.asarray(inp['ctx'][b], np.float32)
    if rev:
        x = x[::-1]
        ctx = ctx[::-1]
    m['x'] = np.ascontiguousarray(x)
    m['ctx'] = np.ascontiguousarray(ctx)
    m['c_fm'] = np.ascontiguousarray(np.stack([_fm(inp['c'][b], 8), _fm(inp['c_ctx'], 8)], axis=1))
    return m


_CACHE = {}


def kernel(**inputs):
    T = inputs['x'].shape[1]
    Bn = inputs['x'].shape[0]
    if T not in _CACHE:
        _CACHE[T] = build(T)
    kk = _CACHE[T]
    shared = prep_shared(inputs, T)
    shared_r = prep_rev(shared, T)
    in_maps = [prep_core(inputs, shared, b, T, rev=False) for b in range(Bn)] + \
              [prep_core(inputs, shared_r, b, T, rev=True) for b in range(Bn)]
    res = run_bass_kernel_spmd(kk.nc, in_maps, core_ids=list(range(2 * Bn)))
    out = np.empty((Bn, T, D), np.float32)
    for b in range(Bn):
        out[b, :T // 2] = np.asarray(res.results[b]['out'], np.float32)
        out[b, T // 2:] = np.asarray(res.results[Bn + b]['out'], np.float32)[::-1]
    return out
```
